# Optimizing a Trainium2 kernel written in Bass

```python
import math
import jax
import jax.numpy as jnp
from jax import lax
import numpy as np

D_MODEL = 1024
BATCH = 8
SEQ = 4096
DEPTH = 1

HEAD_DIM = 64
NSA_HEADS = 8
NSA_KV_GROUPS = 2
NSA_GROUP = NSA_HEADS // NSA_KV_GROUPS
NSA_WIDTH = NSA_HEADS * HEAD_DIM
NSA_KV_WIDTH = NSA_KV_GROUPS * HEAD_DIM
CMP_BLOCK = 32
CMP_STRIDE = 16
CMP_SPAN = CMP_BLOCK // CMP_STRIDE
CMP_HIDDEN = 256
SLC_BLOCK = 64
SLC_RATIO = SLC_BLOCK // CMP_STRIDE
SLC_TOPK = 16
OVERLAP_W = (1, 2, 2, 2, 1)
WINDOW = 512
WIN_QBLOCK = 128
REL_BUCKETS = 32
REL_MAX_DIST = 128
RWKV_HEADS = 8
RWKV_WIDTH = RWKV_HEADS * HEAD_DIM
LORA_W = 64
LORA_A = 64
LORA_G = 128
GN_EPS = 64e-5
D_FF = 2816
CONV_WIDTH = 3
RMS_EPS = 1e-6
NEG_INF = -1e30
FORCE = 1e9

RWKV_SIZES = (RWKV_WIDTH, RWKV_WIDTH, RWKV_WIDTH, LORA_W, LORA_A, LORA_G)
RWKV_IN_WIDTH = sum(RWKV_SIZES)
RWKV_SPLITS = tuple(np.cumsum(RWKV_SIZES)[:-1].tolist())
IN_SIZES = (NSA_WIDTH, NSA_KV_WIDTH, NSA_KV_WIDTH, NSA_KV_WIDTH, NSA_KV_WIDTH, NSA_KV_WIDTH, NSA_KV_WIDTH,
            NSA_HEADS * 3, RWKV_IN_WIDTH, D_MODEL, D_MODEL)
IN_WIDTH = sum(IN_SIZES)
IN_SPLITS = tuple(np.cumsum(IN_SIZES)[:-1].tolist())

kernel_name = 'hybrid_nsa_rwkv7_convffn'


def rmsnorm(x, g):
    xf = x.astype(jnp.float32)
    y = xf * lax.rsqrt(jnp.mean(xf * xf, axis=-1, keepdims=True) + RMS_EPS)
    return (y * g.astype(jnp.float32)).astype(x.dtype)


def t5_bucket(dist):
    n = jnp.maximum(dist, 0)
    max_exact = REL_BUCKETS // 2
    ratio = jnp.log(jnp.maximum(n, 1).astype(jnp.float32) / max_exact) / math.log(REL_MAX_DIST / max_exact)
    large = jnp.minimum(max_exact + (ratio * (REL_BUCKETS - max_exact)).astype(jnp.int32), REL_BUCKETS - 1)
    return jnp.where(n < max_exact, n, large)


def masked_softmax(s, mask, axis):
    s = jnp.where(mask, s.astype(jnp.float32), NEG_INF)
    e = jnp.exp(s - jnp.max(s, axis=axis, keepdims=True)) * mask
    return e / jnp.maximum(jnp.sum(e, axis=axis, keepdims=True), 1e-30)


def compress_blocks(kv, pe, w1, w2):
    b, s, g, dh = kv.shape
    chunks = kv.reshape(b, s // CMP_STRIDE, CMP_STRIDE, g, dh)
    n_cmp = s // CMP_STRIDE - CMP_SPAN + 1
    blocks = jnp.concatenate([chunks[:, i:i + n_cmp] for i in range(CMP_SPAN)], axis=2)
    blocks = blocks + pe[None, None, :, None, :]
    flat = blocks.transpose(0, 1, 3, 2, 4).reshape(b, n_cmp, g, CMP_BLOCK * dh)
    return jax.nn.gelu(flat @ w1) @ w2


def nsa_mixer(q, k_c, v_c, k_s, v_s, k_w, v_w, gate, rel_bias, q_norm_g, k_norm_g,
              cmp_pe_k, cmp_w1_k, cmp_w2_k, cmp_pe_v, cmp_w1_v, cmp_w2_v):
    b, s, _ = q.shape
    G, R, dh = NSA_KV_GROUPS, NSA_GROUP, HEAD_DIM
    q = rmsnorm(q.reshape(b, s, G, R, dh), q_norm_g) * (dh ** -0.5)
    k_c, v_c, k_s, v_s, k_w, v_w = [t.reshape(b, s, G, dh) for t in (k_c, v_c, k_s, v_s, k_w, v_w)]
    t_pos = jnp.arange(s)
    bias_tab = rel_bias.reshape(REL_BUCKETS, G, R)

    kc = rmsnorm(compress_blocks(k_c, cmp_pe_k, cmp_w1_k, cmp_w2_k), k_norm_g)
    vc = compress_blocks(v_c, cmp_pe_v, cmp_w1_v, cmp_w2_v)
    n_cmp = kc.shape[1]
    blk_end = jnp.arange(n_cmp) * CMP_STRIDE + CMP_BLOCK - 1
    dist_c = t_pos[:, None] - blk_end[None, :]
    bias_c = bias_tab[t5_bucket(dist_c)].transpose(2, 3, 0, 1)
    s_c = jnp.einsum('bsgrd,bcgd->bgrsc', q, kc).astype(jnp.float32) + bias_c
    p_c = masked_softmax(s_c, dist_c >= 0, axis=-1)
    o_cmp = jnp.einsum('bgrsc,bcgd->bsgrd', p_c, vc)

    n_slc = s // SLC_BLOCK
    imp_c = jnp.sum(p_c, axis=2)
    imp_pad = jnp.pad(imp_c, ((0, 0), (0, 0), (0, 0), (CMP_SPAN - 1, CMP_SPAN)))
    imp = OVERLAP_W[0] * imp_pad[..., 0:SLC_RATIO * n_slc:SLC_RATIO]
    for o in range(1, SLC_RATIO + CMP_SPAN - 1):
        imp = imp + OVERLAP_W[o] * imp_pad[..., o:o + SLC_RATIO * n_slc:SLC_RATIO]
    blk = jnp.arange(n_slc)[None, :]
    cur = (t_pos // SLC_BLOCK)[:, None]
    forced = (blk == 0) | (blk == cur) | (blk == cur - 1)
    score = jnp.where(forced, FORCE, jnp.where(blk <= cur, imp, -FORCE))
    n_sel = min(SLC_TOPK, n_slc)
    _, idx = lax.top_k(score, n_sel)
    sel_valid = idx <= cur

    nq = s // SLC_BLOCK
    k_blocks = rmsnorm(k_s, k_norm_g).reshape(b, n_slc, SLC_BLOCK, G, dh).transpose(0, 3, 1, 2, 4)
    v_blocks = v_s.reshape(b, n_slc, SLC_BLOCK, G, dh).transpose(0, 3, 1, 2, 4)
    gather = jax.vmap(jax.vmap(lambda blocks, ix: blocks[ix]))
    gi = jnp.arange(G).reshape(1, G, 1, 1, 1, 1)
    ri = jnp.arange(R).reshape(1, 1, R, 1, 1, 1)

    def slc_block(args):
        q_b, idx_b, val_b, t_b = args
        k_sel = gather(k_blocks, idx_b)
        v_sel = gather(v_blocks, idx_b)
        kpos = idx_b[..., None] * SLC_BLOCK + jnp.arange(SLC_BLOCK)
        dist = t_b[:, None, None] - kpos
        mask = (val_b[..., None] & (dist >= 0))[:, :, None]
        bias = bias_tab[t5_bucket(dist)[:, :, None], gi, ri]
        sc = jnp.einsum('bcgrd,bgcnld->bgrcnl', q_b, k_sel).astype(jnp.float32) + bias
        p = masked_softmax(sc, mask, axis=(-2, -1))
        return jnp.einsum('bgrcnl,bgcnld->bcgrd', p, v_sel)

    q_ch = jnp.moveaxis(q.reshape(b, nq, SLC_BLOCK, G, R, dh), 1, 0)
    idx_ch = jnp.moveaxis(idx.reshape(b, G, nq, SLC_BLOCK, n_sel), 2, 0)
    val_ch = jnp.moveaxis(sel_valid.reshape(b, G, nq, SLC_BLOCK, n_sel), 2, 0)
    t_ch = t_pos.reshape(nq, SLC_BLOCK)
    o_slc = jnp.moveaxis(lax.map(slc_block, (q_ch, idx_ch, val_ch, t_ch)), 0, 1).reshape(b, s, G, R, dh)

    nw = s // WIN_QBLOCK
    k_pad = jnp.pad(rmsnorm(k_w, k_norm_g), ((0, 0), (WINDOW, 0), (0, 0), (0, 0)))
    v_pad = jnp.pad(v_w, ((0, 0), (WINDOW, 0), (0, 0), (0, 0)))

    def win_block(args):
        q_b, i = args
        start = i * WIN_QBLOCK
        kw = lax.dynamic_slice_in_dim(k_pad, start, WIN_QBLOCK + WINDOW, axis=1)
        vw = lax.dynamic_slice_in_dim(v_pad, start, WIN_QBLOCK + WINDOW, axis=1)
        t_b = start + jnp.arange(WIN_QBLOCK)
        s_b = start - WINDOW + jnp.arange(WIN_QBLOCK + WINDOW)
        dist = t_b[:, None] - s_b[None, :]
        mask = (dist >= 0) & (dist < WINDOW) & (s_b[None, :] >= 0)
        bias = bias_tab[t5_bucket(dist)].transpose(2, 3, 0, 1)
        sc = jnp.einsum('bqgrd,bkgd->bgrqk', q_b, kw).astype(jnp.float32) + bias
        p = masked_softmax(sc, mask, axis=-1)
        return jnp.einsum('bgrqk,bkgd->bqgrd', p, vw)

    q_wch = jnp.moveaxis(q.reshape(b, nw, WIN_QBLOCK, G, R, dh), 1, 0)
    o_win = jnp.moveaxis(lax.map(win_block, (q_wch, jnp.arange(nw))), 0, 1).reshape(b, s, G, R, dh)

    gt = jax.nn.sigmoid(gate.reshape(b, s, G, R, 3).astype(jnp.float32))
    o = gt[..., 0:1] * o_cmp + gt[..., 1:2] * o_slc + gt[..., 2:3] * o_win
    return o.reshape(b, s, NSA_WIDTH)


def rwkv7_mixer(p_in, mu, w0, w2, a0, a2, g2, k_k, k_a, r_k, ln_g, ln_b):
    b, s, _ = p_in.shape
    H, N = RWKV_HEADS, HEAD_DIM
    shifted = jnp.pad(p_in, ((0, 0), (1, 0), (0, 0)))[:, :-1]
    xl = p_in + (shifted - p_in) * mu
    r, k, v, xw, xa, xg = jnp.split(xl, RWKV_SPLITS, axis=-1)
    w = -jax.nn.softplus(-(w0 + jnp.tanh(xw) @ w2)) - 0.5
    decay = jnp.exp(-jnp.exp(w.astype(jnp.float32)))
    a = jax.nn.sigmoid(a0 + xa @ a2)
    g = jax.nn.sigmoid(xg) @ g2
    heads = lambda t: t.reshape(b, s, H, N).astype(jnp.float32)
    r, k, v, a, decay = heads(r), heads(k), heads(v), heads(a), heads(decay)
    kk = k * k_k.reshape(H, N).astype(jnp.float32)
    kk = kk / jnp.maximum(jnp.sqrt(jnp.sum(kk * kk, axis=-1, keepdims=True)), 1e-12)
    k = k * (1.0 + (a - 1.0) * k_a.reshape(H, N).astype(jnp.float32))
    xs = tuple(jnp.moveaxis(t, 1, 0) for t in (r, decay, k, v, -kk, kk * a))

    def step(state, inp):
        r_t, w_t, k_t, v_t, a_t, b_t = inp
        sa = jnp.einsum('bhij,bhj->bhi', state, a_t)
        state = state * w_t[:, :, None, :] + sa[..., None] * b_t[:, :, None, :] + v_t[..., None] * k_t[:, :, None, :]
        return state, jnp.einsum('bhij,bhj->bhi', state, r_t)

    state0 = jnp.zeros((b, H, N, N), jnp.float32)
    _, y = lax.scan(step, state0, xs)
    y = jnp.moveaxis(y, 0, 1)
    mean = jnp.mean(y, axis=-1, keepdims=True)
    var = jnp.mean(jnp.square(y - mean), axis=-1, keepdims=True)
    y = (y - mean) * lax.rsqrt(var + GN_EPS) * ln_g.reshape(H, N) + ln_b.reshape(H, N)
    y = y + jnp.sum(r * k * r_k, axis=-1, keepdims=True) * v
    return y.reshape(b, s, RWKV_WIDTH) * g


def conv_ffn(h, w_up, conv_w, conv_b, w_down):
    u = h @ w_up
    s = u.shape[1]
    up = jnp.pad(u, ((0, 0), (CONV_WIDTH - 1, 0), (0, 0)))
    c = conv_b + conv_w[0] * up[:, 0:s]
    for j in range(1, CONV_WIDTH):
        c = c + conv_w[j] * up[:, j:j + s]
    val, gate = jnp.split(c, 2, axis=-1)
    return (jax.nn.silu(gate) * val) @ w_down


def setup_inputs(seed: int = 0) -> dict:
    key = jax.random.key(seed)
    ks = iter(jax.random.split(key, 40))
    L, D = DEPTH, D_MODEL

    def nrm(shape, scale):
        return scale * jax.random.normal(next(ks), shape, jnp.float32)

    def unif(shape, lo, hi):
        return jax.random.uniform(next(ks), shape, jnp.float32, lo, hi)

    return {
        'x': nrm((BATCH, SEQ, D), 1.0),
        'attn_norm_g': 1.0 + nrm((L, D), 0.1),
        'w_in': nrm((L, D, IN_WIDTH), D ** -0.5),
        'rel_bias': nrm((REL_BUCKETS, NSA_HEADS), 0.5),
        'q_norm_g': 1.0 + nrm((L, HEAD_DIM), 0.1),
        'k_norm_g': 1.0 + nrm((L, HEAD_DIM), 0.1),
        'cmp_pe_k': nrm((L, CMP_BLOCK, HEAD_DIM), 0.1),
        'cmp_w1_k': nrm((L, CMP_BLOCK * HEAD_DIM, CMP_HIDDEN), (CMP_BLOCK * HEAD_DIM) ** -0.5),
        'cmp_w2_k': nrm((L, CMP_HIDDEN, HEAD_DIM), CMP_HIDDEN ** -0.5),
        'cmp_pe_v': nrm((L, CMP_BLOCK, HEAD_DIM), 0.1),
        'cmp_w1_v': nrm((L, CMP_BLOCK * HEAD_DIM, CMP_HIDDEN), (CMP_BLOCK * HEAD_DIM) ** -0.5),
        'cmp_w2_v': nrm((L, CMP_HIDDEN, HEAD_DIM), CMP_HIDDEN ** -0.5),
        'rwkv_mu': unif((L, RWKV_IN_WIDTH), 0.0, 1.0),
        'rwkv_w0': unif((L, RWKV_WIDTH), -6.0, -1.0),
        'rwkv_w2': nrm((L, LORA_W, RWKV_WIDTH), 0.5 * LORA_W ** -0.5),
        'rwkv_a0': nrm((L, RWKV_WIDTH), 0.1),
        'rwkv_a2': nrm((L, LORA_A, RWKV_WIDTH), 0.5 * LORA_A ** -0.5),
        'rwkv_g2': nrm((L, LORA_G, RWKV_WIDTH), LORA_G ** -0.5),
        'rwkv_k_k': 0.85 + nrm((L, RWKV_WIDTH), 0.05),
        'rwkv_k_a': 1.0 + nrm((L, RWKV_WIDTH), 0.05),
        'rwkv_r_k': nrm((L, RWKV_HEADS, HEAD_DIM), 0.1),
        'rwkv_ln_g': 1.0 + nrm((L, RWKV_WIDTH), 0.1),
        'rwkv_ln_b': nrm((L, RWKV_WIDTH), 0.01),
        'w_proj_a': nrm((L, NSA_WIDTH, D), NSA_WIDTH ** -0.5),
        'w_proj_b': nrm((L, RWKV_WIDTH, D), RWKV_WIDTH ** -0.5),
        'w_out': nrm((L, D, D), D ** -0.5),
        'ffn_norm_g': 1.0 + nrm((L, D), 0.1),
        'w_up': nrm((L, D, 2 * D_FF), D ** -0.5),
        'conv_w': nrm((L, CONV_WIDTH, 2 * D_FF), CONV_WIDTH ** -0.5),
        'conv_b': nrm((L, 2 * D_FF), 0.01),
        'w_down': nrm((L, D_FF, D), D_FF ** -0.5),
    }


def reference(x, attn_norm_g, w_in, rel_bias, q_norm_g, k_norm_g, cmp_pe_k, cmp_w1_k, cmp_w2_k,
              cmp_pe_v, cmp_w1_v, cmp_w2_v, rwkv_mu, rwkv_w0, rwkv_w2, rwkv_a0, rwkv_a2, rwkv_g2,
              rwkv_k_k, rwkv_k_a, rwkv_r_k, rwkv_ln_g, rwkv_ln_b, w_proj_a, w_proj_b, w_out,
              ffn_norm_g, w_up, conv_w, conv_b, w_down):
    for l in range(DEPTH):
        h = rmsnorm(x, attn_norm_g[l])
        proj = h @ w_in[l]
        q, k_c, v_c, k_s, v_s, k_w, v_w, nsa_gate, rwkv_in, gate_a, gate_b = jnp.split(proj, IN_SPLITS, axis=-1)
        o_a = nsa_mixer(q, k_c, v_c, k_s, v_s, k_w, v_w, nsa_gate, rel_bias, q_norm_g[l], k_norm_g[l],
                        cmp_pe_k[l], cmp_w1_k[l], cmp_w2_k[l], cmp_pe_v[l], cmp_w1_v[l], cmp_w2_v[l])
        o_b = rwkv7_mixer(rwkv_in, rwkv_mu[l], rwkv_w0[l], rwkv_w2[l], rwkv_a0[l], rwkv_a2[l], rwkv_g2[l],
                          rwkv_k_k[l], rwkv_k_a[l], rwkv_r_k[l], rwkv_ln_g[l], rwkv_ln_b[l])
        merged = jax.nn.sigmoid(gate_a) * (o_a @ w_proj_a[l]) + jax.nn.sigmoid(gate_b) * (o_b @ w_proj_b[l])
        x = x + (merged @ w_out[l]).astype(x.dtype)
        h2 = rmsnorm(x, ffn_norm_g[l])
        x = x + conv_ffn(h2, w_up[l], conv_w[l], conv_b[l], w_down[l]).astype(x.dtype)
    return x
```

```python
import contextlib
import numpy as np
import ml_dtypes
import concourse.bass as bass
import concourse.mybir as mybir
from concourse.bass_utils import run_bass_kernel_spmd

F32 = mybir.dt.float32
BF16 = mybir.dt.bfloat16
AF = mybir.ActivationFunctionType
ALU = mybir.AluOpType
AX = mybir.AxisListType

S = 4096
D = 1024
NT = S // 128
IN_WIDTH = 5144
RW0 = 1304
GA0 = 3096
GB0 = 4120
DFF = 2816
RMS_EPS = 1e-6


class Buf:
    def __init__(self, t, name):
        self.t = t
        self.name = name
        self.w = None
        self.r = {}
        self.psum = False

    def __getitem__(self, idx):
        return self.t[idx]


class Builder:
    SEM_ROLL = 30000

    def __init__(self, nc):
        self.nc = nc
        self.stack = contextlib.ExitStack()
        self.root = self.stack
        self.eng = {"pe": nc.tensor, "act": nc.scalar, "dve": nc.vector,
                    "pool": nc.gpsimd, "sp": nc.sync}
        self.sem = {}
        self.cnt = {}
        self.seen = {e: {} for e in self.eng}
        self.nsem = 0
        self.lanes = {}
        self.lane_rr = {}
        self.last_tok = {}
        for e in self.eng:
            self._roll(e)

    def newsem(self, name):
        self.nsem += 1
        return self.root.enter_context(self.nc.semaphore(f"{name}_{self.nsem}"))

    def sb(self, name, shape, dt=F32):
        self.nsem += 1
        name = f"sb{self.nsem}_{name}"
        return Buf(self.stack.enter_context(self.nc.sbuf_tensor(name, list(shape), dt)), name)

    def ps(self, name, shape, dt=F32):
        self.nsem += 1
        name = f"ps{self.nsem}_{name}"
        bf = Buf(self.stack.enter_context(self.nc.psum_tensor(name, list(shape), dt)), name)
        bf.psum = True
        return bf

    def dram(self, name, shape, dt=F32, kind="Internal"):
        return Buf(self.nc.dram_tensor(name, list(shape), dt, kind=kind), name)

    def _roll(self, e):
        self.sem[e] = self.newsem("s" + e)
        self.cnt[e] = 0

    def _wait(self, e, tok):
        sem, val = tok
        k = id(sem)
        if self.seen[e].get(k, 0) < val:
            self.eng[e].wait_ge(sem, val)
            self.seen[e][k] = val

    def _deps(self, e, reads, writes):
        for b in reads:
            if b.w is not None:
                we, tok = b.w
                self._wait(e, tok)
            if b.psum:
                for re_, tok in b.r.items():
                    if re_ != e:
                        self._wait(e, tok)
        for b in writes:
            if b.w is not None:
                we, tok = b.w
                if we != e:
                    self._wait(e, tok)
            for re_, tok in b.r.items():
                if re_ != e:
                    self._wait(e, tok)

    def op(self, e, fn, reads=(), writes=()):
        if self.cnt[e] >= self.SEM_ROLL:
            self._roll(e)
        self._deps(e, reads, writes)
        ins = fn(self.eng[e])
        self.cnt[e] += 1
        tok = (self.sem[e], self.cnt[e])
        ins.then_inc(self.sem[e], 1)
        self.last_tok[e] = tok
        for b in reads:
            b.r[e] = tok
        for b in writes:
            b.w = (e, tok)
            b.r = {}
        return tok

    def dma(self, q, out, in_, reads=(), writes=(), nlanes=6, **kw):
        if q not in self.lanes:
            self.lanes[q] = [[self.newsem("l" + q), 0] for _ in range(nlanes)]
            self.lane_rr[q] = 0
        li = self.lane_rr[q]
        self.lane_rr[q] = (li + 1) % len(self.lanes[q])
        lane = self.lanes[q][li]
        if lane[1] >= 1800:
            self._wait(q, (lane[0], 16 * lane[1]))
            lane[0] = self.newsem("l" + q)
            lane[1] = 0
        if lane[1] > 0:
            self._wait(q, (lane[0], 16 * lane[1]))
        self._deps_dma(q, reads, writes)
        ins = self.eng[q].dma_start(out=out, in_=in_, **kw)
        lane[1] += 1
        tok = (lane[0], 16 * lane[1])
        ins.then_inc(lane[0], 16)
        key = "dma_" + q + str(li)
        for b in reads:
            b.r[key] = tok
        for b in writes:
            b.w = (key, tok)
            b.r = {}
        return tok

    def _deps_dma(self, q, reads, writes):
        for b in reads:
            if b.w is not None:
                self._wait(q, b.w[1])
        for b in writes:
            if b.w is not None:
                self._wait(q, b.w[1])
            for re_, tok in b.r.items():
                self._wait(q, tok)

    def barrier(self):
        toks = list(self.last_tok.values())
        for q, lanes in self.lanes.items():
            for lane in lanes:
                if lane[1] > 0:
                    toks.append((lane[0], 16 * lane[1]))
        for e in self.eng:
            for tok in toks:
                self._wait(e, tok)

    def wait_all_on(self, e):
        toks = list(self.last_tok.values())
        for q, lanes in self.lanes.items():
            for lane in lanes:
                if lane[1] > 0:
                    toks.append((lane[0], 16 * lane[1]))
        for tok in toks:
            self._wait(e, tok)

    @contextlib.contextmanager
    def scope(self):
        old = self.stack
        self.stack = contextlib.ExitStack()
        try:
            yield
            self.barrier()
        finally:
            self.stack.close()
            self.stack = old

    def close(self):
        self.stack.close()


NEG = -30000.0


def _bucket(dist):
    n = np.maximum(dist, 0)
    ratio = np.log(np.maximum(n, 1).astype(np.float32) / np.float32(16.0)) / np.float32(np.log(8.0))
    large = np.minimum(16 + (ratio * 16).astype(np.int32), 31)
    return np.where(n < 16, n, large)


def host_consts(rel_bias):
    rel = np.asarray(rel_bias, np.float32)
    c = {}
    c["ident"] = np.eye(128, dtype=np.float32).astype(ml_dtypes.bfloat16)
    c["identf"] = np.eye(128, dtype=np.float32)
    kp = np.arange(128)[:, None]
    cc = np.arange(640)[None, :]
    dist = cc - kp
    bt = rel[_bucket(dist)]
    tw = np.where(((dist >= 0) & (dist < 512))[..., None], bt, np.float32(NEG))
    ts = np.where((dist >= 0)[..., None], bt, np.float32(NEG))
    c["tw"] = np.ascontiguousarray(tw.transpose(0, 2, 1)).astype(np.float32)
    c["ts"] = np.ascontiguousarray(ts.transpose(0, 2, 1)).astype(np.float32)
    cidx = np.arange(256)[:, None]
    qidx = np.arange(S)[None, :]
    dc = qidx - 16 * cidx - 31
    bcg = rel[_bucket(dc)]
    ok = (dc >= 0) & (cidx < 255)
    bc = np.where(ok[..., None], bcg, np.float32(NEG))
    c["biasc"] = np.ascontiguousarray(bc.transpose(2, 0, 1)).reshape(8, 2, 128, S).astype(np.float32)
    A = np.zeros((256, 64), np.float32)
    Wt = (1, 2, 2, 2, 1)
    for ci in range(255):
        for j in range(64):
            o = ci + 1 - 4 * j
            if 0 <= o <= 4:
                A[ci, j] = Wt[o]
    c["amat"] = A.reshape(2, 128, 64)
    E = np.zeros((64, S), np.float32)
    E[np.arange(S) // 64, np.arange(S)] = 1.0
    c["emat"] = E.astype(ml_dtypes.bfloat16)
    qp = np.arange(128)[:, None, None]
    qt = np.arange(32)[None, :, None]
    j = np.arange(64)[None, None, :]
    cur = (128 * qt + qp) // 64
    cand = (j >= 1) & (j <= cur - 2)
    c["candneg"] = np.where(cand, 0.0, -1e9).astype(np.float32)
    c["fz"] = ((j == 0) | (j == cur) | (j == cur - 1)).astype(np.float32)
    tri = np.triu(np.ones((64, 64), np.float32))
    c["rwmask"] = np.ascontiguousarray(np.stack([np.triu(np.ones((64, 64), np.float32), 1), tri, np.tril(np.ones((64, 64), np.float32), -1)], axis=1))
    rr = np.ones((64, 256), np.float32)
    rr[:, ::64] = 0.0
    c["rwreset"] = rr
    c["b31"] = np.ascontiguousarray(np.broadcast_to(rel[31][None, :], (128, 8))).astype(np.float32)
    return c


CONST_SPECS = {
    "ident": ([128, 128], BF16), "identf": ([128, 128], F32),
    "tw": ([128, 8, 640], F32), "ts": ([128, 8, 640], F32),
    "biasc": ([8, 2, 128, S], F32), "amat": ([2, 128, 64], F32),
    "emat": ([64, S], BF16), "candneg": ([128, 32, 64], F32), "fz": ([128, 32, 64], F32),
    "b31": ([128, 8], F32), "rwmask": ([64, 3, 64], F32), "rwreset": ([64, 256], F32),
}

W_SPECS = {
    "x": [S, D], "attn_norm_g": [1, D], "w_in": [1, D, IN_WIDTH], "q_norm_g": [1, 64], "k_norm_g": [1, 64],
    "cmp_pe_k": [1, 32, 64], "cmp_w1_k": [1, 2048, 256], "cmp_w2_k": [1, 256, 64],
    "cmp_pe_v": [1, 32, 64], "cmp_w1_v": [1, 2048, 256], "cmp_w2_v": [1, 256, 64],
    "rwkv_mu": [1, 1792], "rwkv_w0": [1, 512], "rwkv_w2": [1, 64, 512], "rwkv_a0": [1, 512],
    "rwkv_a2": [1, 64, 512], "rwkv_g2": [1, 128, 512], "rwkv_k_k": [1, 512], "rwkv_k_a": [1, 512],
    "rwkv_r_k": [1, 8, 64], "rwkv_ln_g": [1, 512], "rwkv_ln_b": [1, 512],
    "w_proj_a": [1, 512, D], "w_proj_b": [1, 512, D], "w_out": [1, D, D], "ffn_norm_g": [1, D],
    "w_up": [1, D, 2 * DFF], "conv_w": [1, 3, 2 * DFF], "conv_b": [1, 2 * DFF], "w_down": [1, DFF, D],
}


class Prog:
    def __init__(self, debug=()):
        self.debug = set(debug)
        nc = bass.Bass("TRN2", target_bir_lowering=False)
        self.nc = nc
        self.inp = {}
        for k, shp in W_SPECS.items():
            self.inp[k] = nc.dram_tensor(k, list(shp), F32, kind="ExternalInput").ap()
        for k, (shp, dt) in CONST_SPECS.items():
            self.inp[k] = nc.dram_tensor(k, list(shp), dt, kind="ExternalInput").ap()
        self.out = nc.dram_tensor("out", [S, D], F32, kind="ExternalOutput").ap()
        self.dbg = {}
        self.b = Builder(nc)

    def dbg_out(self, name, shape, dt=F32):
        t = self.nc.dram_tensor("dbg_" + name, list(shape), dt, kind="ExternalOutput").ap()
        self.dbg[name] = t
        return t

    def load_weight(self, dst, src, ncols, gvec=None, kch=8, stage=None, eng="act"):
        b = self.b
        for c in range(kch):
            st = stage[c % len(stage)]
            b.dma("sp", st[:, :ncols], src[c * 128:(c + 1) * 128, :], writes=[st])
            if gvec is not None:
                b.op(eng, lambda e: e.activation(out=dst[:, c, :], in_=st[:, :ncols], func=AF.Copy, scale=gvec[:, c:c + 1])
                     if eng == "act" else e.tensor_scalar_mul(out=dst[:, c, :], in0=st[:, :ncols], scalar1=gvec[:, c:c + 1]),
                     reads=[st, gvec], writes=[dst])
            else:
                b.op(eng, lambda e: e.copy(out=dst[:, c, :], in_=st[:, :ncols]) if eng == "act"
                     else e.tensor_copy(out=dst[:, c, :], in_=st[:, :ncols]), reads=[st], writes=[dst])

    def load_gain(self, name, src_vec, kch=8):
        b = self.b
        g = b.sb(name, [128, kch], F32)
        b.dma("sp", g[:], src_vec.rearrange("(c p) -> p c", p=128), writes=[g], allow_slow_non_contiguous=True)
        return g

    def bcast_row(self, name, src_row, n):
        b = self.b
        t = b.sb(name, [128, n], F32)
        b.dma("sp", t[:], src_row.partition_broadcast(128), writes=[t])
        return t

    def make_hT(self, x_ap, t, xt, junk, ss, hb, pt, hT, ident):
        b = self.b
        b.dma("sp", xt[:], x_ap[t * 128:(t + 1) * 128, :], writes=[xt])
        b.op("act", lambda e: e.activation(out=junk[:], in_=xt[:], func=AF.Square, accum_out=ss[:]), reads=[xt], writes=[junk, ss])
        b.op("act", lambda e: e.activation(out=ss[:], in_=ss[:], func=AF.Sqrt, scale=1.0 / D, bias=RMS_EPS), reads=[ss], writes=[ss])
        b.op("dve", lambda e: e.reciprocal(out=ss[:], in_=ss[:]), reads=[ss], writes=[ss])
        b.op("dve", lambda e: e.tensor_scalar_mul(out=hb[:], in0=xt[:], scalar1=ss[:]), reads=[xt, ss], writes=[hb])
        for c in range(8):
            b.op("pe", lambda e: e.transpose(out=pt[:, c, :], in_=hb[:, c * 128:(c + 1) * 128], identity=ident[:]),
                 reads=[hb, ident], writes=[pt])
        b.op("act", lambda e: e.copy(out=hT[:], in_=pt[:]), reads=[pt], writes=[hT])

    def alloc_root(self):
        b = self.b
        I = self.inp
        self.ident = b.sb("ident", [128, 128], BF16)
        b.dma("sp", self.ident[:], I["ident"], writes=[self.ident])
        self.identf = b.sb("identf", [128, 128], F32)
        b.dma("sp", self.identf[:], I["identf"], writes=[self.identf])

    def alloc_persistent(self):
        b = self.b
        I = self.inp
        if not hasattr(self, "ident"):
            self.alloc_root()
        self.ksE = b.sb("ksE", [128, 2, S], BF16)
        self.kwT = b.sb("kwT", [64, 2, S], BF16)
        self.vaug_s = b.sb("vaug_s", [128, NT, 2, 65], BF16)
        self.vaug_w = b.sb("vaug_w", [128, NT, 2, 65], BF16)
        self.gts = b.sb("gts", [128, NT, 24], F32)
        self.kcT = b.sb("kcT", [64, 2, 256], BF16)
        self.vcA = b.sb("vcA", [128, 2, 2, 129], F32)
        self.qT_d = b.dram("qT_d", [8, 64, S], BF16)
        self.oaT_d = b.dram("oaT_d", [4, 128, S], BF16)
        self.obT_d = b.dram("obT_d", [4, 128, S], BF16)
        for g in range(2):
            b.dma("sp", self.ksE[64:128, g, :], I["emat"], writes=[self.ksE])
        b.op("pool", lambda e: e.memset(self.vaug_s[:, :, :, 64:65], 1.0), writes=[self.vaug_s])
        b.op("pool", lambda e: e.memset(self.vaug_w[:, :, :, 64:65], 1.0), writes=[self.vaug_w])
        b.op("pool", lambda e: e.memset(self.vcA[:, :, :, 64:65], 1.0), writes=[self.vcA])
        for g in range(2):
            for ct in range(2):
                b.dma("sp", self.vcA[:, g, ct, 65:129], I["amat"][ct], writes=[self.vcA])

    def phase_nsa_proj(self):
        b = self.b
        I = self.inp
        with b.scope():
            gat = self.load_gain("gat", I["attn_norm_g"][0])
            wn = b.sb("wn", [128, 8, RW0], BF16)
            stage = [b.sb(f"wst{i}", [128, RW0], F32) for i in range(2)]
            self.load_weight(wn, I["w_in"][0][:, 0:RW0], RW0, gvec=gat, stage=stage)
            gq = self.bcast_row("gq", I["q_norm_g"][0], 64)
            gk = self.bcast_row("gk", I["k_norm_g"][0], 64)
            gq_rep = b.sb("gq_rep", [128, 8, 64], F32)
            gk_rep = b.sb("gk_rep", [128, 2, 64], F32)
            b.op("act", lambda e: e.activation(out=gq_rep[:], in_=gq[:, None, :].to_broadcast([128, 8, 64]), func=AF.Copy, scale=0.125),
                 reads=[gq], writes=[gq_rep])
            b.op("act", lambda e: e.activation(out=gk_rep[:], in_=gk[:, None, :].to_broadcast([128, 2, 64]), func=AF.Copy, scale=1.0),
                 reads=[gk], writes=[gk_rep])
            if getattr(self, 'stop_at', 99) <= 0:
                return
            kcdup = b.sb("kcdup", [128, 2, S + 1], BF16)
            vcdup = b.sb("vcdup", [128, 2, S + 1], BF16)
            xt = [b.sb(f"xt{i}", [128, D], F32) for i in range(2)]
            junk = b.sb("junk", [128, D], BF16)
            ss = [b.sb(f"ss{i}", [128, 1], F32) for i in range(2)]
            hb = [b.sb(f"hb{i}", [128, D], BF16) for i in range(2)]
            hT = [b.sb(f"hT{i}", [128, 8, 128], BF16) for i in range(2)]
            sq = b.sb("sq", [128, 12, 64], F32)
            ssq = b.sb("ssq", [128, 12], F32)
            tmpq = b.sb("tmpq", [128, 8, 64], F32)
            tmpk = b.sb("tmpk", [128, 4, 64], F32)
            qb = b.sb("qb", [128, 512], BF16)
            kb = b.sb("kb", [128, 4, 64], BF16)
            cb = b.sb("cb", [128, 4, 2, 64], BF16)
            qst = [b.sb(f"qst{i}", [64, 8, 128], BF16) for i in range(2)]
            pt = b.ps("pt", [128, 8, 128], BF16)
            pm = [b.ps(f"pm{i}", [128, 512], F32) for i in range(3)]
            ptq = b.ps("ptq", [128, 8, 128], BF16)
            ptk = b.ps("ptk", [128, 8, 128], BF16)
            colgroups = [(0, 512), (512, 1024), (1024, RW0)]
            for t in range(getattr(self, 'nt_limit', NT)):
                i = t % 2
                self.make_hT(I["x"], t, xt[i], junk, ss[i], hb[i], pt, hT[i], self.ident)
                for n, (c0, c1) in enumerate(colgroups):
                    for c in range(8):
                        b.op("pe", lambda e: e.matmul(pm[n][:, :c1 - c0], lhsT=hT[i][:, c, :], rhs=wn[:, c, c0:c1],
                                                      start=(c == 0), stop=(c == 7)), reads=[hT[i], wn], writes=[pm[n]])
                if getattr(self, 'stop_at', 99) <= 1:
                    continue
                b.op("act", lambda e: e.activation(out=sq[:, 0:8, :], in_=pm[0][:, 0:512].rearrange("p (h d) -> p h d", d=64), func=AF.Square),
                     reads=[pm[0]], writes=[sq])
                b.op("act", lambda e: e.activation(out=sq[:, 8:10, :], in_=pm[1][:, 256:384].rearrange("p (h d) -> p h d", d=64), func=AF.Square),
                     reads=[pm[1]], writes=[sq])
                b.op("act", lambda e: e.activation(out=sq[:, 10:12, :], in_=pm[2][:, 0:128].rearrange("p (h d) -> p h d", d=64), func=AF.Square),
                     reads=[pm[2]], writes=[sq])
                b.op("dve", lambda e: e.tensor_reduce(out=ssq[:], in_=sq[:], axis=AX.X, op=ALU.add), reads=[sq], writes=[ssq])
                b.op("act", lambda e: e.activation(out=ssq[:], in_=ssq[:], func=AF.Sqrt, scale=1.0 / 64, bias=RMS_EPS), reads=[ssq], writes=[ssq])
                b.op("dve", lambda e: e.reciprocal(out=ssq[:], in_=ssq[:]), reads=[ssq], writes=[ssq])
                if getattr(self, 'stop_at', 99) <= 2:
                    continue
                b.op("dve", lambda e: e.tensor_tensor(out=tmpq[:], in0=pm[0][:, 0:512].rearrange("p (h d) -> p h d", d=64),
                                                      in1=ssq[:, 0:8].unsqueeze(2).to_broadcast([128, 8, 64]), op=ALU.mult),
                     reads=[pm[0], ssq], writes=[tmpq])
                b.op("pool", lambda e: e.tensor_tensor(out=qb[:].rearrange("p (h d) -> p h d", d=64), in0=tmpq[:], in1=gq_rep[:], op=ALU.mult),
                     reads=[tmpq, gq_rep], writes=[qb])
                for h in range(8):
                    b.op("pe", lambda e: e.transpose(out=ptq[0:64, h, :], in_=qb[:, h * 64:(h + 1) * 64], identity=self.ident[:]),
                         reads=[qb, self.ident], writes=[ptq])
                b.op("act", lambda e: e.copy(out=qst[i][:], in_=ptq[0:64, :, :]), reads=[ptq], writes=[qst[i]])
                b.dma("pool", self.qT_d[:, :, t * 128:(t + 1) * 128].rearrange("h d t -> d h t"), qst[i][:], reads=[qst[i]], writes=[self.qT_d])
                if getattr(self, 'stop_at', 99) <= 3:
                    continue
                b.op("dve", lambda e: e.tensor_tensor(out=tmpk[:, 0:2, :], in0=pm[1][:, 256:384].rearrange("p (h d) -> p h d", d=64),
                                                      in1=ssq[:, 8:10].unsqueeze(2).to_broadcast([128, 2, 64]), op=ALU.mult),
                     reads=[pm[1], ssq], writes=[tmpk])
                b.op("dve", lambda e: e.tensor_tensor(out=tmpk[:, 2:4, :], in0=pm[2][:, 0:128].rearrange("p (h d) -> p h d", d=64),
                                                      in1=ssq[:, 10:12].unsqueeze(2).to_broadcast([128, 2, 64]), op=ALU.mult),
                     reads=[pm[2], ssq], writes=[tmpk])
                b.op("pool", lambda e: e.tensor_tensor(out=kb[:].rearrange("p (a g) d -> p a g d", a=2), in0=tmpk[:].rearrange("p (a g) d -> p a g d", a=2),
                                                       in1=gk_rep[:, None, :, :].to_broadcast([128, 2, 2, 64]), op=ALU.mult),
                     reads=[tmpk, gk_rep], writes=[kb])
                for j in range(4):
                    b.op("pe", lambda e: e.transpose(out=ptk[0:64, j, :], in_=kb[:, j, :], identity=self.ident[:]),
                         reads=[kb, self.ident], writes=[ptk])
                if getattr(self, 'stop_at', 99) <= 4:
                    continue
                for du in range(2):
                    b.op("act", lambda e: e.copy(out=cb[:, :, du, :], in_=pm[1][:, 0:256].rearrange("p (a d) -> p a d", d=64)),
                         reads=[pm[1]], writes=[cb])
                for j in range(4):
                    b.op("pe", lambda e: e.transpose(out=ptk[:, 4 + j, :], in_=cb[:, j, :, :].rearrange("p a d -> p (a d)"), identity=self.ident[:]),
                         reads=[cb, self.ident], writes=[ptk])
                c0 = t * 128
                b.op("dve", lambda e: e.tensor_copy(out=self.ksE[0:64, :, c0:c0 + 128], in_=ptk[0:64, 0:2, :]), reads=[ptk], writes=[self.ksE])
                b.op("dve", lambda e: e.tensor_copy(out=self.kwT[0:64, :, c0:c0 + 128], in_=ptk[0:64, 2:4, :]), reads=[ptk], writes=[self.kwT])
                b.op("act", lambda e: e.copy(out=kcdup[0:64, :, 1 + c0:1 + c0 + 128], in_=ptk[0:64, 4:6, :]), reads=[ptk], writes=[kcdup])
                b.op("act", lambda e: e.copy(out=kcdup[64:128, :, c0:c0 + 128], in_=ptk[64:128, 4:6, :]), reads=[ptk], writes=[kcdup])
                b.op("dve", lambda e: e.tensor_copy(out=vcdup[0:64, :, 1 + c0:1 + c0 + 128], in_=ptk[0:64, 6:8, :]), reads=[ptk], writes=[vcdup])
                b.op("dve", lambda e: e.tensor_copy(out=vcdup[64:128, :, c0:c0 + 128], in_=ptk[64:128, 6:8, :]), reads=[ptk], writes=[vcdup])
                if getattr(self, 'stop_at', 99) <= 5:
                    continue
                b.op("act", lambda e: e.copy(out=self.vaug_s[:, t, :, 0:64], in_=pm[1][:, 384:512].rearrange("p (g d) -> p g d", d=64)),
                     reads=[pm[1]], writes=[self.vaug_s])
                b.op("act", lambda e: e.copy(out=self.vaug_w[:, t, :, 0:64], in_=pm[2][:, 128:256].rearrange("p (g d) -> p g d", d=64)),
                     reads=[pm[2]], writes=[self.vaug_w])
                b.op("act", lambda e: e.activation(out=self.gts[:, t, :], in_=pm[2][:, 256:280], func=AF.Sigmoid), reads=[pm[2]], writes=[self.gts])
            if "nsa_proj" in self.debug:
                d = self.dbg_out("ksE", [128, 2, S], BF16)
                b.dma("pool", d, self.ksE[:], reads=[self.ksE])
                d = self.dbg_out("kcdup", [128, 2, S + 1], BF16)
                b.dma("pool", d, kcdup[:], reads=[kcdup])
                d = self.dbg_out("vaug_w", [128, NT, 2, 65], BF16)
                b.dma("pool", d, self.vaug_w[:], reads=[self.vaug_w])
                d = self.dbg_out("gts", [128, NT, 24], F32)
                b.dma("pool", d, self.gts[:], reads=[self.gts])
            if not getattr(self, 'skip_compress', False):
                self.compress(kcdup, vcdup, gk_rep, [pm[0], pm[1]], pm[2], ptk)

    def compress(self, kcdup, vcdup, gk_rep, ph, po, ptc):
        b = self.b
        I = self.inp
        C2 = 2.0 * 0.7978845608028654
        w1 = b.sb("w1", [128, 16, 256], BF16)
        w2 = b.sb("w2", [128, 2, 64], BF16)
        w1st = [b.sb(f"w1st{i}", [128, 256], F32) for i in range(2)]
        peT = b.sb("peT", [128, 16], F32)
        peTb = b.sb("peTb", [128, 16], BF16)
        hTc = b.sb("hTc", [128, 2, 256], BF16)
        pbias = b.sb("pbias", [128, 2], F32)
        xh = b.sb("xh", [128, 255], F32)
        x2 = b.sb("x2", [128, 255], F32)
        sg = b.sb("sg", [128, 255], F32)
        ctmp = b.sb("ctmp", [128, 64], F32)
        csq = b.sb("csq", [128, 64], F32)
        cs1 = b.sb("cs1", [128, 1], F32)
        kcb = b.sb("kcb", [128, 64], BF16)
        b.op("pool", lambda e: e.memset(hTc[:], 0.0), writes=[hTc])
        for kv, (dup, pe_n, w1_n, w2_n) in enumerate([(kcdup, "cmp_pe_k", "cmp_w1_k", "cmp_w2_k"), (vcdup, "cmp_pe_v", "cmp_w1_v", "cmp_w2_v")]):
            self.load_weight(w1, I[w1_n][0], 256, kch=16, stage=w1st, eng="dve")
            self.load_weight(w2, I[w2_n][0], 64, kch=2, stage=w1st, eng="dve")
            for two in range(2):
                b.dma("sp", peT[two * 64:(two + 1) * 64, :], I[pe_n][0].rearrange("(pp two) d -> two d pp", two=2)[two],
                      writes=[peT], allow_slow_non_contiguous=True)
            b.op("dve", lambda e: e.tensor_copy(out=peTb[:], in_=peT[:]), reads=[peT], writes=[peTb])
            for ft in range(2):
                for pp in range(16):
                    b.op("pe", lambda e: e.matmul(po[:, ft:ft + 1], lhsT=w1[:, pp, ft * 128:(ft + 1) * 128], rhs=peTb[:, pp:pp + 1],
                                                  start=(pp == 0), stop=(pp == 15)), reads=[w1, peTb], writes=[po])
            b.op("dve", lambda e: e.tensor_copy(out=pbias[:], in_=po[:, 0:2]), reads=[po], writes=[pbias])
            for g in range(2):
                for ft in range(2):
                    p = ph[ft]
                    for pp in range(16):
                        b.op("pe", lambda e: e.matmul(p[:, 0:255], lhsT=w1[:, pp, ft * 128:(ft + 1) * 128],
                                                      rhs=dup[:, g, 1 + 2 * pp:1 + 2 * pp + 16 * 254 + 1:16],
                                                      start=(pp == 0), stop=(pp == 15)), reads=[w1, dup], writes=[p])
                    b.op("act", lambda e: e.activation(out=xh[:], in_=p[:, 0:255], func=AF.Identity, bias=pbias[:, ft:ft + 1]), reads=[p, pbias], writes=[xh])
                    b.op("dve", lambda e: e.tensor_tensor(out=x2[:], in0=xh[:], in1=xh[:], op=ALU.mult), reads=[xh], writes=[x2])
                    b.op("dve", lambda e: e.tensor_scalar(out=x2[:], in0=x2[:], scalar1=0.044715, scalar2=1.0, op0=ALU.mult, op1=ALU.add), reads=[x2], writes=[x2])
                    b.op("dve", lambda e: e.tensor_tensor(out=x2[:], in0=x2[:], in1=xh[:], op=ALU.mult), reads=[x2, xh], writes=[x2])
                    b.op("act", lambda e: e.activation(out=sg[:], in_=x2[:], func=AF.Sigmoid, scale=C2), reads=[x2], writes=[sg])
                    b.op("dve", lambda e: e.tensor_tensor(out=hTc[:, ft, 0:255], in0=xh[:], in1=sg[:], op=ALU.mult), reads=[xh, sg], writes=[hTc])
                for ct in range(2):
                    for ft in range(2):
                        b.op("pe", lambda e: e.matmul(po[:, 64:128], lhsT=hTc[:, ft, ct * 128:(ct + 1) * 128], rhs=w2[:, ft, :],
                                                      start=(ft == 0), stop=(ft == 1)), reads=[hTc, w2], writes=[po])
                    if kv == 0:
                        b.op("act", lambda e: e.activation(out=csq[:], in_=po[:, 64:128], func=AF.Square, accum_out=cs1[:]), reads=[po], writes=[csq, cs1])
                        b.op("act", lambda e: e.activation(out=cs1[:], in_=cs1[:], func=AF.Sqrt, scale=1.0 / 64, bias=RMS_EPS), reads=[cs1], writes=[cs1])
                        b.op("dve", lambda e: e.reciprocal(out=cs1[:], in_=cs1[:]), reads=[cs1], writes=[cs1])
                        b.op("dve", lambda e: e.tensor_scalar_mul(out=ctmp[:], in0=po[:, 64:128], scalar1=cs1[:]), reads=[po, cs1], writes=[ctmp])
                        b.op("dve", lambda e: e.tensor_tensor(out=kcb[:], in0=ctmp[:], in1=gk_rep[:, 0, :], op=ALU.mult), reads=[ctmp, gk_rep], writes=[kcb])
                        b.op("pe", lambda e: e.transpose(out=ptc[0:64, 0, :], in_=kcb[:], identity=self.ident[:]), reads=[kcb, self.ident], writes=[ptc])
                        b.op("dve", lambda e: e.tensor_copy(out=self.kcT[:, g, ct * 128:(ct + 1) * 128], in_=ptc[0:64, 0, :]), reads=[ptc], writes=[self.kcT])
                    else:
                        b.op("dve", lambda e: e.tensor_copy(out=self.vcA[:, g, ct, 0:64], in_=po[:, 64:128]), reads=[po], writes=[self.vcA])
        if "compress" in self.debug:
            d = self.dbg_out("kcT", [64, 2, 256], BF16)
            b.dma("pool", d, self.kcT[:], reads=[self.kcT])
            d = self.dbg_out("vcA", [128, 2, 2, 129], F32)
            b.dma("pool", d, self.vcA[:], reads=[self.vcA])

    def finish(self):
        b = self.b
        b.wait_all_on("pool")
        b.barrier()
        b.close()
        return self.nc


def _phase_attn(self):
    b = self.b
    I = self.inp
    with b.scope():
        tw = b.sb("tw", [128, 8, 640], F32)
        ts = b.sb("ts", [128, 8, 640], F32)
        b.dma("sp", tw[:], I["tw"], writes=[tw])
        b.dma("sp", ts[:], I["ts"], writes=[ts])
        candneg = b.sb("candneg", [128, 32, 64], F32)
        fz = b.sb("fz", [128, 32, 64], F32)
        b.dma("sp", candneg[:], I["candneg"], writes=[candneg])
        b.dma("sp", fz[:], I["fz"], writes=[fz])
        b31 = b.sb("b31", [128, 8], F32)
        b.dma("sp", b31[:], I["b31"], writes=[b31])
        kwp = b.sb("kwp", [128, 2, S], BF16)
        b.op("pool", lambda e: e.memset(kwp[64:128, :, :], 0.0), writes=[kwp])
        b.op("pool", lambda e: e.tensor_copy(out=kwp[0:64, :, :], in_=self.kwT[:]), reads=[self.kwT], writes=[kwp])
        kcp = b.sb("kcp", [128, 2, 256], BF16)
        b.op("pool", lambda e: e.memset(kcp[64:128, :, :], 0.0), writes=[kcp])
        b.op("pool", lambda e: e.tensor_copy(out=kcp[0:64, :, :], in_=self.kcT[:]), reads=[self.kcT], writes=[kcp])
        zer = b.sb("zer", [128, 512], BF16)
        b.op("pool", lambda e: e.memset(zer[:], 0.0), writes=[zer])
        qm = [b.sb(f"qm{i}", [128, 8, 512], BF16) for i in range(2)]
        bct = [b.sb(f"bct{i}", [128, 512], F32) for i in range(3)]
        scf = [b.sb(f"scf{i}", [128, 640], F32) for i in range(2)]
        pcT = [b.sb(f"pcT{i}", [128, 2, 512], F32) for i in range(2)]
        pT = [b.sb(f"pT{i}", [128, 640], BF16) for i in range(3)]
        oacc = b.sb("oacc", [128, 4, 512], F32)
        imp = b.sb("imp", [128, 4, 2, 64], F32)
        impm = b.sb("impm", [128, 64], F32)
        impm2 = b.sb("impm2", [128, 64], F32)
        m8a = b.sb("m8a", [128, 8], F32)
        m8b = b.sb("m8b", [128, 8], F32)
        msk = b.sb("msk", [128, 64], F32)
        mb = b.sb("mb", [128, 128], BF16)
        b.op("pool", lambda e: e.memset(mb[:], 0.0), writes=[mb])
        rs = b.sb("rs", [128, 4], F32)
        rg = b.sb("rg", [128, 4], F32)
        oab = b.sb("oab", [128, 512], BF16)
        oaT = [b.sb(f"oaT{i}", [128, 4, 128], BF16) for i in range(2)]
        pS = [b.ps(f"pS{i}", [128, 512], F32) for i in range(2)]
        pS2 = b.ps("pS2", [128, 512], F32)
        pO = [b.ps(f"pO{i}", [128, 512], F32) for i in range(3)]
        pTr = b.ps("pTr", [128, 8, 128], BF16)
        nrot = {"bct": 0, "scf": 0, "pT": 0, "pS": 0}

        def rot(name, lst):
            nrot[name] += 1
            return lst[nrot[name] % len(lst)]

        def finalize(po, ncol_off, h, qs, branch, first):
            qt = qs_base + qs
            o0 = ncol_off
            b.op("dve", lambda e: e.tensor_scalar_max(out=rs[:, 0:1], in0=po[:, o0 + 64:o0 + 65], scalar1=1e-30), reads=[po], writes=[rs])
            b.op("dve", lambda e: e.reciprocal(out=rs[:, 1:2], in_=rs[:, 0:1]), reads=[rs], writes=[rs])
            b.op("dve", lambda e: e.tensor_tensor(out=rg[:, 0:1], in0=rs[:, 1:2], in1=self.gts[:, qt, h * 3 + branch:h * 3 + branch + 1], op=ALU.mult),
                 reads=[rs, self.gts], writes=[rg])
            if first:
                b.op("dve", lambda e: e.tensor_scalar_mul(out=oacc[:, qs, h * 64:(h + 1) * 64], in0=po[:, o0:o0 + 64], scalar1=rg[:, 0:1]),
                     reads=[po, rg], writes=[oacc])
            else:
                b.op("dve", lambda e: e.scalar_tensor_tensor(out=oacc[:, qs, h * 64:(h + 1) * 64], in0=po[:, o0:o0 + 64], scalar=rg[:, 0:1],
                                                             in1=oacc[:, qs, h * 64:(h + 1) * 64], op0=ALU.mult, op1=ALU.add),
                     reads=[po, rg, oacc], writes=[oacc])

        nqg = getattr(self, "nqg_limit", 8)
        for qg in range(nqg):
            qs_base = 4 * qg
            q0 = 512 * qg
            Q = qm[qg % 2]
            b.dma("sp", Q[0:64, :, :], self.qT_d[:, :, q0:q0 + 512].rearrange("h d t -> d h t"), reads=[self.qT_d], writes=[Q])
            if qg < 2:
                b.op("pool", lambda e: e.memset(Q[64:128, :, :], 0.0), writes=[Q])
            for h in range(8):
                g = h // 4
                pc = pcT[h % 2]
                for ct in range(2):
                    p = rot("pS", pS)
                    b.op("pe", lambda e: e.matmul(p[:, :], lhsT=kcp[:, g, ct * 128:(ct + 1) * 128], rhs=Q[:, h, :], start=True, stop=True),
                         reads=[kcp, Q], writes=[p])
                    bt = rot("bct", bct)
                    b.dma("sp", bt[:], I["biasc"][h, ct, :, q0:q0 + 512], writes=[bt])
                    sc = rot("scf", scf)
                    b.op("dve", lambda e: e.tensor_tensor(out=sc[:, 0:512], in0=p[:, :], in1=bt[:], op=ALU.add), reads=[p, bt], writes=[sc])
                    b.op("act", lambda e: e.activation(out=pc[:, ct, :], in_=sc[:, 0:512], func=AF.Exp), reads=[sc], writes=[pc])
                po = pO[0]
                for qs in range(4):
                    for ct in range(2):
                        b.op("pe", lambda e: e.matmul(po[:, qs * 128:qs * 128 + 129] if False else po[:, 0:129], lhsT=pc[:, ct, qs * 128:(qs + 1) * 128],
                                                      rhs=self.vcA[:, g, ct, :], start=(ct == 0), stop=(ct == 1)), reads=[pc, self.vcA], writes=[po])
                    finalize(po, 0, h, qs, 0, True)
                    if h % 4 == 0:
                        b.op("dve", lambda e: e.tensor_scalar_mul(out=imp[:, qs, g, :], in0=po[:, 65:129], scalar1=rs[:, 1:2]), reads=[po, rs], writes=[imp])
                    else:
                        b.op("dve", lambda e: e.scalar_tensor_tensor(out=imp[:, qs, g, :], in0=po[:, 65:129], scalar=rs[:, 1:2], in1=imp[:, qs, g, :],
                                                                     op0=ALU.mult, op1=ALU.add), reads=[po, rs, imp], writes=[imp])
            if qg >= 2:
                for qs in range(4):
                    qt = qs_base + qs
                    for g in range(2):
                        b.op("dve", lambda e: e.tensor_tensor(out=impm[:], in0=imp[:, qs, g, :], in1=candneg[:, qt, :], op=ALU.add), reads=[imp, candneg], writes=[impm])
                        b.op("dve", lambda e: e.max(out=m8a[:], in_=impm[:]), reads=[impm], writes=[m8a])
                        b.op("dve", lambda e: e.match_replace(out=impm2[:], in_to_replace=m8a[:], in_values=impm[:], imm_value=-1e9), reads=[m8a, impm], writes=[impm2])
                        b.op("dve", lambda e: e.max(out=m8b[:], in_=impm2[:]), reads=[impm2], writes=[m8b])
                        b.op("dve", lambda e: e.tensor_scalar(out=msk[:], in0=impm[:], scalar1=m8b[:, 4:5], scalar2=None, op0=ALU.is_ge), reads=[impm, m8b], writes=[msk])
                        b.op("dve", lambda e: e.tensor_tensor(out=msk[:], in0=msk[:], in1=fz[:, qt, :], op=ALU.max), reads=[msk, fz], writes=[msk])
                        b.op("dve", lambda e: e.tensor_scalar(out=mb[:, 64:128], in0=msk[:], scalar1=-NEG, scalar2=NEG, op0=ALU.mult, op1=ALU.add), reads=[msk], writes=[mb])
                        b.op("pe", lambda e: e.transpose(out=pTr[:, 0, :], in_=mb[:], identity=self.ident[:]), reads=[mb, self.ident], writes=[pTr])
                        b.op("act", lambda e: e.copy(out=Q[64:128, 4 * g:4 * g + 4, qs * 128:(qs + 1) * 128],
                                                     in_=pTr[64:128, 0:1, :].to_broadcast([64, 4, 128])), reads=[pTr], writes=[Q])
            for h in range(8):
                g = h // 4
                po_s, po_w = pO[1], pO[2]
                for po in (po_s, po_w):
                    b.op("pe", lambda e: e.matmul(po[:, 0:260], lhsT=zer[:, 0:128], rhs=zer[:, 0:260], start=True, stop=True), reads=[zer], writes=[po])
                nkt = 4 * (qg + 1)
                for kt in range(nkt):
                    dlt = 4 * qg - kt
                    qstart = 0 if dlt >= 0 else -dlt * 128
                    N = 512 - qstart
                    p = rot("pS", pS)
                    b.op("pe", lambda e: e.matmul(p[:, 0:N], lhsT=self.ksE[:, g, kt * 128:(kt + 1) * 128], rhs=Q[:, h, qstart:512], start=True, stop=True),
                         reads=[self.ksE, Q], writes=[p])
                    pt_ = rot("pT", pT)
                    if dlt <= 1:
                        c0 = 128 if dlt == 1 else 0
                        sc = rot("scf", scf)
                        b.op("dve", lambda e: e.tensor_tensor(out=sc[:, 0:N], in0=p[:, 0:N], in1=ts[:, h, c0:c0 + N], op=ALU.add), reads=[p, ts], writes=[sc])
                        b.op("act", lambda e: e.activation(out=pt_[:, 0:N], in_=sc[:, 0:N], func=AF.Exp), reads=[sc], writes=[pt_])
                    else:
                        b.op("act", lambda e: e.activation(out=pt_[:, 0:N], in_=p[:, 0:N], func=AF.Exp, bias=b31[:, h:h + 1]), reads=[p, b31], writes=[pt_])
                    for qs in range(qstart // 128, 4):
                        o = qs * 128 - qstart
                        b.op("pe", lambda e: e.matmul(po_s[:, qs * 65:(qs + 1) * 65], lhsT=pt_[:, o:o + 128], rhs=self.vaug_s[:, kt, g, :],
                                                      start=False, stop=(kt == nkt - 1), skip_group_check=True), reads=[pt_, self.vaug_s], writes=[po_s])
                kts = [kt for kt in range(4 * qg - 4, 4 * qg + 4) if kt >= 0]
                for kt in kts:
                    qs_lo = max(0, kt - 4 * qg)
                    qs_hi = min(3, kt + 4 - 4 * qg)
                    N = (qs_hi - qs_lo + 1) * 128
                    c0 = 128 * (4 * qg + qs_lo - kt)
                    p = rot("pS", pS)
                    b.op("pe", lambda e: e.matmul(p[:, 0:N], lhsT=kwp[:, g, kt * 128:(kt + 1) * 128], rhs=Q[:, h, qs_lo * 128:(qs_hi + 1) * 128], start=True, stop=True),
                         reads=[kwp, Q], writes=[p])
                    sc = rot("scf", scf)
                    b.op("dve", lambda e: e.tensor_tensor(out=sc[:, 0:N], in0=p[:, 0:N], in1=tw[:, h, c0:c0 + N], op=ALU.add), reads=[p, tw], writes=[sc])
                    pt_ = rot("pT", pT)
                    b.op("act", lambda e: e.activation(out=pt_[:, 0:N], in_=sc[:, 0:N], func=AF.Exp), reads=[sc], writes=[pt_])
                    for qs in range(qs_lo, qs_hi + 1):
                        o = (qs - qs_lo) * 128
                        b.op("pe", lambda e: e.matmul(po_w[:, qs * 65:(qs + 1) * 65], lhsT=pt_[:, o:o + 128], rhs=self.vaug_w[:, kt, g, :],
                                                      start=False, stop=(kt == kts[-1]), skip_group_check=True), reads=[pt_, self.vaug_w], writes=[po_w])
                for qs in range(4):
                    finalize(po_s, qs * 65, h, qs, 1, False)
                    finalize(po_w, qs * 65, h, qs, 2, False)
            for qs in range(4):
                qt = qs_base + qs
                ot = oaT[qs % 2]
                b.op("act", lambda e: e.copy(out=oab[:], in_=oacc[:, qs, :]), reads=[oacc], writes=[oab])
                for c in range(4):
                    b.op("pe", lambda e: e.transpose(out=pTr[:, 4 + c, :], in_=oab[:, c * 128:(c + 1) * 128], identity=self.ident[:]), reads=[oab, self.ident], writes=[pTr])
                b.op("act", lambda e: e.copy(out=ot[:], in_=pTr[:, 4:8, :]), reads=[pTr], writes=[ot])
                b.dma("pool", self.oaT_d[:, :, qt * 128:(qt + 1) * 128].rearrange("c p t -> p c t"), ot[:], reads=[ot], writes=[self.oaT_d])
        if "attn" in self.debug:
            d = self.dbg_out("oaT", [4, 128, S], BF16)
            b.dma("pool", d, self.oaT_d[:], reads=[self.oaT_d])


Prog.phase_attn = _phase_attn


def _phase_merge(self):
    b = self.b
    I = self.inp
    self.x1_d = b.dram("x1_d", [S, D], F32)
    with b.scope():
        gat = self.load_gain("gat2", I["attn_norm_g"][0])
        stage = [b.sb(f"mst{i}", [128, 1024], F32) for i in range(2)]
        wg = b.sb("wg", [128, 8, 2048], BF16)
        for n in range(2):
            for c in range(8):
                st = stage[c % 2]
                b.dma("sp", st[:], I["w_in"][0][c * 128:(c + 1) * 128, GA0 + n * 1024:GA0 + (n + 1) * 1024], writes=[st])
                b.op("act", lambda e: e.activation(out=wg[:, c, n * 1024:(n + 1) * 1024], in_=st[:], func=AF.Copy, scale=gat[:, c:c + 1]),
                     reads=[st, gat], writes=[wg])
        wa = b.sb("wa", [128, 4, 1024], BF16)
        wb = b.sb("wb", [128, 4, 1024], BF16)
        wo = b.sb("wo", [128, 8, 1024], BF16)
        self.load_weight(wa, I["w_proj_a"][0], 1024, kch=4, stage=stage, eng="dve")
        self.load_weight(wb, I["w_proj_b"][0], 1024, kch=4, stage=stage, eng="dve")
        self.load_weight(wo, I["w_out"][0], 1024, kch=8, stage=stage, eng="dve")
        xt = [b.sb(f"mxt{i}", [128, D], F32) for i in range(2)]
        junk = b.sb("mjunk", [128, D], BF16)
        ss = [b.sb(f"mss{i}", [128, 1], F32) for i in range(2)]
        hb = [b.sb(f"mhb{i}", [128, D], BF16) for i in range(2)]
        hT = [b.sb(f"mhT{i}", [128, 8, 128], BF16) for i in range(2)]
        oat = [b.sb(f"oat{i}", [128, 4, 128], BF16) for i in range(2)]
        obt = [b.sb(f"obt{i}", [128, 4, 128], BF16) for i in range(2)]
        sg = b.sb("msg", [128, 2048], F32)
        m1 = b.sb("m1", [128, 1024], F32)
        m2 = b.sb("m2", [128, 1024], F32)
        mgb = b.sb("mgb", [128, 1024], BF16)
        mT = b.sb("mT", [128, 8, 128], BF16)
        x1t = [b.sb(f"x1t{i}", [128, D], F32) for i in range(2)]
        pt = b.ps("mpt", [128, 8, 128], BF16)
        pg = [b.ps(f"mpg{i}", [128, 512], F32) for i in range(2)]
        pa = [b.ps(f"mpa{i}", [128, 512], F32) for i in range(2)]
        pb = [b.ps(f"mpb{i}", [128, 512], F32) for i in range(2)]
        for t in range(getattr(self, "nt_limit", NT)):
            i = t % 2
            self.make_hT(I["x"], t, xt[i], junk, ss[i], hb[i], pt, hT[i], self.ident)
            b.dma("sp", oat[i][:], self.oaT_d[:, :, t * 128:(t + 1) * 128].rearrange("c p t -> p c t"), reads=[self.oaT_d], writes=[oat[i]])
            b.dma("sp", obt[i][:], self.obT_d[:, :, t * 128:(t + 1) * 128].rearrange("c p t -> p c t"), reads=[self.obT_d], writes=[obt[i]])
            for n in range(4):
                p = pg[n % 2]
                for c in range(8):
                    b.op("pe", lambda e: e.matmul(p[:, :], lhsT=hT[i][:, c, :], rhs=wg[:, c, n * 512:(n + 1) * 512], start=(c == 0), stop=(c == 7)),
                         reads=[hT[i], wg], writes=[p])
                b.op("act", lambda e: e.activation(out=sg[:, n * 512:(n + 1) * 512], in_=p[:, :], func=AF.Sigmoid), reads=[p], writes=[sg])
            for n in range(2):
                for c in range(4):
                    b.op("pe", lambda e: e.matmul(pa[n][:, :], lhsT=oat[i][:, c, :], rhs=wa[:, c, n * 512:(n + 1) * 512], start=(c == 0), stop=(c == 3)),
                         reads=[oat[i], wa], writes=[pa[n]])
                for c in range(4):
                    b.op("pe", lambda e: e.matmul(pb[n][:, :], lhsT=obt[i][:, c, :], rhs=wb[:, c, n * 512:(n + 1) * 512], start=(c == 0), stop=(c == 3)),
                         reads=[obt[i], wb], writes=[pb[n]])
                b.op("dve", lambda e: e.tensor_tensor(out=m1[:, n * 512:(n + 1) * 512], in0=pa[n][:, :], in1=sg[:, n * 512:(n + 1) * 512], op=ALU.mult),
                     reads=[pa[n], sg], writes=[m1])
                b.op("dve", lambda e: e.tensor_tensor(out=m2[:, n * 512:(n + 1) * 512], in0=pb[n][:, :], in1=sg[:, 1024 + n * 512:1024 + (n + 1) * 512], op=ALU.mult),
                     reads=[pb[n], sg], writes=[m2])
            b.op("pool", lambda e: e.tensor_tensor(out=mgb[:], in0=m1[:], in1=m2[:], op=ALU.add), reads=[m1, m2], writes=[mgb])
            for c in range(8):
                b.op("pe", lambda e: e.transpose(out=pt[:, c, :], in_=mgb[:, c * 128:(c + 1) * 128], identity=self.ident[:]), reads=[mgb, self.ident], writes=[pt])
            b.op("act", lambda e: e.copy(out=mT[:], in_=pt[:]), reads=[pt], writes=[mT])
            for n in range(2):
                for c in range(8):
                    b.op("pe", lambda e: e.matmul(pa[n][:, :], lhsT=mT[:, c, :], rhs=wo[:, c, n * 512:(n + 1) * 512], start=(c == 0), stop=(c == 7)),
                         reads=[mT, wo], writes=[pa[n]])
                b.op("dve", lambda e: e.tensor_tensor(out=x1t[i][:, n * 512:(n + 1) * 512], in0=pa[n][:, :], in1=xt[i][:, n * 512:(n + 1) * 512], op=ALU.add),
                     reads=[pa[n], xt[i]], writes=[x1t[i]])
            b.dma("pool", self.x1_d[t * 128:(t + 1) * 128, :], x1t[i][:], reads=[x1t[i]], writes=[self.x1_d])
        if "merge" in self.debug:
            d = self.dbg_out("x1", [S, D], F32)
            b.dma("pool", d, self.x1_d[:], reads=[self.x1_d])


def _phase_ffn(self):
    b = self.b
    I = self.inp
    TG = 128
    NFT = 44
    with b.scope():
        gf = self.load_gain("gf", I["ffn_norm_g"][0])
        stage = [b.sb(f"fst{i}", [128, 1024], F32) for i in range(2)]
        wu = b.sb("wu", [128, 8, 2 * DFF], BF16)
        for n in range(8):
            for c in range(8):
                st = stage[c % 2]
                b.dma("sp", st[:, 0:704], I["w_up"][0][c * 128:(c + 1) * 128, n * 704:(n + 1) * 704], writes=[st])
                b.op("act", lambda e: e.activation(out=wu[:, c, n * 704:(n + 1) * 704], in_=st[:, 0:704], func=AF.Copy, scale=gf[:, c:c + 1]),
                     reads=[st, gf], writes=[wu])
        wd = b.sb("wd", [128, 22, D], BF16)
        self.load_weight(wd, I["w_down"][0], D, kch=22, stage=stage, eng="dve")
        cw = b.sb("cw", [128, 3, NFT], F32)
        for j in range(3):
            b.dma("sp", cw[:, j, :], I["conv_w"][0][j].rearrange("(c p) -> p c", p=128), writes=[cw], allow_slow_non_contiguous=True)
        cbias = self.load_gain("cbias", I["conv_b"][0], kch=NFT)
        carry = b.sb("carry", [128, NFT, 2], F32)
        b.op("pool", lambda e: e.memset(carry[:], 0.0), writes=[carry])
        xt = [b.sb(f"fxt{i}", [128, D], F32) for i in range(2)]
        junk = b.sb("fjunk", [128, D], BF16)
        ss = [b.sb(f"fss{i}", [128, 1], F32) for i in range(2)]
        hb = [b.sb(f"fhb{i}", [128, D], BF16) for i in range(2)]
        hT1 = [b.sb(f"fhT{i}", [128, 8, 128], BF16) for i in range(2)]
        hTg = b.sb("fhTg", [128, 8, TG], BF16)
        ub = [b.sb(f"ub{i}", [128, TG + 2], F32) for i in range(2)]
        cv = [b.sb(f"cv{i}", [128, TG], F32) for i in range(2)]
        sgl = b.sb("sgl", [128, TG], F32)
        actT = b.sb("actT", [128, 22, TG], BF16)
        self._val = b.sb("fval", [128, 22, TG], BF16)
        ot = xt
        pt = b.ps("fpt", [128, 8, 128], BF16)
        pu = [b.ps(f"fpu{i}", [128, 512], F32) for i in range(3)]
        pd = [b.ps(f"fpd{i}", [128, 512], F32) for i in range(2)]
        ng = getattr(self, "nt_limit", NT) * 128 // TG
        for gi in range(ng):
            for s_ in range(TG // 128):
                t = gi * (TG // 128) + s_
                self.make_hT(self.x1_d, t, xt[s_], junk, ss[s_], hb[s_], pt, hT1[s_], self.ident)
                b.op("pool", lambda e: e.tensor_copy(out=hTg[:, :, s_ * 128:(s_ + 1) * 128], in_=hT1[s_][:]), reads=[hT1[s_]], writes=[hTg])
            for ft in range(NFT):
                p = pu[ft % 3]
                u = ub[ft % 2]
                c_ = cv[(ft // 22) % 2] if False else cv[ft % 2]
                for c in range(8):
                    b.op("pe", lambda e: e.matmul(p[:, 0:TG], lhsT=wu[:, c, ft * 128:(ft + 1) * 128], rhs=hTg[:, c, :], start=(c == 0), stop=(c == 7)),
                         reads=[wu, hTg], writes=[p])
                b.op("act", lambda e: e.copy(out=u[:, 2:TG + 2], in_=p[:, 0:TG]), reads=[p], writes=[u])
                b.op("pool", lambda e: e.tensor_copy(out=u[:, 0:2], in_=carry[:, ft, :]), reads=[carry], writes=[u])
                b.op("pool", lambda e: e.tensor_copy(out=carry[:, ft, :], in_=u[:, TG:TG + 2]), reads=[u], writes=[carry])
                b.op("dve", lambda e: e.tensor_scalar(out=c_[:], in0=u[:, 0:TG], scalar1=cw[:, 0, ft:ft + 1], scalar2=cbias[:, ft:ft + 1], op0=ALU.mult, op1=ALU.add),
                     reads=[u, cw, cbias], writes=[c_])
                b.op("dve", lambda e: e.scalar_tensor_tensor(out=c_[:], in0=u[:, 1:TG + 1], scalar=cw[:, 1, ft:ft + 1], in1=c_[:], op0=ALU.mult, op1=ALU.add),
                     reads=[u, cw, c_], writes=[c_])
                if ft < 22:
                    b.op("dve", lambda e: e.scalar_tensor_tensor(out=self._val[:, ft, :], in0=u[:, 2:TG + 2], scalar=cw[:, 2, ft:ft + 1], in1=c_[:], op0=ALU.mult, op1=ALU.add),
                         reads=[u, cw, c_], writes=[self._val])
                else:
                    b.op("dve", lambda e: e.scalar_tensor_tensor(out=c_[:], in0=u[:, 2:TG + 2], scalar=cw[:, 2, ft:ft + 1], in1=c_[:], op0=ALU.mult, op1=ALU.add),
                         reads=[u, cw, c_], writes=[c_])
                    b.op("act", lambda e: e.activation(out=sgl[:], in_=c_[:], func=AF.Silu), reads=[c_], writes=[sgl])
                    b.op("dve", lambda e: e.tensor_tensor(out=actT[:, ft - 22, :], in0=sgl[:], in1=self._val[:, ft - 22, :], op=ALU.mult),
                         reads=[sgl, self._val], writes=[actT])
            for s_ in range(TG // 128):
                t = gi * (TG // 128) + s_
                for n in range(2):
                    for f in range(22):
                        b.op("pe", lambda e: e.matmul(pd[n][:, :], lhsT=actT[:, f, s_ * 128:(s_ + 1) * 128], rhs=wd[:, f, n * 512:(n + 1) * 512], start=(f == 0), stop=(f == 21)),
                             reads=[actT, wd], writes=[pd[n]])
                    b.op("dve", lambda e: e.tensor_tensor(out=ot[s_][:, n * 512:(n + 1) * 512], in0=pd[n][:, :], in1=xt[s_][:, n * 512:(n + 1) * 512], op=ALU.add),
                         reads=[pd[n], xt[s_]], writes=[ot[s_]])
                b.dma("pool", self.out[t * 128:(t + 1) * 128, :], ot[s_][:], reads=[ot[s_]])


Prog.phase_merge = _phase_merge
Prog.phase_ffn = _phase_ffn


def _phase_rwkv(self):
    b = self.b
    I = self.inp
    TG = 256
    NCH = TG // 64
    tt = lambda eng, out, in0, in1, op, rd, wr: b.op(eng, lambda e: e.tensor_tensor(out=out, in0=in0, in1=in1, op=op), reads=rd, writes=wr)
    with b.scope():
        gat = self.load_gain("gat3", I["attn_norm_g"][0])
        stage = [b.sb(f"rst{i}", [128, 1792], F32) for i in range(2)]
        wr = b.sb("wr", [128, 8, 1792], BF16)
        self.load_weight(wr, I["w_in"][0][:, RW0:RW0 + 1792], 1792, gvec=gat, stage=stage)

        def colvec(name, src, n):
            t = b.sb(name, [64, n], F32)
            b.dma("sp", t[:], src.rearrange("(c p) -> p c", p=64), writes=[t], allow_slow_non_contiguous=True)
            return t
        mu = colvec("mu", I["rwkv_mu"][0], 28)
        w0 = colvec("w0", I["rwkv_w0"][0], 8)
        a0 = colvec("a0", I["rwkv_a0"][0], 8)
        k_k = colvec("k_k", I["rwkv_k_k"][0], 8)
        k_a = colvec("k_a", I["rwkv_k_a"][0], 8)
        r_k = colvec("r_k", I["rwkv_r_k"][0].rearrange("h d -> (h d)"), 8)
        w2s = b.sb("w2s", [64, 512], F32)
        a2s = b.sb("a2s", [64, 512], F32)
        g2s = b.sb("g2s", [64, 2, 512], F32)
        b.dma("sp", w2s[:], I["rwkv_w2"][0], writes=[w2s])
        b.dma("sp", a2s[:], I["rwkv_a2"][0], writes=[a2s])
        b.dma("sp", g2s[:], I["rwkv_g2"][0].rearrange("(two l) f -> l two f", two=2), writes=[g2s])
        lng = b.sb("lng", [64, 512], F32)
        lnb = b.sb("lnb", [64, 512], F32)
        b.dma("sp", lng[:], I["rwkv_ln_g"][0].partition_broadcast(64), writes=[lng])
        b.dma("sp", lnb[:], I["rwkv_ln_b"][0].partition_broadcast(64), writes=[lnb])
        msk = b.sb("rmsk", [64, 3, 64], F32)
        b.dma("sp", msk[:], I["rwmask"], writes=[msk])
        rstm = b.sb("rstm", [64, TG], F32)
        b.dma("sp", rstm[:], I["rwreset"], writes=[rstm])
        ones = b.sb("ones64", [64, 64], F32)
        b.op("pool", lambda e: e.memset(ones[:], 1.0), writes=[ones])
        idf = self.identf
        carry = b.sb("rcarry", [64, 28], F32)
        b.op("pool", lambda e: e.memset(carry[:], 0.0), writes=[carry])
        Hs = [[b.sb(f"H{h}_{i}", [64, 64], F32) for i in range(2)] for h in range(8)]
        for h in range(8):
            b.op("pool", lambda e: e.memset(Hs[h][0][:], 0.0), writes=[Hs[h][0]])
        xt = [b.sb(f"rxt{i}", [128, D], F32) for i in range(2)]
        junk = b.sb("rjunk", [128, D], BF16)
        ss = [b.sb(f"rss{i}", [128, 1], F32) for i in range(2)]
        hb = [b.sb(f"rhb{i}", [128, D], BF16) for i in range(2)]
        hT1 = [b.sb(f"rhT{i}", [128, 8, 128], BF16) for i in range(2)]
        hTg = b.sb("rhTg", [128, 8, TG], BF16)
        pbuf = [b.sb(f"rpb{i}", [64, TG + 1], F32) for i in range(2)]
        dtmp = b.sb("rdtmp", [64, TG], F32)
        X = [b.sb(f"rX{w}", [64, 8, TG], F32) for w in range(3)]
        xs = b.sb("rxs", [64, 4, TG], F32)
        BV = b.sb("rBV", [64, 8, TG], F32)
        Ytm = b.sb("rYtm", [64, NCH, 8, 64], F32)
        sqv = b.sb("rsqv", [64, NCH, 8, 64], F32)
        st1 = b.sb("rst1", [64, NCH * 8], F32)
        st2 = b.sb("rst2", [64, NCH * 8], F32)
        T = {n: b.sb("r" + n, [64, TG], F32) for n in ["lw", "as", "kk", "sq", "kkn", "bv", "kp", "t1", "L", "Lx", "Ep", "Em", "Ex", "BT", "KT", "BG", "KG", "rk"]}
        AR = b.sb("rAR", [64, NCH, 2, 64], F32)
        TM = [b.sb(f"rTM{i}", [64, 3, 64], F32) for i in range(2)]
        XM = [b.sb(f"rXM{i}", [64, 4, 64], F32) for i in range(2)]
        AA = [b.sb(f"rAA{i}", [64, 2, 64], F32) for i in range(3)]
        PP = [b.sb(f"rPP{i}", [64, 64], F32) for i in range(3)]
        Xs = b.sb("rXs", [64, 64], F32)
        Us = b.sb("rUs", [64, 64], F32)
        obf = [b.sb(f"robf{i}", [64, TG], BF16) for i in range(2)]
        otmp = b.sb("rotmp", [64, TG], F32)
        pt = b.ps("rpt", [128, 8, 128], BF16)
        pp = [b.ps(f"rpp{i}", [128, 512], F32) for i in range(2)]
        pq = [b.ps(f"rpq{i}", [128, 512], F32) for i in range(2)]
        pd = [b.ps(f"rpd{i}", [128, 512], F32) for i in range(2)]
        pz = b.ps("rpz", [128, 512], F32)
        cnt = {"pp": 0, "pq": 0, "pd": 0, "aa": 0, "ppb": 0, "tm": 0, "xm": 0, "pb": 0}

        def nxt(k, lst):
            cnt[k] += 1
            return lst[cnt[k] % len(lst)]

        ngr = getattr(self, "nrg_limit", S // TG)
        for gi in range(ngr):
            q0 = gi * TG
            for s_ in range(TG // 128):
                t = gi * (TG // 128) + s_
                self.make_hT(I["x"], t, xt[s_], junk, ss[s_], hb[s_], pt, hT1[s_], self.ident)
                b.op("pool", lambda e: e.tensor_copy(out=hTg[:, :, s_ * 128:(s_ + 1) * 128], in_=hT1[s_][:]), reads=[hT1[s_]], writes=[hTg])

            def proj_lerp(fc, out_ap, out_buf, post=None):
                p = nxt("pp", pp)
                for c in range(8):
                    b.op("pe", lambda e: e.matmul(p[0:64, 0:TG], lhsT=wr[:, c, fc * 64:(fc + 1) * 64], rhs=hTg[:, c, :], start=(c == 0), stop=(c == 7)),
                         reads=[wr, hTg], writes=[p])
                pb_ = nxt("pb", pbuf)
                b.op("act", lambda e: e.copy(out=pb_[:, 1:TG + 1], in_=p[0:64, 0:TG]), reads=[p], writes=[pb_])
                b.op("pool", lambda e: e.tensor_copy(out=pb_[:, 0:1], in_=carry[:, fc:fc + 1]), reads=[carry], writes=[pb_])
                b.op("pool", lambda e: e.tensor_copy(out=carry[:, fc:fc + 1], in_=pb_[:, TG:TG + 1]), reads=[pb_], writes=[carry])
                tt("dve", dtmp[:], pb_[:, 0:TG], pb_[:, 1:TG + 1], ALU.subtract, [pb_], [dtmp])
                b.op("dve", lambda e: e.scalar_tensor_tensor(out=out_ap, in0=dtmp[:], scalar=mu[:, fc:fc + 1], in1=pb_[:, 1:TG + 1], op0=ALU.mult, op1=ALU.add),
                     reads=[dtmp, mu, pb_], writes=[out_buf])

            for w in range(3):
                for h in range(8):
                    proj_lerp(w * 8 + h, X[w][:, h, :], X[w])
            for j in range(4):
                proj_lerp(24 + j, xs[:, j, :], xs)
            b.op("act", lambda e: e.activation(out=xs[:, 0, :], in_=xs[:, 0, :], func=AF.Tanh), reads=[xs], writes=[xs])
            b.op("act", lambda e: e.activation(out=xs[:, 2:4, :], in_=xs[:, 2:4, :], func=AF.Sigmoid), reads=[xs], writes=[xs])

            for h in range(8):
                hs = slice(h * 64, (h + 1) * 64)
                R_, K_, V_ = X[0][:, h, :], X[1][:, h, :], X[2][:, h, :]
                p = nxt("pp", pp)
                b.op("pe", lambda e: e.matmul(p[0:64, 0:TG], lhsT=w2s[:, hs], rhs=xs[:, 0, :], start=True, stop=True), reads=[w2s, xs], writes=[p])
                b.op("act", lambda e: e.activation(out=T["lw"][:], in_=p[0:64, 0:TG], func=AF.Sigmoid, bias=w0[:, h:h + 1]), reads=[p, w0], writes=[T["lw"]])
                b.op("pool", lambda e: e.tensor_scalar_mul(out=T["lw"][:], in0=T["lw"][:], scalar1=-0.6065306597126334), reads=[T["lw"]], writes=[T["lw"]])
                p = nxt("pp", pp)
                b.op("pe", lambda e: e.matmul(p[0:64, 0:TG], lhsT=a2s[:, hs], rhs=xs[:, 1, :], start=True, stop=True), reads=[a2s, xs], writes=[p])
                b.op("act", lambda e: e.activation(out=T["as"][:], in_=p[0:64, 0:TG], func=AF.Sigmoid, bias=a0[:, h:h + 1]), reads=[p, a0], writes=[T["as"]])
                b.op("dve", lambda e: e.tensor_scalar_mul(out=T["kk"][:], in0=K_, scalar1=k_k[:, h:h + 1]), reads=[X[1], k_k], writes=[T["kk"]])
                tt("pool", T["sq"][:], T["kk"][:], T["kk"][:], ALU.mult, [T["kk"]], [T["sq"]])
                p = nxt("pp", pp)
                b.op("pe", lambda e: e.matmul(p[0:64, 0:TG], lhsT=ones[:], rhs=T["sq"][:], start=True, stop=True), reads=[ones, T["sq"]], writes=[p])
                b.op("act", lambda e: e.activation(out=T["sq"][:], in_=p[0:64, 0:TG], func=AF.Sqrt), reads=[p], writes=[T["sq"]])
                b.op("dve", lambda e: e.tensor_scalar_max(out=T["sq"][:], in0=T["sq"][:], scalar1=1e-12), reads=[T["sq"]], writes=[T["sq"]])
                b.op("dve", lambda e: e.reciprocal(out=T["sq"][:], in_=T["sq"][:]), reads=[T["sq"]], writes=[T["sq"]])
                tt("dve", T["kkn"][:], T["kk"][:], T["sq"][:], ALU.mult, [T["kk"], T["sq"]], [T["kkn"]])
                tt("pool", T["bv"][:], T["kkn"][:], T["as"][:], ALU.mult, [T["kkn"], T["as"]], [T["bv"]])
                b.op("dve", lambda e: e.tensor_scalar(out=T["t1"][:], in0=T["as"][:], scalar1=-1.0, scalar2=k_a[:, h:h + 1], op0=ALU.add, op1=ALU.mult),
                     reads=[T["as"], k_a], writes=[T["t1"]])
                b.op("dve", lambda e: e.scalar_tensor_tensor(out=T["kp"][:], in0=T["t1"][:], scalar=1.0, in1=K_, op0=ALU.add, op1=ALU.mult),
                     reads=[T["t1"], X[1]], writes=[T["kp"]])
                tt("pool", T["rk"][:], R_, T["kp"][:], ALU.mult, [X[0], T["kp"]], [T["rk"]])
                b.op("pool", lambda e: e.tensor_scalar_mul(out=T["rk"][:], in0=T["rk"][:], scalar1=r_k[:, h:h + 1]), reads=[T["rk"], r_k], writes=[T["rk"]])
                p = nxt("pp", pp)
                b.op("pe", lambda e: e.matmul(p[0:64, 0:TG], lhsT=ones[:], rhs=T["rk"][:], start=True, stop=True), reads=[ones, T["rk"]], writes=[p])
                tt("dve", BV[:, h, :], p[0:64, 0:TG], V_, ALU.mult, [p, X[2]], [BV])
                b.op("dve", lambda e: e.tensor_tensor_scan(out=T["L"][:], data0=rstm[:], data1=T["lw"][:], initial=0.0, op0=ALU.mult, op1=ALU.add),
                     reads=[rstm, T["lw"]], writes=[T["L"]])
                tt("pool", T["Lx"][:], T["L"][:], T["lw"][:], ALU.subtract, [T["L"], T["lw"]], [T["Lx"]])
                b.op("act", lambda e: e.activation(out=T["Ep"][:], in_=T["L"][:], func=AF.Exp), reads=[T["L"]], writes=[T["Ep"]])
                b.op("act", lambda e: e.activation(out=T["Em"][:], in_=T["L"][:], func=AF.Exp, scale=-1.0), reads=[T["L"]], writes=[T["Em"]])
                b.op("act", lambda e: e.activation(out=T["Ex"][:], in_=T["Lx"][:], func=AF.Exp), reads=[T["Lx"]], writes=[T["Ex"]])
                c3 = lambda ap: ap.rearrange("p (c t) -> p c t", t=64)
                b.op("dve", lambda e: e.scalar_tensor_tensor(out=AR[:, :, 0, :], in0=c3(T["kkn"][:]), scalar=-1.0, in1=c3(T["Ex"][:]), op0=ALU.mult, op1=ALU.mult),
                     reads=[T["kkn"], T["Ex"]], writes=[AR])
                tt("pool", AR[:, :, 1, :], c3(R_), c3(T["Ep"][:]), ALU.mult, [X[0], T["Ep"]], [AR])
                tt("dve", T["BT"][:], T["bv"][:], T["Em"][:], ALU.mult, [T["bv"], T["Em"]], [T["BT"]])
                tt("pool", T["KT"][:], T["kp"][:], T["Em"][:], ALU.mult, [T["kp"], T["Em"]], [T["KT"]])
                gC = c3(T["Ep"][:])[:, :, 63:64].to_broadcast([64, NCH, 64])
                tt("dve", c3(T["BG"][:]), c3(T["BT"][:]), gC, ALU.mult, [T["BT"], T["Ep"]], [T["BG"]])
                tt("pool", c3(T["KG"][:]), c3(T["KT"][:]), gC, ALU.mult, [T["KT"], T["Ep"]], [T["KG"]])
                for c in range(NCH):
                    cs = slice(c * 64, (c + 1) * 64)
                    Hc = Hs[h][(gi * NCH + c) % 2]
                    Hn = Hs[h][(gi * NCH + c + 1) % 2]
                    p = nxt("pq", pq)
                    for j, (src, sb_) in enumerate([(V_[:, cs], X[2]), (T["BG"][:, cs], T["BG"]), (T["KG"][:, cs], T["KG"])]):
                        b.op("pe", lambda e: e.transpose(out=p[0:64, j * 64:(j + 1) * 64], in_=src, identity=idf[0:64, 0:64]), reads=[sb_, idf], writes=[p])
                    tm = nxt("tm", TM)
                    b.op("act", lambda e: e.copy(out=tm[:].rearrange("p a b -> p (a b)"), in_=p[0:64, 0:192]), reads=[p], writes=[tm])
                    p = nxt("pq", pq)
                    arc = AR[:, c, :, :].rearrange("p a t -> p (a t)")
                    b.op("pe", lambda e: e.matmul(p[0:64, 0:128], lhsT=T["BT"][:, cs], rhs=arc, start=True, stop=True), reads=[T["BT"], AR], writes=[p])
                    b.op("pe", lambda e: e.matmul(p[0:64, 128:256], lhsT=T["KT"][:, cs], rhs=arc, start=True, stop=True), reads=[T["KT"], AR], writes=[p])
                    b.op("pe", lambda e: e.matmul(p[0:64, 256:320], lhsT=AR[:, c, 0, :], rhs=T["BT"][:, cs], start=True, stop=True), reads=[T["BT"], AR], writes=[p])
                    xm = nxt("xm", XM)
                    tt("dve", xm[:].rearrange("p (a m) t -> p a m t", a=2), p[0:64, 0:256].rearrange("p (a m t) -> p a m t", a=2, m=2),
                       msk[:, None, 0:2, :].to_broadcast([64, 2, 2, 64]), ALU.mult, [p, msk], [xm])
                    aa = nxt("aa", AA)
                    b.op("pool", lambda e: e.tensor_copy(out=aa[:, 0, :], in_=xm[:, 0, :]), reads=[xm], writes=[aa])
                    tt("dve", aa[:, 1, :], p[0:64, 256:320], msk[:, 2, :], ALU.mult, [p, msk], [aa])
                    P_ = nxt("ppb", PP)
                    tt("pool", P_[:], xm[:, 0, :], idf[0:64, 0:64], ALU.add, [xm, idf], [P_])
                    for step in range(5):
                        pdb = nxt("pd", pd)
                        b.op("pe", lambda e: e.matmul(pdb[0:64, 0:64], lhsT=aa[:, 1, :], rhs=aa[:, 0, :], start=True, stop=True), reads=[aa], writes=[pdb])
                        b.op("pe", lambda e: e.matmul(pdb[0:64, 64:128], lhsT=aa[:, 0, :], rhs=aa[:, 1, :], start=True, stop=True), reads=[aa], writes=[pdb])
                        aa2 = nxt("aa", AA)
                        b.op("act", lambda e: e.copy(out=aa2[:].rearrange("p a t -> p (a t)"), in_=pdb[0:64, 0:128]), reads=[pdb], writes=[aa2])
                        b.op("pe", lambda e: e.matmul(pdb[0:64, 128:192], lhsT=aa2[:, 1, :], rhs=P_[:], start=True, stop=True), reads=[aa2, P_], writes=[pdb])
                        P2 = nxt("ppb", PP)
                        tt("dve", P2[:], pdb[0:64, 128:192], P_[:], ALU.add, [pdb, P_], [P2])
                        aa, P_ = aa2, P2
                    b.op("pe", lambda e: e.matmul(pz[0:64, 0:64], lhsT=xm[:, 2, :], rhs=tm[:, 0, :], start=True, stop=False), reads=[xm, tm], writes=[pz])
                    b.op("pe", lambda e: e.matmul(pz[0:64, 0:64], lhsT=AR[:, c, 0, :], rhs=Hc[:], start=False, stop=True), reads=[AR, Hc], writes=[pz])
                    b.op("act", lambda e: e.copy(out=Xs[:], in_=pz[0:64, 0:64]), reads=[pz], writes=[Xs])
                    b.op("pe", lambda e: e.matmul(pz[0:64, 64:128], lhsT=P_[:], rhs=Xs[:], start=True, stop=True), reads=[P_, Xs], writes=[pz])
                    b.op("act", lambda e: e.copy(out=Us[:], in_=pz[0:64, 64:128]), reads=[pz], writes=[Us])
                    b.op("pe", lambda e: e.matmul(pz[0:64, 128:192], lhsT=AR[:, c, 1, :], rhs=Hc[:], start=True, stop=False), reads=[AR, Hc], writes=[pz])
                    b.op("pe", lambda e: e.matmul(pz[0:64, 128:192], lhsT=xm[:, 1, :], rhs=Us[:], start=False, stop=False), reads=[xm, Us], writes=[pz])
                    b.op("pe", lambda e: e.matmul(pz[0:64, 128:192], lhsT=xm[:, 3, :], rhs=tm[:, 0, :], start=False, stop=True), reads=[xm, tm], writes=[pz])
                    b.op("pe", lambda e: e.matmul(pz[0:64, 192:256], lhsT=tm[:, 1, :], rhs=Us[:], start=True, stop=False), reads=[tm, Us], writes=[pz])
                    b.op("pe", lambda e: e.matmul(pz[0:64, 192:256], lhsT=tm[:, 2, :], rhs=tm[:, 0, :], start=False, stop=True), reads=[tm], writes=[pz])
                    b.op("act", lambda e: e.copy(out=Ytm[:, c, h, :], in_=pz[0:64, 128:192]), reads=[pz], writes=[Ytm])
                    b.op("dve", lambda e: e.scalar_tensor_tensor(out=Hn[:], in0=Hc[:], scalar=T["Ep"][:, c * 64 + 63:c * 64 + 64], in1=pz[0:64, 192:256],
                                                                 op0=ALU.mult, op1=ALU.add), reads=[Hc, T["Ep"], pz], writes=[Hn])
            Y3 = Ytm[:].rearrange("p c h i -> p (c h) i")
            S3 = sqv[:].rearrange("p c h i -> p (c h) i")
            b.op("dve", lambda e: e.tensor_reduce(out=st1[:], in_=Y3, axis=AX.X, op=ALU.add), reads=[Ytm], writes=[st1])
            b.op("pool", lambda e: e.tensor_scalar_mul(out=st1[:], in0=st1[:], scalar1=1.0 / 64), reads=[st1], writes=[st1])
            tt("dve", Y3, Y3, st1[:].unsqueeze(2).to_broadcast([64, NCH * 8, 64]), ALU.subtract, [Ytm, st1], [Ytm])
            tt("pool", S3, Y3, Y3, ALU.mult, [Ytm], [sqv])
            b.op("dve", lambda e: e.tensor_reduce(out=st2[:], in_=S3, axis=AX.X, op=ALU.add), reads=[sqv], writes=[st2])
            b.op("act", lambda e: e.activation(out=st2[:], in_=st2[:], func=AF.Sqrt, scale=1.0 / 64, bias=64e-5), reads=[st2], writes=[st2])
            b.op("dve", lambda e: e.reciprocal(out=st2[:], in_=st2[:]), reads=[st2], writes=[st2])
            tt("dve", Y3, Y3, st2[:].unsqueeze(2).to_broadcast([64, NCH * 8, 64]), ALU.mult, [Ytm, st2], [Ytm])
            lg = lng[:].rearrange("p (h i) -> p h i", i=64)[:, None, :, :].to_broadcast([64, NCH, 8, 64])
            lb = lnb[:].rearrange("p (h i) -> p h i", i=64)[:, None, :, :].to_broadcast([64, NCH, 8, 64])
            tt("pool", Ytm[:], Ytm[:], lg, ALU.mult, [Ytm, lng], [Ytm])
            tt("dve", Ytm[:], Ytm[:], lb, ALU.add, [Ytm, lnb], [Ytm])
            for h in range(8):
                p = nxt("pq", pq)
                for c in range(NCH):
                    b.op("pe", lambda e: e.transpose(out=p[0:64, c * 64:(c + 1) * 64], in_=Ytm[:, c, h, :], identity=idf[0:64, 0:64]), reads=[Ytm, idf], writes=[p])
                tt("dve", otmp[:], p[0:64, 0:TG], BV[:, h, :], ALU.add, [p, BV], [otmp])
                pg_ = nxt("pp", pp)
                b.op("pe", lambda e: e.matmul(pg_[0:64, 0:TG], lhsT=g2s[:, 0, h * 64:(h + 1) * 64], rhs=xs[:, 2, :], start=True, stop=False), reads=[g2s, xs], writes=[pg_])
                b.op("pe", lambda e: e.matmul(pg_[0:64, 0:TG], lhsT=g2s[:, 1, h * 64:(h + 1) * 64], rhs=xs[:, 3, :], start=False, stop=True), reads=[g2s, xs], writes=[pg_])
                ob_ = obf[h % 2]
                tt("dve", ob_[:], otmp[:], pg_[0:64, 0:TG], ALU.mult, [otmp, pg_], [ob_])
                b.dma("pool", self.obT_d[h // 2, (h % 2) * 64:(h % 2) * 64 + 64, q0:q0 + TG], ob_[:], reads=[ob_], writes=[self.obT_d])
        if "rwkv" in self.debug:
            d = self.dbg_out("obT", [4, 128, S], BF16)
            b.dma("pool", d, self.obT_d[:], reads=[self.obT_d])


Prog.phase_rwkv = _phase_rwkv


def build_full():
    p = Prog()
    b = p.b
    p.alloc_root()
    with b.scope():
        p.alloc_persistent()
        p.phase_nsa_proj()
        p.phase_attn()
    p.phase_rwkv()
    p.phase_merge()
    p.phase_ffn()
    p.finish()
    return p


def kernel(**inputs):
    p = build_full()
    consts = host_consts(inputs["rel_bias"])
    shared = {k: np.ascontiguousarray(np.asarray(inputs[k], np.float32)) for k in W_SPECS if k != "x"}
    shared.update(consts)
    x = np.asarray(inputs["x"], np.float32)
    in_maps = []
    for c in range(8):
        m = dict(shared)
        m["x"] = np.ascontiguousarray(x[c])
        in_maps.append(m)
    res = run_bass_kernel_spmd(p.nc, in_maps, core_ids=list(range(8)))
    return np.stack([np.asarray(r["out"], np.float32) for r in res.results], axis=0)
```

```python
import contextlib
import numpy as np
import ml_dtypes
import concourse.bass as bass
import concourse.mybir as mybir
from concourse.bass_utils import run_bass_kernel_spmd

F32 = mybir.dt.float32
BF16 = mybir.dt.bfloat16
AF = mybir.ActivationFunctionType
ALU = mybir.AluOpType
AX = mybir.AxisListType

S = 4096
D = 1024
NT = S // 128
IN_WIDTH = 5144
RW0 = 1304
GA0 = 3096
GB0 = 4120
DFF = 2816
RMS_EPS = 1e-6


class Buf:
    def __init__(self, t, name):
        self.t = t
        self.name = name
        self.w = None
        self.r = {}
        self.psum = False

    def __getitem__(self, idx):
        return self.t[idx]


class Builder:
    SEM_ROLL = 30000

    def __init__(self, nc):
        self.nc = nc
        self.stack = contextlib.ExitStack()
        self.root = self.stack
        self.eng = {"pe": nc.tensor, "act": nc.scalar, "dve": nc.vector,
                    "pool": nc.gpsimd, "sp": nc.sync}
        self.sem = {}
        self.cnt = {}
        self.seen = {e: {} for e in self.eng}
        self.nsem = 0
        self.lanes = {}
        self.lane_rr = {}
        self.last_tok = {}
        for e in self.eng:
            self._roll(e)

    def newsem(self, name):
        self.nsem += 1
        return self.root.enter_context(self.nc.semaphore(f"{name}_{self.nsem}"))

    def sb(self, name, shape, dt=F32):
        self.nsem += 1
        name = f"sb{self.nsem}_{name}"
        return Buf(self.stack.enter_context(self.nc.sbuf_tensor(name, list(shape), dt)), name)

    def ps(self, name, shape, dt=F32):
        self.nsem += 1
        name = f"ps{self.nsem}_{name}"
        bf = Buf(self.stack.enter_context(self.nc.psum_tensor(name, list(shape), dt)), name)
        bf.psum = True
        return bf

    def dram(self, name, shape, dt=F32, kind="Internal"):
        return Buf(self.nc.dram_tensor(name, list(shape), dt, kind=kind), name)

    def _roll(self, e):
        self.sem[e] = self.newsem("s" + e)
        self.cnt[e] = 0

    def _wait(self, e, tok):
        sem, val = tok
        k = id(sem)
        if self.seen[e].get(k, 0) < val:
            self.eng[e].wait_ge(sem, val)
            self.seen[e][k] = val

    def _deps(self, e, reads, writes):
        for b in reads:
            if b.w is not None:
                we, tok = b.w
                self._wait(e, tok)
            if b.psum:
                for re_, tok in b.r.items():
                    if re_ != e:
                        self._wait(e, tok)
        for b in writes:
            if b.w is not None:
                we, tok = b.w
                if we != e:
                    self._wait(e, tok)
            for re_, tok in b.r.items():
                if re_ != e:
                    self._wait(e, tok)

    def op(self, e, fn, reads=(), writes=()):
        if self.cnt[e] >= self.SEM_ROLL:
            self._roll(e)
        self._deps(e, reads, writes)
        ins = fn(self.eng[e])
        self.cnt[e] += 1
        tok = (self.sem[e], self.cnt[e])
        ins.then_inc(self.sem[e], 1)
        self.last_tok[e] = tok
        for b in reads:
            b.r[e] = tok
        for b in writes:
            b.w = (e, tok)
            b.r = {}
        return tok

    def dma(self, q, out, in_, reads=(), writes=(), nlanes=6, **kw):
        if q not in self.lanes:
            self.lanes[q] = [[self.newsem("l" + q), 0] for _ in range(nlanes)]
            self.lane_rr[q] = 0
        li = self.lane_rr[q]
        self.lane_rr[q] = (li + 1) % len(self.lanes[q])
        lane = self.lanes[q][li]
        if lane[1] >= 1800:
            self._wait(q, (lane[0], 16 * lane[1]))
            lane[0] = self.newsem("l" + q)
            lane[1] = 0
        if lane[1] > 0:
            self._wait(q, (lane[0], 16 * lane[1]))
        self._deps_dma(q, reads, writes)
        ins = self.eng[q].dma_start(out=out, in_=in_, **kw)
        lane[1] += 1
        tok = (lane[0], 16 * lane[1])
        ins.then_inc(lane[0], 16)
        key = "dma_" + q + str(li)
        for b in reads:
            b.r[key] = tok
        for b in writes:
            b.w = (key, tok)
            b.r = {}
        return tok

    def _deps_dma(self, q, reads, writes):
        for b in reads:
            if b.w is not None:
                self._wait(q, b.w[1])
        for b in writes:
            if b.w is not None:
                self._wait(q, b.w[1])
            for re_, tok in b.r.items():
                self._wait(q, tok)

    def barrier(self):
        toks = list(self.last_tok.values())
        for q, lanes in self.lanes.items():
            for lane in lanes:
                if lane[1] > 0:
                    toks.append((lane[0], 16 * lane[1]))
        for e in self.eng:
            for tok in toks:
                self._wait(e, tok)

    def wait_all_on(self, e):
        toks = list(self.last_tok.values())
        for q, lanes in self.lanes.items():
            for lane in lanes:
                if lane[1] > 0:
                    toks.append((lane[0], 16 * lane[1]))
        for tok in toks:
            self._wait(e, tok)

    @contextlib.contextmanager
    def scope(self):
        old = self.stack
        self.stack = contextlib.ExitStack()
        try:
            yield
            self.barrier()
        finally:
            self.stack.close()
            self.stack = old

    def close(self):
        self.stack.close()


NEG = -30000.0


def _bucket(dist):
    n = np.maximum(dist, 0)
    ratio = np.log(np.maximum(n, 1).astype(np.float32) / np.float32(16.0)) / np.float32(np.log(8.0))
    large = np.minimum(16 + (ratio * 16).astype(np.int32), 31)
    return np.where(n < 16, n, large)


def host_consts(rel_bias):
    rel = np.asarray(rel_bias, np.float32)
    c = {}
    c["ident"] = np.eye(128, dtype=np.float32).astype(ml_dtypes.bfloat16)
    c["identf"] = np.eye(128, dtype=np.float32)
    kp = np.arange(128)[:, None]
    cc = np.arange(640)[None, :]
    dist = cc - kp
    bt = rel[_bucket(dist)]
    tw = np.where(((dist >= 0) & (dist < 512))[..., None], bt, np.float32(NEG))
    ts = np.where((dist >= 0)[..., None], bt, np.float32(NEG))
    c["tw"] = np.ascontiguousarray(tw.transpose(0, 2, 1)).astype(np.float32)
    c["ts"] = np.ascontiguousarray(ts.transpose(0, 2, 1)).astype(np.float32)
    cidx = np.arange(256)[:, None]
    qidx = np.arange(S)[None, :]
    dc = qidx - 16 * cidx - 31
    bcg = rel[_bucket(dc)]
    ok = (dc >= 0) & (cidx < 255)
    bc = np.where(ok[..., None], bcg, np.float32(NEG))
    c["biasc"] = np.ascontiguousarray(bc.transpose(2, 0, 1)).reshape(8, 2, 128, S).astype(np.float32)
    A = np.zeros((256, 64), np.float32)
    Wt = (1, 2, 2, 2, 1)
    for ci in range(255):
        for j in range(64):
            o = ci + 1 - 4 * j
            if 0 <= o <= 4:
                A[ci, j] = Wt[o]
    c["amat"] = A.reshape(2, 128, 64)
    E = np.zeros((64, S), np.float32)
    E[np.arange(S) // 64, np.arange(S)] = 1.0
    c["emat"] = E.astype(ml_dtypes.bfloat16)
    qp = np.arange(128)[:, None, None]
    qt = np.arange(32)[None, :, None]
    j = np.arange(64)[None, None, :]
    cur = (128 * qt + qp) // 64
    cand = (j >= 1) & (j <= cur - 2)
    c["candneg"] = np.where(cand, 0.0, -1e9).astype(np.float32)
    c["fz"] = ((j == 0) | (j == cur) | (j == cur - 1)).astype(np.float32)
    tri = np.triu(np.ones((64, 64), np.float32))
    c["rwmask"] = np.ascontiguousarray(np.stack([np.triu(np.ones((64, 64), np.float32), 1), tri, np.tril(np.ones((64, 64), np.float32), -1)], axis=1))
    rr = np.ones((64, 1024), np.float32)
    rr[:, ::64] = 0.0
    c["rwreset"] = rr
    c["b31"] = np.ascontiguousarray(np.broadcast_to(rel[31][None, :], (128, 8))).astype(np.float32)
    return c


CONST_SPECS = {
    "ident": ([128, 128], BF16), "identf": ([128, 128], F32),
    "tw": ([128, 8, 640], F32), "ts": ([128, 8, 640], F32),
    "biasc": ([8, 2, 128, S], F32), "amat": ([2, 128, 64], F32),
    "emat": ([64, S], BF16), "candneg": ([128, 32, 64], F32), "fz": ([128, 32, 64], F32),
    "b31": ([128, 8], F32), "rwmask": ([64, 3, 64], F32), "rwreset": ([64, 1024], F32),
}

W_SPECS = {
    "x": [S, D], "attn_norm_g": [1, D], "w_in": [1, D, IN_WIDTH], "q_norm_g": [1, 64], "k_norm_g": [1, 64],
    "cmp_pe_k": [1, 32, 64], "cmp_w1_k": [1, 2048, 256], "cmp_w2_k": [1, 256, 64],
    "cmp_pe_v": [1, 32, 64], "cmp_w1_v": [1, 2048, 256], "cmp_w2_v": [1, 256, 64],
    "rwkv_mu": [1, 1792], "rwkv_w0": [1, 512], "rwkv_w2": [1, 64, 512], "rwkv_a0": [1, 512],
    "rwkv_a2": [1, 64, 512], "rwkv_g2": [1, 128, 512], "rwkv_k_k": [1, 512], "rwkv_k_a": [1, 512],
    "rwkv_r_k": [1, 8, 64], "rwkv_ln_g": [1, 512], "rwkv_ln_b": [1, 512],
    "w_proj_a": [1, 512, D], "w_proj_b": [1, 512, D], "w_out": [1, D, D], "ffn_norm_g": [1, D],
    "w_up": [1, D, 2 * DFF], "conv_w": [1, 3, 2 * DFF], "conv_b": [1, 2 * DFF], "w_down": [1, DFF, D],
}


class Prog:
    def __init__(self, debug=()):
        self.debug = set(debug)
        nc = bass.Bass("TRN2", target_bir_lowering=False)
        self.nc = nc
        self.inp = {}
        for k, shp in W_SPECS.items():
            self.inp[k] = nc.dram_tensor(k, list(shp), F32, kind="ExternalInput").ap()
        for k, (shp, dt) in CONST_SPECS.items():
            self.inp[k] = nc.dram_tensor(k, list(shp), dt, kind="ExternalInput").ap()
        self.out = nc.dram_tensor("out", [S, D], F32, kind="ExternalOutput").ap()
        self.dbg = {}
        self.b = Builder(nc)

    def dbg_out(self, name, shape, dt=F32):
        t = self.nc.dram_tensor("dbg_" + name, list(shape), dt, kind="ExternalOutput").ap()
        self.dbg[name] = t
        return t

    def load_weight(self, dst, src, ncols, gvec=None, kch=8, stage=None, eng="act"):
        b = self.b
        for c in range(kch):
            st = stage[c % len(stage)]
            b.dma("sp", st[:, :ncols], src[c * 128:(c + 1) * 128, :], writes=[st])
            if gvec is not None:
                b.op(eng, lambda e: e.activation(out=dst[:, c, :], in_=st[:, :ncols], func=AF.Copy, scale=gvec[:, c:c + 1])
                     if eng == "act" else e.tensor_scalar_mul(out=dst[:, c, :], in0=st[:, :ncols], scalar1=gvec[:, c:c + 1]),
                     reads=[st, gvec], writes=[dst])
            else:
                b.op(eng, lambda e: e.copy(out=dst[:, c, :], in_=st[:, :ncols]) if eng == "act"
                     else e.tensor_copy(out=dst[:, c, :], in_=st[:, :ncols]), reads=[st], writes=[dst])

    def load_gain(self, name, src_vec, kch=8):
        b = self.b
        g = b.sb(name, [128, kch], F32)
        b.dma("sp", g[:], src_vec.rearrange("(c p) -> p c", p=128), writes=[g], allow_slow_non_contiguous=True)
        return g

    def bcast_row(self, name, src_row, n):
        b = self.b
        t = b.sb(name, [128, n], F32)
        b.dma("sp", t[:], src_row.partition_broadcast(128), writes=[t])
        return t

    def make_hT(self, x_ap, t, xt, junk, ss, hb, pt, hT, ident):
        b = self.b
        b.dma("sp", xt[:], x_ap[t * 128:(t + 1) * 128, :], writes=[xt])
        b.op("act", lambda e: e.activation(out=junk[:], in_=xt[:], func=AF.Square, accum_out=ss[:]), reads=[xt], writes=[junk, ss])
        b.op("act", lambda e: e.activation(out=ss[:], in_=ss[:], func=AF.Sqrt, scale=1.0 / D, bias=RMS_EPS), reads=[ss], writes=[ss])
        b.op("dve", lambda e: e.reciprocal(out=ss[:], in_=ss[:]), reads=[ss], writes=[ss])
        b.op("dve", lambda e: e.tensor_scalar_mul(out=hb[:], in0=xt[:], scalar1=ss[:]), reads=[xt, ss], writes=[hb])
        for c in range(8):
            b.op("pe", lambda e: e.transpose(out=pt[:, c, :], in_=hb[:, c * 128:(c + 1) * 128], identity=ident[:]),
                 reads=[hb, ident], writes=[pt])
        b.op("act", lambda e: e.copy(out=hT[:], in_=pt[:]), reads=[pt], writes=[hT])

    def alloc_root(self):
        b = self.b
        I = self.inp
        self.ident = b.sb("ident", [128, 128], BF16)
        b.dma("sp", self.ident[:], I["ident"], writes=[self.ident])
        self.identf = b.sb("identf", [128, 128], F32)
        b.dma("sp", self.identf[:], I["identf"], writes=[self.identf])

    def alloc_persistent(self):
        b = self.b
        I = self.inp
        if not hasattr(self, "ident"):
            self.alloc_root()
        self.ksE = b.sb("ksE", [128, 2, S], BF16)
        self.kwT = b.sb("kwT", [64, 2, S], BF16)
        self.vaug_s = b.sb("vaug_s", [128, NT, 2, 65], BF16)
        self.vaug_w = b.sb("vaug_w", [128, NT, 2, 65], BF16)
        self.gts = b.sb("gts", [128, NT, 24], F32)
        self.kcT = b.sb("kcT", [64, 2, 256], BF16)
        self.vcA = b.sb("vcA", [128, 2, 2, 129], F32)
        self.qT_d = b.dram("qT_d", [8, 64, S], BF16)
        self.oaT_d = b.dram("oaT_d", [4, 128, S], BF16)
        self.obT_d = b.dram("obT_d", [4, 128, S], BF16)
        for g in range(2):
            b.dma("sp", self.ksE[64:128, g, :], I["emat"], writes=[self.ksE])
        b.op("pool", lambda e: e.memset(self.vaug_s[:, :, :, 64:65], 1.0), writes=[self.vaug_s])
        b.op("pool", lambda e: e.memset(self.vaug_w[:, :, :, 64:65], 1.0), writes=[self.vaug_w])
        b.op("pool", lambda e: e.memset(self.vcA[:, :, :, 64:65], 1.0), writes=[self.vcA])
        for g in range(2):
            for ct in range(2):
                b.dma("sp", self.vcA[:, g, ct, 65:129], I["amat"][ct], writes=[self.vcA])

    def phase_nsa_proj(self):
        b = self.b
        I = self.inp
        with b.scope():
            gat = self.load_gain("gat", I["attn_norm_g"][0])
            wn = b.sb("wn", [128, 8, RW0], BF16)
            stage = [b.sb(f"wst{i}", [128, RW0], F32) for i in range(2)]
            self.load_weight(wn, I["w_in"][0][:, 0:RW0], RW0, gvec=gat, stage=stage)
            gq = self.bcast_row("gq", I["q_norm_g"][0], 64)
            gk = self.bcast_row("gk", I["k_norm_g"][0], 64)
            gq_rep = b.sb("gq_rep", [128, 8, 64], F32)
            gk_rep = b.sb("gk_rep", [128, 2, 64], F32)
            b.op("act", lambda e: e.activation(out=gq_rep[:], in_=gq[:, None, :].to_broadcast([128, 8, 64]), func=AF.Copy, scale=0.125),
                 reads=[gq], writes=[gq_rep])
            b.op("act", lambda e: e.activation(out=gk_rep[:], in_=gk[:, None, :].to_broadcast([128, 2, 64]), func=AF.Copy, scale=1.0),
                 reads=[gk], writes=[gk_rep])
            if getattr(self, 'stop_at', 99) <= 0:
                return
            kcdup = b.sb("kcdup", [128, 2, S + 1], BF16)
            vcdup = b.sb("vcdup", [128, 2, S + 1], BF16)
            xt = [b.sb(f"xt{i}", [128, D], F32) for i in range(2)]
            junk = b.sb("junk", [128, D], BF16)
            ss = [b.sb(f"ss{i}", [128, 1], F32) for i in range(2)]
            hb = [b.sb(f"hb{i}", [128, D], BF16) for i in range(2)]
            hT = [b.sb(f"hT{i}", [128, 8, 128], BF16) for i in range(2)]
            sq = b.sb("sq", [128, 12, 64], F32)
            ssq = b.sb("ssq", [128, 12], F32)
            tmpq = b.sb("tmpq", [128, 8, 64], F32)
            tmpk = b.sb("tmpk", [128, 4, 64], F32)
            qb = b.sb("qb", [128, 512], BF16)
            kb = b.sb("kb", [128, 4, 64], BF16)
            cb = b.sb("cb", [128, 4, 2, 64], BF16)
            qst = [b.sb(f"qst{i}", [64, 8, 128], BF16) for i in range(2)]
            pt = b.ps("pt", [128, 8, 128], BF16)
            pm = [b.ps(f"pm{i}", [128, 512], F32) for i in range(3)]
            ptq = b.ps("ptq", [128, 8, 128], BF16)
            ptk = b.ps("ptk", [128, 8, 128], BF16)
            colgroups = [(0, 512), (512, 1024), (1024, RW0)]
            for t in range(getattr(self, 'nt_limit', NT)):
                i = t % 2
                self.make_hT(I["x"], t, xt[i], junk, ss[i], hb[i], pt, hT[i], self.ident)
                for n, (c0, c1) in enumerate(colgroups):
                    for c in range(8):
                        b.op("pe", lambda e: e.matmul(pm[n][:, :c1 - c0], lhsT=hT[i][:, c, :], rhs=wn[:, c, c0:c1],
                                                      start=(c == 0), stop=(c == 7)), reads=[hT[i], wn], writes=[pm[n]])
                if getattr(self, 'stop_at', 99) <= 1:
                    continue
                b.op("act", lambda e: e.activation(out=sq[:, 0:8, :], in_=pm[0][:, 0:512].rearrange("p (h d) -> p h d", d=64), func=AF.Square),
                     reads=[pm[0]], writes=[sq])
                b.op("act", lambda e: e.activation(out=sq[:, 8:10, :], in_=pm[1][:, 256:384].rearrange("p (h d) -> p h d", d=64), func=AF.Square),
                     reads=[pm[1]], writes=[sq])
                b.op("act", lambda e: e.activation(out=sq[:, 10:12, :], in_=pm[2][:, 0:128].rearrange("p (h d) -> p h d", d=64), func=AF.Square),
                     reads=[pm[2]], writes=[sq])
                b.op("dve", lambda e: e.tensor_reduce(out=ssq[:], in_=sq[:], axis=AX.X, op=ALU.add), reads=[sq], writes=[ssq])
                b.op("act", lambda e: e.activation(out=ssq[:], in_=ssq[:], func=AF.Sqrt, scale=1.0 / 64, bias=RMS_EPS), reads=[ssq], writes=[ssq])
                b.op("dve", lambda e: e.reciprocal(out=ssq[:], in_=ssq[:]), reads=[ssq], writes=[ssq])
                if getattr(self, 'stop_at', 99) <= 2:
                    continue
                b.op("dve", lambda e: e.tensor_tensor(out=tmpq[:], in0=pm[0][:, 0:512].rearrange("p (h d) -> p h d", d=64),
                                                      in1=ssq[:, 0:8].unsqueeze(2).to_broadcast([128, 8, 64]), op=ALU.mult),
                     reads=[pm[0], ssq], writes=[tmpq])
                b.op("pool", lambda e: e.tensor_tensor(out=qb[:].rearrange("p (h d) -> p h d", d=64), in0=tmpq[:], in1=gq_rep[:], op=ALU.mult),
                     reads=[tmpq, gq_rep], writes=[qb])
                for h in range(8):
                    b.op("pe", lambda e: e.transpose(out=ptq[0:64, h, :], in_=qb[:, h * 64:(h + 1) * 64], identity=self.ident[:]),
                         reads=[qb, self.ident], writes=[ptq])
                b.op("act", lambda e: e.copy(out=qst[i][:], in_=ptq[0:64, :, :]), reads=[ptq], writes=[qst[i]])
                b.dma("pool", self.qT_d[:, :, t * 128:(t + 1) * 128].rearrange("h d t -> d h t"), qst[i][:], reads=[qst[i]], writes=[self.qT_d])
                if getattr(self, 'stop_at', 99) <= 3:
                    continue
                b.op("dve", lambda e: e.tensor_tensor(out=tmpk[:, 0:2, :], in0=pm[1][:, 256:384].rearrange("p (h d) -> p h d", d=64),
                                                      in1=ssq[:, 8:10].unsqueeze(2).to_broadcast([128, 2, 64]), op=ALU.mult),
                     reads=[pm[1], ssq], writes=[tmpk])
                b.op("dve", lambda e: e.tensor_tensor(out=tmpk[:, 2:4, :], in0=pm[2][:, 0:128].rearrange("p (h d) -> p h d", d=64),
                                                      in1=ssq[:, 10:12].unsqueeze(2).to_broadcast([128, 2, 64]), op=ALU.mult),
                     reads=[pm[2], ssq], writes=[tmpk])
                b.op("pool", lambda e: e.tensor_tensor(out=kb[:].rearrange("p (a g) d -> p a g d", a=2), in0=tmpk[:].rearrange("p (a g) d -> p a g d", a=2),
                                                       in1=gk_rep[:, None, :, :].to_broadcast([128, 2, 2, 64]), op=ALU.mult),
                     reads=[tmpk, gk_rep], writes=[kb])
                for j in range(4):
                    b.op("pe", lambda e: e.transpose(out=ptk[0:64, j, :], in_=kb[:, j, :], identity=self.ident[:]),
                         reads=[kb, self.ident], writes=[ptk])
                if getattr(self, 'stop_at', 99) <= 4:
                    continue
                for du in range(2):
                    b.op("act", lambda e: e.copy(out=cb[:, :, du, :], in_=pm[1][:, 0:256].rearrange("p (a d) -> p a d", d=64)),
                         reads=[pm[1]], writes=[cb])
                for j in range(4):
                    b.op("pe", lambda e: e.transpose(out=ptk[:, 4 + j, :], in_=cb[:, j, :, :].rearrange("p a d -> p (a d)"), identity=self.ident[:]),
                         reads=[cb, self.ident], writes=[ptk])
                c0 = t * 128
                b.op("dve", lambda e: e.tensor_copy(out=self.ksE[0:64, :, c0:c0 + 128], in_=ptk[0:64, 0:2, :]), reads=[ptk], writes=[self.ksE])
                b.op("dve", lambda e: e.tensor_copy(out=self.kwT[0:64, :, c0:c0 + 128], in_=ptk[0:64, 2:4, :]), reads=[ptk], writes=[self.kwT])
                b.op("act", lambda e: e.copy(out=kcdup[0:64, :, 1 + c0:1 + c0 + 128], in_=ptk[0:64, 4:6, :]), reads=[ptk], writes=[kcdup])
                b.op("act", lambda e: e.copy(out=kcdup[64:128, :, c0:c0 + 128], in_=ptk[64:128, 4:6, :]), reads=[ptk], writes=[kcdup])
                b.op("dve", lambda e: e.tensor_copy(out=vcdup[0:64, :, 1 + c0:1 + c0 + 128], in_=ptk[0:64, 6:8, :]), reads=[ptk], writes=[vcdup])
                b.op("dve", lambda e: e.tensor_copy(out=vcdup[64:128, :, c0:c0 + 128], in_=ptk[64:128, 6:8, :]), reads=[ptk], writes=[vcdup])
                if getattr(self, 'stop_at', 99) <= 5:
                    continue
                b.op("act", lambda e: e.copy(out=self.vaug_s[:, t, :, 0:64], in_=pm[1][:, 384:512].rearrange("p (g d) -> p g d", d=64)),
                     reads=[pm[1]], writes=[self.vaug_s])
                b.op("act", lambda e: e.copy(out=self.vaug_w[:, t, :, 0:64], in_=pm[2][:, 128:256].rearrange("p (g d) -> p g d", d=64)),
                     reads=[pm[2]], writes=[self.vaug_w])
                b.op("act", lambda e: e.activation(out=self.gts[:, t, :], in_=pm[2][:, 256:280], func=AF.Sigmoid), reads=[pm[2]], writes=[self.gts])
            if "nsa_proj" in self.debug:
                d = self.dbg_out("ksE", [128, 2, S], BF16)
                b.dma("pool", d, self.ksE[:], reads=[self.ksE])
                d = self.dbg_out("kcdup", [128, 2, S + 1], BF16)
                b.dma("pool", d, kcdup[:], reads=[kcdup])
                d = self.dbg_out("vaug_w", [128, NT, 2, 65], BF16)
                b.dma("pool", d, self.vaug_w[:], reads=[self.vaug_w])
                d = self.dbg_out("gts", [128, NT, 24], F32)
                b.dma("pool", d, self.gts[:], reads=[self.gts])
            if not getattr(self, 'skip_compress', False):
                self.compress(kcdup, vcdup, gk_rep, [pm[0], pm[1]], pm[2], ptk)

    def compress(self, kcdup, vcdup, gk_rep, ph, po, ptc):
        b = self.b
        I = self.inp
        C2 = 2.0 * 0.7978845608028654
        w1 = b.sb("w1", [128, 16, 256], BF16)
        w2 = b.sb("w2", [128, 2, 64], BF16)
        w1st = [b.sb(f"w1st{i}", [128, 256], F32) for i in range(2)]
        peT = b.sb("peT", [128, 16], F32)
        peTb = b.sb("peTb", [128, 16], BF16)
        hTc = b.sb("hTc", [128, 2, 256], BF16)
        pbias = b.sb("pbias", [128, 2], F32)
        xh = b.sb("xh", [128, 255], F32)
        x2 = b.sb("x2", [128, 255], F32)
        sg = b.sb("sg", [128, 255], F32)
        ctmp = b.sb("ctmp", [128, 64], F32)
        csq = b.sb("csq", [128, 64], F32)
        cs1 = b.sb("cs1", [128, 1], F32)
        kcb = b.sb("kcb", [128, 64], BF16)
        b.op("pool", lambda e: e.memset(hTc[:], 0.0), writes=[hTc])
        for kv, (dup, pe_n, w1_n, w2_n) in enumerate([(kcdup, "cmp_pe_k", "cmp_w1_k", "cmp_w2_k"), (vcdup, "cmp_pe_v", "cmp_w1_v", "cmp_w2_v")]):
            self.load_weight(w1, I[w1_n][0], 256, kch=16, stage=w1st, eng="dve")
            self.load_weight(w2, I[w2_n][0], 64, kch=2, stage=w1st, eng="dve")
            for two in range(2):
                b.dma("sp", peT[two * 64:(two + 1) * 64, :], I[pe_n][0].rearrange("(pp two) d -> two d pp", two=2)[two],
                      writes=[peT], allow_slow_non_contiguous=True)
            b.op("dve", lambda e: e.tensor_copy(out=peTb[:], in_=peT[:]), reads=[peT], writes=[peTb])
            for ft in range(2):
                for pp in range(16):
                    b.op("pe", lambda e: e.matmul(po[:, ft:ft + 1], lhsT=w1[:, pp, ft * 128:(ft + 1) * 128], rhs=peTb[:, pp:pp + 1],
                                                  start=(pp == 0), stop=(pp == 15)), reads=[w1, peTb], writes=[po])
            b.op("dve", lambda e: e.tensor_copy(out=pbias[:], in_=po[:, 0:2]), reads=[po], writes=[pbias])
            for g in range(2):
                for ft in range(2):
                    p = ph[ft]
                    for pp in range(16):
                        b.op("pe", lambda e: e.matmul(p[:, 0:255], lhsT=w1[:, pp, ft * 128:(ft + 1) * 128],
                                                      rhs=dup[:, g, 1 + 2 * pp:1 + 2 * pp + 16 * 254 + 1:16],
                                                      start=(pp == 0), stop=(pp == 15)), reads=[w1, dup], writes=[p])
                    b.op("act", lambda e: e.activation(out=xh[:], in_=p[:, 0:255], func=AF.Identity, bias=pbias[:, ft:ft + 1]), reads=[p, pbias], writes=[xh])
                    b.op("dve", lambda e: e.tensor_tensor(out=x2[:], in0=xh[:], in1=xh[:], op=ALU.mult), reads=[xh], writes=[x2])
                    b.op("dve", lambda e: e.tensor_scalar(out=x2[:], in0=x2[:], scalar1=0.044715, scalar2=1.0, op0=ALU.mult, op1=ALU.add), reads=[x2], writes=[x2])
                    b.op("dve", lambda e: e.tensor_tensor(out=x2[:], in0=x2[:], in1=xh[:], op=ALU.mult), reads=[x2, xh], writes=[x2])
                    b.op("act", lambda e: e.activation(out=sg[:], in_=x2[:], func=AF.Sigmoid, scale=C2), reads=[x2], writes=[sg])
                    b.op("dve", lambda e: e.tensor_tensor(out=hTc[:, ft, 0:255], in0=xh[:], in1=sg[:], op=ALU.mult), reads=[xh, sg], writes=[hTc])
                for ct in range(2):
                    for ft in range(2):
                        b.op("pe", lambda e: e.matmul(po[:, 64:128], lhsT=hTc[:, ft, ct * 128:(ct + 1) * 128], rhs=w2[:, ft, :],
                                                      start=(ft == 0), stop=(ft == 1)), reads=[hTc, w2], writes=[po])
                    if kv == 0:
                        b.op("act", lambda e: e.activation(out=csq[:], in_=po[:, 64:128], func=AF.Square, accum_out=cs1[:]), reads=[po], writes=[csq, cs1])
                        b.op("act", lambda e: e.activation(out=cs1[:], in_=cs1[:], func=AF.Sqrt, scale=1.0 / 64, bias=RMS_EPS), reads=[cs1], writes=[cs1])
                        b.op("dve", lambda e: e.reciprocal(out=cs1[:], in_=cs1[:]), reads=[cs1], writes=[cs1])
                        b.op("dve", lambda e: e.tensor_scalar_mul(out=ctmp[:], in0=po[:, 64:128], scalar1=cs1[:]), reads=[po, cs1], writes=[ctmp])
                        b.op("dve", lambda e: e.tensor_tensor(out=kcb[:], in0=ctmp[:], in1=gk_rep[:, 0, :], op=ALU.mult), reads=[ctmp, gk_rep], writes=[kcb])
                        b.op("pe", lambda e: e.transpose(out=ptc[0:64, 0, :], in_=kcb[:], identity=self.ident[:]), reads=[kcb, self.ident], writes=[ptc])
                        b.op("dve", lambda e: e.tensor_copy(out=self.kcT[:, g, ct * 128:(ct + 1) * 128], in_=ptc[0:64, 0, :]), reads=[ptc], writes=[self.kcT])
                    else:
                        b.op("dve", lambda e: e.tensor_copy(out=self.vcA[:, g, ct, 0:64], in_=po[:, 64:128]), reads=[po], writes=[self.vcA])
        if "compress" in self.debug:
            d = self.dbg_out("kcT", [64, 2, 256], BF16)
            b.dma("pool", d, self.kcT[:], reads=[self.kcT])
            d = self.dbg_out("vcA", [128, 2, 2, 129], F32)
            b.dma("pool", d, self.vcA[:], reads=[self.vcA])

    def finish(self):
        b = self.b
        b.wait_all_on("pool")
        b.barrier()
        b.close()
        return self.nc


def _phase_attn(self):
    b = self.b
    I = self.inp
    with b.scope():
        tw = b.sb("tw", [128, 8, 640], F32)
        ts = b.sb("ts", [128, 8, 640], F32)
        b.dma("sp", tw[:], I["tw"], writes=[tw])
        b.dma("sp", ts[:], I["ts"], writes=[ts])
        candneg = b.sb("candneg", [128, 32, 64], F32)
        fz = b.sb("fz", [128, 32, 64], F32)
        b.dma("sp", candneg[:], I["candneg"], writes=[candneg])
        b.dma("sp", fz[:], I["fz"], writes=[fz])
        b31 = b.sb("b31", [128, 8], F32)
        b.dma("sp", b31[:], I["b31"], writes=[b31])
        kwp = b.sb("kwp", [128, 2, S], BF16)
        b.op("pool", lambda e: e.memset(kwp[64:128, :, :], 0.0), writes=[kwp])
        b.op("pool", lambda e: e.tensor_copy(out=kwp[0:64, :, :], in_=self.kwT[:]), reads=[self.kwT], writes=[kwp])
        kcp = b.sb("kcp", [128, 2, 256], BF16)
        b.op("pool", lambda e: e.memset(kcp[64:128, :, :], 0.0), writes=[kcp])
        b.op("pool", lambda e: e.tensor_copy(out=kcp[0:64, :, :], in_=self.kcT[:]), reads=[self.kcT], writes=[kcp])
        zer = b.sb("zer", [128, 512], BF16)
        b.op("pool", lambda e: e.memset(zer[:], 0.0), writes=[zer])
        qm = [b.sb(f"qm{i}", [128, 8, 512], BF16) for i in range(2)]
        bct = [b.sb(f"bct{i}", [128, 512], F32) for i in range(3)]
        scf = [b.sb(f"scf{i}", [128, 640], F32) for i in range(2)]
        pcT = [b.sb(f"pcT{i}", [128, 2, 512], F32) for i in range(2)]
        pT = [b.sb(f"pT{i}", [128, 640], BF16) for i in range(3)]
        oacc = b.sb("oacc", [128, 4, 512], F32)
        imp = b.sb("imp", [128, 4, 2, 64], F32)
        impm = b.sb("impm", [128, 64], F32)
        impm2 = b.sb("impm2", [128, 64], F32)
        m8a = b.sb("m8a", [128, 8], F32)
        m8b = b.sb("m8b", [128, 8], F32)
        msk = b.sb("msk", [128, 64], F32)
        mb = b.sb("mb", [128, 128], BF16)
        b.op("pool", lambda e: e.memset(mb[:], 0.0), writes=[mb])
        rs = b.sb("rs", [128, 4], F32)
        rg = b.sb("rg", [128, 4], F32)
        oab = b.sb("oab", [128, 512], BF16)
        oaT = [b.sb(f"oaT{i}", [128, 4, 128], BF16) for i in range(2)]
        pS = [b.ps(f"pS{i}", [128, 512], F32) for i in range(2)]
        pS2 = b.ps("pS2", [128, 512], F32)
        pO = [b.ps(f"pO{i}", [128, 512], F32) for i in range(3)]
        pTr = b.ps("pTr", [128, 8, 128], BF16)
        nrot = {"bct": 0, "scf": 0, "pT": 0, "pS": 0}

        def rot(name, lst):
            nrot[name] += 1
            return lst[nrot[name] % len(lst)]

        def finalize(po, ncol_off, h, qs, branch, first):
            qt = qs_base + qs
            o0 = ncol_off
            b.op("dve", lambda e: e.tensor_scalar_max(out=rs[:, 0:1], in0=po[:, o0 + 64:o0 + 65], scalar1=1e-30), reads=[po], writes=[rs])
            b.op("dve", lambda e: e.reciprocal(out=rs[:, 1:2], in_=rs[:, 0:1]), reads=[rs], writes=[rs])
            b.op("dve", lambda e: e.tensor_tensor(out=rg[:, 0:1], in0=rs[:, 1:2], in1=self.gts[:, qt, h * 3 + branch:h * 3 + branch + 1], op=ALU.mult),
                 reads=[rs, self.gts], writes=[rg])
            if first:
                b.op("dve", lambda e: e.tensor_scalar_mul(out=oacc[:, qs, h * 64:(h + 1) * 64], in0=po[:, o0:o0 + 64], scalar1=rg[:, 0:1]),
                     reads=[po, rg], writes=[oacc])
            else:
                b.op("dve", lambda e: e.scalar_tensor_tensor(out=oacc[:, qs, h * 64:(h + 1) * 64], in0=po[:, o0:o0 + 64], scalar=rg[:, 0:1],
                                                             in1=oacc[:, qs, h * 64:(h + 1) * 64], op0=ALU.mult, op1=ALU.add),
                     reads=[po, rg, oacc], writes=[oacc])

        nqg = getattr(self, "nqg_limit", 8)
        for qg in range(nqg):
            qs_base = 4 * qg
            q0 = 512 * qg
            Q = qm[qg % 2]
            b.dma("sp", Q[0:64, :, :], self.qT_d[:, :, q0:q0 + 512].rearrange("h d t -> d h t"), reads=[self.qT_d], writes=[Q])
            if qg < 2:
                b.op("pool", lambda e: e.memset(Q[64:128, :, :], 0.0), writes=[Q])
            for h in range(8):
                g = h // 4
                pc = pcT[h % 2]
                for ct in range(2):
                    p = rot("pS", pS)
                    b.op("pe", lambda e: e.matmul(p[:, :], lhsT=kcp[:, g, ct * 128:(ct + 1) * 128], rhs=Q[:, h, :], start=True, stop=True),
                         reads=[kcp, Q], writes=[p])
                    bt = rot("bct", bct)
                    b.dma("sp", bt[:], I["biasc"][h, ct, :, q0:q0 + 512], writes=[bt])
                    sc = rot("scf", scf)
                    b.op("dve", lambda e: e.tensor_tensor(out=sc[:, 0:512], in0=p[:, :], in1=bt[:], op=ALU.add), reads=[p, bt], writes=[sc])
                    b.op("act", lambda e: e.activation(out=pc[:, ct, :], in_=sc[:, 0:512], func=AF.Exp), reads=[sc], writes=[pc])
                po = pO[0]
                for qs in range(4):
                    for ct in range(2):
                        b.op("pe", lambda e: e.matmul(po[:, qs * 128:qs * 128 + 129] if False else po[:, 0:129], lhsT=pc[:, ct, qs * 128:(qs + 1) * 128],
                                                      rhs=self.vcA[:, g, ct, :], start=(ct == 0), stop=(ct == 1)), reads=[pc, self.vcA], writes=[po])
                    finalize(po, 0, h, qs, 0, True)
                    if h % 4 == 0:
                        b.op("dve", lambda e: e.tensor_scalar_mul(out=imp[:, qs, g, :], in0=po[:, 65:129], scalar1=rs[:, 1:2]), reads=[po, rs], writes=[imp])
                    else:
                        b.op("dve", lambda e: e.scalar_tensor_tensor(out=imp[:, qs, g, :], in0=po[:, 65:129], scalar=rs[:, 1:2], in1=imp[:, qs, g, :],
                                                                     op0=ALU.mult, op1=ALU.add), reads=[po, rs, imp], writes=[imp])
            if qg >= 2:
                for qs in range(4):
                    qt = qs_base + qs
                    for g in range(2):
                        b.op("dve", lambda e: e.tensor_tensor(out=impm[:], in0=imp[:, qs, g, :], in1=candneg[:, qt, :], op=ALU.add), reads=[imp, candneg], writes=[impm])
                        b.op("dve", lambda e: e.max(out=m8a[:], in_=impm[:]), reads=[impm], writes=[m8a])
                        b.op("dve", lambda e: e.match_replace(out=impm2[:], in_to_replace=m8a[:], in_values=impm[:], imm_value=-1e9), reads=[m8a, impm], writes=[impm2])
                        b.op("dve", lambda e: e.max(out=m8b[:], in_=impm2[:]), reads=[impm2], writes=[m8b])
                        b.op("dve", lambda e: e.tensor_scalar(out=msk[:], in0=impm[:], scalar1=m8b[:, 4:5], scalar2=None, op0=ALU.is_ge), reads=[impm, m8b], writes=[msk])
                        b.op("dve", lambda e: e.tensor_tensor(out=msk[:], in0=msk[:], in1=fz[:, qt, :], op=ALU.max), reads=[msk, fz], writes=[msk])
                        b.op("dve", lambda e: e.tensor_scalar(out=mb[:, 64:128], in0=msk[:], scalar1=-NEG, scalar2=NEG, op0=ALU.mult, op1=ALU.add), reads=[msk], writes=[mb])
                        b.op("pe", lambda e: e.transpose(out=pTr[:, 0, :], in_=mb[:], identity=self.ident[:]), reads=[mb, self.ident], writes=[pTr])
                        b.op("act", lambda e: e.copy(out=Q[64:128, 4 * g:4 * g + 4, qs * 128:(qs + 1) * 128],
                                                     in_=pTr[64:128, 0:1, :].to_broadcast([64, 4, 128])), reads=[pTr], writes=[Q])
            for h in range(8):
                g = h // 4
                po_s, po_w = pO[1], pO[2]
                for po in (po_s, po_w):
                    b.op("pe", lambda e: e.matmul(po[:, 0:260], lhsT=zer[:, 0:128], rhs=zer[:, 0:260], start=True, stop=True), reads=[zer], writes=[po])
                nkt = 4 * (qg + 1)
                for kt in range(nkt):
                    dlt = 4 * qg - kt
                    qstart = 0 if dlt >= 0 else -dlt * 128
                    N = 512 - qstart
                    p = rot("pS", pS)
                    b.op("pe", lambda e: e.matmul(p[:, 0:N], lhsT=self.ksE[:, g, kt * 128:(kt + 1) * 128], rhs=Q[:, h, qstart:512], start=True, stop=True),
                         reads=[self.ksE, Q], writes=[p])
                    pt_ = rot("pT", pT)
                    if dlt <= 1:
                        c0 = 128 if dlt == 1 else 0
                        sc = rot("scf", scf)
                        b.op("dve", lambda e: e.tensor_tensor(out=sc[:, 0:N], in0=p[:, 0:N], in1=ts[:, h, c0:c0 + N], op=ALU.add), reads=[p, ts], writes=[sc])
                        b.op("act", lambda e: e.activation(out=pt_[:, 0:N], in_=sc[:, 0:N], func=AF.Exp), reads=[sc], writes=[pt_])
                    else:
                        b.op("act", lambda e: e.activation(out=pt_[:, 0:N], in_=p[:, 0:N], func=AF.Exp, bias=b31[:, h:h + 1]), reads=[p, b31], writes=[pt_])
                    for qs in range(qstart // 128, 4):
                        o = qs * 128 - qstart
                        b.op("pe", lambda e: e.matmul(po_s[:, qs * 65:(qs + 1) * 65], lhsT=pt_[:, o:o + 128], rhs=self.vaug_s[:, kt, g, :],
                                                      start=False, stop=(kt == nkt - 1), skip_group_check=True), reads=[pt_, self.vaug_s], writes=[po_s])
                kts = [kt for kt in range(4 * qg - 4, 4 * qg + 4) if kt >= 0]
                for kt in kts:
                    qs_lo = max(0, kt - 4 * qg)
                    qs_hi = min(3, kt + 4 - 4 * qg)
                    N = (qs_hi - qs_lo + 1) * 128
                    c0 = 128 * (4 * qg + qs_lo - kt)
                    p = rot("pS", pS)
                    b.op("pe", lambda e: e.matmul(p[:, 0:N], lhsT=kwp[:, g, kt * 128:(kt + 1) * 128], rhs=Q[:, h, qs_lo * 128:(qs_hi + 1) * 128], start=True, stop=True),
                         reads=[kwp, Q], writes=[p])
                    sc = rot("scf", scf)
                    b.op("dve", lambda e: e.tensor_tensor(out=sc[:, 0:N], in0=p[:, 0:N], in1=tw[:, h, c0:c0 + N], op=ALU.add), reads=[p, tw], writes=[sc])
                    pt_ = rot("pT", pT)
                    b.op("act", lambda e: e.activation(out=pt_[:, 0:N], in_=sc[:, 0:N], func=AF.Exp), reads=[sc], writes=[pt_])
                    for qs in range(qs_lo, qs_hi + 1):
                        o = (qs - qs_lo) * 128
                        b.op("pe", lambda e: e.matmul(po_w[:, qs * 65:(qs + 1) * 65], lhsT=pt_[:, o:o + 128], rhs=self.vaug_w[:, kt, g, :],
                                                      start=False, stop=(kt == kts[-1]), skip_group_check=True), reads=[pt_, self.vaug_w], writes=[po_w])
                for qs in range(4):
                    finalize(po_s, qs * 65, h, qs, 1, False)
                    finalize(po_w, qs * 65, h, qs, 2, False)
            for qs in range(4):
                qt = qs_base + qs
                ot = oaT[qs % 2]
                b.op("act", lambda e: e.copy(out=oab[:], in_=oacc[:, qs, :]), reads=[oacc], writes=[oab])
                for c in range(4):
                    b.op("pe", lambda e: e.transpose(out=pTr[:, 4 + c, :], in_=oab[:, c * 128:(c + 1) * 128], identity=self.ident[:]), reads=[oab, self.ident], writes=[pTr])
                b.op("act", lambda e: e.copy(out=ot[:], in_=pTr[:, 4:8, :]), reads=[pTr], writes=[ot])
                b.dma("pool", self.oaT_d[:, :, qt * 128:(qt + 1) * 128].rearrange("c p t -> p c t"), ot[:], reads=[ot], writes=[self.oaT_d])
        if "attn" in self.debug:
            d = self.dbg_out("oaT", [4, 128, S], BF16)
            b.dma("pool", d, self.oaT_d[:], reads=[self.oaT_d])


Prog.phase_attn = _phase_attn


def _phase_merge(self):
    b = self.b
    I = self.inp
    self.x1_d = b.dram("x1_d", [S, D], F32)
    with b.scope():
        gat = self.load_gain("gat2", I["attn_norm_g"][0])
        stage = [b.sb(f"mst{i}", [128, 1024], F32) for i in range(2)]
        wg = b.sb("wg", [128, 8, 2048], BF16)
        for n in range(2):
            for c in range(8):
                st = stage[c % 2]
                b.dma("sp", st[:], I["w_in"][0][c * 128:(c + 1) * 128, GA0 + n * 1024:GA0 + (n + 1) * 1024], writes=[st])
                b.op("act", lambda e: e.activation(out=wg[:, c, n * 1024:(n + 1) * 1024], in_=st[:], func=AF.Copy, scale=gat[:, c:c + 1]),
                     reads=[st, gat], writes=[wg])
        wa = b.sb("wa", [128, 4, 1024], BF16)
        wb = b.sb("wb", [128, 4, 1024], BF16)
        wo = b.sb("wo", [128, 8, 1024], BF16)
        self.load_weight(wa, I["w_proj_a"][0], 1024, kch=4, stage=stage, eng="dve")
        self.load_weight(wb, I["w_proj_b"][0], 1024, kch=4, stage=stage, eng="dve")
        self.load_weight(wo, I["w_out"][0], 1024, kch=8, stage=stage, eng="dve")
        xt = [b.sb(f"mxt{i}", [128, D], F32) for i in range(2)]
        junk = b.sb("mjunk", [128, D], BF16)
        ss = [b.sb(f"mss{i}", [128, 1], F32) for i in range(2)]
        hb = [b.sb(f"mhb{i}", [128, D], BF16) for i in range(2)]
        hT = [b.sb(f"mhT{i}", [128, 8, 128], BF16) for i in range(2)]
        oat = [b.sb(f"oat{i}", [128, 4, 128], BF16) for i in range(2)]
        obt = [b.sb(f"obt{i}", [128, 4, 128], BF16) for i in range(2)]
        sg = b.sb("msg", [128, 2048], F32)
        m1 = b.sb("m1", [128, 1024], F32)
        m2 = b.sb("m2", [128, 1024], F32)
        mgb = b.sb("mgb", [128, 1024], BF16)
        mT = b.sb("mT", [128, 8, 128], BF16)
        x1t = [b.sb(f"x1t{i}", [128, D], F32) for i in range(2)]
        pt = b.ps("mpt", [128, 8, 128], BF16)
        pg = [b.ps(f"mpg{i}", [128, 512], F32) for i in range(2)]
        pa = [b.ps(f"mpa{i}", [128, 512], F32) for i in range(2)]
        pb = [b.ps(f"mpb{i}", [128, 512], F32) for i in range(2)]
        for t in range(getattr(self, "nt_limit", NT)):
            i = t % 2
            self.make_hT(I["x"], t, xt[i], junk, ss[i], hb[i], pt, hT[i], self.ident)
            b.dma("sp", oat[i][:], self.oaT_d[:, :, t * 128:(t + 1) * 128].rearrange("c p t -> p c t"), reads=[self.oaT_d], writes=[oat[i]])
            b.dma("sp", obt[i][:], self.obT_d[:, :, t * 128:(t + 1) * 128].rearrange("c p t -> p c t"), reads=[self.obT_d], writes=[obt[i]])
            for n in range(4):
                p = pg[n % 2]
                for c in range(8):
                    b.op("pe", lambda e: e.matmul(p[:, :], lhsT=hT[i][:, c, :], rhs=wg[:, c, n * 512:(n + 1) * 512], start=(c == 0), stop=(c == 7)),
                         reads=[hT[i], wg], writes=[p])
                b.op("act", lambda e: e.activation(out=sg[:, n * 512:(n + 1) * 512], in_=p[:, :], func=AF.Sigmoid), reads=[p], writes=[sg])
            for n in range(2):
                for c in range(4):
                    b.op("pe", lambda e: e.matmul(pa[n][:, :], lhsT=oat[i][:, c, :], rhs=wa[:, c, n * 512:(n + 1) * 512], start=(c == 0), stop=(c == 3)),
                         reads=[oat[i], wa], writes=[pa[n]])
                for c in range(4):
                    b.op("pe", lambda e: e.matmul(pb[n][:, :], lhsT=obt[i][:, c, :], rhs=wb[:, c, n * 512:(n + 1) * 512], start=(c == 0), stop=(c == 3)),
                         reads=[obt[i], wb], writes=[pb[n]])
                b.op("dve", lambda e: e.tensor_tensor(out=m1[:, n * 512:(n + 1) * 512], in0=pa[n][:, :], in1=sg[:, n * 512:(n + 1) * 512], op=ALU.mult),
                     reads=[pa[n], sg], writes=[m1])
                b.op("dve", lambda e: e.tensor_tensor(out=m2[:, n * 512:(n + 1) * 512], in0=pb[n][:, :], in1=sg[:, 1024 + n * 512:1024 + (n + 1) * 512], op=ALU.mult),
                     reads=[pb[n], sg], writes=[m2])
            b.op("pool", lambda e: e.tensor_tensor(out=mgb[:], in0=m1[:], in1=m2[:], op=ALU.add), reads=[m1, m2], writes=[mgb])
            for c in range(8):
                b.op("pe", lambda e: e.transpose(out=pt[:, c, :], in_=mgb[:, c * 128:(c + 1) * 128], identity=self.ident[:]), reads=[mgb, self.ident], writes=[pt])
            b.op("act", lambda e: e.copy(out=mT[:], in_=pt[:]), reads=[pt], writes=[mT])
            for n in range(2):
                for c in range(8):
                    b.op("pe", lambda e: e.matmul(pa[n][:, :], lhsT=mT[:, c, :], rhs=wo[:, c, n * 512:(n + 1) * 512], start=(c == 0), stop=(c == 7)),
                         reads=[mT, wo], writes=[pa[n]])
                b.op("dve", lambda e: e.tensor_tensor(out=x1t[i][:, n * 512:(n + 1) * 512], in0=pa[n][:, :], in1=xt[i][:, n * 512:(n + 1) * 512], op=ALU.add),
                     reads=[pa[n], xt[i]], writes=[x1t[i]])
            b.dma("pool", self.x1_d[t * 128:(t + 1) * 128, :], x1t[i][:], reads=[x1t[i]], writes=[self.x1_d])
        if "merge" in self.debug:
            d = self.dbg_out("x1", [S, D], F32)
            b.dma("pool", d, self.x1_d[:], reads=[self.x1_d])


def _phase_ffn(self):
    b = self.b
    I = self.inp
    TG = 128
    NFT = 44
    with b.scope():
        gf = self.load_gain("gf", I["ffn_norm_g"][0])
        stage = [b.sb(f"fst{i}", [128, 1024], F32) for i in range(2)]
        wu = b.sb("wu", [128, 8, 2 * DFF], BF16)
        for n in range(8):
            for c in range(8):
                st = stage[c % 2]
                b.dma("sp", st[:, 0:704], I["w_up"][0][c * 128:(c + 1) * 128, n * 704:(n + 1) * 704], writes=[st])
                b.op("act", lambda e: e.activation(out=wu[:, c, n * 704:(n + 1) * 704], in_=st[:, 0:704], func=AF.Copy, scale=gf[:, c:c + 1]),
                     reads=[st, gf], writes=[wu])
        wd = b.sb("wd", [128, 22, D], BF16)
        self.load_weight(wd, I["w_down"][0], D, kch=22, stage=stage, eng="dve")
        cw = b.sb("cw", [128, 3, NFT], F32)
        for j in range(3):
            b.dma("sp", cw[:, j, :], I["conv_w"][0][j].rearrange("(c p) -> p c", p=128), writes=[cw], allow_slow_non_contiguous=True)
        cbias = self.load_gain("cbias", I["conv_b"][0], kch=NFT)
        carry = b.sb("carry", [128, NFT, 2], F32)
        b.op("pool", lambda e: e.memset(carry[:], 0.0), writes=[carry])
        xt = [b.sb(f"fxt{i}", [128, D], F32) for i in range(2)]
        junk = b.sb("fjunk", [128, D], BF16)
        ss = [b.sb(f"fss{i}", [128, 1], F32) for i in range(2)]
        hb = [b.sb(f"fhb{i}", [128, D], BF16) for i in range(2)]
        hT1 = [b.sb(f"fhT{i}", [128, 8, 128], BF16) for i in range(2)]
        hTg = b.sb("fhTg", [128, 8, TG], BF16)
        ub = [b.sb(f"ub{i}", [128, TG + 2], F32) for i in range(2)]
        cv = [b.sb(f"cv{i}", [128, TG], F32) for i in range(2)]
        sgl = b.sb("sgl", [128, TG], F32)
        actT = b.sb("actT", [128, 22, TG], BF16)
        self._val = b.sb("fval", [128, 22, TG], BF16)
        ot = xt
        pt = b.ps("fpt", [128, 8, 128], BF16)
        pu = [b.ps(f"fpu{i}", [128, 512], F32) for i in range(3)]
        pd = [b.ps(f"fpd{i}", [128, 512], F32) for i in range(2)]
        ng = getattr(self, "nt_limit", NT) * 128 // TG
        for gi in range(ng):
            for s_ in range(TG // 128):
                t = gi * (TG // 128) + s_
                self.make_hT(self.x1_d, t, xt[s_], junk, ss[s_], hb[s_], pt, hT1[s_], self.ident)
                b.op("pool", lambda e: e.tensor_copy(out=hTg[:, :, s_ * 128:(s_ + 1) * 128], in_=hT1[s_][:]), reads=[hT1[s_]], writes=[hTg])
            for ft in range(NFT):
                p = pu[ft % 3]
                u = ub[ft % 2]
                c_ = cv[(ft // 22) % 2] if False else cv[ft % 2]
                for c in range(8):
                    b.op("pe", lambda e: e.matmul(p[:, 0:TG], lhsT=wu[:, c, ft * 128:(ft + 1) * 128], rhs=hTg[:, c, :], start=(c == 0), stop=(c == 7)),
                         reads=[wu, hTg], writes=[p])
                b.op("act", lambda e: e.copy(out=u[:, 2:TG + 2], in_=p[:, 0:TG]), reads=[p], writes=[u])
                b.op("pool", lambda e: e.tensor_copy(out=u[:, 0:2], in_=carry[:, ft, :]), reads=[carry], writes=[u])
                b.op("pool", lambda e: e.tensor_copy(out=carry[:, ft, :], in_=u[:, TG:TG + 2]), reads=[u], writes=[carry])
                b.op("dve", lambda e: e.tensor_scalar(out=c_[:], in0=u[:, 0:TG], scalar1=cw[:, 0, ft:ft + 1], scalar2=cbias[:, ft:ft + 1], op0=ALU.mult, op1=ALU.add),
                     reads=[u, cw, cbias], writes=[c_])
                b.op("dve", lambda e: e.scalar_tensor_tensor(out=c_[:], in0=u[:, 1:TG + 1], scalar=cw[:, 1, ft:ft + 1], in1=c_[:], op0=ALU.mult, op1=ALU.add),
                     reads=[u, cw, c_], writes=[c_])
                if ft < 22:
                    b.op("dve", lambda e: e.scalar_tensor_tensor(out=self._val[:, ft, :], in0=u[:, 2:TG + 2], scalar=cw[:, 2, ft:ft + 1], in1=c_[:], op0=ALU.mult, op1=ALU.add),
                         reads=[u, cw, c_], writes=[self._val])
                else:
                    b.op("dve", lambda e: e.scalar_tensor_tensor(out=c_[:], in0=u[:, 2:TG + 2], scalar=cw[:, 2, ft:ft + 1], in1=c_[:], op0=ALU.mult, op1=ALU.add),
                         reads=[u, cw, c_], writes=[c_])
                    b.op("act", lambda e: e.activation(out=sgl[:], in_=c_[:], func=AF.Silu), reads=[c_], writes=[sgl])
                    b.op("dve", lambda e: e.tensor_tensor(out=actT[:, ft - 22, :], in0=sgl[:], in1=self._val[:, ft - 22, :], op=ALU.mult),
                         reads=[sgl, self._val], writes=[actT])
            for s_ in range(TG // 128):
                t = gi * (TG // 128) + s_
                for n in range(2):
                    for f in range(22):
                        b.op("pe", lambda e: e.matmul(pd[n][:, :], lhsT=actT[:, f, s_ * 128:(s_ + 1) * 128], rhs=wd[:, f, n * 512:(n + 1) * 512], start=(f == 0), stop=(f == 21)),
                             reads=[actT, wd], writes=[pd[n]])
                    b.op("dve", lambda e: e.tensor_tensor(out=ot[s_][:, n * 512:(n + 1) * 512], in0=pd[n][:, :], in1=xt[s_][:, n * 512:(n + 1) * 512], op=ALU.add),
                         reads=[pd[n], xt[s_]], writes=[ot[s_]])
                b.dma("pool", self.out[t * 128:(t + 1) * 128, :], ot[s_][:], reads=[ot[s_]])


Prog.phase_merge = _phase_merge
Prog.phase_ffn = _phase_ffn


def _phase_rwkv(self):
    b = self.b
    I = self.inp
    TG = 256
    NCH = TG // 64
    tt = lambda eng, out, in0, in1, op, rd, wr: b.op(eng, lambda e: e.tensor_tensor(out=out, in0=in0, in1=in1, op=op), reads=rd, writes=wr)
    with b.scope():
        gat = self.load_gain("gat3", I["attn_norm_g"][0])
        stage = [b.sb(f"rst{i}", [128, 1792], F32) for i in range(2)]
        wr = b.sb("wr", [128, 8, 1792], BF16)
        self.load_weight(wr, I["w_in"][0][:, RW0:RW0 + 1792], 1792, gvec=gat, stage=stage)

        def colvec(name, src, n):
            t = b.sb(name, [64, n], F32)
            b.dma("sp", t[:], src.rearrange("(c p) -> p c", p=64), writes=[t], allow_slow_non_contiguous=True)
            return t
        mu = colvec("mu", I["rwkv_mu"][0], 28)
        w0 = colvec("w0", I["rwkv_w0"][0], 8)
        a0 = colvec("a0", I["rwkv_a0"][0], 8)
        k_k = colvec("k_k", I["rwkv_k_k"][0], 8)
        k_a = colvec("k_a", I["rwkv_k_a"][0], 8)
        r_k = colvec("r_k", I["rwkv_r_k"][0].rearrange("h d -> (h d)"), 8)
        w2s = b.sb("w2s", [64, 512], F32)
        a2s = b.sb("a2s", [64, 512], F32)
        g2s = b.sb("g2s", [64, 2, 512], F32)
        b.dma("sp", w2s[:], I["rwkv_w2"][0], writes=[w2s])
        b.dma("sp", a2s[:], I["rwkv_a2"][0], writes=[a2s])
        b.dma("sp", g2s[:], I["rwkv_g2"][0].rearrange("(two l) f -> l two f", two=2), writes=[g2s])
        lng = b.sb("lng", [64, 512], F32)
        lnb = b.sb("lnb", [64, 512], F32)
        b.dma("sp", lng[:], I["rwkv_ln_g"][0].partition_broadcast(64), writes=[lng])
        b.dma("sp", lnb[:], I["rwkv_ln_b"][0].partition_broadcast(64), writes=[lnb])
        msk = b.sb("rmsk", [64, 3, 64], F32)
        b.dma("sp", msk[:], I["rwmask"], writes=[msk])
        rstm = b.sb("rstm", [64, TG], F32)
        b.dma("sp", rstm[:], I["rwreset"][:, 0:TG], writes=[rstm])
        ones = b.sb("ones64", [64, 64], F32)
        b.op("pool", lambda e: e.memset(ones[:], 1.0), writes=[ones])
        idf = self.identf
        carry = b.sb("rcarry", [64, 28], F32)
        b.op("pool", lambda e: e.memset(carry[:], 0.0), writes=[carry])
        Hs = [[b.sb(f"H{h}_{i}", [64, 64], F32) for i in range(2)] for h in range(8)]
        for h in range(8):
            b.op("pool", lambda e: e.memset(Hs[h][0][:], 0.0), writes=[Hs[h][0]])
        xt = [b.sb(f"rxt{i}", [128, D], F32) for i in range(2)]
        junk = b.sb("rjunk", [128, D], BF16)
        ss = [b.sb(f"rss{i}", [128, 1], F32) for i in range(2)]
        hb = [b.sb(f"rhb{i}", [128, D], BF16) for i in range(2)]
        hT1 = [b.sb(f"rhT{i}", [128, 8, 128], BF16) for i in range(2)]
        hTg = b.sb("rhTg", [128, 8, TG], BF16)
        pbuf = [b.sb(f"rpb{i}", [64, TG + 1], F32) for i in range(2)]
        dtmp = b.sb("rdtmp", [64, TG], F32)
        X = [b.sb(f"rX{w}", [64, 8, TG], F32) for w in range(3)]
        xs = b.sb("rxs", [64, 4, TG], F32)
        BV = b.sb("rBV", [64, 8, TG], F32)
        Ytm = b.sb("rYtm", [64, NCH, 8, 64], F32)
        sqv = b.sb("rsqv", [64, NCH, 8, 64], F32)
        st1 = b.sb("rst1", [64, NCH * 8], F32)
        st2 = b.sb("rst2", [64, NCH * 8], F32)
        T = {n: b.sb("r" + n, [64, TG], F32) for n in ["lw", "as", "kk", "sq", "kkn", "bv", "kp", "t1", "L", "Lx", "Ep", "Em", "Ex", "BT", "KT", "BG", "KG", "rk"]}
        AR = b.sb("rAR", [64, NCH, 2, 64], F32)
        TM = [b.sb(f"rTM{i}", [64, 3, 64], F32) for i in range(2)]
        XM = [b.sb(f"rXM{i}", [64, 4, 64], F32) for i in range(2)]
        AA = [b.sb(f"rAA{i}", [64, 2, 64], F32) for i in range(3)]
        PP = [b.sb(f"rPP{i}", [64, 64], F32) for i in range(3)]
        Xs = b.sb("rXs", [64, 64], F32)
        Us = b.sb("rUs", [64, 64], F32)
        obf = [b.sb(f"robf{i}", [64, TG], BF16) for i in range(2)]
        otmp = b.sb("rotmp", [64, TG], F32)
        pt = b.ps("rpt", [128, 8, 128], BF16)
        pp = [b.ps(f"rpp{i}", [128, 512], F32) for i in range(2)]
        pq = [b.ps(f"rpq{i}", [128, 512], F32) for i in range(2)]
        pd = [b.ps(f"rpd{i}", [128, 512], F32) for i in range(2)]
        pz = b.ps("rpz", [128, 512], F32)
        cnt = {"pp": 0, "pq": 0, "pd": 0, "aa": 0, "ppb": 0, "tm": 0, "xm": 0, "pb": 0}

        def nxt(k, lst):
            cnt[k] += 1
            return lst[cnt[k] % len(lst)]

        ngr = getattr(self, "nrg_limit", S // TG)
        for gi in range(ngr):
            q0 = gi * TG
            for s_ in range(TG // 128):
                t = gi * (TG // 128) + s_
                self.make_hT(I["x"], t, xt[s_], junk, ss[s_], hb[s_], pt, hT1[s_], self.ident)
                b.op("pool", lambda e: e.tensor_copy(out=hTg[:, :, s_ * 128:(s_ + 1) * 128], in_=hT1[s_][:]), reads=[hT1[s_]], writes=[hTg])

            def proj_lerp(fc, out_ap, out_buf, post=None):
                p = nxt("pp", pp)
                for c in range(8):
                    b.op("pe", lambda e: e.matmul(p[0:64, 0:TG], lhsT=wr[:, c, fc * 64:(fc + 1) * 64], rhs=hTg[:, c, :], start=(c == 0), stop=(c == 7)),
                         reads=[wr, hTg], writes=[p])
                pb_ = nxt("pb", pbuf)
                b.op("act", lambda e: e.copy(out=pb_[:, 1:TG + 1], in_=p[0:64, 0:TG]), reads=[p], writes=[pb_])
                b.op("pool", lambda e: e.tensor_copy(out=pb_[:, 0:1], in_=carry[:, fc:fc + 1]), reads=[carry], writes=[pb_])
                b.op("pool", lambda e: e.tensor_copy(out=carry[:, fc:fc + 1], in_=pb_[:, TG:TG + 1]), reads=[pb_], writes=[carry])
                tt("dve", dtmp[:], pb_[:, 0:TG], pb_[:, 1:TG + 1], ALU.subtract, [pb_], [dtmp])
                b.op("dve", lambda e: e.scalar_tensor_tensor(out=out_ap, in0=dtmp[:], scalar=mu[:, fc:fc + 1], in1=pb_[:, 1:TG + 1], op0=ALU.mult, op1=ALU.add),
                     reads=[dtmp, mu, pb_], writes=[out_buf])

            for w in range(3):
                for h in range(8):
                    proj_lerp(w * 8 + h, X[w][:, h, :], X[w])
            for j in range(4):
                proj_lerp(24 + j, xs[:, j, :], xs)
            b.op("act", lambda e: e.activation(out=xs[:, 0, :], in_=xs[:, 0, :], func=AF.Tanh), reads=[xs], writes=[xs])
            b.op("act", lambda e: e.activation(out=xs[:, 2:4, :], in_=xs[:, 2:4, :], func=AF.Sigmoid), reads=[xs], writes=[xs])

            for h in range(8):
                hs = slice(h * 64, (h + 1) * 64)
                R_, K_, V_ = X[0][:, h, :], X[1][:, h, :], X[2][:, h, :]
                p = nxt("pp", pp)
                b.op("pe", lambda e: e.matmul(p[0:64, 0:TG], lhsT=w2s[:, hs], rhs=xs[:, 0, :], start=True, stop=True), reads=[w2s, xs], writes=[p])
                b.op("act", lambda e: e.activation(out=T["lw"][:], in_=p[0:64, 0:TG], func=AF.Sigmoid, bias=w0[:, h:h + 1]), reads=[p, w0], writes=[T["lw"]])
                b.op("pool", lambda e: e.tensor_scalar_mul(out=T["lw"][:], in0=T["lw"][:], scalar1=-0.6065306597126334), reads=[T["lw"]], writes=[T["lw"]])
                p = nxt("pp", pp)
                b.op("pe", lambda e: e.matmul(p[0:64, 0:TG], lhsT=a2s[:, hs], rhs=xs[:, 1, :], start=True, stop=True), reads=[a2s, xs], writes=[p])
                b.op("act", lambda e: e.activation(out=T["as"][:], in_=p[0:64, 0:TG], func=AF.Sigmoid, bias=a0[:, h:h + 1]), reads=[p, a0], writes=[T["as"]])
                b.op("dve", lambda e: e.tensor_scalar_mul(out=T["kk"][:], in0=K_, scalar1=k_k[:, h:h + 1]), reads=[X[1], k_k], writes=[T["kk"]])
                tt("pool", T["sq"][:], T["kk"][:], T["kk"][:], ALU.mult, [T["kk"]], [T["sq"]])
                p = nxt("pp", pp)
                b.op("pe", lambda e: e.matmul(p[0:64, 0:TG], lhsT=ones[:], rhs=T["sq"][:], start=True, stop=True), reads=[ones, T["sq"]], writes=[p])
                b.op("act", lambda e: e.activation(out=T["sq"][:], in_=p[0:64, 0:TG], func=AF.Sqrt), reads=[p], writes=[T["sq"]])
                b.op("dve", lambda e: e.tensor_scalar_max(out=T["sq"][:], in0=T["sq"][:], scalar1=1e-12), reads=[T["sq"]], writes=[T["sq"]])
                b.op("dve", lambda e: e.reciprocal(out=T["sq"][:], in_=T["sq"][:]), reads=[T["sq"]], writes=[T["sq"]])
                tt("dve", T["kkn"][:], T["kk"][:], T["sq"][:], ALU.mult, [T["kk"], T["sq"]], [T["kkn"]])
                tt("pool", T["bv"][:], T["kkn"][:], T["as"][:], ALU.mult, [T["kkn"], T["as"]], [T["bv"]])
                b.op("dve", lambda e: e.tensor_scalar(out=T["t1"][:], in0=T["as"][:], scalar1=-1.0, scalar2=k_a[:, h:h + 1], op0=ALU.add, op1=ALU.mult),
                     reads=[T["as"], k_a], writes=[T["t1"]])
                b.op("dve", lambda e: e.scalar_tensor_tensor(out=T["kp"][:], in0=T["t1"][:], scalar=1.0, in1=K_, op0=ALU.add, op1=ALU.mult),
                     reads=[T["t1"], X[1]], writes=[T["kp"]])
                tt("pool", T["rk"][:], R_, T["kp"][:], ALU.mult, [X[0], T["kp"]], [T["rk"]])
                b.op("pool", lambda e: e.tensor_scalar_mul(out=T["rk"][:], in0=T["rk"][:], scalar1=r_k[:, h:h + 1]), reads=[T["rk"], r_k], writes=[T["rk"]])
                p = nxt("pp", pp)
                b.op("pe", lambda e: e.matmul(p[0:64, 0:TG], lhsT=ones[:], rhs=T["rk"][:], start=True, stop=True), reads=[ones, T["rk"]], writes=[p])
                tt("dve", BV[:, h, :], p[0:64, 0:TG], V_, ALU.mult, [p, X[2]], [BV])
                b.op("dve", lambda e: e.tensor_tensor_scan(out=T["L"][:], data0=rstm[:], data1=T["lw"][:], initial=0.0, op0=ALU.mult, op1=ALU.add),
                     reads=[rstm, T["lw"]], writes=[T["L"]])
                tt("pool", T["Lx"][:], T["L"][:], T["lw"][:], ALU.subtract, [T["L"], T["lw"]], [T["Lx"]])
                b.op("act", lambda e: e.activation(out=T["Ep"][:], in_=T["L"][:], func=AF.Exp), reads=[T["L"]], writes=[T["Ep"]])
                b.op("act", lambda e: e.activation(out=T["Em"][:], in_=T["L"][:], func=AF.Exp, scale=-1.0), reads=[T["L"]], writes=[T["Em"]])
                b.op("act", lambda e: e.activation(out=T["Ex"][:], in_=T["Lx"][:], func=AF.Exp), reads=[T["Lx"]], writes=[T["Ex"]])
                c3 = lambda ap: ap.rearrange("p (c t) -> p c t", t=64)
                b.op("dve", lambda e: e.scalar_tensor_tensor(out=AR[:, :, 0, :], in0=c3(T["kkn"][:]), scalar=-1.0, in1=c3(T["Ex"][:]), op0=ALU.mult, op1=ALU.mult),
                     reads=[T["kkn"], T["Ex"]], writes=[AR])
                tt("pool", AR[:, :, 1, :], c3(R_), c3(T["Ep"][:]), ALU.mult, [X[0], T["Ep"]], [AR])
                tt("dve", T["BT"][:], T["bv"][:], T["Em"][:], ALU.mult, [T["bv"], T["Em"]], [T["BT"]])
                tt("pool", T["KT"][:], T["kp"][:], T["Em"][:], ALU.mult, [T["kp"], T["Em"]], [T["KT"]])
                gC = c3(T["Ep"][:])[:, :, 63:64].to_broadcast([64, NCH, 64])
                tt("dve", c3(T["BG"][:]), c3(T["BT"][:]), gC, ALU.mult, [T["BT"], T["Ep"]], [T["BG"]])
                tt("pool", c3(T["KG"][:]), c3(T["KT"][:]), gC, ALU.mult, [T["KT"], T["Ep"]], [T["KG"]])
                for c in range(NCH):
                    cs = slice(c * 64, (c + 1) * 64)
                    Hc = Hs[h][(gi * NCH + c) % 2]
                    Hn = Hs[h][(gi * NCH + c + 1) % 2]
                    p = nxt("pq", pq)
                    for j, (src, sb_) in enumerate([(V_[:, cs], X[2]), (T["BG"][:, cs], T["BG"]), (T["KG"][:, cs], T["KG"])]):
                        b.op("pe", lambda e: e.transpose(out=p[0:64, j * 64:(j + 1) * 64], in_=src, identity=idf[0:64, 0:64]), reads=[sb_, idf], writes=[p])
                    tm = nxt("tm", TM)
                    b.op("act", lambda e: e.copy(out=tm[:].rearrange("p a b -> p (a b)"), in_=p[0:64, 0:192]), reads=[p], writes=[tm])
                    p = nxt("pq", pq)
                    arc = AR[:, c, :, :].rearrange("p a t -> p (a t)")
                    b.op("pe", lambda e: e.matmul(p[0:64, 0:128], lhsT=T["BT"][:, cs], rhs=arc, start=True, stop=True), reads=[T["BT"], AR], writes=[p])
                    b.op("pe", lambda e: e.matmul(p[0:64, 128:256], lhsT=T["KT"][:, cs], rhs=arc, start=True, stop=True), reads=[T["KT"], AR], writes=[p])
                    b.op("pe", lambda e: e.matmul(p[0:64, 256:320], lhsT=AR[:, c, 0, :], rhs=T["BT"][:, cs], start=True, stop=True), reads=[T["BT"], AR], writes=[p])
                    xm = nxt("xm", XM)
                    tt("dve", xm[:].rearrange("p (a m) t -> p a m t", a=2), p[0:64, 0:256].rearrange("p (a m t) -> p a m t", a=2, m=2),
                       msk[:, None, 0:2, :].to_broadcast([64, 2, 2, 64]), ALU.mult, [p, msk], [xm])
                    aa = nxt("aa", AA)
                    b.op("pool", lambda e: e.tensor_copy(out=aa[:, 0, :], in_=xm[:, 0, :]), reads=[xm], writes=[aa])
                    tt("dve", aa[:, 1, :], p[0:64, 256:320], msk[:, 2, :], ALU.mult, [p, msk], [aa])
                    P_ = nxt("ppb", PP)
                    tt("pool", P_[:], xm[:, 0, :], idf[0:64, 0:64], ALU.add, [xm, idf], [P_])
                    for step in range(5):
                        pdb = nxt("pd", pd)
                        b.op("pe", lambda e: e.matmul(pdb[0:64, 0:64], lhsT=aa[:, 1, :], rhs=aa[:, 0, :], start=True, stop=True), reads=[aa], writes=[pdb])
                        b.op("pe", lambda e: e.matmul(pdb[0:64, 64:128], lhsT=aa[:, 0, :], rhs=aa[:, 1, :], start=True, stop=True), reads=[aa], writes=[pdb])
                        aa2 = nxt("aa", AA)
                        b.op("act", lambda e: e.copy(out=aa2[:].rearrange("p a t -> p (a t)"), in_=pdb[0:64, 0:128]), reads=[pdb], writes=[aa2])
                        b.op("pe", lambda e: e.matmul(pdb[0:64, 128:192], lhsT=aa2[:, 1, :], rhs=P_[:], start=True, stop=True), reads=[aa2, P_], writes=[pdb])
                        P2 = nxt("ppb", PP)
                        tt("dve", P2[:], pdb[0:64, 128:192], P_[:], ALU.add, [pdb, P_], [P2])
                        aa, P_ = aa2, P2
                    b.op("pe", lambda e: e.matmul(pz[0:64, 0:64], lhsT=xm[:, 2, :], rhs=tm[:, 0, :], start=True, stop=False), reads=[xm, tm], writes=[pz])
                    b.op("pe", lambda e: e.matmul(pz[0:64, 0:64], lhsT=AR[:, c, 0, :], rhs=Hc[:], start=False, stop=True), reads=[AR, Hc], writes=[pz])
                    b.op("act", lambda e: e.copy(out=Xs[:], in_=pz[0:64, 0:64]), reads=[pz], writes=[Xs])
                    b.op("pe", lambda e: e.matmul(pz[0:64, 64:128], lhsT=P_[:], rhs=Xs[:], start=True, stop=True), reads=[P_, Xs], writes=[pz])
                    b.op("act", lambda e: e.copy(out=Us[:], in_=pz[0:64, 64:128]), reads=[pz], writes=[Us])
                    b.op("pe", lambda e: e.matmul(pz[0:64, 128:192], lhsT=AR[:, c, 1, :], rhs=Hc[:], start=True, stop=False), reads=[AR, Hc], writes=[pz])
                    b.op("pe", lambda e: e.matmul(pz[0:64, 128:192], lhsT=xm[:, 1, :], rhs=Us[:], start=False, stop=False), reads=[xm, Us], writes=[pz])
                    b.op("pe", lambda e: e.matmul(pz[0:64, 128:192], lhsT=xm[:, 3, :], rhs=tm[:, 0, :], start=False, stop=True), reads=[xm, tm], writes=[pz])
                    b.op("pe", lambda e: e.matmul(pz[0:64, 192:256], lhsT=tm[:, 1, :], rhs=Us[:], start=True, stop=False), reads=[tm, Us], writes=[pz])
                    b.op("pe", lambda e: e.matmul(pz[0:64, 192:256], lhsT=tm[:, 2, :], rhs=tm[:, 0, :], start=False, stop=True), reads=[tm], writes=[pz])
                    b.op("act", lambda e: e.copy(out=Ytm[:, c, h, :], in_=pz[0:64, 128:192]), reads=[pz], writes=[Ytm])
                    b.op("dve", lambda e: e.scalar_tensor_tensor(out=Hn[:], in0=Hc[:], scalar=T["Ep"][:, c * 64 + 63:c * 64 + 64], in1=pz[0:64, 192:256],
                                                                 op0=ALU.mult, op1=ALU.add), reads=[Hc, T["Ep"], pz], writes=[Hn])
            Y3 = Ytm[:].rearrange("p c h i -> p (c h) i")
            S3 = sqv[:].rearrange("p c h i -> p (c h) i")
            b.op("dve", lambda e: e.tensor_reduce(out=st1[:], in_=Y3, axis=AX.X, op=ALU.add), reads=[Ytm], writes=[st1])
            b.op("pool", lambda e: e.tensor_scalar_mul(out=st1[:], in0=st1[:], scalar1=1.0 / 64), reads=[st1], writes=[st1])
            tt("dve", Y3, Y3, st1[:].unsqueeze(2).to_broadcast([64, NCH * 8, 64]), ALU.subtract, [Ytm, st1], [Ytm])
            tt("pool", S3, Y3, Y3, ALU.mult, [Ytm], [sqv])
            b.op("dve", lambda e: e.tensor_reduce(out=st2[:], in_=S3, axis=AX.X, op=ALU.add), reads=[sqv], writes=[st2])
            b.op("act", lambda e: e.activation(out=st2[:], in_=st2[:], func=AF.Sqrt, scale=1.0 / 64, bias=64e-5), reads=[st2], writes=[st2])
            b.op("dve", lambda e: e.reciprocal(out=st2[:], in_=st2[:]), reads=[st2], writes=[st2])
            tt("dve", Y3, Y3, st2[:].unsqueeze(2).to_broadcast([64, NCH * 8, 64]), ALU.mult, [Ytm, st2], [Ytm])
            lg = lng[:].rearrange("p (h i) -> p h i", i=64)[:, None, :, :].to_broadcast([64, NCH, 8, 64])
            lb = lnb[:].rearrange("p (h i) -> p h i", i=64)[:, None, :, :].to_broadcast([64, NCH, 8, 64])
            tt("pool", Ytm[:], Ytm[:], lg, ALU.mult, [Ytm, lng], [Ytm])
            tt("dve", Ytm[:], Ytm[:], lb, ALU.add, [Ytm, lnb], [Ytm])
            for h in range(8):
                p = nxt("pq", pq)
                for c in range(NCH):
                    b.op("pe", lambda e: e.transpose(out=p[0:64, c * 64:(c + 1) * 64], in_=Ytm[:, c, h, :], identity=idf[0:64, 0:64]), reads=[Ytm, idf], writes=[p])
                tt("dve", otmp[:], p[0:64, 0:TG], BV[:, h, :], ALU.add, [p, BV], [otmp])
                pg_ = nxt("pp", pp)
                b.op("pe", lambda e: e.matmul(pg_[0:64, 0:TG], lhsT=g2s[:, 0, h * 64:(h + 1) * 64], rhs=xs[:, 2, :], start=True, stop=False), reads=[g2s, xs], writes=[pg_])
                b.op("pe", lambda e: e.matmul(pg_[0:64, 0:TG], lhsT=g2s[:, 1, h * 64:(h + 1) * 64], rhs=xs[:, 3, :], start=False, stop=True), reads=[g2s, xs], writes=[pg_])
                ob_ = obf[h % 2]
                tt("dve", ob_[:], otmp[:], pg_[0:64, 0:TG], ALU.mult, [otmp, pg_], [ob_])
                b.dma("pool", self.obT_d[h // 2, (h % 2) * 64:(h % 2) * 64 + 64, q0:q0 + TG], ob_[:], reads=[ob_], writes=[self.obT_d])
        if "rwkv" in self.debug:
            d = self.dbg_out("obT", [4, 128, S], BF16)
            b.dma("pool", d, self.obT_d[:], reads=[self.obT_d])


Prog.phase_rwkv = _phase_rwkv


def build_full():
    p = Prog()
    b = p.b
    p.alloc_root()
    with b.scope():
        p.alloc_persistent()
        p.phase_nsa_proj()
        p.phase_attn()
    p.phase_rwkv2()
    p.phase_merge()
    p.phase_ffn()
    p.finish()
    return p


def kernel(**inputs):
    p = build_full()
    consts = host_consts(inputs["rel_bias"])
    shared = {k: np.ascontiguousarray(np.asarray(inputs[k], np.float32)) for k in W_SPECS if k != "x"}
    shared.update(consts)
    x = np.asarray(inputs["x"], np.float32)
    in_maps = []
    for c in range(8):
        m = dict(shared)
        m["x"] = np.ascontiguousarray(x[c])
        in_maps.append(m)
    res = run_bass_kernel_spmd(p.nc, in_maps, core_ids=list(range(8)))
    return np.stack([np.asarray(r["out"], np.float32) for r in res.results], axis=0)


def _phase_rwkv2(self):
    b = self.b
    I = self.inp
    TG = 128
    NCH = 2
    tt = lambda eng, out, in0, in1, op, rd, wr: b.op(eng, lambda e: e.tensor_tensor(out=out, in0=in0, in1=in1, op=op), reads=rd, writes=wr)
    with b.scope():
        W1 = b.sb("W1", [128, 8, 1792], BF16)
        W2 = b.sb("W2", [128, 8, 1792], BF16)
        with b.scope():
            gat = self.load_gain("gat3", I["attn_norm_g"][0])
            stage = [b.sb(f"rst{i}", [128, 1792], F32) for i in range(2)]
            tmpw = [b.sb(f"rtw{i}", [128, 1792], F32) for i in range(2)]
            mur = self.bcast_row("mur", I["rwkv_mu"][0], 1792)
            for c in range(8):
                st = stage[c % 2]
                tw_ = tmpw[c % 2]
                b.dma("sp", st[:], I["w_in"][0][c * 128:(c + 1) * 128, RW0:RW0 + 1792], writes=[st])
                tt("dve", tw_[:], st[:], mur[:], ALU.mult, [st, mur], [tw_])
                b.op("act", lambda e: e.activation(out=W2[:, c, :], in_=tw_[:], func=AF.Copy, scale=gat[:, c:c + 1]), reads=[tw_, gat], writes=[W2])
                tt("pool", st[:], st[:], tw_[:], ALU.subtract, [st, tw_], [st])
                b.op("act", lambda e: e.activation(out=W1[:, c, :], in_=st[:], func=AF.Copy, scale=gat[:, c:c + 1]), reads=[st, gat], writes=[W1])

        def colvec(name, src, n):
            t = b.sb(name, [64, n], F32)
            b.dma("sp", t[:], src.rearrange("(c p) -> p c", p=64), writes=[t], allow_slow_non_contiguous=True)
            return t
        w0 = colvec("w0", I["rwkv_w0"][0], 8)
        a0 = colvec("a0", I["rwkv_a0"][0], 8)
        k_k = colvec("k_k", I["rwkv_k_k"][0], 8)
        k_a = colvec("k_a", I["rwkv_k_a"][0], 8)
        r_k = colvec("r_k", I["rwkv_r_k"][0].rearrange("h d -> (h d)"), 8)
        w2s = b.sb("w2s", [64, 512], F32)
        a2s = b.sb("a2s", [64, 512], F32)
        g2s = b.sb("g2s", [64, 2, 512], F32)
        b.dma("sp", w2s[:], I["rwkv_w2"][0], writes=[w2s])
        b.dma("sp", a2s[:], I["rwkv_a2"][0], writes=[a2s])
        b.dma("sp", g2s[:], I["rwkv_g2"][0].rearrange("(two l) f -> l two f", two=2), writes=[g2s])
        lng = b.sb("lng", [64, 512], F32)
        lnb = b.sb("lnb", [64, 512], F32)
        b.dma("sp", lng[:], I["rwkv_ln_g"][0].partition_broadcast(64), writes=[lng])
        b.dma("sp", lnb[:], I["rwkv_ln_b"][0].partition_broadcast(64), writes=[lnb])
        msk = b.sb("rmsk", [64, 3, 64], F32)
        b.dma("sp", msk[:], I["rwmask"], writes=[msk])
        rstm = b.sb("rstm", [64, 8 * TG], F32)
        b.dma("sp", rstm[:], I["rwreset"], writes=[rstm])
        ones = b.sb("ones64", [64, 64], F32)
        b.op("pool", lambda e: e.memset(ones[:], 1.0), writes=[ones])
        idf = self.identf
        Hst = b.sb("rH", [64, 2, 8, 64], F32)
        b.op("pool", lambda e: e.memset(Hst[:], 0.0), writes=[Hst])
        xt = [b.sb(f"rxt{i}", [128, D], F32) for i in range(1)] * 2
        junk = b.sb("rjunk", [128, D], BF16)
        ss = [b.sb(f"rss{i}", [128, 1], F32) for i in range(1)] * 2
        hb = [b.sb(f"rhb{i}", [128, D], BF16) for i in range(1)] * 2
        hT1 = [b.sb(f"rhT{i}", [128, 8, 128], BF16) for i in range(1)] * 2
        hTs = b.sb("rhTs", [128, 8, TG + 1], BF16)
        b.op("pool", lambda e: e.memset(hTs[:], 0.0), writes=[hTs])
        XL = b.sb("rXL", [64, 20, TG], F32)
        Vtm = b.sb("rVtm", [64, NCH, 512], F32)
        names = ["LW", "AS", "KKN", "BVc", "KP", "RK", "L", "EP", "EM", "BG", "KG"]
        T = {n: b.sb("r" + n, [64, 8, TG], F32) for n in names}
        T["NR"] = T["RK"]
        T["T1"] = T["BG"]
        T["KK"] = T["KG"]
        T["EX"] = T["L"]
        T["BT"] = T["LW"]
        T["KT"] = T["AS"]
        AR = b.sb("rAR", [64, 8, NCH, 2, 64], F32)
        BON = b.sb("rBON", [64, NCH * 8], F32)
        Ytm = b.sb("rYtm", [64, NCH, 8, 64], F32)
        sqv = b.sb("rsqv", [64, NCH, 8, 64], F32)
        st1 = b.sb("rst1", [64, NCH * 8], F32)
        st2 = b.sb("rst2", [64, NCH * 8], F32)
        TM4 = [b.sb(f"rTM{i}", [64, 4, 2, 64], F32) for i in range(2)]
        XM4 = [b.sb(f"rXM{i}", [64, 4, 4, 64], F32) for i in range(2)]
        AA4 = [b.sb(f"rAA{i}", [64, 4, 2, 64], F32) for i in range(2)]
        PP4 = [b.sb(f"rPP{i}", [64, 4, 64], F32) for i in range(2)]
        Xs4 = b.sb("rXs4", [64, 4, 64], F32)
        Us4 = b.sb("rUs4", [64, 4, 64], F32)
        Ht4 = b.sb("rHt4", [64, 4, 64], F32)
        OBb = b.sb("rOBb", [64, NCH, 512], BF16)
        obT = [b.sb(f"robT{i}", [128, 4, TG], BF16) for i in range(1)] * 2
        pt = b.ps("rpt", [128, 8, 128], BF16)
        pP = b.ps("rpP", [128, 512], F32)
        pA = b.ps("rpA", [128, 1024], F32)
        pB = b.ps("rpB", [128, 512], F32)
        pC = b.ps("rpC", [128, 512], F32)
        pD = b.ps("rpD", [128, 512], F32)
        pZ = b.ps("rpZ", [128, 512], F32)
        cnt = {}

        def nxt(k, lst):
            cnt[k] = cnt.get(k, 0) + 1
            return lst[cnt[k] % len(lst)]
        bc = lambda v: v[:].unsqueeze(2).to_broadcast([64, 8, TG])
        f2 = lambda t_: t_[:].rearrange("p h t -> p (h t)")
        c16 = lambda t_: t_[:].rearrange("p h (c t) -> p (h c) t", t=64)

        ngr = getattr(self, "nrg_limit", S // TG)
        for gi in range(ngr):
            q0 = gi * TG
            i = gi % 2
            self.make_hT(I["x"], gi, xt[i], junk, ss[i], hb[i], pt, hT1[i], self.ident)
            b.op("pool", lambda e: e.tensor_copy(out=hTs[:, :, 0:1], in_=hTs[:, :, TG:TG + 1]), reads=[hTs], writes=[hTs])
            b.op("pool", lambda e: e.tensor_copy(out=hTs[:, :, 1:TG + 1], in_=hT1[i][:]), reads=[hT1[i]], writes=[hTs])
            ftiles = list(range(0, 16)) + [24, 25, 26, 27]
            for q4 in range(5):
                for j in range(4):
                    fc = ftiles[q4 * 4 + j]
                    for c in range(8):
                        b.op("pe", lambda e: e.matmul(pP[0:64, j * TG:(j + 1) * TG], lhsT=W1[:, c, fc * 64:(fc + 1) * 64], rhs=hTs[:, c, 1:TG + 1], start=(c == 0), stop=False),
                             reads=[W1, hTs], writes=[pP])
                    for c in range(8):
                        b.op("pe", lambda e: e.matmul(pP[0:64, j * TG:(j + 1) * TG], lhsT=W2[:, c, fc * 64:(fc + 1) * 64], rhs=hTs[:, c, 0:TG], start=False, stop=(c == 7)),
                             reads=[W2, hTs], writes=[pP])
                b.op("act", lambda e: e.copy(out=XL[:, q4 * 4:(q4 + 1) * 4, :].rearrange("p a t -> p (a t)"), in_=pP[0:64, :]), reads=[pP], writes=[XL])
            for c_ in range(NCH):
                for c in range(8):
                    b.op("pe", lambda e: e.matmul(pP[0:64, :], lhsT=hTs[:, c, 1 + c_ * 64:1 + (c_ + 1) * 64], rhs=W1[:, c, 1024:1536], start=(c == 0), stop=False),
                         reads=[W1, hTs], writes=[pP])
                for c in range(8):
                    b.op("pe", lambda e: e.matmul(pP[0:64, :], lhsT=hTs[:, c, c_ * 64:(c_ + 1) * 64], rhs=W2[:, c, 1024:1536], start=False, stop=(c == 7)),
                         reads=[W2, hTs], writes=[pP])
                b.op("act", lambda e: e.copy(out=Vtm[:, c_, :], in_=pP[0:64, :]), reads=[pP], writes=[Vtm])
            R_ = XL[:, 0:8, :]
            K_ = XL[:, 8:16, :]
            b.op("act", lambda e: e.activation(out=XL[:, 16, :], in_=XL[:, 16, :], func=AF.Tanh), reads=[XL], writes=[XL])
            b.op("act", lambda e: e.activation(out=XL[:, 18:20, :], in_=XL[:, 18:20, :], func=AF.Sigmoid), reads=[XL], writes=[XL])
            for (ws_, src, bias_, dst) in [(w2s, 16, w0, "LW"), (a2s, 17, a0, "AS")]:
                for half in range(2):
                    for j in range(4):
                        h = half * 4 + j
                        b.op("pe", lambda e: e.matmul(pP[0:64, j * TG:(j + 1) * TG], lhsT=ws_[:, h * 64:(h + 1) * 64], rhs=XL[:, src, :], start=True, stop=True),
                             reads=[ws_, XL], writes=[pP])
                    for j in range(4):
                        h = half * 4 + j
                        b.op("act", lambda e: e.activation(out=T[dst][:, h, :], in_=pP[0:64, j * TG:(j + 1) * TG], func=AF.Sigmoid, bias=bias_[:, h:h + 1]),
                             reads=[pP, bias_], writes=[T[dst]])
            b.op("pool", lambda e: e.tensor_scalar_mul(out=f2(T["LW"]), in0=f2(T["LW"]), scalar1=-0.6065306597126334), reads=[T["LW"]], writes=[T["LW"]])
            tt("dve", T["KK"][:], K_, bc(k_k), ALU.mult, [XL, k_k], [T["KK"]])
            tt("pool", T["NR"][:], T["KK"][:], T["KK"][:], ALU.mult, [T["KK"]], [T["NR"]])
            for half in range(2):
                b.op("pe", lambda e: e.matmul(pP[0:64, :], lhsT=ones[:], rhs=T["NR"][:, half * 4:(half + 1) * 4, :].rearrange("p h t -> p (h t)"), start=True, stop=True),
                     reads=[ones, T["NR"]], writes=[pP])
                b.op("act", lambda e: e.activation(out=T["KKN"][:, half * 4:(half + 1) * 4, :].rearrange("p h t -> p (h t)"), in_=pP[0:64, :], func=AF.Sqrt),
                     reads=[pP], writes=[T["KKN"]])
            b.op("dve", lambda e: e.tensor_scalar_max(out=f2(T["KKN"]), in0=f2(T["KKN"]), scalar1=1e-12), reads=[T["KKN"]], writes=[T["KKN"]])
            b.op("dve", lambda e: e.reciprocal(out=f2(T["KKN"]), in_=f2(T["KKN"])), reads=[T["KKN"]], writes=[T["KKN"]])
            tt("dve", T["KKN"][:], T["KKN"][:], T["KK"][:], ALU.mult, [T["KKN"], T["KK"]], [T["KKN"]])
            tt("pool", T["BVc"][:], T["KKN"][:], T["AS"][:], ALU.mult, [T["KKN"], T["AS"]], [T["BVc"]])
            b.op("pool", lambda e: e.tensor_scalar_add(out=f2(T["T1"]), in0=f2(T["AS"]), scalar1=-1.0), reads=[T["AS"]], writes=[T["T1"]])
            tt("pool", T["T1"][:], T["T1"][:], bc(k_a), ALU.mult, [T["T1"], k_a], [T["T1"]])
            b.op("dve", lambda e: e.scalar_tensor_tensor(out=f2(T["KP"]), in0=f2(T["T1"]), scalar=1.0, in1=K_.rearrange("p h t -> p (h t)"), op0=ALU.add, op1=ALU.mult),
                 reads=[T["T1"], XL], writes=[T["KP"]])
            tt("pool", T["RK"][:], R_, T["KP"][:], ALU.mult, [XL, T["KP"]], [T["RK"]])
            tt("pool", T["RK"][:], T["RK"][:], bc(r_k), ALU.mult, [T["RK"], r_k], [T["RK"]])
            for c_ in range(NCH):
                for h in range(8):
                    b.op("pe", lambda e: e.matmul(pD[0:64, c_ * 8 + h:c_ * 8 + h + 1], lhsT=T["RK"][:, h, c_ * 64:(c_ + 1) * 64], rhs=ones[:, 0:1], start=True, stop=True),
                         reads=[T["RK"], ones], writes=[pD])
            b.op("act", lambda e: e.copy(out=BON[:], in_=pD[0:64, 0:NCH * 8]), reads=[pD], writes=[BON])
            b.op("dve", lambda e: e.tensor_tensor_scan(out=f2(T["L"]), data0=rstm[:], data1=f2(T["LW"]), initial=0.0, op0=ALU.mult, op1=ALU.add),
                 reads=[rstm, T["LW"]], writes=[T["L"]])
            b.op("act", lambda e: e.activation(out=f2(T["EP"]), in_=f2(T["L"]), func=AF.Exp), reads=[T["L"]], writes=[T["EP"]])
            b.op("act", lambda e: e.activation(out=f2(T["EM"]), in_=f2(T["L"]), func=AF.Exp, scale=-1.0), reads=[T["L"]], writes=[T["EM"]])
            tt("pool", T["L"][:], T["L"][:], T["LW"][:], ALU.subtract, [T["L"], T["LW"]], [T["L"]])
            b.op("act", lambda e: e.activation(out=f2(T["EX"]), in_=f2(T["L"]), func=AF.Exp), reads=[T["L"]], writes=[T["EX"]])
            ar0 = AR[:, :, :, 0, :].rearrange("p h c t -> p (h c) t")
            ar1 = AR[:, :, :, 1, :].rearrange("p h c t -> p (h c) t")
            b.op("dve", lambda e: e.scalar_tensor_tensor(out=ar0, in0=c16(T["KKN"]), scalar=-1.0, in1=c16(T["EX"]), op0=ALU.mult, op1=ALU.mult),
                 reads=[T["KKN"], T["EX"]], writes=[AR])
            tt("pool", ar1, R_.rearrange("p h (c t) -> p (h c) t", t=64), c16(T["EP"]), ALU.mult, [XL, T["EP"]], [AR])
            tt("dve", T["BT"][:], T["BVc"][:], T["EM"][:], ALU.mult, [T["BVc"], T["EM"]], [T["BT"]])
            tt("pool", T["KT"][:], T["KP"][:], T["EM"][:], ALU.mult, [T["KP"], T["EM"]], [T["KT"]])
            gC = c16(T["EP"])[:, :, 63:64].to_broadcast([64, 16, 64])
            tt("dve", c16(T["BG"]), c16(T["BT"]), gC, ALU.mult, [T["BT"], T["EP"]], [T["BG"]])
            tt("pool", c16(T["KG"]), c16(T["KT"]), gC, ALU.mult, [T["KT"], T["EP"]], [T["KG"]])
            for c_ in range(NCH):
                cs = slice(c_ * 64, (c_ + 1) * 64)
                cur = (gi * NCH + c_) % 2
                for hb_ in range(2):
                    heads = list(range(hb_ * 4, hb_ * 4 + 4))
                    for j, h in enumerate(heads):
                        b.op("pe", lambda e: e.transpose(out=pC[0:64, j * 128:j * 128 + 64], in_=T["BG"][:, h, cs], identity=idf[0:64, 0:64]), reads=[T["BG"], idf], writes=[pC])
                        b.op("pe", lambda e: e.transpose(out=pC[0:64, j * 128 + 64:(j + 1) * 128], in_=T["KG"][:, h, cs], identity=idf[0:64, 0:64]), reads=[T["KG"], idf], writes=[pC])
                    tm = nxt("tm", TM4)
                    b.op("act", lambda e: e.copy(out=tm[:].rearrange("p h a t -> p (h a t)"), in_=pC[0:64, 0:512]), reads=[pC], writes=[tm])
                    for j, h in enumerate(heads):
                        arc = AR[:, h, c_, :, :].rearrange("p a t -> p (a t)")
                        b.op("pe", lambda e: e.matmul(pA[0:64, j * 256:j * 256 + 128], lhsT=T["BT"][:, h, cs], rhs=arc, start=True, stop=True), reads=[T["BT"], AR], writes=[pA])
                        b.op("pe", lambda e: e.matmul(pA[0:64, j * 256 + 128:(j + 1) * 256], lhsT=T["KT"][:, h, cs], rhs=arc, start=True, stop=True), reads=[T["KT"], AR], writes=[pA])
                        b.op("pe", lambda e: e.matmul(pB[0:64, j * 64:(j + 1) * 64], lhsT=AR[:, h, c_, 0, :], rhs=T["BT"][:, h, cs], start=True, stop=True), reads=[T["BT"], AR], writes=[pB])
                    xm = nxt("xm", XM4)
                    tt("dve", xm[:].rearrange("p h (a m) t -> p (h a) m t", a=2), pA[0:64, :].rearrange("p (ha m t) -> p ha m t", m=2, t=64),
                       msk[:, None, 0:2, :].to_broadcast([64, 8, 2, 64]), ALU.mult, [pA, msk], [xm])
                    aa = nxt("aa", AA4)
                    b.op("pool", lambda e: e.tensor_copy(out=aa[:, :, 0, :], in_=xm[:, :, 0, :]), reads=[xm], writes=[aa])
                    tt("dve", aa[:, :, 1, :], pB[0:64, 0:256].rearrange("p (h t) -> p h t", t=64), msk[:, 2:3, :].to_broadcast([64, 4, 64]), ALU.mult, [pB, msk], [aa])
                    P_ = nxt("pp4", PP4)
                    tt("pool", P_[:], xm[:, :, 0, :], idf[0:64, None, 0:64].to_broadcast([64, 4, 64]), ALU.add, [xm, idf], [P_])
                    for step in range(5):
                        for j in range(4):
                            b.op("pe", lambda e: e.matmul(pD[0:64, j * 128:j * 128 + 64], lhsT=aa[:, j, 1, :], rhs=aa[:, j, 0, :], start=True, stop=True), reads=[aa], writes=[pD])
                            b.op("pe", lambda e: e.matmul(pD[0:64, j * 128 + 64:(j + 1) * 128], lhsT=aa[:, j, 0, :], rhs=aa[:, j, 1, :], start=True, stop=True), reads=[aa], writes=[pD])
                        aa2 = nxt("aa", AA4)
                        b.op("act", lambda e: e.copy(out=aa2[:].rearrange("p h a t -> p (h a t)"), in_=pD[0:64, :]), reads=[pD], writes=[aa2])
                        for j in range(4):
                            b.op("pe", lambda e: e.matmul(pB[0:64, 256 + j * 64:256 + (j + 1) * 64], lhsT=aa2[:, j, 1, :], rhs=P_[:, j, :], start=True, stop=True), reads=[aa2, P_], writes=[pB])
                        P2 = nxt("pp4", PP4)
                        tt("dve", P2[:], pB[0:64, 256:512].rearrange("p (h t) -> p h t", t=64), P_[:], ALU.add, [pB, P_], [P2])
                        aa, P_ = aa2, P2
                    for j, h in enumerate(heads):
                        b.op("pe", lambda e: e.matmul(pZ[0:64, j * 64:(j + 1) * 64], lhsT=xm[:, j, 2, :], rhs=Vtm[:, c_, h * 64:(h + 1) * 64], start=True, stop=False), reads=[xm, Vtm], writes=[pZ])
                        b.op("pe", lambda e: e.matmul(pZ[0:64, j * 64:(j + 1) * 64], lhsT=AR[:, h, c_, 0, :], rhs=Hst[:, cur, h, :], start=False, stop=True), reads=[AR, Hst], writes=[pZ])
                    b.op("act", lambda e: e.copy(out=Xs4[:].rearrange("p h t -> p (h t)"), in_=pZ[0:64, 0:256]), reads=[pZ], writes=[Xs4])
                    for j in range(4):
                        b.op("pe", lambda e: e.matmul(pZ[0:64, 256 + j * 64:256 + (j + 1) * 64], lhsT=P_[:, j, :], rhs=Xs4[:, j, :], start=True, stop=True), reads=[P_, Xs4], writes=[pZ])
                    b.op("act", lambda e: e.copy(out=Us4[:].rearrange("p h t -> p (h t)"), in_=pZ[0:64, 256:512]), reads=[pZ], writes=[Us4])
                    for j, h in enumerate(heads):
                        o = slice(j * 64, (j + 1) * 64)
                        vh = Vtm[:, c_, h * 64:(h + 1) * 64]
                        b.op("pe", lambda e: e.matmul(pZ[0:64, o], lhsT=AR[:, h, c_, 1, :], rhs=Hst[:, cur, h, :], start=True, stop=False), reads=[AR, Hst], writes=[pZ])
                        b.op("pe", lambda e: e.matmul(pZ[0:64, o], lhsT=xm[:, j, 1, :], rhs=Us4[:, j, :], start=False, stop=False), reads=[xm, Us4], writes=[pZ])
                        b.op("pe", lambda e: e.matmul(pZ[0:64, o], lhsT=xm[:, j, 3, :], rhs=vh, start=False, stop=True), reads=[xm, Vtm], writes=[pZ])
                    for j, h in enumerate(heads):
                        o = slice(256 + j * 64, 256 + (j + 1) * 64)
                        vh = Vtm[:, c_, h * 64:(h + 1) * 64]
                        b.op("pe", lambda e: e.matmul(pZ[0:64, o], lhsT=tm[:, j, 0, :], rhs=Us4[:, j, :], start=True, stop=False), reads=[tm, Us4], writes=[pZ])
                        b.op("pe", lambda e: e.matmul(pZ[0:64, o], lhsT=tm[:, j, 1, :], rhs=vh, start=False, stop=True), reads=[tm, Vtm], writes=[pZ])
                    b.op("act", lambda e: e.copy(out=Ytm[:, c_, hb_ * 4:(hb_ + 1) * 4, :].rearrange("p h t -> p (h t)"), in_=pZ[0:64, 0:256]), reads=[pZ], writes=[Ytm])
                    gH = T["EP"][:, hb_ * 4:(hb_ + 1) * 4, c_ * 64 + 63:c_ * 64 + 64].to_broadcast([64, 4, 64])
                    tt("pool", Ht4[:], Hst[:, cur, hb_ * 4:(hb_ + 1) * 4, :], gH, ALU.mult, [Hst, T["EP"]], [Ht4])
                    tt("dve", Hst[:, 1 - cur, hb_ * 4:(hb_ + 1) * 4, :], pZ[0:64, 256:512].rearrange("p (h t) -> p h t", t=64), Ht4[:], ALU.add, [pZ, Ht4], [Hst])
            Y3 = Ytm[:].rearrange("p c h i -> p (c h) i")
            S3 = sqv[:].rearrange("p c h i -> p (c h) i")
            b.op("dve", lambda e: e.tensor_reduce(out=st1[:], in_=Y3, axis=AX.X, op=ALU.add), reads=[Ytm], writes=[st1])
            b.op("pool", lambda e: e.tensor_scalar_mul(out=st1[:], in0=st1[:], scalar1=1.0 / 64), reads=[st1], writes=[st1])
            tt("dve", Y3, Y3, st1[:].unsqueeze(2).to_broadcast([64, NCH * 8, 64]), ALU.subtract, [Ytm, st1], [Ytm])
            tt("pool", S3, Y3, Y3, ALU.mult, [Ytm], [sqv])
            b.op("dve", lambda e: e.tensor_reduce(out=st2[:], in_=S3, axis=AX.X, op=ALU.add), reads=[sqv], writes=[st2])
            b.op("act", lambda e: e.activation(out=st2[:], in_=st2[:], func=AF.Sqrt, scale=1.0 / 64, bias=64e-5), reads=[st2], writes=[st2])
            b.op("dve", lambda e: e.reciprocal(out=st2[:], in_=st2[:]), reads=[st2], writes=[st2])
            tt("dve", Y3, Y3, st2[:].unsqueeze(2).to_broadcast([64, NCH * 8, 64]), ALU.mult, [Ytm, st2], [Ytm])
            lg = lng[:].rearrange("p (h i) -> p h i", i=64)[:, None, :, :].to_broadcast([64, NCH, 8, 64])
            lb = lnb[:].rearrange("p (h i) -> p h i", i=64)[:, None, :, :].to_broadcast([64, NCH, 8, 64])
            tt("pool", Ytm[:], Ytm[:], lg, ALU.mult, [Ytm, lng], [Ytm])
            tt("dve", Ytm[:], Ytm[:], lb, ALU.add, [Ytm, lnb], [Ytm])
            V3 = Vtm[:].rearrange("p c (h i) -> p (c h) i", i=64)
            tt("pool", S3, V3, BON[:].unsqueeze(2).to_broadcast([64, NCH * 8, 64]), ALU.mult, [Vtm, BON], [sqv])
            tt("dve", Y3, Y3, S3, ALU.add, [Ytm, sqv], [Ytm])
            for c_ in range(NCH):
                for two in range(2):
                    b.op("pe", lambda e: e.matmul(pP[0:64, :], lhsT=XL[:, 18 + two, c_ * 64:(c_ + 1) * 64], rhs=g2s[:, two, :], start=(two == 0), stop=(two == 1)),
                         reads=[XL, g2s], writes=[pP])
                tt("dve", OBb[:, c_, :], Ytm[:, c_, :, :].rearrange("p h i -> p (h i)"), pP[0:64, :], ALU.mult, [Ytm, pP], [OBb])
                for k4 in range(4):
                    b.op("pe", lambda e: e.transpose(out=pt[:, k4, c_ * 64:(c_ + 1) * 64], in_=OBb[:, c_, k4 * 128:(k4 + 1) * 128], identity=self.ident[0:64, 0:64]),
                         reads=[OBb, self.ident], writes=[pt])
            ot = obT[gi % 2]
            b.op("act", lambda e: e.copy(out=ot[:], in_=pt[:, 0:4, :]), reads=[pt], writes=[ot])
            b.dma("pool", self.obT_d[:, :, q0:q0 + TG].rearrange("c p t -> p c t"), ot[:], reads=[ot], writes=[self.obT_d])
        if "rwkv" in self.debug:
            d = self.dbg_out("obT", [4, 128, S], BF16)
            b.dma("pool", d, self.obT_d[:], reads=[self.obT_d])


Prog.phase_rwkv2 = _phase_rwkv2
```

```python
import contextlib
import numpy as np
import ml_dtypes
import concourse.bass as bass
import concourse.mybir as mybir
from concourse.bass_utils import run_bass_kernel_spmd

F32 = mybir.dt.float32
BF16 = mybir.dt.bfloat16
AF = mybir.ActivationFunctionType
ALU = mybir.AluOpType
AX = mybir.AxisListType

S = 4096
D = 1024
NT = S // 128
IN_WIDTH = 5144
RW0 = 1304
GA0 = 3096
GB0 = 4120
DFF = 2816
RMS_EPS = 1e-6


class Buf:
    def __init__(self, t, name):
        self.t = t
        self.name = name
        self.w = None
        self.r = {}
        self.psum = False

    def __getitem__(self, idx):
        return self.t[idx]


class Builder:
    SEM_ROLL = 30000

    def __init__(self, nc):
        self.nc = nc
        self.stack = contextlib.ExitStack()
        self.root = self.stack
        self.eng = {"pe": nc.tensor, "act": nc.scalar, "dve": nc.vector,
                    "pool": nc.gpsimd, "sp": nc.sync}
        self.sem = {}
        self.cnt = {}
        self.seen = {e: {} for e in self.eng}
        self.nsem = 0
        self.lanes = {}
        self.lane_rr = {}
        self.last_tok = {}
        for e in self.eng:
            self._roll(e)

    def newsem(self, name):
        self.nsem += 1
        return self.root.enter_context(self.nc.semaphore(f"{name}_{self.nsem}"))

    def sb(self, name, shape, dt=F32):
        self.nsem += 1
        name = f"sb{self.nsem}_{name}"
        return Buf(self.stack.enter_context(self.nc.sbuf_tensor(name, list(shape), dt)), name)

    def ps(self, name, shape, dt=F32):
        self.nsem += 1
        name = f"ps{self.nsem}_{name}"
        bf = Buf(self.stack.enter_context(self.nc.psum_tensor(name, list(shape), dt)), name)
        bf.psum = True
        return bf

    def dram(self, name, shape, dt=F32, kind="Internal"):
        return Buf(self.nc.dram_tensor(name, list(shape), dt, kind=kind), name)

    def _roll(self, e):
        self.sem[e] = self.newsem("s" + e)
        self.cnt[e] = 0

    def _wait(self, e, tok):
        sem, val = tok
        k = id(sem)
        if self.seen[e].get(k, 0) < val:
            self.eng[e].wait_ge(sem, val)
            self.seen[e][k] = val

    def _deps(self, e, reads, writes):
        for b in reads:
            if b.w is not None:
                we, tok = b.w
                self._wait(e, tok)
            if b.psum:
                for re_, tok in b.r.items():
                    if re_ != e:
                        self._wait(e, tok)
        for b in writes:
            if b.w is not None:
                we, tok = b.w
                if we != e:
                    self._wait(e, tok)
            for re_, tok in b.r.items():
                if re_ != e:
                    self._wait(e, tok)

    def op(self, e, fn, reads=(), writes=()):
        if self.cnt[e] >= self.SEM_ROLL:
            self._roll(e)
        self._deps(e, reads, writes)
        ins = fn(self.eng[e])
        self.cnt[e] += 1
        tok = (self.sem[e], self.cnt[e])
        ins.then_inc(self.sem[e], 1)
        self.last_tok[e] = tok
        for b in reads:
            b.r[e] = tok
        for b in writes:
            b.w = (e, tok)
            b.r = {}
        return tok

    def dma(self, q, out, in_, reads=(), writes=(), nlanes=6, **kw):
        if q not in self.lanes:
            self.lanes[q] = [[self.newsem("l" + q), 0] for _ in range(nlanes)]
            self.lane_rr[q] = 0
        li = self.lane_rr[q]
        self.lane_rr[q] = (li + 1) % len(self.lanes[q])
        lane = self.lanes[q][li]
        if lane[1] >= 1800:
            self._wait(q, (lane[0], 16 * lane[1]))
            lane[0] = self.newsem("l" + q)
            lane[1] = 0
        if lane[1] > 0:
            self._wait(q, (lane[0], 16 * lane[1]))
        self._deps_dma(q, reads, writes)
        ins = self.eng[q].dma_start(out=out, in_=in_, **kw)
        lane[1] += 1
        tok = (lane[0], 16 * lane[1])
        ins.then_inc(lane[0], 16)
        key = "dma_" + q + str(li)
        for b in reads:
            b.r[key] = tok
        for b in writes:
            b.w = (key, tok)
            b.r = {}
        return tok

    def _deps_dma(self, q, reads, writes):
        for b in reads:
            if b.w is not None:
                self._wait(q, b.w[1])
        for b in writes:
            if b.w is not None:
                self._wait(q, b.w[1])
            for re_, tok in b.r.items():
                self._wait(q, tok)

    def barrier(self):
        toks = list(self.last_tok.values())
        for q, lanes in self.lanes.items():
            for lane in lanes:
                if lane[1] > 0:
                    toks.append((lane[0], 16 * lane[1]))
        for e in self.eng:
            for tok in toks:
                self._wait(e, tok)

    def wait_all_on(self, e):
        toks = list(self.last_tok.values())
        for q, lanes in self.lanes.items():
            for lane in lanes:
                if lane[1] > 0:
                    toks.append((lane[0], 16 * lane[1]))
        for tok in toks:
            self._wait(e, tok)

    @contextlib.contextmanager
    def scope(self):
        old = self.stack
        self.stack = contextlib.ExitStack()
        try:
            yield
            self.barrier()
        finally:
            self.stack.close()
            self.stack = old

    def close(self):
        self.stack.close()


NEG = -30000.0


def _bucket(dist):
    n = np.maximum(dist, 0)
    ratio = np.log(np.maximum(n, 1).astype(np.float32) / np.float32(16.0)) / np.float32(np.log(8.0))
    large = np.minimum(16 + (ratio * 16).astype(np.int32), 31)
    return np.where(n < 16, n, large)


def host_consts(rel_bias):
    rel = np.asarray(rel_bias, np.float32)
    c = {}
    c["ident"] = np.eye(128, dtype=np.float32).astype(ml_dtypes.bfloat16)
    c["identf"] = np.eye(128, dtype=np.float32)
    kp = np.arange(128)[:, None]
    cc = np.arange(640)[None, :]
    dist = cc - kp
    bt = rel[_bucket(dist)]
    tw = np.where(((dist >= 0) & (dist < 512))[..., None], bt, np.float32(NEG))
    ts = np.where((dist >= 0)[..., None], bt, np.float32(NEG))
    c["tw"] = np.ascontiguousarray(tw.transpose(0, 2, 1)).astype(np.float32)
    c["ts"] = np.ascontiguousarray(ts.transpose(0, 2, 1)).astype(np.float32)
    cidx = np.arange(256)[:, None]
    qidx = np.arange(S)[None, :]
    dc = qidx - 16 * cidx - 31
    bcg = rel[_bucket(dc)]
    ok = (dc >= 0) & (cidx < 255)
    bc = np.where(ok[..., None], bcg, np.float32(NEG))
    c["biasc"] = np.ascontiguousarray(bc.transpose(2, 0, 1)).reshape(8, 2, 128, S).astype(np.float32)
    A = np.zeros((256, 64), np.float32)
    Wt = (1, 2, 2, 2, 1)
    for ci in range(255):
        for j in range(64):
            o = ci + 1 - 4 * j
            if 0 <= o <= 4:
                A[ci, j] = Wt[o]
    c["amat"] = A.reshape(2, 128, 64)
    E = np.zeros((64, S), np.float32)
    E[np.arange(S) // 64, np.arange(S)] = 1.0
    c["emat"] = E.astype(ml_dtypes.bfloat16)
    qp = np.arange(128)[:, None, None]
    qt = np.arange(32)[None, :, None]
    j = np.arange(64)[None, None, :]
    cur = (128 * qt + qp) // 64
    cand = (j >= 1) & (j <= cur - 2)
    c["candneg"] = np.where(cand, 0.0, -1e9).astype(np.float32)
    c["fz"] = ((j == 0) | (j == cur) | (j == cur - 1)).astype(np.float32)
    tri = np.triu(np.ones((64, 64), np.float32))
    c["rwmask"] = np.ascontiguousarray(np.stack([np.triu(np.ones((64, 64), np.float32), 1), tri, np.tril(np.ones((64, 64), np.float32), -1)], axis=1))
    rr = np.ones((64, 1024), np.float32)
    rr[:, ::64] = 0.0
    c["rwreset"] = rr
    c["b31"] = np.ascontiguousarray(np.broadcast_to(rel[31][None, :], (128, 8))).astype(np.float32)
    return c


CONST_SPECS = {
    "ident": ([128, 128], BF16), "identf": ([128, 128], F32),
    "tw": ([128, 8, 640], F32), "ts": ([128, 8, 640], F32),
    "biasc": ([8, 2, 128, S], F32), "amat": ([2, 128, 64], F32),
    "emat": ([64, S], BF16), "candneg": ([128, 32, 64], F32), "fz": ([128, 32, 64], F32),
    "b31": ([128, 8], F32), "rwmask": ([64, 3, 64], F32), "rwreset": ([64, 1024], F32),
}

W_SPECS = {
    "x": [S, D], "attn_norm_g": [1, D], "w_in": [1, D, IN_WIDTH], "q_norm_g": [1, 64], "k_norm_g": [1, 64],
    "cmp_pe_k": [1, 32, 64], "cmp_w1_k": [1, 2048, 256], "cmp_w2_k": [1, 256, 64],
    "cmp_pe_v": [1, 32, 64], "cmp_w1_v": [1, 2048, 256], "cmp_w2_v": [1, 256, 64],
    "rwkv_mu": [1, 1792], "rwkv_w0": [1, 512], "rwkv_w2": [1, 64, 512], "rwkv_a0": [1, 512],
    "rwkv_a2": [1, 64, 512], "rwkv_g2": [1, 128, 512], "rwkv_k_k": [1, 512], "rwkv_k_a": [1, 512],
    "rwkv_r_k": [1, 8, 64], "rwkv_ln_g": [1, 512], "rwkv_ln_b": [1, 512],
    "w_proj_a": [1, 512, D], "w_proj_b": [1, 512, D], "w_out": [1, D, D], "ffn_norm_g": [1, D],
    "w_up": [1, D, 2 * DFF], "conv_w": [1, 3, 2 * DFF], "conv_b": [1, 2 * DFF], "w_down": [1, DFF, D],
}


class Prog:
    def __init__(self, debug=()):
        self.debug = set(debug)
        nc = bass.Bass("TRN2", target_bir_lowering=False)
        self.nc = nc
        self.inp = {}
        for k, shp in W_SPECS.items():
            self.inp[k] = nc.dram_tensor(k, list(shp), F32, kind="ExternalInput").ap()
        for k, (shp, dt) in CONST_SPECS.items():
            self.inp[k] = nc.dram_tensor(k, list(shp), dt, kind="ExternalInput").ap()
        self.out = nc.dram_tensor("out", [S, D], F32, kind="ExternalOutput").ap()
        self.dbg = {}
        self.b = Builder(nc)

    def dbg_out(self, name, shape, dt=F32):
        t = self.nc.dram_tensor("dbg_" + name, list(shape), dt, kind="ExternalOutput").ap()
        self.dbg[name] = t
        return t

    def load_weight(self, dst, src, ncols, gvec=None, kch=8, stage=None, eng="act"):
        b = self.b
        for c in range(kch):
            st = stage[c % len(stage)]
            b.dma("sp", st[:, :ncols], src[c * 128:(c + 1) * 128, :], writes=[st])
            if gvec is not None:
                b.op(eng, lambda e: e.activation(out=dst[:, c, :], in_=st[:, :ncols], func=AF.Copy, scale=gvec[:, c:c + 1])
                     if eng == "act" else e.tensor_scalar_mul(out=dst[:, c, :], in0=st[:, :ncols], scalar1=gvec[:, c:c + 1]),
                     reads=[st, gvec], writes=[dst])
            else:
                b.op(eng, lambda e: e.copy(out=dst[:, c, :], in_=st[:, :ncols]) if eng == "act"
                     else e.tensor_copy(out=dst[:, c, :], in_=st[:, :ncols]), reads=[st], writes=[dst])

    def load_gain(self, name, src_vec, kch=8):
        b = self.b
        g = b.sb(name, [128, kch], F32)
        b.dma("sp", g[:], src_vec.rearrange("(c p) -> p c", p=128), writes=[g], allow_slow_non_contiguous=True)
        return g

    def bcast_row(self, name, src_row, n):
        b = self.b
        t = b.sb(name, [128, n], F32)
        b.dma("sp", t[:], src_row.partition_broadcast(128), writes=[t])
        return t

    def make_hT(self, x_ap, t, xt, junk, ss, hb, pt, hT, ident, hT_ap=None):
        b = self.b
        b.dma("sp", xt[:], x_ap[t * 128:(t + 1) * 128, :], writes=[xt])
        b.op("act", lambda e: e.activation(out=junk[:], in_=xt[:], func=AF.Square, accum_out=ss[:]), reads=[xt], writes=[junk, ss])
        b.op("act", lambda e: e.activation(out=ss[:], in_=ss[:], func=AF.Sqrt, scale=1.0 / D, bias=RMS_EPS), reads=[ss], writes=[ss])
        b.op("dve", lambda e: e.reciprocal(out=ss[:], in_=ss[:]), reads=[ss], writes=[ss])
        b.op("dve", lambda e: e.tensor_scalar_mul(out=hb[:], in0=xt[:], scalar1=ss[:]), reads=[xt, ss], writes=[hb])
        for c in range(8):
            b.op("pe", lambda e: e.transpose(out=pt[:, c, :], in_=hb[:, c * 128:(c + 1) * 128], identity=ident[:]),
                 reads=[hb, ident], writes=[pt])
        b.op("act", lambda e: e.copy(out=(hT[:] if hT_ap is None else hT_ap), in_=pt[:]), reads=[pt], writes=[hT])

    def alloc_root(self):
        b = self.b
        I = self.inp
        self.ident = b.sb("ident", [128, 128], BF16)
        b.dma("sp", self.ident[:], I["ident"], writes=[self.ident])
        self.identf = b.sb("identf", [128, 128], F32)
        b.dma("sp", self.identf[:], I["identf"], writes=[self.identf])

    def alloc_persistent(self):
        b = self.b
        I = self.inp
        if not hasattr(self, "ident"):
            self.alloc_root()
        self.ksE = b.sb("ksE", [128, 2, S], BF16)
        self.kwT = b.sb("kwT", [64, 2, S], BF16)
        self.vaug_s = b.sb("vaug_s", [128, NT, 2, 65], BF16)
        self.vaug_w = b.sb("vaug_w", [128, NT, 2, 65], BF16)
        self.gts = b.sb("gts", [128, NT, 24], F32)
        self.kcT = b.sb("kcT", [64, 2, 256], BF16)
        self.vcA = b.sb("vcA", [128, 2, 2, 129], F32)
        self.qT_d = b.dram("qT_d", [8, 64, S], BF16)
        self.oaT_d = b.dram("oaT_d", [4, 128, S], BF16)
        self.obT_d = b.dram("obT_d", [4, 128, S], BF16)
        for g in range(2):
            b.dma("sp", self.ksE[64:128, g, :], I["emat"], writes=[self.ksE])
        b.op("pool", lambda e: e.memset(self.vaug_s[:, :, :, 64:65], 1.0), writes=[self.vaug_s])
        b.op("pool", lambda e: e.memset(self.vaug_w[:, :, :, 64:65], 1.0), writes=[self.vaug_w])
        b.op("pool", lambda e: e.memset(self.vcA[:, :, :, 64:65], 1.0), writes=[self.vcA])
        for g in range(2):
            for ct in range(2):
                b.dma("sp", self.vcA[:, g, ct, 65:129], I["amat"][ct], writes=[self.vcA])

    def phase_nsa_proj(self):
        b = self.b
        I = self.inp
        with b.scope():
            gat = self.load_gain("gat", I["attn_norm_g"][0])
            wn = b.sb("wn", [128, 8, RW0], BF16)
            stage = [b.sb(f"wst{i}", [128, RW0], F32) for i in range(2)]
            self.load_weight(wn, I["w_in"][0][:, 0:RW0], RW0, gvec=gat, stage=stage)
            gq = self.bcast_row("gq", I["q_norm_g"][0], 64)
            gk = self.bcast_row("gk", I["k_norm_g"][0], 64)
            gq_rep = b.sb("gq_rep", [128, 8, 64], F32)
            gk_rep = b.sb("gk_rep", [128, 2, 64], F32)
            b.op("act", lambda e: e.activation(out=gq_rep[:], in_=gq[:, None, :].to_broadcast([128, 8, 64]), func=AF.Copy, scale=0.125),
                 reads=[gq], writes=[gq_rep])
            b.op("act", lambda e: e.activation(out=gk_rep[:], in_=gk[:, None, :].to_broadcast([128, 2, 64]), func=AF.Copy, scale=1.0),
                 reads=[gk], writes=[gk_rep])
            if getattr(self, 'stop_at', 99) <= 0:
                return
            kcdup = b.sb("kcdup", [128, 2, S + 1], BF16)
            vcdup = b.sb("vcdup", [128, 2, S + 1], BF16)
            xt = [b.sb(f"xt{i}", [128, D], F32) for i in range(2)]
            junk = b.sb("junk", [128, D], BF16)
            ss = [b.sb(f"ss{i}", [128, 1], F32) for i in range(2)]
            hb = [b.sb(f"hb{i}", [128, D], BF16) for i in range(2)]
            hT = [b.sb(f"hT{i}", [128, 8, 128], BF16) for i in range(2)]
            sq = b.sb("sq", [128, 12, 64], F32)
            ssq = b.sb("ssq", [128, 12], F32)
            tmpq = b.sb("tmpq", [128, 8, 64], F32)
            tmpk = b.sb("tmpk", [128, 4, 64], F32)
            qb = b.sb("qb", [128, 512], BF16)
            kb = b.sb("kb", [128, 4, 64], BF16)
            cb = b.sb("cb", [128, 4, 2, 64], BF16)
            qst = [b.sb(f"qst{i}", [64, 8, 128], BF16) for i in range(2)]
            pt = b.ps("pt", [128, 8, 128], BF16)
            pm = [b.ps(f"pm{i}", [128, 512], F32) for i in range(3)]
            ptq = b.ps("ptq", [128, 8, 128], BF16)
            ptk = b.ps("ptk", [128, 8, 128], BF16)
            colgroups = [(0, 512), (512, 1024), (1024, RW0)]
            for t in range(getattr(self, 'nt_limit', NT)):
                i = t % 2
                self.make_hT(I["x"], t, xt[i], junk, ss[i], hb[i], pt, hT[i], self.ident)
                for n, (c0, c1) in enumerate(colgroups):
                    for c in range(8):
                        b.op("pe", lambda e: e.matmul(pm[n][:, :c1 - c0], lhsT=hT[i][:, c, :], rhs=wn[:, c, c0:c1],
                                                      start=(c == 0), stop=(c == 7)), reads=[hT[i], wn], writes=[pm[n]])
                if getattr(self, 'stop_at', 99) <= 1:
                    continue
                b.op("act", lambda e: e.activation(out=sq[:, 0:8, :], in_=pm[0][:, 0:512].rearrange("p (h d) -> p h d", d=64), func=AF.Square),
                     reads=[pm[0]], writes=[sq])
                b.op("act", lambda e: e.activation(out=sq[:, 8:10, :], in_=pm[1][:, 256:384].rearrange("p (h d) -> p h d", d=64), func=AF.Square),
                     reads=[pm[1]], writes=[sq])
                b.op("act", lambda e: e.activation(out=sq[:, 10:12, :], in_=pm[2][:, 0:128].rearrange("p (h d) -> p h d", d=64), func=AF.Square),
                     reads=[pm[2]], writes=[sq])
                b.op("dve", lambda e: e.tensor_reduce(out=ssq[:], in_=sq[:], axis=AX.X, op=ALU.add), reads=[sq], writes=[ssq])
                b.op("act", lambda e: e.activation(out=ssq[:], in_=ssq[:], func=AF.Sqrt, scale=1.0 / 64, bias=RMS_EPS), reads=[ssq], writes=[ssq])
                b.op("dve", lambda e: e.reciprocal(out=ssq[:], in_=ssq[:]), reads=[ssq], writes=[ssq])
                if getattr(self, 'stop_at', 99) <= 2:
                    continue
                b.op("dve", lambda e: e.tensor_tensor(out=tmpq[:], in0=pm[0][:, 0:512].rearrange("p (h d) -> p h d", d=64),
                                                      in1=ssq[:, 0:8].unsqueeze(2).to_broadcast([128, 8, 64]), op=ALU.mult),
                     reads=[pm[0], ssq], writes=[tmpq])
                b.op("pool", lambda e: e.tensor_tensor(out=qb[:].rearrange("p (h d) -> p h d", d=64), in0=tmpq[:], in1=gq_rep[:], op=ALU.mult),
                     reads=[tmpq, gq_rep], writes=[qb])
                for h in range(8):
                    b.op("pe", lambda e: e.transpose(out=ptq[0:64, h, :], in_=qb[:, h * 64:(h + 1) * 64], identity=self.ident[:]),
                         reads=[qb, self.ident], writes=[ptq])
                b.op("act", lambda e: e.copy(out=qst[i][:], in_=ptq[0:64, :, :]), reads=[ptq], writes=[qst[i]])
                b.dma("pool", self.qT_d[:, :, t * 128:(t + 1) * 128].rearrange("h d t -> d h t"), qst[i][:], reads=[qst[i]], writes=[self.qT_d])
                if getattr(self, 'stop_at', 99) <= 3:
                    continue
                b.op("dve", lambda e: e.tensor_tensor(out=tmpk[:, 0:2, :], in0=pm[1][:, 256:384].rearrange("p (h d) -> p h d", d=64),
                                                      in1=ssq[:, 8:10].unsqueeze(2).to_broadcast([128, 2, 64]), op=ALU.mult),
                     reads=[pm[1], ssq], writes=[tmpk])
                b.op("dve", lambda e: e.tensor_tensor(out=tmpk[:, 2:4, :], in0=pm[2][:, 0:128].rearrange("p (h d) -> p h d", d=64),
                                                      in1=ssq[:, 10:12].unsqueeze(2).to_broadcast([128, 2, 64]), op=ALU.mult),
                     reads=[pm[2], ssq], writes=[tmpk])
                b.op("pool", lambda e: e.tensor_tensor(out=kb[:].rearrange("p (a g) d -> p a g d", a=2), in0=tmpk[:].rearrange("p (a g) d -> p a g d", a=2),
                                                       in1=gk_rep[:, None, :, :].to_broadcast([128, 2, 2, 64]), op=ALU.mult),
                     reads=[tmpk, gk_rep], writes=[kb])
                for j in range(4):
                    b.op("pe", lambda e: e.transpose(out=ptk[0:64, j, :], in_=kb[:, j, :], identity=self.ident[:]),
                         reads=[kb, self.ident], writes=[ptk])
                if getattr(self, 'stop_at', 99) <= 4:
                    continue
                for du in range(2):
                    b.op("act", lambda e: e.copy(out=cb[:, :, du, :], in_=pm[1][:, 0:256].rearrange("p (a d) -> p a d", d=64)),
                         reads=[pm[1]], writes=[cb])
                for j in range(4):
                    b.op("pe", lambda e: e.transpose(out=ptk[:, 4 + j, :], in_=cb[:, j, :, :].rearrange("p a d -> p (a d)"), identity=self.ident[:]),
                         reads=[cb, self.ident], writes=[ptk])
                c0 = t * 128
                b.op("dve", lambda e: e.tensor_copy(out=self.ksE[0:64, :, c0:c0 + 128], in_=ptk[0:64, 0:2, :]), reads=[ptk], writes=[self.ksE])
                b.op("dve", lambda e: e.tensor_copy(out=self.kwT[0:64, :, c0:c0 + 128], in_=ptk[0:64, 2:4, :]), reads=[ptk], writes=[self.kwT])
                b.op("act", lambda e: e.copy(out=kcdup[0:64, :, 1 + c0:1 + c0 + 128], in_=ptk[0:64, 4:6, :]), reads=[ptk], writes=[kcdup])
                b.op("act", lambda e: e.copy(out=kcdup[64:128, :, c0:c0 + 128], in_=ptk[64:128, 4:6, :]), reads=[ptk], writes=[kcdup])
                b.op("dve", lambda e: e.tensor_copy(out=vcdup[0:64, :, 1 + c0:1 + c0 + 128], in_=ptk[0:64, 6:8, :]), reads=[ptk], writes=[vcdup])
                b.op("dve", lambda e: e.tensor_copy(out=vcdup[64:128, :, c0:c0 + 128], in_=ptk[64:128, 6:8, :]), reads=[ptk], writes=[vcdup])
                if getattr(self, 'stop_at', 99) <= 5:
                    continue
                b.op("act", lambda e: e.copy(out=self.vaug_s[:, t, :, 0:64], in_=pm[1][:, 384:512].rearrange("p (g d) -> p g d", d=64)),
                     reads=[pm[1]], writes=[self.vaug_s])
                b.op("act", lambda e: e.copy(out=self.vaug_w[:, t, :, 0:64], in_=pm[2][:, 128:256].rearrange("p (g d) -> p g d", d=64)),
                     reads=[pm[2]], writes=[self.vaug_w])
                b.op("act", lambda e: e.activation(out=self.gts[:, t, :], in_=pm[2][:, 256:280], func=AF.Sigmoid), reads=[pm[2]], writes=[self.gts])
            if "nsa_proj" in self.debug:
                d = self.dbg_out("ksE", [128, 2, S], BF16)
                b.dma("pool", d, self.ksE[:], reads=[self.ksE])
                d = self.dbg_out("kcdup", [128, 2, S + 1], BF16)
                b.dma("pool", d, kcdup[:], reads=[kcdup])
                d = self.dbg_out("vaug_w", [128, NT, 2, 65], BF16)
                b.dma("pool", d, self.vaug_w[:], reads=[self.vaug_w])
                d = self.dbg_out("gts", [128, NT, 24], F32)
                b.dma("pool", d, self.gts[:], reads=[self.gts])
            if not getattr(self, 'skip_compress', False):
                self.compress(kcdup, vcdup, gk_rep, [pm[0], pm[1]], pm[2], ptk)

    def compress(self, kcdup, vcdup, gk_rep, ph, po, ptc):
        b = self.b
        I = self.inp
        C2 = 2.0 * 0.7978845608028654
        w1 = b.sb("w1", [128, 16, 256], BF16)
        w2 = b.sb("w2", [128, 2, 64], BF16)
        w1st = [b.sb(f"w1st{i}", [128, 256], F32) for i in range(2)]
        peT = b.sb("peT", [128, 16], F32)
        peTb = b.sb("peTb", [128, 16], BF16)
        hTc = b.sb("hTc", [128, 2, 256], BF16)
        pbias = b.sb("pbias", [128, 2], F32)
        xh = b.sb("xh", [128, 255], F32)
        x2 = b.sb("x2", [128, 255], F32)
        sg = b.sb("sg", [128, 255], F32)
        ctmp = b.sb("ctmp", [128, 64], F32)
        csq = b.sb("csq", [128, 64], F32)
        cs1 = b.sb("cs1", [128, 1], F32)
        kcb = b.sb("kcb", [128, 64], BF16)
        b.op("pool", lambda e: e.memset(hTc[:], 0.0), writes=[hTc])
        for kv, (dup, pe_n, w1_n, w2_n) in enumerate([(kcdup, "cmp_pe_k", "cmp_w1_k", "cmp_w2_k"), (vcdup, "cmp_pe_v", "cmp_w1_v", "cmp_w2_v")]):
            self.load_weight(w1, I[w1_n][0], 256, kch=16, stage=w1st, eng="dve")
            self.load_weight(w2, I[w2_n][0], 64, kch=2, stage=w1st, eng="dve")
            for two in range(2):
                b.dma("sp", peT[two * 64:(two + 1) * 64, :], I[pe_n][0].rearrange("(pp two) d -> two d pp", two=2)[two],
                      writes=[peT], allow_slow_non_contiguous=True)
            b.op("dve", lambda e: e.tensor_copy(out=peTb[:], in_=peT[:]), reads=[peT], writes=[peTb])
            for ft in range(2):
                for pp in range(16):
                    b.op("pe", lambda e: e.matmul(po[:, ft:ft + 1], lhsT=w1[:, pp, ft * 128:(ft + 1) * 128], rhs=peTb[:, pp:pp + 1],
                                                  start=(pp == 0), stop=(pp == 15)), reads=[w1, peTb], writes=[po])
            b.op("dve", lambda e: e.tensor_copy(out=pbias[:], in_=po[:, 0:2]), reads=[po], writes=[pbias])
            for g in range(2):
                for ft in range(2):
                    p = ph[ft]
                    for pp in range(16):
                        b.op("pe", lambda e: e.matmul(p[:, 0:255], lhsT=w1[:, pp, ft * 128:(ft + 1) * 128],
                                                      rhs=dup[:, g, 1 + 2 * pp:1 + 2 * pp + 16 * 254 + 1:16],
                                                      start=(pp == 0), stop=(pp == 15)), reads=[w1, dup], writes=[p])
                    b.op("act", lambda e: e.activation(out=xh[:], in_=p[:, 0:255], func=AF.Identity, bias=pbias[:, ft:ft + 1]), reads=[p, pbias], writes=[xh])
                    b.op("dve", lambda e: e.tensor_tensor(out=x2[:], in0=xh[:], in1=xh[:], op=ALU.mult), reads=[xh], writes=[x2])
                    b.op("dve", lambda e: e.tensor_scalar(out=x2[:], in0=x2[:], scalar1=0.044715, scalar2=1.0, op0=ALU.mult, op1=ALU.add), reads=[x2], writes=[x2])
                    b.op("dve", lambda e: e.tensor_tensor(out=x2[:], in0=x2[:], in1=xh[:], op=ALU.mult), reads=[x2, xh], writes=[x2])
                    b.op("act", lambda e: e.activation(out=sg[:], in_=x2[:], func=AF.Sigmoid, scale=C2), reads=[x2], writes=[sg])
                    b.op("dve", lambda e: e.tensor_tensor(out=hTc[:, ft, 0:255], in0=xh[:], in1=sg[:], op=ALU.mult), reads=[xh, sg], writes=[hTc])
                for ct in range(2):
                    for ft in range(2):
                        b.op("pe", lambda e: e.matmul(po[:, 64:128], lhsT=hTc[:, ft, ct * 128:(ct + 1) * 128], rhs=w2[:, ft, :],
                                                      start=(ft == 0), stop=(ft == 1)), reads=[hTc, w2], writes=[po])
                    if kv == 0:
                        b.op("act", lambda e: e.activation(out=csq[:], in_=po[:, 64:128], func=AF.Square, accum_out=cs1[:]), reads=[po], writes=[csq, cs1])
                        b.op("act", lambda e: e.activation(out=cs1[:], in_=cs1[:], func=AF.Sqrt, scale=1.0 / 64, bias=RMS_EPS), reads=[cs1], writes=[cs1])
                        b.op("dve", lambda e: e.reciprocal(out=cs1[:], in_=cs1[:]), reads=[cs1], writes=[cs1])
                        b.op("dve", lambda e: e.tensor_scalar_mul(out=ctmp[:], in0=po[:, 64:128], scalar1=cs1[:]), reads=[po, cs1], writes=[ctmp])
                        b.op("dve", lambda e: e.tensor_tensor(out=kcb[:], in0=ctmp[:], in1=gk_rep[:, 0, :], op=ALU.mult), reads=[ctmp, gk_rep], writes=[kcb])
                        b.op("pe", lambda e: e.transpose(out=ptc[0:64, 0, :], in_=kcb[:], identity=self.ident[:]), reads=[kcb, self.ident], writes=[ptc])
                        b.op("dve", lambda e: e.tensor_copy(out=self.kcT[:, g, ct * 128:(ct + 1) * 128], in_=ptc[0:64, 0, :]), reads=[ptc], writes=[self.kcT])
                    else:
                        b.op("dve", lambda e: e.tensor_copy(out=self.vcA[:, g, ct, 0:64], in_=po[:, 64:128]), reads=[po], writes=[self.vcA])
        if "compress" in self.debug:
            d = self.dbg_out("kcT", [64, 2, 256], BF16)
            b.dma("pool", d, self.kcT[:], reads=[self.kcT])
            d = self.dbg_out("vcA", [128, 2, 2, 129], F32)
            b.dma("pool", d, self.vcA[:], reads=[self.vcA])

    def finish(self):
        b = self.b
        b.wait_all_on("pool")
        b.barrier()
        b.close()
        return self.nc


def _phase_attn(self):
    b = self.b
    I = self.inp
    with b.scope():
        tw = b.sb("tw", [128, 8, 640], F32)
        ts = b.sb("ts", [128, 8, 640], F32)
        b.dma("sp", tw[:], I["tw"], writes=[tw])
        b.dma("sp", ts[:], I["ts"], writes=[ts])
        candneg = b.sb("candneg", [128, 32, 64], F32)
        fz = b.sb("fz", [128, 32, 64], F32)
        b.dma("sp", candneg[:], I["candneg"], writes=[candneg])
        b.dma("sp", fz[:], I["fz"], writes=[fz])
        b31 = b.sb("b31", [128, 8], F32)
        b.dma("sp", b31[:], I["b31"], writes=[b31])
        kwp = b.sb("kwp", [128, 2, S], BF16)
        b.op("pool", lambda e: e.memset(kwp[64:128, :, :], 0.0), writes=[kwp])
        b.op("pool", lambda e: e.tensor_copy(out=kwp[0:64, :, :], in_=self.kwT[:]), reads=[self.kwT], writes=[kwp])
        kcp = b.sb("kcp", [128, 2, 256], BF16)
        b.op("pool", lambda e: e.memset(kcp[64:128, :, :], 0.0), writes=[kcp])
        b.op("pool", lambda e: e.tensor_copy(out=kcp[0:64, :, :], in_=self.kcT[:]), reads=[self.kcT], writes=[kcp])
        zer = b.sb("zer", [128, 512], BF16)
        b.op("pool", lambda e: e.memset(zer[:], 0.0), writes=[zer])
        qm = [b.sb(f"qm{i}", [128, 8, 512], BF16) for i in range(2)]
        bct = [b.sb(f"bct{i}", [128, 512], F32) for i in range(3)]
        scf = [b.sb(f"scf{i}", [128, 640], F32) for i in range(2)]
        pcT = [b.sb(f"pcT{i}", [128, 2, 512], F32) for i in range(2)]
        pT = [b.sb(f"pT{i}", [128, 640], BF16) for i in range(3)]
        oacc = b.sb("oacc", [128, 4, 512], F32)
        imp = b.sb("imp", [128, 4, 2, 64], F32)
        impm = b.sb("impm", [128, 64], F32)
        impm2 = b.sb("impm2", [128, 64], F32)
        m8a = b.sb("m8a", [128, 8], F32)
        m8b = b.sb("m8b", [128, 8], F32)
        msk = b.sb("msk", [128, 64], F32)
        mb = b.sb("mb", [128, 128], BF16)
        b.op("pool", lambda e: e.memset(mb[:], 0.0), writes=[mb])
        rs = b.sb("rs", [128, 4], F32)
        rg = b.sb("rg", [128, 4], F32)
        oab = b.sb("oab", [128, 512], BF16)
        oaT = [b.sb(f"oaT{i}", [128, 4, 128], BF16) for i in range(2)]
        pS = [b.ps(f"pS{i}", [128, 512], F32) for i in range(2)]
        pS2 = b.ps("pS2", [128, 512], F32)
        pO = [b.ps(f"pO{i}", [128, 512], F32) for i in range(3)]
        pTr = b.ps("pTr", [128, 8, 128], BF16)
        nrot = {"bct": 0, "scf": 0, "pT": 0, "pS": 0}

        def rot(name, lst):
            nrot[name] += 1
            return lst[nrot[name] % len(lst)]

        def finalize(po, ncol_off, h, qs, branch, first):
            qt = qs_base + qs
            o0 = ncol_off
            b.op("dve", lambda e: e.tensor_scalar_max(out=rs[:, 0:1], in0=po[:, o0 + 64:o0 + 65], scalar1=1e-30), reads=[po], writes=[rs])
            b.op("dve", lambda e: e.reciprocal(out=rs[:, 1:2], in_=rs[:, 0:1]), reads=[rs], writes=[rs])
            b.op("dve", lambda e: e.tensor_tensor(out=rg[:, 0:1], in0=rs[:, 1:2], in1=self.gts[:, qt, h * 3 + branch:h * 3 + branch + 1], op=ALU.mult),
                 reads=[rs, self.gts], writes=[rg])
            if first:
                b.op("dve", lambda e: e.tensor_scalar_mul(out=oacc[:, qs, h * 64:(h + 1) * 64], in0=po[:, o0:o0 + 64], scalar1=rg[:, 0:1]),
                     reads=[po, rg], writes=[oacc])
            else:
                b.op("dve", lambda e: e.scalar_tensor_tensor(out=oacc[:, qs, h * 64:(h + 1) * 64], in0=po[:, o0:o0 + 64], scalar=rg[:, 0:1],
                                                             in1=oacc[:, qs, h * 64:(h + 1) * 64], op0=ALU.mult, op1=ALU.add),
                     reads=[po, rg, oacc], writes=[oacc])

        nqg = getattr(self, "nqg_limit", 8)
        for qg in range(nqg):
            qs_base = 4 * qg
            q0 = 512 * qg
            Q = qm[qg % 2]
            b.dma("sp", Q[0:64, :, :], self.qT_d[:, :, q0:q0 + 512].rearrange("h d t -> d h t"), reads=[self.qT_d], writes=[Q])
            if qg < 2:
                b.op("pool", lambda e: e.memset(Q[64:128, :, :], 0.0), writes=[Q])
            for h in range(8):
                g = h // 4
                pc = pcT[h % 2]
                for ct in range(2):
                    p = rot("pS", pS)
                    b.op("pe", lambda e: e.matmul(p[:, :], lhsT=kcp[:, g, ct * 128:(ct + 1) * 128], rhs=Q[:, h, :], start=True, stop=True),
                         reads=[kcp, Q], writes=[p])
                    bt = rot("bct", bct)
                    b.dma("sp", bt[:], I["biasc"][h, ct, :, q0:q0 + 512], writes=[bt])
                    sc = rot("scf", scf)
                    b.op("dve", lambda e: e.tensor_tensor(out=sc[:, 0:512], in0=p[:, :], in1=bt[:], op=ALU.add), reads=[p, bt], writes=[sc])
                    b.op("act", lambda e: e.activation(out=pc[:, ct, :], in_=sc[:, 0:512], func=AF.Exp), reads=[sc], writes=[pc])
                po = pO[0]
                for qs in range(4):
                    for ct in range(2):
                        b.op("pe", lambda e: e.matmul(po[:, qs * 128:qs * 128 + 129] if False else po[:, 0:129], lhsT=pc[:, ct, qs * 128:(qs + 1) * 128],
                                                      rhs=self.vcA[:, g, ct, :], start=(ct == 0), stop=(ct == 1)), reads=[pc, self.vcA], writes=[po])
                    finalize(po, 0, h, qs, 0, True)
                    if h % 4 == 0:
                        b.op("dve", lambda e: e.tensor_scalar_mul(out=imp[:, qs, g, :], in0=po[:, 65:129], scalar1=rs[:, 1:2]), reads=[po, rs], writes=[imp])
                    else:
                        b.op("dve", lambda e: e.scalar_tensor_tensor(out=imp[:, qs, g, :], in0=po[:, 65:129], scalar=rs[:, 1:2], in1=imp[:, qs, g, :],
                                                                     op0=ALU.mult, op1=ALU.add), reads=[po, rs, imp], writes=[imp])
            if qg >= 2:
                for qs in range(4):
                    qt = qs_base + qs
                    for g in range(2):
                        b.op("dve", lambda e: e.tensor_tensor(out=impm[:], in0=imp[:, qs, g, :], in1=candneg[:, qt, :], op=ALU.add), reads=[imp, candneg], writes=[impm])
                        b.op("dve", lambda e: e.max(out=m8a[:], in_=impm[:]), reads=[impm], writes=[m8a])
                        b.op("dve", lambda e: e.match_replace(out=impm2[:], in_to_replace=m8a[:], in_values=impm[:], imm_value=-1e9), reads=[m8a, impm], writes=[impm2])
                        b.op("dve", lambda e: e.max(out=m8b[:], in_=impm2[:]), reads=[impm2], writes=[m8b])
                        b.op("dve", lambda e: e.tensor_scalar(out=msk[:], in0=impm[:], scalar1=m8b[:, 4:5], scalar2=None, op0=ALU.is_ge), reads=[impm, m8b], writes=[msk])
                        b.op("dve", lambda e: e.tensor_tensor(out=msk[:], in0=msk[:], in1=fz[:, qt, :], op=ALU.max), reads=[msk, fz], writes=[msk])
                        b.op("dve", lambda e: e.tensor_scalar(out=mb[:, 64:128], in0=msk[:], scalar1=-NEG, scalar2=NEG, op0=ALU.mult, op1=ALU.add), reads=[msk], writes=[mb])
                        b.op("pe", lambda e: e.transpose(out=pTr[:, 0, :], in_=mb[:], identity=self.ident[:]), reads=[mb, self.ident], writes=[pTr])
                        b.op("act", lambda e: e.copy(out=Q[64:128, 4 * g:4 * g + 4, qs * 128:(qs + 1) * 128],
                                                     in_=pTr[64:128, 0:1, :].to_broadcast([64, 4, 128])), reads=[pTr], writes=[Q])
            for h in range(8):
                g = h // 4
                po_s, po_w = pO[1], pO[2]
                for po in (po_s, po_w):
                    b.op("pe", lambda e: e.matmul(po[:, 0:260], lhsT=zer[:, 0:128], rhs=zer[:, 0:260], start=True, stop=True), reads=[zer], writes=[po])
                nkt = 4 * (qg + 1)
                for kt in range(nkt):
                    dlt = 4 * qg - kt
                    qstart = 0 if dlt >= 0 else -dlt * 128
                    N = 512 - qstart
                    p = rot("pS", pS)
                    b.op("pe", lambda e: e.matmul(p[:, 0:N], lhsT=self.ksE[:, g, kt * 128:(kt + 1) * 128], rhs=Q[:, h, qstart:512], start=True, stop=True),
                         reads=[self.ksE, Q], writes=[p])
                    pt_ = rot("pT", pT)
                    if dlt <= 1:
                        c0 = 128 if dlt == 1 else 0
                        sc = rot("scf", scf)
                        b.op("dve", lambda e: e.tensor_tensor(out=sc[:, 0:N], in0=p[:, 0:N], in1=ts[:, h, c0:c0 + N], op=ALU.add), reads=[p, ts], writes=[sc])
                        b.op("act", lambda e: e.activation(out=pt_[:, 0:N], in_=sc[:, 0:N], func=AF.Exp), reads=[sc], writes=[pt_])
                    else:
                        b.op("act", lambda e: e.activation(out=pt_[:, 0:N], in_=p[:, 0:N], func=AF.Exp, bias=b31[:, h:h + 1]), reads=[p, b31], writes=[pt_])
                    for qs in range(qstart // 128, 4):
                        o = qs * 128 - qstart
                        b.op("pe", lambda e: e.matmul(po_s[:, qs * 65:(qs + 1) * 65], lhsT=pt_[:, o:o + 128], rhs=self.vaug_s[:, kt, g, :],
                                                      start=False, stop=(kt == nkt - 1), skip_group_check=True), reads=[pt_, self.vaug_s], writes=[po_s])
                kts = [kt for kt in range(4 * qg - 4, 4 * qg + 4) if kt >= 0]
                for kt in kts:
                    qs_lo = max(0, kt - 4 * qg)
                    qs_hi = min(3, kt + 4 - 4 * qg)
                    N = (qs_hi - qs_lo + 1) * 128
                    c0 = 128 * (4 * qg + qs_lo - kt)
                    p = rot("pS", pS)
                    b.op("pe", lambda e: e.matmul(p[:, 0:N], lhsT=kwp[:, g, kt * 128:(kt + 1) * 128], rhs=Q[:, h, qs_lo * 128:(qs_hi + 1) * 128], start=True, stop=True),
                         reads=[kwp, Q], writes=[p])
                    sc = rot("scf", scf)
                    b.op("dve", lambda e: e.tensor_tensor(out=sc[:, 0:N], in0=p[:, 0:N], in1=tw[:, h, c0:c0 + N], op=ALU.add), reads=[p, tw], writes=[sc])
                    pt_ = rot("pT", pT)
                    b.op("act", lambda e: e.activation(out=pt_[:, 0:N], in_=sc[:, 0:N], func=AF.Exp), reads=[sc], writes=[pt_])
                    for qs in range(qs_lo, qs_hi + 1):
                        o = (qs - qs_lo) * 128
                        b.op("pe", lambda e: e.matmul(po_w[:, qs * 65:(qs + 1) * 65], lhsT=pt_[:, o:o + 128], rhs=self.vaug_w[:, kt, g, :],
                                                      start=False, stop=(kt == kts[-1]), skip_group_check=True), reads=[pt_, self.vaug_w], writes=[po_w])
                for qs in range(4):
                    finalize(po_s, qs * 65, h, qs, 1, False)
                    finalize(po_w, qs * 65, h, qs, 2, False)
            for qs in range(4):
                qt = qs_base + qs
                ot = oaT[qs % 2]
                b.op("act", lambda e: e.copy(out=oab[:], in_=oacc[:, qs, :]), reads=[oacc], writes=[oab])
                for c in range(4):
                    b.op("pe", lambda e: e.transpose(out=pTr[:, 4 + c, :], in_=oab[:, c * 128:(c + 1) * 128], identity=self.ident[:]), reads=[oab, self.ident], writes=[pTr])
                b.op("act", lambda e: e.copy(out=ot[:], in_=pTr[:, 4:8, :]), reads=[pTr], writes=[ot])
                b.dma("pool", self.oaT_d[:, :, qt * 128:(qt + 1) * 128].rearrange("c p t -> p c t"), ot[:], reads=[ot], writes=[self.oaT_d])
        if "attn" in self.debug:
            d = self.dbg_out("oaT", [4, 128, S], BF16)
            b.dma("pool", d, self.oaT_d[:], reads=[self.oaT_d])


Prog.phase_attn = _phase_attn


def _phase_merge(self):
    b = self.b
    I = self.inp
    self.x1_d = b.dram("x1_d", [S, D], F32)
    with b.scope():
        gat = self.load_gain("gat2", I["attn_norm_g"][0])
        stage = [b.sb(f"mst{i}", [128, 1024], F32) for i in range(2)]
        wg = b.sb("wg", [128, 8, 2048], BF16)
        for n in range(2):
            for c in range(8):
                st = stage[c % 2]
                b.dma("sp", st[:], I["w_in"][0][c * 128:(c + 1) * 128, GA0 + n * 1024:GA0 + (n + 1) * 1024], writes=[st])
                b.op("act", lambda e: e.activation(out=wg[:, c, n * 1024:(n + 1) * 1024], in_=st[:], func=AF.Copy, scale=gat[:, c:c + 1]),
                     reads=[st, gat], writes=[wg])
        wa = b.sb("wa", [128, 4, 1024], BF16)
        wb = b.sb("wb", [128, 4, 1024], BF16)
        wo = b.sb("wo", [128, 8, 1024], BF16)
        self.load_weight(wa, I["w_proj_a"][0], 1024, kch=4, stage=stage, eng="dve")
        self.load_weight(wb, I["w_proj_b"][0], 1024, kch=4, stage=stage, eng="dve")
        self.load_weight(wo, I["w_out"][0], 1024, kch=8, stage=stage, eng="dve")
        xt = [b.sb(f"mxt{i}", [128, D], F32) for i in range(2)]
        junk = b.sb("mjunk", [128, D], BF16)
        ss = [b.sb(f"mss{i}", [128, 1], F32) for i in range(2)]
        hb = [b.sb(f"mhb{i}", [128, D], BF16) for i in range(2)]
        hT = [b.sb(f"mhT{i}", [128, 8, 128], BF16) for i in range(2)]
        oat = [b.sb(f"oat{i}", [128, 4, 128], BF16) for i in range(2)]
        obt = [b.sb(f"obt{i}", [128, 4, 128], BF16) for i in range(2)]
        sg = b.sb("msg", [128, 2048], F32)
        m1 = b.sb("m1", [128, 1024], F32)
        m2 = b.sb("m2", [128, 1024], F32)
        mgb = b.sb("mgb", [128, 1024], BF16)
        mT = b.sb("mT", [128, 8, 128], BF16)
        x1t = [b.sb(f"x1t{i}", [128, D], F32) for i in range(2)]
        pt = b.ps("mpt", [128, 8, 128], BF16)
        pg = [b.ps(f"mpg{i}", [128, 512], F32) for i in range(2)]
        pa = [b.ps(f"mpa{i}", [128, 512], F32) for i in range(2)]
        pb = [b.ps(f"mpb{i}", [128, 512], F32) for i in range(2)]
        for t in range(getattr(self, "nt_limit", NT)):
            i = t % 2
            self.make_hT(I["x"], t, xt[i], junk, ss[i], hb[i], pt, hT[i], self.ident)
            b.dma("sp", oat[i][:], self.oaT_d[:, :, t * 128:(t + 1) * 128].rearrange("c p t -> p c t"), reads=[self.oaT_d], writes=[oat[i]])
            b.dma("sp", obt[i][:], self.obT_d[:, :, t * 128:(t + 1) * 128].rearrange("c p t -> p c t"), reads=[self.obT_d], writes=[obt[i]])
            for n in range(4):
                p = pg[n % 2]
                for c in range(8):
                    b.op("pe", lambda e: e.matmul(p[:, :], lhsT=hT[i][:, c, :], rhs=wg[:, c, n * 512:(n + 1) * 512], start=(c == 0), stop=(c == 7)),
                         reads=[hT[i], wg], writes=[p])
                b.op("act", lambda e: e.activation(out=sg[:, n * 512:(n + 1) * 512], in_=p[:, :], func=AF.Sigmoid), reads=[p], writes=[sg])
            for n in range(2):
                for c in range(4):
                    b.op("pe", lambda e: e.matmul(pa[n][:, :], lhsT=oat[i][:, c, :], rhs=wa[:, c, n * 512:(n + 1) * 512], start=(c == 0), stop=(c == 3)),
                         reads=[oat[i], wa], writes=[pa[n]])
                for c in range(4):
                    b.op("pe", lambda e: e.matmul(pb[n][:, :], lhsT=obt[i][:, c, :], rhs=wb[:, c, n * 512:(n + 1) * 512], start=(c == 0), stop=(c == 3)),
                         reads=[obt[i], wb], writes=[pb[n]])
                b.op("dve", lambda e: e.tensor_tensor(out=m1[:, n * 512:(n + 1) * 512], in0=pa[n][:, :], in1=sg[:, n * 512:(n + 1) * 512], op=ALU.mult),
                     reads=[pa[n], sg], writes=[m1])
                b.op("dve", lambda e: e.tensor_tensor(out=m2[:, n * 512:(n + 1) * 512], in0=pb[n][:, :], in1=sg[:, 1024 + n * 512:1024 + (n + 1) * 512], op=ALU.mult),
                     reads=[pb[n], sg], writes=[m2])
            b.op("pool", lambda e: e.tensor_tensor(out=mgb[:], in0=m1[:], in1=m2[:], op=ALU.add), reads=[m1, m2], writes=[mgb])
            for c in range(8):
                b.op("pe", lambda e: e.transpose(out=pt[:, c, :], in_=mgb[:, c * 128:(c + 1) * 128], identity=self.ident[:]), reads=[mgb, self.ident], writes=[pt])
            b.op("act", lambda e: e.copy(out=mT[:], in_=pt[:]), reads=[pt], writes=[mT])
            for n in range(2):
                for c in range(8):
                    b.op("pe", lambda e: e.matmul(pa[n][:, :], lhsT=mT[:, c, :], rhs=wo[:, c, n * 512:(n + 1) * 512], start=(c == 0), stop=(c == 7)),
                         reads=[mT, wo], writes=[pa[n]])
                b.op("dve", lambda e: e.tensor_tensor(out=x1t[i][:, n * 512:(n + 1) * 512], in0=pa[n][:, :], in1=xt[i][:, n * 512:(n + 1) * 512], op=ALU.add),
                     reads=[pa[n], xt[i]], writes=[x1t[i]])
            b.dma("pool", self.x1_d[t * 128:(t + 1) * 128, :], x1t[i][:], reads=[x1t[i]], writes=[self.x1_d])
        if "merge" in self.debug:
            d = self.dbg_out("x1", [S, D], F32)
            b.dma("pool", d, self.x1_d[:], reads=[self.x1_d])


def _phase_ffn(self):
    b = self.b
    I = self.inp
    TG = 128
    NFT = 44
    with b.scope():
        gf = self.load_gain("gf", I["ffn_norm_g"][0])
        stage = [b.sb(f"fst{i}", [128, 1024], F32) for i in range(2)]
        wu = b.sb("wu", [128, 8, 2 * DFF], BF16)
        for n in range(8):
            for c in range(8):
                st = stage[c % 2]
                b.dma("sp", st[:, 0:704], I["w_up"][0][c * 128:(c + 1) * 128, n * 704:(n + 1) * 704], writes=[st])
                b.op("act", lambda e: e.activation(out=wu[:, c, n * 704:(n + 1) * 704], in_=st[:, 0:704], func=AF.Copy, scale=gf[:, c:c + 1]),
                     reads=[st, gf], writes=[wu])
        wd = b.sb("wd", [128, 22, D], BF16)
        self.load_weight(wd, I["w_down"][0], D, kch=22, stage=stage, eng="dve")
        cw = b.sb("cw", [128, 3, NFT], F32)
        for j in range(3):
            b.dma("sp", cw[:, j, :], I["conv_w"][0][j].rearrange("(c p) -> p c", p=128), writes=[cw], allow_slow_non_contiguous=True)
        cbias = self.load_gain("cbias", I["conv_b"][0], kch=NFT)
        carry = b.sb("carry", [128, NFT, 2], F32)
        b.op("pool", lambda e: e.memset(carry[:], 0.0), writes=[carry])
        xt = [b.sb(f"fxt{i}", [128, D], F32) for i in range(2)]
        junk = b.sb("fjunk", [128, D], BF16)
        ss = [b.sb(f"fss{i}", [128, 1], F32) for i in range(2)]
        hb = [b.sb(f"fhb{i}", [128, D], BF16) for i in range(2)]
        hT1 = [b.sb(f"fhT{i}", [128, 8, 128], BF16) for i in range(2)]
        hTg = b.sb("fhTg", [128, 8, TG], BF16)
        ub = [b.sb(f"ub{i}", [128, TG + 2], F32) for i in range(2)]
        cv = [b.sb(f"cv{i}", [128, TG], F32) for i in range(2)]
        sgl = b.sb("sgl", [128, TG], F32)
        actT = b.sb("actT", [128, 22, TG], BF16)
        self._val = b.sb("fval", [128, 22, TG], BF16)
        ot = xt
        pt = b.ps("fpt", [128, 8, 128], BF16)
        pu = [b.ps(f"fpu{i}", [128, 512], F32) for i in range(3)]
        pd = [b.ps(f"fpd{i}", [128, 512], F32) for i in range(2)]
        ng = getattr(self, "nt_limit", NT) * 128 // TG
        for gi in range(ng):
            for s_ in range(TG // 128):
                t = gi * (TG // 128) + s_
                self.make_hT(self.x1_d, t, xt[s_], junk, ss[s_], hb[s_], pt, hT1[s_], self.ident)
                b.op("pool", lambda e: e.tensor_copy(out=hTg[:, :, s_ * 128:(s_ + 1) * 128], in_=hT1[s_][:]), reads=[hT1[s_]], writes=[hTg])
            for ft in range(NFT):
                p = pu[ft % 3]
                u = ub[ft % 2]
                c_ = cv[(ft // 22) % 2] if False else cv[ft % 2]
                for c in range(8):
                    b.op("pe", lambda e: e.matmul(p[:, 0:TG], lhsT=wu[:, c, ft * 128:(ft + 1) * 128], rhs=hTg[:, c, :], start=(c == 0), stop=(c == 7)),
                         reads=[wu, hTg], writes=[p])
                b.op("act", lambda e: e.copy(out=u[:, 2:TG + 2], in_=p[:, 0:TG]), reads=[p], writes=[u])
                b.op("pool", lambda e: e.tensor_copy(out=u[:, 0:2], in_=carry[:, ft, :]), reads=[carry], writes=[u])
                b.op("pool", lambda e: e.tensor_copy(out=carry[:, ft, :], in_=u[:, TG:TG + 2]), reads=[u], writes=[carry])
                b.op("dve", lambda e: e.tensor_scalar(out=c_[:], in0=u[:, 0:TG], scalar1=cw[:, 0, ft:ft + 1], scalar2=cbias[:, ft:ft + 1], op0=ALU.mult, op1=ALU.add),
                     reads=[u, cw, cbias], writes=[c_])
                b.op("dve", lambda e: e.scalar_tensor_tensor(out=c_[:], in0=u[:, 1:TG + 1], scalar=cw[:, 1, ft:ft + 1], in1=c_[:], op0=ALU.mult, op1=ALU.add),
                     reads=[u, cw, c_], writes=[c_])
                if ft < 22:
                    b.op("dve", lambda e: e.scalar_tensor_tensor(out=self._val[:, ft, :], in0=u[:, 2:TG + 2], scalar=cw[:, 2, ft:ft + 1], in1=c_[:], op0=ALU.mult, op1=ALU.add),
                         reads=[u, cw, c_], writes=[self._val])
                else:
                    b.op("dve", lambda e: e.scalar_tensor_tensor(out=c_[:], in0=u[:, 2:TG + 2], scalar=cw[:, 2, ft:ft + 1], in1=c_[:], op0=ALU.mult, op1=ALU.add),
                         reads=[u, cw, c_], writes=[c_])
                    b.op("act", lambda e: e.activation(out=sgl[:], in_=c_[:], func=AF.Silu), reads=[c_], writes=[sgl])
                    b.op("dve", lambda e: e.tensor_tensor(out=actT[:, ft - 22, :], in0=sgl[:], in1=self._val[:, ft - 22, :], op=ALU.mult),
                         reads=[sgl, self._val], writes=[actT])
            for s_ in range(TG // 128):
                t = gi * (TG // 128) + s_
                for n in range(2):
                    for f in range(22):
                        b.op("pe", lambda e: e.matmul(pd[n][:, :], lhsT=actT[:, f, s_ * 128:(s_ + 1) * 128], rhs=wd[:, f, n * 512:(n + 1) * 512], start=(f == 0), stop=(f == 21)),
                             reads=[actT, wd], writes=[pd[n]])
                    b.op("dve", lambda e: e.tensor_tensor(out=ot[s_][:, n * 512:(n + 1) * 512], in0=pd[n][:, :], in1=xt[s_][:, n * 512:(n + 1) * 512], op=ALU.add),
                         reads=[pd[n], xt[s_]], writes=[ot[s_]])
                b.dma("pool", self.out[t * 128:(t + 1) * 128, :], ot[s_][:], reads=[ot[s_]])


Prog.phase_merge = _phase_merge
Prog.phase_ffn = _phase_ffn


def _phase_rwkv(self):
    b = self.b
    I = self.inp
    TG = 256
    NCH = TG // 64
    tt = lambda eng, out, in0, in1, op, rd, wr: b.op(eng, lambda e: e.tensor_tensor(out=out, in0=in0, in1=in1, op=op), reads=rd, writes=wr)
    with b.scope():
        gat = self.load_gain("gat3", I["attn_norm_g"][0])
        stage = [b.sb(f"rst{i}", [128, 1792], F32) for i in range(2)]
        wr = b.sb("wr", [128, 8, 1792], BF16)
        self.load_weight(wr, I["w_in"][0][:, RW0:RW0 + 1792], 1792, gvec=gat, stage=stage)

        def colvec(name, src, n):
            t = b.sb(name, [64, n], F32)
            b.dma("sp", t[:], src.rearrange("(c p) -> p c", p=64), writes=[t], allow_slow_non_contiguous=True)
            return t
        mu = colvec("mu", I["rwkv_mu"][0], 28)
        w0 = colvec("w0", I["rwkv_w0"][0], 8)
        a0 = colvec("a0", I["rwkv_a0"][0], 8)
        k_k = colvec("k_k", I["rwkv_k_k"][0], 8)
        k_a = colvec("k_a", I["rwkv_k_a"][0], 8)
        r_k = colvec("r_k", I["rwkv_r_k"][0].rearrange("h d -> (h d)"), 8)
        w2s = b.sb("w2s", [64, 512], F32)
        a2s = b.sb("a2s", [64, 512], F32)
        g2s = b.sb("g2s", [64, 2, 512], F32)
        b.dma("sp", w2s[:], I["rwkv_w2"][0], writes=[w2s])
        b.dma("sp", a2s[:], I["rwkv_a2"][0], writes=[a2s])
        b.dma("sp", g2s[:], I["rwkv_g2"][0].rearrange("(two l) f -> l two f", two=2), writes=[g2s])
        lng = b.sb("lng", [64, 512], F32)
        lnb = b.sb("lnb", [64, 512], F32)
        b.dma("sp", lng[:], I["rwkv_ln_g"][0].partition_broadcast(64), writes=[lng])
        b.dma("sp", lnb[:], I["rwkv_ln_b"][0].partition_broadcast(64), writes=[lnb])
        msk = b.sb("rmsk", [64, 3, 64], F32)
        b.dma("sp", msk[:], I["rwmask"], writes=[msk])
        rstm = b.sb("rstm", [64, TG], F32)
        b.dma("sp", rstm[:], I["rwreset"][:, 0:TG], writes=[rstm])
        ones = b.sb("ones64", [64, 64], F32)
        b.op("pool", lambda e: e.memset(ones[:], 1.0), writes=[ones])
        idf = self.identf
        carry = b.sb("rcarry", [64, 28], F32)
        b.op("pool", lambda e: e.memset(carry[:], 0.0), writes=[carry])
        Hs = [[b.sb(f"H{h}_{i}", [64, 64], F32) for i in range(2)] for h in range(8)]
        for h in range(8):
            b.op("pool", lambda e: e.memset(Hs[h][0][:], 0.0), writes=[Hs[h][0]])
        xt = [b.sb(f"rxt{i}", [128, D], F32) for i in range(2)]
        junk = b.sb("rjunk", [128, D], BF16)
        ss = [b.sb(f"rss{i}", [128, 1], F32) for i in range(2)]
        hb = [b.sb(f"rhb{i}", [128, D], BF16) for i in range(2)]
        hT1 = [b.sb(f"rhT{i}", [128, 8, 128], BF16) for i in range(2)]
        hTg = b.sb("rhTg", [128, 8, TG], BF16)
        pbuf = [b.sb(f"rpb{i}", [64, TG + 1], F32) for i in range(2)]
        dtmp = b.sb("rdtmp", [64, TG], F32)
        X = [b.sb(f"rX{w}", [64, 8, TG], F32) for w in range(3)]
        xs = b.sb("rxs", [64, 4, TG], F32)
        BV = b.sb("rBV", [64, 8, TG], F32)
        Ytm = b.sb("rYtm", [64, NCH, 8, 64], F32)
        sqv = b.sb("rsqv", [64, NCH, 8, 64], F32)
        st1 = b.sb("rst1", [64, NCH * 8], F32)
        st2 = b.sb("rst2", [64, NCH * 8], F32)
        T = {n: b.sb("r" + n, [64, TG], F32) for n in ["lw", "as", "kk", "sq", "kkn", "bv", "kp", "t1", "L", "Lx", "Ep", "Em", "Ex", "BT", "KT", "BG", "KG", "rk"]}
        AR = b.sb("rAR", [64, NCH, 2, 64], F32)
        TM = [b.sb(f"rTM{i}", [64, 3, 64], F32) for i in range(2)]
        XM = [b.sb(f"rXM{i}", [64, 4, 64], F32) for i in range(2)]
        AA = [b.sb(f"rAA{i}", [64, 2, 64], F32) for i in range(3)]
        PP = [b.sb(f"rPP{i}", [64, 64], F32) for i in range(3)]
        Xs = b.sb("rXs", [64, 64], F32)
        Us = b.sb("rUs", [64, 64], F32)
        obf = [b.sb(f"robf{i}", [64, TG], BF16) for i in range(2)]
        otmp = b.sb("rotmp", [64, TG], F32)
        pt = b.ps("rpt", [128, 8, 128], BF16)
        pp = [b.ps(f"rpp{i}", [128, 512], F32) for i in range(2)]
        pq = [b.ps(f"rpq{i}", [128, 512], F32) for i in range(2)]
        pd = [b.ps(f"rpd{i}", [128, 512], F32) for i in range(2)]
        pz = b.ps("rpz", [128, 512], F32)
        cnt = {"pp": 0, "pq": 0, "pd": 0, "aa": 0, "ppb": 0, "tm": 0, "xm": 0, "pb": 0}

        def nxt(k, lst):
            cnt[k] += 1
            return lst[cnt[k] % len(lst)]

        ngr = getattr(self, "nrg_limit", S // TG)
        for gi in range(ngr):
            q0 = gi * TG
            for s_ in range(TG // 128):
                t = gi * (TG // 128) + s_
                self.make_hT(I["x"], t, xt[s_], junk, ss[s_], hb[s_], pt, hT1[s_], self.ident)
                b.op("pool", lambda e: e.tensor_copy(out=hTg[:, :, s_ * 128:(s_ + 1) * 128], in_=hT1[s_][:]), reads=[hT1[s_]], writes=[hTg])

            def proj_lerp(fc, out_ap, out_buf, post=None):
                p = nxt("pp", pp)
                for c in range(8):
                    b.op("pe", lambda e: e.matmul(p[0:64, 0:TG], lhsT=wr[:, c, fc * 64:(fc + 1) * 64], rhs=hTg[:, c, :], start=(c == 0), stop=(c == 7)),
                         reads=[wr, hTg], writes=[p])
                pb_ = nxt("pb", pbuf)
                b.op("act", lambda e: e.copy(out=pb_[:, 1:TG + 1], in_=p[0:64, 0:TG]), reads=[p], writes=[pb_])
                b.op("pool", lambda e: e.tensor_copy(out=pb_[:, 0:1], in_=carry[:, fc:fc + 1]), reads=[carry], writes=[pb_])
                b.op("pool", lambda e: e.tensor_copy(out=carry[:, fc:fc + 1], in_=pb_[:, TG:TG + 1]), reads=[pb_], writes=[carry])
                tt("dve", dtmp[:], pb_[:, 0:TG], pb_[:, 1:TG + 1], ALU.subtract, [pb_], [dtmp])
                b.op("dve", lambda e: e.scalar_tensor_tensor(out=out_ap, in0=dtmp[:], scalar=mu[:, fc:fc + 1], in1=pb_[:, 1:TG + 1], op0=ALU.mult, op1=ALU.add),
                     reads=[dtmp, mu, pb_], writes=[out_buf])

            for w in range(3):
                for h in range(8):
                    proj_lerp(w * 8 + h, X[w][:, h, :], X[w])
            for j in range(4):
                proj_lerp(24 + j, xs[:, j, :], xs)
            b.op("act", lambda e: e.activation(out=xs[:, 0, :], in_=xs[:, 0, :], func=AF.Tanh), reads=[xs], writes=[xs])
            b.op("act", lambda e: e.activation(out=xs[:, 2:4, :], in_=xs[:, 2:4, :], func=AF.Sigmoid), reads=[xs], writes=[xs])

            for h in range(8):
                hs = slice(h * 64, (h + 1) * 64)
                R_, K_, V_ = X[0][:, h, :], X[1][:, h, :], X[2][:, h, :]
                p = nxt("pp", pp)
                b.op("pe", lambda e: e.matmul(p[0:64, 0:TG], lhsT=w2s[:, hs], rhs=xs[:, 0, :], start=True, stop=True), reads=[w2s, xs], writes=[p])
                b.op("act", lambda e: e.activation(out=T["lw"][:], in_=p[0:64, 0:TG], func=AF.Sigmoid, bias=w0[:, h:h + 1]), reads=[p, w0], writes=[T["lw"]])
                b.op("pool", lambda e: e.tensor_scalar_mul(out=T["lw"][:], in0=T["lw"][:], scalar1=-0.6065306597126334), reads=[T["lw"]], writes=[T["lw"]])
                p = nxt("pp", pp)
                b.op("pe", lambda e: e.matmul(p[0:64, 0:TG], lhsT=a2s[:, hs], rhs=xs[:, 1, :], start=True, stop=True), reads=[a2s, xs], writes=[p])
                b.op("act", lambda e: e.activation(out=T["as"][:], in_=p[0:64, 0:TG], func=AF.Sigmoid, bias=a0[:, h:h + 1]), reads=[p, a0], writes=[T["as"]])
                b.op("dve", lambda e: e.tensor_scalar_mul(out=T["kk"][:], in0=K_, scalar1=k_k[:, h:h + 1]), reads=[X[1], k_k], writes=[T["kk"]])
                tt("pool", T["sq"][:], T["kk"][:], T["kk"][:], ALU.mult, [T["kk"]], [T["sq"]])
                p = nxt("pp", pp)
                b.op("pe", lambda e: e.matmul(p[0:64, 0:TG], lhsT=ones[:], rhs=T["sq"][:], start=True, stop=True), reads=[ones, T["sq"]], writes=[p])
                b.op("act", lambda e: e.activation(out=T["sq"][:], in_=p[0:64, 0:TG], func=AF.Sqrt), reads=[p], writes=[T["sq"]])
                b.op("dve", lambda e: e.tensor_scalar_max(out=T["sq"][:], in0=T["sq"][:], scalar1=1e-12), reads=[T["sq"]], writes=[T["sq"]])
                b.op("dve", lambda e: e.reciprocal(out=T["sq"][:], in_=T["sq"][:]), reads=[T["sq"]], writes=[T["sq"]])
                tt("dve", T["kkn"][:], T["kk"][:], T["sq"][:], ALU.mult, [T["kk"], T["sq"]], [T["kkn"]])
                tt("pool", T["bv"][:], T["kkn"][:], T["as"][:], ALU.mult, [T["kkn"], T["as"]], [T["bv"]])
                b.op("dve", lambda e: e.tensor_scalar(out=T["t1"][:], in0=T["as"][:], scalar1=-1.0, scalar2=k_a[:, h:h + 1], op0=ALU.add, op1=ALU.mult),
                     reads=[T["as"], k_a], writes=[T["t1"]])
                b.op("dve", lambda e: e.scalar_tensor_tensor(out=T["kp"][:], in0=T["t1"][:], scalar=1.0, in1=K_, op0=ALU.add, op1=ALU.mult),
                     reads=[T["t1"], X[1]], writes=[T["kp"]])
                tt("pool", T["rk"][:], R_, T["kp"][:], ALU.mult, [X[0], T["kp"]], [T["rk"]])
                b.op("pool", lambda e: e.tensor_scalar_mul(out=T["rk"][:], in0=T["rk"][:], scalar1=r_k[:, h:h + 1]), reads=[T["rk"], r_k], writes=[T["rk"]])
                p = nxt("pp", pp)
                b.op("pe", lambda e: e.matmul(p[0:64, 0:TG], lhsT=ones[:], rhs=T["rk"][:], start=True, stop=True), reads=[ones, T["rk"]], writes=[p])
                tt("dve", BV[:, h, :], p[0:64, 0:TG], V_, ALU.mult, [p, X[2]], [BV])
                b.op("dve", lambda e: e.tensor_tensor_scan(out=T["L"][:], data0=rstm[:], data1=T["lw"][:], initial=0.0, op0=ALU.mult, op1=ALU.add),
                     reads=[rstm, T["lw"]], writes=[T["L"]])
                tt("pool", T["Lx"][:], T["L"][:], T["lw"][:], ALU.subtract, [T["L"], T["lw"]], [T["Lx"]])
                b.op("act", lambda e: e.activation(out=T["Ep"][:], in_=T["L"][:], func=AF.Exp), reads=[T["L"]], writes=[T["Ep"]])
                b.op("act", lambda e: e.activation(out=T["Em"][:], in_=T["L"][:], func=AF.Exp, scale=-1.0), reads=[T["L"]], writes=[T["Em"]])
                b.op("act", lambda e: e.activation(out=T["Ex"][:], in_=T["Lx"][:], func=AF.Exp), reads=[T["Lx"]], writes=[T["Ex"]])
                c3 = lambda ap: ap.rearrange("p (c t) -> p c t", t=64)
                b.op("dve", lambda e: e.scalar_tensor_tensor(out=AR[:, :, 0, :], in0=c3(T["kkn"][:]), scalar=-1.0, in1=c3(T["Ex"][:]), op0=ALU.mult, op1=ALU.mult),
                     reads=[T["kkn"], T["Ex"]], writes=[AR])
                tt("pool", AR[:, :, 1, :], c3(R_), c3(T["Ep"][:]), ALU.mult, [X[0], T["Ep"]], [AR])
                tt("dve", T["BT"][:], T["bv"][:], T["Em"][:], ALU.mult, [T["bv"], T["Em"]], [T["BT"]])
                tt("pool", T["KT"][:], T["kp"][:], T["Em"][:], ALU.mult, [T["kp"], T["Em"]], [T["KT"]])
                gC = c3(T["Ep"][:])[:, :, 63:64].to_broadcast([64, NCH, 64])
                tt("dve", c3(T["BG"][:]), c3(T["BT"][:]), gC, ALU.mult, [T["BT"], T["Ep"]], [T["BG"]])
                tt("pool", c3(T["KG"][:]), c3(T["KT"][:]), gC, ALU.mult, [T["KT"], T["Ep"]], [T["KG"]])
                for c in range(NCH):
                    cs = slice(c * 64, (c + 1) * 64)
                    Hc = Hs[h][(gi * NCH + c) % 2]
                    Hn = Hs[h][(gi * NCH + c + 1) % 2]
                    p = nxt("pq", pq)
                    for j, (src, sb_) in enumerate([(V_[:, cs], X[2]), (T["BG"][:, cs], T["BG"]), (T["KG"][:, cs], T["KG"])]):
                        b.op("pe", lambda e: e.transpose(out=p[0:64, j * 64:(j + 1) * 64], in_=src, identity=idf[0:64, 0:64]), reads=[sb_, idf], writes=[p])
                    tm = nxt("tm", TM)
                    b.op("act", lambda e: e.copy(out=tm[:].rearrange("p a b -> p (a b)"), in_=p[0:64, 0:192]), reads=[p], writes=[tm])
                    p = nxt("pq", pq)
                    arc = AR[:, c, :, :].rearrange("p a t -> p (a t)")
                    b.op("pe", lambda e: e.matmul(p[0:64, 0:128], lhsT=T["BT"][:, cs], rhs=arc, start=True, stop=True), reads=[T["BT"], AR], writes=[p])
                    b.op("pe", lambda e: e.matmul(p[0:64, 128:256], lhsT=T["KT"][:, cs], rhs=arc, start=True, stop=True), reads=[T["KT"], AR], writes=[p])
                    b.op("pe", lambda e: e.matmul(p[0:64, 256:320], lhsT=AR[:, c, 0, :], rhs=T["BT"][:, cs], start=True, stop=True), reads=[T["BT"], AR], writes=[p])
                    xm = nxt("xm", XM)
                    tt("dve", xm[:].rearrange("p (a m) t -> p a m t", a=2), p[0:64, 0:256].rearrange("p (a m t) -> p a m t", a=2, m=2),
                       msk[:, None, 0:2, :].to_broadcast([64, 2, 2, 64]), ALU.mult, [p, msk], [xm])
                    aa = nxt("aa", AA)
                    b.op("pool", lambda e: e.tensor_copy(out=aa[:, 0, :], in_=xm[:, 0, :]), reads=[xm], writes=[aa])
                    tt("dve", aa[:, 1, :], p[0:64, 256:320], msk[:, 2, :], ALU.mult, [p, msk], [aa])
                    P_ = nxt("ppb", PP)
                    tt("pool", P_[:], xm[:, 0, :], idf[0:64, 0:64], ALU.add, [xm, idf], [P_])
                    for step in range(5):
                        pdb = nxt("pd", pd)
                        b.op("pe", lambda e: e.matmul(pdb[0:64, 0:64], lhsT=aa[:, 1, :], rhs=aa[:, 0, :], start=True, stop=True), reads=[aa], writes=[pdb])
                        b.op("pe", lambda e: e.matmul(pdb[0:64, 64:128], lhsT=aa[:, 0, :], rhs=aa[:, 1, :], start=True, stop=True), reads=[aa], writes=[pdb])
                        aa2 = nxt("aa", AA)
                        b.op("act", lambda e: e.copy(out=aa2[:].rearrange("p a t -> p (a t)"), in_=pdb[0:64, 0:128]), reads=[pdb], writes=[aa2])
                        b.op("pe", lambda e: e.matmul(pdb[0:64, 128:192], lhsT=aa2[:, 1, :], rhs=P_[:], start=True, stop=True), reads=[aa2, P_], writes=[pdb])
                        P2 = nxt("ppb", PP)
                        tt("dve", P2[:], pdb[0:64, 128:192], P_[:], ALU.add, [pdb, P_], [P2])
                        aa, P_ = aa2, P2
                    b.op("pe", lambda e: e.matmul(pz[0:64, 0:64], lhsT=xm[:, 2, :], rhs=tm[:, 0, :], start=True, stop=False), reads=[xm, tm], writes=[pz])
                    b.op("pe", lambda e: e.matmul(pz[0:64, 0:64], lhsT=AR[:, c, 0, :], rhs=Hc[:], start=False, stop=True), reads=[AR, Hc], writes=[pz])
                    b.op("act", lambda e: e.copy(out=Xs[:], in_=pz[0:64, 0:64]), reads=[pz], writes=[Xs])
                    b.op("pe", lambda e: e.matmul(pz[0:64, 64:128], lhsT=P_[:], rhs=Xs[:], start=True, stop=True), reads=[P_, Xs], writes=[pz])
                    b.op("act", lambda e: e.copy(out=Us[:], in_=pz[0:64, 64:128]), reads=[pz], writes=[Us])
                    b.op("pe", lambda e: e.matmul(pz[0:64, 128:192], lhsT=AR[:, c, 1, :], rhs=Hc[:], start=True, stop=False), reads=[AR, Hc], writes=[pz])
                    b.op("pe", lambda e: e.matmul(pz[0:64, 128:192], lhsT=xm[:, 1, :], rhs=Us[:], start=False, stop=False), reads=[xm, Us], writes=[pz])
                    b.op("pe", lambda e: e.matmul(pz[0:64, 128:192], lhsT=xm[:, 3, :], rhs=tm[:, 0, :], start=False, stop=True), reads=[xm, tm], writes=[pz])
                    b.op("pe", lambda e: e.matmul(pz[0:64, 192:256], lhsT=tm[:, 1, :], rhs=Us[:], start=True, stop=False), reads=[tm, Us], writes=[pz])
                    b.op("pe", lambda e: e.matmul(pz[0:64, 192:256], lhsT=tm[:, 2, :], rhs=tm[:, 0, :], start=False, stop=True), reads=[tm], writes=[pz])
                    b.op("act", lambda e: e.copy(out=Ytm[:, c, h, :], in_=pz[0:64, 128:192]), reads=[pz], writes=[Ytm])
                    b.op("dve", lambda e: e.scalar_tensor_tensor(out=Hn[:], in0=Hc[:], scalar=T["Ep"][:, c * 64 + 63:c * 64 + 64], in1=pz[0:64, 192:256],
                                                                 op0=ALU.mult, op1=ALU.add), reads=[Hc, T["Ep"], pz], writes=[Hn])
            Y3 = Ytm[:].rearrange("p c h i -> p (c h) i")
            S3 = sqv[:].rearrange("p c h i -> p (c h) i")
            b.op("dve", lambda e: e.tensor_reduce(out=st1[:], in_=Y3, axis=AX.X, op=ALU.add), reads=[Ytm], writes=[st1])
            b.op("pool", lambda e: e.tensor_scalar_mul(out=st1[:], in0=st1[:], scalar1=1.0 / 64), reads=[st1], writes=[st1])
            tt("dve", Y3, Y3, st1[:].unsqueeze(2).to_broadcast([64, NCH * 8, 64]), ALU.subtract, [Ytm, st1], [Ytm])
            tt("pool", S3, Y3, Y3, ALU.mult, [Ytm], [sqv])
            b.op("dve", lambda e: e.tensor_reduce(out=st2[:], in_=S3, axis=AX.X, op=ALU.add), reads=[sqv], writes=[st2])
            b.op("act", lambda e: e.activation(out=st2[:], in_=st2[:], func=AF.Sqrt, scale=1.0 / 64, bias=64e-5), reads=[st2], writes=[st2])
            b.op("dve", lambda e: e.reciprocal(out=st2[:], in_=st2[:]), reads=[st2], writes=[st2])
            tt("dve", Y3, Y3, st2[:].unsqueeze(2).to_broadcast([64, NCH * 8, 64]), ALU.mult, [Ytm, st2], [Ytm])
            lg = lng[:].rearrange("p (h i) -> p h i", i=64)[:, None, :, :].to_broadcast([64, NCH, 8, 64])
            lb = lnb[:].rearrange("p (h i) -> p h i", i=64)[:, None, :, :].to_broadcast([64, NCH, 8, 64])
            tt("pool", Ytm[:], Ytm[:], lg, ALU.mult, [Ytm, lng], [Ytm])
            tt("dve", Ytm[:], Ytm[:], lb, ALU.add, [Ytm, lnb], [Ytm])
            for h in range(8):
                p = nxt("pq", pq)
                for c in range(NCH):
                    b.op("pe", lambda e: e.transpose(out=p[0:64, c * 64:(c + 1) * 64], in_=Ytm[:, c, h, :], identity=idf[0:64, 0:64]), reads=[Ytm, idf], writes=[p])
                tt("dve", otmp[:], p[0:64, 0:TG], BV[:, h, :], ALU.add, [p, BV], [otmp])
                pg_ = nxt("pp", pp)
                b.op("pe", lambda e: e.matmul(pg_[0:64, 0:TG], lhsT=g2s[:, 0, h * 64:(h + 1) * 64], rhs=xs[:, 2, :], start=True, stop=False), reads=[g2s, xs], writes=[pg_])
                b.op("pe", lambda e: e.matmul(pg_[0:64, 0:TG], lhsT=g2s[:, 1, h * 64:(h + 1) * 64], rhs=xs[:, 3, :], start=False, stop=True), reads=[g2s, xs], writes=[pg_])
                ob_ = obf[h % 2]
                tt("dve", ob_[:], otmp[:], pg_[0:64, 0:TG], ALU.mult, [otmp, pg_], [ob_])
                b.dma("pool", self.obT_d[h // 2, (h % 2) * 64:(h % 2) * 64 + 64, q0:q0 + TG], ob_[:], reads=[ob_], writes=[self.obT_d])
        if "rwkv" in self.debug:
            d = self.dbg_out("obT", [4, 128, S], BF16)
            b.dma("pool", d, self.obT_d[:], reads=[self.obT_d])


Prog.phase_rwkv = _phase_rwkv


def build_full():
    p = Prog()
    b = p.b
    p.alloc_root()
    with b.scope():
        p.alloc_persistent()
        p.phase_nsa_proj()
        p.phase_attn()
    p.phase_rwkv2()
    p.phase_merge()
    p.phase_ffn2()
    p.finish()
    return p


def kernel(**inputs):
    p = build_full()
    consts = host_consts(inputs["rel_bias"])
    shared = {k: np.ascontiguousarray(np.asarray(inputs[k], np.float32)) for k in W_SPECS if k != "x"}
    shared.update(consts)
    x = np.asarray(inputs["x"], np.float32)
    in_maps = []
    for c in range(8):
        m = dict(shared)
        m["x"] = np.ascontiguousarray(x[c])
        in_maps.append(m)
    res = run_bass_kernel_spmd(p.nc, in_maps, core_ids=list(range(8)))
    return np.stack([np.asarray(r["out"], np.float32) for r in res.results], axis=0)


def _phase_rwkv2(self):
    b = self.b
    I = self.inp
    TG = 128
    NCH = 2
    tt = lambda eng, out, in0, in1, op, rd, wr: b.op(eng, lambda e: e.tensor_tensor(out=out, in0=in0, in1=in1, op=op), reads=rd, writes=wr)
    with b.scope():
        W1 = b.sb("W1", [128, 8, 1792], BF16)
        W2 = b.sb("W2", [128, 8, 1792], BF16)
        with b.scope():
            gat = self.load_gain("gat3", I["attn_norm_g"][0])
            stage = [b.sb(f"rst{i}", [128, 1792], F32) for i in range(2)]
            tmpw = [b.sb(f"rtw{i}", [128, 1792], F32) for i in range(2)]
            mur = self.bcast_row("mur", I["rwkv_mu"][0], 1792)
            for c in range(8):
                st = stage[c % 2]
                tw_ = tmpw[c % 2]
                b.dma("sp", st[:], I["w_in"][0][c * 128:(c + 1) * 128, RW0:RW0 + 1792], writes=[st])
                tt("dve", tw_[:], st[:], mur[:], ALU.mult, [st, mur], [tw_])
                b.op("act", lambda e: e.activation(out=W2[:, c, :], in_=tw_[:], func=AF.Copy, scale=gat[:, c:c + 1]), reads=[tw_, gat], writes=[W2])
                tt("pool", st[:], st[:], tw_[:], ALU.subtract, [st, tw_], [st])
                b.op("act", lambda e: e.activation(out=W1[:, c, :], in_=st[:], func=AF.Copy, scale=gat[:, c:c + 1]), reads=[st, gat], writes=[W1])

        def colvec(name, src, n):
            t = b.sb(name, [64, n], F32)
            b.dma("sp", t[:], src.rearrange("(c p) -> p c", p=64), writes=[t], allow_slow_non_contiguous=True)
            return t
        w0 = colvec("w0", I["rwkv_w0"][0], 8)
        a0 = colvec("a0", I["rwkv_a0"][0], 8)
        k_k = colvec("k_k", I["rwkv_k_k"][0], 8)
        k_a = colvec("k_a", I["rwkv_k_a"][0], 8)
        r_k = colvec("r_k", I["rwkv_r_k"][0].rearrange("h d -> (h d)"), 8)
        w2s = b.sb("w2s", [64, 512], F32)
        a2s = b.sb("a2s", [64, 512], F32)
        g2s = b.sb("g2s", [64, 2, 512], F32)
        b.dma("sp", w2s[:], I["rwkv_w2"][0], writes=[w2s])
        b.dma("sp", a2s[:], I["rwkv_a2"][0], writes=[a2s])
        b.dma("sp", g2s[:], I["rwkv_g2"][0].rearrange("(two l) f -> l two f", two=2), writes=[g2s])
        lng = b.sb("lng", [64, 512], F32)
        lnb = b.sb("lnb", [64, 512], F32)
        b.dma("sp", lng[:], I["rwkv_ln_g"][0].partition_broadcast(64), writes=[lng])
        b.dma("sp", lnb[:], I["rwkv_ln_b"][0].partition_broadcast(64), writes=[lnb])
        msk = b.sb("rmsk", [64, 3, 64], F32)
        b.dma("sp", msk[:], I["rwmask"], writes=[msk])
        rstm = b.sb("rstm", [64, 8 * TG], F32)
        b.dma("sp", rstm[:], I["rwreset"], writes=[rstm])
        ones = b.sb("ones64", [64, 64], F32)
        b.op("pool", lambda e: e.memset(ones[:], 1.0), writes=[ones])
        idf = self.identf
        Hst = b.sb("rH", [64, 2, 8, 64], F32)
        b.op("pool", lambda e: e.memset(Hst[:], 0.0), writes=[Hst])
        xt = [b.sb(f"rxt{i}", [128, D], F32) for i in range(1)] * 2
        junk = b.sb("rjunk", [128, D], BF16)
        ss = [b.sb(f"rss{i}", [128, 1], F32) for i in range(1)] * 2
        hb = [b.sb(f"rhb{i}", [128, D], BF16) for i in range(1)] * 2
        hT1 = [b.sb(f"rhT{i}", [128, 8, 128], BF16) for i in range(1)] * 2
        hTs = b.sb("rhTs", [128, 8, TG + 1], BF16)
        b.op("pool", lambda e: e.memset(hTs[:], 0.0), writes=[hTs])
        XL = b.sb("rXL", [64, 20, TG], F32)
        Vtm = b.sb("rVtm", [64, NCH, 512], F32)
        names = ["LW", "AS", "KKN", "BVc", "KP", "RK", "L", "EP", "EM", "BG", "KG"]
        T = {n: b.sb("r" + n, [64, 8, TG], F32) for n in names}
        T["NR"] = T["RK"]
        T["T1"] = T["BG"]
        T["KK"] = T["KG"]
        T["EX"] = T["L"]
        T["BT"] = T["LW"]
        T["KT"] = T["AS"]
        AR = b.sb("rAR", [64, 8, NCH, 2, 64], F32)
        BON = b.sb("rBON", [64, NCH * 8], F32)
        Ytm = b.sb("rYtm", [64, NCH, 8, 64], F32)
        sqv = b.sb("rsqv", [64, NCH, 8, 64], F32)
        st1 = b.sb("rst1", [64, NCH * 8], F32)
        st2 = b.sb("rst2", [64, NCH * 8], F32)
        TM4 = [b.sb(f"rTM{i}", [64, 4, 2, 64], F32) for i in range(2)]
        XM4 = [b.sb(f"rXM{i}", [64, 4, 4, 64], F32) for i in range(2)]
        AA4 = [b.sb(f"rAA{i}", [64, 4, 2, 64], F32) for i in range(2)]
        PP4 = [b.sb(f"rPP{i}", [64, 4, 64], F32) for i in range(2)]
        Xs4 = b.sb("rXs4", [64, 4, 64], F32)
        Us4 = b.sb("rUs4", [64, 4, 64], F32)
        Ht4 = b.sb("rHt4", [64, 4, 64], F32)
        OBb = b.sb("rOBb", [64, NCH, 512], BF16)
        obT = [b.sb(f"robT{i}", [128, 4, TG], BF16) for i in range(1)] * 2
        pt = b.ps("rpt", [128, 8, 128], BF16)
        pP = b.ps("rpP", [128, 512], F32)
        pA = b.ps("rpA", [128, 1024], F32)
        pB = b.ps("rpB", [128, 512], F32)
        pC = b.ps("rpC", [128, 512], F32)
        pD = b.ps("rpD", [128, 512], F32)
        pZ = b.ps("rpZ", [128, 512], F32)
        cnt = {}

        def nxt(k, lst):
            cnt[k] = cnt.get(k, 0) + 1
            return lst[cnt[k] % len(lst)]
        bc = lambda v: v[:].unsqueeze(2).to_broadcast([64, 8, TG])
        f2 = lambda t_: t_[:].rearrange("p h t -> p (h t)")
        c16 = lambda t_: t_[:].rearrange("p h (c t) -> p (h c) t", t=64)

        ngr = getattr(self, "nrg_limit", S // TG)
        for gi in range(ngr):
            q0 = gi * TG
            i = gi % 2
            self.make_hT(I["x"], gi, xt[i], junk, ss[i], hb[i], pt, hT1[i], self.ident)
            b.op("pool", lambda e: e.tensor_copy(out=hTs[:, :, 0:1], in_=hTs[:, :, TG:TG + 1]), reads=[hTs], writes=[hTs])
            b.op("pool", lambda e: e.tensor_copy(out=hTs[:, :, 1:TG + 1], in_=hT1[i][:]), reads=[hT1[i]], writes=[hTs])
            ftiles = list(range(0, 16)) + [24, 25, 26, 27]
            for q4 in range(5):
                for j in range(4):
                    fc = ftiles[q4 * 4 + j]
                    for c in range(8):
                        b.op("pe", lambda e: e.matmul(pP[0:64, j * TG:(j + 1) * TG], lhsT=W1[:, c, fc * 64:(fc + 1) * 64], rhs=hTs[:, c, 1:TG + 1], start=(c == 0), stop=False),
                             reads=[W1, hTs], writes=[pP])
                    for c in range(8):
                        b.op("pe", lambda e: e.matmul(pP[0:64, j * TG:(j + 1) * TG], lhsT=W2[:, c, fc * 64:(fc + 1) * 64], rhs=hTs[:, c, 0:TG], start=False, stop=(c == 7)),
                             reads=[W2, hTs], writes=[pP])
                b.op("act", lambda e: e.copy(out=XL[:, q4 * 4:(q4 + 1) * 4, :].rearrange("p a t -> p (a t)"), in_=pP[0:64, :]), reads=[pP], writes=[XL])
            for c_ in range(NCH):
                for c in range(8):
                    b.op("pe", lambda e: e.matmul(pP[0:64, :], lhsT=hTs[:, c, 1 + c_ * 64:1 + (c_ + 1) * 64], rhs=W1[:, c, 1024:1536], start=(c == 0), stop=False),
                         reads=[W1, hTs], writes=[pP])
                for c in range(8):
                    b.op("pe", lambda e: e.matmul(pP[0:64, :], lhsT=hTs[:, c, c_ * 64:(c_ + 1) * 64], rhs=W2[:, c, 1024:1536], start=False, stop=(c == 7)),
                         reads=[W2, hTs], writes=[pP])
                b.op("act", lambda e: e.copy(out=Vtm[:, c_, :], in_=pP[0:64, :]), reads=[pP], writes=[Vtm])
            R_ = XL[:, 0:8, :]
            K_ = XL[:, 8:16, :]
            b.op("act", lambda e: e.activation(out=XL[:, 16, :], in_=XL[:, 16, :], func=AF.Tanh), reads=[XL], writes=[XL])
            b.op("act", lambda e: e.activation(out=XL[:, 18:20, :], in_=XL[:, 18:20, :], func=AF.Sigmoid), reads=[XL], writes=[XL])
            for (ws_, src, bias_, dst) in [(w2s, 16, w0, "LW"), (a2s, 17, a0, "AS")]:
                for half in range(2):
                    for j in range(4):
                        h = half * 4 + j
                        b.op("pe", lambda e: e.matmul(pP[0:64, j * TG:(j + 1) * TG], lhsT=ws_[:, h * 64:(h + 1) * 64], rhs=XL[:, src, :], start=True, stop=True),
                             reads=[ws_, XL], writes=[pP])
                    for j in range(4):
                        h = half * 4 + j
                        b.op("act", lambda e: e.activation(out=T[dst][:, h, :], in_=pP[0:64, j * TG:(j + 1) * TG], func=AF.Sigmoid, bias=bias_[:, h:h + 1]),
                             reads=[pP, bias_], writes=[T[dst]])
            b.op("pool", lambda e: e.tensor_scalar_mul(out=f2(T["LW"]), in0=f2(T["LW"]), scalar1=-0.6065306597126334), reads=[T["LW"]], writes=[T["LW"]])
            tt("dve", T["KK"][:], K_, bc(k_k), ALU.mult, [XL, k_k], [T["KK"]])
            tt("pool", T["NR"][:], T["KK"][:], T["KK"][:], ALU.mult, [T["KK"]], [T["NR"]])
            for half in range(2):
                b.op("pe", lambda e: e.matmul(pP[0:64, :], lhsT=ones[:], rhs=T["NR"][:, half * 4:(half + 1) * 4, :].rearrange("p h t -> p (h t)"), start=True, stop=True),
                     reads=[ones, T["NR"]], writes=[pP])
                b.op("act", lambda e: e.activation(out=T["KKN"][:, half * 4:(half + 1) * 4, :].rearrange("p h t -> p (h t)"), in_=pP[0:64, :], func=AF.Sqrt),
                     reads=[pP], writes=[T["KKN"]])
            b.op("dve", lambda e: e.tensor_scalar_max(out=f2(T["KKN"]), in0=f2(T["KKN"]), scalar1=1e-12), reads=[T["KKN"]], writes=[T["KKN"]])
            b.op("dve", lambda e: e.reciprocal(out=f2(T["KKN"]), in_=f2(T["KKN"])), reads=[T["KKN"]], writes=[T["KKN"]])
            tt("dve", T["KKN"][:], T["KKN"][:], T["KK"][:], ALU.mult, [T["KKN"], T["KK"]], [T["KKN"]])
            tt("pool", T["BVc"][:], T["KKN"][:], T["AS"][:], ALU.mult, [T["KKN"], T["AS"]], [T["BVc"]])
            b.op("pool", lambda e: e.tensor_scalar_add(out=f2(T["T1"]), in0=f2(T["AS"]), scalar1=-1.0), reads=[T["AS"]], writes=[T["T1"]])
            tt("pool", T["T1"][:], T["T1"][:], bc(k_a), ALU.mult, [T["T1"], k_a], [T["T1"]])
            b.op("dve", lambda e: e.scalar_tensor_tensor(out=f2(T["KP"]), in0=f2(T["T1"]), scalar=1.0, in1=K_.rearrange("p h t -> p (h t)"), op0=ALU.add, op1=ALU.mult),
                 reads=[T["T1"], XL], writes=[T["KP"]])
            tt("pool", T["RK"][:], R_, T["KP"][:], ALU.mult, [XL, T["KP"]], [T["RK"]])
            tt("pool", T["RK"][:], T["RK"][:], bc(r_k), ALU.mult, [T["RK"], r_k], [T["RK"]])
            for c_ in range(NCH):
                for h in range(8):
                    b.op("pe", lambda e: e.matmul(pD[0:64, c_ * 8 + h:c_ * 8 + h + 1], lhsT=T["RK"][:, h, c_ * 64:(c_ + 1) * 64], rhs=ones[:, 0:1], start=True, stop=True),
                         reads=[T["RK"], ones], writes=[pD])
            b.op("act", lambda e: e.copy(out=BON[:], in_=pD[0:64, 0:NCH * 8]), reads=[pD], writes=[BON])
            b.op("dve", lambda e: e.tensor_tensor_scan(out=f2(T["L"]), data0=rstm[:], data1=f2(T["LW"]), initial=0.0, op0=ALU.mult, op1=ALU.add),
                 reads=[rstm, T["LW"]], writes=[T["L"]])
            b.op("act", lambda e: e.activation(out=f2(T["EP"]), in_=f2(T["L"]), func=AF.Exp), reads=[T["L"]], writes=[T["EP"]])
            b.op("act", lambda e: e.activation(out=f2(T["EM"]), in_=f2(T["L"]), func=AF.Exp, scale=-1.0), reads=[T["L"]], writes=[T["EM"]])
            tt("pool", T["L"][:], T["L"][:], T["LW"][:], ALU.subtract, [T["L"], T["LW"]], [T["L"]])
            b.op("act", lambda e: e.activation(out=f2(T["EX"]), in_=f2(T["L"]), func=AF.Exp), reads=[T["L"]], writes=[T["EX"]])
            ar0 = AR[:, :, :, 0, :].rearrange("p h c t -> p (h c) t")
            ar1 = AR[:, :, :, 1, :].rearrange("p h c t -> p (h c) t")
            b.op("dve", lambda e: e.scalar_tensor_tensor(out=ar0, in0=c16(T["KKN"]), scalar=-1.0, in1=c16(T["EX"]), op0=ALU.mult, op1=ALU.mult),
                 reads=[T["KKN"], T["EX"]], writes=[AR])
            tt("pool", ar1, R_.rearrange("p h (c t) -> p (h c) t", t=64), c16(T["EP"]), ALU.mult, [XL, T["EP"]], [AR])
            tt("dve", T["BT"][:], T["BVc"][:], T["EM"][:], ALU.mult, [T["BVc"], T["EM"]], [T["BT"]])
            tt("pool", T["KT"][:], T["KP"][:], T["EM"][:], ALU.mult, [T["KP"], T["EM"]], [T["KT"]])
            gC = c16(T["EP"])[:, :, 63:64].to_broadcast([64, 16, 64])
            tt("dve", c16(T["BG"]), c16(T["BT"]), gC, ALU.mult, [T["BT"], T["EP"]], [T["BG"]])
            tt("pool", c16(T["KG"]), c16(T["KT"]), gC, ALU.mult, [T["KT"], T["EP"]], [T["KG"]])
            for c_ in range(NCH):
                cs = slice(c_ * 64, (c_ + 1) * 64)
                cur = (gi * NCH + c_) % 2
                for hb_ in range(2):
                    heads = list(range(hb_ * 4, hb_ * 4 + 4))
                    for j, h in enumerate(heads):
                        b.op("pe", lambda e: e.transpose(out=pC[0:64, j * 128:j * 128 + 64], in_=T["BG"][:, h, cs], identity=idf[0:64, 0:64]), reads=[T["BG"], idf], writes=[pC])
                        b.op("pe", lambda e: e.transpose(out=pC[0:64, j * 128 + 64:(j + 1) * 128], in_=T["KG"][:, h, cs], identity=idf[0:64, 0:64]), reads=[T["KG"], idf], writes=[pC])
                    tm = nxt("tm", TM4)
                    b.op("act", lambda e: e.copy(out=tm[:].rearrange("p h a t -> p (h a t)"), in_=pC[0:64, 0:512]), reads=[pC], writes=[tm])
                    for j, h in enumerate(heads):
                        arc = AR[:, h, c_, :, :].rearrange("p a t -> p (a t)")
                        b.op("pe", lambda e: e.matmul(pA[0:64, j * 256:j * 256 + 128], lhsT=T["BT"][:, h, cs], rhs=arc, start=True, stop=True), reads=[T["BT"], AR], writes=[pA])
                        b.op("pe", lambda e: e.matmul(pA[0:64, j * 256 + 128:(j + 1) * 256], lhsT=T["KT"][:, h, cs], rhs=arc, start=True, stop=True), reads=[T["KT"], AR], writes=[pA])
                        b.op("pe", lambda e: e.matmul(pB[0:64, j * 64:(j + 1) * 64], lhsT=AR[:, h, c_, 0, :], rhs=T["BT"][:, h, cs], start=True, stop=True), reads=[T["BT"], AR], writes=[pB])
                    xm = nxt("xm", XM4)
                    tt("dve", xm[:].rearrange("p h (a m) t -> p (h a) m t", a=2), pA[0:64, :].rearrange("p (ha m t) -> p ha m t", m=2, t=64),
                       msk[:, None, 0:2, :].to_broadcast([64, 8, 2, 64]), ALU.mult, [pA, msk], [xm])
                    aa = nxt("aa", AA4)
                    b.op("pool", lambda e: e.tensor_copy(out=aa[:, :, 0, :], in_=xm[:, :, 0, :]), reads=[xm], writes=[aa])
                    tt("dve", aa[:, :, 1, :], pB[0:64, 0:256].rearrange("p (h t) -> p h t", t=64), msk[:, 2:3, :].to_broadcast([64, 4, 64]), ALU.mult, [pB, msk], [aa])
                    P_ = nxt("pp4", PP4)
                    tt("pool", P_[:], xm[:, :, 0, :], idf[0:64, None, 0:64].to_broadcast([64, 4, 64]), ALU.add, [xm, idf], [P_])
                    for step in range(5):
                        for j in range(4):
                            b.op("pe", lambda e: e.matmul(pD[0:64, j * 128:j * 128 + 64], lhsT=aa[:, j, 1, :], rhs=aa[:, j, 0, :], start=True, stop=True), reads=[aa], writes=[pD])
                            b.op("pe", lambda e: e.matmul(pD[0:64, j * 128 + 64:(j + 1) * 128], lhsT=aa[:, j, 0, :], rhs=aa[:, j, 1, :], start=True, stop=True), reads=[aa], writes=[pD])
                        aa2 = nxt("aa", AA4)
                        b.op("act", lambda e: e.copy(out=aa2[:].rearrange("p h a t -> p (h a t)"), in_=pD[0:64, :]), reads=[pD], writes=[aa2])
                        for j in range(4):
                            b.op("pe", lambda e: e.matmul(pB[0:64, 256 + j * 64:256 + (j + 1) * 64], lhsT=aa2[:, j, 1, :], rhs=P_[:, j, :], start=True, stop=True), reads=[aa2, P_], writes=[pB])
                        P2 = nxt("pp4", PP4)
                        tt("dve", P2[:], pB[0:64, 256:512].rearrange("p (h t) -> p h t", t=64), P_[:], ALU.add, [pB, P_], [P2])
                        aa, P_ = aa2, P2
                    for j, h in enumerate(heads):
                        b.op("pe", lambda e: e.matmul(pZ[0:64, j * 64:(j + 1) * 64], lhsT=xm[:, j, 2, :], rhs=Vtm[:, c_, h * 64:(h + 1) * 64], start=True, stop=False), reads=[xm, Vtm], writes=[pZ])
                        b.op("pe", lambda e: e.matmul(pZ[0:64, j * 64:(j + 1) * 64], lhsT=AR[:, h, c_, 0, :], rhs=Hst[:, cur, h, :], start=False, stop=True), reads=[AR, Hst], writes=[pZ])
                    b.op("act", lambda e: e.copy(out=Xs4[:].rearrange("p h t -> p (h t)"), in_=pZ[0:64, 0:256]), reads=[pZ], writes=[Xs4])
                    for j in range(4):
                        b.op("pe", lambda e: e.matmul(pZ[0:64, 256 + j * 64:256 + (j + 1) * 64], lhsT=P_[:, j, :], rhs=Xs4[:, j, :], start=True, stop=True), reads=[P_, Xs4], writes=[pZ])
                    b.op("act", lambda e: e.copy(out=Us4[:].rearrange("p h t -> p (h t)"), in_=pZ[0:64, 256:512]), reads=[pZ], writes=[Us4])
                    for j, h in enumerate(heads):
                        o = slice(j * 64, (j + 1) * 64)
                        vh = Vtm[:, c_, h * 64:(h + 1) * 64]
                        b.op("pe", lambda e: e.matmul(pZ[0:64, o], lhsT=AR[:, h, c_, 1, :], rhs=Hst[:, cur, h, :], start=True, stop=False), reads=[AR, Hst], writes=[pZ])
                        b.op("pe", lambda e: e.matmul(pZ[0:64, o], lhsT=xm[:, j, 1, :], rhs=Us4[:, j, :], start=False, stop=False), reads=[xm, Us4], writes=[pZ])
                        b.op("pe", lambda e: e.matmul(pZ[0:64, o], lhsT=xm[:, j, 3, :], rhs=vh, start=False, stop=True), reads=[xm, Vtm], writes=[pZ])
                    for j, h in enumerate(heads):
                        o = slice(256 + j * 64, 256 + (j + 1) * 64)
                        vh = Vtm[:, c_, h * 64:(h + 1) * 64]
                        b.op("pe", lambda e: e.matmul(pZ[0:64, o], lhsT=tm[:, j, 0, :], rhs=Us4[:, j, :], start=True, stop=False), reads=[tm, Us4], writes=[pZ])
                        b.op("pe", lambda e: e.matmul(pZ[0:64, o], lhsT=tm[:, j, 1, :], rhs=vh, start=False, stop=True), reads=[tm, Vtm], writes=[pZ])
                    b.op("act", lambda e: e.copy(out=Ytm[:, c_, hb_ * 4:(hb_ + 1) * 4, :].rearrange("p h t -> p (h t)"), in_=pZ[0:64, 0:256]), reads=[pZ], writes=[Ytm])
                    gH = T["EP"][:, hb_ * 4:(hb_ + 1) * 4, c_ * 64 + 63:c_ * 64 + 64].to_broadcast([64, 4, 64])
                    tt("pool", Ht4[:], Hst[:, cur, hb_ * 4:(hb_ + 1) * 4, :], gH, ALU.mult, [Hst, T["EP"]], [Ht4])
                    tt("dve", Hst[:, 1 - cur, hb_ * 4:(hb_ + 1) * 4, :], pZ[0:64, 256:512].rearrange("p (h t) -> p h t", t=64), Ht4[:], ALU.add, [pZ, Ht4], [Hst])
            Y3 = Ytm[:].rearrange("p c h i -> p (c h) i")
            S3 = sqv[:].rearrange("p c h i -> p (c h) i")
            b.op("dve", lambda e: e.tensor_reduce(out=st1[:], in_=Y3, axis=AX.X, op=ALU.add), reads=[Ytm], writes=[st1])
            b.op("pool", lambda e: e.tensor_scalar_mul(out=st1[:], in0=st1[:], scalar1=1.0 / 64), reads=[st1], writes=[st1])
            tt("dve", Y3, Y3, st1[:].unsqueeze(2).to_broadcast([64, NCH * 8, 64]), ALU.subtract, [Ytm, st1], [Ytm])
            tt("pool", S3, Y3, Y3, ALU.mult, [Ytm], [sqv])
            b.op("dve", lambda e: e.tensor_reduce(out=st2[:], in_=S3, axis=AX.X, op=ALU.add), reads=[sqv], writes=[st2])
            b.op("act", lambda e: e.activation(out=st2[:], in_=st2[:], func=AF.Sqrt, scale=1.0 / 64, bias=64e-5), reads=[st2], writes=[st2])
            b.op("dve", lambda e: e.reciprocal(out=st2[:], in_=st2[:]), reads=[st2], writes=[st2])
            tt("dve", Y3, Y3, st2[:].unsqueeze(2).to_broadcast([64, NCH * 8, 64]), ALU.mult, [Ytm, st2], [Ytm])
            lg = lng[:].rearrange("p (h i) -> p h i", i=64)[:, None, :, :].to_broadcast([64, NCH, 8, 64])
            lb = lnb[:].rearrange("p (h i) -> p h i", i=64)[:, None, :, :].to_broadcast([64, NCH, 8, 64])
            tt("pool", Ytm[:], Ytm[:], lg, ALU.mult, [Ytm, lng], [Ytm])
            tt("dve", Ytm[:], Ytm[:], lb, ALU.add, [Ytm, lnb], [Ytm])
            V3 = Vtm[:].rearrange("p c (h i) -> p (c h) i", i=64)
            tt("pool", S3, V3, BON[:].unsqueeze(2).to_broadcast([64, NCH * 8, 64]), ALU.mult, [Vtm, BON], [sqv])
            tt("dve", Y3, Y3, S3, ALU.add, [Ytm, sqv], [Ytm])
            for c_ in range(NCH):
                for two in range(2):
                    b.op("pe", lambda e: e.matmul(pP[0:64, :], lhsT=XL[:, 18 + two, c_ * 64:(c_ + 1) * 64], rhs=g2s[:, two, :], start=(two == 0), stop=(two == 1)),
                         reads=[XL, g2s], writes=[pP])
                tt("dve", OBb[:, c_, :], Ytm[:, c_, :, :].rearrange("p h i -> p (h i)"), pP[0:64, :], ALU.mult, [Ytm, pP], [OBb])
                for k4 in range(4):
                    b.op("pe", lambda e: e.transpose(out=pt[:, k4, c_ * 64:(c_ + 1) * 64], in_=OBb[:, c_, k4 * 128:(k4 + 1) * 128], identity=self.ident[0:64, 0:64]),
                         reads=[OBb, self.ident], writes=[pt])
            ot = obT[gi % 2]
            b.op("act", lambda e: e.copy(out=ot[:], in_=pt[:, 0:4, :]), reads=[pt], writes=[ot])
            b.dma("pool", self.obT_d[:, :, q0:q0 + TG].rearrange("c p t -> p c t"), ot[:], reads=[ot], writes=[self.obT_d])
        if "rwkv" in self.debug:
            d = self.dbg_out("obT", [4, 128, S], BF16)
            b.dma("pool", d, self.obT_d[:], reads=[self.obT_d])


Prog.phase_rwkv2 = _phase_rwkv2


def _phase_ffn2(self):
    b = self.b
    I = self.inp
    TG = 256
    NFT = 44
    with b.scope():
        gf = self.load_gain("gf", I["ffn_norm_g"][0])
        stage = [b.sb(f"fst{i}", [128, 1024], F32) for i in range(2)]
        wu = b.sb("wu", [128, 8, 2 * DFF], BF16)
        for n in range(8):
            for c in range(8):
                st = stage[c % 2]
                b.dma("sp", st[:, 0:704], I["w_up"][0][c * 128:(c + 1) * 128, n * 704:(n + 1) * 704], writes=[st])
                b.op("act", lambda e: e.activation(out=wu[:, c, n * 704:(n + 1) * 704], in_=st[:, 0:704], func=AF.Copy, scale=gf[:, c:c + 1]),
                     reads=[st, gf], writes=[wu])
        wd = b.sb("wd", [128, 22, D], BF16)
        self.load_weight(wd, I["w_down"][0], D, kch=22, stage=stage, eng="dve")
        cw = b.sb("cw", [128, 3, NFT], F32)
        for j in range(3):
            b.dma("sp", cw[:, j, :], I["conv_w"][0][j].rearrange("(c p) -> p c", p=128), writes=[cw], allow_slow_non_contiguous=True)
        cbias = self.load_gain("cbias", I["conv_b"][0], kch=NFT)
        xt = [b.sb(f"fxt{i}", [128, D], F32) for i in range(2)]
        junk = b.sb("fjunk", [128, D], BF16)
        ss = [b.sb(f"fss{i}", [128, 1], F32) for i in range(2)]
        hb = [b.sb(f"fhb{i}", [128, D], BF16) for i in range(2)]
        hTg = b.sb("fhTg", [128, 8, TG + 2], BF16)
        b.op("pool", lambda e: e.memset(hTg[:], 0.0), writes=[hTg])
        cv = [b.sb(f"cv{i}", [128, TG], F32) for i in range(3)]
        sgl = [b.sb(f"sgl{i}", [128, TG], BF16) for i in range(2)]
        actT = b.sb("actT", [128, 22, TG], BF16)
        val = b.sb("fval", [128, 22, TG], BF16)
        pt = b.ps("fpt", [128, 8, 128], BF16)
        pu = [b.ps(f"fpu{i}", [128, 512], F32) for i in range(4)]
        pd = [b.ps(f"fpd{i}", [128, 512], F32) for i in range(2)]
        ng = getattr(self, "nt_limit", NT) * 128 // TG
        for gi in range(ng):
            b.op("pool", lambda e: e.tensor_copy(out=hTg[:, :, 0:2], in_=hTg[:, :, TG:TG + 2]), reads=[hTg], writes=[hTg])
            for s_ in range(TG // 128):
                t = gi * (TG // 128) + s_
                self.make_hT(self.x1_d, t, xt[s_], junk, ss[s_], hb[s_], pt, hTg, self.ident, hT_ap=hTg[:, :, 2 + s_ * 128:2 + (s_ + 1) * 128])
            for ft in range(NFT):
                p = pu[ft % 4]
                c_ = cv[ft % 3]
                for c in range(8):
                    b.op("pe", lambda e: e.matmul(p[:, 0:TG + 2], lhsT=wu[:, c, ft * 128:(ft + 1) * 128], rhs=hTg[:, c, :], start=(c == 0), stop=(c == 7)),
                         reads=[wu, hTg], writes=[p])
                b.op("act", lambda e: e.activation(out=c_[:], in_=p[:, 0:TG], func=AF.Identity, scale=cw[:, 0, ft:ft + 1], bias=cbias[:, ft:ft + 1]),
                     reads=[p, cw, cbias], writes=[c_])
                b.op("dve", lambda e: e.scalar_tensor_tensor(out=c_[:], in0=p[:, 1:TG + 1], scalar=cw[:, 1, ft:ft + 1], in1=c_[:], op0=ALU.mult, op1=ALU.add),
                     reads=[p, cw, c_], writes=[c_])
                if ft < 22:
                    b.op("dve", lambda e: e.scalar_tensor_tensor(out=val[:, ft, :], in0=p[:, 2:TG + 2], scalar=cw[:, 2, ft:ft + 1], in1=c_[:], op0=ALU.mult, op1=ALU.add),
                         reads=[p, cw, c_], writes=[val])
                else:
                    sg_ = sgl[ft % 2]
                    b.op("dve", lambda e: e.scalar_tensor_tensor(out=c_[:], in0=p[:, 2:TG + 2], scalar=cw[:, 2, ft:ft + 1], in1=c_[:], op0=ALU.mult, op1=ALU.add),
                         reads=[p, cw, c_], writes=[c_])
                    b.op("act", lambda e: e.activation(out=sg_[:], in_=c_[:], func=AF.Silu), reads=[c_], writes=[sg_])
                    b.op("pool", lambda e: e.tensor_tensor(out=actT[:, ft - 22, :], in0=sg_[:], in1=val[:, ft - 22, :], op=ALU.mult),
                         reads=[sg_, val], writes=[actT])
            for s_ in range(TG // 128):
                t = gi * (TG // 128) + s_
                for n in range(2):
                    for f in range(22):
                        b.op("pe", lambda e: e.matmul(pd[n][:, :], lhsT=actT[:, f, s_ * 128:(s_ + 1) * 128], rhs=wd[:, f, n * 512:(n + 1) * 512], start=(f == 0), stop=(f == 21)),
                             reads=[actT, wd], writes=[pd[n]])
                    b.op("dve", lambda e: e.tensor_tensor(out=xt[s_][:, n * 512:(n + 1) * 512], in0=pd[n][:, :], in1=xt[s_][:, n * 512:(n + 1) * 512], op=ALU.add),
                         reads=[pd[n], xt[s_]], writes=[xt[s_]])
                b.dma("pool", self.out[t * 128:(t + 1) * 128, :], xt[s_][:], reads=[xt[s_]])


Prog.phase_ffn2 = _phase_ffn2
```

```python
import contextlib
import numpy as np
import ml_dtypes
import concourse.bass as bass
import concourse.mybir as mybir
from concourse.bass_utils import run_bass_kernel_spmd

F32 = mybir.dt.float32
BF16 = mybir.dt.bfloat16
AF = mybir.ActivationFunctionType
ALU = mybir.AluOpType
AX = mybir.AxisListType

S = 4096
D = 1024
NT = S // 128
IN_WIDTH = 5144
RW0 = 1304
GA0 = 3096
GB0 = 4120
DFF = 2816
RMS_EPS = 1e-6


class Buf:
    def __init__(self, t, name):
        self.t = t
        self.name = name
        self.w = None
        self.r = {}
        self.psum = False

    def __getitem__(self, idx):
        return self.t[idx]


class Builder:
    SEM_ROLL = 30000

    def __init__(self, nc):
        self.nc = nc
        self.stack = contextlib.ExitStack()
        self.root = self.stack
        self.eng = {"pe": nc.tensor, "act": nc.scalar, "dve": nc.vector,
                    "pool": nc.gpsimd, "sp": nc.sync}
        self.sem = {}
        self.cnt = {}
        self.seen = {e: {} for e in self.eng}
        self.nsem = 0
        self.lanes = {}
        self.lane_rr = {}
        self.last_tok = {}
        for e in self.eng:
            self._roll(e)

    def newsem(self, name):
        self.nsem += 1
        return self.root.enter_context(self.nc.semaphore(f"{name}_{self.nsem}"))

    def sb(self, name, shape, dt=F32):
        self.nsem += 1
        name = f"sb{self.nsem}_{name}"
        return Buf(self.stack.enter_context(self.nc.sbuf_tensor(name, list(shape), dt)), name)

    def ps(self, name, shape, dt=F32):
        self.nsem += 1
        name = f"ps{self.nsem}_{name}"
        bf = Buf(self.stack.enter_context(self.nc.psum_tensor(name, list(shape), dt)), name)
        bf.psum = True
        return bf

    def dram(self, name, shape, dt=F32, kind="Internal"):
        return Buf(self.nc.dram_tensor(name, list(shape), dt, kind=kind), name)

    def _roll(self, e):
        self.sem[e] = self.newsem("s" + e)
        self.cnt[e] = 0

    def _wait(self, e, tok):
        sem, val = tok
        k = id(sem)
        if self.seen[e].get(k, 0) < val:
            self.eng[e].wait_ge(sem, val)
            self.seen[e][k] = val

    def _deps(self, e, reads, writes):
        for b in reads:
            if b.w is not None:
                we, tok = b.w
                self._wait(e, tok)
            if b.psum:
                for re_, tok in b.r.items():
                    if re_ != e:
                        self._wait(e, tok)
        for b in writes:
            if b.w is not None:
                we, tok = b.w
                if we != e:
                    self._wait(e, tok)
            for re_, tok in b.r.items():
                if re_ != e:
                    self._wait(e, tok)

    def op(self, e, fn, reads=(), writes=()):
        if self.cnt[e] >= self.SEM_ROLL:
            self._roll(e)
        self._deps(e, reads, writes)
        ins = fn(self.eng[e])
        self.cnt[e] += 1
        tok = (self.sem[e], self.cnt[e])
        ins.then_inc(self.sem[e], 1)
        self.last_tok[e] = tok
        for b in reads:
            b.r[e] = tok
        for b in writes:
            b.w = (e, tok)
            b.r = {}
        return tok

    def dma(self, q, out, in_, reads=(), writes=(), nlanes=6, **kw):
        if q not in self.lanes:
            self.lanes[q] = [[self.newsem("l" + q), 0] for _ in range(nlanes)]
            self.lane_rr[q] = 0
        li = self.lane_rr[q]
        self.lane_rr[q] = (li + 1) % len(self.lanes[q])
        lane = self.lanes[q][li]
        if lane[1] >= 1800:
            self._wait(q, (lane[0], 16 * lane[1]))
            lane[0] = self.newsem("l" + q)
            lane[1] = 0
        if lane[1] > 0:
            self._wait(q, (lane[0], 16 * lane[1]))
        self._deps_dma(q, reads, writes)
        ins = self.eng[q].dma_start(out=out, in_=in_, **kw)
        lane[1] += 1
        tok = (lane[0], 16 * lane[1])
        ins.then_inc(lane[0], 16)
        key = "dma_" + q + str(li)
        for b in reads:
            b.r[key] = tok
        for b in writes:
            b.w = (key, tok)
            b.r = {}
        return tok

    def _deps_dma(self, q, reads, writes):
        for b in reads:
            if b.w is not None:
                self._wait(q, b.w[1])
        for b in writes:
            if b.w is not None:
                self._wait(q, b.w[1])
            for re_, tok in b.r.items():
                self._wait(q, tok)

    def barrier(self):
        toks = list(self.last_tok.values())
        for q, lanes in self.lanes.items():
            for lane in lanes:
                if lane[1] > 0:
                    toks.append((lane[0], 16 * lane[1]))
        for e in self.eng:
            for tok in toks:
                self._wait(e, tok)

    def wait_all_on(self, e):
        toks = list(self.last_tok.values())
        for q, lanes in self.lanes.items():
            for lane in lanes:
                if lane[1] > 0:
                    toks.append((lane[0], 16 * lane[1]))
        for tok in toks:
            self._wait(e, tok)

    @contextlib.contextmanager
    def scope(self):
        old = self.stack
        self.stack = contextlib.ExitStack()
        try:
            yield
            self.barrier()
        finally:
            self.stack.close()
            self.stack = old

    def close(self):
        self.stack.close()


NEG = -30000.0


def _bucket(dist):
    n = np.maximum(dist, 0)
    ratio = np.log(np.maximum(n, 1).astype(np.float32) / np.float32(16.0)) / np.float32(np.log(8.0))
    large = np.minimum(16 + (ratio * 16).astype(np.int32), 31)
    return np.where(n < 16, n, large)


def host_consts(rel_bias):
    rel = np.asarray(rel_bias, np.float32)
    c = {}
    c["ident"] = np.eye(128, dtype=np.float32).astype(ml_dtypes.bfloat16)
    c["identf"] = np.eye(128, dtype=np.float32)
    kp = np.arange(128)[:, None]
    cc = np.arange(640)[None, :]
    dist = cc - kp
    bt = rel[_bucket(dist)]
    tw = np.where(((dist >= 0) & (dist < 512))[..., None], bt, np.float32(NEG))
    ts = np.where((dist >= 0)[..., None], bt, np.float32(NEG))
    c["tw"] = np.ascontiguousarray(tw.transpose(0, 2, 1)).astype(np.float32)
    c["ts"] = np.ascontiguousarray(ts.transpose(0, 2, 1)).astype(np.float32)
    cidx = np.arange(256)[:, None]
    qidx = np.arange(S)[None, :]
    dc = qidx - 16 * cidx - 31
    bcg = rel[_bucket(dc)]
    ok = (dc >= 0) & (cidx < 255)
    bc = np.where(ok[..., None], bcg, np.float32(NEG))
    c["biasc"] = np.ascontiguousarray(bc.transpose(2, 0, 1)).reshape(8, 2, 128, S).astype(np.float32)
    A = np.zeros((256, 64), np.float32)
    Wt = (1, 2, 2, 2, 1)
    for ci in range(255):
        for j in range(64):
            o = ci + 1 - 4 * j
            if 0 <= o <= 4:
                A[ci, j] = Wt[o]
    c["amat"] = A.reshape(2, 128, 64)
    E = np.zeros((64, S), np.float32)
    E[np.arange(S) // 64, np.arange(S)] = 1.0
    c["emat"] = E.astype(ml_dtypes.bfloat16)
    qp = np.arange(128)[:, None, None]
    qt = np.arange(32)[None, :, None]
    j = np.arange(64)[None, None, :]
    cur = (128 * qt + qp) // 64
    cand = (j >= 1) & (j <= cur - 2)
    c["candneg"] = np.where(cand, 0.0, -1e9).astype(np.float32)
    c["fz"] = ((j == 0) | (j == cur) | (j == cur - 1)).astype(np.float32)
    tri = np.triu(np.ones((64, 64), np.float32))
    c["rwmask"] = np.ascontiguousarray(np.stack([np.triu(np.ones((64, 64), np.float32), 1), tri, np.tril(np.ones((64, 64), np.float32), -1)], axis=1))
    rr = np.ones((64, 1024), np.float32)
    rr[:, ::64] = 0.0
    c["rwreset"] = rr
    c["b31"] = np.ascontiguousarray(np.broadcast_to(rel[31][None, :], (128, 8))).astype(np.float32)
    return c


CONST_SPECS = {
    "ident": ([128, 128], BF16), "identf": ([128, 128], F32),
    "tw": ([128, 8, 640], F32), "ts": ([128, 8, 640], F32),
    "biasc": ([8, 2, 128, S], F32), "amat": ([2, 128, 64], F32),
    "emat": ([64, S], BF16), "candneg": ([128, 32, 64], F32), "fz": ([128, 32, 64], F32),
    "b31": ([128, 8], F32), "rwmask": ([64, 3, 64], F32), "rwreset": ([64, 1024], F32),
}

W_SPECS = {
    "x": [S, D], "attn_norm_g": [1, D], "w_in": [1, D, IN_WIDTH], "q_norm_g": [1, 64], "k_norm_g": [1, 64],
    "cmp_pe_k": [1, 32, 64], "cmp_w1_k": [1, 2048, 256], "cmp_w2_k": [1, 256, 64],
    "cmp_pe_v": [1, 32, 64], "cmp_w1_v": [1, 2048, 256], "cmp_w2_v": [1, 256, 64],
    "rwkv_mu": [1, 1792], "rwkv_w0": [1, 512], "rwkv_w2": [1, 64, 512], "rwkv_a0": [1, 512],
    "rwkv_a2": [1, 64, 512], "rwkv_g2": [1, 128, 512], "rwkv_k_k": [1, 512], "rwkv_k_a": [1, 512],
    "rwkv_r_k": [1, 8, 64], "rwkv_ln_g": [1, 512], "rwkv_ln_b": [1, 512],
    "w_proj_a": [1, 512, D], "w_proj_b": [1, 512, D], "w_out": [1, D, D], "ffn_norm_g": [1, D],
    "w_up": [1, D, 2 * DFF], "conv_w": [1, 3, 2 * DFF], "conv_b": [1, 2 * DFF], "w_down": [1, DFF, D],
}


class Prog:
    def __init__(self, debug=()):
        self.debug = set(debug)
        nc = bass.Bass("TRN2", target_bir_lowering=False)
        self.nc = nc
        self.inp = {}
        for k, shp in W_SPECS.items():
            self.inp[k] = nc.dram_tensor(k, list(shp), F32, kind="ExternalInput").ap()
        for k, (shp, dt) in CONST_SPECS.items():
            self.inp[k] = nc.dram_tensor(k, list(shp), dt, kind="ExternalInput").ap()
        self.out = nc.dram_tensor("out", [S, D], F32, kind="ExternalOutput").ap()
        self.dbg = {}
        self.b = Builder(nc)

    def dbg_out(self, name, shape, dt=F32):
        t = self.nc.dram_tensor("dbg_" + name, list(shape), dt, kind="ExternalOutput").ap()
        self.dbg[name] = t
        return t

    def load_weight(self, dst, src, ncols, gvec=None, kch=8, stage=None, eng="act"):
        b = self.b
        for c in range(kch):
            st = stage[c % len(stage)]
            b.dma("sp", st[:, :ncols], src[c * 128:(c + 1) * 128, :], writes=[st])
            if gvec is not None:
                b.op(eng, lambda e: e.activation(out=dst[:, c, :], in_=st[:, :ncols], func=AF.Copy, scale=gvec[:, c:c + 1])
                     if eng == "act" else e.tensor_scalar_mul(out=dst[:, c, :], in0=st[:, :ncols], scalar1=gvec[:, c:c + 1]),
                     reads=[st, gvec], writes=[dst])
            else:
                b.op(eng, lambda e: e.copy(out=dst[:, c, :], in_=st[:, :ncols]) if eng == "act"
                     else e.tensor_copy(out=dst[:, c, :], in_=st[:, :ncols]), reads=[st], writes=[dst])

    def load_gain(self, name, src_vec, kch=8):
        b = self.b
        g = b.sb(name, [128, kch], F32)
        b.dma("sp", g[:], src_vec.rearrange("(c p) -> p c", p=128), writes=[g], allow_slow_non_contiguous=True)
        return g

    def bcast_row(self, name, src_row, n):
        b = self.b
        t = b.sb(name, [128, n], F32)
        b.dma("sp", t[:], src_row.partition_broadcast(128), writes=[t])
        return t

    def make_hT(self, x_ap, t, xt, junk, ss, hb, pt, hT, ident, hT_ap=None):
        b = self.b
        b.dma("sp", xt[:], x_ap[t * 128:(t + 1) * 128, :], writes=[xt])
        b.op("act", lambda e: e.activation(out=junk[:], in_=xt[:], func=AF.Square, accum_out=ss[:]), reads=[xt], writes=[junk, ss])
        b.op("act", lambda e: e.activation(out=ss[:], in_=ss[:], func=AF.Sqrt, scale=1.0 / D, bias=RMS_EPS), reads=[ss], writes=[ss])
        b.op("dve", lambda e: e.reciprocal(out=ss[:], in_=ss[:]), reads=[ss], writes=[ss])
        b.op("dve", lambda e: e.tensor_scalar_mul(out=hb[:], in0=xt[:], scalar1=ss[:]), reads=[xt, ss], writes=[hb])
        for c in range(8):
            b.op("pe", lambda e: e.transpose(out=pt[:, c, :], in_=hb[:, c * 128:(c + 1) * 128], identity=ident[:]),
                 reads=[hb, ident], writes=[pt])
        b.op("act", lambda e: e.copy(out=(hT[:] if hT_ap is None else hT_ap), in_=pt[:]), reads=[pt], writes=[hT])

    def alloc_root(self):
        b = self.b
        I = self.inp
        self.ident = b.sb("ident", [128, 128], BF16)
        b.dma("sp", self.ident[:], I["ident"], writes=[self.ident])
        self.identf = b.sb("identf", [128, 128], F32)
        b.dma("sp", self.identf[:], I["identf"], writes=[self.identf])

    def alloc_persistent(self):
        b = self.b
        I = self.inp
        if not hasattr(self, "ident"):
            self.alloc_root()
        self.ksE = b.sb("ksE", [128, 2, S], BF16)
        self.kwT = b.sb("kwT", [64, 2, S], BF16)
        self.vaug_s = b.sb("vaug_s", [128, NT, 2, 65], BF16)
        self.vaug_w = b.sb("vaug_w", [128, NT, 2, 65], BF16)
        self.gts = b.sb("gts", [128, NT, 24], F32)
        self.kcT = b.sb("kcT", [64, 2, 256], BF16)
        self.vcA = b.sb("vcA", [128, 2, 2, 129], F32)
        self.qT_d = b.dram("qT_d", [8, 64, S], BF16)
        self.oaT_d = b.dram("oaT_d", [4, 128, S], BF16)
        self.obT_d = b.dram("obT_d", [4, 128, S], BF16)
        for g in range(2):
            b.dma("sp", self.ksE[64:128, g, :], I["emat"], writes=[self.ksE])
        b.op("pool", lambda e: e.memset(self.vaug_s[:, :, :, 64:65], 1.0), writes=[self.vaug_s])
        b.op("pool", lambda e: e.memset(self.vaug_w[:, :, :, 64:65], 1.0), writes=[self.vaug_w])
        b.op("pool", lambda e: e.memset(self.vcA[:, :, :, 64:65], 1.0), writes=[self.vcA])
        for g in range(2):
            for ct in range(2):
                b.dma("sp", self.vcA[:, g, ct, 65:129], I["amat"][ct], writes=[self.vcA])

    def phase_nsa_proj(self):
        b = self.b
        I = self.inp
        with b.scope():
            gat = self.load_gain("gat", I["attn_norm_g"][0])
            wn = b.sb("wn", [128, 8, RW0], BF16)
            stage = [b.sb(f"wst{i}", [128, RW0], F32) for i in range(2)]
            self.load_weight(wn, I["w_in"][0][:, 0:RW0], RW0, gvec=gat, stage=stage)
            gq = self.bcast_row("gq", I["q_norm_g"][0], 64)
            gk = self.bcast_row("gk", I["k_norm_g"][0], 64)
            gq_rep = b.sb("gq_rep", [128, 8, 64], F32)
            gk_rep = b.sb("gk_rep", [128, 2, 64], F32)
            b.op("act", lambda e: e.activation(out=gq_rep[:], in_=gq[:, None, :].to_broadcast([128, 8, 64]), func=AF.Copy, scale=0.125),
                 reads=[gq], writes=[gq_rep])
            b.op("act", lambda e: e.activation(out=gk_rep[:], in_=gk[:, None, :].to_broadcast([128, 2, 64]), func=AF.Copy, scale=1.0),
                 reads=[gk], writes=[gk_rep])
            if getattr(self, 'stop_at', 99) <= 0:
                return
            kcdup = b.sb("kcdup", [128, 2, S + 1], BF16)
            vcdup = b.sb("vcdup", [128, 2, S + 1], BF16)
            xt = [b.sb(f"xt{i}", [128, D], F32) for i in range(2)]
            junk = b.sb("junk", [128, D], BF16)
            ss = [b.sb(f"ss{i}", [128, 1], F32) for i in range(2)]
            hb = [b.sb(f"hb{i}", [128, D], BF16) for i in range(2)]
            hT = [b.sb(f"hT{i}", [128, 8, 128], BF16) for i in range(2)]
            sq = b.sb("sq", [128, 12, 64], F32)
            ssq = b.sb("ssq", [128, 12], F32)
            tmpq = b.sb("tmpq", [128, 8, 64], F32)
            tmpk = b.sb("tmpk", [128, 4, 64], F32)
            qb = b.sb("qb", [128, 512], BF16)
            kb = b.sb("kb", [128, 4, 64], BF16)
            cb = b.sb("cb", [128, 4, 2, 64], BF16)
            qst = [b.sb(f"qst{i}", [64, 8, 128], BF16) for i in range(2)]
            pt = b.ps("pt", [128, 8, 128], BF16)
            pm = [b.ps(f"pm{i}", [128, 512], F32) for i in range(3)]
            ptq = b.ps("ptq", [128, 8, 128], BF16)
            ptk = b.ps("ptk", [128, 8, 128], BF16)
            colgroups = [(0, 512), (512, 1024), (1024, RW0)]
            for t in range(getattr(self, 'nt_limit', NT)):
                i = t % 2
                self.make_hT(I["x"], t, xt[i], junk, ss[i], hb[i], pt, hT[i], self.ident)
                for n, (c0, c1) in enumerate(colgroups):
                    for c in range(8):
                        b.op("pe", lambda e: e.matmul(pm[n][:, :c1 - c0], lhsT=hT[i][:, c, :], rhs=wn[:, c, c0:c1],
                                                      start=(c == 0), stop=(c == 7)), reads=[hT[i], wn], writes=[pm[n]])
                if getattr(self, 'stop_at', 99) <= 1:
                    continue
                b.op("act", lambda e: e.activation(out=sq[:, 0:8, :], in_=pm[0][:, 0:512].rearrange("p (h d) -> p h d", d=64), func=AF.Square),
                     reads=[pm[0]], writes=[sq])
                b.op("act", lambda e: e.activation(out=sq[:, 8:10, :], in_=pm[1][:, 256:384].rearrange("p (h d) -> p h d", d=64), func=AF.Square),
                     reads=[pm[1]], writes=[sq])
                b.op("act", lambda e: e.activation(out=sq[:, 10:12, :], in_=pm[2][:, 0:128].rearrange("p (h d) -> p h d", d=64), func=AF.Square),
                     reads=[pm[2]], writes=[sq])
                b.op("dve", lambda e: e.tensor_reduce(out=ssq[:], in_=sq[:], axis=AX.X, op=ALU.add), reads=[sq], writes=[ssq])
                b.op("act", lambda e: e.activation(out=ssq[:], in_=ssq[:], func=AF.Sqrt, scale=1.0 / 64, bias=RMS_EPS), reads=[ssq], writes=[ssq])
                b.op("dve", lambda e: e.reciprocal(out=ssq[:], in_=ssq[:]), reads=[ssq], writes=[ssq])
                if getattr(self, 'stop_at', 99) <= 2:
                    continue
                b.op("dve", lambda e: e.tensor_tensor(out=tmpq[:], in0=pm[0][:, 0:512].rearrange("p (h d) -> p h d", d=64),
                                                      in1=ssq[:, 0:8].unsqueeze(2).to_broadcast([128, 8, 64]), op=ALU.mult),
                     reads=[pm[0], ssq], writes=[tmpq])
                b.op("pool", lambda e: e.tensor_tensor(out=qb[:].rearrange("p (h d) -> p h d", d=64), in0=tmpq[:], in1=gq_rep[:], op=ALU.mult),
                     reads=[tmpq, gq_rep], writes=[qb])
                for h in range(8):
                    b.op("pe", lambda e: e.transpose(out=ptq[0:64, h, :], in_=qb[:, h * 64:(h + 1) * 64], identity=self.ident[:]),
                         reads=[qb, self.ident], writes=[ptq])
                b.op("act", lambda e: e.copy(out=qst[i][:], in_=ptq[0:64, :, :]), reads=[ptq], writes=[qst[i]])
                b.dma("pool", self.qT_d[:, :, t * 128:(t + 1) * 128].rearrange("h d t -> d h t"), qst[i][:], reads=[qst[i]], writes=[self.qT_d])
                if getattr(self, 'stop_at', 99) <= 3:
                    continue
                b.op("dve", lambda e: e.tensor_tensor(out=tmpk[:, 0:2, :], in0=pm[1][:, 256:384].rearrange("p (h d) -> p h d", d=64),
                                                      in1=ssq[:, 8:10].unsqueeze(2).to_broadcast([128, 2, 64]), op=ALU.mult),
                     reads=[pm[1], ssq], writes=[tmpk])
                b.op("dve", lambda e: e.tensor_tensor(out=tmpk[:, 2:4, :], in0=pm[2][:, 0:128].rearrange("p (h d) -> p h d", d=64),
                                                      in1=ssq[:, 10:12].unsqueeze(2).to_broadcast([128, 2, 64]), op=ALU.mult),
                     reads=[pm[2], ssq], writes=[tmpk])
                b.op("pool", lambda e: e.tensor_tensor(out=kb[:].rearrange("p (a g) d -> p a g d", a=2), in0=tmpk[:].rearrange("p (a g) d -> p a g d", a=2),
                                                       in1=gk_rep[:, None, :, :].to_broadcast([128, 2, 2, 64]), op=ALU.mult),
                     reads=[tmpk, gk_rep], writes=[kb])
                for j in range(4):
                    b.op("pe", lambda e: e.transpose(out=ptk[0:64, j, :], in_=kb[:, j, :], identity=self.ident[:]),
                         reads=[kb, self.ident], writes=[ptk])
                if getattr(self, 'stop_at', 99) <= 4:
                    continue
                for du in range(2):
                    b.op("act", lambda e: e.copy(out=cb[:, :, du, :], in_=pm[1][:, 0:256].rearrange("p (a d) -> p a d", d=64)),
                         reads=[pm[1]], writes=[cb])
                for j in range(4):
                    b.op("pe", lambda e: e.transpose(out=ptk[:, 4 + j, :], in_=cb[:, j, :, :].rearrange("p a d -> p (a d)"), identity=self.ident[:]),
                         reads=[cb, self.ident], writes=[ptk])
                c0 = t * 128
                b.op("dve", lambda e: e.tensor_copy(out=self.ksE[0:64, :, c0:c0 + 128], in_=ptk[0:64, 0:2, :]), reads=[ptk], writes=[self.ksE])
                b.op("dve", lambda e: e.tensor_copy(out=self.kwT[0:64, :, c0:c0 + 128], in_=ptk[0:64, 2:4, :]), reads=[ptk], writes=[self.kwT])
                b.op("act", lambda e: e.copy(out=kcdup[0:64, :, 1 + c0:1 + c0 + 128], in_=ptk[0:64, 4:6, :]), reads=[ptk], writes=[kcdup])
                b.op("act", lambda e: e.copy(out=kcdup[64:128, :, c0:c0 + 128], in_=ptk[64:128, 4:6, :]), reads=[ptk], writes=[kcdup])
                b.op("dve", lambda e: e.tensor_copy(out=vcdup[0:64, :, 1 + c0:1 + c0 + 128], in_=ptk[0:64, 6:8, :]), reads=[ptk], writes=[vcdup])
                b.op("dve", lambda e: e.tensor_copy(out=vcdup[64:128, :, c0:c0 + 128], in_=ptk[64:128, 6:8, :]), reads=[ptk], writes=[vcdup])
                if getattr(self, 'stop_at', 99) <= 5:
                    continue
                b.op("act", lambda e: e.copy(out=self.vaug_s[:, t, :, 0:64], in_=pm[1][:, 384:512].rearrange("p (g d) -> p g d", d=64)),
                     reads=[pm[1]], writes=[self.vaug_s])
                b.op("act", lambda e: e.copy(out=self.vaug_w[:, t, :, 0:64], in_=pm[2][:, 128:256].rearrange("p (g d) -> p g d", d=64)),
                     reads=[pm[2]], writes=[self.vaug_w])
                b.op("act", lambda e: e.activation(out=self.gts[:, t, :], in_=pm[2][:, 256:280], func=AF.Sigmoid), reads=[pm[2]], writes=[self.gts])
            if "nsa_proj" in self.debug:
                d = self.dbg_out("ksE", [128, 2, S], BF16)
                b.dma("pool", d, self.ksE[:], reads=[self.ksE])
                d = self.dbg_out("kcdup", [128, 2, S + 1], BF16)
                b.dma("pool", d, kcdup[:], reads=[kcdup])
                d = self.dbg_out("vaug_w", [128, NT, 2, 65], BF16)
                b.dma("pool", d, self.vaug_w[:], reads=[self.vaug_w])
                d = self.dbg_out("gts", [128, NT, 24], F32)
                b.dma("pool", d, self.gts[:], reads=[self.gts])
            if not getattr(self, 'skip_compress', False):
                self.compress(kcdup, vcdup, gk_rep, [pm[0], pm[1]], pm[2], ptk)

    def compress(self, kcdup, vcdup, gk_rep, ph, po, ptc):
        b = self.b
        I = self.inp
        C2 = 2.0 * 0.7978845608028654
        w1 = b.sb("w1", [128, 16, 256], BF16)
        w2 = b.sb("w2", [128, 2, 64], BF16)
        w1st = [b.sb(f"w1st{i}", [128, 256], F32) for i in range(2)]
        peT = b.sb("peT", [128, 16], F32)
        peTb = b.sb("peTb", [128, 16], BF16)
        hTc = b.sb("hTc", [128, 2, 256], BF16)
        pbias = b.sb("pbias", [128, 2], F32)
        xh = b.sb("xh", [128, 255], F32)
        x2 = b.sb("x2", [128, 255], F32)
        sg = b.sb("sg", [128, 255], F32)
        ctmp = b.sb("ctmp", [128, 64], F32)
        csq = b.sb("csq", [128, 64], F32)
        cs1 = b.sb("cs1", [128, 1], F32)
        kcb = b.sb("kcb", [128, 64], BF16)
        b.op("pool", lambda e: e.memset(hTc[:], 0.0), writes=[hTc])
        for kv, (dup, pe_n, w1_n, w2_n) in enumerate([(kcdup, "cmp_pe_k", "cmp_w1_k", "cmp_w2_k"), (vcdup, "cmp_pe_v", "cmp_w1_v", "cmp_w2_v")]):
            self.load_weight(w1, I[w1_n][0], 256, kch=16, stage=w1st, eng="dve")
            self.load_weight(w2, I[w2_n][0], 64, kch=2, stage=w1st, eng="dve")
            for two in range(2):
                b.dma("sp", peT[two * 64:(two + 1) * 64, :], I[pe_n][0].rearrange("(pp two) d -> two d pp", two=2)[two],
                      writes=[peT], allow_slow_non_contiguous=True)
            b.op("dve", lambda e: e.tensor_copy(out=peTb[:], in_=peT[:]), reads=[peT], writes=[peTb])
            for ft in range(2):
                for pp in range(16):
                    b.op("pe", lambda e: e.matmul(po[:, ft:ft + 1], lhsT=w1[:, pp, ft * 128:(ft + 1) * 128], rhs=peTb[:, pp:pp + 1],
                                                  start=(pp == 0), stop=(pp == 15)), reads=[w1, peTb], writes=[po])
            b.op("dve", lambda e: e.tensor_copy(out=pbias[:], in_=po[:, 0:2]), reads=[po], writes=[pbias])
            for g in range(2):
                for ft in range(2):
                    p = ph[ft]
                    for pp in range(16):
                        b.op("pe", lambda e: e.matmul(p[:, 0:255], lhsT=w1[:, pp, ft * 128:(ft + 1) * 128],
                                                      rhs=dup[:, g, 1 + 2 * pp:1 + 2 * pp + 16 * 254 + 1:16],
                                                      start=(pp == 0), stop=(pp == 15)), reads=[w1, dup], writes=[p])
                    b.op("act", lambda e: e.activation(out=xh[:], in_=p[:, 0:255], func=AF.Identity, bias=pbias[:, ft:ft + 1]), reads=[p, pbias], writes=[xh])
                    b.op("dve", lambda e: e.tensor_tensor(out=x2[:], in0=xh[:], in1=xh[:], op=ALU.mult), reads=[xh], writes=[x2])
                    b.op("dve", lambda e: e.tensor_scalar(out=x2[:], in0=x2[:], scalar1=0.044715, scalar2=1.0, op0=ALU.mult, op1=ALU.add), reads=[x2], writes=[x2])
                    b.op("dve", lambda e: e.tensor_tensor(out=x2[:], in0=x2[:], in1=xh[:], op=ALU.mult), reads=[x2, xh], writes=[x2])
                    b.op("act", lambda e: e.activation(out=sg[:], in_=x2[:], func=AF.Sigmoid, scale=C2), reads=[x2], writes=[sg])
                    b.op("dve", lambda e: e.tensor_tensor(out=hTc[:, ft, 0:255], in0=xh[:], in1=sg[:], op=ALU.mult), reads=[xh, sg], writes=[hTc])
                for ct in range(2):
                    for ft in range(2):
                        b.op("pe", lambda e: e.matmul(po[:, 64:128], lhsT=hTc[:, ft, ct * 128:(ct + 1) * 128], rhs=w2[:, ft, :],
                                                      start=(ft == 0), stop=(ft == 1)), reads=[hTc, w2], writes=[po])
                    if kv == 0:
                        b.op("act", lambda e: e.activation(out=csq[:], in_=po[:, 64:128], func=AF.Square, accum_out=cs1[:]), reads=[po], writes=[csq, cs1])
                        b.op("act", lambda e: e.activation(out=cs1[:], in_=cs1[:], func=AF.Sqrt, scale=1.0 / 64, bias=RMS_EPS), reads=[cs1], writes=[cs1])
                        b.op("dve", lambda e: e.reciprocal(out=cs1[:], in_=cs1[:]), reads=[cs1], writes=[cs1])
                        b.op("dve", lambda e: e.tensor_scalar_mul(out=ctmp[:], in0=po[:, 64:128], scalar1=cs1[:]), reads=[po, cs1], writes=[ctmp])
                        b.op("dve", lambda e: e.tensor_tensor(out=kcb[:], in0=ctmp[:], in1=gk_rep[:, 0, :], op=ALU.mult), reads=[ctmp, gk_rep], writes=[kcb])
                        b.op("pe", lambda e: e.transpose(out=ptc[0:64, 0, :], in_=kcb[:], identity=self.ident[:]), reads=[kcb, self.ident], writes=[ptc])
                        b.op("dve", lambda e: e.tensor_copy(out=self.kcT[:, g, ct * 128:(ct + 1) * 128], in_=ptc[0:64, 0, :]), reads=[ptc], writes=[self.kcT])
                    else:
                        b.op("dve", lambda e: e.tensor_copy(out=self.vcA[:, g, ct, 0:64], in_=po[:, 64:128]), reads=[po], writes=[self.vcA])
        if "compress" in self.debug:
            d = self.dbg_out("kcT", [64, 2, 256], BF16)
            b.dma("pool", d, self.kcT[:], reads=[self.kcT])
            d = self.dbg_out("vcA", [128, 2, 2, 129], F32)
            b.dma("pool", d, self.vcA[:], reads=[self.vcA])

    def finish(self):
        b = self.b
        b.wait_all_on("pool")
        b.barrier()
        b.close()
        return self.nc


def _phase_attn(self):
    b = self.b
    I = self.inp
    with b.scope():
        tw = b.sb("tw", [128, 8, 640], F32)
        ts = b.sb("ts", [128, 8, 640], F32)
        b.dma("sp", tw[:], I["tw"], writes=[tw])
        b.dma("sp", ts[:], I["ts"], writes=[ts])
        candneg = b.sb("candneg", [128, 32, 64], F32)
        fz = b.sb("fz", [128, 32, 64], F32)
        b.dma("sp", candneg[:], I["candneg"], writes=[candneg])
        b.dma("sp", fz[:], I["fz"], writes=[fz])
        b31 = b.sb("b31", [128, 8], F32)
        b.dma("sp", b31[:], I["b31"], writes=[b31])
        kwp = b.sb("kwp", [128, 2, S], BF16)
        b.op("pool", lambda e: e.memset(kwp[64:128, :, :], 0.0), writes=[kwp])
        b.op("pool", lambda e: e.tensor_copy(out=kwp[0:64, :, :], in_=self.kwT[:]), reads=[self.kwT], writes=[kwp])
        kcp = b.sb("kcp", [128, 2, 256], BF16)
        b.op("pool", lambda e: e.memset(kcp[64:128, :, :], 0.0), writes=[kcp])
        b.op("pool", lambda e: e.tensor_copy(out=kcp[0:64, :, :], in_=self.kcT[:]), reads=[self.kcT], writes=[kcp])
        zer = b.sb("zer", [128, 512], BF16)
        b.op("pool", lambda e: e.memset(zer[:], 0.0), writes=[zer])
        qm = [b.sb(f"qm{i}", [128, 8, 512], BF16) for i in range(2)]
        bct = [b.sb(f"bct{i}", [128, 512], F32) for i in range(3)]
        scf = [b.sb(f"scf{i}", [128, 640], F32) for i in range(2)]
        pcT = [b.sb(f"pcT{i}", [128, 2, 512], F32) for i in range(2)]
        pT = [b.sb(f"pT{i}", [128, 640], BF16) for i in range(3)]
        oacc = b.sb("oacc", [128, 4, 512], F32)
        imp = b.sb("imp", [128, 4, 2, 64], F32)
        impm = b.sb("impm", [128, 64], F32)
        impm2 = b.sb("impm2", [128, 64], F32)
        m8a = b.sb("m8a", [128, 8], F32)
        m8b = b.sb("m8b", [128, 8], F32)
        msk = b.sb("msk", [128, 64], F32)
        mb = b.sb("mb", [128, 128], BF16)
        b.op("pool", lambda e: e.memset(mb[:], 0.0), writes=[mb])
        rs = b.sb("rs", [128, 4], F32)
        rg = b.sb("rg", [128, 4], F32)
        oab = b.sb("oab", [128, 512], BF16)
        oaT = [b.sb(f"oaT{i}", [128, 4, 128], BF16) for i in range(2)]
        pS = [b.ps(f"pS{i}", [128, 512], F32) for i in range(2)]
        pS2 = b.ps("pS2", [128, 512], F32)
        pO = [b.ps(f"pO{i}", [128, 512], F32) for i in range(3)]
        pTr = b.ps("pTr", [128, 8, 128], BF16)
        nrot = {"bct": 0, "scf": 0, "pT": 0, "pS": 0}

        def rot(name, lst):
            nrot[name] += 1
            return lst[nrot[name] % len(lst)]

        def finalize(po, ncol_off, h, qs, branch, first):
            qt = qs_base + qs
            o0 = ncol_off
            b.op("dve", lambda e: e.tensor_scalar_max(out=rs[:, 0:1], in0=po[:, o0 + 64:o0 + 65], scalar1=1e-30), reads=[po], writes=[rs])
            b.op("dve", lambda e: e.reciprocal(out=rs[:, 1:2], in_=rs[:, 0:1]), reads=[rs], writes=[rs])
            b.op("dve", lambda e: e.tensor_tensor(out=rg[:, 0:1], in0=rs[:, 1:2], in1=self.gts[:, qt, h * 3 + branch:h * 3 + branch + 1], op=ALU.mult),
                 reads=[rs, self.gts], writes=[rg])
            if first:
                b.op("dve", lambda e: e.tensor_scalar_mul(out=oacc[:, qs, h * 64:(h + 1) * 64], in0=po[:, o0:o0 + 64], scalar1=rg[:, 0:1]),
                     reads=[po, rg], writes=[oacc])
            else:
                b.op("dve", lambda e: e.scalar_tensor_tensor(out=oacc[:, qs, h * 64:(h + 1) * 64], in0=po[:, o0:o0 + 64], scalar=rg[:, 0:1],
                                                             in1=oacc[:, qs, h * 64:(h + 1) * 64], op0=ALU.mult, op1=ALU.add),
                     reads=[po, rg, oacc], writes=[oacc])

        nqg = getattr(self, "nqg_limit", 8)
        for qg in range(nqg):
            qs_base = 4 * qg
            q0 = 512 * qg
            Q = qm[qg % 2]
            b.dma("sp", Q[0:64, :, :], self.qT_d[:, :, q0:q0 + 512].rearrange("h d t -> d h t"), reads=[self.qT_d], writes=[Q])
            if qg < 2:
                b.op("pool", lambda e: e.memset(Q[64:128, :, :], 0.0), writes=[Q])
            for h in range(8):
                g = h // 4
                pc = pcT[h % 2]
                for ct in range(2):
                    p = rot("pS", pS)
                    b.op("pe", lambda e: e.matmul(p[:, :], lhsT=kcp[:, g, ct * 128:(ct + 1) * 128], rhs=Q[:, h, :], start=True, stop=True),
                         reads=[kcp, Q], writes=[p])
                    bt = rot("bct", bct)
                    b.dma("sp", bt[:], I["biasc"][h, ct, :, q0:q0 + 512], writes=[bt])
                    sc = rot("scf", scf)
                    b.op("dve", lambda e: e.tensor_tensor(out=sc[:, 0:512], in0=p[:, :], in1=bt[:], op=ALU.add), reads=[p, bt], writes=[sc])
                    b.op("act", lambda e: e.activation(out=pc[:, ct, :], in_=sc[:, 0:512], func=AF.Exp), reads=[sc], writes=[pc])
                po = pO[0]
                for qs in range(4):
                    for ct in range(2):
                        b.op("pe", lambda e: e.matmul(po[:, qs * 128:qs * 128 + 129] if False else po[:, 0:129], lhsT=pc[:, ct, qs * 128:(qs + 1) * 128],
                                                      rhs=self.vcA[:, g, ct, :], start=(ct == 0), stop=(ct == 1)), reads=[pc, self.vcA], writes=[po])
                    finalize(po, 0, h, qs, 0, True)
                    if h % 4 == 0:
                        b.op("dve", lambda e: e.tensor_scalar_mul(out=imp[:, qs, g, :], in0=po[:, 65:129], scalar1=rs[:, 1:2]), reads=[po, rs], writes=[imp])
                    else:
                        b.op("dve", lambda e: e.scalar_tensor_tensor(out=imp[:, qs, g, :], in0=po[:, 65:129], scalar=rs[:, 1:2], in1=imp[:, qs, g, :],
                                                                     op0=ALU.mult, op1=ALU.add), reads=[po, rs, imp], writes=[imp])
            if qg >= 2:
                for qs in range(4):
                    qt = qs_base + qs
                    for g in range(2):
                        b.op("dve", lambda e: e.tensor_tensor(out=impm[:], in0=imp[:, qs, g, :], in1=candneg[:, qt, :], op=ALU.add), reads=[imp, candneg], writes=[impm])
                        b.op("dve", lambda e: e.max(out=m8a[:], in_=impm[:]), reads=[impm], writes=[m8a])
                        b.op("dve", lambda e: e.match_replace(out=impm2[:], in_to_replace=m8a[:], in_values=impm[:], imm_value=-1e9), reads=[m8a, impm], writes=[impm2])
                        b.op("dve", lambda e: e.max(out=m8b[:], in_=impm2[:]), reads=[impm2], writes=[m8b])
                        b.op("dve", lambda e: e.tensor_scalar(out=msk[:], in0=impm[:], scalar1=m8b[:, 4:5], scalar2=None, op0=ALU.is_ge), reads=[impm, m8b], writes=[msk])
                        b.op("dve", lambda e: e.tensor_tensor(out=msk[:], in0=msk[:], in1=fz[:, qt, :], op=ALU.max), reads=[msk, fz], writes=[msk])
                        b.op("dve", lambda e: e.tensor_scalar(out=mb[:, 64:128], in0=msk[:], scalar1=-NEG, scalar2=NEG, op0=ALU.mult, op1=ALU.add), reads=[msk], writes=[mb])
                        b.op("pe", lambda e: e.transpose(out=pTr[:, 0, :], in_=mb[:], identity=self.ident[:]), reads=[mb, self.ident], writes=[pTr])
                        b.op("act", lambda e: e.copy(out=Q[64:128, 4 * g:4 * g + 4, qs * 128:(qs + 1) * 128],
                                                     in_=pTr[64:128, 0:1, :].to_broadcast([64, 4, 128])), reads=[pTr], writes=[Q])
            for h in range(8):
                g = h // 4
                po_s, po_w = pO[1], pO[2]
                for po in (po_s, po_w):
                    b.op("pe", lambda e: e.matmul(po[:, 0:260], lhsT=zer[:, 0:128], rhs=zer[:, 0:260], start=True, stop=True), reads=[zer], writes=[po])
                nkt = 4 * (qg + 1)
                for kt in range(nkt):
                    dlt = 4 * qg - kt
                    qstart = 0 if dlt >= 0 else -dlt * 128
                    N = 512 - qstart
                    p = rot("pS", pS)
                    b.op("pe", lambda e: e.matmul(p[:, 0:N], lhsT=self.ksE[:, g, kt * 128:(kt + 1) * 128], rhs=Q[:, h, qstart:512], start=True, stop=True),
                         reads=[self.ksE, Q], writes=[p])
                    pt_ = rot("pT", pT)
                    if dlt <= 1:
                        c0 = 128 if dlt == 1 else 0
                        sc = rot("scf", scf)
                        b.op("dve", lambda e: e.tensor_tensor(out=sc[:, 0:N], in0=p[:, 0:N], in1=ts[:, h, c0:c0 + N], op=ALU.add), reads=[p, ts], writes=[sc])
                        b.op("act", lambda e: e.activation(out=pt_[:, 0:N], in_=sc[:, 0:N], func=AF.Exp), reads=[sc], writes=[pt_])
                    else:
                        b.op("act", lambda e: e.activation(out=pt_[:, 0:N], in_=p[:, 0:N], func=AF.Exp, bias=b31[:, h:h + 1]), reads=[p, b31], writes=[pt_])
                    for qs in range(qstart // 128, 4):
                        o = qs * 128 - qstart
                        b.op("pe", lambda e: e.matmul(po_s[:, qs * 65:(qs + 1) * 65], lhsT=pt_[:, o:o + 128], rhs=self.vaug_s[:, kt, g, :],
                                                      start=False, stop=(kt == nkt - 1), skip_group_check=True), reads=[pt_, self.vaug_s], writes=[po_s])
                kts = [kt for kt in range(4 * qg - 4, 4 * qg + 4) if kt >= 0]
                for kt in kts:
                    qs_lo = max(0, kt - 4 * qg)
                    qs_hi = min(3, kt + 4 - 4 * qg)
                    N = (qs_hi - qs_lo + 1) * 128
                    c0 = 128 * (4 * qg + qs_lo - kt)
                    p = rot("pS", pS)
                    b.op("pe", lambda e: e.matmul(p[:, 0:N], lhsT=kwp[:, g, kt * 128:(kt + 1) * 128], rhs=Q[:, h, qs_lo * 128:(qs_hi + 1) * 128], start=True, stop=True),
                         reads=[kwp, Q], writes=[p])
                    sc = rot("scf", scf)
                    b.op("dve", lambda e: e.tensor_tensor(out=sc[:, 0:N], in0=p[:, 0:N], in1=tw[:, h, c0:c0 + N], op=ALU.add), reads=[p, tw], writes=[sc])
                    pt_ = rot("pT", pT)
                    b.op("act", lambda e: e.activation(out=pt_[:, 0:N], in_=sc[:, 0:N], func=AF.Exp), reads=[sc], writes=[pt_])
                    for qs in range(qs_lo, qs_hi + 1):
                        o = (qs - qs_lo) * 128
                        b.op("pe", lambda e: e.matmul(po_w[:, qs * 65:(qs + 1) * 65], lhsT=pt_[:, o:o + 128], rhs=self.vaug_w[:, kt, g, :],
                                                      start=False, stop=(kt == kts[-1]), skip_group_check=True), reads=[pt_, self.vaug_w], writes=[po_w])
                for qs in range(4):
                    finalize(po_s, qs * 65, h, qs, 1, False)
                    finalize(po_w, qs * 65, h, qs, 2, False)
            for qs in range(4):
                qt = qs_base + qs
                ot = oaT[qs % 2]
                b.op("act", lambda e: e.copy(out=oab[:], in_=oacc[:, qs, :]), reads=[oacc], writes=[oab])
                for c in range(4):
                    b.op("pe", lambda e: e.transpose(out=pTr[:, 4 + c, :], in_=oab[:, c * 128:(c + 1) * 128], identity=self.ident[:]), reads=[oab, self.ident], writes=[pTr])
                b.op("act", lambda e: e.copy(out=ot[:], in_=pTr[:, 4:8, :]), reads=[pTr], writes=[ot])
                b.dma("pool", self.oaT_d[:, :, qt * 128:(qt + 1) * 128].rearrange("c p t -> p c t"), ot[:], reads=[ot], writes=[self.oaT_d])
        if "attn" in self.debug:
            d = self.dbg_out("oaT", [4, 128, S], BF16)
            b.dma("pool", d, self.oaT_d[:], reads=[self.oaT_d])


Prog.phase_attn = _phase_attn


def _phase_attn2(self):
    b = self.b
    I = self.inp
    with b.scope():
        tw = b.sb("tw", [128, 8, 640], F32)
        ts = b.sb("ts", [128, 8, 640], F32)
        b.dma("sp", tw[:], I["tw"], writes=[tw])
        b.dma("sp", ts[:], I["ts"], writes=[ts])
        candneg = b.sb("candneg", [128, 32, 64], F32)
        fz = b.sb("fz", [128, 32, 64], F32)
        b.dma("sp", candneg[:], I["candneg"], writes=[candneg])
        b.dma("sp", fz[:], I["fz"], writes=[fz])
        b31 = b.sb("b31", [128, 8], F32)
        b.dma("sp", b31[:], I["b31"], writes=[b31])
        kwp = b.sb("kwp", [128, 2, S], BF16)
        b.op("pool", lambda e: e.memset(kwp[64:128, :, :], 0.0), writes=[kwp])
        b.op("pool", lambda e: e.tensor_copy(out=kwp[0:64, :, :], in_=self.kwT[:]), reads=[self.kwT], writes=[kwp])
        kcp = b.sb("kcp", [128, 2, 256], BF16)
        b.op("pool", lambda e: e.memset(kcp[64:128, :, :], 0.0), writes=[kcp])
        b.op("pool", lambda e: e.tensor_copy(out=kcp[0:64, :, :], in_=self.kcT[:]), reads=[self.kcT], writes=[kcp])
        zer = b.sb("zer", [128, 512], BF16)
        b.op("pool", lambda e: e.memset(zer[:], 0.0), writes=[zer])
        qm = [b.sb(f"qm{i}", [128, 8, 512], BF16) for i in range(2)]
        bct = [b.sb(f"bct{i}", [128, 512], F32) for i in range(3)]
        scf = [b.sb(f"scf{i}", [128, 640], F32) for i in range(3)]
        pcT = [b.sb(f"pcT{i}", [128, 2, 512], F32) for i in range(2)]
        pT = [b.sb(f"pT{i}", [128, 640], BF16) for i in range(4)]
        oacc = b.sb("oacc", [128, 4, 512], F32)
        imp = b.sb("imp", [128, 4, 2, 64], F32)
        impm = b.sb("impm", [128, 64], F32)
        impm2 = b.sb("impm2", [128, 64], F32)
        m8a = b.sb("m8a", [128, 8], F32)
        m8b = b.sb("m8b", [128, 8], F32)
        msk = b.sb("msk", [128, 64], F32)
        mb = b.sb("mb", [128, 128], BF16)
        b.op("pool", lambda e: e.memset(mb[:], 0.0), writes=[mb])
        rs = b.sb("rs", [128, 4], F32)
        rg = b.sb("rg", [128, 4], F32)
        oab = b.sb("oab", [128, 512], BF16)
        oaT = [b.sb(f"oaT{i}", [128, 4, 128], BF16) for i in range(2)]
        pS = [b.ps(f"pS{i}", [128, 512], F32) for i in range(3)]
        pOs = [b.ps(f"pOs{i}", [128, 512], F32) for i in range(2)]
        pOw = [b.ps(f"pOw{i}", [128, 512], F32) for i in range(2)]
        pTr = b.ps("pTr", [128, 8, 128], BF16)
        nrot = {"bct": 0, "scf": 0, "pT": 0, "pS": 0}

        def rot(name, lst):
            nrot[name] += 1
            return lst[nrot[name] % len(lst)]

        def finalize(po, ncol_off, h, qs, branch, first):
            qt = qs_base + qs
            o0 = ncol_off
            b.op("dve", lambda e: e.tensor_scalar_max(out=rs[:, 0:1], in0=po[:, o0 + 64:o0 + 65], scalar1=1e-30), reads=[po], writes=[rs])
            b.op("dve", lambda e: e.reciprocal(out=rs[:, 1:2], in_=rs[:, 0:1]), reads=[rs], writes=[rs])
            b.op("dve", lambda e: e.tensor_tensor(out=rg[:, 0:1], in0=rs[:, 1:2], in1=self.gts[:, qt, h * 3 + branch:h * 3 + branch + 1], op=ALU.mult),
                 reads=[rs, self.gts], writes=[rg])
            if first:
                b.op("dve", lambda e: e.tensor_scalar_mul(out=oacc[:, qs, h * 64:(h + 1) * 64], in0=po[:, o0:o0 + 64], scalar1=rg[:, 0:1]),
                     reads=[po, rg], writes=[oacc])
            else:
                b.op("dve", lambda e: e.scalar_tensor_tensor(out=oacc[:, qs, h * 64:(h + 1) * 64], in0=po[:, o0:o0 + 64], scalar=rg[:, 0:1],
                                                             in1=oacc[:, qs, h * 64:(h + 1) * 64], op0=ALU.mult, op1=ALU.add),
                     reads=[po, rg, oacc], writes=[oacc])

        nqg = getattr(self, "nqg_limit", 8)
        for qg in range(nqg):
            qs_base = 4 * qg
            q0 = 512 * qg
            Q = qm[qg % 2]
            b.dma("sp", Q[0:64, :, :], self.qT_d[:, :, q0:q0 + 512].rearrange("h d t -> d h t"), reads=[self.qT_d], writes=[Q])
            if qg < 2:
                b.op("pool", lambda e: e.memset(Q[64:128, :, :], 0.0), writes=[Q])
            for h in range(8):
                g = h // 4
                pc = pcT[h % 2]
                for ct in range(2):
                    p = rot("pS", pS)
                    b.op("pe", lambda e: e.matmul(p[:, :], lhsT=kcp[:, g, ct * 128:(ct + 1) * 128], rhs=Q[:, h, :], start=True, stop=True),
                         reads=[kcp, Q], writes=[p])
                    bt = rot("bct", bct)
                    b.dma("sp", bt[:], I["biasc"][h, ct, :, q0:q0 + 512], writes=[bt])
                    sc = rot("scf", scf)
                    b.op("dve", lambda e: e.tensor_tensor(out=sc[:, 0:512], in0=p[:, :], in1=bt[:], op=ALU.add), reads=[p, bt], writes=[sc])
                    b.op("act", lambda e: e.activation(out=pc[:, ct, :], in_=sc[:, 0:512], func=AF.Exp), reads=[sc], writes=[pc])
                po = pOs[h % 2]
                for qs in range(4):
                    for ct in range(2):
                        b.op("pe", lambda e: e.matmul(po[:, qs * 128:qs * 128 + 129] if False else po[:, 0:129], lhsT=pc[:, ct, qs * 128:(qs + 1) * 128],
                                                      rhs=self.vcA[:, g, ct, :], start=(ct == 0), stop=(ct == 1)), reads=[pc, self.vcA], writes=[po])
                    finalize(po, 0, h, qs, 0, True)
                    if h % 4 == 0:
                        b.op("dve", lambda e: e.tensor_scalar_mul(out=imp[:, qs, g, :], in0=po[:, 65:129], scalar1=rs[:, 1:2]), reads=[po, rs], writes=[imp])
                    else:
                        b.op("dve", lambda e: e.scalar_tensor_tensor(out=imp[:, qs, g, :], in0=po[:, 65:129], scalar=rs[:, 1:2], in1=imp[:, qs, g, :],
                                                                     op0=ALU.mult, op1=ALU.add), reads=[po, rs, imp], writes=[imp])
            if qg >= 2:
                for qs in range(4):
                    qt = qs_base + qs
                    for g in range(2):
                        b.op("dve", lambda e: e.tensor_tensor(out=impm[:], in0=imp[:, qs, g, :], in1=candneg[:, qt, :], op=ALU.add), reads=[imp, candneg], writes=[impm])
                        b.op("dve", lambda e: e.max(out=m8a[:], in_=impm[:]), reads=[impm], writes=[m8a])
                        b.op("dve", lambda e: e.match_replace(out=impm2[:], in_to_replace=m8a[:], in_values=impm[:], imm_value=-1e9), reads=[m8a, impm], writes=[impm2])
                        b.op("dve", lambda e: e.max(out=m8b[:], in_=impm2[:]), reads=[impm2], writes=[m8b])
                        b.op("dve", lambda e: e.tensor_scalar(out=msk[:], in0=impm[:], scalar1=m8b[:, 4:5], scalar2=None, op0=ALU.is_ge), reads=[impm, m8b], writes=[msk])
                        b.op("dve", lambda e: e.tensor_tensor(out=msk[:], in0=msk[:], in1=fz[:, qt, :], op=ALU.max), reads=[msk, fz], writes=[msk])
                        b.op("dve", lambda e: e.tensor_scalar(out=mb[:, 64:128], in0=msk[:], scalar1=-NEG, scalar2=NEG, op0=ALU.mult, op1=ALU.add), reads=[msk], writes=[mb])
                        b.op("pe", lambda e: e.transpose(out=pTr[:, 0, :], in_=mb[:], identity=self.ident[:]), reads=[mb, self.ident], writes=[pTr])
                        b.op("act", lambda e: e.copy(out=Q[64:128, 4 * g:4 * g + 4, qs * 128:(qs + 1) * 128],
                                                     in_=pTr[64:128, 0:1, :].to_broadcast([64, 4, 128])), reads=[pTr], writes=[Q])
            jobs = []
            for h in range(8):
                g = h // 4
                nkt = 4 * (qg + 1)
                for kt in range(nkt):
                    dlt = 4 * qg - kt
                    qstart = 0 if dlt >= 0 else -dlt * 128
                    jobs.append(dict(kind="s", h=h, g=g, kt=kt, qlo=qstart // 128, qhi=3, first=(kt == 0), last=False, lastkt=(kt == nkt - 1),
                                     tab=(ts, (128 if dlt == 1 else 0)) if dlt <= 1 else None))
                kts = [kt for kt in range(4 * qg - 4, 4 * qg + 4) if kt >= 0]
                for kt in kts:
                    qs_lo = max(0, kt - 4 * qg)
                    qs_hi = min(3, kt + 4 - 4 * qg)
                    jobs.append(dict(kind="w", h=h, g=g, kt=kt, qlo=qs_lo, qhi=qs_hi, first=False, last=(kt == kts[-1]), lastkt=(kt == kts[-1]),
                                     tab=(tw, 128 * (4 * qg + qs_lo - kt))))

            def emitS(j):
                h, g, kt = j["h"], j["g"], j["kt"]
                N = (j["qhi"] - j["qlo"] + 1) * 128
                p = rot("pS", pS)
                kmat = self.ksE if j["kind"] == "s" else kwp
                b.op("pe", lambda e: e.matmul(p[:, 0:N], lhsT=kmat[:, g, kt * 128:(kt + 1) * 128], rhs=Q[:, h, j["qlo"] * 128:(j["qhi"] + 1) * 128], start=True, stop=True),
                     reads=[kmat, Q], writes=[p])
                j["p"] = p
                j["N"] = N

            def emitE(j):
                h = j["h"]
                p, N = j["p"], j["N"]
                pt_ = rot("pT", pT)
                if j["tab"] is not None:
                    tab, c0 = j["tab"]
                    sc = rot("scf", scf)
                    b.op("dve", lambda e: e.tensor_tensor(out=sc[:, 0:N], in0=p[:, 0:N], in1=tab[:, h, c0:c0 + N], op=ALU.add), reads=[p, tab], writes=[sc])
                    b.op("act", lambda e: e.activation(out=pt_[:, 0:N], in_=sc[:, 0:N], func=AF.Exp), reads=[sc], writes=[pt_])
                else:
                    b.op("act", lambda e: e.activation(out=pt_[:, 0:N], in_=p[:, 0:N], func=AF.Exp, bias=b31[:, h:h + 1]), reads=[p, b31], writes=[pt_])
                j["pt"] = pt_

            def emitPV(j):
                h, g, kt = j["h"], j["g"], j["kt"]
                po_s, po_w = pOs[h % 2], pOw[h % 2]
                if j["first"]:
                    for po in (po_s, po_w):
                        b.op("pe", lambda e: e.matmul(po[:, 0:260], lhsT=zer[:, 0:128], rhs=zer[:, 0:260], start=True, stop=True), reads=[zer], writes=[po])
                po = po_s if j["kind"] == "s" else po_w
                va = self.vaug_s if j["kind"] == "s" else self.vaug_w
                for qs in range(j["qlo"], j["qhi"] + 1):
                    o = (qs - j["qlo"]) * 128
                    b.op("pe", lambda e: e.matmul(po[:, qs * 65:(qs + 1) * 65], lhsT=j["pt"][:, o:o + 128], rhs=va[:, kt, g, :],
                                                  start=False, stop=j["lastkt"], skip_group_check=True), reads=[j["pt"], va], writes=[po])
                if j["last"]:
                    for qs in range(4):
                        finalize(po_s, qs * 65, h, qs, 1, False)
                        finalize(po_w, qs * 65, h, qs, 2, False)

            LA = 2
            for i_ in range(len(jobs) + LA):
                if i_ < len(jobs):
                    emitS(jobs[i_])
                if i_ >= LA:
                    emitE(jobs[i_ - LA])
                    emitPV(jobs[i_ - LA])
            for qs in range(4):
                qt = qs_base + qs
                ot = oaT[qs % 2]
                b.op("act", lambda e: e.copy(out=oab[:], in_=oacc[:, qs, :]), reads=[oacc], writes=[oab])
                for c in range(4):
                    b.op("pe", lambda e: e.transpose(out=pTr[:, 4 + c, :], in_=oab[:, c * 128:(c + 1) * 128], identity=self.ident[:]), reads=[oab, self.ident], writes=[pTr])
                b.op("act", lambda e: e.copy(out=ot[:], in_=pTr[:, 4:8, :]), reads=[pTr], writes=[ot])
                b.dma("pool", self.oaT_d[:, :, qt * 128:(qt + 1) * 128].rearrange("c p t -> p c t"), ot[:], reads=[ot], writes=[self.oaT_d])
        if "attn" in self.debug:
            d = self.dbg_out("oaT", [4, 128, S], BF16)
            b.dma("pool", d, self.oaT_d[:], reads=[self.oaT_d])


Prog.phase_attn2 = _phase_attn2


def _phase_merge(self):
    b = self.b
    I = self.inp
    self.x1_d = b.dram("x1_d", [S, D], F32)
    with b.scope():
        gat = self.load_gain("gat2", I["attn_norm_g"][0])
        stage = [b.sb(f"mst{i}", [128, 1024], F32) for i in range(2)]
        wg = b.sb("wg", [128, 8, 2048], BF16)
        for n in range(2):
            for c in range(8):
                st = stage[c % 2]
                b.dma("sp", st[:], I["w_in"][0][c * 128:(c + 1) * 128, GA0 + n * 1024:GA0 + (n + 1) * 1024], writes=[st])
                b.op("act", lambda e: e.activation(out=wg[:, c, n * 1024:(n + 1) * 1024], in_=st[:], func=AF.Copy, scale=gat[:, c:c + 1]),
                     reads=[st, gat], writes=[wg])
        wa = b.sb("wa", [128, 4, 1024], BF16)
        wb = b.sb("wb", [128, 4, 1024], BF16)
        wo = b.sb("wo", [128, 8, 1024], BF16)
        self.load_weight(wa, I["w_proj_a"][0], 1024, kch=4, stage=stage, eng="dve")
        self.load_weight(wb, I["w_proj_b"][0], 1024, kch=4, stage=stage, eng="dve")
        self.load_weight(wo, I["w_out"][0], 1024, kch=8, stage=stage, eng="dve")
        xt = [b.sb(f"mxt{i}", [128, D], F32) for i in range(2)]
        junk = b.sb("mjunk", [128, D], BF16)
        ss = [b.sb(f"mss{i}", [128, 1], F32) for i in range(2)]
        hb = [b.sb(f"mhb{i}", [128, D], BF16) for i in range(2)]
        hT = [b.sb(f"mhT{i}", [128, 8, 128], BF16) for i in range(2)]
        oat = [b.sb(f"oat{i}", [128, 4, 128], BF16) for i in range(2)]
        obt = [b.sb(f"obt{i}", [128, 4, 128], BF16) for i in range(2)]
        sg = b.sb("msg", [128, 2048], F32)
        m1 = b.sb("m1", [128, 1024], F32)
        m2 = b.sb("m2", [128, 1024], F32)
        mgb = b.sb("mgb", [128, 1024], BF16)
        mT = b.sb("mT", [128, 8, 128], BF16)
        x1t = [b.sb(f"x1t{i}", [128, D], F32) for i in range(2)]
        pt = b.ps("mpt", [128, 8, 128], BF16)
        pg = [b.ps(f"mpg{i}", [128, 512], F32) for i in range(2)]
        pa = [b.ps(f"mpa{i}", [128, 512], F32) for i in range(2)]
        pb = [b.ps(f"mpb{i}", [128, 512], F32) for i in range(2)]
        for t in range(getattr(self, "nt_limit", NT)):
            i = t % 2
            self.make_hT(I["x"], t, xt[i], junk, ss[i], hb[i], pt, hT[i], self.ident)
            b.dma("sp", oat[i][:], self.oaT_d[:, :, t * 128:(t + 1) * 128].rearrange("c p t -> p c t"), reads=[self.oaT_d], writes=[oat[i]])
            b.dma("sp", obt[i][:], self.obT_d[:, :, t * 128:(t + 1) * 128].rearrange("c p t -> p c t"), reads=[self.obT_d], writes=[obt[i]])
            for n in range(4):
                p = pg[n % 2]
                for c in range(8):
                    b.op("pe", lambda e: e.matmul(p[:, :], lhsT=hT[i][:, c, :], rhs=wg[:, c, n * 512:(n + 1) * 512], start=(c == 0), stop=(c == 7)),
                         reads=[hT[i], wg], writes=[p])
                b.op("act", lambda e: e.activation(out=sg[:, n * 512:(n + 1) * 512], in_=p[:, :], func=AF.Sigmoid), reads=[p], writes=[sg])
            for n in range(2):
                for c in range(4):
                    b.op("pe", lambda e: e.matmul(pa[n][:, :], lhsT=oat[i][:, c, :], rhs=wa[:, c, n * 512:(n + 1) * 512], start=(c == 0), stop=(c == 3)),
                         reads=[oat[i], wa], writes=[pa[n]])
                for c in range(4):
                    b.op("pe", lambda e: e.matmul(pb[n][:, :], lhsT=obt[i][:, c, :], rhs=wb[:, c, n * 512:(n + 1) * 512], start=(c == 0), stop=(c == 3)),
                         reads=[obt[i], wb], writes=[pb[n]])
                b.op("dve", lambda e: e.tensor_tensor(out=m1[:, n * 512:(n + 1) * 512], in0=pa[n][:, :], in1=sg[:, n * 512:(n + 1) * 512], op=ALU.mult),
                     reads=[pa[n], sg], writes=[m1])
                b.op("dve", lambda e: e.tensor_tensor(out=m2[:, n * 512:(n + 1) * 512], in0=pb[n][:, :], in1=sg[:, 1024 + n * 512:1024 + (n + 1) * 512], op=ALU.mult),
                     reads=[pb[n], sg], writes=[m2])
            b.op("pool", lambda e: e.tensor_tensor(out=mgb[:], in0=m1[:], in1=m2[:], op=ALU.add), reads=[m1, m2], writes=[mgb])
            for c in range(8):
                b.op("pe", lambda e: e.transpose(out=pt[:, c, :], in_=mgb[:, c * 128:(c + 1) * 128], identity=self.ident[:]), reads=[mgb, self.ident], writes=[pt])
            b.op("act", lambda e: e.copy(out=mT[:], in_=pt[:]), reads=[pt], writes=[mT])
            for n in range(2):
                for c in range(8):
                    b.op("pe", lambda e: e.matmul(pa[n][:, :], lhsT=mT[:, c, :], rhs=wo[:, c, n * 512:(n + 1) * 512], start=(c == 0), stop=(c == 7)),
                         reads=[mT, wo], writes=[pa[n]])
                b.op("dve", lambda e: e.tensor_tensor(out=x1t[i][:, n * 512:(n + 1) * 512], in0=pa[n][:, :], in1=xt[i][:, n * 512:(n + 1) * 512], op=ALU.add),
                     reads=[pa[n], xt[i]], writes=[x1t[i]])
            b.dma("pool", self.x1_d[t * 128:(t + 1) * 128, :], x1t[i][:], reads=[x1t[i]], writes=[self.x1_d])
        if "merge" in self.debug:
            d = self.dbg_out("x1", [S, D], F32)
            b.dma("pool", d, self.x1_d[:], reads=[self.x1_d])


def _phase_ffn(self):
    b = self.b
    I = self.inp
    TG = 128
    NFT = 44
    with b.scope():
        gf = self.load_gain("gf", I["ffn_norm_g"][0])
        stage = [b.sb(f"fst{i}", [128, 1024], F32) for i in range(2)]
        wu = b.sb("wu", [128, 8, 2 * DFF], BF16)
        for n in range(8):
            for c in range(8):
                st = stage[c % 2]
                b.dma("sp", st[:, 0:704], I["w_up"][0][c * 128:(c + 1) * 128, n * 704:(n + 1) * 704], writes=[st])
                b.op("act", lambda e: e.activation(out=wu[:, c, n * 704:(n + 1) * 704], in_=st[:, 0:704], func=AF.Copy, scale=gf[:, c:c + 1]),
                     reads=[st, gf], writes=[wu])
        wd = b.sb("wd", [128, 22, D], BF16)
        self.load_weight(wd, I["w_down"][0], D, kch=22, stage=stage, eng="dve")
        cw = b.sb("cw", [128, 3, NFT], F32)
        for j in range(3):
            b.dma("sp", cw[:, j, :], I["conv_w"][0][j].rearrange("(c p) -> p c", p=128), writes=[cw], allow_slow_non_contiguous=True)
        cbias = self.load_gain("cbias", I["conv_b"][0], kch=NFT)
        carry = b.sb("carry", [128, NFT, 2], F32)
        b.op("pool", lambda e: e.memset(carry[:], 0.0), writes=[carry])
        xt = [b.sb(f"fxt{i}", [128, D], F32) for i in range(2)]
        junk = b.sb("fjunk", [128, D], BF16)
        ss = [b.sb(f"fss{i}", [128, 1], F32) for i in range(2)]
        hb = [b.sb(f"fhb{i}", [128, D], BF16) for i in range(2)]
        hT1 = [b.sb(f"fhT{i}", [128, 8, 128], BF16) for i in range(2)]
        hTg = b.sb("fhTg", [128, 8, TG], BF16)
        ub = [b.sb(f"ub{i}", [128, TG + 2], F32) for i in range(2)]
        cv = [b.sb(f"cv{i}", [128, TG], F32) for i in range(2)]
        sgl = b.sb("sgl", [128, TG], F32)
        actT = b.sb("actT", [128, 22, TG], BF16)
        self._val = b.sb("fval", [128, 22, TG], BF16)
        ot = xt
        pt = b.ps("fpt", [128, 8, 128], BF16)
        pu = [b.ps(f"fpu{i}", [128, 512], F32) for i in range(3)]
        pd = [b.ps(f"fpd{i}", [128, 512], F32) for i in range(2)]
        ng = getattr(self, "nt_limit", NT) * 128 // TG
        for gi in range(ng):
            for s_ in range(TG // 128):
                t = gi * (TG // 128) + s_
                self.make_hT(self.x1_d, t, xt[s_], junk, ss[s_], hb[s_], pt, hT1[s_], self.ident)
                b.op("pool", lambda e: e.tensor_copy(out=hTg[:, :, s_ * 128:(s_ + 1) * 128], in_=hT1[s_][:]), reads=[hT1[s_]], writes=[hTg])
            for ft in range(NFT):
                p = pu[ft % 3]
                u = ub[ft % 2]
                c_ = cv[(ft // 22) % 2] if False else cv[ft % 2]
                for c in range(8):
                    b.op("pe", lambda e: e.matmul(p[:, 0:TG], lhsT=wu[:, c, ft * 128:(ft + 1) * 128], rhs=hTg[:, c, :], start=(c == 0), stop=(c == 7)),
                         reads=[wu, hTg], writes=[p])
                b.op("act", lambda e: e.copy(out=u[:, 2:TG + 2], in_=p[:, 0:TG]), reads=[p], writes=[u])
                b.op("pool", lambda e: e.tensor_copy(out=u[:, 0:2], in_=carry[:, ft, :]), reads=[carry], writes=[u])
                b.op("pool", lambda e: e.tensor_copy(out=carry[:, ft, :], in_=u[:, TG:TG + 2]), reads=[u], writes=[carry])
                b.op("dve", lambda e: e.tensor_scalar(out=c_[:], in0=u[:, 0:TG], scalar1=cw[:, 0, ft:ft + 1], scalar2=cbias[:, ft:ft + 1], op0=ALU.mult, op1=ALU.add),
                     reads=[u, cw, cbias], writes=[c_])
                b.op("dve", lambda e: e.scalar_tensor_tensor(out=c_[:], in0=u[:, 1:TG + 1], scalar=cw[:, 1, ft:ft + 1], in1=c_[:], op0=ALU.mult, op1=ALU.add),
                     reads=[u, cw, c_], writes=[c_])
                if ft < 22:
                    b.op("dve", lambda e: e.scalar_tensor_tensor(out=self._val[:, ft, :], in0=u[:, 2:TG + 2], scalar=cw[:, 2, ft:ft + 1], in1=c_[:], op0=ALU.mult, op1=ALU.add),
                         reads=[u, cw, c_], writes=[self._val])
                else:
                    b.op("dve", lambda e: e.scalar_tensor_tensor(out=c_[:], in0=u[:, 2:TG + 2], scalar=cw[:, 2, ft:ft + 1], in1=c_[:], op0=ALU.mult, op1=ALU.add),
                         reads=[u, cw, c_], writes=[c_])
                    b.op("act", lambda e: e.activation(out=sgl[:], in_=c_[:], func=AF.Silu), reads=[c_], writes=[sgl])
                    b.op("dve", lambda e: e.tensor_tensor(out=actT[:, ft - 22, :], in0=sgl[:], in1=self._val[:, ft - 22, :], op=ALU.mult),
                         reads=[sgl, self._val], writes=[actT])
            for s_ in range(TG // 128):
                t = gi * (TG // 128) + s_
                for n in range(2):
                    for f in range(22):
                        b.op("pe", lambda e: e.matmul(pd[n][:, :], lhsT=actT[:, f, s_ * 128:(s_ + 1) * 128], rhs=wd[:, f, n * 512:(n + 1) * 512], start=(f == 0), stop=(f == 21)),
                             reads=[actT, wd], writes=[pd[n]])
                    b.op("dve", lambda e: e.tensor_tensor(out=ot[s_][:, n * 512:(n + 1) * 512], in0=pd[n][:, :], in1=xt[s_][:, n * 512:(n + 1) * 512], op=ALU.add),
                         reads=[pd[n], xt[s_]], writes=[ot[s_]])
                b.dma("pool", self.out[t * 128:(t + 1) * 128, :], ot[s_][:], reads=[ot[s_]])


Prog.phase_merge = _phase_merge
Prog.phase_ffn = _phase_ffn


def _phase_rwkv(self):
    b = self.b
    I = self.inp
    TG = 256
    NCH = TG // 64
    tt = lambda eng, out, in0, in1, op, rd, wr: b.op(eng, lambda e: e.tensor_tensor(out=out, in0=in0, in1=in1, op=op), reads=rd, writes=wr)
    with b.scope():
        gat = self.load_gain("gat3", I["attn_norm_g"][0])
        stage = [b.sb(f"rst{i}", [128, 1792], F32) for i in range(2)]
        wr = b.sb("wr", [128, 8, 1792], BF16)
        self.load_weight(wr, I["w_in"][0][:, RW0:RW0 + 1792], 1792, gvec=gat, stage=stage)

        def colvec(name, src, n):
            t = b.sb(name, [64, n], F32)
            b.dma("sp", t[:], src.rearrange("(c p) -> p c", p=64), writes=[t], allow_slow_non_contiguous=True)
            return t
        mu = colvec("mu", I["rwkv_mu"][0], 28)
        w0 = colvec("w0", I["rwkv_w0"][0], 8)
        a0 = colvec("a0", I["rwkv_a0"][0], 8)
        k_k = colvec("k_k", I["rwkv_k_k"][0], 8)
        k_a = colvec("k_a", I["rwkv_k_a"][0], 8)
        r_k = colvec("r_k", I["rwkv_r_k"][0].rearrange("h d -> (h d)"), 8)
        w2s = b.sb("w2s", [64, 512], F32)
        a2s = b.sb("a2s", [64, 512], F32)
        g2s = b.sb("g2s", [64, 2, 512], F32)
        b.dma("sp", w2s[:], I["rwkv_w2"][0], writes=[w2s])
        b.dma("sp", a2s[:], I["rwkv_a2"][0], writes=[a2s])
        b.dma("sp", g2s[:], I["rwkv_g2"][0].rearrange("(two l) f -> l two f", two=2), writes=[g2s])
        lng = b.sb("lng", [64, 512], F32)
        lnb = b.sb("lnb", [64, 512], F32)
        b.dma("sp", lng[:], I["rwkv_ln_g"][0].partition_broadcast(64), writes=[lng])
        b.dma("sp", lnb[:], I["rwkv_ln_b"][0].partition_broadcast(64), writes=[lnb])
        msk = b.sb("rmsk", [64, 3, 64], F32)
        b.dma("sp", msk[:], I["rwmask"], writes=[msk])
        rstm = b.sb("rstm", [64, TG], F32)
        b.dma("sp", rstm[:], I["rwreset"][:, 0:TG], writes=[rstm])
        ones = b.sb("ones64", [64, 64], F32)
        b.op("pool", lambda e: e.memset(ones[:], 1.0), writes=[ones])
        idf = self.identf
        carry = b.sb("rcarry", [64, 28], F32)
        b.op("pool", lambda e: e.memset(carry[:], 0.0), writes=[carry])
        Hs = [[b.sb(f"H{h}_{i}", [64, 64], F32) for i in range(2)] for h in range(8)]
        for h in range(8):
            b.op("pool", lambda e: e.memset(Hs[h][0][:], 0.0), writes=[Hs[h][0]])
        xt = [b.sb(f"rxt{i}", [128, D], F32) for i in range(2)]
        junk = b.sb("rjunk", [128, D], BF16)
        ss = [b.sb(f"rss{i}", [128, 1], F32) for i in range(2)]
        hb = [b.sb(f"rhb{i}", [128, D], BF16) for i in range(2)]
        hT1 = [b.sb(f"rhT{i}", [128, 8, 128], BF16) for i in range(2)]
        hTg = b.sb("rhTg", [128, 8, TG], BF16)
        pbuf = [b.sb(f"rpb{i}", [64, TG + 1], F32) for i in range(2)]
        dtmp = b.sb("rdtmp", [64, TG], F32)
        X = [b.sb(f"rX{w}", [64, 8, TG], F32) for w in range(3)]
        xs = b.sb("rxs", [64, 4, TG], F32)
        BV = b.sb("rBV", [64, 8, TG], F32)
        Ytm = b.sb("rYtm", [64, NCH, 8, 64], F32)
        sqv = b.sb("rsqv", [64, NCH, 8, 64], F32)
        st1 = b.sb("rst1", [64, NCH * 8], F32)
        st2 = b.sb("rst2", [64, NCH * 8], F32)
        T = {n: b.sb("r" + n, [64, TG], F32) for n in ["lw", "as", "kk", "sq", "kkn", "bv", "kp", "t1", "L", "Lx", "Ep", "Em", "Ex", "BT", "KT", "BG", "KG", "rk"]}
        AR = b.sb("rAR", [64, NCH, 2, 64], F32)
        TM = [b.sb(f"rTM{i}", [64, 3, 64], F32) for i in range(2)]
        XM = [b.sb(f"rXM{i}", [64, 4, 64], F32) for i in range(2)]
        AA = [b.sb(f"rAA{i}", [64, 2, 64], F32) for i in range(3)]
        PP = [b.sb(f"rPP{i}", [64, 64], F32) for i in range(3)]
        Xs = b.sb("rXs", [64, 64], F32)
        Us = b.sb("rUs", [64, 64], F32)
        obf = [b.sb(f"robf{i}", [64, TG], BF16) for i in range(2)]
        otmp = b.sb("rotmp", [64, TG], F32)
        pt = b.ps("rpt", [128, 8, 128], BF16)
        pp = [b.ps(f"rpp{i}", [128, 512], F32) for i in range(2)]
        pq = [b.ps(f"rpq{i}", [128, 512], F32) for i in range(2)]
        pd = [b.ps(f"rpd{i}", [128, 512], F32) for i in range(2)]
        pz = b.ps("rpz", [128, 512], F32)
        cnt = {"pp": 0, "pq": 0, "pd": 0, "aa": 0, "ppb": 0, "tm": 0, "xm": 0, "pb": 0}

        def nxt(k, lst):
            cnt[k] += 1
            return lst[cnt[k] % len(lst)]

        ngr = getattr(self, "nrg_limit", S // TG)
        for gi in range(ngr):
            q0 = gi * TG
            for s_ in range(TG // 128):
                t = gi * (TG // 128) + s_
                self.make_hT(I["x"], t, xt[s_], junk, ss[s_], hb[s_], pt, hT1[s_], self.ident)
                b.op("pool", lambda e: e.tensor_copy(out=hTg[:, :, s_ * 128:(s_ + 1) * 128], in_=hT1[s_][:]), reads=[hT1[s_]], writes=[hTg])

            def proj_lerp(fc, out_ap, out_buf, post=None):
                p = nxt("pp", pp)
                for c in range(8):
                    b.op("pe", lambda e: e.matmul(p[0:64, 0:TG], lhsT=wr[:, c, fc * 64:(fc + 1) * 64], rhs=hTg[:, c, :], start=(c == 0), stop=(c == 7)),
                         reads=[wr, hTg], writes=[p])
                pb_ = nxt("pb", pbuf)
                b.op("act", lambda e: e.copy(out=pb_[:, 1:TG + 1], in_=p[0:64, 0:TG]), reads=[p], writes=[pb_])
                b.op("pool", lambda e: e.tensor_copy(out=pb_[:, 0:1], in_=carry[:, fc:fc + 1]), reads=[carry], writes=[pb_])
                b.op("pool", lambda e: e.tensor_copy(out=carry[:, fc:fc + 1], in_=pb_[:, TG:TG + 1]), reads=[pb_], writes=[carry])
                tt("dve", dtmp[:], pb_[:, 0:TG], pb_[:, 1:TG + 1], ALU.subtract, [pb_], [dtmp])
                b.op("dve", lambda e: e.scalar_tensor_tensor(out=out_ap, in0=dtmp[:], scalar=mu[:, fc:fc + 1], in1=pb_[:, 1:TG + 1], op0=ALU.mult, op1=ALU.add),
                     reads=[dtmp, mu, pb_], writes=[out_buf])

            for w in range(3):
                for h in range(8):
                    proj_lerp(w * 8 + h, X[w][:, h, :], X[w])
            for j in range(4):
                proj_lerp(24 + j, xs[:, j, :], xs)
            b.op("act", lambda e: e.activation(out=xs[:, 0, :], in_=xs[:, 0, :], func=AF.Tanh), reads=[xs], writes=[xs])
            b.op("act", lambda e: e.activation(out=xs[:, 2:4, :], in_=xs[:, 2:4, :], func=AF.Sigmoid), reads=[xs], writes=[xs])

            for h in range(8):
                hs = slice(h * 64, (h + 1) * 64)
                R_, K_, V_ = X[0][:, h, :], X[1][:, h, :], X[2][:, h, :]
                p = nxt("pp", pp)
                b.op("pe", lambda e: e.matmul(p[0:64, 0:TG], lhsT=w2s[:, hs], rhs=xs[:, 0, :], start=True, stop=True), reads=[w2s, xs], writes=[p])
                b.op("act", lambda e: e.activation(out=T["lw"][:], in_=p[0:64, 0:TG], func=AF.Sigmoid, bias=w0[:, h:h + 1]), reads=[p, w0], writes=[T["lw"]])
                b.op("pool", lambda e: e.tensor_scalar_mul(out=T["lw"][:], in0=T["lw"][:], scalar1=-0.6065306597126334), reads=[T["lw"]], writes=[T["lw"]])
                p = nxt("pp", pp)
                b.op("pe", lambda e: e.matmul(p[0:64, 0:TG], lhsT=a2s[:, hs], rhs=xs[:, 1, :], start=True, stop=True), reads=[a2s, xs], writes=[p])
                b.op("act", lambda e: e.activation(out=T["as"][:], in_=p[0:64, 0:TG], func=AF.Sigmoid, bias=a0[:, h:h + 1]), reads=[p, a0], writes=[T["as"]])
                b.op("dve", lambda e: e.tensor_scalar_mul(out=T["kk"][:], in0=K_, scalar1=k_k[:, h:h + 1]), reads=[X[1], k_k], writes=[T["kk"]])
                tt("pool", T["sq"][:], T["kk"][:], T["kk"][:], ALU.mult, [T["kk"]], [T["sq"]])
                p = nxt("pp", pp)
                b.op("pe", lambda e: e.matmul(p[0:64, 0:TG], lhsT=ones[:], rhs=T["sq"][:], start=True, stop=True), reads=[ones, T["sq"]], writes=[p])
                b.op("act", lambda e: e.activation(out=T["sq"][:], in_=p[0:64, 0:TG], func=AF.Sqrt), reads=[p], writes=[T["sq"]])
                b.op("dve", lambda e: e.tensor_scalar_max(out=T["sq"][:], in0=T["sq"][:], scalar1=1e-12), reads=[T["sq"]], writes=[T["sq"]])
                b.op("dve", lambda e: e.reciprocal(out=T["sq"][:], in_=T["sq"][:]), reads=[T["sq"]], writes=[T["sq"]])
                tt("dve", T["kkn"][:], T["kk"][:], T["sq"][:], ALU.mult, [T["kk"], T["sq"]], [T["kkn"]])
                tt("pool", T["bv"][:], T["kkn"][:], T["as"][:], ALU.mult, [T["kkn"], T["as"]], [T["bv"]])
                b.op("dve", lambda e: e.tensor_scalar(out=T["t1"][:], in0=T["as"][:], scalar1=-1.0, scalar2=k_a[:, h:h + 1], op0=ALU.add, op1=ALU.mult),
                     reads=[T["as"], k_a], writes=[T["t1"]])
                b.op("dve", lambda e: e.scalar_tensor_tensor(out=T["kp"][:], in0=T["t1"][:], scalar=1.0, in1=K_, op0=ALU.add, op1=ALU.mult),
                     reads=[T["t1"], X[1]], writes=[T["kp"]])
                tt("pool", T["rk"][:], R_, T["kp"][:], ALU.mult, [X[0], T["kp"]], [T["rk"]])
                b.op("pool", lambda e: e.tensor_scalar_mul(out=T["rk"][:], in0=T["rk"][:], scalar1=r_k[:, h:h + 1]), reads=[T["rk"], r_k], writes=[T["rk"]])
                p = nxt("pp", pp)
                b.op("pe", lambda e: e.matmul(p[0:64, 0:TG], lhsT=ones[:], rhs=T["rk"][:], start=True, stop=True), reads=[ones, T["rk"]], writes=[p])
                tt("dve", BV[:, h, :], p[0:64, 0:TG], V_, ALU.mult, [p, X[2]], [BV])
                b.op("dve", lambda e: e.tensor_tensor_scan(out=T["L"][:], data0=rstm[:], data1=T["lw"][:], initial=0.0, op0=ALU.mult, op1=ALU.add),
                     reads=[rstm, T["lw"]], writes=[T["L"]])
                tt("pool", T["Lx"][:], T["L"][:], T["lw"][:], ALU.subtract, [T["L"], T["lw"]], [T["Lx"]])
                b.op("act", lambda e: e.activation(out=T["Ep"][:], in_=T["L"][:], func=AF.Exp), reads=[T["L"]], writes=[T["Ep"]])
                b.op("act", lambda e: e.activation(out=T["Em"][:], in_=T["L"][:], func=AF.Exp, scale=-1.0), reads=[T["L"]], writes=[T["Em"]])
                b.op("act", lambda e: e.activation(out=T["Ex"][:], in_=T["Lx"][:], func=AF.Exp), reads=[T["Lx"]], writes=[T["Ex"]])
                c3 = lambda ap: ap.rearrange("p (c t) -> p c t", t=64)
                b.op("dve", lambda e: e.scalar_tensor_tensor(out=AR[:, :, 0, :], in0=c3(T["kkn"][:]), scalar=-1.0, in1=c3(T["Ex"][:]), op0=ALU.mult, op1=ALU.mult),
                     reads=[T["kkn"], T["Ex"]], writes=[AR])
                tt("pool", AR[:, :, 1, :], c3(R_), c3(T["Ep"][:]), ALU.mult, [X[0], T["Ep"]], [AR])
                tt("dve", T["BT"][:], T["bv"][:], T["Em"][:], ALU.mult, [T["bv"], T["Em"]], [T["BT"]])
                tt("pool", T["KT"][:], T["kp"][:], T["Em"][:], ALU.mult, [T["kp"], T["Em"]], [T["KT"]])
                gC = c3(T["Ep"][:])[:, :, 63:64].to_broadcast([64, NCH, 64])
                tt("dve", c3(T["BG"][:]), c3(T["BT"][:]), gC, ALU.mult, [T["BT"], T["Ep"]], [T["BG"]])
                tt("pool", c3(T["KG"][:]), c3(T["KT"][:]), gC, ALU.mult, [T["KT"], T["Ep"]], [T["KG"]])
                for c in range(NCH):
                    cs = slice(c * 64, (c + 1) * 64)
                    Hc = Hs[h][(gi * NCH + c) % 2]
                    Hn = Hs[h][(gi * NCH + c + 1) % 2]
                    p = nxt("pq", pq)
                    for j, (src, sb_) in enumerate([(V_[:, cs], X[2]), (T["BG"][:, cs], T["BG"]), (T["KG"][:, cs], T["KG"])]):
                        b.op("pe", lambda e: e.transpose(out=p[0:64, j * 64:(j + 1) * 64], in_=src, identity=idf[0:64, 0:64]), reads=[sb_, idf], writes=[p])
                    tm = nxt("tm", TM)
                    b.op("act", lambda e: e.copy(out=tm[:].rearrange("p a b -> p (a b)"), in_=p[0:64, 0:192]), reads=[p], writes=[tm])
                    p = nxt("pq", pq)
                    arc = AR[:, c, :, :].rearrange("p a t -> p (a t)")
                    b.op("pe", lambda e: e.matmul(p[0:64, 0:128], lhsT=T["BT"][:, cs], rhs=arc, start=True, stop=True), reads=[T["BT"], AR], writes=[p])
                    b.op("pe", lambda e: e.matmul(p[0:64, 128:256], lhsT=T["KT"][:, cs], rhs=arc, start=True, stop=True), reads=[T["KT"], AR], writes=[p])
                    b.op("pe", lambda e: e.matmul(p[0:64, 256:320], lhsT=AR[:, c, 0, :], rhs=T["BT"][:, cs], start=True, stop=True), reads=[T["BT"], AR], writes=[p])
                    xm = nxt("xm", XM)
                    tt("dve", xm[:].rearrange("p (a m) t -> p a m t", a=2), p[0:64, 0:256].rearrange("p (a m t) -> p a m t", a=2, m=2),
                       msk[:, None, 0:2, :].to_broadcast([64, 2, 2, 64]), ALU.mult, [p, msk], [xm])
                    aa = nxt("aa", AA)
                    b.op("pool", lambda e: e.tensor_copy(out=aa[:, 0, :], in_=xm[:, 0, :]), reads=[xm], writes=[aa])
                    tt("dve", aa[:, 1, :], p[0:64, 256:320], msk[:, 2, :], ALU.mult, [p, msk], [aa])
                    P_ = nxt("ppb", PP)
                    tt("pool", P_[:], xm[:, 0, :], idf[0:64, 0:64], ALU.add, [xm, idf], [P_])
                    for step in range(5):
                        pdb = nxt("pd", pd)
                        b.op("pe", lambda e: e.matmul(pdb[0:64, 0:64], lhsT=aa[:, 1, :], rhs=aa[:, 0, :], start=True, stop=True), reads=[aa], writes=[pdb])
                        b.op("pe", lambda e: e.matmul(pdb[0:64, 64:128], lhsT=aa[:, 0, :], rhs=aa[:, 1, :], start=True, stop=True), reads=[aa], writes=[pdb])
                        aa2 = nxt("aa", AA)
                        b.op("act", lambda e: e.copy(out=aa2[:].rearrange("p a t -> p (a t)"), in_=pdb[0:64, 0:128]), reads=[pdb], writes=[aa2])
                        b.op("pe", lambda e: e.matmul(pdb[0:64, 128:192], lhsT=aa2[:, 1, :], rhs=P_[:], start=True, stop=True), reads=[aa2, P_], writes=[pdb])
                        P2 = nxt("ppb", PP)
                        tt("dve", P2[:], pdb[0:64, 128:192], P_[:], ALU.add, [pdb, P_], [P2])
                        aa, P_ = aa2, P2
                    b.op("pe", lambda e: e.matmul(pz[0:64, 0:64], lhsT=xm[:, 2, :], rhs=tm[:, 0, :], start=True, stop=False), reads=[xm, tm], writes=[pz])
                    b.op("pe", lambda e: e.matmul(pz[0:64, 0:64], lhsT=AR[:, c, 0, :], rhs=Hc[:], start=False, stop=True), reads=[AR, Hc], writes=[pz])
                    b.op("act", lambda e: e.copy(out=Xs[:], in_=pz[0:64, 0:64]), reads=[pz], writes=[Xs])
                    b.op("pe", lambda e: e.matmul(pz[0:64, 64:128], lhsT=P_[:], rhs=Xs[:], start=True, stop=True), reads=[P_, Xs], writes=[pz])
                    b.op("act", lambda e: e.copy(out=Us[:], in_=pz[0:64, 64:128]), reads=[pz], writes=[Us])
                    b.op("pe", lambda e: e.matmul(pz[0:64, 128:192], lhsT=AR[:, c, 1, :], rhs=Hc[:], start=True, stop=False), reads=[AR, Hc], writes=[pz])
                    b.op("pe", lambda e: e.matmul(pz[0:64, 128:192], lhsT=xm[:, 1, :], rhs=Us[:], start=False, stop=False), reads=[xm, Us], writes=[pz])
                    b.op("pe", lambda e: e.matmul(pz[0:64, 128:192], lhsT=xm[:, 3, :], rhs=tm[:, 0, :], start=False, stop=True), reads=[xm, tm], writes=[pz])
                    b.op("pe", lambda e: e.matmul(pz[0:64, 192:256], lhsT=tm[:, 1, :], rhs=Us[:], start=True, stop=False), reads=[tm, Us], writes=[pz])
                    b.op("pe", lambda e: e.matmul(pz[0:64, 192:256], lhsT=tm[:, 2, :], rhs=tm[:, 0, :], start=False, stop=True), reads=[tm], writes=[pz])
                    b.op("act", lambda e: e.copy(out=Ytm[:, c, h, :], in_=pz[0:64, 128:192]), reads=[pz], writes=[Ytm])
                    b.op("dve", lambda e: e.scalar_tensor_tensor(out=Hn[:], in0=Hc[:], scalar=T["Ep"][:, c * 64 + 63:c * 64 + 64], in1=pz[0:64, 192:256],
                                                                 op0=ALU.mult, op1=ALU.add), reads=[Hc, T["Ep"], pz], writes=[Hn])
            Y3 = Ytm[:].rearrange("p c h i -> p (c h) i")
            S3 = sqv[:].rearrange("p c h i -> p (c h) i")
            b.op("dve", lambda e: e.tensor_reduce(out=st1[:], in_=Y3, axis=AX.X, op=ALU.add), reads=[Ytm], writes=[st1])
            b.op("pool", lambda e: e.tensor_scalar_mul(out=st1[:], in0=st1[:], scalar1=1.0 / 64), reads=[st1], writes=[st1])
            tt("dve", Y3, Y3, st1[:].unsqueeze(2).to_broadcast([64, NCH * 8, 64]), ALU.subtract, [Ytm, st1], [Ytm])
            tt("pool", S3, Y3, Y3, ALU.mult, [Ytm], [sqv])
            b.op("dve", lambda e: e.tensor_reduce(out=st2[:], in_=S3, axis=AX.X, op=ALU.add), reads=[sqv], writes=[st2])
            b.op("act", lambda e: e.activation(out=st2[:], in_=st2[:], func=AF.Sqrt, scale=1.0 / 64, bias=64e-5), reads=[st2], writes=[st2])
            b.op("dve", lambda e: e.reciprocal(out=st2[:], in_=st2[:]), reads=[st2], writes=[st2])
            tt("dve", Y3, Y3, st2[:].unsqueeze(2).to_broadcast([64, NCH * 8, 64]), ALU.mult, [Ytm, st2], [Ytm])
            lg = lng[:].rearrange("p (h i) -> p h i", i=64)[:, None, :, :].to_broadcast([64, NCH, 8, 64])
            lb = lnb[:].rearrange("p (h i) -> p h i", i=64)[:, None, :, :].to_broadcast([64, NCH, 8, 64])
            tt("pool", Ytm[:], Ytm[:], lg, ALU.mult, [Ytm, lng], [Ytm])
            tt("dve", Ytm[:], Ytm[:], lb, ALU.add, [Ytm, lnb], [Ytm])
            for h in range(8):
                p = nxt("pq", pq)
                for c in range(NCH):
                    b.op("pe", lambda e: e.transpose(out=p[0:64, c * 64:(c + 1) * 64], in_=Ytm[:, c, h, :], identity=idf[0:64, 0:64]), reads=[Ytm, idf], writes=[p])
                tt("dve", otmp[:], p[0:64, 0:TG], BV[:, h, :], ALU.add, [p, BV], [otmp])
                pg_ = nxt("pp", pp)
                b.op("pe", lambda e: e.matmul(pg_[0:64, 0:TG], lhsT=g2s[:, 0, h * 64:(h + 1) * 64], rhs=xs[:, 2, :], start=True, stop=False), reads=[g2s, xs], writes=[pg_])
                b.op("pe", lambda e: e.matmul(pg_[0:64, 0:TG], lhsT=g2s[:, 1, h * 64:(h + 1) * 64], rhs=xs[:, 3, :], start=False, stop=True), reads=[g2s, xs], writes=[pg_])
                ob_ = obf[h % 2]
                tt("dve", ob_[:], otmp[:], pg_[0:64, 0:TG], ALU.mult, [otmp, pg_], [ob_])
                b.dma("pool", self.obT_d[h // 2, (h % 2) * 64:(h % 2) * 64 + 64, q0:q0 + TG], ob_[:], reads=[ob_], writes=[self.obT_d])
        if "rwkv" in self.debug:
            d = self.dbg_out("obT", [4, 128, S], BF16)
            b.dma("pool", d, self.obT_d[:], reads=[self.obT_d])


Prog.phase_rwkv = _phase_rwkv


def build_full():
    p = Prog()
    b = p.b
    p.alloc_root()
    with b.scope():
        p.alloc_persistent()
        p.phase_nsa_proj()
        p.phase_attn2()
    p.phase_rwkv2()
    p.phase_merge()
    p.phase_ffn2()
    p.finish()
    return p


def kernel(**inputs):
    p = build_full()
    consts = host_consts(inputs["rel_bias"])
    shared = {k: np.ascontiguousarray(np.asarray(inputs[k], np.float32)) for k in W_SPECS if k != "x"}
    shared.update(consts)
    x = np.asarray(inputs["x"], np.float32)
    in_maps = []
    for c in range(8):
        m = dict(shared)
        m["x"] = np.ascontiguousarray(x[c])
        in_maps.append(m)
    res = run_bass_kernel_spmd(p.nc, in_maps, core_ids=list(range(8)))
    return np.stack([np.asarray(r["out"], np.float32) for r in res.results], axis=0)


def _phase_rwkv2(self):
    b = self.b
    I = self.inp
    TG = 128
    NCH = 2
    tt = lambda eng, out, in0, in1, op, rd, wr: b.op(eng, lambda e: e.tensor_tensor(out=out, in0=in0, in1=in1, op=op), reads=rd, writes=wr)
    with b.scope():
        W1 = b.sb("W1", [128, 8, 1792], BF16)
        W2 = b.sb("W2", [128, 8, 1792], BF16)
        with b.scope():
            gat = self.load_gain("gat3", I["attn_norm_g"][0])
            stage = [b.sb(f"rst{i}", [128, 1792], F32) for i in range(2)]
            tmpw = [b.sb(f"rtw{i}", [128, 1792], F32) for i in range(2)]
            mur = self.bcast_row("mur", I["rwkv_mu"][0], 1792)
            for c in range(8):
                st = stage[c % 2]
                tw_ = tmpw[c % 2]
                b.dma("sp", st[:], I["w_in"][0][c * 128:(c + 1) * 128, RW0:RW0 + 1792], writes=[st])
                tt("dve", tw_[:], st[:], mur[:], ALU.mult, [st, mur], [tw_])
                b.op("act", lambda e: e.activation(out=W2[:, c, :], in_=tw_[:], func=AF.Copy, scale=gat[:, c:c + 1]), reads=[tw_, gat], writes=[W2])
                tt("pool", st[:], st[:], tw_[:], ALU.subtract, [st, tw_], [st])
                b.op("act", lambda e: e.activation(out=W1[:, c, :], in_=st[:], func=AF.Copy, scale=gat[:, c:c + 1]), reads=[st, gat], writes=[W1])

        def colvec(name, src, n):
            t = b.sb(name, [64, n], F32)
            b.dma("sp", t[:], src.rearrange("(c p) -> p c", p=64), writes=[t], allow_slow_non_contiguous=True)
            return t
        w0 = colvec("w0", I["rwkv_w0"][0], 8)
        a0 = colvec("a0", I["rwkv_a0"][0], 8)
        k_k = colvec("k_k", I["rwkv_k_k"][0], 8)
        k_a = colvec("k_a", I["rwkv_k_a"][0], 8)
        r_k = colvec("r_k", I["rwkv_r_k"][0].rearrange("h d -> (h d)"), 8)
        w2s = b.sb("w2s", [64, 512], F32)
        a2s = b.sb("a2s", [64, 512], F32)
        g2s = b.sb("g2s", [64, 2, 512], F32)
        b.dma("sp", w2s[:], I["rwkv_w2"][0], writes=[w2s])
        b.dma("sp", a2s[:], I["rwkv_a2"][0], writes=[a2s])
        b.dma("sp", g2s[:], I["rwkv_g2"][0].rearrange("(two l) f -> l two f", two=2), writes=[g2s])
        lng = b.sb("lng", [64, 512], F32)
        lnb = b.sb("lnb", [64, 512], F32)
        b.dma("sp", lng[:], I["rwkv_ln_g"][0].partition_broadcast(64), writes=[lng])
        b.dma("sp", lnb[:], I["rwkv_ln_b"][0].partition_broadcast(64), writes=[lnb])
        msk = b.sb("rmsk", [64, 3, 64], F32)
        b.dma("sp", msk[:], I["rwmask"], writes=[msk])
        rstm = b.sb("rstm", [64, 8 * TG], F32)
        b.dma("sp", rstm[:], I["rwreset"], writes=[rstm])
        ones = b.sb("ones64", [64, 64], F32)
        b.op("pool", lambda e: e.memset(ones[:], 1.0), writes=[ones])
        idf = self.identf
        Hst = b.sb("rH", [64, 2, 8, 64], F32)
        b.op("pool", lambda e: e.memset(Hst[:], 0.0), writes=[Hst])
        xt = [b.sb(f"rxt{i}", [128, D], F32) for i in range(1)] * 2
        junk = b.sb("rjunk", [128, D], BF16)
        ss = [b.sb(f"rss{i}", [128, 1], F32) for i in range(1)] * 2
        hb = [b.sb(f"rhb{i}", [128, D], BF16) for i in range(1)] * 2
        hT1 = [b.sb(f"rhT{i}", [128, 8, 128], BF16) for i in range(1)] * 2
        hTs = b.sb("rhTs", [128, 8, TG + 1], BF16)
        b.op("pool", lambda e: e.memset(hTs[:], 0.0), writes=[hTs])
        XL = b.sb("rXL", [64, 20, TG], F32)
        Vtm = b.sb("rVtm", [64, NCH, 512], F32)
        names = ["LW", "AS", "KKN", "BVc", "KP", "RK", "L", "EP", "EM", "BG", "KG"]
        T = {n: b.sb("r" + n, [64, 8, TG], F32) for n in names}
        T["NR"] = T["RK"]
        T["T1"] = T["BG"]
        T["KK"] = T["KG"]
        T["EX"] = T["L"]
        T["BT"] = T["LW"]
        T["KT"] = T["AS"]
        AR = b.sb("rAR", [64, 8, NCH, 2, 64], F32)
        BON = b.sb("rBON", [64, NCH * 8], F32)
        Ytm = b.sb("rYtm", [64, NCH, 8, 64], F32)
        sqv = b.sb("rsqv", [64, NCH, 8, 64], F32)
        st1 = b.sb("rst1", [64, NCH * 8], F32)
        st2 = b.sb("rst2", [64, NCH * 8], F32)
        TM4 = [b.sb(f"rTM{i}", [64, 4, 2, 64], F32) for i in range(2)]
        XM4 = [b.sb(f"rXM{i}", [64, 4, 4, 64], F32) for i in range(2)]
        AA4 = [b.sb(f"rAA{i}", [64, 4, 2, 64], F32) for i in range(2)]
        PP4 = [b.sb(f"rPP{i}", [64, 4, 64], F32) for i in range(2)]
        Xs4 = b.sb("rXs4", [64, 4, 64], F32)
        Us4 = b.sb("rUs4", [64, 4, 64], F32)
        Ht4 = b.sb("rHt4", [64, 4, 64], F32)
        OBb = b.sb("rOBb", [64, NCH, 512], BF16)
        obT = [b.sb(f"robT{i}", [128, 4, TG], BF16) for i in range(1)] * 2
        pt = b.ps("rpt", [128, 8, 128], BF16)
        pP = b.ps("rpP", [128, 512], F32)
        pA = b.ps("rpA", [128, 1024], F32)
        pB = b.ps("rpB", [128, 512], F32)
        pC = b.ps("rpC", [128, 512], F32)
        pD = b.ps("rpD", [128, 512], F32)
        pZ = b.ps("rpZ", [128, 512], F32)
        cnt = {}

        def nxt(k, lst):
            cnt[k] = cnt.get(k, 0) + 1
            return lst[cnt[k] % len(lst)]
        bc = lambda v: v[:].unsqueeze(2).to_broadcast([64, 8, TG])
        f2 = lambda t_: t_[:].rearrange("p h t -> p (h t)")
        c16 = lambda t_: t_[:].rearrange("p h (c t) -> p (h c) t", t=64)

        ngr = getattr(self, "nrg_limit", S // TG)
        for gi in range(ngr):
            q0 = gi * TG
            i = gi % 2
            self.make_hT(I["x"], gi, xt[i], junk, ss[i], hb[i], pt, hT1[i], self.ident)
            b.op("pool", lambda e: e.tensor_copy(out=hTs[:, :, 0:1], in_=hTs[:, :, TG:TG + 1]), reads=[hTs], writes=[hTs])
            b.op("pool", lambda e: e.tensor_copy(out=hTs[:, :, 1:TG + 1], in_=hT1[i][:]), reads=[hT1[i]], writes=[hTs])
            ftiles = list(range(0, 16)) + [24, 25, 26, 27]
            for q4 in range(5):
                for j in range(4):
                    fc = ftiles[q4 * 4 + j]
                    for c in range(8):
                        b.op("pe", lambda e: e.matmul(pP[0:64, j * TG:(j + 1) * TG], lhsT=W1[:, c, fc * 64:(fc + 1) * 64], rhs=hTs[:, c, 1:TG + 1], start=(c == 0), stop=False),
                             reads=[W1, hTs], writes=[pP])
                    for c in range(8):
                        b.op("pe", lambda e: e.matmul(pP[0:64, j * TG:(j + 1) * TG], lhsT=W2[:, c, fc * 64:(fc + 1) * 64], rhs=hTs[:, c, 0:TG], start=False, stop=(c == 7)),
                             reads=[W2, hTs], writes=[pP])
                b.op("act", lambda e: e.copy(out=XL[:, q4 * 4:(q4 + 1) * 4, :].rearrange("p a t -> p (a t)"), in_=pP[0:64, :]), reads=[pP], writes=[XL])
            for c_ in range(NCH):
                for c in range(8):
                    b.op("pe", lambda e: e.matmul(pP[0:64, :], lhsT=hTs[:, c, 1 + c_ * 64:1 + (c_ + 1) * 64], rhs=W1[:, c, 1024:1536], start=(c == 0), stop=False),
                         reads=[W1, hTs], writes=[pP])
                for c in range(8):
                    b.op("pe", lambda e: e.matmul(pP[0:64, :], lhsT=hTs[:, c, c_ * 64:(c_ + 1) * 64], rhs=W2[:, c, 1024:1536], start=False, stop=(c == 7)),
                         reads=[W2, hTs], writes=[pP])
                b.op("act", lambda e: e.copy(out=Vtm[:, c_, :], in_=pP[0:64, :]), reads=[pP], writes=[Vtm])
            R_ = XL[:, 0:8, :]
            K_ = XL[:, 8:16, :]
            b.op("act", lambda e: e.activation(out=XL[:, 16, :], in_=XL[:, 16, :], func=AF.Tanh), reads=[XL], writes=[XL])
            b.op("act", lambda e: e.activation(out=XL[:, 18:20, :], in_=XL[:, 18:20, :], func=AF.Sigmoid), reads=[XL], writes=[XL])
            for (ws_, src, bias_, dst) in [(w2s, 16, w0, "LW"), (a2s, 17, a0, "AS")]:
                for half in range(2):
                    for j in range(4):
                        h = half * 4 + j
                        b.op("pe", lambda e: e.matmul(pP[0:64, j * TG:(j + 1) * TG], lhsT=ws_[:, h * 64:(h + 1) * 64], rhs=XL[:, src, :], start=True, stop=True),
                             reads=[ws_, XL], writes=[pP])
                    for j in range(4):
                        h = half * 4 + j
                        b.op("act", lambda e: e.activation(out=T[dst][:, h, :], in_=pP[0:64, j * TG:(j + 1) * TG], func=AF.Sigmoid, bias=bias_[:, h:h + 1]),
                             reads=[pP, bias_], writes=[T[dst]])
            b.op("pool", lambda e: e.tensor_scalar_mul(out=f2(T["LW"]), in0=f2(T["LW"]), scalar1=-0.6065306597126334), reads=[T["LW"]], writes=[T["LW"]])
            tt("dve", T["KK"][:], K_, bc(k_k), ALU.mult, [XL, k_k], [T["KK"]])
            tt("pool", T["NR"][:], T["KK"][:], T["KK"][:], ALU.mult, [T["KK"]], [T["NR"]])
            for half in range(2):
                b.op("pe", lambda e: e.matmul(pP[0:64, :], lhsT=ones[:], rhs=T["NR"][:, half * 4:(half + 1) * 4, :].rearrange("p h t -> p (h t)"), start=True, stop=True),
                     reads=[ones, T["NR"]], writes=[pP])
                b.op("act", lambda e: e.activation(out=T["KKN"][:, half * 4:(half + 1) * 4, :].rearrange("p h t -> p (h t)"), in_=pP[0:64, :], func=AF.Sqrt),
                     reads=[pP], writes=[T["KKN"]])
            b.op("dve", lambda e: e.tensor_scalar_max(out=f2(T["KKN"]), in0=f2(T["KKN"]), scalar1=1e-12), reads=[T["KKN"]], writes=[T["KKN"]])
            b.op("dve", lambda e: e.reciprocal(out=f2(T["KKN"]), in_=f2(T["KKN"])), reads=[T["KKN"]], writes=[T["KKN"]])
            tt("dve", T["KKN"][:], T["KKN"][:], T["KK"][:], ALU.mult, [T["KKN"], T["KK"]], [T["KKN"]])
            tt("pool", T["BVc"][:], T["KKN"][:], T["AS"][:], ALU.mult, [T["KKN"], T["AS"]], [T["BVc"]])
            b.op("pool", lambda e: e.tensor_scalar_add(out=f2(T["T1"]), in0=f2(T["AS"]), scalar1=-1.0), reads=[T["AS"]], writes=[T["T1"]])
            tt("pool", T["T1"][:], T["T1"][:], bc(k_a), ALU.mult, [T["T1"], k_a], [T["T1"]])
            b.op("dve", lambda e: e.scalar_tensor_tensor(out=f2(T["KP"]), in0=f2(T["T1"]), scalar=1.0, in1=K_.rearrange("p h t -> p (h t)"), op0=ALU.add, op1=ALU.mult),
                 reads=[T["T1"], XL], writes=[T["KP"]])
            tt("pool", T["RK"][:], R_, T["KP"][:], ALU.mult, [XL, T["KP"]], [T["RK"]])
            tt("pool", T["RK"][:], T["RK"][:], bc(r_k), ALU.mult, [T["RK"], r_k], [T["RK"]])
            for c_ in range(NCH):
                for h in range(8):
                    b.op("pe", lambda e: e.matmul(pD[0:64, c_ * 8 + h:c_ * 8 + h + 1], lhsT=T["RK"][:, h, c_ * 64:(c_ + 1) * 64], rhs=ones[:, 0:1], start=True, stop=True),
                         reads=[T["RK"], ones], writes=[pD])
            b.op("act", lambda e: e.copy(out=BON[:], in_=pD[0:64, 0:NCH * 8]), reads=[pD], writes=[BON])
            b.op("dve", lambda e: e.tensor_tensor_scan(out=f2(T["L"]), data0=rstm[:], data1=f2(T["LW"]), initial=0.0, op0=ALU.mult, op1=ALU.add),
                 reads=[rstm, T["LW"]], writes=[T["L"]])
            b.op("act", lambda e: e.activation(out=f2(T["EP"]), in_=f2(T["L"]), func=AF.Exp), reads=[T["L"]], writes=[T["EP"]])
            b.op("act", lambda e: e.activation(out=f2(T["EM"]), in_=f2(T["L"]), func=AF.Exp, scale=-1.0), reads=[T["L"]], writes=[T["EM"]])
            tt("pool", T["L"][:], T["L"][:], T["LW"][:], ALU.subtract, [T["L"], T["LW"]], [T["L"]])
            b.op("act", lambda e: e.activation(out=f2(T["EX"]), in_=f2(T["L"]), func=AF.Exp), reads=[T["L"]], writes=[T["EX"]])
            ar0 = AR[:, :, :, 0, :].rearrange("p h c t -> p (h c) t")
            ar1 = AR[:, :, :, 1, :].rearrange("p h c t -> p (h c) t")
            b.op("dve", lambda e: e.scalar_tensor_tensor(out=ar0, in0=c16(T["KKN"]), scalar=-1.0, in1=c16(T["EX"]), op0=ALU.mult, op1=ALU.mult),
                 reads=[T["KKN"], T["EX"]], writes=[AR])
            tt("pool", ar1, R_.rearrange("p h (c t) -> p (h c) t", t=64), c16(T["EP"]), ALU.mult, [XL, T["EP"]], [AR])
            tt("dve", T["BT"][:], T["BVc"][:], T["EM"][:], ALU.mult, [T["BVc"], T["EM"]], [T["BT"]])
            tt("pool", T["KT"][:], T["KP"][:], T["EM"][:], ALU.mult, [T["KP"], T["EM"]], [T["KT"]])
            gC = c16(T["EP"])[:, :, 63:64].to_broadcast([64, 16, 64])
            tt("dve", c16(T["BG"]), c16(T["BT"]), gC, ALU.mult, [T["BT"], T["EP"]], [T["BG"]])
            tt("pool", c16(T["KG"]), c16(T["KT"]), gC, ALU.mult, [T["KT"], T["EP"]], [T["KG"]])
            for c_ in range(NCH):
                cs = slice(c_ * 64, (c_ + 1) * 64)
                cur = (gi * NCH + c_) % 2
                for hb_ in range(2):
                    heads = list(range(hb_ * 4, hb_ * 4 + 4))
                    for j, h in enumerate(heads):
                        b.op("pe", lambda e: e.transpose(out=pC[0:64, j * 128:j * 128 + 64], in_=T["BG"][:, h, cs], identity=idf[0:64, 0:64]), reads=[T["BG"], idf], writes=[pC])
                        b.op("pe", lambda e: e.transpose(out=pC[0:64, j * 128 + 64:(j + 1) * 128], in_=T["KG"][:, h, cs], identity=idf[0:64, 0:64]), reads=[T["KG"], idf], writes=[pC])
                    tm = nxt("tm", TM4)
                    b.op("act", lambda e: e.copy(out=tm[:].rearrange("p h a t -> p (h a t)"), in_=pC[0:64, 0:512]), reads=[pC], writes=[tm])
                    for j, h in enumerate(heads):
                        arc = AR[:, h, c_, :, :].rearrange("p a t -> p (a t)")
                        b.op("pe", lambda e: e.matmul(pA[0:64, j * 256:j * 256 + 128], lhsT=T["BT"][:, h, cs], rhs=arc, start=True, stop=True), reads=[T["BT"], AR], writes=[pA])
                        b.op("pe", lambda e: e.matmul(pA[0:64, j * 256 + 128:(j + 1) * 256], lhsT=T["KT"][:, h, cs], rhs=arc, start=True, stop=True), reads=[T["KT"], AR], writes=[pA])
                        b.op("pe", lambda e: e.matmul(pB[0:64, j * 64:(j + 1) * 64], lhsT=AR[:, h, c_, 0, :], rhs=T["BT"][:, h, cs], start=True, stop=True), reads=[T["BT"], AR], writes=[pB])
                    xm = nxt("xm", XM4)
                    tt("dve", xm[:].rearrange("p h (a m) t -> p (h a) m t", a=2), pA[0:64, :].rearrange("p (ha m t) -> p ha m t", m=2, t=64),
                       msk[:, None, 0:2, :].to_broadcast([64, 8, 2, 64]), ALU.mult, [pA, msk], [xm])
                    aa = nxt("aa", AA4)
                    b.op("pool", lambda e: e.tensor_copy(out=aa[:, :, 0, :], in_=xm[:, :, 0, :]), reads=[xm], writes=[aa])
                    tt("dve", aa[:, :, 1, :], pB[0:64, 0:256].rearrange("p (h t) -> p h t", t=64), msk[:, 2:3, :].to_broadcast([64, 4, 64]), ALU.mult, [pB, msk], [aa])
                    P_ = nxt("pp4", PP4)
                    tt("pool", P_[:], xm[:, :, 0, :], idf[0:64, None, 0:64].to_broadcast([64, 4, 64]), ALU.add, [xm, idf], [P_])
                    for step in range(5):
                        for j in range(4):
                            b.op("pe", lambda e: e.matmul(pD[0:64, j * 128:j * 128 + 64], lhsT=aa[:, j, 1, :], rhs=aa[:, j, 0, :], start=True, stop=True), reads=[aa], writes=[pD])
                            b.op("pe", lambda e: e.matmul(pD[0:64, j * 128 + 64:(j + 1) * 128], lhsT=aa[:, j, 0, :], rhs=aa[:, j, 1, :], start=True, stop=True), reads=[aa], writes=[pD])
                        aa2 = nxt("aa", AA4)
                        b.op("act", lambda e: e.copy(out=aa2[:].rearrange("p h a t -> p (h a t)"), in_=pD[0:64, :]), reads=[pD], writes=[aa2])
                        for j in range(4):
                            b.op("pe", lambda e: e.matmul(pB[0:64, 256 + j * 64:256 + (j + 1) * 64], lhsT=aa2[:, j, 1, :], rhs=P_[:, j, :], start=True, stop=True), reads=[aa2, P_], writes=[pB])
                        P2 = nxt("pp4", PP4)
                        tt("dve", P2[:], pB[0:64, 256:512].rearrange("p (h t) -> p h t", t=64), P_[:], ALU.add, [pB, P_], [P2])
                        aa, P_ = aa2, P2
                    for j, h in enumerate(heads):
                        b.op("pe", lambda e: e.matmul(pZ[0:64, j * 64:(j + 1) * 64], lhsT=xm[:, j, 2, :], rhs=Vtm[:, c_, h * 64:(h + 1) * 64], start=True, stop=False), reads=[xm, Vtm], writes=[pZ])
                        b.op("pe", lambda e: e.matmul(pZ[0:64, j * 64:(j + 1) * 64], lhsT=AR[:, h, c_, 0, :], rhs=Hst[:, cur, h, :], start=False, stop=True), reads=[AR, Hst], writes=[pZ])
                    b.op("act", lambda e: e.copy(out=Xs4[:].rearrange("p h t -> p (h t)"), in_=pZ[0:64, 0:256]), reads=[pZ], writes=[Xs4])
                    for j in range(4):
                        b.op("pe", lambda e: e.matmul(pZ[0:64, 256 + j * 64:256 + (j + 1) * 64], lhsT=P_[:, j, :], rhs=Xs4[:, j, :], start=True, stop=True), reads=[P_, Xs4], writes=[pZ])
                    b.op("act", lambda e: e.copy(out=Us4[:].rearrange("p h t -> p (h t)"), in_=pZ[0:64, 256:512]), reads=[pZ], writes=[Us4])
                    for j, h in enumerate(heads):
                        o = slice(j * 64, (j + 1) * 64)
                        vh = Vtm[:, c_, h * 64:(h + 1) * 64]
                        b.op("pe", lambda e: e.matmul(pZ[0:64, o], lhsT=AR[:, h, c_, 1, :], rhs=Hst[:, cur, h, :], start=True, stop=False), reads=[AR, Hst], writes=[pZ])
                        b.op("pe", lambda e: e.matmul(pZ[0:64, o], lhsT=xm[:, j, 1, :], rhs=Us4[:, j, :], start=False, stop=False), reads=[xm, Us4], writes=[pZ])
                        b.op("pe", lambda e: e.matmul(pZ[0:64, o], lhsT=xm[:, j, 3, :], rhs=vh, start=False, stop=True), reads=[xm, Vtm], writes=[pZ])
                    for j, h in enumerate(heads):
                        o = slice(256 + j * 64, 256 + (j + 1) * 64)
                        vh = Vtm[:, c_, h * 64:(h + 1) * 64]
                        b.op("pe", lambda e: e.matmul(pZ[0:64, o], lhsT=tm[:, j, 0, :], rhs=Us4[:, j, :], start=True, stop=False), reads=[tm, Us4], writes=[pZ])
                        b.op("pe", lambda e: e.matmul(pZ[0:64, o], lhsT=tm[:, j, 1, :], rhs=vh, start=False, stop=True), reads=[tm, Vtm], writes=[pZ])
                    b.op("act", lambda e: e.copy(out=Ytm[:, c_, hb_ * 4:(hb_ + 1) * 4, :].rearrange("p h t -> p (h t)"), in_=pZ[0:64, 0:256]), reads=[pZ], writes=[Ytm])
                    gH = T["EP"][:, hb_ * 4:(hb_ + 1) * 4, c_ * 64 + 63:c_ * 64 + 64].to_broadcast([64, 4, 64])
                    tt("pool", Ht4[:], Hst[:, cur, hb_ * 4:(hb_ + 1) * 4, :], gH, ALU.mult, [Hst, T["EP"]], [Ht4])
                    tt("dve", Hst[:, 1 - cur, hb_ * 4:(hb_ + 1) * 4, :], pZ[0:64, 256:512].rearrange("p (h t) -> p h t", t=64), Ht4[:], ALU.add, [pZ, Ht4], [Hst])
            Y3 = Ytm[:].rearrange("p c h i -> p (c h) i")
            S3 = sqv[:].rearrange("p c h i -> p (c h) i")
            b.op("dve", lambda e: e.tensor_reduce(out=st1[:], in_=Y3, axis=AX.X, op=ALU.add), reads=[Ytm], writes=[st1])
            b.op("pool", lambda e: e.tensor_scalar_mul(out=st1[:], in0=st1[:], scalar1=1.0 / 64), reads=[st1], writes=[st1])
            tt("dve", Y3, Y3, st1[:].unsqueeze(2).to_broadcast([64, NCH * 8, 64]), ALU.subtract, [Ytm, st1], [Ytm])
            tt("pool", S3, Y3, Y3, ALU.mult, [Ytm], [sqv])
            b.op("dve", lambda e: e.tensor_reduce(out=st2[:], in_=S3, axis=AX.X, op=ALU.add), reads=[sqv], writes=[st2])
            b.op("act", lambda e: e.activation(out=st2[:], in_=st2[:], func=AF.Sqrt, scale=1.0 / 64, bias=64e-5), reads=[st2], writes=[st2])
            b.op("dve", lambda e: e.reciprocal(out=st2[:], in_=st2[:]), reads=[st2], writes=[st2])
            tt("dve", Y3, Y3, st2[:].unsqueeze(2).to_broadcast([64, NCH * 8, 64]), ALU.mult, [Ytm, st2], [Ytm])
            lg = lng[:].rearrange("p (h i) -> p h i", i=64)[:, None, :, :].to_broadcast([64, NCH, 8, 64])
            lb = lnb[:].rearrange("p (h i) -> p h i", i=64)[:, None, :, :].to_broadcast([64, NCH, 8, 64])
            tt("pool", Ytm[:], Ytm[:], lg, ALU.mult, [Ytm, lng], [Ytm])
            tt("dve", Ytm[:], Ytm[:], lb, ALU.add, [Ytm, lnb], [Ytm])
            V3 = Vtm[:].rearrange("p c (h i) -> p (c h) i", i=64)
            tt("pool", S3, V3, BON[:].unsqueeze(2).to_broadcast([64, NCH * 8, 64]), ALU.mult, [Vtm, BON], [sqv])
            tt("dve", Y3, Y3, S3, ALU.add, [Ytm, sqv], [Ytm])
            for c_ in range(NCH):
                for two in range(2):
                    b.op("pe", lambda e: e.matmul(pP[0:64, :], lhsT=XL[:, 18 + two, c_ * 64:(c_ + 1) * 64], rhs=g2s[:, two, :], start=(two == 0), stop=(two == 1)),
                         reads=[XL, g2s], writes=[pP])
                tt("dve", OBb[:, c_, :], Ytm[:, c_, :, :].rearrange("p h i -> p (h i)"), pP[0:64, :], ALU.mult, [Ytm, pP], [OBb])
                for k4 in range(4):
                    b.op("pe", lambda e: e.transpose(out=pt[:, k4, c_ * 64:(c_ + 1) * 64], in_=OBb[:, c_, k4 * 128:(k4 + 1) * 128], identity=self.ident[0:64, 0:64]),
                         reads=[OBb, self.ident], writes=[pt])
            ot = obT[gi % 2]
            b.op("act", lambda e: e.copy(out=ot[:], in_=pt[:, 0:4, :]), reads=[pt], writes=[ot])
            b.dma("pool", self.obT_d[:, :, q0:q0 + TG].rearrange("c p t -> p c t"), ot[:], reads=[ot], writes=[self.obT_d])
        if "rwkv" in self.debug:
            d = self.dbg_out("obT", [4, 128, S], BF16)
            b.dma("pool", d, self.obT_d[:], reads=[self.obT_d])


Prog.phase_rwkv2 = _phase_rwkv2


def _phase_ffn2(self):
    b = self.b
    I = self.inp
    TG = 256
    NFT = 44
    with b.scope():
        gf = self.load_gain("gf", I["ffn_norm_g"][0])
        stage = [b.sb(f"fst{i}", [128, 1024], F32) for i in range(2)]
        wu = b.sb("wu", [128, 8, 2 * DFF], BF16)
        for n in range(8):
            for c in range(8):
                st = stage[c % 2]
                b.dma("sp", st[:, 0:704], I["w_up"][0][c * 128:(c + 1) * 128, n * 704:(n + 1) * 704], writes=[st])
                b.op("act", lambda e: e.activation(out=wu[:, c, n * 704:(n + 1) * 704], in_=st[:, 0:704], func=AF.Copy, scale=gf[:, c:c + 1]),
                     reads=[st, gf], writes=[wu])
        wd = b.sb("wd", [128, 22, D], BF16)
        self.load_weight(wd, I["w_down"][0], D, kch=22, stage=stage, eng="dve")
        cw = b.sb("cw", [128, 3, NFT], F32)
        for j in range(3):
            b.dma("sp", cw[:, j, :], I["conv_w"][0][j].rearrange("(c p) -> p c", p=128), writes=[cw], allow_slow_non_contiguous=True)
        cbias = self.load_gain("cbias", I["conv_b"][0], kch=NFT)
        xt = [b.sb(f"fxt{i}", [128, D], F32) for i in range(2)]
        junk = b.sb("fjunk", [128, D], BF16)
        ss = [b.sb(f"fss{i}", [128, 1], F32) for i in range(2)]
        hb = [b.sb(f"fhb{i}", [128, D], BF16) for i in range(2)]
        hTg = b.sb("fhTg", [128, 8, TG + 2], BF16)
        b.op("pool", lambda e: e.memset(hTg[:], 0.0), writes=[hTg])
        cv = [b.sb(f"cv{i}", [128, TG], F32) for i in range(3)]
        sgl = [b.sb(f"sgl{i}", [128, TG], BF16) for i in range(2)]
        actT = b.sb("actT", [128, 22, TG], BF16)
        val = b.sb("fval", [128, 22, TG], BF16)
        pt = b.ps("fpt", [128, 8, 128], BF16)
        pu = [b.ps(f"fpu{i}", [128, 512], F32) for i in range(4)]
        pd = [b.ps(f"fpd{i}", [128, 512], F32) for i in range(2)]
        ng = getattr(self, "nt_limit", NT) * 128 // TG
        for gi in range(ng):
            b.op("pool", lambda e: e.tensor_copy(out=hTg[:, :, 0:2], in_=hTg[:, :, TG:TG + 2]), reads=[hTg], writes=[hTg])
            for s_ in range(TG // 128):
                t = gi * (TG // 128) + s_
                self.make_hT(self.x1_d, t, xt[s_], junk, ss[s_], hb[s_], pt, hTg, self.ident, hT_ap=hTg[:, :, 2 + s_ * 128:2 + (s_ + 1) * 128])
            for ft in range(NFT):
                p = pu[ft % 4]
                c_ = cv[ft % 3]
                for c in range(8):
                    b.op("pe", lambda e: e.matmul(p[:, 0:TG + 2], lhsT=wu[:, c, ft * 128:(ft + 1) * 128], rhs=hTg[:, c, :], start=(c == 0), stop=(c == 7)),
                         reads=[wu, hTg], writes=[p])
                b.op("act", lambda e: e.activation(out=c_[:], in_=p[:, 0:TG], func=AF.Identity, scale=cw[:, 0, ft:ft + 1], bias=cbias[:, ft:ft + 1]),
                     reads=[p, cw, cbias], writes=[c_])
                b.op("dve", lambda e: e.scalar_tensor_tensor(out=c_[:], in0=p[:, 1:TG + 1], scalar=cw[:, 1, ft:ft + 1], in1=c_[:], op0=ALU.mult, op1=ALU.add),
                     reads=[p, cw, c_], writes=[c_])
                if ft < 22:
                    b.op("dve", lambda e: e.scalar_tensor_tensor(out=val[:, ft, :], in0=p[:, 2:TG + 2], scalar=cw[:, 2, ft:ft + 1], in1=c_[:], op0=ALU.mult, op1=ALU.add),
                         reads=[p, cw, c_], writes=[val])
                else:
                    sg_ = sgl[ft % 2]
                    b.op("dve", lambda e: e.scalar_tensor_tensor(out=c_[:], in0=p[:, 2:TG + 2], scalar=cw[:, 2, ft:ft + 1], in1=c_[:], op0=ALU.mult, op1=ALU.add),
                         reads=[p, cw, c_], writes=[c_])
                    b.op("act", lambda e: e.activation(out=sg_[:], in_=c_[:], func=AF.Silu), reads=[c_], writes=[sg_])
                    b.op("pool", lambda e: e.tensor_tensor(out=actT[:, ft - 22, :], in0=sg_[:], in1=val[:, ft - 22, :], op=ALU.mult),
                         reads=[sg_, val], writes=[actT])
            for s_ in range(TG // 128):
                t = gi * (TG // 128) + s_
                for n in range(2):
                    for f in range(22):
                        b.op("pe", lambda e: e.matmul(pd[n][:, :], lhsT=actT[:, f, s_ * 128:(s_ + 1) * 128], rhs=wd[:, f, n * 512:(n + 1) * 512], start=(f == 0), stop=(f == 21)),
                             reads=[actT, wd], writes=[pd[n]])
                    b.op("dve", lambda e: e.tensor_tensor(out=xt[s_][:, n * 512:(n + 1) * 512], in0=pd[n][:, :], in1=xt[s_][:, n * 512:(n + 1) * 512], op=ALU.add),
                         reads=[pd[n], xt[s_]], writes=[xt[s_]])
                b.dma("pool", self.out[t * 128:(t + 1) * 128, :], xt[s_][:], reads=[xt[s_]])


Prog.phase_ffn2 = _phase_ffn2
```

```python
import contextlib
import numpy as np
import ml_dtypes
import concourse.bass as bass
import concourse.mybir as mybir
from concourse.bass_utils import run_bass_kernel_spmd

F32 = mybir.dt.float32
BF16 = mybir.dt.bfloat16
AF = mybir.ActivationFunctionType
ALU = mybir.AluOpType
AX = mybir.AxisListType

S = 4096
D = 1024
NT = S // 128
IN_WIDTH = 5144
RW0 = 1304
GA0 = 3096
GB0 = 4120
DFF = 2816
RMS_EPS = 1e-6


class Buf:
    def __init__(self, t, name):
        self.t = t
        self.name = name
        self.w = None
        self.r = {}
        self.psum = False

    def __getitem__(self, idx):
        return self.t[idx]


class Builder:
    SEM_ROLL = 30000

    def __init__(self, nc):
        self.nc = nc
        self.stack = contextlib.ExitStack()
        self.root = self.stack
        self.eng = {"pe": nc.tensor, "act": nc.scalar, "dve": nc.vector,
                    "pool": nc.gpsimd, "sp": nc.sync}
        self.sem = {}
        self.cnt = {}
        self.seen = {e: {} for e in self.eng}
        self.nsem = 0
        self.lanes = {}
        self.lane_rr = {}
        self.last_tok = {}
        for e in self.eng:
            self._roll(e)

    def newsem(self, name):
        self.nsem += 1
        return self.root.enter_context(self.nc.semaphore(f"{name}_{self.nsem}"))

    def sb(self, name, shape, dt=F32):
        self.nsem += 1
        name = f"sb{self.nsem}_{name}"
        return Buf(self.stack.enter_context(self.nc.sbuf_tensor(name, list(shape), dt)), name)

    def ps(self, name, shape, dt=F32):
        self.nsem += 1
        name = f"ps{self.nsem}_{name}"
        bf = Buf(self.stack.enter_context(self.nc.psum_tensor(name, list(shape), dt)), name)
        bf.psum = True
        return bf

    def dram(self, name, shape, dt=F32, kind="Internal"):
        return Buf(self.nc.dram_tensor(name, list(shape), dt, kind=kind), name)

    def _roll(self, e):
        self.sem[e] = self.newsem("s" + e)
        self.cnt[e] = 0

    def _wait(self, e, tok):
        sem, val = tok
        k = id(sem)
        if self.seen[e].get(k, 0) < val:
            self.eng[e].wait_ge(sem, val)
            self.seen[e][k] = val

    def _deps(self, e, reads, writes):
        for b in reads:
            if b.w is not None:
                we, tok = b.w
                self._wait(e, tok)
            if b.psum:
                for re_, tok in b.r.items():
                    if re_ != e:
                        self._wait(e, tok)
        for b in writes:
            if b.w is not None:
                we, tok = b.w
                if we != e:
                    self._wait(e, tok)
            for re_, tok in b.r.items():
                if re_ != e:
                    self._wait(e, tok)

    def op(self, e, fn, reads=(), writes=()):
        if self.cnt[e] >= self.SEM_ROLL:
            self._roll(e)
        self._deps(e, reads, writes)
        ins = fn(self.eng[e])
        self.cnt[e] += 1
        tok = (self.sem[e], self.cnt[e])
        ins.then_inc(self.sem[e], 1)
        self.last_tok[e] = tok
        for b in reads:
            b.r[e] = tok
        for b in writes:
            b.w = (e, tok)
            b.r = {}
        return tok

    def dma(self, q, out, in_, reads=(), writes=(), nlanes=6, **kw):
        if q not in self.lanes:
            self.lanes[q] = [[self.newsem("l" + q), 0] for _ in range(nlanes)]
            self.lane_rr[q] = 0
        li = self.lane_rr[q]
        self.lane_rr[q] = (li + 1) % len(self.lanes[q])
        lane = self.lanes[q][li]
        if lane[1] >= 1800:
            self._wait(q, (lane[0], 16 * lane[1]))
            lane[0] = self.newsem("l" + q)
            lane[1] = 0
        if lane[1] > 0:
            self._wait(q, (lane[0], 16 * lane[1]))
        self._deps_dma(q, reads, writes)
        ins = self.eng[q].dma_start(out=out, in_=in_, **kw)
        lane[1] += 1
        tok = (lane[0], 16 * lane[1])
        ins.then_inc(lane[0], 16)
        key = "dma_" + q + str(li)
        for b in reads:
            b.r[key] = tok
        for b in writes:
            b.w = (key, tok)
            b.r = {}
        return tok

    def _deps_dma(self, q, reads, writes):
        for b in reads:
            if b.w is not None:
                self._wait(q, b.w[1])
        for b in writes:
            if b.w is not None:
                self._wait(q, b.w[1])
            for re_, tok in b.r.items():
                self._wait(q, tok)

    def barrier(self):
        toks = list(self.last_tok.values())
        for q, lanes in self.lanes.items():
            for lane in lanes:
                if lane[1] > 0:
                    toks.append((lane[0], 16 * lane[1]))
        for e in self.eng:
            for tok in toks:
                self._wait(e, tok)

    def wait_all_on(self, e):
        toks = list(self.last_tok.values())
        for q, lanes in self.lanes.items():
            for lane in lanes:
                if lane[1] > 0:
                    toks.append((lane[0], 16 * lane[1]))
        for tok in toks:
            self._wait(e, tok)

    @contextlib.contextmanager
    def scope(self):
        old = self.stack
        self.stack = contextlib.ExitStack()
        try:
            yield
            self.barrier()
        finally:
            self.stack.close()
            self.stack = old

    def close(self):
        self.stack.close()


NEG = -30000.0


def _bucket(dist):
    n = np.maximum(dist, 0)
    ratio = np.log(np.maximum(n, 1).astype(np.float32) / np.float32(16.0)) / np.float32(np.log(8.0))
    large = np.minimum(16 + (ratio * 16).astype(np.int32), 31)
    return np.where(n < 16, n, large)


def host_consts(rel_bias):
    rel = np.asarray(rel_bias, np.float32)
    c = {}
    c["ident"] = np.eye(128, dtype=np.float32).astype(ml_dtypes.bfloat16)
    c["identf"] = np.eye(128, dtype=np.float32)
    kp = np.arange(128)[:, None]
    cc = np.arange(640)[None, :]
    dist = cc - kp
    bt = rel[_bucket(dist)]
    tw = np.where(((dist >= 0) & (dist < 512))[..., None], bt, np.float32(NEG))
    ts = np.where((dist >= 0)[..., None], bt, np.float32(NEG))
    c["tw"] = np.ascontiguousarray(tw.transpose(0, 2, 1)).astype(np.float32)
    c["ts"] = np.ascontiguousarray(ts.transpose(0, 2, 1)).astype(np.float32)
    cidx = np.arange(256)[:, None]
    qidx = np.arange(S)[None, :]
    dc = qidx - 16 * cidx - 31
    bcg = rel[_bucket(dc)]
    ok = (dc >= 0) & (cidx < 255)
    bc = np.where(ok[..., None], bcg, np.float32(NEG))
    c["biasc"] = np.ascontiguousarray(bc.transpose(2, 0, 1)).reshape(8, 2, 128, S).astype(np.float32)
    A = np.zeros((256, 64), np.float32)
    Wt = (1, 2, 2, 2, 1)
    for ci in range(255):
        for j in range(64):
            o = ci + 1 - 4 * j
            if 0 <= o <= 4:
                A[ci, j] = Wt[o]
    c["amat"] = A.reshape(2, 128, 64)
    E = np.zeros((64, S), np.float32)
    E[np.arange(S) // 64, np.arange(S)] = 1.0
    c["emat"] = E.astype(ml_dtypes.bfloat16)
    qp = np.arange(128)[:, None, None]
    qt = np.arange(32)[None, :, None]
    j = np.arange(64)[None, None, :]
    cur = (128 * qt + qp) // 64
    cand = (j >= 1) & (j <= cur - 2)
    c["candneg"] = np.where(cand, 0.0, -1e9).astype(np.float32)
    c["fz"] = ((j == 0) | (j == cur) | (j == cur - 1)).astype(np.float32)
    tri = np.triu(np.ones((64, 64), np.float32))
    c["rwmask"] = np.ascontiguousarray(np.stack([np.triu(np.ones((64, 64), np.float32), 1), tri, np.tril(np.ones((64, 64), np.float32), -1)], axis=1))
    rr = np.ones((64, 1024), np.float32)
    rr[:, ::64] = 0.0
    c["rwreset"] = rr
    c["b31"] = np.ascontiguousarray(np.broadcast_to(rel[31][None, :], (128, 8))).astype(np.float32)
    return c


CONST_SPECS = {
    "ident": ([128, 128], BF16), "identf": ([128, 128], F32),
    "tw": ([128, 8, 640], F32), "ts": ([128, 8, 640], F32),
    "biasc": ([8, 2, 128, S], F32), "amat": ([2, 128, 64], F32),
    "emat": ([64, S], BF16), "candneg": ([128, 32, 64], F32), "fz": ([128, 32, 64], F32),
    "b31": ([128, 8], F32), "rwmask": ([64, 3, 64], F32), "rwreset": ([64, 1024], F32),
}

W_SPECS = {
    "x": [S, D], "attn_norm_g": [1, D], "w_in": [1, D, IN_WIDTH], "q_norm_g": [1, 64], "k_norm_g": [1, 64],
    "cmp_pe_k": [1, 32, 64], "cmp_w1_k": [1, 2048, 256], "cmp_w2_k": [1, 256, 64],
    "cmp_pe_v": [1, 32, 64], "cmp_w1_v": [1, 2048, 256], "cmp_w2_v": [1, 256, 64],
    "rwkv_mu": [1, 1792], "rwkv_w0": [1, 512], "rwkv_w2": [1, 64, 512], "rwkv_a0": [1, 512],
    "rwkv_a2": [1, 64, 512], "rwkv_g2": [1, 128, 512], "rwkv_k_k": [1, 512], "rwkv_k_a": [1, 512],
    "rwkv_r_k": [1, 8, 64], "rwkv_ln_g": [1, 512], "rwkv_ln_b": [1, 512],
    "w_proj_a": [1, 512, D], "w_proj_b": [1, 512, D], "w_out": [1, D, D], "ffn_norm_g": [1, D],
    "w_up": [1, D, 2 * DFF], "conv_w": [1, 3, 2 * DFF], "conv_b": [1, 2 * DFF], "w_down": [1, DFF, D],
}


class Prog:
    def __init__(self, debug=()):
        self.debug = set(debug)
        nc = bass.Bass("TRN2", target_bir_lowering=False)
        self.nc = nc
        self.inp = {}
        for k, shp in W_SPECS.items():
            self.inp[k] = nc.dram_tensor(k, list(shp), F32, kind="ExternalInput").ap()
        for k, (shp, dt) in CONST_SPECS.items():
            self.inp[k] = nc.dram_tensor(k, list(shp), dt, kind="ExternalInput").ap()
        self.out = nc.dram_tensor("out", [S, D], F32, kind="ExternalOutput").ap()
        self.dbg = {}
        self.b = Builder(nc)

    def dbg_out(self, name, shape, dt=F32):
        t = self.nc.dram_tensor("dbg_" + name, list(shape), dt, kind="ExternalOutput").ap()
        self.dbg[name] = t
        return t

    def load_weight(self, dst, src, ncols, gvec=None, kch=8, stage=None, eng="act"):
        b = self.b
        for c in range(kch):
            st = stage[c % len(stage)]
            b.dma("sp", st[:, :ncols], src[c * 128:(c + 1) * 128, :], writes=[st])
            if gvec is not None:
                b.op(eng, lambda e: e.activation(out=dst[:, c, :], in_=st[:, :ncols], func=AF.Copy, scale=gvec[:, c:c + 1])
                     if eng == "act" else e.tensor_scalar_mul(out=dst[:, c, :], in0=st[:, :ncols], scalar1=gvec[:, c:c + 1]),
                     reads=[st, gvec], writes=[dst])
            else:
                b.op(eng, lambda e: e.copy(out=dst[:, c, :], in_=st[:, :ncols]) if eng == "act"
                     else e.tensor_copy(out=dst[:, c, :], in_=st[:, :ncols]), reads=[st], writes=[dst])

    def load_gain(self, name, src_vec, kch=8):
        b = self.b
        g = b.sb(name, [128, kch], F32)
        b.dma("sp", g[:], src_vec.rearrange("(c p) -> p c", p=128), writes=[g], allow_slow_non_contiguous=True)
        return g

    def bcast_row(self, name, src_row, n):
        b = self.b
        t = b.sb(name, [128, n], F32)
        b.dma("sp", t[:], src_row.partition_broadcast(128), writes=[t])
        return t

    def make_hT(self, x_ap, t, xt, junk, ss, hb, pt, hT, ident, hT_ap=None):
        b = self.b
        b.dma("sp", xt[:], x_ap[t * 128:(t + 1) * 128, :], writes=[xt])
        b.op("act", lambda e: e.activation(out=junk[:], in_=xt[:], func=AF.Square, accum_out=ss[:]), reads=[xt], writes=[junk, ss])
        b.op("act", lambda e: e.activation(out=ss[:], in_=ss[:], func=AF.Sqrt, scale=1.0 / D, bias=RMS_EPS), reads=[ss], writes=[ss])
        b.op("dve", lambda e: e.reciprocal(out=ss[:], in_=ss[:]), reads=[ss], writes=[ss])
        b.op("dve", lambda e: e.tensor_scalar_mul(out=hb[:], in0=xt[:], scalar1=ss[:]), reads=[xt, ss], writes=[hb])
        for c in range(8):
            b.op("pe", lambda e: e.transpose(out=pt[:, c, :], in_=hb[:, c * 128:(c + 1) * 128], identity=ident[:]),
                 reads=[hb, ident], writes=[pt])
        b.op("act", lambda e: e.copy(out=(hT[:] if hT_ap is None else hT_ap), in_=pt[:]), reads=[pt], writes=[hT])

    def alloc_root(self):
        b = self.b
        I = self.inp
        self.ident = b.sb("ident", [128, 128], BF16)
        b.dma("sp", self.ident[:], I["ident"], writes=[self.ident])
        self.identf = b.sb("identf", [128, 128], F32)
        b.dma("sp", self.identf[:], I["identf"], writes=[self.identf])

    def alloc_persistent(self):
        b = self.b
        I = self.inp
        if not hasattr(self, "ident"):
            self.alloc_root()
        self.ksE = b.sb("ksE", [128, 2, S], BF16)
        self.kwT = b.sb("kwT", [64, 2, S], BF16)
        self.vaug_s = b.sb("vaug_s", [128, NT, 2, 65], BF16)
        self.vaug_w = b.sb("vaug_w", [128, NT, 2, 65], BF16)
        self.gts = b.sb("gts", [128, NT, 24], F32)
        self.kcT = b.sb("kcT", [64, 2, 256], BF16)
        self.vcA = b.sb("vcA", [128, 2, 2, 129], F32)
        self.qT_d = b.dram("qT_d", [8, 64, S], BF16)
        self.oaT_d = b.dram("oaT_d", [4, 128, S], BF16)
        self.obT_d = b.dram("obT_d", [4, 128, S], BF16)
        for g in range(2):
            b.dma("sp", self.ksE[64:128, g, :], I["emat"], writes=[self.ksE])
        b.op("pool", lambda e: e.memset(self.vaug_s[:, :, :, 64:65], 1.0), writes=[self.vaug_s])
        b.op("pool", lambda e: e.memset(self.vaug_w[:, :, :, 64:65], 1.0), writes=[self.vaug_w])
        b.op("pool", lambda e: e.memset(self.vcA[:, :, :, 64:65], 1.0), writes=[self.vcA])
        for g in range(2):
            for ct in range(2):
                b.dma("sp", self.vcA[:, g, ct, 65:129], I["amat"][ct], writes=[self.vcA])

    def phase_nsa_proj(self):
        b = self.b
        I = self.inp
        with b.scope():
            gat = self.load_gain("gat", I["attn_norm_g"][0])
            wn = b.sb("wn", [128, 8, RW0], BF16)
            stage = [b.sb(f"wst{i}", [128, RW0], F32) for i in range(2)]
            self.load_weight(wn, I["w_in"][0][:, 0:RW0], RW0, gvec=gat, stage=stage)
            gq = self.bcast_row("gq", I["q_norm_g"][0], 64)
            gk = self.bcast_row("gk", I["k_norm_g"][0], 64)
            gq_rep = b.sb("gq_rep", [128, 8, 64], F32)
            gk_rep = b.sb("gk_rep", [128, 2, 64], F32)
            b.op("act", lambda e: e.activation(out=gq_rep[:], in_=gq[:, None, :].to_broadcast([128, 8, 64]), func=AF.Copy, scale=0.125),
                 reads=[gq], writes=[gq_rep])
            b.op("act", lambda e: e.activation(out=gk_rep[:], in_=gk[:, None, :].to_broadcast([128, 2, 64]), func=AF.Copy, scale=1.0),
                 reads=[gk], writes=[gk_rep])
            if getattr(self, 'stop_at', 99) <= 0:
                return
            kcdup = b.sb("kcdup", [128, 2, S + 1], BF16)
            vcdup = b.sb("vcdup", [128, 2, S + 1], BF16)
            xt = [b.sb(f"xt{i}", [128, D], F32) for i in range(2)]
            junk = b.sb("junk", [128, D], BF16)
            ss = [b.sb(f"ss{i}", [128, 1], F32) for i in range(2)]
            hb = [b.sb(f"hb{i}", [128, D], BF16) for i in range(2)]
            hT = [b.sb(f"hT{i}", [128, 8, 128], BF16) for i in range(2)]
            sq = b.sb("sq", [128, 12, 64], F32)
            ssq = b.sb("ssq", [128, 12], F32)
            tmpq = b.sb("tmpq", [128, 8, 64], F32)
            tmpk = b.sb("tmpk", [128, 4, 64], F32)
            qb = b.sb("qb", [128, 512], BF16)
            kb = b.sb("kb", [128, 4, 64], BF16)
            cb = b.sb("cb", [128, 4, 2, 64], BF16)
            qst = [b.sb(f"qst{i}", [64, 8, 128], BF16) for i in range(2)]
            pt = b.ps("pt", [128, 8, 128], BF16)
            pm = [b.ps(f"pm{i}", [128, 512], F32) for i in range(3)]
            ptq = b.ps("ptq", [128, 8, 128], BF16)
            ptk = b.ps("ptk", [128, 8, 128], BF16)
            colgroups = [(0, 512), (512, 1024), (1024, RW0)]
            for t in range(getattr(self, 'nt_limit', NT)):
                i = t % 2
                self.make_hT(I["x"], t, xt[i], junk, ss[i], hb[i], pt, hT[i], self.ident)
                for n, (c0, c1) in enumerate(colgroups):
                    for c in range(8):
                        b.op("pe", lambda e: e.matmul(pm[n][:, :c1 - c0], lhsT=hT[i][:, c, :], rhs=wn[:, c, c0:c1],
                                                      start=(c == 0), stop=(c == 7)), reads=[hT[i], wn], writes=[pm[n]])
                if getattr(self, 'stop_at', 99) <= 1:
                    continue
                b.op("act", lambda e: e.activation(out=sq[:, 0:8, :], in_=pm[0][:, 0:512].rearrange("p (h d) -> p h d", d=64), func=AF.Square),
                     reads=[pm[0]], writes=[sq])
                b.op("act", lambda e: e.activation(out=sq[:, 8:10, :], in_=pm[1][:, 256:384].rearrange("p (h d) -> p h d", d=64), func=AF.Square),
                     reads=[pm[1]], writes=[sq])
                b.op("act", lambda e: e.activation(out=sq[:, 10:12, :], in_=pm[2][:, 0:128].rearrange("p (h d) -> p h d", d=64), func=AF.Square),
                     reads=[pm[2]], writes=[sq])
                b.op("dve", lambda e: e.tensor_reduce(out=ssq[:], in_=sq[:], axis=AX.X, op=ALU.add), reads=[sq], writes=[ssq])
                b.op("act", lambda e: e.activation(out=ssq[:], in_=ssq[:], func=AF.Sqrt, scale=1.0 / 64, bias=RMS_EPS), reads=[ssq], writes=[ssq])
                b.op("dve", lambda e: e.reciprocal(out=ssq[:], in_=ssq[:]), reads=[ssq], writes=[ssq])
                if getattr(self, 'stop_at', 99) <= 2:
                    continue
                b.op("dve", lambda e: e.tensor_tensor(out=tmpq[:], in0=pm[0][:, 0:512].rearrange("p (h d) -> p h d", d=64),
                                                      in1=ssq[:, 0:8].unsqueeze(2).to_broadcast([128, 8, 64]), op=ALU.mult),
                     reads=[pm[0], ssq], writes=[tmpq])
                b.op("pool", lambda e: e.tensor_tensor(out=qb[:].rearrange("p (h d) -> p h d", d=64), in0=tmpq[:], in1=gq_rep[:], op=ALU.mult),
                     reads=[tmpq, gq_rep], writes=[qb])
                for h in range(8):
                    b.op("pe", lambda e: e.transpose(out=ptq[0:64, h, :], in_=qb[:, h * 64:(h + 1) * 64], identity=self.ident[:]),
                         reads=[qb, self.ident], writes=[ptq])
                b.op("act", lambda e: e.copy(out=qst[i][:], in_=ptq[0:64, :, :]), reads=[ptq], writes=[qst[i]])
                b.dma("pool", self.qT_d[:, :, t * 128:(t + 1) * 128].rearrange("h d t -> d h t"), qst[i][:], reads=[qst[i]], writes=[self.qT_d])
                if getattr(self, 'stop_at', 99) <= 3:
                    continue
                b.op("dve", lambda e: e.tensor_tensor(out=tmpk[:, 0:2, :], in0=pm[1][:, 256:384].rearrange("p (h d) -> p h d", d=64),
                                                      in1=ssq[:, 8:10].unsqueeze(2).to_broadcast([128, 2, 64]), op=ALU.mult),
                     reads=[pm[1], ssq], writes=[tmpk])
                b.op("dve", lambda e: e.tensor_tensor(out=tmpk[:, 2:4, :], in0=pm[2][:, 0:128].rearrange("p (h d) -> p h d", d=64),
                                                      in1=ssq[:, 10:12].unsqueeze(2).to_broadcast([128, 2, 64]), op=ALU.mult),
                     reads=[pm[2], ssq], writes=[tmpk])
                b.op("pool", lambda e: e.tensor_tensor(out=kb[:].rearrange("p (a g) d -> p a g d", a=2), in0=tmpk[:].rearrange("p (a g) d -> p a g d", a=2),
                                                       in1=gk_rep[:, None, :, :].to_broadcast([128, 2, 2, 64]), op=ALU.mult),
                     reads=[tmpk, gk_rep], writes=[kb])
                for j in range(4):
                    b.op("pe", lambda e: e.transpose(out=ptk[0:64, j, :], in_=kb[:, j, :], identity=self.ident[:]),
                         reads=[kb, self.ident], writes=[ptk])
                if getattr(self, 'stop_at', 99) <= 4:
                    continue
                for du in range(2):
                    b.op("act", lambda e: e.copy(out=cb[:, :, du, :], in_=pm[1][:, 0:256].rearrange("p (a d) -> p a d", d=64)),
                         reads=[pm[1]], writes=[cb])
                for j in range(4):
                    b.op("pe", lambda e: e.transpose(out=ptk[:, 4 + j, :], in_=cb[:, j, :, :].rearrange("p a d -> p (a d)"), identity=self.ident[:]),
                         reads=[cb, self.ident], writes=[ptk])
                c0 = t * 128
                b.op("dve", lambda e: e.tensor_copy(out=self.ksE[0:64, :, c0:c0 + 128], in_=ptk[0:64, 0:2, :]), reads=[ptk], writes=[self.ksE])
                b.op("dve", lambda e: e.tensor_copy(out=self.kwT[0:64, :, c0:c0 + 128], in_=ptk[0:64, 2:4, :]), reads=[ptk], writes=[self.kwT])
                b.op("act", lambda e: e.copy(out=kcdup[0:64, :, 1 + c0:1 + c0 + 128], in_=ptk[0:64, 4:6, :]), reads=[ptk], writes=[kcdup])
                b.op("act", lambda e: e.copy(out=kcdup[64:128, :, c0:c0 + 128], in_=ptk[64:128, 4:6, :]), reads=[ptk], writes=[kcdup])
                b.op("dve", lambda e: e.tensor_copy(out=vcdup[0:64, :, 1 + c0:1 + c0 + 128], in_=ptk[0:64, 6:8, :]), reads=[ptk], writes=[vcdup])
                b.op("dve", lambda e: e.tensor_copy(out=vcdup[64:128, :, c0:c0 + 128], in_=ptk[64:128, 6:8, :]), reads=[ptk], writes=[vcdup])
                if getattr(self, 'stop_at', 99) <= 5:
                    continue
                b.op("act", lambda e: e.copy(out=self.vaug_s[:, t, :, 0:64], in_=pm[1][:, 384:512].rearrange("p (g d) -> p g d", d=64)),
                     reads=[pm[1]], writes=[self.vaug_s])
                b.op("act", lambda e: e.copy(out=self.vaug_w[:, t, :, 0:64], in_=pm[2][:, 128:256].rearrange("p (g d) -> p g d", d=64)),
                     reads=[pm[2]], writes=[self.vaug_w])
                b.op("act", lambda e: e.activation(out=self.gts[:, t, :], in_=pm[2][:, 256:280], func=AF.Sigmoid), reads=[pm[2]], writes=[self.gts])
            if "nsa_proj" in self.debug:
                d = self.dbg_out("ksE", [128, 2, S], BF16)
                b.dma("pool", d, self.ksE[:], reads=[self.ksE])
                d = self.dbg_out("kcdup", [128, 2, S + 1], BF16)
                b.dma("pool", d, kcdup[:], reads=[kcdup])
                d = self.dbg_out("vaug_w", [128, NT, 2, 65], BF16)
                b.dma("pool", d, self.vaug_w[:], reads=[self.vaug_w])
                d = self.dbg_out("gts", [128, NT, 24], F32)
                b.dma("pool", d, self.gts[:], reads=[self.gts])
            if not getattr(self, 'skip_compress', False):
                self.compress(kcdup, vcdup, gk_rep, [pm[0], pm[1]], pm[2], ptk)

    def compress(self, kcdup, vcdup, gk_rep, ph, po, ptc):
        b = self.b
        I = self.inp
        C2 = 2.0 * 0.7978845608028654
        w1 = b.sb("w1", [128, 16, 256], BF16)
        w2 = b.sb("w2", [128, 2, 64], BF16)
        w1st = [b.sb(f"w1st{i}", [128, 256], F32) for i in range(2)]
        peT = b.sb("peT", [128, 16], F32)
        peTb = b.sb("peTb", [128, 16], BF16)
        hTc = b.sb("hTc", [128, 2, 256], BF16)
        pbias = b.sb("pbias", [128, 2], F32)
        xh = b.sb("xh", [128, 255], F32)
        x2 = b.sb("x2", [128, 255], F32)
        sg = b.sb("sg", [128, 255], F32)
        ctmp = b.sb("ctmp", [128, 64], F32)
        csq = b.sb("csq", [128, 64], F32)
        cs1 = b.sb("cs1", [128, 1], F32)
        kcb = b.sb("kcb", [128, 64], BF16)
        b.op("pool", lambda e: e.memset(hTc[:], 0.0), writes=[hTc])
        for kv, (dup, pe_n, w1_n, w2_n) in enumerate([(kcdup, "cmp_pe_k", "cmp_w1_k", "cmp_w2_k"), (vcdup, "cmp_pe_v", "cmp_w1_v", "cmp_w2_v")]):
            self.load_weight(w1, I[w1_n][0], 256, kch=16, stage=w1st, eng="dve")
            self.load_weight(w2, I[w2_n][0], 64, kch=2, stage=w1st, eng="dve")
            for two in range(2):
                b.dma("sp", peT[two * 64:(two + 1) * 64, :], I[pe_n][0].rearrange("(pp two) d -> two d pp", two=2)[two],
                      writes=[peT], allow_slow_non_contiguous=True)
            b.op("dve", lambda e: e.tensor_copy(out=peTb[:], in_=peT[:]), reads=[peT], writes=[peTb])
            for ft in range(2):
                for pp in range(16):
                    b.op("pe", lambda e: e.matmul(po[:, ft:ft + 1], lhsT=w1[:, pp, ft * 128:(ft + 1) * 128], rhs=peTb[:, pp:pp + 1],
                                                  start=(pp == 0), stop=(pp == 15)), reads=[w1, peTb], writes=[po])
            b.op("dve", lambda e: e.tensor_copy(out=pbias[:], in_=po[:, 0:2]), reads=[po], writes=[pbias])
            for g in range(2):
                for ft in range(2):
                    p = ph[ft]
                    for pp in range(16):
                        b.op("pe", lambda e: e.matmul(p[:, 0:255], lhsT=w1[:, pp, ft * 128:(ft + 1) * 128],
                                                      rhs=dup[:, g, 1 + 2 * pp:1 + 2 * pp + 16 * 254 + 1:16],
                                                      start=(pp == 0), stop=(pp == 15)), reads=[w1, dup], writes=[p])
                    b.op("act", lambda e: e.activation(out=xh[:], in_=p[:, 0:255], func=AF.Identity, bias=pbias[:, ft:ft + 1]), reads=[p, pbias], writes=[xh])
                    b.op("dve", lambda e: e.tensor_tensor(out=x2[:], in0=xh[:], in1=xh[:], op=ALU.mult), reads=[xh], writes=[x2])
                    b.op("dve", lambda e: e.tensor_scalar(out=x2[:], in0=x2[:], scalar1=0.044715, scalar2=1.0, op0=ALU.mult, op1=ALU.add), reads=[x2], writes=[x2])
                    b.op("dve", lambda e: e.tensor_tensor(out=x2[:], in0=x2[:], in1=xh[:], op=ALU.mult), reads=[x2, xh], writes=[x2])
                    b.op("act", lambda e: e.activation(out=sg[:], in_=x2[:], func=AF.Sigmoid, scale=C2), reads=[x2], writes=[sg])
                    b.op("dve", lambda e: e.tensor_tensor(out=hTc[:, ft, 0:255], in0=xh[:], in1=sg[:], op=ALU.mult), reads=[xh, sg], writes=[hTc])
                for ct in range(2):
                    for ft in range(2):
                        b.op("pe", lambda e: e.matmul(po[:, 64:128], lhsT=hTc[:, ft, ct * 128:(ct + 1) * 128], rhs=w2[:, ft, :],
                                                      start=(ft == 0), stop=(ft == 1)), reads=[hTc, w2], writes=[po])
                    if kv == 0:
                        b.op("act", lambda e: e.activation(out=csq[:], in_=po[:, 64:128], func=AF.Square, accum_out=cs1[:]), reads=[po], writes=[csq, cs1])
                        b.op("act", lambda e: e.activation(out=cs1[:], in_=cs1[:], func=AF.Sqrt, scale=1.0 / 64, bias=RMS_EPS), reads=[cs1], writes=[cs1])
                        b.op("dve", lambda e: e.reciprocal(out=cs1[:], in_=cs1[:]), reads=[cs1], writes=[cs1])
                        b.op("dve", lambda e: e.tensor_scalar_mul(out=ctmp[:], in0=po[:, 64:128], scalar1=cs1[:]), reads=[po, cs1], writes=[ctmp])
                        b.op("dve", lambda e: e.tensor_tensor(out=kcb[:], in0=ctmp[:], in1=gk_rep[:, 0, :], op=ALU.mult), reads=[ctmp, gk_rep], writes=[kcb])
                        b.op("pe", lambda e: e.transpose(out=ptc[0:64, 0, :], in_=kcb[:], identity=self.ident[:]), reads=[kcb, self.ident], writes=[ptc])
                        b.op("dve", lambda e: e.tensor_copy(out=self.kcT[:, g, ct * 128:(ct + 1) * 128], in_=ptc[0:64, 0, :]), reads=[ptc], writes=[self.kcT])
                    else:
                        b.op("dve", lambda e: e.tensor_copy(out=self.vcA[:, g, ct, 0:64], in_=po[:, 64:128]), reads=[po], writes=[self.vcA])
        if "compress" in self.debug:
            d = self.dbg_out("kcT", [64, 2, 256], BF16)
            b.dma("pool", d, self.kcT[:], reads=[self.kcT])
            d = self.dbg_out("vcA", [128, 2, 2, 129], F32)
            b.dma("pool", d, self.vcA[:], reads=[self.vcA])

    def finish(self):
        b = self.b
        b.wait_all_on("pool")
        b.barrier()
        b.close()
        return self.nc


def _phase_attn(self):
    b = self.b
    I = self.inp
    with b.scope():
        tw = b.sb("tw", [128, 8, 640], F32)
        ts = b.sb("ts", [128, 8, 640], F32)
        b.dma("sp", tw[:], I["tw"], writes=[tw])
        b.dma("sp", ts[:], I["ts"], writes=[ts])
        candneg = b.sb("candneg", [128, 32, 64], F32)
        fz = b.sb("fz", [128, 32, 64], F32)
        b.dma("sp", candneg[:], I["candneg"], writes=[candneg])
        b.dma("sp", fz[:], I["fz"], writes=[fz])
        b31 = b.sb("b31", [128, 8], F32)
        b.dma("sp", b31[:], I["b31"], writes=[b31])
        kwp = b.sb("kwp", [128, 2, S], BF16)
        b.op("pool", lambda e: e.memset(kwp[64:128, :, :], 0.0), writes=[kwp])
        b.op("pool", lambda e: e.tensor_copy(out=kwp[0:64, :, :], in_=self.kwT[:]), reads=[self.kwT], writes=[kwp])
        kcp = b.sb("kcp", [128, 2, 256], BF16)
        b.op("pool", lambda e: e.memset(kcp[64:128, :, :], 0.0), writes=[kcp])
        b.op("pool", lambda e: e.tensor_copy(out=kcp[0:64, :, :], in_=self.kcT[:]), reads=[self.kcT], writes=[kcp])
        zer = b.sb("zer", [128, 512], BF16)
        b.op("pool", lambda e: e.memset(zer[:], 0.0), writes=[zer])
        qm = [b.sb(f"qm{i}", [128, 8, 512], BF16) for i in range(2)]
        bct = [b.sb(f"bct{i}", [128, 512], F32) for i in range(3)]
        scf = [b.sb(f"scf{i}", [128, 640], F32) for i in range(2)]
        pcT = [b.sb(f"pcT{i}", [128, 2, 512], F32) for i in range(2)]
        pT = [b.sb(f"pT{i}", [128, 640], BF16) for i in range(3)]
        oacc = b.sb("oacc", [128, 4, 512], F32)
        imp = b.sb("imp", [128, 4, 2, 64], F32)
        impm = b.sb("impm", [128, 64], F32)
        impm2 = b.sb("impm2", [128, 64], F32)
        m8a = b.sb("m8a", [128, 8], F32)
        m8b = b.sb("m8b", [128, 8], F32)
        msk = b.sb("msk", [128, 64], F32)
        mb = b.sb("mb", [128, 128], BF16)
        b.op("pool", lambda e: e.memset(mb[:], 0.0), writes=[mb])
        rs = b.sb("rs", [128, 4], F32)
        rg = b.sb("rg", [128, 4], F32)
        oab = b.sb("oab", [128, 512], BF16)
        oaT = [b.sb(f"oaT{i}", [128, 4, 128], BF16) for i in range(2)]
        pS = [b.ps(f"pS{i}", [128, 512], F32) for i in range(2)]
        pS2 = b.ps("pS2", [128, 512], F32)
        pO = [b.ps(f"pO{i}", [128, 512], F32) for i in range(3)]
        pTr = b.ps("pTr", [128, 8, 128], BF16)
        nrot = {"bct": 0, "scf": 0, "pT": 0, "pS": 0}

        def rot(name, lst):
            nrot[name] += 1
            return lst[nrot[name] % len(lst)]

        def finalize(po, ncol_off, h, qs, branch, first):
            qt = qs_base + qs
            o0 = ncol_off
            b.op("dve", lambda e: e.tensor_scalar_max(out=rs[:, 0:1], in0=po[:, o0 + 64:o0 + 65], scalar1=1e-30), reads=[po], writes=[rs])
            b.op("dve", lambda e: e.reciprocal(out=rs[:, 1:2], in_=rs[:, 0:1]), reads=[rs], writes=[rs])
            b.op("dve", lambda e: e.tensor_tensor(out=rg[:, 0:1], in0=rs[:, 1:2], in1=self.gts[:, qt, h * 3 + branch:h * 3 + branch + 1], op=ALU.mult),
                 reads=[rs, self.gts], writes=[rg])
            if first:
                b.op("dve", lambda e: e.tensor_scalar_mul(out=oacc[:, qs, h * 64:(h + 1) * 64], in0=po[:, o0:o0 + 64], scalar1=rg[:, 0:1]),
                     reads=[po, rg], writes=[oacc])
            else:
                b.op("dve", lambda e: e.scalar_tensor_tensor(out=oacc[:, qs, h * 64:(h + 1) * 64], in0=po[:, o0:o0 + 64], scalar=rg[:, 0:1],
                                                             in1=oacc[:, qs, h * 64:(h + 1) * 64], op0=ALU.mult, op1=ALU.add),
                     reads=[po, rg, oacc], writes=[oacc])

        nqg = getattr(self, "nqg_limit", 8)
        for qg in range(nqg):
            qs_base = 4 * qg
            q0 = 512 * qg
            Q = qm[qg % 2]
            b.dma("sp", Q[0:64, :, :], self.qT_d[:, :, q0:q0 + 512].rearrange("h d t -> d h t"), reads=[self.qT_d], writes=[Q])
            if qg < 2:
                b.op("pool", lambda e: e.memset(Q[64:128, :, :], 0.0), writes=[Q])
            for h in range(8):
                g = h // 4
                pc = pcT[h % 2]
                for ct in range(2):
                    p = rot("pS", pS)
                    b.op("pe", lambda e: e.matmul(p[:, :], lhsT=kcp[:, g, ct * 128:(ct + 1) * 128], rhs=Q[:, h, :], start=True, stop=True),
                         reads=[kcp, Q], writes=[p])
                    bt = rot("bct", bct)
                    b.dma("sp", bt[:], I["biasc"][h, ct, :, q0:q0 + 512], writes=[bt])
                    sc = rot("scf", scf)
                    b.op("dve", lambda e: e.tensor_tensor(out=sc[:, 0:512], in0=p[:, :], in1=bt[:], op=ALU.add), reads=[p, bt], writes=[sc])
                    b.op("act", lambda e: e.activation(out=pc[:, ct, :], in_=sc[:, 0:512], func=AF.Exp), reads=[sc], writes=[pc])
                po = pO[0]
                for qs in range(4):
                    for ct in range(2):
                        b.op("pe", lambda e: e.matmul(po[:, qs * 128:qs * 128 + 129] if False else po[:, 0:129], lhsT=pc[:, ct, qs * 128:(qs + 1) * 128],
                                                      rhs=self.vcA[:, g, ct, :], start=(ct == 0), stop=(ct == 1)), reads=[pc, self.vcA], writes=[po])
                    finalize(po, 0, h, qs, 0, True)
                    if h % 4 == 0:
                        b.op("dve", lambda e: e.tensor_scalar_mul(out=imp[:, qs, g, :], in0=po[:, 65:129], scalar1=rs[:, 1:2]), reads=[po, rs], writes=[imp])
                    else:
                        b.op("dve", lambda e: e.scalar_tensor_tensor(out=imp[:, qs, g, :], in0=po[:, 65:129], scalar=rs[:, 1:2], in1=imp[:, qs, g, :],
                                                                     op0=ALU.mult, op1=ALU.add), reads=[po, rs, imp], writes=[imp])
            if qg >= 2:
                for qs in range(4):
                    qt = qs_base + qs
                    for g in range(2):
                        b.op("dve", lambda e: e.tensor_tensor(out=impm[:], in0=imp[:, qs, g, :], in1=candneg[:, qt, :], op=ALU.add), reads=[imp, candneg], writes=[impm])
                        b.op("dve", lambda e: e.max(out=m8a[:], in_=impm[:]), reads=[impm], writes=[m8a])
                        b.op("dve", lambda e: e.match_replace(out=impm2[:], in_to_replace=m8a[:], in_values=impm[:], imm_value=-1e9), reads=[m8a, impm], writes=[impm2])
                        b.op("dve", lambda e: e.max(out=m8b[:], in_=impm2[:]), reads=[impm2], writes=[m8b])
                        b.op("dve", lambda e: e.tensor_scalar(out=msk[:], in0=impm[:], scalar1=m8b[:, 4:5], scalar2=None, op0=ALU.is_ge), reads=[impm, m8b], writes=[msk])
                        b.op("dve", lambda e: e.tensor_tensor(out=msk[:], in0=msk[:], in1=fz[:, qt, :], op=ALU.max), reads=[msk, fz], writes=[msk])
                        b.op("dve", lambda e: e.tensor_scalar(out=mb[:, 64:128], in0=msk[:], scalar1=-NEG, scalar2=NEG, op0=ALU.mult, op1=ALU.add), reads=[msk], writes=[mb])
                        b.op("pe", lambda e: e.transpose(out=pTr[:, 0, :], in_=mb[:], identity=self.ident[:]), reads=[mb, self.ident], writes=[pTr])
                        b.op("act", lambda e: e.copy(out=Q[64:128, 4 * g:4 * g + 4, qs * 128:(qs + 1) * 128],
                                                     in_=pTr[64:128, 0:1, :].to_broadcast([64, 4, 128])), reads=[pTr], writes=[Q])
            for h in range(8):
                g = h // 4
                po_s, po_w = pO[1], pO[2]
                for po in (po_s, po_w):
                    b.op("pe", lambda e: e.matmul(po[:, 0:260], lhsT=zer[:, 0:128], rhs=zer[:, 0:260], start=True, stop=True), reads=[zer], writes=[po])
                nkt = 4 * (qg + 1)
                for kt in range(nkt):
                    dlt = 4 * qg - kt
                    qstart = 0 if dlt >= 0 else -dlt * 128
                    N = 512 - qstart
                    p = rot("pS", pS)
                    b.op("pe", lambda e: e.matmul(p[:, 0:N], lhsT=self.ksE[:, g, kt * 128:(kt + 1) * 128], rhs=Q[:, h, qstart:512], start=True, stop=True),
                         reads=[self.ksE, Q], writes=[p])
                    pt_ = rot("pT", pT)
                    if dlt <= 1:
                        c0 = 128 if dlt == 1 else 0
                        sc = rot("scf", scf)
                        b.op("dve", lambda e: e.tensor_tensor(out=sc[:, 0:N], in0=p[:, 0:N], in1=ts[:, h, c0:c0 + N], op=ALU.add), reads=[p, ts], writes=[sc])
                        b.op("act", lambda e: e.activation(out=pt_[:, 0:N], in_=sc[:, 0:N], func=AF.Exp), reads=[sc], writes=[pt_])
                    else:
                        b.op("act", lambda e: e.activation(out=pt_[:, 0:N], in_=p[:, 0:N], func=AF.Exp, bias=b31[:, h:h + 1]), reads=[p, b31], writes=[pt_])
                    for qs in range(qstart // 128, 4):
                        o = qs * 128 - qstart
                        b.op("pe", lambda e: e.matmul(po_s[:, qs * 65:(qs + 1) * 65], lhsT=pt_[:, o:o + 128], rhs=self.vaug_s[:, kt, g, :],
                                                      start=False, stop=(kt == nkt - 1), skip_group_check=True), reads=[pt_, self.vaug_s], writes=[po_s])
                kts = [kt for kt in range(4 * qg - 4, 4 * qg + 4) if kt >= 0]
                for kt in kts:
                    qs_lo = max(0, kt - 4 * qg)
                    qs_hi = min(3, kt + 4 - 4 * qg)
                    N = (qs_hi - qs_lo + 1) * 128
                    c0 = 128 * (4 * qg + qs_lo - kt)
                    p = rot("pS", pS)
                    b.op("pe", lambda e: e.matmul(p[:, 0:N], lhsT=kwp[:, g, kt * 128:(kt + 1) * 128], rhs=Q[:, h, qs_lo * 128:(qs_hi + 1) * 128], start=True, stop=True),
                         reads=[kwp, Q], writes=[p])
                    sc = rot("scf", scf)
                    b.op("dve", lambda e: e.tensor_tensor(out=sc[:, 0:N], in0=p[:, 0:N], in1=tw[:, h, c0:c0 + N], op=ALU.add), reads=[p, tw], writes=[sc])
                    pt_ = rot("pT", pT)
                    b.op("act", lambda e: e.activation(out=pt_[:, 0:N], in_=sc[:, 0:N], func=AF.Exp), reads=[sc], writes=[pt_])
                    for qs in range(qs_lo, qs_hi + 1):
                        o = (qs - qs_lo) * 128
                        b.op("pe", lambda e: e.matmul(po_w[:, qs * 65:(qs + 1) * 65], lhsT=pt_[:, o:o + 128], rhs=self.vaug_w[:, kt, g, :],
                                                      start=False, stop=(kt == kts[-1]), skip_group_check=True), reads=[pt_, self.vaug_w], writes=[po_w])
                for qs in range(4):
                    finalize(po_s, qs * 65, h, qs, 1, False)
                    finalize(po_w, qs * 65, h, qs, 2, False)
            for qs in range(4):
                qt = qs_base + qs
                ot = oaT[qs % 2]
                b.op("act", lambda e: e.copy(out=oab[:], in_=oacc[:, qs, :]), reads=[oacc], writes=[oab])
                for c in range(4):
                    b.op("pe", lambda e: e.transpose(out=pTr[:, 4 + c, :], in_=oab[:, c * 128:(c + 1) * 128], identity=self.ident[:]), reads=[oab, self.ident], writes=[pTr])
                b.op("act", lambda e: e.copy(out=ot[:], in_=pTr[:, 4:8, :]), reads=[pTr], writes=[ot])
                b.dma("pool", self.oaT_d[:, :, qt * 128:(qt + 1) * 128].rearrange("c p t -> p c t"), ot[:], reads=[ot], writes=[self.oaT_d])
        if "attn" in self.debug:
            d = self.dbg_out("oaT", [4, 128, S], BF16)
            b.dma("pool", d, self.oaT_d[:], reads=[self.oaT_d])


Prog.phase_attn = _phase_attn


def _phase_attn2(self):
    b = self.b
    I = self.inp
    with b.scope():
        tw = b.sb("tw", [128, 8, 640], F32)
        ts = b.sb("ts", [128, 8, 640], F32)
        b.dma("sp", tw[:], I["tw"], writes=[tw])
        b.dma("sp", ts[:], I["ts"], writes=[ts])
        candneg = b.sb("candneg", [128, 32, 64], F32)
        fz = b.sb("fz", [128, 32, 64], F32)
        b.dma("sp", candneg[:], I["candneg"], writes=[candneg])
        b.dma("sp", fz[:], I["fz"], writes=[fz])
        b31 = b.sb("b31", [128, 8], F32)
        b.dma("sp", b31[:], I["b31"], writes=[b31])
        kwp = b.sb("kwp", [128, 2, S], BF16)
        b.op("pool", lambda e: e.memset(kwp[64:128, :, :], 0.0), writes=[kwp])
        b.op("pool", lambda e: e.tensor_copy(out=kwp[0:64, :, :], in_=self.kwT[:]), reads=[self.kwT], writes=[kwp])
        kcp = b.sb("kcp", [128, 2, 256], BF16)
        b.op("pool", lambda e: e.memset(kcp[64:128, :, :], 0.0), writes=[kcp])
        b.op("pool", lambda e: e.tensor_copy(out=kcp[0:64, :, :], in_=self.kcT[:]), reads=[self.kcT], writes=[kcp])
        zer = b.sb("zer", [128, 512], BF16)
        b.op("pool", lambda e: e.memset(zer[:], 0.0), writes=[zer])
        qm = [b.sb(f"qm{i}", [128, 8, 512], BF16) for i in range(2)]
        bct = [b.sb(f"bct{i}", [128, 512], F32) for i in range(3)]
        scf = [b.sb(f"scf{i}", [128, 640], F32) for i in range(3)]
        pcT = [b.sb(f"pcT{i}", [128, 2, 512], F32) for i in range(2)]
        pT = [b.sb(f"pT{i}", [128, 640], BF16) for i in range(4)]
        oacc = b.sb("oacc", [128, 4, 512], F32)
        imp = b.sb("imp", [128, 4, 2, 64], F32)
        impm = b.sb("impm", [128, 64], F32)
        impm2 = b.sb("impm2", [128, 64], F32)
        m8a = b.sb("m8a", [128, 8], F32)
        m8b = b.sb("m8b", [128, 8], F32)
        msk = b.sb("msk", [128, 64], F32)
        mb = b.sb("mb", [128, 128], BF16)
        b.op("pool", lambda e: e.memset(mb[:], 0.0), writes=[mb])
        rs = b.sb("rs", [128, 4], F32)
        rg = b.sb("rg", [128, 4], F32)
        oab = b.sb("oab", [128, 512], BF16)
        oaT = [b.sb(f"oaT{i}", [128, 4, 128], BF16) for i in range(2)]
        pS = [b.ps(f"pS{i}", [128, 512], F32) for i in range(3)]
        pOs = [b.ps(f"pOs{i}", [128, 512], F32) for i in range(2)]
        pOw = [b.ps(f"pOw{i}", [128, 512], F32) for i in range(2)]
        pTr = b.ps("pTr", [128, 8, 128], BF16)
        nrot = {"bct": 0, "scf": 0, "pT": 0, "pS": 0}

        def rot(name, lst):
            nrot[name] += 1
            return lst[nrot[name] % len(lst)]

        def finalize(po, ncol_off, h, qs, branch, first):
            qt = qs_base + qs
            o0 = ncol_off
            b.op("dve", lambda e: e.tensor_scalar_max(out=rs[:, 0:1], in0=po[:, o0 + 64:o0 + 65], scalar1=1e-30), reads=[po], writes=[rs])
            b.op("dve", lambda e: e.reciprocal(out=rs[:, 1:2], in_=rs[:, 0:1]), reads=[rs], writes=[rs])
            b.op("dve", lambda e: e.tensor_tensor(out=rg[:, 0:1], in0=rs[:, 1:2], in1=self.gts[:, qt, h * 3 + branch:h * 3 + branch + 1], op=ALU.mult),
                 reads=[rs, self.gts], writes=[rg])
            if first:
                b.op("dve", lambda e: e.tensor_scalar_mul(out=oacc[:, qs, h * 64:(h + 1) * 64], in0=po[:, o0:o0 + 64], scalar1=rg[:, 0:1]),
                     reads=[po, rg], writes=[oacc])
            else:
                b.op("dve", lambda e: e.scalar_tensor_tensor(out=oacc[:, qs, h * 64:(h + 1) * 64], in0=po[:, o0:o0 + 64], scalar=rg[:, 0:1],
                                                             in1=oacc[:, qs, h * 64:(h + 1) * 64], op0=ALU.mult, op1=ALU.add),
                     reads=[po, rg, oacc], writes=[oacc])

        nqg = getattr(self, "nqg_limit", 8)
        for qg in range(nqg):
            qs_base = 4 * qg
            q0 = 512 * qg
            Q = qm[qg % 2]
            b.dma("sp", Q[0:64, :, :], self.qT_d[:, :, q0:q0 + 512].rearrange("h d t -> d h t"), reads=[self.qT_d], writes=[Q])
            if qg < 2:
                b.op("pool", lambda e: e.memset(Q[64:128, :, :], 0.0), writes=[Q])
            for h in range(8):
                g = h // 4
                pc = pcT[h % 2]
                for ct in range(2):
                    p = rot("pS", pS)
                    b.op("pe", lambda e: e.matmul(p[:, :], lhsT=kcp[:, g, ct * 128:(ct + 1) * 128], rhs=Q[:, h, :], start=True, stop=True),
                         reads=[kcp, Q], writes=[p])
                    bt = rot("bct", bct)
                    b.dma("sp", bt[:], I["biasc"][h, ct, :, q0:q0 + 512], writes=[bt])
                    sc = rot("scf", scf)
                    b.op("dve", lambda e: e.tensor_tensor(out=sc[:, 0:512], in0=p[:, :], in1=bt[:], op=ALU.add), reads=[p, bt], writes=[sc])
                    b.op("act", lambda e: e.activation(out=pc[:, ct, :], in_=sc[:, 0:512], func=AF.Exp), reads=[sc], writes=[pc])
                po = pOs[h % 2]
                for qs in range(4):
                    for ct in range(2):
                        b.op("pe", lambda e: e.matmul(po[:, qs * 128:qs * 128 + 129] if False else po[:, 0:129], lhsT=pc[:, ct, qs * 128:(qs + 1) * 128],
                                                      rhs=self.vcA[:, g, ct, :], start=(ct == 0), stop=(ct == 1)), reads=[pc, self.vcA], writes=[po])
                    finalize(po, 0, h, qs, 0, True)
                    if h % 4 == 0:
                        b.op("dve", lambda e: e.tensor_scalar_mul(out=imp[:, qs, g, :], in0=po[:, 65:129], scalar1=rs[:, 1:2]), reads=[po, rs], writes=[imp])
                    else:
                        b.op("dve", lambda e: e.scalar_tensor_tensor(out=imp[:, qs, g, :], in0=po[:, 65:129], scalar=rs[:, 1:2], in1=imp[:, qs, g, :],
                                                                     op0=ALU.mult, op1=ALU.add), reads=[po, rs, imp], writes=[imp])
            if qg >= 2:
                for qs in range(4):
                    qt = qs_base + qs
                    for g in range(2):
                        b.op("dve", lambda e: e.tensor_tensor(out=impm[:], in0=imp[:, qs, g, :], in1=candneg[:, qt, :], op=ALU.add), reads=[imp, candneg], writes=[impm])
                        b.op("dve", lambda e: e.max(out=m8a[:], in_=impm[:]), reads=[impm], writes=[m8a])
                        b.op("dve", lambda e: e.match_replace(out=impm2[:], in_to_replace=m8a[:], in_values=impm[:], imm_value=-1e9), reads=[m8a, impm], writes=[impm2])
                        b.op("dve", lambda e: e.max(out=m8b[:], in_=impm2[:]), reads=[impm2], writes=[m8b])
                        b.op("dve", lambda e: e.tensor_scalar(out=msk[:], in0=impm[:], scalar1=m8b[:, 4:5], scalar2=None, op0=ALU.is_ge), reads=[impm, m8b], writes=[msk])
                        b.op("dve", lambda e: e.tensor_tensor(out=msk[:], in0=msk[:], in1=fz[:, qt, :], op=ALU.max), reads=[msk, fz], writes=[msk])
                        b.op("dve", lambda e: e.tensor_scalar(out=mb[:, 64:128], in0=msk[:], scalar1=-NEG, scalar2=NEG, op0=ALU.mult, op1=ALU.add), reads=[msk], writes=[mb])
                        b.op("pe", lambda e: e.transpose(out=pTr[:, 0, :], in_=mb[:], identity=self.ident[:]), reads=[mb, self.ident], writes=[pTr])
                        b.op("act", lambda e: e.copy(out=Q[64:128, 4 * g:4 * g + 4, qs * 128:(qs + 1) * 128],
                                                     in_=pTr[64:128, 0:1, :].to_broadcast([64, 4, 128])), reads=[pTr], writes=[Q])
            jobs = []
            for h in range(8):
                g = h // 4
                nkt = 4 * (qg + 1)
                for kt in range(nkt):
                    dlt = 4 * qg - kt
                    qstart = 0 if dlt >= 0 else -dlt * 128
                    jobs.append(dict(kind="s", h=h, g=g, kt=kt, qlo=qstart // 128, qhi=3, first=(kt == 0), last=False, lastkt=(kt == nkt - 1),
                                     tab=(ts, (128 if dlt == 1 else 0)) if dlt <= 1 else None))
                kts = [kt for kt in range(4 * qg - 4, 4 * qg + 4) if kt >= 0]
                for kt in kts:
                    qs_lo = max(0, kt - 4 * qg)
                    qs_hi = min(3, kt + 4 - 4 * qg)
                    jobs.append(dict(kind="w", h=h, g=g, kt=kt, qlo=qs_lo, qhi=qs_hi, first=False, last=(kt == kts[-1]), lastkt=(kt == kts[-1]),
                                     tab=(tw, 128 * (4 * qg + qs_lo - kt))))

            def emitS(j):
                h, g, kt = j["h"], j["g"], j["kt"]
                N = (j["qhi"] - j["qlo"] + 1) * 128
                p = rot("pS", pS)
                kmat = self.ksE if j["kind"] == "s" else kwp
                b.op("pe", lambda e: e.matmul(p[:, 0:N], lhsT=kmat[:, g, kt * 128:(kt + 1) * 128], rhs=Q[:, h, j["qlo"] * 128:(j["qhi"] + 1) * 128], start=True, stop=True),
                     reads=[kmat, Q], writes=[p])
                j["p"] = p
                j["N"] = N

            def emitE(j):
                h = j["h"]
                p, N = j["p"], j["N"]
                pt_ = rot("pT", pT)
                if j["tab"] is not None:
                    tab, c0 = j["tab"]
                    sc = rot("scf", scf)
                    b.op("dve", lambda e: e.tensor_tensor(out=sc[:, 0:N], in0=p[:, 0:N], in1=tab[:, h, c0:c0 + N], op=ALU.add), reads=[p, tab], writes=[sc])
                    b.op("act", lambda e: e.activation(out=pt_[:, 0:N], in_=sc[:, 0:N], func=AF.Exp), reads=[sc], writes=[pt_])
                else:
                    b.op("act", lambda e: e.activation(out=pt_[:, 0:N], in_=p[:, 0:N], func=AF.Exp, bias=b31[:, h:h + 1]), reads=[p, b31], writes=[pt_])
                j["pt"] = pt_

            def emitPV(j):
                h, g, kt = j["h"], j["g"], j["kt"]
                po_s, po_w = pOs[h % 2], pOw[h % 2]
                if j["first"]:
                    for po in (po_s, po_w):
                        b.op("pe", lambda e: e.matmul(po[:, 0:260], lhsT=zer[:, 0:128], rhs=zer[:, 0:260], start=True, stop=True), reads=[zer], writes=[po])
                po = po_s if j["kind"] == "s" else po_w
                va = self.vaug_s if j["kind"] == "s" else self.vaug_w
                for qs in range(j["qlo"], j["qhi"] + 1):
                    o = (qs - j["qlo"]) * 128
                    b.op("pe", lambda e: e.matmul(po[:, qs * 65:(qs + 1) * 65], lhsT=j["pt"][:, o:o + 128], rhs=va[:, kt, g, :],
                                                  start=False, stop=j["lastkt"], skip_group_check=True), reads=[j["pt"], va], writes=[po])
                if j["last"]:
                    for qs in range(4):
                        finalize(po_s, qs * 65, h, qs, 1, False)
                        finalize(po_w, qs * 65, h, qs, 2, False)

            LA = 2
            for i_ in range(len(jobs) + LA):
                if i_ < len(jobs):
                    emitS(jobs[i_])
                if i_ >= LA:
                    emitE(jobs[i_ - LA])
                    emitPV(jobs[i_ - LA])
            for qs in range(4):
                qt = qs_base + qs
                ot = oaT[qs % 2]
                b.op("act", lambda e: e.copy(out=oab[:], in_=oacc[:, qs, :]), reads=[oacc], writes=[oab])
                for c in range(4):
                    b.op("pe", lambda e: e.transpose(out=pTr[:, 4 + c, :], in_=oab[:, c * 128:(c + 1) * 128], identity=self.ident[:]), reads=[oab, self.ident], writes=[pTr])
                b.op("act", lambda e: e.copy(out=ot[:], in_=pTr[:, 4:8, :]), reads=[pTr], writes=[ot])
                b.dma("pool", self.oaT_d[:, :, qt * 128:(qt + 1) * 128].rearrange("c p t -> p c t"), ot[:], reads=[ot], writes=[self.oaT_d])
        if "attn" in self.debug:
            d = self.dbg_out("oaT", [4, 128, S], BF16)
            b.dma("pool", d, self.oaT_d[:], reads=[self.oaT_d])


Prog.phase_attn2 = _phase_attn2


def _phase_merge(self):
    b = self.b
    I = self.inp
    self.x1_d = b.dram("x1_d", [S, D], F32)
    with b.scope():
        gat = self.load_gain("gat2", I["attn_norm_g"][0])
        stage = [b.sb(f"mst{i}", [128, 1024], F32) for i in range(2)]
        wg = b.sb("wg", [128, 8, 2048], BF16)
        for n in range(2):
            for c in range(8):
                st = stage[c % 2]
                b.dma("sp", st[:], I["w_in"][0][c * 128:(c + 1) * 128, GA0 + n * 1024:GA0 + (n + 1) * 1024], writes=[st])
                b.op("act", lambda e: e.activation(out=wg[:, c, n * 1024:(n + 1) * 1024], in_=st[:], func=AF.Copy, scale=gat[:, c:c + 1]),
                     reads=[st, gat], writes=[wg])
        wa = b.sb("wa", [128, 4, 1024], BF16)
        wb = b.sb("wb", [128, 4, 1024], BF16)
        wo = b.sb("wo", [128, 8, 1024], BF16)
        self.load_weight(wa, I["w_proj_a"][0], 1024, kch=4, stage=stage, eng="dve")
        self.load_weight(wb, I["w_proj_b"][0], 1024, kch=4, stage=stage, eng="dve")
        self.load_weight(wo, I["w_out"][0], 1024, kch=8, stage=stage, eng="dve")
        xt = [b.sb(f"mxt{i}", [128, D], F32) for i in range(2)]
        junk = b.sb("mjunk", [128, D], BF16)
        ss = [b.sb(f"mss{i}", [128, 1], F32) for i in range(2)]
        hb = [b.sb(f"mhb{i}", [128, D], BF16) for i in range(2)]
        hT = [b.sb(f"mhT{i}", [128, 8, 128], BF16) for i in range(2)]
        oat = [b.sb(f"oat{i}", [128, 4, 128], BF16) for i in range(2)]
        obt = [b.sb(f"obt{i}", [128, 4, 128], BF16) for i in range(2)]
        sg = b.sb("msg", [128, 2048], F32)
        m1 = b.sb("m1", [128, 1024], F32)
        m2 = b.sb("m2", [128, 1024], F32)
        mgb = b.sb("mgb", [128, 1024], BF16)
        mT = b.sb("mT", [128, 8, 128], BF16)
        x1t = [b.sb(f"x1t{i}", [128, D], F32) for i in range(2)]
        pt = b.ps("mpt", [128, 8, 128], BF16)
        pg = [b.ps(f"mpg{i}", [128, 512], F32) for i in range(2)]
        pa = [b.ps(f"mpa{i}", [128, 512], F32) for i in range(2)]
        pb = [b.ps(f"mpb{i}", [128, 512], F32) for i in range(2)]
        for t in range(getattr(self, "nt_limit", NT)):
            i = t % 2
            self.make_hT(I["x"], t, xt[i], junk, ss[i], hb[i], pt, hT[i], self.ident)
            b.dma("sp", oat[i][:], self.oaT_d[:, :, t * 128:(t + 1) * 128].rearrange("c p t -> p c t"), reads=[self.oaT_d], writes=[oat[i]])
            b.dma("sp", obt[i][:], self.obT_d[:, :, t * 128:(t + 1) * 128].rearrange("c p t -> p c t"), reads=[self.obT_d], writes=[obt[i]])
            for n in range(4):
                p = pg[n % 2]
                for c in range(8):
                    b.op("pe", lambda e: e.matmul(p[:, :], lhsT=hT[i][:, c, :], rhs=wg[:, c, n * 512:(n + 1) * 512], start=(c == 0), stop=(c == 7)),
                         reads=[hT[i], wg], writes=[p])
                b.op("act", lambda e: e.activation(out=sg[:, n * 512:(n + 1) * 512], in_=p[:, :], func=AF.Sigmoid), reads=[p], writes=[sg])
            for n in range(2):
                for c in range(4):
                    b.op("pe", lambda e: e.matmul(pa[n][:, :], lhsT=oat[i][:, c, :], rhs=wa[:, c, n * 512:(n + 1) * 512], start=(c == 0), stop=(c == 3)),
                         reads=[oat[i], wa], writes=[pa[n]])
                for c in range(4):
                    b.op("pe", lambda e: e.matmul(pb[n][:, :], lhsT=obt[i][:, c, :], rhs=wb[:, c, n * 512:(n + 1) * 512], start=(c == 0), stop=(c == 3)),
                         reads=[obt[i], wb], writes=[pb[n]])
                b.op("dve", lambda e: e.tensor_tensor(out=m1[:, n * 512:(n + 1) * 512], in0=pa[n][:, :], in1=sg[:, n * 512:(n + 1) * 512], op=ALU.mult),
                     reads=[pa[n], sg], writes=[m1])
                b.op("dve", lambda e: e.tensor_tensor(out=m2[:, n * 512:(n + 1) * 512], in0=pb[n][:, :], in1=sg[:, 1024 + n * 512:1024 + (n + 1) * 512], op=ALU.mult),
                     reads=[pb[n], sg], writes=[m2])
            b.op("pool", lambda e: e.tensor_tensor(out=mgb[:], in0=m1[:], in1=m2[:], op=ALU.add), reads=[m1, m2], writes=[mgb])
            for c in range(8):
                b.op("pe", lambda e: e.transpose(out=pt[:, c, :], in_=mgb[:, c * 128:(c + 1) * 128], identity=self.ident[:]), reads=[mgb, self.ident], writes=[pt])
            b.op("act", lambda e: e.copy(out=mT[:], in_=pt[:]), reads=[pt], writes=[mT])
            for n in range(2):
                for c in range(8):
                    b.op("pe", lambda e: e.matmul(pa[n][:, :], lhsT=mT[:, c, :], rhs=wo[:, c, n * 512:(n + 1) * 512], start=(c == 0), stop=(c == 7)),
                         reads=[mT, wo], writes=[pa[n]])
                b.op("dve", lambda e: e.tensor_tensor(out=x1t[i][:, n * 512:(n + 1) * 512], in0=pa[n][:, :], in1=xt[i][:, n * 512:(n + 1) * 512], op=ALU.add),
                     reads=[pa[n], xt[i]], writes=[x1t[i]])
            b.dma("pool", self.x1_d[t * 128:(t + 1) * 128, :], x1t[i][:], reads=[x1t[i]], writes=[self.x1_d])
        if "merge" in self.debug:
            d = self.dbg_out("x1", [S, D], F32)
            b.dma("pool", d, self.x1_d[:], reads=[self.x1_d])


def _phase_ffn(self):
    b = self.b
    I = self.inp
    TG = 128
    NFT = 44
    with b.scope():
        gf = self.load_gain("gf", I["ffn_norm_g"][0])
        stage = [b.sb(f"fst{i}", [128, 1024], F32) for i in range(2)]
        wu = b.sb("wu", [128, 8, 2 * DFF], BF16)
        for n in range(8):
            for c in range(8):
                st = stage[c % 2]
                b.dma("sp", st[:, 0:704], I["w_up"][0][c * 128:(c + 1) * 128, n * 704:(n + 1) * 704], writes=[st])
                b.op("act", lambda e: e.activation(out=wu[:, c, n * 704:(n + 1) * 704], in_=st[:, 0:704], func=AF.Copy, scale=gf[:, c:c + 1]),
                     reads=[st, gf], writes=[wu])
        wd = b.sb("wd", [128, 22, D], BF16)
        self.load_weight(wd, I["w_down"][0], D, kch=22, stage=stage, eng="dve")
        cw = b.sb("cw", [128, 3, NFT], F32)
        for j in range(3):
            b.dma("sp", cw[:, j, :], I["conv_w"][0][j].rearrange("(c p) -> p c", p=128), writes=[cw], allow_slow_non_contiguous=True)
        cbias = self.load_gain("cbias", I["conv_b"][0], kch=NFT)
        carry = b.sb("carry", [128, NFT, 2], F32)
        b.op("pool", lambda e: e.memset(carry[:], 0.0), writes=[carry])
        xt = [b.sb(f"fxt{i}", [128, D], F32) for i in range(2)]
        junk = b.sb("fjunk", [128, D], BF16)
        ss = [b.sb(f"fss{i}", [128, 1], F32) for i in range(2)]
        hb = [b.sb(f"fhb{i}", [128, D], BF16) for i in range(2)]
        hT1 = [b.sb(f"fhT{i}", [128, 8, 128], BF16) for i in range(2)]
        hTg = b.sb("fhTg", [128, 8, TG], BF16)
        ub = [b.sb(f"ub{i}", [128, TG + 2], F32) for i in range(2)]
        cv = [b.sb(f"cv{i}", [128, TG], F32) for i in range(2)]
        sgl = b.sb("sgl", [128, TG], F32)
        actT = b.sb("actT", [128, 22, TG], BF16)
        self._val = b.sb("fval", [128, 22, TG], BF16)
        ot = xt
        pt = b.ps("fpt", [128, 8, 128], BF16)
        pu = [b.ps(f"fpu{i}", [128, 512], F32) for i in range(3)]
        pd = [b.ps(f"fpd{i}", [128, 512], F32) for i in range(2)]
        ng = getattr(self, "nt_limit", NT) * 128 // TG
        for gi in range(ng):
            for s_ in range(TG // 128):
                t = gi * (TG // 128) + s_
                self.make_hT(self.x1_d, t, xt[s_], junk, ss[s_], hb[s_], pt, hT1[s_], self.ident)
                b.op("pool", lambda e: e.tensor_copy(out=hTg[:, :, s_ * 128:(s_ + 1) * 128], in_=hT1[s_][:]), reads=[hT1[s_]], writes=[hTg])
            for ft in range(NFT):
                p = pu[ft % 3]
                u = ub[ft % 2]
                c_ = cv[(ft // 22) % 2] if False else cv[ft % 2]
                for c in range(8):
                    b.op("pe", lambda e: e.matmul(p[:, 0:TG], lhsT=wu[:, c, ft * 128:(ft + 1) * 128], rhs=hTg[:, c, :], start=(c == 0), stop=(c == 7)),
                         reads=[wu, hTg], writes=[p])
                b.op("act", lambda e: e.copy(out=u[:, 2:TG + 2], in_=p[:, 0:TG]), reads=[p], writes=[u])
                b.op("pool", lambda e: e.tensor_copy(out=u[:, 0:2], in_=carry[:, ft, :]), reads=[carry], writes=[u])
                b.op("pool", lambda e: e.tensor_copy(out=carry[:, ft, :], in_=u[:, TG:TG + 2]), reads=[u], writes=[carry])
                b.op("dve", lambda e: e.tensor_scalar(out=c_[:], in0=u[:, 0:TG], scalar1=cw[:, 0, ft:ft + 1], scalar2=cbias[:, ft:ft + 1], op0=ALU.mult, op1=ALU.add),
                     reads=[u, cw, cbias], writes=[c_])
                b.op("dve", lambda e: e.scalar_tensor_tensor(out=c_[:], in0=u[:, 1:TG + 1], scalar=cw[:, 1, ft:ft + 1], in1=c_[:], op0=ALU.mult, op1=ALU.add),
                     reads=[u, cw, c_], writes=[c_])
                if ft < 22:
                    b.op("dve", lambda e: e.scalar_tensor_tensor(out=self._val[:, ft, :], in0=u[:, 2:TG + 2], scalar=cw[:, 2, ft:ft + 1], in1=c_[:], op0=ALU.mult, op1=ALU.add),
                         reads=[u, cw, c_], writes=[self._val])
                else:
                    b.op("dve", lambda e: e.scalar_tensor_tensor(out=c_[:], in0=u[:, 2:TG + 2], scalar=cw[:, 2, ft:ft + 1], in1=c_[:], op0=ALU.mult, op1=ALU.add),
                         reads=[u, cw, c_], writes=[c_])
                    b.op("act", lambda e: e.activation(out=sgl[:], in_=c_[:], func=AF.Silu), reads=[c_], writes=[sgl])
                    b.op("dve", lambda e: e.tensor_tensor(out=actT[:, ft - 22, :], in0=sgl[:], in1=self._val[:, ft - 22, :], op=ALU.mult),
                         reads=[sgl, self._val], writes=[actT])
            for s_ in range(TG // 128):
                t = gi * (TG // 128) + s_
                for n in range(2):
                    for f in range(22):
                        b.op("pe", lambda e: e.matmul(pd[n][:, :], lhsT=actT[:, f, s_ * 128:(s_ + 1) * 128], rhs=wd[:, f, n * 512:(n + 1) * 512], start=(f == 0), stop=(f == 21)),
                             reads=[actT, wd], writes=[pd[n]])
                    b.op("dve", lambda e: e.tensor_tensor(out=ot[s_][:, n * 512:(n + 1) * 512], in0=pd[n][:, :], in1=xt[s_][:, n * 512:(n + 1) * 512], op=ALU.add),
                         reads=[pd[n], xt[s_]], writes=[ot[s_]])
                b.dma("pool", self.out[t * 128:(t + 1) * 128, :], ot[s_][:], reads=[ot[s_]])


Prog.phase_merge = _phase_merge
Prog.phase_ffn = _phase_ffn


def _phase_rwkv(self):
    b = self.b
    I = self.inp
    TG = 256
    NCH = TG // 64
    tt = lambda eng, out, in0, in1, op, rd, wr: b.op(eng, lambda e: e.tensor_tensor(out=out, in0=in0, in1=in1, op=op), reads=rd, writes=wr)
    with b.scope():
        gat = self.load_gain("gat3", I["attn_norm_g"][0])
        stage = [b.sb(f"rst{i}", [128, 1792], F32) for i in range(2)]
        wr = b.sb("wr", [128, 8, 1792], BF16)
        self.load_weight(wr, I["w_in"][0][:, RW0:RW0 + 1792], 1792, gvec=gat, stage=stage)

        def colvec(name, src, n):
            t = b.sb(name, [64, n], F32)
            b.dma("sp", t[:], src.rearrange("(c p) -> p c", p=64), writes=[t], allow_slow_non_contiguous=True)
            return t
        mu = colvec("mu", I["rwkv_mu"][0], 28)
        w0 = colvec("w0", I["rwkv_w0"][0], 8)
        a0 = colvec("a0", I["rwkv_a0"][0], 8)
        k_k = colvec("k_k", I["rwkv_k_k"][0], 8)
        k_a = colvec("k_a", I["rwkv_k_a"][0], 8)
        r_k = colvec("r_k", I["rwkv_r_k"][0].rearrange("h d -> (h d)"), 8)
        w2s = b.sb("w2s", [64, 512], F32)
        a2s = b.sb("a2s", [64, 512], F32)
        g2s = b.sb("g2s", [64, 2, 512], F32)
        b.dma("sp", w2s[:], I["rwkv_w2"][0], writes=[w2s])
        b.dma("sp", a2s[:], I["rwkv_a2"][0], writes=[a2s])
        b.dma("sp", g2s[:], I["rwkv_g2"][0].rearrange("(two l) f -> l two f", two=2), writes=[g2s])
        lng = b.sb("lng", [64, 512], F32)
        lnb = b.sb("lnb", [64, 512], F32)
        b.dma("sp", lng[:], I["rwkv_ln_g"][0].partition_broadcast(64), writes=[lng])
        b.dma("sp", lnb[:], I["rwkv_ln_b"][0].partition_broadcast(64), writes=[lnb])
        msk = b.sb("rmsk", [64, 3, 64], F32)
        b.dma("sp", msk[:], I["rwmask"], writes=[msk])
        rstm = b.sb("rstm", [64, TG], F32)
        b.dma("sp", rstm[:], I["rwreset"][:, 0:TG], writes=[rstm])
        ones = b.sb("ones64", [64, 64], F32)
        b.op("pool", lambda e: e.memset(ones[:], 1.0), writes=[ones])
        idf = self.identf
        carry = b.sb("rcarry", [64, 28], F32)
        b.op("pool", lambda e: e.memset(carry[:], 0.0), writes=[carry])
        Hs = [[b.sb(f"H{h}_{i}", [64, 64], F32) for i in range(2)] for h in range(8)]
        for h in range(8):
            b.op("pool", lambda e: e.memset(Hs[h][0][:], 0.0), writes=[Hs[h][0]])
        xt = [b.sb(f"rxt{i}", [128, D], F32) for i in range(2)]
        junk = b.sb("rjunk", [128, D], BF16)
        ss = [b.sb(f"rss{i}", [128, 1], F32) for i in range(2)]
        hb = [b.sb(f"rhb{i}", [128, D], BF16) for i in range(2)]
        hT1 = [b.sb(f"rhT{i}", [128, 8, 128], BF16) for i in range(2)]
        hTg = b.sb("rhTg", [128, 8, TG], BF16)
        pbuf = [b.sb(f"rpb{i}", [64, TG + 1], F32) for i in range(2)]
        dtmp = b.sb("rdtmp", [64, TG], F32)
        X = [b.sb(f"rX{w}", [64, 8, TG], F32) for w in range(3)]
        xs = b.sb("rxs", [64, 4, TG], F32)
        BV = b.sb("rBV", [64, 8, TG], F32)
        Ytm = b.sb("rYtm", [64, NCH, 8, 64], F32)
        sqv = b.sb("rsqv", [64, NCH, 8, 64], F32)
        st1 = b.sb("rst1", [64, NCH * 8], F32)
        st2 = b.sb("rst2", [64, NCH * 8], F32)
        T = {n: b.sb("r" + n, [64, TG], F32) for n in ["lw", "as", "kk", "sq", "kkn", "bv", "kp", "t1", "L", "Lx", "Ep", "Em", "Ex", "BT", "KT", "BG", "KG", "rk"]}
        AR = b.sb("rAR", [64, NCH, 2, 64], F32)
        TM = [b.sb(f"rTM{i}", [64, 3, 64], F32) for i in range(2)]
        XM = [b.sb(f"rXM{i}", [64, 4, 64], F32) for i in range(2)]
        AA = [b.sb(f"rAA{i}", [64, 2, 64], F32) for i in range(3)]
        PP = [b.sb(f"rPP{i}", [64, 64], F32) for i in range(3)]
        Xs = b.sb("rXs", [64, 64], F32)
        Us = b.sb("rUs", [64, 64], F32)
        obf = [b.sb(f"robf{i}", [64, TG], BF16) for i in range(2)]
        otmp = b.sb("rotmp", [64, TG], F32)
        pt = b.ps("rpt", [128, 8, 128], BF16)
        pp = [b.ps(f"rpp{i}", [128, 512], F32) for i in range(2)]
        pq = [b.ps(f"rpq{i}", [128, 512], F32) for i in range(2)]
        pd = [b.ps(f"rpd{i}", [128, 512], F32) for i in range(2)]
        pz = b.ps("rpz", [128, 512], F32)
        cnt = {"pp": 0, "pq": 0, "pd": 0, "aa": 0, "ppb": 0, "tm": 0, "xm": 0, "pb": 0}

        def nxt(k, lst):
            cnt[k] += 1
            return lst[cnt[k] % len(lst)]

        ngr = getattr(self, "nrg_limit", S // TG)
        for gi in range(ngr):
            q0 = gi * TG
            for s_ in range(TG // 128):
                t = gi * (TG // 128) + s_
                self.make_hT(I["x"], t, xt[s_], junk, ss[s_], hb[s_], pt, hT1[s_], self.ident)
                b.op("pool", lambda e: e.tensor_copy(out=hTg[:, :, s_ * 128:(s_ + 1) * 128], in_=hT1[s_][:]), reads=[hT1[s_]], writes=[hTg])

            def proj_lerp(fc, out_ap, out_buf, post=None):
                p = nxt("pp", pp)
                for c in range(8):
                    b.op("pe", lambda e: e.matmul(p[0:64, 0:TG], lhsT=wr[:, c, fc * 64:(fc + 1) * 64], rhs=hTg[:, c, :], start=(c == 0), stop=(c == 7)),
                         reads=[wr, hTg], writes=[p])
                pb_ = nxt("pb", pbuf)
                b.op("act", lambda e: e.copy(out=pb_[:, 1:TG + 1], in_=p[0:64, 0:TG]), reads=[p], writes=[pb_])
                b.op("pool", lambda e: e.tensor_copy(out=pb_[:, 0:1], in_=carry[:, fc:fc + 1]), reads=[carry], writes=[pb_])
                b.op("pool", lambda e: e.tensor_copy(out=carry[:, fc:fc + 1], in_=pb_[:, TG:TG + 1]), reads=[pb_], writes=[carry])
                tt("dve", dtmp[:], pb_[:, 0:TG], pb_[:, 1:TG + 1], ALU.subtract, [pb_], [dtmp])
                b.op("dve", lambda e: e.scalar_tensor_tensor(out=out_ap, in0=dtmp[:], scalar=mu[:, fc:fc + 1], in1=pb_[:, 1:TG + 1], op0=ALU.mult, op1=ALU.add),
                     reads=[dtmp, mu, pb_], writes=[out_buf])

            for w in range(3):
                for h in range(8):
                    proj_lerp(w * 8 + h, X[w][:, h, :], X[w])
            for j in range(4):
                proj_lerp(24 + j, xs[:, j, :], xs)
            b.op("act", lambda e: e.activation(out=xs[:, 0, :], in_=xs[:, 0, :], func=AF.Tanh), reads=[xs], writes=[xs])
            b.op("act", lambda e: e.activation(out=xs[:, 2:4, :], in_=xs[:, 2:4, :], func=AF.Sigmoid), reads=[xs], writes=[xs])

            for h in range(8):
                hs = slice(h * 64, (h + 1) * 64)
                R_, K_, V_ = X[0][:, h, :], X[1][:, h, :], X[2][:, h, :]
                p = nxt("pp", pp)
                b.op("pe", lambda e: e.matmul(p[0:64, 0:TG], lhsT=w2s[:, hs], rhs=xs[:, 0, :], start=True, stop=True), reads=[w2s, xs], writes=[p])
                b.op("act", lambda e: e.activation(out=T["lw"][:], in_=p[0:64, 0:TG], func=AF.Sigmoid, bias=w0[:, h:h + 1]), reads=[p, w0], writes=[T["lw"]])
                b.op("pool", lambda e: e.tensor_scalar_mul(out=T["lw"][:], in0=T["lw"][:], scalar1=-0.6065306597126334), reads=[T["lw"]], writes=[T["lw"]])
                p = nxt("pp", pp)
                b.op("pe", lambda e: e.matmul(p[0:64, 0:TG], lhsT=a2s[:, hs], rhs=xs[:, 1, :], start=True, stop=True), reads=[a2s, xs], writes=[p])
                b.op("act", lambda e: e.activation(out=T["as"][:], in_=p[0:64, 0:TG], func=AF.Sigmoid, bias=a0[:, h:h + 1]), reads=[p, a0], writes=[T["as"]])
                b.op("dve", lambda e: e.tensor_scalar_mul(out=T["kk"][:], in0=K_, scalar1=k_k[:, h:h + 1]), reads=[X[1], k_k], writes=[T["kk"]])
                tt("pool", T["sq"][:], T["kk"][:], T["kk"][:], ALU.mult, [T["kk"]], [T["sq"]])
                p = nxt("pp", pp)
                b.op("pe", lambda e: e.matmul(p[0:64, 0:TG], lhsT=ones[:], rhs=T["sq"][:], start=True, stop=True), reads=[ones, T["sq"]], writes=[p])
                b.op("act", lambda e: e.activation(out=T["sq"][:], in_=p[0:64, 0:TG], func=AF.Sqrt), reads=[p], writes=[T["sq"]])
                b.op("dve", lambda e: e.tensor_scalar_max(out=T["sq"][:], in0=T["sq"][:], scalar1=1e-12), reads=[T["sq"]], writes=[T["sq"]])
                b.op("dve", lambda e: e.reciprocal(out=T["sq"][:], in_=T["sq"][:]), reads=[T["sq"]], writes=[T["sq"]])
                tt("dve", T["kkn"][:], T["kk"][:], T["sq"][:], ALU.mult, [T["kk"], T["sq"]], [T["kkn"]])
                tt("pool", T["bv"][:], T["kkn"][:], T["as"][:], ALU.mult, [T["kkn"], T["as"]], [T["bv"]])
                b.op("dve", lambda e: e.tensor_scalar(out=T["t1"][:], in0=T["as"][:], scalar1=-1.0, scalar2=k_a[:, h:h + 1], op0=ALU.add, op1=ALU.mult),
                     reads=[T["as"], k_a], writes=[T["t1"]])
                b.op("dve", lambda e: e.scalar_tensor_tensor(out=T["kp"][:], in0=T["t1"][:], scalar=1.0, in1=K_, op0=ALU.add, op1=ALU.mult),
                     reads=[T["t1"], X[1]], writes=[T["kp"]])
                tt("pool", T["rk"][:], R_, T["kp"][:], ALU.mult, [X[0], T["kp"]], [T["rk"]])
                b.op("pool", lambda e: e.tensor_scalar_mul(out=T["rk"][:], in0=T["rk"][:], scalar1=r_k[:, h:h + 1]), reads=[T["rk"], r_k], writes=[T["rk"]])
                p = nxt("pp", pp)
                b.op("pe", lambda e: e.matmul(p[0:64, 0:TG], lhsT=ones[:], rhs=T["rk"][:], start=True, stop=True), reads=[ones, T["rk"]], writes=[p])
                tt("dve", BV[:, h, :], p[0:64, 0:TG], V_, ALU.mult, [p, X[2]], [BV])
                b.op("dve", lambda e: e.tensor_tensor_scan(out=T["L"][:], data0=rstm[:], data1=T["lw"][:], initial=0.0, op0=ALU.mult, op1=ALU.add),
                     reads=[rstm, T["lw"]], writes=[T["L"]])
                tt("pool", T["Lx"][:], T["L"][:], T["lw"][:], ALU.subtract, [T["L"], T["lw"]], [T["Lx"]])
                b.op("act", lambda e: e.activation(out=T["Ep"][:], in_=T["L"][:], func=AF.Exp), reads=[T["L"]], writes=[T["Ep"]])
                b.op("act", lambda e: e.activation(out=T["Em"][:], in_=T["L"][:], func=AF.Exp, scale=-1.0), reads=[T["L"]], writes=[T["Em"]])
                b.op("act", lambda e: e.activation(out=T["Ex"][:], in_=T["Lx"][:], func=AF.Exp), reads=[T["Lx"]], writes=[T["Ex"]])
                c3 = lambda ap: ap.rearrange("p (c t) -> p c t", t=64)
                b.op("dve", lambda e: e.scalar_tensor_tensor(out=AR[:, :, 0, :], in0=c3(T["kkn"][:]), scalar=-1.0, in1=c3(T["Ex"][:]), op0=ALU.mult, op1=ALU.mult),
                     reads=[T["kkn"], T["Ex"]], writes=[AR])
                tt("pool", AR[:, :, 1, :], c3(R_), c3(T["Ep"][:]), ALU.mult, [X[0], T["Ep"]], [AR])
                tt("dve", T["BT"][:], T["bv"][:], T["Em"][:], ALU.mult, [T["bv"], T["Em"]], [T["BT"]])
                tt("pool", T["KT"][:], T["kp"][:], T["Em"][:], ALU.mult, [T["kp"], T["Em"]], [T["KT"]])
                gC = c3(T["Ep"][:])[:, :, 63:64].to_broadcast([64, NCH, 64])
                tt("dve", c3(T["BG"][:]), c3(T["BT"][:]), gC, ALU.mult, [T["BT"], T["Ep"]], [T["BG"]])
                tt("pool", c3(T["KG"][:]), c3(T["KT"][:]), gC, ALU.mult, [T["KT"], T["Ep"]], [T["KG"]])
                for c in range(NCH):
                    cs = slice(c * 64, (c + 1) * 64)
                    Hc = Hs[h][(gi * NCH + c) % 2]
                    Hn = Hs[h][(gi * NCH + c + 1) % 2]
                    p = nxt("pq", pq)
                    for j, (src, sb_) in enumerate([(V_[:, cs], X[2]), (T["BG"][:, cs], T["BG"]), (T["KG"][:, cs], T["KG"])]):
                        b.op("pe", lambda e: e.transpose(out=p[0:64, j * 64:(j + 1) * 64], in_=src, identity=idf[0:64, 0:64]), reads=[sb_, idf], writes=[p])
                    tm = nxt("tm", TM)
                    b.op("act", lambda e: e.copy(out=tm[:].rearrange("p a b -> p (a b)"), in_=p[0:64, 0:192]), reads=[p], writes=[tm])
                    p = nxt("pq", pq)
                    arc = AR[:, c, :, :].rearrange("p a t -> p (a t)")
                    b.op("pe", lambda e: e.matmul(p[0:64, 0:128], lhsT=T["BT"][:, cs], rhs=arc, start=True, stop=True), reads=[T["BT"], AR], writes=[p])
                    b.op("pe", lambda e: e.matmul(p[0:64, 128:256], lhsT=T["KT"][:, cs], rhs=arc, start=True, stop=True), reads=[T["KT"], AR], writes=[p])
                    b.op("pe", lambda e: e.matmul(p[0:64, 256:320], lhsT=AR[:, c, 0, :], rhs=T["BT"][:, cs], start=True, stop=True), reads=[T["BT"], AR], writes=[p])
                    xm = nxt("xm", XM)
                    tt("dve", xm[:].rearrange("p (a m) t -> p a m t", a=2), p[0:64, 0:256].rearrange("p (a m t) -> p a m t", a=2, m=2),
                       msk[:, None, 0:2, :].to_broadcast([64, 2, 2, 64]), ALU.mult, [p, msk], [xm])
                    aa = nxt("aa", AA)
                    b.op("pool", lambda e: e.tensor_copy(out=aa[:, 0, :], in_=xm[:, 0, :]), reads=[xm], writes=[aa])
                    tt("dve", aa[:, 1, :], p[0:64, 256:320], msk[:, 2, :], ALU.mult, [p, msk], [aa])
                    P_ = nxt("ppb", PP)
                    tt("pool", P_[:], xm[:, 0, :], idf[0:64, 0:64], ALU.add, [xm, idf], [P_])
                    for step in range(5):
                        pdb = nxt("pd", pd)
                        b.op("pe", lambda e: e.matmul(pdb[0:64, 0:64], lhsT=aa[:, 1, :], rhs=aa[:, 0, :], start=True, stop=True), reads=[aa], writes=[pdb])
                        b.op("pe", lambda e: e.matmul(pdb[0:64, 64:128], lhsT=aa[:, 0, :], rhs=aa[:, 1, :], start=True, stop=True), reads=[aa], writes=[pdb])
                        aa2 = nxt("aa", AA)
                        b.op("act", lambda e: e.copy(out=aa2[:].rearrange("p a t -> p (a t)"), in_=pdb[0:64, 0:128]), reads=[pdb], writes=[aa2])
                        b.op("pe", lambda e: e.matmul(pdb[0:64, 128:192], lhsT=aa2[:, 1, :], rhs=P_[:], start=True, stop=True), reads=[aa2, P_], writes=[pdb])
                        P2 = nxt("ppb", PP)
                        tt("dve", P2[:], pdb[0:64, 128:192], P_[:], ALU.add, [pdb, P_], [P2])
                        aa, P_ = aa2, P2
                    b.op("pe", lambda e: e.matmul(pz[0:64, 0:64], lhsT=xm[:, 2, :], rhs=tm[:, 0, :], start=True, stop=False), reads=[xm, tm], writes=[pz])
                    b.op("pe", lambda e: e.matmul(pz[0:64, 0:64], lhsT=AR[:, c, 0, :], rhs=Hc[:], start=False, stop=True), reads=[AR, Hc], writes=[pz])
                    b.op("act", lambda e: e.copy(out=Xs[:], in_=pz[0:64, 0:64]), reads=[pz], writes=[Xs])
                    b.op("pe", lambda e: e.matmul(pz[0:64, 64:128], lhsT=P_[:], rhs=Xs[:], start=True, stop=True), reads=[P_, Xs], writes=[pz])
                    b.op("act", lambda e: e.copy(out=Us[:], in_=pz[0:64, 64:128]), reads=[pz], writes=[Us])
                    b.op("pe", lambda e: e.matmul(pz[0:64, 128:192], lhsT=AR[:, c, 1, :], rhs=Hc[:], start=True, stop=False), reads=[AR, Hc], writes=[pz])
                    b.op("pe", lambda e: e.matmul(pz[0:64, 128:192], lhsT=xm[:, 1, :], rhs=Us[:], start=False, stop=False), reads=[xm, Us], writes=[pz])
                    b.op("pe", lambda e: e.matmul(pz[0:64, 128:192], lhsT=xm[:, 3, :], rhs=tm[:, 0, :], start=False, stop=True), reads=[xm, tm], writes=[pz])
                    b.op("pe", lambda e: e.matmul(pz[0:64, 192:256], lhsT=tm[:, 1, :], rhs=Us[:], start=True, stop=False), reads=[tm, Us], writes=[pz])
                    b.op("pe", lambda e: e.matmul(pz[0:64, 192:256], lhsT=tm[:, 2, :], rhs=tm[:, 0, :], start=False, stop=True), reads=[tm], writes=[pz])
                    b.op("act", lambda e: e.copy(out=Ytm[:, c, h, :], in_=pz[0:64, 128:192]), reads=[pz], writes=[Ytm])
                    b.op("dve", lambda e: e.scalar_tensor_tensor(out=Hn[:], in0=Hc[:], scalar=T["Ep"][:, c * 64 + 63:c * 64 + 64], in1=pz[0:64, 192:256],
                                                                 op0=ALU.mult, op1=ALU.add), reads=[Hc, T["Ep"], pz], writes=[Hn])
            Y3 = Ytm[:].rearrange("p c h i -> p (c h) i")
            S3 = sqv[:].rearrange("p c h i -> p (c h) i")
            b.op("dve", lambda e: e.tensor_reduce(out=st1[:], in_=Y3, axis=AX.X, op=ALU.add), reads=[Ytm], writes=[st1])
            b.op("pool", lambda e: e.tensor_scalar_mul(out=st1[:], in0=st1[:], scalar1=1.0 / 64), reads=[st1], writes=[st1])
            tt("dve", Y3, Y3, st1[:].unsqueeze(2).to_broadcast([64, NCH * 8, 64]), ALU.subtract, [Ytm, st1], [Ytm])
            tt("pool", S3, Y3, Y3, ALU.mult, [Ytm], [sqv])
            b.op("dve", lambda e: e.tensor_reduce(out=st2[:], in_=S3, axis=AX.X, op=ALU.add), reads=[sqv], writes=[st2])
            b.op("act", lambda e: e.activation(out=st2[:], in_=st2[:], func=AF.Sqrt, scale=1.0 / 64, bias=64e-5), reads=[st2], writes=[st2])
            b.op("dve", lambda e: e.reciprocal(out=st2[:], in_=st2[:]), reads=[st2], writes=[st2])
            tt("dve", Y3, Y3, st2[:].unsqueeze(2).to_broadcast([64, NCH * 8, 64]), ALU.mult, [Ytm, st2], [Ytm])
            lg = lng[:].rearrange("p (h i) -> p h i", i=64)[:, None, :, :].to_broadcast([64, NCH, 8, 64])
            lb = lnb[:].rearrange("p (h i) -> p h i", i=64)[:, None, :, :].to_broadcast([64, NCH, 8, 64])
            tt("pool", Ytm[:], Ytm[:], lg, ALU.mult, [Ytm, lng], [Ytm])
            tt("dve", Ytm[:], Ytm[:], lb, ALU.add, [Ytm, lnb], [Ytm])
            for h in range(8):
                p = nxt("pq", pq)
                for c in range(NCH):
                    b.op("pe", lambda e: e.transpose(out=p[0:64, c * 64:(c + 1) * 64], in_=Ytm[:, c, h, :], identity=idf[0:64, 0:64]), reads=[Ytm, idf], writes=[p])
                tt("dve", otmp[:], p[0:64, 0:TG], BV[:, h, :], ALU.add, [p, BV], [otmp])
                pg_ = nxt("pp", pp)
                b.op("pe", lambda e: e.matmul(pg_[0:64, 0:TG], lhsT=g2s[:, 0, h * 64:(h + 1) * 64], rhs=xs[:, 2, :], start=True, stop=False), reads=[g2s, xs], writes=[pg_])
                b.op("pe", lambda e: e.matmul(pg_[0:64, 0:TG], lhsT=g2s[:, 1, h * 64:(h + 1) * 64], rhs=xs[:, 3, :], start=False, stop=True), reads=[g2s, xs], writes=[pg_])
                ob_ = obf[h % 2]
                tt("dve", ob_[:], otmp[:], pg_[0:64, 0:TG], ALU.mult, [otmp, pg_], [ob_])
                b.dma("pool", self.obT_d[h // 2, (h % 2) * 64:(h % 2) * 64 + 64, q0:q0 + TG], ob_[:], reads=[ob_], writes=[self.obT_d])
        if "rwkv" in self.debug:
            d = self.dbg_out("obT", [4, 128, S], BF16)
            b.dma("pool", d, self.obT_d[:], reads=[self.obT_d])


Prog.phase_rwkv = _phase_rwkv


def build_full():
    p = Prog()
    b = p.b
    p.alloc_root()
    with b.scope():
        p.alloc_persistent()
        p.phase_nsa_proj()
        p.phase_attn2()
    p.phase_rwkv3()
    p.phase_merge()
    p.phase_ffn2()
    p.finish()
    return p


def kernel(**inputs):
    p = build_full()
    consts = host_consts(inputs["rel_bias"])
    shared = {k: np.ascontiguousarray(np.asarray(inputs[k], np.float32)) for k in W_SPECS if k != "x"}
    shared.update(consts)
    x = np.asarray(inputs["x"], np.float32)
    in_maps = []
    for c in range(8):
        m = dict(shared)
        m["x"] = np.ascontiguousarray(x[c])
        in_maps.append(m)
    res = run_bass_kernel_spmd(p.nc, in_maps, core_ids=list(range(8)))
    return np.stack([np.asarray(r["out"], np.float32) for r in res.results], axis=0)


def _phase_rwkv2(self):
    b = self.b
    I = self.inp
    TG = 128
    NCH = 2
    tt = lambda eng, out, in0, in1, op, rd, wr: b.op(eng, lambda e: e.tensor_tensor(out=out, in0=in0, in1=in1, op=op), reads=rd, writes=wr)
    with b.scope():
        W1 = b.sb("W1", [128, 8, 1792], BF16)
        W2 = b.sb("W2", [128, 8, 1792], BF16)
        with b.scope():
            gat = self.load_gain("gat3", I["attn_norm_g"][0])
            stage = [b.sb(f"rst{i}", [128, 1792], F32) for i in range(2)]
            tmpw = [b.sb(f"rtw{i}", [128, 1792], F32) for i in range(2)]
            mur = self.bcast_row("mur", I["rwkv_mu"][0], 1792)
            for c in range(8):
                st = stage[c % 2]
                tw_ = tmpw[c % 2]
                b.dma("sp", st[:], I["w_in"][0][c * 128:(c + 1) * 128, RW0:RW0 + 1792], writes=[st])
                tt("dve", tw_[:], st[:], mur[:], ALU.mult, [st, mur], [tw_])
                b.op("act", lambda e: e.activation(out=W2[:, c, :], in_=tw_[:], func=AF.Copy, scale=gat[:, c:c + 1]), reads=[tw_, gat], writes=[W2])
                tt("pool", st[:], st[:], tw_[:], ALU.subtract, [st, tw_], [st])
                b.op("act", lambda e: e.activation(out=W1[:, c, :], in_=st[:], func=AF.Copy, scale=gat[:, c:c + 1]), reads=[st, gat], writes=[W1])

        def colvec(name, src, n):
            t = b.sb(name, [64, n], F32)
            b.dma("sp", t[:], src.rearrange("(c p) -> p c", p=64), writes=[t], allow_slow_non_contiguous=True)
            return t
        w0 = colvec("w0", I["rwkv_w0"][0], 8)
        a0 = colvec("a0", I["rwkv_a0"][0], 8)
        k_k = colvec("k_k", I["rwkv_k_k"][0], 8)
        k_a = colvec("k_a", I["rwkv_k_a"][0], 8)
        r_k = colvec("r_k", I["rwkv_r_k"][0].rearrange("h d -> (h d)"), 8)
        w2s = b.sb("w2s", [64, 512], F32)
        a2s = b.sb("a2s", [64, 512], F32)
        g2s = b.sb("g2s", [64, 2, 512], F32)
        b.dma("sp", w2s[:], I["rwkv_w2"][0], writes=[w2s])
        b.dma("sp", a2s[:], I["rwkv_a2"][0], writes=[a2s])
        b.dma("sp", g2s[:], I["rwkv_g2"][0].rearrange("(two l) f -> l two f", two=2), writes=[g2s])
        lng = b.sb("lng", [64, 512], F32)
        lnb = b.sb("lnb", [64, 512], F32)
        b.dma("sp", lng[:], I["rwkv_ln_g"][0].partition_broadcast(64), writes=[lng])
        b.dma("sp", lnb[:], I["rwkv_ln_b"][0].partition_broadcast(64), writes=[lnb])
        msk = b.sb("rmsk", [64, 3, 64], F32)
        b.dma("sp", msk[:], I["rwmask"], writes=[msk])
        rstm = b.sb("rstm", [64, 8 * TG], F32)
        b.dma("sp", rstm[:], I["rwreset"], writes=[rstm])
        ones = b.sb("ones64", [64, 64], F32)
        b.op("pool", lambda e: e.memset(ones[:], 1.0), writes=[ones])
        idf = self.identf
        Hst = b.sb("rH", [64, 2, 8, 64], F32)
        b.op("pool", lambda e: e.memset(Hst[:], 0.0), writes=[Hst])
        xt = [b.sb(f"rxt{i}", [128, D], F32) for i in range(1)] * 2
        junk = b.sb("rjunk", [128, D], BF16)
        ss = [b.sb(f"rss{i}", [128, 1], F32) for i in range(1)] * 2
        hb = [b.sb(f"rhb{i}", [128, D], BF16) for i in range(1)] * 2
        hT1 = [b.sb(f"rhT{i}", [128, 8, 128], BF16) for i in range(1)] * 2
        hTs = b.sb("rhTs", [128, 8, TG + 1], BF16)
        b.op("pool", lambda e: e.memset(hTs[:], 0.0), writes=[hTs])
        XL = b.sb("rXL", [64, 20, TG], F32)
        Vtm = b.sb("rVtm", [64, NCH, 512], F32)
        names = ["LW", "AS", "KKN", "BVc", "KP", "RK", "L", "EP", "EM", "BG", "KG"]
        T = {n: b.sb("r" + n, [64, 8, TG], F32) for n in names}
        T["NR"] = T["RK"]
        T["T1"] = T["BG"]
        T["KK"] = T["KG"]
        T["EX"] = T["L"]
        T["BT"] = T["LW"]
        T["KT"] = T["AS"]
        AR = b.sb("rAR", [64, 8, NCH, 2, 64], F32)
        BON = b.sb("rBON", [64, NCH * 8], F32)
        Ytm = b.sb("rYtm", [64, NCH, 8, 64], F32)
        sqv = b.sb("rsqv", [64, NCH, 8, 64], F32)
        st1 = b.sb("rst1", [64, NCH * 8], F32)
        st2 = b.sb("rst2", [64, NCH * 8], F32)
        TM4 = [b.sb(f"rTM{i}", [64, 4, 2, 64], F32) for i in range(2)]
        XM4 = [b.sb(f"rXM{i}", [64, 4, 4, 64], F32) for i in range(2)]
        AA4 = [b.sb(f"rAA{i}", [64, 4, 2, 64], F32) for i in range(2)]
        PP4 = [b.sb(f"rPP{i}", [64, 4, 64], F32) for i in range(2)]
        Xs4 = b.sb("rXs4", [64, 4, 64], F32)
        Us4 = b.sb("rUs4", [64, 4, 64], F32)
        Ht4 = b.sb("rHt4", [64, 4, 64], F32)
        OBb = b.sb("rOBb", [64, NCH, 512], BF16)
        obT = [b.sb(f"robT{i}", [128, 4, TG], BF16) for i in range(1)] * 2
        pt = b.ps("rpt", [128, 8, 128], BF16)
        pP = b.ps("rpP", [128, 512], F32)
        pA = b.ps("rpA", [128, 1024], F32)
        pB = b.ps("rpB", [128, 512], F32)
        pC = b.ps("rpC", [128, 512], F32)
        pD = b.ps("rpD", [128, 512], F32)
        pZ = b.ps("rpZ", [128, 512], F32)
        cnt = {}

        def nxt(k, lst):
            cnt[k] = cnt.get(k, 0) + 1
            return lst[cnt[k] % len(lst)]
        bc = lambda v: v[:].unsqueeze(2).to_broadcast([64, 8, TG])
        f2 = lambda t_: t_[:].rearrange("p h t -> p (h t)")
        c16 = lambda t_: t_[:].rearrange("p h (c t) -> p (h c) t", t=64)

        ngr = getattr(self, "nrg_limit", S // TG)
        for gi in range(ngr):
            q0 = gi * TG
            i = gi % 2
            self.make_hT(I["x"], gi, xt[i], junk, ss[i], hb[i], pt, hT1[i], self.ident)
            b.op("pool", lambda e: e.tensor_copy(out=hTs[:, :, 0:1], in_=hTs[:, :, TG:TG + 1]), reads=[hTs], writes=[hTs])
            b.op("pool", lambda e: e.tensor_copy(out=hTs[:, :, 1:TG + 1], in_=hT1[i][:]), reads=[hT1[i]], writes=[hTs])
            ftiles = list(range(0, 16)) + [24, 25, 26, 27]
            for q4 in range(5):
                for j in range(4):
                    fc = ftiles[q4 * 4 + j]
                    for c in range(8):
                        b.op("pe", lambda e: e.matmul(pP[0:64, j * TG:(j + 1) * TG], lhsT=W1[:, c, fc * 64:(fc + 1) * 64], rhs=hTs[:, c, 1:TG + 1], start=(c == 0), stop=False),
                             reads=[W1, hTs], writes=[pP])
                    for c in range(8):
                        b.op("pe", lambda e: e.matmul(pP[0:64, j * TG:(j + 1) * TG], lhsT=W2[:, c, fc * 64:(fc + 1) * 64], rhs=hTs[:, c, 0:TG], start=False, stop=(c == 7)),
                             reads=[W2, hTs], writes=[pP])
                b.op("act", lambda e: e.copy(out=XL[:, q4 * 4:(q4 + 1) * 4, :].rearrange("p a t -> p (a t)"), in_=pP[0:64, :]), reads=[pP], writes=[XL])
            for c_ in range(NCH):
                for c in range(8):
                    b.op("pe", lambda e: e.matmul(pP[0:64, :], lhsT=hTs[:, c, 1 + c_ * 64:1 + (c_ + 1) * 64], rhs=W1[:, c, 1024:1536], start=(c == 0), stop=False),
                         reads=[W1, hTs], writes=[pP])
                for c in range(8):
                    b.op("pe", lambda e: e.matmul(pP[0:64, :], lhsT=hTs[:, c, c_ * 64:(c_ + 1) * 64], rhs=W2[:, c, 1024:1536], start=False, stop=(c == 7)),
                         reads=[W2, hTs], writes=[pP])
                b.op("act", lambda e: e.copy(out=Vtm[:, c_, :], in_=pP[0:64, :]), reads=[pP], writes=[Vtm])
            R_ = XL[:, 0:8, :]
            K_ = XL[:, 8:16, :]
            b.op("act", lambda e: e.activation(out=XL[:, 16, :], in_=XL[:, 16, :], func=AF.Tanh), reads=[XL], writes=[XL])
            b.op("act", lambda e: e.activation(out=XL[:, 18:20, :], in_=XL[:, 18:20, :], func=AF.Sigmoid), reads=[XL], writes=[XL])
            for (ws_, src, bias_, dst) in [(w2s, 16, w0, "LW"), (a2s, 17, a0, "AS")]:
                for half in range(2):
                    for j in range(4):
                        h = half * 4 + j
                        b.op("pe", lambda e: e.matmul(pP[0:64, j * TG:(j + 1) * TG], lhsT=ws_[:, h * 64:(h + 1) * 64], rhs=XL[:, src, :], start=True, stop=True),
                             reads=[ws_, XL], writes=[pP])
                    for j in range(4):
                        h = half * 4 + j
                        b.op("act", lambda e: e.activation(out=T[dst][:, h, :], in_=pP[0:64, j * TG:(j + 1) * TG], func=AF.Sigmoid, bias=bias_[:, h:h + 1]),
                             reads=[pP, bias_], writes=[T[dst]])
            b.op("pool", lambda e: e.tensor_scalar_mul(out=f2(T["LW"]), in0=f2(T["LW"]), scalar1=-0.6065306597126334), reads=[T["LW"]], writes=[T["LW"]])
            tt("dve", T["KK"][:], K_, bc(k_k), ALU.mult, [XL, k_k], [T["KK"]])
            tt("pool", T["NR"][:], T["KK"][:], T["KK"][:], ALU.mult, [T["KK"]], [T["NR"]])
            for half in range(2):
                b.op("pe", lambda e: e.matmul(pP[0:64, :], lhsT=ones[:], rhs=T["NR"][:, half * 4:(half + 1) * 4, :].rearrange("p h t -> p (h t)"), start=True, stop=True),
                     reads=[ones, T["NR"]], writes=[pP])
                b.op("act", lambda e: e.activation(out=T["KKN"][:, half * 4:(half + 1) * 4, :].rearrange("p h t -> p (h t)"), in_=pP[0:64, :], func=AF.Sqrt),
                     reads=[pP], writes=[T["KKN"]])
            b.op("dve", lambda e: e.tensor_scalar_max(out=f2(T["KKN"]), in0=f2(T["KKN"]), scalar1=1e-12), reads=[T["KKN"]], writes=[T["KKN"]])
            b.op("dve", lambda e: e.reciprocal(out=f2(T["KKN"]), in_=f2(T["KKN"])), reads=[T["KKN"]], writes=[T["KKN"]])
            tt("dve", T["KKN"][:], T["KKN"][:], T["KK"][:], ALU.mult, [T["KKN"], T["KK"]], [T["KKN"]])
            tt("pool", T["BVc"][:], T["KKN"][:], T["AS"][:], ALU.mult, [T["KKN"], T["AS"]], [T["BVc"]])
            b.op("pool", lambda e: e.tensor_scalar_add(out=f2(T["T1"]), in0=f2(T["AS"]), scalar1=-1.0), reads=[T["AS"]], writes=[T["T1"]])
            tt("pool", T["T1"][:], T["T1"][:], bc(k_a), ALU.mult, [T["T1"], k_a], [T["T1"]])
            b.op("dve", lambda e: e.scalar_tensor_tensor(out=f2(T["KP"]), in0=f2(T["T1"]), scalar=1.0, in1=K_.rearrange("p h t -> p (h t)"), op0=ALU.add, op1=ALU.mult),
                 reads=[T["T1"], XL], writes=[T["KP"]])
            tt("pool", T["RK"][:], R_, T["KP"][:], ALU.mult, [XL, T["KP"]], [T["RK"]])
            tt("pool", T["RK"][:], T["RK"][:], bc(r_k), ALU.mult, [T["RK"], r_k], [T["RK"]])
            for c_ in range(NCH):
                for h in range(8):
                    b.op("pe", lambda e: e.matmul(pD[0:64, c_ * 8 + h:c_ * 8 + h + 1], lhsT=T["RK"][:, h, c_ * 64:(c_ + 1) * 64], rhs=ones[:, 0:1], start=True, stop=True),
                         reads=[T["RK"], ones], writes=[pD])
            b.op("act", lambda e: e.copy(out=BON[:], in_=pD[0:64, 0:NCH * 8]), reads=[pD], writes=[BON])
            b.op("dve", lambda e: e.tensor_tensor_scan(out=f2(T["L"]), data0=rstm[:], data1=f2(T["LW"]), initial=0.0, op0=ALU.mult, op1=ALU.add),
                 reads=[rstm, T["LW"]], writes=[T["L"]])
            b.op("act", lambda e: e.activation(out=f2(T["EP"]), in_=f2(T["L"]), func=AF.Exp), reads=[T["L"]], writes=[T["EP"]])
            b.op("act", lambda e: e.activation(out=f2(T["EM"]), in_=f2(T["L"]), func=AF.Exp, scale=-1.0), reads=[T["L"]], writes=[T["EM"]])
            tt("pool", T["L"][:], T["L"][:], T["LW"][:], ALU.subtract, [T["L"], T["LW"]], [T["L"]])
            b.op("act", lambda e: e.activation(out=f2(T["EX"]), in_=f2(T["L"]), func=AF.Exp), reads=[T["L"]], writes=[T["EX"]])
            ar0 = AR[:, :, :, 0, :].rearrange("p h c t -> p (h c) t")
            ar1 = AR[:, :, :, 1, :].rearrange("p h c t -> p (h c) t")
            b.op("dve", lambda e: e.scalar_tensor_tensor(out=ar0, in0=c16(T["KKN"]), scalar=-1.0, in1=c16(T["EX"]), op0=ALU.mult, op1=ALU.mult),
                 reads=[T["KKN"], T["EX"]], writes=[AR])
            tt("pool", ar1, R_.rearrange("p h (c t) -> p (h c) t", t=64), c16(T["EP"]), ALU.mult, [XL, T["EP"]], [AR])
            tt("dve", T["BT"][:], T["BVc"][:], T["EM"][:], ALU.mult, [T["BVc"], T["EM"]], [T["BT"]])
            tt("pool", T["KT"][:], T["KP"][:], T["EM"][:], ALU.mult, [T["KP"], T["EM"]], [T["KT"]])
            gC = c16(T["EP"])[:, :, 63:64].to_broadcast([64, 16, 64])
            tt("dve", c16(T["BG"]), c16(T["BT"]), gC, ALU.mult, [T["BT"], T["EP"]], [T["BG"]])
            tt("pool", c16(T["KG"]), c16(T["KT"]), gC, ALU.mult, [T["KT"], T["EP"]], [T["KG"]])
            for c_ in range(NCH):
                cs = slice(c_ * 64, (c_ + 1) * 64)
                cur = (gi * NCH + c_) % 2
                for hb_ in range(2):
                    heads = list(range(hb_ * 4, hb_ * 4 + 4))
                    for j, h in enumerate(heads):
                        b.op("pe", lambda e: e.transpose(out=pC[0:64, j * 128:j * 128 + 64], in_=T["BG"][:, h, cs], identity=idf[0:64, 0:64]), reads=[T["BG"], idf], writes=[pC])
                        b.op("pe", lambda e: e.transpose(out=pC[0:64, j * 128 + 64:(j + 1) * 128], in_=T["KG"][:, h, cs], identity=idf[0:64, 0:64]), reads=[T["KG"], idf], writes=[pC])
                    tm = nxt("tm", TM4)
                    b.op("act", lambda e: e.copy(out=tm[:].rearrange("p h a t -> p (h a t)"), in_=pC[0:64, 0:512]), reads=[pC], writes=[tm])
                    for j, h in enumerate(heads):
                        arc = AR[:, h, c_, :, :].rearrange("p a t -> p (a t)")
                        b.op("pe", lambda e: e.matmul(pA[0:64, j * 256:j * 256 + 128], lhsT=T["BT"][:, h, cs], rhs=arc, start=True, stop=True), reads=[T["BT"], AR], writes=[pA])
                        b.op("pe", lambda e: e.matmul(pA[0:64, j * 256 + 128:(j + 1) * 256], lhsT=T["KT"][:, h, cs], rhs=arc, start=True, stop=True), reads=[T["KT"], AR], writes=[pA])
                        b.op("pe", lambda e: e.matmul(pB[0:64, j * 64:(j + 1) * 64], lhsT=AR[:, h, c_, 0, :], rhs=T["BT"][:, h, cs], start=True, stop=True), reads=[T["BT"], AR], writes=[pB])
                    xm = nxt("xm", XM4)
                    tt("dve", xm[:].rearrange("p h (a m) t -> p (h a) m t", a=2), pA[0:64, :].rearrange("p (ha m t) -> p ha m t", m=2, t=64),
                       msk[:, None, 0:2, :].to_broadcast([64, 8, 2, 64]), ALU.mult, [pA, msk], [xm])
                    aa = nxt("aa", AA4)
                    b.op("pool", lambda e: e.tensor_copy(out=aa[:, :, 0, :], in_=xm[:, :, 0, :]), reads=[xm], writes=[aa])
                    tt("dve", aa[:, :, 1, :], pB[0:64, 0:256].rearrange("p (h t) -> p h t", t=64), msk[:, 2:3, :].to_broadcast([64, 4, 64]), ALU.mult, [pB, msk], [aa])
                    P_ = nxt("pp4", PP4)
                    tt("pool", P_[:], xm[:, :, 0, :], idf[0:64, None, 0:64].to_broadcast([64, 4, 64]), ALU.add, [xm, idf], [P_])
                    for step in range(5):
                        for j in range(4):
                            b.op("pe", lambda e: e.matmul(pD[0:64, j * 128:j * 128 + 64], lhsT=aa[:, j, 1, :], rhs=aa[:, j, 0, :], start=True, stop=True), reads=[aa], writes=[pD])
                            b.op("pe", lambda e: e.matmul(pD[0:64, j * 128 + 64:(j + 1) * 128], lhsT=aa[:, j, 0, :], rhs=aa[:, j, 1, :], start=True, stop=True), reads=[aa], writes=[pD])
                        aa2 = nxt("aa", AA4)
                        b.op("act", lambda e: e.copy(out=aa2[:].rearrange("p h a t -> p (h a t)"), in_=pD[0:64, :]), reads=[pD], writes=[aa2])
                        for j in range(4):
                            b.op("pe", lambda e: e.matmul(pB[0:64, 256 + j * 64:256 + (j + 1) * 64], lhsT=aa2[:, j, 1, :], rhs=P_[:, j, :], start=True, stop=True), reads=[aa2, P_], writes=[pB])
                        P2 = nxt("pp4", PP4)
                        tt("dve", P2[:], pB[0:64, 256:512].rearrange("p (h t) -> p h t", t=64), P_[:], ALU.add, [pB, P_], [P2])
                        aa, P_ = aa2, P2
                    for j, h in enumerate(heads):
                        b.op("pe", lambda e: e.matmul(pZ[0:64, j * 64:(j + 1) * 64], lhsT=xm[:, j, 2, :], rhs=Vtm[:, c_, h * 64:(h + 1) * 64], start=True, stop=False), reads=[xm, Vtm], writes=[pZ])
                        b.op("pe", lambda e: e.matmul(pZ[0:64, j * 64:(j + 1) * 64], lhsT=AR[:, h, c_, 0, :], rhs=Hst[:, cur, h, :], start=False, stop=True), reads=[AR, Hst], writes=[pZ])
                    b.op("act", lambda e: e.copy(out=Xs4[:].rearrange("p h t -> p (h t)"), in_=pZ[0:64, 0:256]), reads=[pZ], writes=[Xs4])
                    for j in range(4):
                        b.op("pe", lambda e: e.matmul(pZ[0:64, 256 + j * 64:256 + (j + 1) * 64], lhsT=P_[:, j, :], rhs=Xs4[:, j, :], start=True, stop=True), reads=[P_, Xs4], writes=[pZ])
                    b.op("act", lambda e: e.copy(out=Us4[:].rearrange("p h t -> p (h t)"), in_=pZ[0:64, 256:512]), reads=[pZ], writes=[Us4])
                    for j, h in enumerate(heads):
                        o = slice(j * 64, (j + 1) * 64)
                        vh = Vtm[:, c_, h * 64:(h + 1) * 64]
                        b.op("pe", lambda e: e.matmul(pZ[0:64, o], lhsT=AR[:, h, c_, 1, :], rhs=Hst[:, cur, h, :], start=True, stop=False), reads=[AR, Hst], writes=[pZ])
                        b.op("pe", lambda e: e.matmul(pZ[0:64, o], lhsT=xm[:, j, 1, :], rhs=Us4[:, j, :], start=False, stop=False), reads=[xm, Us4], writes=[pZ])
                        b.op("pe", lambda e: e.matmul(pZ[0:64, o], lhsT=xm[:, j, 3, :], rhs=vh, start=False, stop=True), reads=[xm, Vtm], writes=[pZ])
                    for j, h in enumerate(heads):
                        o = slice(256 + j * 64, 256 + (j + 1) * 64)
                        vh = Vtm[:, c_, h * 64:(h + 1) * 64]
                        b.op("pe", lambda e: e.matmul(pZ[0:64, o], lhsT=tm[:, j, 0, :], rhs=Us4[:, j, :], start=True, stop=False), reads=[tm, Us4], writes=[pZ])
                        b.op("pe", lambda e: e.matmul(pZ[0:64, o], lhsT=tm[:, j, 1, :], rhs=vh, start=False, stop=True), reads=[tm, Vtm], writes=[pZ])
                    b.op("act", lambda e: e.copy(out=Ytm[:, c_, hb_ * 4:(hb_ + 1) * 4, :].rearrange("p h t -> p (h t)"), in_=pZ[0:64, 0:256]), reads=[pZ], writes=[Ytm])
                    gH = T["EP"][:, hb_ * 4:(hb_ + 1) * 4, c_ * 64 + 63:c_ * 64 + 64].to_broadcast([64, 4, 64])
                    tt("pool", Ht4[:], Hst[:, cur, hb_ * 4:(hb_ + 1) * 4, :], gH, ALU.mult, [Hst, T["EP"]], [Ht4])
                    tt("dve", Hst[:, 1 - cur, hb_ * 4:(hb_ + 1) * 4, :], pZ[0:64, 256:512].rearrange("p (h t) -> p h t", t=64), Ht4[:], ALU.add, [pZ, Ht4], [Hst])
            Y3 = Ytm[:].rearrange("p c h i -> p (c h) i")
            S3 = sqv[:].rearrange("p c h i -> p (c h) i")
            b.op("dve", lambda e: e.tensor_reduce(out=st1[:], in_=Y3, axis=AX.X, op=ALU.add), reads=[Ytm], writes=[st1])
            b.op("pool", lambda e: e.tensor_scalar_mul(out=st1[:], in0=st1[:], scalar1=1.0 / 64), reads=[st1], writes=[st1])
            tt("dve", Y3, Y3, st1[:].unsqueeze(2).to_broadcast([64, NCH * 8, 64]), ALU.subtract, [Ytm, st1], [Ytm])
            tt("pool", S3, Y3, Y3, ALU.mult, [Ytm], [sqv])
            b.op("dve", lambda e: e.tensor_reduce(out=st2[:], in_=S3, axis=AX.X, op=ALU.add), reads=[sqv], writes=[st2])
            b.op("act", lambda e: e.activation(out=st2[:], in_=st2[:], func=AF.Sqrt, scale=1.0 / 64, bias=64e-5), reads=[st2], writes=[st2])
            b.op("dve", lambda e: e.reciprocal(out=st2[:], in_=st2[:]), reads=[st2], writes=[st2])
            tt("dve", Y3, Y3, st2[:].unsqueeze(2).to_broadcast([64, NCH * 8, 64]), ALU.mult, [Ytm, st2], [Ytm])
            lg = lng[:].rearrange("p (h i) -> p h i", i=64)[:, None, :, :].to_broadcast([64, NCH, 8, 64])
            lb = lnb[:].rearrange("p (h i) -> p h i", i=64)[:, None, :, :].to_broadcast([64, NCH, 8, 64])
            tt("pool", Ytm[:], Ytm[:], lg, ALU.mult, [Ytm, lng], [Ytm])
            tt("dve", Ytm[:], Ytm[:], lb, ALU.add, [Ytm, lnb], [Ytm])
            V3 = Vtm[:].rearrange("p c (h i) -> p (c h) i", i=64)
            tt("pool", S3, V3, BON[:].unsqueeze(2).to_broadcast([64, NCH * 8, 64]), ALU.mult, [Vtm, BON], [sqv])
            tt("dve", Y3, Y3, S3, ALU.add, [Ytm, sqv], [Ytm])
            for c_ in range(NCH):
                for two in range(2):
                    b.op("pe", lambda e: e.matmul(pP[0:64, :], lhsT=XL[:, 18 + two, c_ * 64:(c_ + 1) * 64], rhs=g2s[:, two, :], start=(two == 0), stop=(two == 1)),
                         reads=[XL, g2s], writes=[pP])
                tt("dve", OBb[:, c_, :], Ytm[:, c_, :, :].rearrange("p h i -> p (h i)"), pP[0:64, :], ALU.mult, [Ytm, pP], [OBb])
                for k4 in range(4):
                    b.op("pe", lambda e: e.transpose(out=pt[:, k4, c_ * 64:(c_ + 1) * 64], in_=OBb[:, c_, k4 * 128:(k4 + 1) * 128], identity=self.ident[0:64, 0:64]),
                         reads=[OBb, self.ident], writes=[pt])
            ot = obT[gi % 2]
            b.op("act", lambda e: e.copy(out=ot[:], in_=pt[:, 0:4, :]), reads=[pt], writes=[ot])
            b.dma("pool", self.obT_d[:, :, q0:q0 + TG].rearrange("c p t -> p c t"), ot[:], reads=[ot], writes=[self.obT_d])
        if "rwkv" in self.debug:
            d = self.dbg_out("obT", [4, 128, S], BF16)
            b.dma("pool", d, self.obT_d[:], reads=[self.obT_d])


Prog.phase_rwkv2 = _phase_rwkv2


def _phase_rwkv3(self):
    b = self.b
    I = self.inp
    TG = 128
    NCH = 2
    tt = lambda eng, out, in0, in1, op, rd, wr: b.op(eng, lambda e: e.tensor_tensor(out=out, in0=in0, in1=in1, op=op), reads=rd, writes=wr)
    with b.scope():
        W1 = b.sb("W1", [128, 8, 1792], BF16)
        with b.scope():
            gat = self.load_gain("gat3", I["attn_norm_g"][0])
            stage = [b.sb(f"rst{i}", [128, 1792], F32) for i in range(2)]
            self.load_weight(W1, I["w_in"][0][:, RW0:RW0 + 1792], 1792, gvec=gat, stage=stage)

        def colvec(name, src, n):
            t = b.sb(name, [64, n], F32)
            b.dma("sp", t[:], src.rearrange("(c p) -> p c", p=64), writes=[t], allow_slow_non_contiguous=True)
            return t
        mu = colvec("mu", I["rwkv_mu"][0], 28)
        w0 = colvec("w0", I["rwkv_w0"][0], 8)
        a0 = colvec("a0", I["rwkv_a0"][0], 8)
        k_k = colvec("k_k", I["rwkv_k_k"][0], 8)
        k_a = colvec("k_a", I["rwkv_k_a"][0], 8)
        r_k = colvec("r_k", I["rwkv_r_k"][0].rearrange("h d -> (h d)"), 8)
        w2s = b.sb("w2s", [64, 512], F32)
        a2s = b.sb("a2s", [64, 512], F32)
        g2s = b.sb("g2s", [64, 2, 512], F32)
        b.dma("sp", w2s[:], I["rwkv_w2"][0], writes=[w2s])
        b.dma("sp", a2s[:], I["rwkv_a2"][0], writes=[a2s])
        b.dma("sp", g2s[:], I["rwkv_g2"][0].rearrange("(two l) f -> l two f", two=2), writes=[g2s])
        lng = b.sb("lng", [64, 512], F32)
        lnb = b.sb("lnb", [64, 512], F32)
        b.dma("sp", lng[:], I["rwkv_ln_g"][0].partition_broadcast(64), writes=[lng])
        b.dma("sp", lnb[:], I["rwkv_ln_b"][0].partition_broadcast(64), writes=[lnb])
        msk = b.sb("rmsk", [64, 3, 64], F32)
        b.dma("sp", msk[:], I["rwmask"], writes=[msk])
        rstm = b.sb("rstm", [64, 8 * TG], F32)
        b.dma("sp", rstm[:], I["rwreset"], writes=[rstm])
        ones = b.sb("ones64", [64, 64], F32)
        b.op("pool", lambda e: e.memset(ones[:], 1.0), writes=[ones])
        idf = self.identf
        Hst = b.sb("rH", [64, 2, 8, 64], F32)
        b.op("pool", lambda e: e.memset(Hst[:], 0.0), writes=[Hst])
        xt = [b.sb(f"rxt{i}", [128, D], F32) for i in range(1)] * 2
        junk = b.sb("rjunk", [128, D], BF16)
        ss = [b.sb(f"rss{i}", [128, 1], F32) for i in range(1)] * 2
        hb = [b.sb(f"rhb{i}", [128, D], BF16) for i in range(1)] * 2
        hT1 = [b.sb(f"rhT{i}", [128, 8, 128], BF16) for i in range(1)] * 2
        PB = b.sb("rPB", [64, 28, TG + 1], F32)
        b.op("pool", lambda e: e.memset(PB[:], 0.0), writes=[PB])
        XL = b.sb("rXL", [64, 28, TG], F32)
        Vtm = b.sb("rVtm", [64, NCH, 512], F32)
        names = ["LW", "AS", "KKN", "BVc", "KP", "RK", "L", "EP", "EM", "BG", "KG"]
        T = {n: b.sb("r" + n, [64, 8, TG], F32) for n in names}
        T["NR"] = T["RK"]
        T["T1"] = T["BG"]
        T["KK"] = T["KG"]
        T["EX"] = T["L"]
        T["BT"] = T["LW"]
        T["KT"] = T["AS"]
        AR = b.sb("rAR", [64, 8, NCH, 2, 64], F32)
        BON = b.sb("rBON", [64, NCH * 8], F32)
        Ytm = b.sb("rYtm", [64, NCH, 8, 64], F32)
        sqv = b.sb("rsqv", [64, NCH, 8, 64], F32)
        st1 = b.sb("rst1", [64, NCH * 8], F32)
        st2 = b.sb("rst2", [64, NCH * 8], F32)
        TM4 = [b.sb(f"rTM{i}", [64, 4, 2, 64], F32) for i in range(2)]
        XM4 = [b.sb(f"rXM{i}", [64, 4, 4, 64], F32) for i in range(2)]
        AA4 = [[b.sb(f"rAA{u}_{i}", [64, 4, 2, 64], F32) for i in range(2)] for u in range(2)]
        PP4 = [[b.sb(f"rPP{u}_{i}", [64, 4, 64], F32) for i in range(2)] for u in range(2)]
        Xs8 = b.sb("rXs8", [64, 8, 64], F32)
        Us8 = b.sb("rUs8", [64, 8, 64], F32)
        Ht8 = b.sb("rHt8", [64, 8, 64], F32)
        OBb = b.sb("rOBb", [64, NCH, 512], BF16)
        obT = [b.sb(f"robT{i}", [128, 4, TG], BF16) for i in range(1)] * 2
        pt = b.ps("rpt", [128, 8, 128], BF16)
        pP = b.ps("rpP", [128, 512], F32)
        pA = b.ps("rpA", [128, 1024], F32)
        pB = b.ps("rpB", [128, 512], F32)
        pC = b.ps("rpC", [128, 512], F32)
        pD = b.ps("rpD", [128, 512], F32)
        pZ = b.ps("rpZ", [128, 512], F32)
        cnt = {}

        def nxt(k, lst):
            cnt[k] = cnt.get(k, 0) + 1
            return lst[cnt[k] % len(lst)]
        bc = lambda v: v[:].unsqueeze(2).to_broadcast([64, 8, TG])
        f2 = lambda t_: t_[:].rearrange("p h t -> p (h t)")
        c16 = lambda t_: t_[:].rearrange("p h (c t) -> p (h c) t", t=64)

        ngr = getattr(self, "nrg_limit", S // TG)

        def emit_inproj(gi):
            i = gi % 2
            self.make_hT(I["x"], gi, xt[i], junk, ss[i], hb[i], pt, hT1[i], self.ident)
            b.op("dve", lambda e: e.tensor_copy(out=PB[:, :, 0:1], in_=PB[:, :, TG:TG + 1]), reads=[PB], writes=[PB])
            for r7 in range(7):
                for j in range(4):
                    fc = r7 * 4 + j
                    for c in range(8):
                        b.op("pe", lambda e: e.matmul(pP[0:64, j * TG:(j + 1) * TG], lhsT=W1[:, c, fc * 64:(fc + 1) * 64], rhs=hT1[i][:, c, :], start=(c == 0), stop=(c == 7)),
                             reads=[W1, hT1[i]], writes=[pP])
                b.op("act", lambda e: e.copy(out=PB[:, r7 * 4:(r7 + 1) * 4, 1:TG + 1], in_=pP[0:64, :].rearrange("p (a t) -> p a t", t=TG)), reads=[pP], writes=[PB])

        emit_inproj(0)
        for gi in range(ngr):
            q0 = gi * TG
            tt("dve", XL[:], PB[:, :, 0:TG], PB[:, :, 1:TG + 1], ALU.subtract, [PB], [XL])
            tt("dve", XL[:], XL[:], mu[:].unsqueeze(2).to_broadcast([64, 28, TG]), ALU.mult, [XL, mu], [XL])
            tt("dve", XL[:], XL[:], PB[:, :, 1:TG + 1], ALU.add, [XL, PB], [XL])
            for c_ in range(NCH):
                for h in range(8):
                    b.op("pe", lambda e: e.transpose(out=pC[0:64, h * 64:(h + 1) * 64], in_=XL[:, 16 + h, c_ * 64:(c_ + 1) * 64], identity=idf[0:64, 0:64]), reads=[XL, idf], writes=[pC])
                b.op("act", lambda e: e.copy(out=Vtm[:, c_, :], in_=pC[0:64, :]), reads=[pC], writes=[Vtm])
            R_ = XL[:, 0:8, :]
            K_ = XL[:, 8:16, :]
            b.op("act", lambda e: e.activation(out=XL[:, 24, :], in_=XL[:, 24, :], func=AF.Tanh), reads=[XL], writes=[XL])
            b.op("act", lambda e: e.activation(out=XL[:, 26:28, :], in_=XL[:, 26:28, :], func=AF.Sigmoid), reads=[XL], writes=[XL])
            for (ws_, src, bias_, dst) in [(w2s, 24, w0, "LW"), (a2s, 25, a0, "AS")]:
                for half in range(2):
                    for j in range(4):
                        h = half * 4 + j
                        b.op("pe", lambda e: e.matmul(pP[0:64, j * TG:(j + 1) * TG], lhsT=ws_[:, h * 64:(h + 1) * 64], rhs=XL[:, src, :], start=True, stop=True),
                             reads=[ws_, XL], writes=[pP])
                    for j in range(4):
                        h = half * 4 + j
                        b.op("act", lambda e: e.activation(out=T[dst][:, h, :], in_=pP[0:64, j * TG:(j + 1) * TG], func=AF.Sigmoid, bias=bias_[:, h:h + 1]),
                             reads=[pP, bias_], writes=[T[dst]])
            b.op("dve", lambda e: e.tensor_scalar_mul(out=f2(T["LW"]), in0=f2(T["LW"]), scalar1=-0.6065306597126334), reads=[T["LW"]], writes=[T["LW"]])
            tt("dve", T["KK"][:], K_, bc(k_k), ALU.mult, [XL, k_k], [T["KK"]])
            tt("dve", T["NR"][:], T["KK"][:], T["KK"][:], ALU.mult, [T["KK"]], [T["NR"]])
            for half in range(2):
                b.op("pe", lambda e: e.matmul(pP[0:64, :], lhsT=ones[:], rhs=T["NR"][:, half * 4:(half + 1) * 4, :].rearrange("p h t -> p (h t)"), start=True, stop=True),
                     reads=[ones, T["NR"]], writes=[pP])
                b.op("act", lambda e: e.activation(out=T["KKN"][:, half * 4:(half + 1) * 4, :].rearrange("p h t -> p (h t)"), in_=pP[0:64, :], func=AF.Sqrt),
                     reads=[pP], writes=[T["KKN"]])
            b.op("dve", lambda e: e.tensor_scalar_max(out=f2(T["KKN"]), in0=f2(T["KKN"]), scalar1=1e-12), reads=[T["KKN"]], writes=[T["KKN"]])
            b.op("dve", lambda e: e.reciprocal(out=f2(T["KKN"]), in_=f2(T["KKN"])), reads=[T["KKN"]], writes=[T["KKN"]])
            tt("dve", T["KKN"][:], T["KKN"][:], T["KK"][:], ALU.mult, [T["KKN"], T["KK"]], [T["KKN"]])
            tt("dve", T["BVc"][:], T["KKN"][:], T["AS"][:], ALU.mult, [T["KKN"], T["AS"]], [T["BVc"]])
            b.op("pool", lambda e: e.tensor_scalar_add(out=f2(T["T1"]), in0=f2(T["AS"]), scalar1=-1.0), reads=[T["AS"]], writes=[T["T1"]])
            tt("pool", T["T1"][:], T["T1"][:], bc(k_a), ALU.mult, [T["T1"], k_a], [T["T1"]])
            b.op("dve", lambda e: e.scalar_tensor_tensor(out=f2(T["KP"]), in0=f2(T["T1"]), scalar=1.0, in1=K_.rearrange("p h t -> p (h t)"), op0=ALU.add, op1=ALU.mult),
                 reads=[T["T1"], XL], writes=[T["KP"]])
            tt("pool", T["RK"][:], R_, T["KP"][:], ALU.mult, [XL, T["KP"]], [T["RK"]])
            tt("pool", T["RK"][:], T["RK"][:], bc(r_k), ALU.mult, [T["RK"], r_k], [T["RK"]])
            for c_ in range(NCH):
                for h in range(8):
                    b.op("pe", lambda e: e.matmul(pD[0:64, c_ * 8 + h:c_ * 8 + h + 1], lhsT=T["RK"][:, h, c_ * 64:(c_ + 1) * 64], rhs=ones[:, 0:1], start=True, stop=True),
                         reads=[T["RK"], ones], writes=[pD])
            b.op("act", lambda e: e.copy(out=BON[:], in_=pD[0:64, 0:NCH * 8]), reads=[pD], writes=[BON])
            b.op("dve", lambda e: e.tensor_tensor_scan(out=f2(T["L"]), data0=rstm[:], data1=f2(T["LW"]), initial=0.0, op0=ALU.mult, op1=ALU.add),
                 reads=[rstm, T["LW"]], writes=[T["L"]])
            b.op("act", lambda e: e.activation(out=f2(T["EP"]), in_=f2(T["L"]), func=AF.Exp), reads=[T["L"]], writes=[T["EP"]])
            b.op("act", lambda e: e.activation(out=f2(T["EM"]), in_=f2(T["L"]), func=AF.Exp, scale=-1.0), reads=[T["L"]], writes=[T["EM"]])
            tt("dve", T["L"][:], T["L"][:], T["LW"][:], ALU.subtract, [T["L"], T["LW"]], [T["L"]])
            b.op("act", lambda e: e.activation(out=f2(T["EX"]), in_=f2(T["L"]), func=AF.Exp), reads=[T["L"]], writes=[T["EX"]])
            ar0 = AR[:, :, :, 0, :].rearrange("p h c t -> p (h c) t")
            ar1 = AR[:, :, :, 1, :].rearrange("p h c t -> p (h c) t")
            b.op("dve", lambda e: e.scalar_tensor_tensor(out=ar0, in0=c16(T["KKN"]), scalar=-1.0, in1=c16(T["EX"]), op0=ALU.mult, op1=ALU.mult),
                 reads=[T["KKN"], T["EX"]], writes=[AR])
            tt("dve", ar1, R_.rearrange("p h (c t) -> p (h c) t", t=64), c16(T["EP"]), ALU.mult, [XL, T["EP"]], [AR])
            tt("dve", T["BT"][:], T["BVc"][:], T["EM"][:], ALU.mult, [T["BVc"], T["EM"]], [T["BT"]])
            tt("dve", T["KT"][:], T["KP"][:], T["EM"][:], ALU.mult, [T["KP"], T["EM"]], [T["KT"]])
            gC = c16(T["EP"])[:, :, 63:64].to_broadcast([64, 16, 64])
            tt("dve", c16(T["BG"]), c16(T["BT"]), gC, ALU.mult, [T["BT"], T["EP"]], [T["BG"]])
            tt("dve", c16(T["KG"]), c16(T["KT"]), gC, ALU.mult, [T["KT"], T["EP"]], [T["KG"]])
            if gi + 1 < ngr:
                emit_inproj(gi + 1)
            for c_ in range(NCH):
                cs = slice(c_ * 64, (c_ + 1) * 64)
                cur = (gi * NCH + c_) % 2
                U_ = []
                for u in range(2):
                    heads = list(range(u * 4, u * 4 + 4))
                    pBu = pB if u == 0 else pC
                    for j, h in enumerate(heads):
                        b.op("pe", lambda e: e.transpose(out=pZ[0:64, j * 128:j * 128 + 64], in_=T["BG"][:, h, cs], identity=idf[0:64, 0:64]), reads=[T["BG"], idf], writes=[pZ])
                        b.op("pe", lambda e: e.transpose(out=pZ[0:64, j * 128 + 64:(j + 1) * 128], in_=T["KG"][:, h, cs], identity=idf[0:64, 0:64]), reads=[T["KG"], idf], writes=[pZ])
                    tm = TM4[u]
                    b.op("act", lambda e: e.copy(out=tm[:].rearrange("p h a t -> p (h a t)"), in_=pZ[0:64, 0:512]), reads=[pZ], writes=[tm])
                    for j, h in enumerate(heads):
                        arc = AR[:, h, c_, :, :].rearrange("p a t -> p (a t)")
                        b.op("pe", lambda e: e.matmul(pA[0:64, j * 256:j * 256 + 128], lhsT=T["BT"][:, h, cs], rhs=arc, start=True, stop=True), reads=[T["BT"], AR], writes=[pA])
                        b.op("pe", lambda e: e.matmul(pA[0:64, j * 256 + 128:(j + 1) * 256], lhsT=T["KT"][:, h, cs], rhs=arc, start=True, stop=True), reads=[T["KT"], AR], writes=[pA])
                        b.op("pe", lambda e: e.matmul(pBu[0:64, j * 64:(j + 1) * 64], lhsT=AR[:, h, c_, 0, :], rhs=T["BT"][:, h, cs], start=True, stop=True), reads=[T["BT"], AR], writes=[pBu])
                    xm = XM4[u]
                    tt("dve", xm[:].rearrange("p h (a m) t -> p (h a) m t", a=2), pA[0:64, :].rearrange("p (ha m t) -> p ha m t", m=2, t=64),
                       msk[:, None, 0:2, :].to_broadcast([64, 8, 2, 64]), ALU.mult, [pA, msk], [xm])
                    aa = AA4[u][0]
                    b.op("act", lambda e: e.copy(out=aa[:, :, 0, :], in_=xm[:, :, 0, :]), reads=[xm], writes=[aa])
                    tt("dve", aa[:, :, 1, :], pBu[0:64, 0:256].rearrange("p (h t) -> p h t", t=64), msk[:, 2:3, :].to_broadcast([64, 4, 64]), ALU.mult, [pBu, msk], [aa])
                    P_ = PP4[u][0]
                    tt("pool", P_[:], xm[:, :, 0, :], idf[0:64, None, 0:64].to_broadcast([64, 4, 64]), ALU.add, [xm, idf], [P_])
                    U_.append(dict(tm=tm, xm=xm, aa=aa, P=P_, pB=pBu, pD=(pD if u == 0 else pP), k=0))
                for step in range(5):
                    for u_ in U_:
                        aa, pDu = u_["aa"], u_["pD"]
                        for j in range(4):
                            b.op("pe", lambda e: e.matmul(pDu[0:64, j * 128:j * 128 + 64], lhsT=aa[:, j, 1, :], rhs=aa[:, j, 0, :], start=True, stop=True), reads=[aa], writes=[pDu])
                            b.op("pe", lambda e: e.matmul(pDu[0:64, j * 128 + 64:(j + 1) * 128], lhsT=aa[:, j, 0, :], rhs=aa[:, j, 1, :], start=True, stop=True), reads=[aa], writes=[pDu])
                    for ui, u_ in enumerate(U_):
                        u_["k"] += 1
                        aa2 = AA4[ui][u_["k"] % 2]
                        b.op("act", lambda e: e.copy(out=aa2[:].rearrange("p h a t -> p (h a t)"), in_=u_["pD"][0:64, :]), reads=[u_["pD"]], writes=[aa2])
                        u_["aa"] = aa2
                    for u_ in U_:
                        for j in range(4):
                            b.op("pe", lambda e: e.matmul(u_["pB"][0:64, 256 + j * 64:256 + (j + 1) * 64], lhsT=u_["aa"][:, j, 1, :], rhs=u_["P"][:, j, :], start=True, stop=True),
                                 reads=[u_["aa"], u_["P"]], writes=[u_["pB"]])
                    for ui, u_ in enumerate(U_):
                        P2 = PP4[ui][u_["k"] % 2]
                        tt("dve", P2[:], u_["pB"][0:64, 256:512].rearrange("p (h t) -> p h t", t=64), u_["P"][:], ALU.add, [u_["pB"], u_["P"]], [P2])
                        u_["P"] = P2
                for h in range(8):
                    u_, j = U_[h // 4], h % 4
                    o = slice(h * 64, (h + 1) * 64)
                    b.op("pe", lambda e: e.matmul(pA[0:64, o], lhsT=u_["xm"][:, j, 2, :], rhs=Vtm[:, c_, o], start=True, stop=False), reads=[u_["xm"], Vtm], writes=[pA])
                    b.op("pe", lambda e: e.matmul(pA[0:64, o], lhsT=AR[:, h, c_, 0, :], rhs=Hst[:, cur, h, :], start=False, stop=True), reads=[AR, Hst], writes=[pA])
                b.op("act", lambda e: e.copy(out=Xs8[:].rearrange("p h t -> p (h t)"), in_=pA[0:64, 0:512]), reads=[pA], writes=[Xs8])
                for h in range(8):
                    u_, j = U_[h // 4], h % 4
                    b.op("pe", lambda e: e.matmul(pA[0:64, 512 + h * 64:512 + (h + 1) * 64], lhsT=u_["P"][:, j, :], rhs=Xs8[:, h, :], start=True, stop=True), reads=[u_["P"], Xs8], writes=[pA])
                b.op("act", lambda e: e.copy(out=Us8[:].rearrange("p h t -> p (h t)"), in_=pA[0:64, 512:1024]), reads=[pA], writes=[Us8])
                for h in range(8):
                    u_, j = U_[h // 4], h % 4
                    o = slice(h * 64, (h + 1) * 64)
                    b.op("pe", lambda e: e.matmul(pD[0:64, o], lhsT=u_["tm"][:, j, 0, :], rhs=Us8[:, h, :], start=True, stop=False), reads=[u_["tm"], Us8], writes=[pD])
                    b.op("pe", lambda e: e.matmul(pD[0:64, o], lhsT=u_["tm"][:, j, 1, :], rhs=Vtm[:, c_, o], start=False, stop=True), reads=[u_["tm"], Vtm], writes=[pD])
                tt("dve", Ht8[:], Hst[:, cur, :, :], T["EP"][:, :, c_ * 64 + 63:c_ * 64 + 64].to_broadcast([64, 8, 64]), ALU.mult, [Hst, T["EP"]], [Ht8])
                tt("dve", Hst[:, 1 - cur, :, :], pD[0:64, :].rearrange("p (h t) -> p h t", t=64), Ht8[:], ALU.add, [pD, Ht8], [Hst])
                for h in range(8):
                    u_, j = U_[h // 4], h % 4
                    o = slice(h * 64, (h + 1) * 64)
                    b.op("pe", lambda e: e.matmul(pZ[0:64, o], lhsT=AR[:, h, c_, 1, :], rhs=Hst[:, cur, h, :], start=True, stop=False), reads=[AR, Hst], writes=[pZ])
                    b.op("pe", lambda e: e.matmul(pZ[0:64, o], lhsT=u_["xm"][:, j, 1, :], rhs=Us8[:, h, :], start=False, stop=False), reads=[u_["xm"], Us8], writes=[pZ])
                    b.op("pe", lambda e: e.matmul(pZ[0:64, o], lhsT=u_["xm"][:, j, 3, :], rhs=Vtm[:, c_, o], start=False, stop=True), reads=[u_["xm"], Vtm], writes=[pZ])
                b.op("act", lambda e: e.copy(out=Ytm[:, c_, :, :].rearrange("p h t -> p (h t)"), in_=pZ[0:64, :]), reads=[pZ], writes=[Ytm])
            Y3 = Ytm[:].rearrange("p c h i -> p (c h) i")
            S3 = sqv[:].rearrange("p c h i -> p (c h) i")
            b.op("dve", lambda e: e.tensor_reduce(out=st1[:], in_=Y3, axis=AX.X, op=ALU.add), reads=[Ytm], writes=[st1])
            b.op("dve", lambda e: e.tensor_scalar_mul(out=st1[:], in0=st1[:], scalar1=1.0 / 64), reads=[st1], writes=[st1])
            tt("dve", Y3, Y3, st1[:].unsqueeze(2).to_broadcast([64, NCH * 8, 64]), ALU.subtract, [Ytm, st1], [Ytm])
            tt("dve", S3, Y3, Y3, ALU.mult, [Ytm], [sqv])
            b.op("dve", lambda e: e.tensor_reduce(out=st2[:], in_=S3, axis=AX.X, op=ALU.add), reads=[sqv], writes=[st2])
            b.op("act", lambda e: e.activation(out=st2[:], in_=st2[:], func=AF.Sqrt, scale=1.0 / 64, bias=64e-5), reads=[st2], writes=[st2])
            b.op("dve", lambda e: e.reciprocal(out=st2[:], in_=st2[:]), reads=[st2], writes=[st2])
            tt("dve", Y3, Y3, st2[:].unsqueeze(2).to_broadcast([64, NCH * 8, 64]), ALU.mult, [Ytm, st2], [Ytm])
            lg = lng[:].rearrange("p (h i) -> p h i", i=64)[:, None, :, :].to_broadcast([64, NCH, 8, 64])
            lb = lnb[:].rearrange("p (h i) -> p h i", i=64)[:, None, :, :].to_broadcast([64, NCH, 8, 64])
            tt("dve", Ytm[:], Ytm[:], lg, ALU.mult, [Ytm, lng], [Ytm])
            tt("dve", Ytm[:], Ytm[:], lb, ALU.add, [Ytm, lnb], [Ytm])
            V3 = Vtm[:].rearrange("p c (h i) -> p (c h) i", i=64)
            tt("pool", S3, V3, BON[:].unsqueeze(2).to_broadcast([64, NCH * 8, 64]), ALU.mult, [Vtm, BON], [sqv])
            tt("dve", Y3, Y3, S3, ALU.add, [Ytm, sqv], [Ytm])
            for c_ in range(NCH):
                for two in range(2):
                    b.op("pe", lambda e: e.matmul(pP[0:64, :], lhsT=XL[:, 26 + two, c_ * 64:(c_ + 1) * 64], rhs=g2s[:, two, :], start=(two == 0), stop=(two == 1)),
                         reads=[XL, g2s], writes=[pP])
                tt("dve", OBb[:, c_, :], Ytm[:, c_, :, :].rearrange("p h i -> p (h i)"), pP[0:64, :], ALU.mult, [Ytm, pP], [OBb])
                for k4 in range(4):
                    b.op("pe", lambda e: e.transpose(out=pt[:, k4, c_ * 64:(c_ + 1) * 64], in_=OBb[:, c_, k4 * 128:(k4 + 1) * 128], identity=self.ident[0:64, 0:64]),
                         reads=[OBb, self.ident], writes=[pt])
            ot = obT[gi % 2]
            b.op("act", lambda e: e.copy(out=ot[:], in_=pt[:, 0:4, :]), reads=[pt], writes=[ot])
            b.dma("pool", self.obT_d[:, :, q0:q0 + TG].rearrange("c p t -> p c t"), ot[:], reads=[ot], writes=[self.obT_d])
        if "rwkv" in self.debug:
            d = self.dbg_out("obT", [4, 128, S], BF16)
            b.dma("pool", d, self.obT_d[:], reads=[self.obT_d])


Prog.phase_rwkv3 = _phase_rwkv3


def _phase_ffn2(self):
    b = self.b
    I = self.inp
    TG = 256
    NFT = 44
    with b.scope():
        gf = self.load_gain("gf", I["ffn_norm_g"][0])
        stage = [b.sb(f"fst{i}", [128, 1024], F32) for i in range(2)]
        wu = b.sb("wu", [128, 8, 2 * DFF], BF16)
        for n in range(8):
            for c in range(8):
                st = stage[c % 2]
                b.dma("sp", st[:, 0:704], I["w_up"][0][c * 128:(c + 1) * 128, n * 704:(n + 1) * 704], writes=[st])
                b.op("act", lambda e: e.activation(out=wu[:, c, n * 704:(n + 1) * 704], in_=st[:, 0:704], func=AF.Copy, scale=gf[:, c:c + 1]),
                     reads=[st, gf], writes=[wu])
        wd = b.sb("wd", [128, 22, D], BF16)
        self.load_weight(wd, I["w_down"][0], D, kch=22, stage=stage, eng="dve")
        cw = b.sb("cw", [128, 3, NFT], F32)
        for j in range(3):
            b.dma("sp", cw[:, j, :], I["conv_w"][0][j].rearrange("(c p) -> p c", p=128), writes=[cw], allow_slow_non_contiguous=True)
        cbias = self.load_gain("cbias", I["conv_b"][0], kch=NFT)
        xt = [b.sb(f"fxt{i}", [128, D], F32) for i in range(2)]
        junk = b.sb("fjunk", [128, D], BF16)
        ss = [b.sb(f"fss{i}", [128, 1], F32) for i in range(2)]
        hb = [b.sb(f"fhb{i}", [128, D], BF16) for i in range(2)]
        hTg = b.sb("fhTg", [128, 8, TG + 2], BF16)
        b.op("pool", lambda e: e.memset(hTg[:], 0.0), writes=[hTg])
        cv = [b.sb(f"cv{i}", [128, TG], F32) for i in range(3)]
        sgl = [b.sb(f"sgl{i}", [128, TG], BF16) for i in range(2)]
        actT = b.sb("actT", [128, 22, TG], BF16)
        val = b.sb("fval", [128, 22, TG], BF16)
        pt = b.ps("fpt", [128, 8, 128], BF16)
        pu = [b.ps(f"fpu{i}", [128, 512], F32) for i in range(4)]
        pd = [b.ps(f"fpd{i}", [128, 512], F32) for i in range(2)]
        ng = getattr(self, "nt_limit", NT) * 128 // TG
        for gi in range(ng):
            b.op("pool", lambda e: e.tensor_copy(out=hTg[:, :, 0:2], in_=hTg[:, :, TG:TG + 2]), reads=[hTg], writes=[hTg])
            for s_ in range(TG // 128):
                t = gi * (TG // 128) + s_
                self.make_hT(self.x1_d, t, xt[s_], junk, ss[s_], hb[s_], pt, hTg, self.ident, hT_ap=hTg[:, :, 2 + s_ * 128:2 + (s_ + 1) * 128])
            for ft in range(NFT):
                p = pu[ft % 4]
                c_ = cv[ft % 3]
                for c in range(8):
                    b.op("pe", lambda e: e.matmul(p[:, 0:TG + 2], lhsT=wu[:, c, ft * 128:(ft + 1) * 128], rhs=hTg[:, c, :], start=(c == 0), stop=(c == 7)),
                         reads=[wu, hTg], writes=[p])
                b.op("act", lambda e: e.activation(out=c_[:], in_=p[:, 0:TG], func=AF.Identity, scale=cw[:, 0, ft:ft + 1], bias=cbias[:, ft:ft + 1]),
                     reads=[p, cw, cbias], writes=[c_])
                b.op("dve", lambda e: e.scalar_tensor_tensor(out=c_[:], in0=p[:, 1:TG + 1], scalar=cw[:, 1, ft:ft + 1], in1=c_[:], op0=ALU.mult, op1=ALU.add),
                     reads=[p, cw, c_], writes=[c_])
                if ft < 22:
                    b.op("dve", lambda e: e.scalar_tensor_tensor(out=val[:, ft, :], in0=p[:, 2:TG + 2], scalar=cw[:, 2, ft:ft + 1], in1=c_[:], op0=ALU.mult, op1=ALU.add),
                         reads=[p, cw, c_], writes=[val])
                else:
                    sg_ = sgl[ft % 2]
                    b.op("dve", lambda e: e.scalar_tensor_tensor(out=c_[:], in0=p[:, 2:TG + 2], scalar=cw[:, 2, ft:ft + 1], in1=c_[:], op0=ALU.mult, op1=ALU.add),
                         reads=[p, cw, c_], writes=[c_])
                    b.op("act", lambda e: e.activation(out=sg_[:], in_=c_[:], func=AF.Silu), reads=[c_], writes=[sg_])
                    b.op("pool", lambda e: e.tensor_tensor(out=actT[:, ft - 22, :], in0=sg_[:], in1=val[:, ft - 22, :], op=ALU.mult),
                         reads=[sg_, val], writes=[actT])
            for s_ in range(TG // 128):
                t = gi * (TG // 128) + s_
                for n in range(2):
                    for f in range(22):
                        b.op("pe", lambda e: e.matmul(pd[n][:, :], lhsT=actT[:, f, s_ * 128:(s_ + 1) * 128], rhs=wd[:, f, n * 512:(n + 1) * 512], start=(f == 0), stop=(f == 21)),
                             reads=[actT, wd], writes=[pd[n]])
                    b.op("dve", lambda e: e.tensor_tensor(out=xt[s_][:, n * 512:(n + 1) * 512], in0=pd[n][:, :], in1=xt[s_][:, n * 512:(n + 1) * 512], op=ALU.add),
                         reads=[pd[n], xt[s_]], writes=[xt[s_]])
                b.dma("pool", self.out[t * 128:(t + 1) * 128, :], xt[s_][:], reads=[xt[s_]])


Prog.phase_ffn2 = _phase_ffn2
```

```python
import contextlib
import numpy as np
import ml_dtypes
import concourse.bass as bass
import concourse.mybir as mybir
from concourse.bass_utils import run_bass_kernel_spmd

F32 = mybir.dt.float32
BF16 = mybir.dt.bfloat16
AF = mybir.ActivationFunctionType
ALU = mybir.AluOpType
AX = mybir.AxisListType

S = 4096
D = 1024
NT = S // 128
IN_WIDTH = 5144
RW0 = 1304
GA0 = 3096
GB0 = 4120
DFF = 2816
RMS_EPS = 1e-6


class Buf:
    def __init__(self, t, name):
        self.t = t
        self.name = name
        self.w = None
        self.r = {}
        self.psum = False

    def __getitem__(self, idx):
        return self.t[idx]


class Builder:
    SEM_ROLL = 30000

    def __init__(self, nc):
        self.nc = nc
        self.stack = contextlib.ExitStack()
        self.root = self.stack
        self.eng = {"pe": nc.tensor, "act": nc.scalar, "dve": nc.vector,
                    "pool": nc.gpsimd, "sp": nc.sync}
        self.sem = {}
        self.cnt = {}
        self.seen = {e: {} for e in self.eng}
        self.nsem = 0
        self.lanes = {}
        self.lane_rr = {}
        self.last_tok = {}
        for e in self.eng:
            self._roll(e)

    def newsem(self, name):
        self.nsem += 1
        return self.root.enter_context(self.nc.semaphore(f"{name}_{self.nsem}"))

    def sb(self, name, shape, dt=F32):
        self.nsem += 1
        name = f"sb{self.nsem}_{name}"
        return Buf(self.stack.enter_context(self.nc.sbuf_tensor(name, list(shape), dt)), name)

    def ps(self, name, shape, dt=F32):
        self.nsem += 1
        name = f"ps{self.nsem}_{name}"
        bf = Buf(self.stack.enter_context(self.nc.psum_tensor(name, list(shape), dt)), name)
        bf.psum = True
        return bf

    def dram(self, name, shape, dt=F32, kind="Internal"):
        return Buf(self.nc.dram_tensor(name, list(shape), dt, kind=kind), name)

    def _roll(self, e):
        self.sem[e] = self.newsem("s" + e)
        self.cnt[e] = 0

    def _wait(self, e, tok):
        sem, val = tok
        k = id(sem)
        if self.seen[e].get(k, 0) < val:
            self.eng[e].wait_ge(sem, val)
            self.seen[e][k] = val

    def _deps(self, e, reads, writes):
        for b in reads:
            if b.w is not None:
                we, tok = b.w
                self._wait(e, tok)
            if b.psum:
                for re_, tok in b.r.items():
                    if re_ != e:
                        self._wait(e, tok)
        for b in writes:
            if b.w is not None:
                we, tok = b.w
                if we != e:
                    self._wait(e, tok)
            for re_, tok in b.r.items():
                if re_ != e:
                    self._wait(e, tok)

    def op(self, e, fn, reads=(), writes=()):
        if self.cnt[e] >= self.SEM_ROLL:
            self._roll(e)
        self._deps(e, reads, writes)
        ins = fn(self.eng[e])
        self.cnt[e] += 1
        tok = (self.sem[e], self.cnt[e])
        ins.then_inc(self.sem[e], 1)
        self.last_tok[e] = tok
        for b in reads:
            b.r[e] = tok
        for b in writes:
            b.w = (e, tok)
            b.r = {}
        return tok

    def dma(self, q, out, in_, reads=(), writes=(), nlanes=6, **kw):
        if q not in self.lanes:
            self.lanes[q] = [[self.newsem("l" + q), 0] for _ in range(nlanes)]
            self.lane_rr[q] = 0
        li = self.lane_rr[q]
        self.lane_rr[q] = (li + 1) % len(self.lanes[q])
        lane = self.lanes[q][li]
        if lane[1] >= 1800:
            self._wait(q, (lane[0], 16 * lane[1]))
            lane[0] = self.newsem("l" + q)
            lane[1] = 0
        if lane[1] > 0:
            self._wait(q, (lane[0], 16 * lane[1]))
        self._deps_dma(q, reads, writes)
        ins = self.eng[q].dma_start(out=out, in_=in_, **kw)
        lane[1] += 1
        tok = (lane[0], 16 * lane[1])
        ins.then_inc(lane[0], 16)
        key = "dma_" + q + str(li)
        for b in reads:
            b.r[key] = tok
        for b in writes:
            b.w = (key, tok)
            b.r = {}
        return tok

    def _deps_dma(self, q, reads, writes):
        for b in reads:
            if b.w is not None:
                self._wait(q, b.w[1])
        for b in writes:
            if b.w is not None:
                self._wait(q, b.w[1])
            for re_, tok in b.r.items():
                self._wait(q, tok)

    def barrier(self):
        toks = list(self.last_tok.values())
        for q, lanes in self.lanes.items():
            for lane in lanes:
                if lane[1] > 0:
                    toks.append((lane[0], 16 * lane[1]))
        for e in self.eng:
            for tok in toks:
                self._wait(e, tok)

    def wait_all_on(self, e):
        toks = list(self.last_tok.values())
        for q, lanes in self.lanes.items():
            for lane in lanes:
                if lane[1] > 0:
                    toks.append((lane[0], 16 * lane[1]))
        for tok in toks:
            self._wait(e, tok)

    @contextlib.contextmanager
    def scope(self):
        old = self.stack
        self.stack = contextlib.ExitStack()
        try:
            yield
            self.barrier()
        finally:
            self.stack.close()
            self.stack = old

    def close(self):
        self.stack.close()


NEG = -30000.0


def _bucket(dist):
    n = np.maximum(dist, 0)
    ratio = np.log(np.maximum(n, 1).astype(np.float32) / np.float32(16.0)) / np.float32(np.log(8.0))
    large = np.minimum(16 + (ratio * 16).astype(np.int32), 31)
    return np.where(n < 16, n, large)


def host_consts(rel_bias):
    rel = np.asarray(rel_bias, np.float32)
    c = {}
    c["ident"] = np.eye(128, dtype=np.float32).astype(ml_dtypes.bfloat16)
    c["identf"] = np.eye(128, dtype=np.float32)
    kp = np.arange(128)[:, None]
    cc = np.arange(640)[None, :]
    dist = cc - kp
    bt = rel[_bucket(dist)]
    tw = np.where(((dist >= 0) & (dist < 512))[..., None], bt, np.float32(NEG))
    ts = np.where((dist >= 0)[..., None], bt, np.float32(NEG))
    c["tw"] = np.ascontiguousarray(tw.transpose(0, 2, 1)).astype(np.float32)
    c["ts"] = np.ascontiguousarray(ts.transpose(0, 2, 1)).astype(np.float32)
    cidx = np.arange(256)[:, None]
    qidx = np.arange(S)[None, :]
    dc = qidx - 16 * cidx - 31
    bcg = rel[_bucket(dc)]
    ok = (dc >= 0) & (cidx < 255)
    bc = np.where(ok[..., None], bcg, np.float32(NEG))
    c["biasc"] = np.ascontiguousarray(bc.transpose(2, 0, 1)).reshape(8, 2, 128, S).astype(np.float32)
    A = np.zeros((256, 64), np.float32)
    Wt = (1, 2, 2, 2, 1)
    for ci in range(255):
        for j in range(64):
            o = ci + 1 - 4 * j
            if 0 <= o <= 4:
                A[ci, j] = Wt[o]
    c["amat"] = A.reshape(2, 128, 64)
    E = np.zeros((64, S), np.float32)
    E[np.arange(S) // 64, np.arange(S)] = 1.0
    c["emat"] = E.astype(ml_dtypes.bfloat16)
    qp = np.arange(128)[:, None, None]
    qt = np.arange(32)[None, :, None]
    j = np.arange(64)[None, None, :]
    cur = (128 * qt + qp) // 64
    cand = (j >= 1) & (j <= cur - 2)
    c["candneg"] = np.where(cand, 0.0, -1e9).astype(np.float32)
    c["fz"] = ((j == 0) | (j == cur) | (j == cur - 1)).astype(np.float32)
    tri = np.triu(np.ones((64, 64), np.float32))
    c["rwmask"] = np.ascontiguousarray(np.stack([np.triu(np.ones((64, 64), np.float32), 1), tri, np.tril(np.ones((64, 64), np.float32), -1)], axis=1))
    rr = np.ones((64, 1024), np.float32)
    rr[:, ::64] = 0.0
    c["rwreset"] = rr
    c["b31"] = np.ascontiguousarray(np.broadcast_to(rel[31][None, :], (128, 8))).astype(np.float32)
    return c


CONST_SPECS = {
    "ident": ([128, 128], BF16), "identf": ([128, 128], F32),
    "tw": ([128, 8, 640], F32), "ts": ([128, 8, 640], F32),
    "biasc": ([8, 2, 128, S], F32), "amat": ([2, 128, 64], F32),
    "emat": ([64, S], BF16), "candneg": ([128, 32, 64], F32), "fz": ([128, 32, 64], F32),
    "b31": ([128, 8], F32), "rwmask": ([64, 3, 64], F32), "rwreset": ([64, 1024], F32),
}

W_SPECS = {
    "x": [S, D], "attn_norm_g": [1, D], "w_in": [1, D, IN_WIDTH], "q_norm_g": [1, 64], "k_norm_g": [1, 64],
    "cmp_pe_k": [1, 32, 64], "cmp_w1_k": [1, 2048, 256], "cmp_w2_k": [1, 256, 64],
    "cmp_pe_v": [1, 32, 64], "cmp_w1_v": [1, 2048, 256], "cmp_w2_v": [1, 256, 64],
    "rwkv_mu": [1, 1792], "rwkv_w0": [1, 512], "rwkv_w2": [1, 64, 512], "rwkv_a0": [1, 512],
    "rwkv_a2": [1, 64, 512], "rwkv_g2": [1, 128, 512], "rwkv_k_k": [1, 512], "rwkv_k_a": [1, 512],
    "rwkv_r_k": [1, 8, 64], "rwkv_ln_g": [1, 512], "rwkv_ln_b": [1, 512],
    "w_proj_a": [1, 512, D], "w_proj_b": [1, 512, D], "w_out": [1, D, D], "ffn_norm_g": [1, D],
    "w_up": [1, D, 2 * DFF], "conv_w": [1, 3, 2 * DFF], "conv_b": [1, 2 * DFF], "w_down": [1, DFF, D],
}


class Prog:
    def __init__(self, debug=()):
        self.debug = set(debug)
        nc = bass.Bass("TRN2", target_bir_lowering=False)
        self.nc = nc
        self.inp = {}
        for k, shp in W_SPECS.items():
            self.inp[k] = nc.dram_tensor(k, list(shp), F32, kind="ExternalInput").ap()
        for k, (shp, dt) in CONST_SPECS.items():
            self.inp[k] = nc.dram_tensor(k, list(shp), dt, kind="ExternalInput").ap()
        self.out = nc.dram_tensor("out", [S, D], F32, kind="ExternalOutput").ap()
        self.dbg = {}
        self.b = Builder(nc)

    def dbg_out(self, name, shape, dt=F32):
        t = self.nc.dram_tensor("dbg_" + name, list(shape), dt, kind="ExternalOutput").ap()
        self.dbg[name] = t
        return t

    def load_weight(self, dst, src, ncols, gvec=None, kch=8, stage=None, eng="act"):
        b = self.b
        for c in range(kch):
            st = stage[c % len(stage)]
            b.dma("sp", st[:, :ncols], src[c * 128:(c + 1) * 128, :], writes=[st])
            if gvec is not None:
                b.op(eng, lambda e: e.activation(out=dst[:, c, :], in_=st[:, :ncols], func=AF.Copy, scale=gvec[:, c:c + 1])
                     if eng == "act" else e.tensor_scalar_mul(out=dst[:, c, :], in0=st[:, :ncols], scalar1=gvec[:, c:c + 1]),
                     reads=[st, gvec], writes=[dst])
            else:
                b.op(eng, lambda e: e.copy(out=dst[:, c, :], in_=st[:, :ncols]) if eng == "act"
                     else e.tensor_copy(out=dst[:, c, :], in_=st[:, :ncols]), reads=[st], writes=[dst])

    def load_gain(self, name, src_vec, kch=8):
        b = self.b
        g = b.sb(name, [128, kch], F32)
        b.dma("sp", g[:], src_vec.rearrange("(c p) -> p c", p=128), writes=[g], allow_slow_non_contiguous=True)
        return g

    def bcast_row(self, name, src_row, n):
        b = self.b
        t = b.sb(name, [128, n], F32)
        b.dma("sp", t[:], src_row.partition_broadcast(128), writes=[t])
        return t

    def make_hT(self, x_ap, t, xt, junk, ss, hb, pt, hT, ident, hT_ap=None):
        b = self.b
        b.dma("sp", xt[:], x_ap[t * 128:(t + 1) * 128, :], writes=[xt])
        b.op("act", lambda e: e.activation(out=junk[:], in_=xt[:], func=AF.Square, accum_out=ss[:]), reads=[xt], writes=[junk, ss])
        b.op("act", lambda e: e.activation(out=ss[:], in_=ss[:], func=AF.Sqrt, scale=1.0 / D, bias=RMS_EPS), reads=[ss], writes=[ss])
        b.op("dve", lambda e: e.reciprocal(out=ss[:], in_=ss[:]), reads=[ss], writes=[ss])
        b.op("dve", lambda e: e.tensor_scalar_mul(out=hb[:], in0=xt[:], scalar1=ss[:]), reads=[xt, ss], writes=[hb])
        for c in range(8):
            b.op("pe", lambda e: e.transpose(out=pt[:, c, :], in_=hb[:, c * 128:(c + 1) * 128], identity=ident[:]),
                 reads=[hb, ident], writes=[pt])
        b.op("act", lambda e: e.copy(out=(hT[:] if hT_ap is None else hT_ap), in_=pt[:]), reads=[pt], writes=[hT])

    def alloc_root(self):
        b = self.b
        I = self.inp
        self.ident = b.sb("ident", [128, 128], BF16)
        b.dma("sp", self.ident[:], I["ident"], writes=[self.ident])
        self.identf = b.sb("identf", [128, 128], F32)
        b.dma("sp", self.identf[:], I["identf"], writes=[self.identf])

    def alloc_persistent(self):
        b = self.b
        I = self.inp
        if not hasattr(self, "ident"):
            self.alloc_root()
        self.ksE = b.sb("ksE", [128, 2, S], BF16)
        self.kwT = b.sb("kwT", [64, 2, S], BF16)
        self.vaug_s = b.sb("vaug_s", [128, NT, 2, 65], BF16)
        self.vaug_w = b.sb("vaug_w", [128, NT, 2, 65], BF16)
        self.gts = b.sb("gts", [128, NT, 24], F32)
        self.kcT = b.sb("kcT", [64, 2, 256], BF16)
        self.vcA = b.sb("vcA", [128, 2, 2, 129], F32)
        self.qT_d = b.dram("qT_d", [8, 64, S], BF16)
        self.oaT_d = b.dram("oaT_d", [4, 128, S], BF16)
        self.obT_d = b.dram("obT_d", [4, 128, S], BF16)
        for g in range(2):
            b.dma("sp", self.ksE[64:128, g, :], I["emat"], writes=[self.ksE])
        b.op("pool", lambda e: e.memset(self.vaug_s[:, :, :, 64:65], 1.0), writes=[self.vaug_s])
        b.op("pool", lambda e: e.memset(self.vaug_w[:, :, :, 64:65], 1.0), writes=[self.vaug_w])
        b.op("pool", lambda e: e.memset(self.vcA[:, :, :, 64:65], 1.0), writes=[self.vcA])
        for g in range(2):
            for ct in range(2):
                b.dma("sp", self.vcA[:, g, ct, 65:129], I["amat"][ct], writes=[self.vcA])

    def phase_nsa_proj(self):
        b = self.b
        I = self.inp
        with b.scope():
            gat = self.load_gain("gat", I["attn_norm_g"][0])
            wn = b.sb("wn", [128, 8, RW0], BF16)
            stage = [b.sb(f"wst{i}", [128, RW0], F32) for i in range(2)]
            self.load_weight(wn, I["w_in"][0][:, 0:RW0], RW0, gvec=gat, stage=stage)
            gq = self.bcast_row("gq", I["q_norm_g"][0], 64)
            gk = self.bcast_row("gk", I["k_norm_g"][0], 64)
            gq_rep = b.sb("gq_rep", [128, 8, 64], F32)
            gk_rep = b.sb("gk_rep", [128, 2, 64], F32)
            b.op("act", lambda e: e.activation(out=gq_rep[:], in_=gq[:, None, :].to_broadcast([128, 8, 64]), func=AF.Copy, scale=0.125),
                 reads=[gq], writes=[gq_rep])
            b.op("act", lambda e: e.activation(out=gk_rep[:], in_=gk[:, None, :].to_broadcast([128, 2, 64]), func=AF.Copy, scale=1.0),
                 reads=[gk], writes=[gk_rep])
            if getattr(self, 'stop_at', 99) <= 0:
                return
            kcdup = b.sb("kcdup", [128, 2, S + 1], BF16)
            vcdup = b.sb("vcdup", [128, 2, S + 1], BF16)
            xt = [b.sb(f"xt{i}", [128, D], F32) for i in range(2)]
            junk = b.sb("junk", [128, D], BF16)
            ss = [b.sb(f"ss{i}", [128, 1], F32) for i in range(2)]
            hb = [b.sb(f"hb{i}", [128, D], BF16) for i in range(2)]
            hT = [b.sb(f"hT{i}", [128, 8, 128], BF16) for i in range(2)]
            sq = b.sb("sq", [128, 12, 64], F32)
            ssq = b.sb("ssq", [128, 12], F32)
            tmpq = b.sb("tmpq", [128, 8, 64], F32)
            tmpk = b.sb("tmpk", [128, 4, 64], F32)
            qb = b.sb("qb", [128, 512], BF16)
            kb = b.sb("kb", [128, 4, 64], BF16)
            cb = b.sb("cb", [128, 4, 2, 64], BF16)
            qst = [b.sb(f"qst{i}", [64, 8, 128], BF16) for i in range(2)]
            pt = b.ps("pt", [128, 8, 128], BF16)
            pm = [b.ps(f"pm{i}", [128, 512], F32) for i in range(3)]
            ptq = b.ps("ptq", [128, 8, 128], BF16)
            ptk = b.ps("ptk", [128, 8, 128], BF16)
            colgroups = [(0, 512), (512, 1024), (1024, RW0)]
            for t in range(getattr(self, 'nt_limit', NT)):
                i = t % 2
                self.make_hT(I["x"], t, xt[i], junk, ss[i], hb[i], pt, hT[i], self.ident)
                for n, (c0, c1) in enumerate(colgroups):
                    for c in range(8):
                        b.op("pe", lambda e: e.matmul(pm[n][:, :c1 - c0], lhsT=hT[i][:, c, :], rhs=wn[:, c, c0:c1],
                                                      start=(c == 0), stop=(c == 7)), reads=[hT[i], wn], writes=[pm[n]])
                if getattr(self, 'stop_at', 99) <= 1:
                    continue
                b.op("act", lambda e: e.activation(out=sq[:, 0:8, :], in_=pm[0][:, 0:512].rearrange("p (h d) -> p h d", d=64), func=AF.Square),
                     reads=[pm[0]], writes=[sq])
                b.op("act", lambda e: e.activation(out=sq[:, 8:10, :], in_=pm[1][:, 256:384].rearrange("p (h d) -> p h d", d=64), func=AF.Square),
                     reads=[pm[1]], writes=[sq])
                b.op("act", lambda e: e.activation(out=sq[:, 10:12, :], in_=pm[2][:, 0:128].rearrange("p (h d) -> p h d", d=64), func=AF.Square),
                     reads=[pm[2]], writes=[sq])
                b.op("dve", lambda e: e.tensor_reduce(out=ssq[:], in_=sq[:], axis=AX.X, op=ALU.add), reads=[sq], writes=[ssq])
                b.op("act", lambda e: e.activation(out=ssq[:], in_=ssq[:], func=AF.Sqrt, scale=1.0 / 64, bias=RMS_EPS), reads=[ssq], writes=[ssq])
                b.op("dve", lambda e: e.reciprocal(out=ssq[:], in_=ssq[:]), reads=[ssq], writes=[ssq])
                if getattr(self, 'stop_at', 99) <= 2:
                    continue
                b.op("dve", lambda e: e.tensor_tensor(out=tmpq[:], in0=pm[0][:, 0:512].rearrange("p (h d) -> p h d", d=64),
                                                      in1=ssq[:, 0:8].unsqueeze(2).to_broadcast([128, 8, 64]), op=ALU.mult),
                     reads=[pm[0], ssq], writes=[tmpq])
                b.op("pool", lambda e: e.tensor_tensor(out=qb[:].rearrange("p (h d) -> p h d", d=64), in0=tmpq[:], in1=gq_rep[:], op=ALU.mult),
                     reads=[tmpq, gq_rep], writes=[qb])
                for h in range(8):
                    b.op("pe", lambda e: e.transpose(out=ptq[0:64, h, :], in_=qb[:, h * 64:(h + 1) * 64], identity=self.ident[:]),
                         reads=[qb, self.ident], writes=[ptq])
                b.op("act", lambda e: e.copy(out=qst[i][:], in_=ptq[0:64, :, :]), reads=[ptq], writes=[qst[i]])
                b.dma("pool", self.qT_d[:, :, t * 128:(t + 1) * 128].rearrange("h d t -> d h t"), qst[i][:], reads=[qst[i]], writes=[self.qT_d])
                if getattr(self, 'stop_at', 99) <= 3:
                    continue
                b.op("dve", lambda e: e.tensor_tensor(out=tmpk[:, 0:2, :], in0=pm[1][:, 256:384].rearrange("p (h d) -> p h d", d=64),
                                                      in1=ssq[:, 8:10].unsqueeze(2).to_broadcast([128, 2, 64]), op=ALU.mult),
                     reads=[pm[1], ssq], writes=[tmpk])
                b.op("dve", lambda e: e.tensor_tensor(out=tmpk[:, 2:4, :], in0=pm[2][:, 0:128].rearrange("p (h d) -> p h d", d=64),
                                                      in1=ssq[:, 10:12].unsqueeze(2).to_broadcast([128, 2, 64]), op=ALU.mult),
                     reads=[pm[2], ssq], writes=[tmpk])
                b.op("pool", lambda e: e.tensor_tensor(out=kb[:].rearrange("p (a g) d -> p a g d", a=2), in0=tmpk[:].rearrange("p (a g) d -> p a g d", a=2),
                                                       in1=gk_rep[:, None, :, :].to_broadcast([128, 2, 2, 64]), op=ALU.mult),
                     reads=[tmpk, gk_rep], writes=[kb])
                for j in range(4):
                    b.op("pe", lambda e: e.transpose(out=ptk[0:64, j, :], in_=kb[:, j, :], identity=self.ident[:]),
                         reads=[kb, self.ident], writes=[ptk])
                if getattr(self, 'stop_at', 99) <= 4:
                    continue
                for du in range(2):
                    b.op("act", lambda e: e.copy(out=cb[:, :, du, :], in_=pm[1][:, 0:256].rearrange("p (a d) -> p a d", d=64)),
                         reads=[pm[1]], writes=[cb])
                for j in range(4):
                    b.op("pe", lambda e: e.transpose(out=ptk[:, 4 + j, :], in_=cb[:, j, :, :].rearrange("p a d -> p (a d)"), identity=self.ident[:]),
                         reads=[cb, self.ident], writes=[ptk])
                c0 = t * 128
                b.op("dve", lambda e: e.tensor_copy(out=self.ksE[0:64, :, c0:c0 + 128], in_=ptk[0:64, 0:2, :]), reads=[ptk], writes=[self.ksE])
                b.op("dve", lambda e: e.tensor_copy(out=self.kwT[0:64, :, c0:c0 + 128], in_=ptk[0:64, 2:4, :]), reads=[ptk], writes=[self.kwT])
                b.op("act", lambda e: e.copy(out=kcdup[0:64, :, 1 + c0:1 + c0 + 128], in_=ptk[0:64, 4:6, :]), reads=[ptk], writes=[kcdup])
                b.op("act", lambda e: e.copy(out=kcdup[64:128, :, c0:c0 + 128], in_=ptk[64:128, 4:6, :]), reads=[ptk], writes=[kcdup])
                b.op("dve", lambda e: e.tensor_copy(out=vcdup[0:64, :, 1 + c0:1 + c0 + 128], in_=ptk[0:64, 6:8, :]), reads=[ptk], writes=[vcdup])
                b.op("dve", lambda e: e.tensor_copy(out=vcdup[64:128, :, c0:c0 + 128], in_=ptk[64:128, 6:8, :]), reads=[ptk], writes=[vcdup])
                if getattr(self, 'stop_at', 99) <= 5:
                    continue
                b.op("act", lambda e: e.copy(out=self.vaug_s[:, t, :, 0:64], in_=pm[1][:, 384:512].rearrange("p (g d) -> p g d", d=64)),
                     reads=[pm[1]], writes=[self.vaug_s])
                b.op("act", lambda e: e.copy(out=self.vaug_w[:, t, :, 0:64], in_=pm[2][:, 128:256].rearrange("p (g d) -> p g d", d=64)),
                     reads=[pm[2]], writes=[self.vaug_w])
                b.op("act", lambda e: e.activation(out=self.gts[:, t, :], in_=pm[2][:, 256:280], func=AF.Sigmoid), reads=[pm[2]], writes=[self.gts])
            if "nsa_proj" in self.debug:
                d = self.dbg_out("ksE", [128, 2, S], BF16)
                b.dma("pool", d, self.ksE[:], reads=[self.ksE])
                d = self.dbg_out("kcdup", [128, 2, S + 1], BF16)
                b.dma("pool", d, kcdup[:], reads=[kcdup])
                d = self.dbg_out("vaug_w", [128, NT, 2, 65], BF16)
                b.dma("pool", d, self.vaug_w[:], reads=[self.vaug_w])
                d = self.dbg_out("gts", [128, NT, 24], F32)
                b.dma("pool", d, self.gts[:], reads=[self.gts])
            if not getattr(self, 'skip_compress', False):
                self.compress(kcdup, vcdup, gk_rep, [pm[0], pm[1]], pm[2], ptk)

    def compress(self, kcdup, vcdup, gk_rep, ph, po, ptc):
        b = self.b
        I = self.inp
        C2 = 2.0 * 0.7978845608028654
        w1 = b.sb("w1", [128, 16, 256], BF16)
        w2 = b.sb("w2", [128, 2, 64], BF16)
        w1st = [b.sb(f"w1st{i}", [128, 256], F32) for i in range(2)]
        peT = b.sb("peT", [128, 16], F32)
        peTb = b.sb("peTb", [128, 16], BF16)
        hTc = b.sb("hTc", [128, 2, 256], BF16)
        pbias = b.sb("pbias", [128, 2], F32)
        xh = b.sb("xh", [128, 255], F32)
        x2 = b.sb("x2", [128, 255], F32)
        sg = b.sb("sg", [128, 255], F32)
        ctmp = b.sb("ctmp", [128, 64], F32)
        csq = b.sb("csq", [128, 64], F32)
        cs1 = b.sb("cs1", [128, 1], F32)
        kcb = b.sb("kcb", [128, 64], BF16)
        b.op("pool", lambda e: e.memset(hTc[:], 0.0), writes=[hTc])
        for kv, (dup, pe_n, w1_n, w2_n) in enumerate([(kcdup, "cmp_pe_k", "cmp_w1_k", "cmp_w2_k"), (vcdup, "cmp_pe_v", "cmp_w1_v", "cmp_w2_v")]):
            self.load_weight(w1, I[w1_n][0], 256, kch=16, stage=w1st, eng="dve")
            self.load_weight(w2, I[w2_n][0], 64, kch=2, stage=w1st, eng="dve")
            for two in range(2):
                b.dma("sp", peT[two * 64:(two + 1) * 64, :], I[pe_n][0].rearrange("(pp two) d -> two d pp", two=2)[two],
                      writes=[peT], allow_slow_non_contiguous=True)
            b.op("dve", lambda e: e.tensor_copy(out=peTb[:], in_=peT[:]), reads=[peT], writes=[peTb])
            for ft in range(2):
                for pp in range(16):
                    b.op("pe", lambda e: e.matmul(po[:, ft:ft + 1], lhsT=w1[:, pp, ft * 128:(ft + 1) * 128], rhs=peTb[:, pp:pp + 1],
                                                  start=(pp == 0), stop=(pp == 15)), reads=[w1, peTb], writes=[po])
            b.op("dve", lambda e: e.tensor_copy(out=pbias[:], in_=po[:, 0:2]), reads=[po], writes=[pbias])
            for g in range(2):
                for ft in range(2):
                    p = ph[ft]
                    for pp in range(16):
                        b.op("pe", lambda e: e.matmul(p[:, 0:255], lhsT=w1[:, pp, ft * 128:(ft + 1) * 128],
                                                      rhs=dup[:, g, 1 + 2 * pp:1 + 2 * pp + 16 * 254 + 1:16],
                                                      start=(pp == 0), stop=(pp == 15)), reads=[w1, dup], writes=[p])
                    b.op("act", lambda e: e.activation(out=xh[:], in_=p[:, 0:255], func=AF.Identity, bias=pbias[:, ft:ft + 1]), reads=[p, pbias], writes=[xh])
                    b.op("dve", lambda e: e.tensor_tensor(out=x2[:], in0=xh[:], in1=xh[:], op=ALU.mult), reads=[xh], writes=[x2])
                    b.op("dve", lambda e: e.tensor_scalar(out=x2[:], in0=x2[:], scalar1=0.044715, scalar2=1.0, op0=ALU.mult, op1=ALU.add), reads=[x2], writes=[x2])
                    b.op("dve", lambda e: e.tensor_tensor(out=x2[:], in0=x2[:], in1=xh[:], op=ALU.mult), reads=[x2, xh], writes=[x2])
                    b.op("act", lambda e: e.activation(out=sg[:], in_=x2[:], func=AF.Sigmoid, scale=C2), reads=[x2], writes=[sg])
                    b.op("dve", lambda e: e.tensor_tensor(out=hTc[:, ft, 0:255], in0=xh[:], in1=sg[:], op=ALU.mult), reads=[xh, sg], writes=[hTc])
                for ct in range(2):
                    for ft in range(2):
                        b.op("pe", lambda e: e.matmul(po[:, 64:128], lhsT=hTc[:, ft, ct * 128:(ct + 1) * 128], rhs=w2[:, ft, :],
                                                      start=(ft == 0), stop=(ft == 1)), reads=[hTc, w2], writes=[po])
                    if kv == 0:
                        b.op("act", lambda e: e.activation(out=csq[:], in_=po[:, 64:128], func=AF.Square, accum_out=cs1[:]), reads=[po], writes=[csq, cs1])
                        b.op("act", lambda e: e.activation(out=cs1[:], in_=cs1[:], func=AF.Sqrt, scale=1.0 / 64, bias=RMS_EPS), reads=[cs1], writes=[cs1])
                        b.op("dve", lambda e: e.reciprocal(out=cs1[:], in_=cs1[:]), reads=[cs1], writes=[cs1])
                        b.op("dve", lambda e: e.tensor_scalar_mul(out=ctmp[:], in0=po[:, 64:128], scalar1=cs1[:]), reads=[po, cs1], writes=[ctmp])
                        b.op("dve", lambda e: e.tensor_tensor(out=kcb[:], in0=ctmp[:], in1=gk_rep[:, 0, :], op=ALU.mult), reads=[ctmp, gk_rep], writes=[kcb])
                        b.op("pe", lambda e: e.transpose(out=ptc[0:64, 0, :], in_=kcb[:], identity=self.ident[:]), reads=[kcb, self.ident], writes=[ptc])
                        b.op("dve", lambda e: e.tensor_copy(out=self.kcT[:, g, ct * 128:(ct + 1) * 128], in_=ptc[0:64, 0, :]), reads=[ptc], writes=[self.kcT])
                    else:
                        b.op("dve", lambda e: e.tensor_copy(out=self.vcA[:, g, ct, 0:64], in_=po[:, 64:128]), reads=[po], writes=[self.vcA])
        if "compress" in self.debug:
            d = self.dbg_out("kcT", [64, 2, 256], BF16)
            b.dma("pool", d, self.kcT[:], reads=[self.kcT])
            d = self.dbg_out("vcA", [128, 2, 2, 129], F32)
            b.dma("pool", d, self.vcA[:], reads=[self.vcA])

    def finish(self):
        b = self.b
        b.wait_all_on("pool")
        b.barrier()
        b.close()
        return self.nc


def _phase_attn(self):
    b = self.b
    I = self.inp
    with b.scope():
        tw = b.sb("tw", [128, 8, 640], F32)
        ts = b.sb("ts", [128, 8, 640], F32)
        b.dma("sp", tw[:], I["tw"], writes=[tw])
        b.dma("sp", ts[:], I["ts"], writes=[ts])
        candneg = b.sb("candneg", [128, 32, 64], F32)
        fz = b.sb("fz", [128, 32, 64], F32)
        b.dma("sp", candneg[:], I["candneg"], writes=[candneg])
        b.dma("sp", fz[:], I["fz"], writes=[fz])
        b31 = b.sb("b31", [128, 8], F32)
        b.dma("sp", b31[:], I["b31"], writes=[b31])
        kwp = b.sb("kwp", [128, 2, S], BF16)
        b.op("pool", lambda e: e.memset(kwp[64:128, :, :], 0.0), writes=[kwp])
        b.op("pool", lambda e: e.tensor_copy(out=kwp[0:64, :, :], in_=self.kwT[:]), reads=[self.kwT], writes=[kwp])
        kcp = b.sb("kcp", [128, 2, 256], BF16)
        b.op("pool", lambda e: e.memset(kcp[64:128, :, :], 0.0), writes=[kcp])
        b.op("pool", lambda e: e.tensor_copy(out=kcp[0:64, :, :], in_=self.kcT[:]), reads=[self.kcT], writes=[kcp])
        zer = b.sb("zer", [128, 512], BF16)
        b.op("pool", lambda e: e.memset(zer[:], 0.0), writes=[zer])
        qm = [b.sb(f"qm{i}", [128, 8, 512], BF16) for i in range(2)]
        bct = [b.sb(f"bct{i}", [128, 512], F32) for i in range(3)]
        scf = [b.sb(f"scf{i}", [128, 640], F32) for i in range(2)]
        pcT = [b.sb(f"pcT{i}", [128, 2, 512], F32) for i in range(2)]
        pT = [b.sb(f"pT{i}", [128, 640], BF16) for i in range(3)]
        oacc = b.sb("oacc", [128, 4, 512], F32)
        imp = b.sb("imp", [128, 4, 2, 64], F32)
        impm = b.sb("impm", [128, 64], F32)
        impm2 = b.sb("impm2", [128, 64], F32)
        m8a = b.sb("m8a", [128, 8], F32)
        m8b = b.sb("m8b", [128, 8], F32)
        msk = b.sb("msk", [128, 64], F32)
        mb = b.sb("mb", [128, 128], BF16)
        b.op("pool", lambda e: e.memset(mb[:], 0.0), writes=[mb])
        rs = b.sb("rs", [128, 4], F32)
        rg = b.sb("rg", [128, 4], F32)
        oab = b.sb("oab", [128, 512], BF16)
        oaT = [b.sb(f"oaT{i}", [128, 4, 128], BF16) for i in range(2)]
        pS = [b.ps(f"pS{i}", [128, 512], F32) for i in range(2)]
        pS2 = b.ps("pS2", [128, 512], F32)
        pO = [b.ps(f"pO{i}", [128, 512], F32) for i in range(3)]
        pTr = b.ps("pTr", [128, 8, 128], BF16)
        nrot = {"bct": 0, "scf": 0, "pT": 0, "pS": 0}

        def rot(name, lst):
            nrot[name] += 1
            return lst[nrot[name] % len(lst)]

        def finalize(po, ncol_off, h, qs, branch, first):
            qt = qs_base + qs
            o0 = ncol_off
            b.op("dve", lambda e: e.tensor_scalar_max(out=rs[:, 0:1], in0=po[:, o0 + 64:o0 + 65], scalar1=1e-30), reads=[po], writes=[rs])
            b.op("dve", lambda e: e.reciprocal(out=rs[:, 1:2], in_=rs[:, 0:1]), reads=[rs], writes=[rs])
            b.op("dve", lambda e: e.tensor_tensor(out=rg[:, 0:1], in0=rs[:, 1:2], in1=self.gts[:, qt, h * 3 + branch:h * 3 + branch + 1], op=ALU.mult),
                 reads=[rs, self.gts], writes=[rg])
            if first:
                b.op("dve", lambda e: e.tensor_scalar_mul(out=oacc[:, qs, h * 64:(h + 1) * 64], in0=po[:, o0:o0 + 64], scalar1=rg[:, 0:1]),
                     reads=[po, rg], writes=[oacc])
            else:
                b.op("dve", lambda e: e.scalar_tensor_tensor(out=oacc[:, qs, h * 64:(h + 1) * 64], in0=po[:, o0:o0 + 64], scalar=rg[:, 0:1],
                                                             in1=oacc[:, qs, h * 64:(h + 1) * 64], op0=ALU.mult, op1=ALU.add),
                     reads=[po, rg, oacc], writes=[oacc])

        nqg = getattr(self, "nqg_limit", 8)
        for qg in range(nqg):
            qs_base = 4 * qg
            q0 = 512 * qg
            Q = qm[qg % 2]
            b.dma("sp", Q[0:64, :, :], self.qT_d[:, :, q0:q0 + 512].rearrange("h d t -> d h t"), reads=[self.qT_d], writes=[Q])
            if qg < 2:
                b.op("pool", lambda e: e.memset(Q[64:128, :, :], 0.0), writes=[Q])
            for h in range(8):
                g = h // 4
                pc = pcT[h % 2]
                for ct in range(2):
                    p = rot("pS", pS)
                    b.op("pe", lambda e: e.matmul(p[:, :], lhsT=kcp[:, g, ct * 128:(ct + 1) * 128], rhs=Q[:, h, :], start=True, stop=True),
                         reads=[kcp, Q], writes=[p])
                    bt = rot("bct", bct)
                    b.dma("sp", bt[:], I["biasc"][h, ct, :, q0:q0 + 512], writes=[bt])
                    sc = rot("scf", scf)
                    b.op("dve", lambda e: e.tensor_tensor(out=sc[:, 0:512], in0=p[:, :], in1=bt[:], op=ALU.add), reads=[p, bt], writes=[sc])
                    b.op("act", lambda e: e.activation(out=pc[:, ct, :], in_=sc[:, 0:512], func=AF.Exp), reads=[sc], writes=[pc])
                po = pO[0]
                for qs in range(4):
                    for ct in range(2):
                        b.op("pe", lambda e: e.matmul(po[:, qs * 128:qs * 128 + 129] if False else po[:, 0:129], lhsT=pc[:, ct, qs * 128:(qs + 1) * 128],
                                                      rhs=self.vcA[:, g, ct, :], start=(ct == 0), stop=(ct == 1)), reads=[pc, self.vcA], writes=[po])
                    finalize(po, 0, h, qs, 0, True)
                    if h % 4 == 0:
                        b.op("dve", lambda e: e.tensor_scalar_mul(out=imp[:, qs, g, :], in0=po[:, 65:129], scalar1=rs[:, 1:2]), reads=[po, rs], writes=[imp])
                    else:
                        b.op("dve", lambda e: e.scalar_tensor_tensor(out=imp[:, qs, g, :], in0=po[:, 65:129], scalar=rs[:, 1:2], in1=imp[:, qs, g, :],
                                                                     op0=ALU.mult, op1=ALU.add), reads=[po, rs, imp], writes=[imp])
            if qg >= 2:
                for qs in range(4):
                    qt = qs_base + qs
                    for g in range(2):
                        b.op("dve", lambda e: e.tensor_tensor(out=impm[:], in0=imp[:, qs, g, :], in1=candneg[:, qt, :], op=ALU.add), reads=[imp, candneg], writes=[impm])
                        b.op("dve", lambda e: e.max(out=m8a[:], in_=impm[:]), reads=[impm], writes=[m8a])
                        b.op("dve", lambda e: e.match_replace(out=impm2[:], in_to_replace=m8a[:], in_values=impm[:], imm_value=-1e9), reads=[m8a, impm], writes=[impm2])
                        b.op("dve", lambda e: e.max(out=m8b[:], in_=impm2[:]), reads=[impm2], writes=[m8b])
                        b.op("dve", lambda e: e.tensor_scalar(out=msk[:], in0=impm[:], scalar1=m8b[:, 4:5], scalar2=None, op0=ALU.is_ge), reads=[impm, m8b], writes=[msk])
                        b.op("dve", lambda e: e.tensor_tensor(out=msk[:], in0=msk[:], in1=fz[:, qt, :], op=ALU.max), reads=[msk, fz], writes=[msk])
                        b.op("dve", lambda e: e.tensor_scalar(out=mb[:, 64:128], in0=msk[:], scalar1=-NEG, scalar2=NEG, op0=ALU.mult, op1=ALU.add), reads=[msk], writes=[mb])
                        b.op("pe", lambda e: e.transpose(out=pTr[:, 0, :], in_=mb[:], identity=self.ident[:]), reads=[mb, self.ident], writes=[pTr])
                        b.op("act", lambda e: e.copy(out=Q[64:128, 4 * g:4 * g + 4, qs * 128:(qs + 1) * 128],
                                                     in_=pTr[64:128, 0:1, :].to_broadcast([64, 4, 128])), reads=[pTr], writes=[Q])
            for h in range(8):
                g = h // 4
                po_s, po_w = pO[1], pO[2]
                for po in (po_s, po_w):
                    b.op("pe", lambda e: e.matmul(po[:, 0:260], lhsT=zer[:, 0:128], rhs=zer[:, 0:260], start=True, stop=True), reads=[zer], writes=[po])
                nkt = 4 * (qg + 1)
                for kt in range(nkt):
                    dlt = 4 * qg - kt
                    qstart = 0 if dlt >= 0 else -dlt * 128
                    N = 512 - qstart
                    p = rot("pS", pS)
                    b.op("pe", lambda e: e.matmul(p[:, 0:N], lhsT=self.ksE[:, g, kt * 128:(kt + 1) * 128], rhs=Q[:, h, qstart:512], start=True, stop=True),
                         reads=[self.ksE, Q], writes=[p])
                    pt_ = rot("pT", pT)
                    if dlt <= 1:
                        c0 = 128 if dlt == 1 else 0
                        sc = rot("scf", scf)
                        b.op("dve", lambda e: e.tensor_tensor(out=sc[:, 0:N], in0=p[:, 0:N], in1=ts[:, h, c0:c0 + N], op=ALU.add), reads=[p, ts], writes=[sc])
                        b.op("act", lambda e: e.activation(out=pt_[:, 0:N], in_=sc[:, 0:N], func=AF.Exp), reads=[sc], writes=[pt_])
                    else:
                        b.op("act", lambda e: e.activation(out=pt_[:, 0:N], in_=p[:, 0:N], func=AF.Exp, bias=b31[:, h:h + 1]), reads=[p, b31], writes=[pt_])
                    for qs in range(qstart // 128, 4):
                        o = qs * 128 - qstart
                        b.op("pe", lambda e: e.matmul(po_s[:, qs * 65:(qs + 1) * 65], lhsT=pt_[:, o:o + 128], rhs=self.vaug_s[:, kt, g, :],
                                                      start=False, stop=(kt == nkt - 1), skip_group_check=True), reads=[pt_, self.vaug_s], writes=[po_s])
                kts = [kt for kt in range(4 * qg - 4, 4 * qg + 4) if kt >= 0]
                for kt in kts:
                    qs_lo = max(0, kt - 4 * qg)
                    qs_hi = min(3, kt + 4 - 4 * qg)
                    N = (qs_hi - qs_lo + 1) * 128
                    c0 = 128 * (4 * qg + qs_lo - kt)
                    p = rot("pS", pS)
                    b.op("pe", lambda e: e.matmul(p[:, 0:N], lhsT=kwp[:, g, kt * 128:(kt + 1) * 128], rhs=Q[:, h, qs_lo * 128:(qs_hi + 1) * 128], start=True, stop=True),
                         reads=[kwp, Q], writes=[p])
                    sc = rot("scf", scf)
                    b.op("dve", lambda e: e.tensor_tensor(out=sc[:, 0:N], in0=p[:, 0:N], in1=tw[:, h, c0:c0 + N], op=ALU.add), reads=[p, tw], writes=[sc])
                    pt_ = rot("pT", pT)
                    b.op("act", lambda e: e.activation(out=pt_[:, 0:N], in_=sc[:, 0:N], func=AF.Exp), reads=[sc], writes=[pt_])
                    for qs in range(qs_lo, qs_hi + 1):
                        o = (qs - qs_lo) * 128
                        b.op("pe", lambda e: e.matmul(po_w[:, qs * 65:(qs + 1) * 65], lhsT=pt_[:, o:o + 128], rhs=self.vaug_w[:, kt, g, :],
                                                      start=False, stop=(kt == kts[-1]), skip_group_check=True), reads=[pt_, self.vaug_w], writes=[po_w])
                for qs in range(4):
                    finalize(po_s, qs * 65, h, qs, 1, False)
                    finalize(po_w, qs * 65, h, qs, 2, False)
            for qs in range(4):
                qt = qs_base + qs
                ot = oaT[qs % 2]
                b.op("act", lambda e: e.copy(out=oab[:], in_=oacc[:, qs, :]), reads=[oacc], writes=[oab])
                for c in range(4):
                    b.op("pe", lambda e: e.transpose(out=pTr[:, 4 + c, :], in_=oab[:, c * 128:(c + 1) * 128], identity=self.ident[:]), reads=[oab, self.ident], writes=[pTr])
                b.op("act", lambda e: e.copy(out=ot[:], in_=pTr[:, 4:8, :]), reads=[pTr], writes=[ot])
                b.dma("pool", self.oaT_d[:, :, qt * 128:(qt + 1) * 128].rearrange("c p t -> p c t"), ot[:], reads=[ot], writes=[self.oaT_d])
        if "attn" in self.debug:
            d = self.dbg_out("oaT", [4, 128, S], BF16)
            b.dma("pool", d, self.oaT_d[:], reads=[self.oaT_d])


Prog.phase_attn = _phase_attn


def _phase_attn2(self):
    b = self.b
    I = self.inp
    with b.scope():
        tw = b.sb("tw", [128, 8, 640], F32)
        ts = b.sb("ts", [128, 8, 640], F32)
        b.dma("sp", tw[:], I["tw"], writes=[tw])
        b.dma("sp", ts[:], I["ts"], writes=[ts])
        candneg = b.sb("candneg", [128, 32, 64], F32)
        fz = b.sb("fz", [128, 32, 64], F32)
        b.dma("sp", candneg[:], I["candneg"], writes=[candneg])
        b.dma("sp", fz[:], I["fz"], writes=[fz])
        b31 = b.sb("b31", [128, 8], F32)
        b.dma("sp", b31[:], I["b31"], writes=[b31])
        kwp = b.sb("kwp", [128, 2, S], BF16)
        b.op("pool", lambda e: e.memset(kwp[64:128, :, :], 0.0), writes=[kwp])
        b.op("pool", lambda e: e.tensor_copy(out=kwp[0:64, :, :], in_=self.kwT[:]), reads=[self.kwT], writes=[kwp])
        kcp = b.sb("kcp", [128, 2, 256], BF16)
        b.op("pool", lambda e: e.memset(kcp[64:128, :, :], 0.0), writes=[kcp])
        b.op("pool", lambda e: e.tensor_copy(out=kcp[0:64, :, :], in_=self.kcT[:]), reads=[self.kcT], writes=[kcp])
        zer = b.sb("zer", [128, 512], BF16)
        b.op("pool", lambda e: e.memset(zer[:], 0.0), writes=[zer])
        qm = [b.sb(f"qm{i}", [128, 8, 512], BF16) for i in range(2)]
        bct = [b.sb(f"bct{i}", [128, 512], F32) for i in range(3)]
        scf = [b.sb(f"scf{i}", [128, 640], F32) for i in range(3)]
        pcT = [b.sb(f"pcT{i}", [128, 2, 512], F32) for i in range(2)]
        pT = [b.sb(f"pT{i}", [128, 640], BF16) for i in range(4)]
        oacc = b.sb("oacc", [128, 4, 512], F32)
        imp = b.sb("imp", [128, 4, 2, 64], F32)
        impm = b.sb("impm", [128, 64], F32)
        impm2 = b.sb("impm2", [128, 64], F32)
        m8a = b.sb("m8a", [128, 8], F32)
        m8b = b.sb("m8b", [128, 8], F32)
        msk = b.sb("msk", [128, 64], F32)
        mb = b.sb("mb", [128, 128], BF16)
        b.op("pool", lambda e: e.memset(mb[:], 0.0), writes=[mb])
        rs = b.sb("rs", [128, 4], F32)
        rg = b.sb("rg", [128, 4], F32)
        oab = b.sb("oab", [128, 512], BF16)
        oaT = [b.sb(f"oaT{i}", [128, 4, 128], BF16) for i in range(2)]
        pS = [b.ps(f"pS{i}", [128, 512], F32) for i in range(3)]
        pOs = [b.ps(f"pOs{i}", [128, 512], F32) for i in range(2)]
        pOw = [b.ps(f"pOw{i}", [128, 512], F32) for i in range(2)]
        pTr = b.ps("pTr", [128, 8, 128], BF16)
        nrot = {"bct": 0, "scf": 0, "pT": 0, "pS": 0}

        def rot(name, lst):
            nrot[name] += 1
            return lst[nrot[name] % len(lst)]

        def finalize(po, ncol_off, h, qs, branch, first):
            qt = qs_base + qs
            o0 = ncol_off
            b.op("dve", lambda e: e.tensor_scalar_max(out=rs[:, 0:1], in0=po[:, o0 + 64:o0 + 65], scalar1=1e-30), reads=[po], writes=[rs])
            b.op("dve", lambda e: e.reciprocal(out=rs[:, 1:2], in_=rs[:, 0:1]), reads=[rs], writes=[rs])
            b.op("dve", lambda e: e.tensor_tensor(out=rg[:, 0:1], in0=rs[:, 1:2], in1=self.gts[:, qt, h * 3 + branch:h * 3 + branch + 1], op=ALU.mult),
                 reads=[rs, self.gts], writes=[rg])
            if first:
                b.op("dve", lambda e: e.tensor_scalar_mul(out=oacc[:, qs, h * 64:(h + 1) * 64], in0=po[:, o0:o0 + 64], scalar1=rg[:, 0:1]),
                     reads=[po, rg], writes=[oacc])
            else:
                b.op("dve", lambda e: e.scalar_tensor_tensor(out=oacc[:, qs, h * 64:(h + 1) * 64], in0=po[:, o0:o0 + 64], scalar=rg[:, 0:1],
                                                             in1=oacc[:, qs, h * 64:(h + 1) * 64], op0=ALU.mult, op1=ALU.add),
                     reads=[po, rg, oacc], writes=[oacc])

        nqg = getattr(self, "nqg_limit", 8)
        for qg in range(nqg):
            qs_base = 4 * qg
            q0 = 512 * qg
            Q = qm[qg % 2]
            b.dma("sp", Q[0:64, :, :], self.qT_d[:, :, q0:q0 + 512].rearrange("h d t -> d h t"), reads=[self.qT_d], writes=[Q])
            if qg < 2:
                b.op("pool", lambda e: e.memset(Q[64:128, :, :], 0.0), writes=[Q])
            for h in range(8):
                g = h // 4
                pc = pcT[h % 2]
                for ct in range(2):
                    p = rot("pS", pS)
                    b.op("pe", lambda e: e.matmul(p[:, :], lhsT=kcp[:, g, ct * 128:(ct + 1) * 128], rhs=Q[:, h, :], start=True, stop=True),
                         reads=[kcp, Q], writes=[p])
                    bt = rot("bct", bct)
                    b.dma("sp", bt[:], I["biasc"][h, ct, :, q0:q0 + 512], writes=[bt])
                    sc = rot("scf", scf)
                    b.op("dve", lambda e: e.tensor_tensor(out=sc[:, 0:512], in0=p[:, :], in1=bt[:], op=ALU.add), reads=[p, bt], writes=[sc])
                    b.op("act", lambda e: e.activation(out=pc[:, ct, :], in_=sc[:, 0:512], func=AF.Exp), reads=[sc], writes=[pc])
                po = pOs[h % 2]
                for qs in range(4):
                    for ct in range(2):
                        b.op("pe", lambda e: e.matmul(po[:, qs * 128:qs * 128 + 129] if False else po[:, 0:129], lhsT=pc[:, ct, qs * 128:(qs + 1) * 128],
                                                      rhs=self.vcA[:, g, ct, :], start=(ct == 0), stop=(ct == 1)), reads=[pc, self.vcA], writes=[po])
                    finalize(po, 0, h, qs, 0, True)
                    if h % 4 == 0:
                        b.op("dve", lambda e: e.tensor_scalar_mul(out=imp[:, qs, g, :], in0=po[:, 65:129], scalar1=rs[:, 1:2]), reads=[po, rs], writes=[imp])
                    else:
                        b.op("dve", lambda e: e.scalar_tensor_tensor(out=imp[:, qs, g, :], in0=po[:, 65:129], scalar=rs[:, 1:2], in1=imp[:, qs, g, :],
                                                                     op0=ALU.mult, op1=ALU.add), reads=[po, rs, imp], writes=[imp])
            if qg >= 2:
                for qs in range(4):
                    qt = qs_base + qs
                    for g in range(2):
                        b.op("dve", lambda e: e.tensor_tensor(out=impm[:], in0=imp[:, qs, g, :], in1=candneg[:, qt, :], op=ALU.add), reads=[imp, candneg], writes=[impm])
                        b.op("dve", lambda e: e.max(out=m8a[:], in_=impm[:]), reads=[impm], writes=[m8a])
                        b.op("dve", lambda e: e.match_replace(out=impm2[:], in_to_replace=m8a[:], in_values=impm[:], imm_value=-1e9), reads=[m8a, impm], writes=[impm2])
                        b.op("dve", lambda e: e.max(out=m8b[:], in_=impm2[:]), reads=[impm2], writes=[m8b])
                        b.op("dve", lambda e: e.tensor_scalar(out=msk[:], in0=impm[:], scalar1=m8b[:, 4:5], scalar2=None, op0=ALU.is_ge), reads=[impm, m8b], writes=[msk])
                        b.op("dve", lambda e: e.tensor_tensor(out=msk[:], in0=msk[:], in1=fz[:, qt, :], op=ALU.max), reads=[msk, fz], writes=[msk])
                        b.op("dve", lambda e: e.tensor_scalar(out=mb[:, 64:128], in0=msk[:], scalar1=-NEG, scalar2=NEG, op0=ALU.mult, op1=ALU.add), reads=[msk], writes=[mb])
                        b.op("pe", lambda e: e.transpose(out=pTr[:, 0, :], in_=mb[:], identity=self.ident[:]), reads=[mb, self.ident], writes=[pTr])
                        b.op("act", lambda e: e.copy(out=Q[64:128, 4 * g:4 * g + 4, qs * 128:(qs + 1) * 128],
                                                     in_=pTr[64:128, 0:1, :].to_broadcast([64, 4, 128])), reads=[pTr], writes=[Q])
            jobs = []
            for h in range(8):
                g = h // 4
                nkt = 4 * (qg + 1)
                for kt in range(nkt):
                    dlt = 4 * qg - kt
                    qstart = 0 if dlt >= 0 else -dlt * 128
                    jobs.append(dict(kind="s", h=h, g=g, kt=kt, qlo=qstart // 128, qhi=3, first=(kt == 0), last=False, lastkt=(kt == nkt - 1),
                                     tab=(ts, (128 if dlt == 1 else 0)) if dlt <= 1 else None))
                kts = [kt for kt in range(4 * qg - 4, 4 * qg + 4) if kt >= 0]
                for kt in kts:
                    qs_lo = max(0, kt - 4 * qg)
                    qs_hi = min(3, kt + 4 - 4 * qg)
                    jobs.append(dict(kind="w", h=h, g=g, kt=kt, qlo=qs_lo, qhi=qs_hi, first=False, last=(kt == kts[-1]), lastkt=(kt == kts[-1]),
                                     tab=(tw, 128 * (4 * qg + qs_lo - kt))))

            def emitS(j):
                h, g, kt = j["h"], j["g"], j["kt"]
                N = (j["qhi"] - j["qlo"] + 1) * 128
                p = rot("pS", pS)
                kmat = self.ksE if j["kind"] == "s" else kwp
                b.op("pe", lambda e: e.matmul(p[:, 0:N], lhsT=kmat[:, g, kt * 128:(kt + 1) * 128], rhs=Q[:, h, j["qlo"] * 128:(j["qhi"] + 1) * 128], start=True, stop=True),
                     reads=[kmat, Q], writes=[p])
                j["p"] = p
                j["N"] = N

            def emitE(j):
                h = j["h"]
                p, N = j["p"], j["N"]
                pt_ = rot("pT", pT)
                if j["tab"] is not None:
                    tab, c0 = j["tab"]
                    sc = rot("scf", scf)
                    b.op("dve", lambda e: e.tensor_tensor(out=sc[:, 0:N], in0=p[:, 0:N], in1=tab[:, h, c0:c0 + N], op=ALU.add), reads=[p, tab], writes=[sc])
                    b.op("act", lambda e: e.activation(out=pt_[:, 0:N], in_=sc[:, 0:N], func=AF.Exp), reads=[sc], writes=[pt_])
                else:
                    b.op("act", lambda e: e.activation(out=pt_[:, 0:N], in_=p[:, 0:N], func=AF.Exp, bias=b31[:, h:h + 1]), reads=[p, b31], writes=[pt_])
                j["pt"] = pt_

            def emitPV(j):
                h, g, kt = j["h"], j["g"], j["kt"]
                po_s, po_w = pOs[h % 2], pOw[h % 2]
                if j["first"]:
                    for po in (po_s, po_w):
                        b.op("pe", lambda e: e.matmul(po[:, 0:260], lhsT=zer[:, 0:128], rhs=zer[:, 0:260], start=True, stop=True), reads=[zer], writes=[po])
                po = po_s if j["kind"] == "s" else po_w
                va = self.vaug_s if j["kind"] == "s" else self.vaug_w
                for qs in range(j["qlo"], j["qhi"] + 1):
                    o = (qs - j["qlo"]) * 128
                    b.op("pe", lambda e: e.matmul(po[:, qs * 65:(qs + 1) * 65], lhsT=j["pt"][:, o:o + 128], rhs=va[:, kt, g, :],
                                                  start=False, stop=j["lastkt"], skip_group_check=True), reads=[j["pt"], va], writes=[po])
                if j["last"]:
                    for qs in range(4):
                        finalize(po_s, qs * 65, h, qs, 1, False)
                        finalize(po_w, qs * 65, h, qs, 2, False)

            LA = 2
            for i_ in range(len(jobs) + LA):
                if i_ < len(jobs):
                    emitS(jobs[i_])
                if i_ >= LA:
                    emitE(jobs[i_ - LA])
                    emitPV(jobs[i_ - LA])
            for qs in range(4):
                qt = qs_base + qs
                ot = oaT[qs % 2]
                b.op("act", lambda e: e.copy(out=oab[:], in_=oacc[:, qs, :]), reads=[oacc], writes=[oab])
                for c in range(4):
                    b.op("pe", lambda e: e.transpose(out=pTr[:, 4 + c, :], in_=oab[:, c * 128:(c + 1) * 128], identity=self.ident[:]), reads=[oab, self.ident], writes=[pTr])
                b.op("act", lambda e: e.copy(out=ot[:], in_=pTr[:, 4:8, :]), reads=[pTr], writes=[ot])
                b.dma("pool", self.oaT_d[:, :, qt * 128:(qt + 1) * 128].rearrange("c p t -> p c t"), ot[:], reads=[ot], writes=[self.oaT_d])
        if "attn" in self.debug:
            d = self.dbg_out("oaT", [4, 128, S], BF16)
            b.dma("pool", d, self.oaT_d[:], reads=[self.oaT_d])


Prog.phase_attn2 = _phase_attn2


def _phase_merge(self):
    b = self.b
    I = self.inp
    self.x1_d = b.dram("x1_d", [S, D], F32)
    with b.scope():
        gat = self.load_gain("gat2", I["attn_norm_g"][0])
        stage = [b.sb(f"mst{i}", [128, 1024], F32) for i in range(2)]
        wg = b.sb("wg", [128, 8, 2048], BF16)
        for n in range(2):
            for c in range(8):
                st = stage[c % 2]
                b.dma("sp", st[:], I["w_in"][0][c * 128:(c + 1) * 128, GA0 + n * 1024:GA0 + (n + 1) * 1024], writes=[st])
                b.op("act", lambda e: e.activation(out=wg[:, c, n * 1024:(n + 1) * 1024], in_=st[:], func=AF.Copy, scale=gat[:, c:c + 1]),
                     reads=[st, gat], writes=[wg])
        wa = b.sb("wa", [128, 4, 1024], BF16)
        wb = b.sb("wb", [128, 4, 1024], BF16)
        wo = b.sb("wo", [128, 8, 1024], BF16)
        self.load_weight(wa, I["w_proj_a"][0], 1024, kch=4, stage=stage, eng="dve")
        self.load_weight(wb, I["w_proj_b"][0], 1024, kch=4, stage=stage, eng="dve")
        self.load_weight(wo, I["w_out"][0], 1024, kch=8, stage=stage, eng="dve")
        xt = [b.sb(f"mxt{i}", [128, D], F32) for i in range(2)]
        junk = b.sb("mjunk", [128, D], BF16)
        ss = [b.sb(f"mss{i}", [128, 1], F32) for i in range(2)]
        hb = [b.sb(f"mhb{i}", [128, D], BF16) for i in range(2)]
        hT = [b.sb(f"mhT{i}", [128, 8, 128], BF16) for i in range(2)]
        oat = [b.sb(f"oat{i}", [128, 4, 128], BF16) for i in range(2)]
        obt = [b.sb(f"obt{i}", [128, 4, 128], BF16) for i in range(2)]
        sg = b.sb("msg", [128, 2048], F32)
        m1 = b.sb("m1", [128, 1024], F32)
        m2 = b.sb("m2", [128, 1024], F32)
        mgb = b.sb("mgb", [128, 1024], BF16)
        mT = b.sb("mT", [128, 8, 128], BF16)
        x1t = [b.sb(f"x1t{i}", [128, D], F32) for i in range(2)]
        pt = b.ps("mpt", [128, 8, 128], BF16)
        pg = [b.ps(f"mpg{i}", [128, 512], F32) for i in range(2)]
        pa = [b.ps(f"mpa{i}", [128, 512], F32) for i in range(2)]
        pb = [b.ps(f"mpb{i}", [128, 512], F32) for i in range(2)]
        for t in range(getattr(self, "nt_limit", NT)):
            i = t % 2
            self.make_hT(I["x"], t, xt[i], junk, ss[i], hb[i], pt, hT[i], self.ident)
            b.dma("sp", oat[i][:], self.oaT_d[:, :, t * 128:(t + 1) * 128].rearrange("c p t -> p c t"), reads=[self.oaT_d], writes=[oat[i]])
            b.dma("sp", obt[i][:], self.obT_d[:, :, t * 128:(t + 1) * 128].rearrange("c p t -> p c t"), reads=[self.obT_d], writes=[obt[i]])
            for n in range(4):
                p = pg[n % 2]
                for c in range(8):
                    b.op("pe", lambda e: e.matmul(p[:, :], lhsT=hT[i][:, c, :], rhs=wg[:, c, n * 512:(n + 1) * 512], start=(c == 0), stop=(c == 7)),
                         reads=[hT[i], wg], writes=[p])
                b.op("act", lambda e: e.activation(out=sg[:, n * 512:(n + 1) * 512], in_=p[:, :], func=AF.Sigmoid), reads=[p], writes=[sg])
            for n in range(2):
                for c in range(4):
                    b.op("pe", lambda e: e.matmul(pa[n][:, :], lhsT=oat[i][:, c, :], rhs=wa[:, c, n * 512:(n + 1) * 512], start=(c == 0), stop=(c == 3)),
                         reads=[oat[i], wa], writes=[pa[n]])
                for c in range(4):
                    b.op("pe", lambda e: e.matmul(pb[n][:, :], lhsT=obt[i][:, c, :], rhs=wb[:, c, n * 512:(n + 1) * 512], start=(c == 0), stop=(c == 3)),
                         reads=[obt[i], wb], writes=[pb[n]])
                b.op("dve", lambda e: e.tensor_tensor(out=m1[:, n * 512:(n + 1) * 512], in0=pa[n][:, :], in1=sg[:, n * 512:(n + 1) * 512], op=ALU.mult),
                     reads=[pa[n], sg], writes=[m1])
                b.op("dve", lambda e: e.tensor_tensor(out=m2[:, n * 512:(n + 1) * 512], in0=pb[n][:, :], in1=sg[:, 1024 + n * 512:1024 + (n + 1) * 512], op=ALU.mult),
                     reads=[pb[n], sg], writes=[m2])
            b.op("pool", lambda e: e.tensor_tensor(out=mgb[:], in0=m1[:], in1=m2[:], op=ALU.add), reads=[m1, m2], writes=[mgb])
            for c in range(8):
                b.op("pe", lambda e: e.transpose(out=pt[:, c, :], in_=mgb[:, c * 128:(c + 1) * 128], identity=self.ident[:]), reads=[mgb, self.ident], writes=[pt])
            b.op("act", lambda e: e.copy(out=mT[:], in_=pt[:]), reads=[pt], writes=[mT])
            for n in range(2):
                for c in range(8):
                    b.op("pe", lambda e: e.matmul(pa[n][:, :], lhsT=mT[:, c, :], rhs=wo[:, c, n * 512:(n + 1) * 512], start=(c == 0), stop=(c == 7)),
                         reads=[mT, wo], writes=[pa[n]])
                b.op("dve", lambda e: e.tensor_tensor(out=x1t[i][:, n * 512:(n + 1) * 512], in0=pa[n][:, :], in1=xt[i][:, n * 512:(n + 1) * 512], op=ALU.add),
                     reads=[pa[n], xt[i]], writes=[x1t[i]])
            b.dma("pool", self.x1_d[t * 128:(t + 1) * 128, :], x1t[i][:], reads=[x1t[i]], writes=[self.x1_d])
        if "merge" in self.debug:
            d = self.dbg_out("x1", [S, D], F32)
            b.dma("pool", d, self.x1_d[:], reads=[self.x1_d])


def _phase_ffn(self):
    b = self.b
    I = self.inp
    TG = 128
    NFT = 44
    with b.scope():
        gf = self.load_gain("gf", I["ffn_norm_g"][0])
        stage = [b.sb(f"fst{i}", [128, 1024], F32) for i in range(2)]
        wu = b.sb("wu", [128, 8, 2 * DFF], BF16)
        for n in range(8):
            for c in range(8):
                st = stage[c % 2]
                b.dma("sp", st[:, 0:704], I["w_up"][0][c * 128:(c + 1) * 128, n * 704:(n + 1) * 704], writes=[st])
                b.op("act", lambda e: e.activation(out=wu[:, c, n * 704:(n + 1) * 704], in_=st[:, 0:704], func=AF.Copy, scale=gf[:, c:c + 1]),
                     reads=[st, gf], writes=[wu])
        wd = b.sb("wd", [128, 22, D], BF16)
        self.load_weight(wd, I["w_down"][0], D, kch=22, stage=stage, eng="dve")
        cw = b.sb("cw", [128, 3, NFT], F32)
        for j in range(3):
            b.dma("sp", cw[:, j, :], I["conv_w"][0][j].rearrange("(c p) -> p c", p=128), writes=[cw], allow_slow_non_contiguous=True)
        cbias = self.load_gain("cbias", I["conv_b"][0], kch=NFT)
        carry = b.sb("carry", [128, NFT, 2], F32)
        b.op("pool", lambda e: e.memset(carry[:], 0.0), writes=[carry])
        xt = [b.sb(f"fxt{i}", [128, D], F32) for i in range(2)]
        junk = b.sb("fjunk", [128, D], BF16)
        ss = [b.sb(f"fss{i}", [128, 1], F32) for i in range(2)]
        hb = [b.sb(f"fhb{i}", [128, D], BF16) for i in range(2)]
        hT1 = [b.sb(f"fhT{i}", [128, 8, 128], BF16) for i in range(2)]
        hTg = b.sb("fhTg", [128, 8, TG], BF16)
        ub = [b.sb(f"ub{i}", [128, TG + 2], F32) for i in range(2)]
        cv = [b.sb(f"cv{i}", [128, TG], F32) for i in range(2)]
        sgl = b.sb("sgl", [128, TG], F32)
        actT = b.sb("actT", [128, 22, TG], BF16)
        self._val = b.sb("fval", [128, 22, TG], BF16)
        ot = xt
        pt = b.ps("fpt", [128, 8, 128], BF16)
        pu = [b.ps(f"fpu{i}", [128, 512], F32) for i in range(3)]
        pd = [b.ps(f"fpd{i}", [128, 512], F32) for i in range(2)]
        ng = getattr(self, "nt_limit", NT) * 128 // TG
        for gi in range(ng):
            for s_ in range(TG // 128):
                t = gi * (TG // 128) + s_
                self.make_hT(self.x1_d, t, xt[s_], junk, ss[s_], hb[s_], pt, hT1[s_], self.ident)
                b.op("pool", lambda e: e.tensor_copy(out=hTg[:, :, s_ * 128:(s_ + 1) * 128], in_=hT1[s_][:]), reads=[hT1[s_]], writes=[hTg])
            for ft in range(NFT):
                p = pu[ft % 3]
                u = ub[ft % 2]
                c_ = cv[(ft // 22) % 2] if False else cv[ft % 2]
                for c in range(8):
                    b.op("pe", lambda e: e.matmul(p[:, 0:TG], lhsT=wu[:, c, ft * 128:(ft + 1) * 128], rhs=hTg[:, c, :], start=(c == 0), stop=(c == 7)),
                         reads=[wu, hTg], writes=[p])
                b.op("act", lambda e: e.copy(out=u[:, 2:TG + 2], in_=p[:, 0:TG]), reads=[p], writes=[u])
                b.op("pool", lambda e: e.tensor_copy(out=u[:, 0:2], in_=carry[:, ft, :]), reads=[carry], writes=[u])
                b.op("pool", lambda e: e.tensor_copy(out=carry[:, ft, :], in_=u[:, TG:TG + 2]), reads=[u], writes=[carry])
                b.op("dve", lambda e: e.tensor_scalar(out=c_[:], in0=u[:, 0:TG], scalar1=cw[:, 0, ft:ft + 1], scalar2=cbias[:, ft:ft + 1], op0=ALU.mult, op1=ALU.add),
                     reads=[u, cw, cbias], writes=[c_])
                b.op("dve", lambda e: e.scalar_tensor_tensor(out=c_[:], in0=u[:, 1:TG + 1], scalar=cw[:, 1, ft:ft + 1], in1=c_[:], op0=ALU.mult, op1=ALU.add),
                     reads=[u, cw, c_], writes=[c_])
                if ft < 22:
                    b.op("dve", lambda e: e.scalar_tensor_tensor(out=self._val[:, ft, :], in0=u[:, 2:TG + 2], scalar=cw[:, 2, ft:ft + 1], in1=c_[:], op0=ALU.mult, op1=ALU.add),
                         reads=[u, cw, c_], writes=[self._val])
                else:
                    b.op("dve", lambda e: e.scalar_tensor_tensor(out=c_[:], in0=u[:, 2:TG + 2], scalar=cw[:, 2, ft:ft + 1], in1=c_[:], op0=ALU.mult, op1=ALU.add),
                         reads=[u, cw, c_], writes=[c_])
                    b.op("act", lambda e: e.activation(out=sgl[:], in_=c_[:], func=AF.Silu), reads=[c_], writes=[sgl])
                    b.op("dve", lambda e: e.tensor_tensor(out=actT[:, ft - 22, :], in0=sgl[:], in1=self._val[:, ft - 22, :], op=ALU.mult),
                         reads=[sgl, self._val], writes=[actT])
            for s_ in range(TG // 128):
                t = gi * (TG // 128) + s_
                for n in range(2):
                    for f in range(22):
                        b.op("pe", lambda e: e.matmul(pd[n][:, :], lhsT=actT[:, f, s_ * 128:(s_ + 1) * 128], rhs=wd[:, f, n * 512:(n + 1) * 512], start=(f == 0), stop=(f == 21)),
                             reads=[actT, wd], writes=[pd[n]])
                    b.op("dve", lambda e: e.tensor_tensor(out=ot[s_][:, n * 512:(n + 1) * 512], in0=pd[n][:, :], in1=xt[s_][:, n * 512:(n + 1) * 512], op=ALU.add),
                         reads=[pd[n], xt[s_]], writes=[ot[s_]])
                b.dma("pool", self.out[t * 128:(t + 1) * 128, :], ot[s_][:], reads=[ot[s_]])


Prog.phase_merge = _phase_merge
Prog.phase_ffn = _phase_ffn


def _phase_rwkv(self):
    b = self.b
    I = self.inp
    TG = 256
    NCH = TG // 64
    tt = lambda eng, out, in0, in1, op, rd, wr: b.op(eng, lambda e: e.tensor_tensor(out=out, in0=in0, in1=in1, op=op), reads=rd, writes=wr)
    with b.scope():
        gat = self.load_gain("gat3", I["attn_norm_g"][0])
        stage = [b.sb(f"rst{i}", [128, 1792], F32) for i in range(2)]
        wr = b.sb("wr", [128, 8, 1792], BF16)
        self.load_weight(wr, I["w_in"][0][:, RW0:RW0 + 1792], 1792, gvec=gat, stage=stage)

        def colvec(name, src, n):
            t = b.sb(name, [64, n], F32)
            b.dma("sp", t[:], src.rearrange("(c p) -> p c", p=64), writes=[t], allow_slow_non_contiguous=True)
            return t
        mu = colvec("mu", I["rwkv_mu"][0], 28)
        w0 = colvec("w0", I["rwkv_w0"][0], 8)
        a0 = colvec("a0", I["rwkv_a0"][0], 8)
        k_k = colvec("k_k", I["rwkv_k_k"][0], 8)
        k_a = colvec("k_a", I["rwkv_k_a"][0], 8)
        r_k = colvec("r_k", I["rwkv_r_k"][0].rearrange("h d -> (h d)"), 8)
        w2s = b.sb("w2s", [64, 512], F32)
        a2s = b.sb("a2s", [64, 512], F32)
        g2s = b.sb("g2s", [64, 2, 512], F32)
        b.dma("sp", w2s[:], I["rwkv_w2"][0], writes=[w2s])
        b.dma("sp", a2s[:], I["rwkv_a2"][0], writes=[a2s])
        b.dma("sp", g2s[:], I["rwkv_g2"][0].rearrange("(two l) f -> l two f", two=2), writes=[g2s])
        lng = b.sb("lng", [64, 512], F32)
        lnb = b.sb("lnb", [64, 512], F32)
        b.dma("sp", lng[:], I["rwkv_ln_g"][0].partition_broadcast(64), writes=[lng])
        b.dma("sp", lnb[:], I["rwkv_ln_b"][0].partition_broadcast(64), writes=[lnb])
        msk = b.sb("rmsk", [64, 3, 64], F32)
        b.dma("sp", msk[:], I["rwmask"], writes=[msk])
        rstm = b.sb("rstm", [64, TG], F32)
        b.dma("sp", rstm[:], I["rwreset"][:, 0:TG], writes=[rstm])
        ones = b.sb("ones64", [64, 64], F32)
        b.op("pool", lambda e: e.memset(ones[:], 1.0), writes=[ones])
        idf = self.identf
        carry = b.sb("rcarry", [64, 28], F32)
        b.op("pool", lambda e: e.memset(carry[:], 0.0), writes=[carry])
        Hs = [[b.sb(f"H{h}_{i}", [64, 64], F32) for i in range(2)] for h in range(8)]
        for h in range(8):
            b.op("pool", lambda e: e.memset(Hs[h][0][:], 0.0), writes=[Hs[h][0]])
        xt = [b.sb(f"rxt{i}", [128, D], F32) for i in range(2)]
        junk = b.sb("rjunk", [128, D], BF16)
        ss = [b.sb(f"rss{i}", [128, 1], F32) for i in range(2)]
        hb = [b.sb(f"rhb{i}", [128, D], BF16) for i in range(2)]
        hT1 = [b.sb(f"rhT{i}", [128, 8, 128], BF16) for i in range(2)]
        hTg = b.sb("rhTg", [128, 8, TG], BF16)
        pbuf = [b.sb(f"rpb{i}", [64, TG + 1], F32) for i in range(2)]
        dtmp = b.sb("rdtmp", [64, TG], F32)
        X = [b.sb(f"rX{w}", [64, 8, TG], F32) for w in range(3)]
        xs = b.sb("rxs", [64, 4, TG], F32)
        BV = b.sb("rBV", [64, 8, TG], F32)
        Ytm = b.sb("rYtm", [64, NCH, 8, 64], F32)
        sqv = b.sb("rsqv", [64, NCH, 8, 64], F32)
        st1 = b.sb("rst1", [64, NCH * 8], F32)
        st2 = b.sb("rst2", [64, NCH * 8], F32)
        T = {n: b.sb("r" + n, [64, TG], F32) for n in ["lw", "as", "kk", "sq", "kkn", "bv", "kp", "t1", "L", "Lx", "Ep", "Em", "Ex", "BT", "KT", "BG", "KG", "rk"]}
        AR = b.sb("rAR", [64, NCH, 2, 64], F32)
        TM = [b.sb(f"rTM{i}", [64, 3, 64], F32) for i in range(2)]
        XM = [b.sb(f"rXM{i}", [64, 4, 64], F32) for i in range(2)]
        AA = [b.sb(f"rAA{i}", [64, 2, 64], F32) for i in range(3)]
        PP = [b.sb(f"rPP{i}", [64, 64], F32) for i in range(3)]
        Xs = b.sb("rXs", [64, 64], F32)
        Us = b.sb("rUs", [64, 64], F32)
        obf = [b.sb(f"robf{i}", [64, TG], BF16) for i in range(2)]
        otmp = b.sb("rotmp", [64, TG], F32)
        pt = b.ps("rpt", [128, 8, 128], BF16)
        pp = [b.ps(f"rpp{i}", [128, 512], F32) for i in range(2)]
        pq = [b.ps(f"rpq{i}", [128, 512], F32) for i in range(2)]
        pd = [b.ps(f"rpd{i}", [128, 512], F32) for i in range(2)]
        pz = b.ps("rpz", [128, 512], F32)
        cnt = {"pp": 0, "pq": 0, "pd": 0, "aa": 0, "ppb": 0, "tm": 0, "xm": 0, "pb": 0}

        def nxt(k, lst):
            cnt[k] += 1
            return lst[cnt[k] % len(lst)]

        ngr = getattr(self, "nrg_limit", S // TG)
        for gi in range(ngr):
            q0 = gi * TG
            for s_ in range(TG // 128):
                t = gi * (TG // 128) + s_
                self.make_hT(I["x"], t, xt[s_], junk, ss[s_], hb[s_], pt, hT1[s_], self.ident)
                b.op("pool", lambda e: e.tensor_copy(out=hTg[:, :, s_ * 128:(s_ + 1) * 128], in_=hT1[s_][:]), reads=[hT1[s_]], writes=[hTg])

            def proj_lerp(fc, out_ap, out_buf, post=None):
                p = nxt("pp", pp)
                for c in range(8):
                    b.op("pe", lambda e: e.matmul(p[0:64, 0:TG], lhsT=wr[:, c, fc * 64:(fc + 1) * 64], rhs=hTg[:, c, :], start=(c == 0), stop=(c == 7)),
                         reads=[wr, hTg], writes=[p])
                pb_ = nxt("pb", pbuf)
                b.op("act", lambda e: e.copy(out=pb_[:, 1:TG + 1], in_=p[0:64, 0:TG]), reads=[p], writes=[pb_])
                b.op("pool", lambda e: e.tensor_copy(out=pb_[:, 0:1], in_=carry[:, fc:fc + 1]), reads=[carry], writes=[pb_])
                b.op("pool", lambda e: e.tensor_copy(out=carry[:, fc:fc + 1], in_=pb_[:, TG:TG + 1]), reads=[pb_], writes=[carry])
                tt("dve", dtmp[:], pb_[:, 0:TG], pb_[:, 1:TG + 1], ALU.subtract, [pb_], [dtmp])
                b.op("dve", lambda e: e.scalar_tensor_tensor(out=out_ap, in0=dtmp[:], scalar=mu[:, fc:fc + 1], in1=pb_[:, 1:TG + 1], op0=ALU.mult, op1=ALU.add),
                     reads=[dtmp, mu, pb_], writes=[out_buf])

            for w in range(3):
                for h in range(8):
                    proj_lerp(w * 8 + h, X[w][:, h, :], X[w])
            for j in range(4):
                proj_lerp(24 + j, xs[:, j, :], xs)
            b.op("act", lambda e: e.activation(out=xs[:, 0, :], in_=xs[:, 0, :], func=AF.Tanh), reads=[xs], writes=[xs])
            b.op("act", lambda e: e.activation(out=xs[:, 2:4, :], in_=xs[:, 2:4, :], func=AF.Sigmoid), reads=[xs], writes=[xs])

            for h in range(8):
                hs = slice(h * 64, (h + 1) * 64)
                R_, K_, V_ = X[0][:, h, :], X[1][:, h, :], X[2][:, h, :]
                p = nxt("pp", pp)
                b.op("pe", lambda e: e.matmul(p[0:64, 0:TG], lhsT=w2s[:, hs], rhs=xs[:, 0, :], start=True, stop=True), reads=[w2s, xs], writes=[p])
                b.op("act", lambda e: e.activation(out=T["lw"][:], in_=p[0:64, 0:TG], func=AF.Sigmoid, bias=w0[:, h:h + 1]), reads=[p, w0], writes=[T["lw"]])
                b.op("pool", lambda e: e.tensor_scalar_mul(out=T["lw"][:], in0=T["lw"][:], scalar1=-0.6065306597126334), reads=[T["lw"]], writes=[T["lw"]])
                p = nxt("pp", pp)
                b.op("pe", lambda e: e.matmul(p[0:64, 0:TG], lhsT=a2s[:, hs], rhs=xs[:, 1, :], start=True, stop=True), reads=[a2s, xs], writes=[p])
                b.op("act", lambda e: e.activation(out=T["as"][:], in_=p[0:64, 0:TG], func=AF.Sigmoid, bias=a0[:, h:h + 1]), reads=[p, a0], writes=[T["as"]])
                b.op("dve", lambda e: e.tensor_scalar_mul(out=T["kk"][:], in0=K_, scalar1=k_k[:, h:h + 1]), reads=[X[1], k_k], writes=[T["kk"]])
                tt("pool", T["sq"][:], T["kk"][:], T["kk"][:], ALU.mult, [T["kk"]], [T["sq"]])
                p = nxt("pp", pp)
                b.op("pe", lambda e: e.matmul(p[0:64, 0:TG], lhsT=ones[:], rhs=T["sq"][:], start=True, stop=True), reads=[ones, T["sq"]], writes=[p])
                b.op("act", lambda e: e.activation(out=T["sq"][:], in_=p[0:64, 0:TG], func=AF.Sqrt), reads=[p], writes=[T["sq"]])
                b.op("dve", lambda e: e.tensor_scalar_max(out=T["sq"][:], in0=T["sq"][:], scalar1=1e-12), reads=[T["sq"]], writes=[T["sq"]])
                b.op("dve", lambda e: e.reciprocal(out=T["sq"][:], in_=T["sq"][:]), reads=[T["sq"]], writes=[T["sq"]])
                tt("dve", T["kkn"][:], T["kk"][:], T["sq"][:], ALU.mult, [T["kk"], T["sq"]], [T["kkn"]])
                tt("pool", T["bv"][:], T["kkn"][:], T["as"][:], ALU.mult, [T["kkn"], T["as"]], [T["bv"]])
                b.op("dve", lambda e: e.tensor_scalar(out=T["t1"][:], in0=T["as"][:], scalar1=-1.0, scalar2=k_a[:, h:h + 1], op0=ALU.add, op1=ALU.mult),
                     reads=[T["as"], k_a], writes=[T["t1"]])
                b.op("dve", lambda e: e.scalar_tensor_tensor(out=T["kp"][:], in0=T["t1"][:], scalar=1.0, in1=K_, op0=ALU.add, op1=ALU.mult),
                     reads=[T["t1"], X[1]], writes=[T["kp"]])
                tt("pool", T["rk"][:], R_, T["kp"][:], ALU.mult, [X[0], T["kp"]], [T["rk"]])
                b.op("pool", lambda e: e.tensor_scalar_mul(out=T["rk"][:], in0=T["rk"][:], scalar1=r_k[:, h:h + 1]), reads=[T["rk"], r_k], writes=[T["rk"]])
                p = nxt("pp", pp)
                b.op("pe", lambda e: e.matmul(p[0:64, 0:TG], lhsT=ones[:], rhs=T["rk"][:], start=True, stop=True), reads=[ones, T["rk"]], writes=[p])
                tt("dve", BV[:, h, :], p[0:64, 0:TG], V_, ALU.mult, [p, X[2]], [BV])
                b.op("dve", lambda e: e.tensor_tensor_scan(out=T["L"][:], data0=rstm[:], data1=T["lw"][:], initial=0.0, op0=ALU.mult, op1=ALU.add),
                     reads=[rstm, T["lw"]], writes=[T["L"]])
                tt("pool", T["Lx"][:], T["L"][:], T["lw"][:], ALU.subtract, [T["L"], T["lw"]], [T["Lx"]])
                b.op("act", lambda e: e.activation(out=T["Ep"][:], in_=T["L"][:], func=AF.Exp), reads=[T["L"]], writes=[T["Ep"]])
                b.op("act", lambda e: e.activation(out=T["Em"][:], in_=T["L"][:], func=AF.Exp, scale=-1.0), reads=[T["L"]], writes=[T["Em"]])
                b.op("act", lambda e: e.activation(out=T["Ex"][:], in_=T["Lx"][:], func=AF.Exp), reads=[T["Lx"]], writes=[T["Ex"]])
                c3 = lambda ap: ap.rearrange("p (c t) -> p c t", t=64)
                b.op("dve", lambda e: e.scalar_tensor_tensor(out=AR[:, :, 0, :], in0=c3(T["kkn"][:]), scalar=-1.0, in1=c3(T["Ex"][:]), op0=ALU.mult, op1=ALU.mult),
                     reads=[T["kkn"], T["Ex"]], writes=[AR])
                tt("pool", AR[:, :, 1, :], c3(R_), c3(T["Ep"][:]), ALU.mult, [X[0], T["Ep"]], [AR])
                tt("dve", T["BT"][:], T["bv"][:], T["Em"][:], ALU.mult, [T["bv"], T["Em"]], [T["BT"]])
                tt("pool", T["KT"][:], T["kp"][:], T["Em"][:], ALU.mult, [T["kp"], T["Em"]], [T["KT"]])
                gC = c3(T["Ep"][:])[:, :, 63:64].to_broadcast([64, NCH, 64])
                tt("dve", c3(T["BG"][:]), c3(T["BT"][:]), gC, ALU.mult, [T["BT"], T["Ep"]], [T["BG"]])
                tt("pool", c3(T["KG"][:]), c3(T["KT"][:]), gC, ALU.mult, [T["KT"], T["Ep"]], [T["KG"]])
                for c in range(NCH):
                    cs = slice(c * 64, (c + 1) * 64)
                    Hc = Hs[h][(gi * NCH + c) % 2]
                    Hn = Hs[h][(gi * NCH + c + 1) % 2]
                    p = nxt("pq", pq)
                    for j, (src, sb_) in enumerate([(V_[:, cs], X[2]), (T["BG"][:, cs], T["BG"]), (T["KG"][:, cs], T["KG"])]):
                        b.op("pe", lambda e: e.transpose(out=p[0:64, j * 64:(j + 1) * 64], in_=src, identity=idf[0:64, 0:64]), reads=[sb_, idf], writes=[p])
                    tm = nxt("tm", TM)
                    b.op("act", lambda e: e.copy(out=tm[:].rearrange("p a b -> p (a b)"), in_=p[0:64, 0:192]), reads=[p], writes=[tm])
                    p = nxt("pq", pq)
                    arc = AR[:, c, :, :].rearrange("p a t -> p (a t)")
                    b.op("pe", lambda e: e.matmul(p[0:64, 0:128], lhsT=T["BT"][:, cs], rhs=arc, start=True, stop=True), reads=[T["BT"], AR], writes=[p])
                    b.op("pe", lambda e: e.matmul(p[0:64, 128:256], lhsT=T["KT"][:, cs], rhs=arc, start=True, stop=True), reads=[T["KT"], AR], writes=[p])
                    b.op("pe", lambda e: e.matmul(p[0:64, 256:320], lhsT=AR[:, c, 0, :], rhs=T["BT"][:, cs], start=True, stop=True), reads=[T["BT"], AR], writes=[p])
                    xm = nxt("xm", XM)
                    tt("dve", xm[:].rearrange("p (a m) t -> p a m t", a=2), p[0:64, 0:256].rearrange("p (a m t) -> p a m t", a=2, m=2),
                       msk[:, None, 0:2, :].to_broadcast([64, 2, 2, 64]), ALU.mult, [p, msk], [xm])
                    aa = nxt("aa", AA)
                    b.op("pool", lambda e: e.tensor_copy(out=aa[:, 0, :], in_=xm[:, 0, :]), reads=[xm], writes=[aa])
                    tt("dve", aa[:, 1, :], p[0:64, 256:320], msk[:, 2, :], ALU.mult, [p, msk], [aa])
                    P_ = nxt("ppb", PP)
                    tt("pool", P_[:], xm[:, 0, :], idf[0:64, 0:64], ALU.add, [xm, idf], [P_])
                    for step in range(5):
                        pdb = nxt("pd", pd)
                        b.op("pe", lambda e: e.matmul(pdb[0:64, 0:64], lhsT=aa[:, 1, :], rhs=aa[:, 0, :], start=True, stop=True), reads=[aa], writes=[pdb])
                        b.op("pe", lambda e: e.matmul(pdb[0:64, 64:128], lhsT=aa[:, 0, :], rhs=aa[:, 1, :], start=True, stop=True), reads=[aa], writes=[pdb])
                        aa2 = nxt("aa", AA)
                        b.op("act", lambda e: e.copy(out=aa2[:].rearrange("p a t -> p (a t)"), in_=pdb[0:64, 0:128]), reads=[pdb], writes=[aa2])
                        b.op("pe", lambda e: e.matmul(pdb[0:64, 128:192], lhsT=aa2[:, 1, :], rhs=P_[:], start=True, stop=True), reads=[aa2, P_], writes=[pdb])
                        P2 = nxt("ppb", PP)
                        tt("dve", P2[:], pdb[0:64, 128:192], P_[:], ALU.add, [pdb, P_], [P2])
                        aa, P_ = aa2, P2
                    b.op("pe", lambda e: e.matmul(pz[0:64, 0:64], lhsT=xm[:, 2, :], rhs=tm[:, 0, :], start=True, stop=False), reads=[xm, tm], writes=[pz])
                    b.op("pe", lambda e: e.matmul(pz[0:64, 0:64], lhsT=AR[:, c, 0, :], rhs=Hc[:], start=False, stop=True), reads=[AR, Hc], writes=[pz])
                    b.op("act", lambda e: e.copy(out=Xs[:], in_=pz[0:64, 0:64]), reads=[pz], writes=[Xs])
                    b.op("pe", lambda e: e.matmul(pz[0:64, 64:128], lhsT=P_[:], rhs=Xs[:], start=True, stop=True), reads=[P_, Xs], writes=[pz])
                    b.op("act", lambda e: e.copy(out=Us[:], in_=pz[0:64, 64:128]), reads=[pz], writes=[Us])
                    b.op("pe", lambda e: e.matmul(pz[0:64, 128:192], lhsT=AR[:, c, 1, :], rhs=Hc[:], start=True, stop=False), reads=[AR, Hc], writes=[pz])
                    b.op("pe", lambda e: e.matmul(pz[0:64, 128:192], lhsT=xm[:, 1, :], rhs=Us[:], start=False, stop=False), reads=[xm, Us], writes=[pz])
                    b.op("pe", lambda e: e.matmul(pz[0:64, 128:192], lhsT=xm[:, 3, :], rhs=tm[:, 0, :], start=False, stop=True), reads=[xm, tm], writes=[pz])
                    b.op("pe", lambda e: e.matmul(pz[0:64, 192:256], lhsT=tm[:, 1, :], rhs=Us[:], start=True, stop=False), reads=[tm, Us], writes=[pz])
                    b.op("pe", lambda e: e.matmul(pz[0:64, 192:256], lhsT=tm[:, 2, :], rhs=tm[:, 0, :], start=False, stop=True), reads=[tm], writes=[pz])
                    b.op("act", lambda e: e.copy(out=Ytm[:, c, h, :], in_=pz[0:64, 128:192]), reads=[pz], writes=[Ytm])
                    b.op("dve", lambda e: e.scalar_tensor_tensor(out=Hn[:], in0=Hc[:], scalar=T["Ep"][:, c * 64 + 63:c * 64 + 64], in1=pz[0:64, 192:256],
                                                                 op0=ALU.mult, op1=ALU.add), reads=[Hc, T["Ep"], pz], writes=[Hn])
            Y3 = Ytm[:].rearrange("p c h i -> p (c h) i")
            S3 = sqv[:].rearrange("p c h i -> p (c h) i")
            b.op("dve", lambda e: e.tensor_reduce(out=st1[:], in_=Y3, axis=AX.X, op=ALU.add), reads=[Ytm], writes=[st1])
            b.op("pool", lambda e: e.tensor_scalar_mul(out=st1[:], in0=st1[:], scalar1=1.0 / 64), reads=[st1], writes=[st1])
            tt("dve", Y3, Y3, st1[:].unsqueeze(2).to_broadcast([64, NCH * 8, 64]), ALU.subtract, [Ytm, st1], [Ytm])
            tt("pool", S3, Y3, Y3, ALU.mult, [Ytm], [sqv])
            b.op("dve", lambda e: e.tensor_reduce(out=st2[:], in_=S3, axis=AX.X, op=ALU.add), reads=[sqv], writes=[st2])
            b.op("act", lambda e: e.activation(out=st2[:], in_=st2[:], func=AF.Sqrt, scale=1.0 / 64, bias=64e-5), reads=[st2], writes=[st2])
            b.op("dve", lambda e: e.reciprocal(out=st2[:], in_=st2[:]), reads=[st2], writes=[st2])
            tt("dve", Y3, Y3, st2[:].unsqueeze(2).to_broadcast([64, NCH * 8, 64]), ALU.mult, [Ytm, st2], [Ytm])
            lg = lng[:].rearrange("p (h i) -> p h i", i=64)[:, None, :, :].to_broadcast([64, NCH, 8, 64])
            lb = lnb[:].rearrange("p (h i) -> p h i", i=64)[:, None, :, :].to_broadcast([64, NCH, 8, 64])
            tt("pool", Ytm[:], Ytm[:], lg, ALU.mult, [Ytm, lng], [Ytm])
            tt("dve", Ytm[:], Ytm[:], lb, ALU.add, [Ytm, lnb], [Ytm])
            for h in range(8):
                p = nxt("pq", pq)
                for c in range(NCH):
                    b.op("pe", lambda e: e.transpose(out=p[0:64, c * 64:(c + 1) * 64], in_=Ytm[:, c, h, :], identity=idf[0:64, 0:64]), reads=[Ytm, idf], writes=[p])
                tt("dve", otmp[:], p[0:64, 0:TG], BV[:, h, :], ALU.add, [p, BV], [otmp])
                pg_ = nxt("pp", pp)
                b.op("pe", lambda e: e.matmul(pg_[0:64, 0:TG], lhsT=g2s[:, 0, h * 64:(h + 1) * 64], rhs=xs[:, 2, :], start=True, stop=False), reads=[g2s, xs], writes=[pg_])
                b.op("pe", lambda e: e.matmul(pg_[0:64, 0:TG], lhsT=g2s[:, 1, h * 64:(h + 1) * 64], rhs=xs[:, 3, :], start=False, stop=True), reads=[g2s, xs], writes=[pg_])
                ob_ = obf[h % 2]
                tt("dve", ob_[:], otmp[:], pg_[0:64, 0:TG], ALU.mult, [otmp, pg_], [ob_])
                b.dma("pool", self.obT_d[h // 2, (h % 2) * 64:(h % 2) * 64 + 64, q0:q0 + TG], ob_[:], reads=[ob_], writes=[self.obT_d])
        if "rwkv" in self.debug:
            d = self.dbg_out("obT", [4, 128, S], BF16)
            b.dma("pool", d, self.obT_d[:], reads=[self.obT_d])


Prog.phase_rwkv = _phase_rwkv


def build_full():
    p = Prog()
    b = p.b
    p.alloc_root()
    with b.scope():
        p.alloc_persistent()
        p.phase_nsa_proj()
        p.phase_attn2()
    p.phase_rwkv3()
    p.phase_merge()
    p.phase_ffn2()
    p.finish()
    return p


def kernel(**inputs):
    p = build_full()
    consts = host_consts(inputs["rel_bias"])
    shared = {k: np.ascontiguousarray(np.asarray(inputs[k], np.float32)) for k in W_SPECS if k != "x"}
    shared.update(consts)
    x = np.asarray(inputs["x"], np.float32)
    in_maps = []
    for c in range(8):
        m = dict(shared)
        m["x"] = np.ascontiguousarray(x[c])
        in_maps.append(m)
    res = run_bass_kernel_spmd(p.nc, in_maps, core_ids=list(range(8)))
    return np.stack([np.asarray(r["out"], np.float32) for r in res.results], axis=0)


def _phase_rwkv2(self):
    b = self.b
    I = self.inp
    TG = 128
    NCH = 2
    tt = lambda eng, out, in0, in1, op, rd, wr: b.op(eng, lambda e: e.tensor_tensor(out=out, in0=in0, in1=in1, op=op), reads=rd, writes=wr)
    with b.scope():
        W1 = b.sb("W1", [128, 8, 1792], BF16)
        W2 = b.sb("W2", [128, 8, 1792], BF16)
        with b.scope():
            gat = self.load_gain("gat3", I["attn_norm_g"][0])
            stage = [b.sb(f"rst{i}", [128, 1792], F32) for i in range(2)]
            tmpw = [b.sb(f"rtw{i}", [128, 1792], F32) for i in range(2)]
            mur = self.bcast_row("mur", I["rwkv_mu"][0], 1792)
            for c in range(8):
                st = stage[c % 2]
                tw_ = tmpw[c % 2]
                b.dma("sp", st[:], I["w_in"][0][c * 128:(c + 1) * 128, RW0:RW0 + 1792], writes=[st])
                tt("dve", tw_[:], st[:], mur[:], ALU.mult, [st, mur], [tw_])
                b.op("act", lambda e: e.activation(out=W2[:, c, :], in_=tw_[:], func=AF.Copy, scale=gat[:, c:c + 1]), reads=[tw_, gat], writes=[W2])
                tt("pool", st[:], st[:], tw_[:], ALU.subtract, [st, tw_], [st])
                b.op("act", lambda e: e.activation(out=W1[:, c, :], in_=st[:], func=AF.Copy, scale=gat[:, c:c + 1]), reads=[st, gat], writes=[W1])

        def colvec(name, src, n):
            t = b.sb(name, [64, n], F32)
            b.dma("sp", t[:], src.rearrange("(c p) -> p c", p=64), writes=[t], allow_slow_non_contiguous=True)
            return t
        w0 = colvec("w0", I["rwkv_w0"][0], 8)
        a0 = colvec("a0", I["rwkv_a0"][0], 8)
        k_k = colvec("k_k", I["rwkv_k_k"][0], 8)
        k_a = colvec("k_a", I["rwkv_k_a"][0], 8)
        r_k = colvec("r_k", I["rwkv_r_k"][0].rearrange("h d -> (h d)"), 8)
        w2s = b.sb("w2s", [64, 512], F32)
        a2s = b.sb("a2s", [64, 512], F32)
        g2s = b.sb("g2s", [64, 2, 512], F32)
        b.dma("sp", w2s[:], I["rwkv_w2"][0], writes=[w2s])
        b.dma("sp", a2s[:], I["rwkv_a2"][0], writes=[a2s])
        b.dma("sp", g2s[:], I["rwkv_g2"][0].rearrange("(two l) f -> l two f", two=2), writes=[g2s])
        lng = b.sb("lng", [64, 512], F32)
        lnb = b.sb("lnb", [64, 512], F32)
        b.dma("sp", lng[:], I["rwkv_ln_g"][0].partition_broadcast(64), writes=[lng])
        b.dma("sp", lnb[:], I["rwkv_ln_b"][0].partition_broadcast(64), writes=[lnb])
        msk = b.sb("rmsk", [64, 3, 64], F32)
        b.dma("sp", msk[:], I["rwmask"], writes=[msk])
        rstm = b.sb("rstm", [64, 8 * TG], F32)
        b.dma("sp", rstm[:], I["rwreset"], writes=[rstm])
        ones = b.sb("ones64", [64, 64], F32)
        b.op("pool", lambda e: e.memset(ones[:], 1.0), writes=[ones])
        idf = self.identf
        Hst = b.sb("rH", [64, 2, 8, 64], F32)
        b.op("pool", lambda e: e.memset(Hst[:], 0.0), writes=[Hst])
        xt = [b.sb(f"rxt{i}", [128, D], F32) for i in range(1)] * 2
        junk = b.sb("rjunk", [128, D], BF16)
        ss = [b.sb(f"rss{i}", [128, 1], F32) for i in range(1)] * 2
        hb = [b.sb(f"rhb{i}", [128, D], BF16) for i in range(1)] * 2
        hT1 = [b.sb(f"rhT{i}", [128, 8, 128], BF16) for i in range(1)] * 2
        hTs = b.sb("rhTs", [128, 8, TG + 1], BF16)
        b.op("pool", lambda e: e.memset(hTs[:], 0.0), writes=[hTs])
        XL = b.sb("rXL", [64, 20, TG], F32)
        Vtm = b.sb("rVtm", [64, NCH, 512], F32)
        names = ["LW", "AS", "KKN", "BVc", "KP", "RK", "L", "EP", "EM", "BG", "KG"]
        T = {n: b.sb("r" + n, [64, 8, TG], F32) for n in names}
        T["NR"] = T["RK"]
        T["T1"] = T["BG"]
        T["KK"] = T["KG"]
        T["EX"] = T["L"]
        T["BT"] = T["LW"]
        T["KT"] = T["AS"]
        AR = b.sb("rAR", [64, 8, NCH, 2, 64], F32)
        BON = b.sb("rBON", [64, NCH * 8], F32)
        Ytm = b.sb("rYtm", [64, NCH, 8, 64], F32)
        sqv = b.sb("rsqv", [64, NCH, 8, 64], F32)
        st1 = b.sb("rst1", [64, NCH * 8], F32)
        st2 = b.sb("rst2", [64, NCH * 8], F32)
        TM4 = [b.sb(f"rTM{i}", [64, 4, 2, 64], F32) for i in range(2)]
        XM4 = [b.sb(f"rXM{i}", [64, 4, 4, 64], F32) for i in range(2)]
        AA4 = [b.sb(f"rAA{i}", [64, 4, 2, 64], F32) for i in range(2)]
        PP4 = [b.sb(f"rPP{i}", [64, 4, 64], F32) for i in range(2)]
        Xs4 = b.sb("rXs4", [64, 4, 64], F32)
        Us4 = b.sb("rUs4", [64, 4, 64], F32)
        Ht4 = b.sb("rHt4", [64, 4, 64], F32)
        OBb = b.sb("rOBb", [64, NCH, 512], BF16)
        obT = [b.sb(f"robT{i}", [128, 4, TG], BF16) for i in range(1)] * 2
        pt = b.ps("rpt", [128, 8, 128], BF16)
        pP = b.ps("rpP", [128, 512], F32)
        pA = b.ps("rpA", [128, 1024], F32)
        pB = b.ps("rpB", [128, 512], F32)
        pC = b.ps("rpC", [128, 512], F32)
        pD = b.ps("rpD", [128, 512], F32)
        pZ = b.ps("rpZ", [128, 512], F32)
        cnt = {}

        def nxt(k, lst):
            cnt[k] = cnt.get(k, 0) + 1
            return lst[cnt[k] % len(lst)]
        bc = lambda v: v[:].unsqueeze(2).to_broadcast([64, 8, TG])
        f2 = lambda t_: t_[:].rearrange("p h t -> p (h t)")
        c16 = lambda t_: t_[:].rearrange("p h (c t) -> p (h c) t", t=64)

        ngr = getattr(self, "nrg_limit", S // TG)
        for gi in range(ngr):
            q0 = gi * TG
            i = gi % 2
            self.make_hT(I["x"], gi, xt[i], junk, ss[i], hb[i], pt, hT1[i], self.ident)
            b.op("pool", lambda e: e.tensor_copy(out=hTs[:, :, 0:1], in_=hTs[:, :, TG:TG + 1]), reads=[hTs], writes=[hTs])
            b.op("pool", lambda e: e.tensor_copy(out=hTs[:, :, 1:TG + 1], in_=hT1[i][:]), reads=[hT1[i]], writes=[hTs])
            ftiles = list(range(0, 16)) + [24, 25, 26, 27]
            for q4 in range(5):
                for j in range(4):
                    fc = ftiles[q4 * 4 + j]
                    for c in range(8):
                        b.op("pe", lambda e: e.matmul(pP[0:64, j * TG:(j + 1) * TG], lhsT=W1[:, c, fc * 64:(fc + 1) * 64], rhs=hTs[:, c, 1:TG + 1], start=(c == 0), stop=False),
                             reads=[W1, hTs], writes=[pP])
                    for c in range(8):
                        b.op("pe", lambda e: e.matmul(pP[0:64, j * TG:(j + 1) * TG], lhsT=W2[:, c, fc * 64:(fc + 1) * 64], rhs=hTs[:, c, 0:TG], start=False, stop=(c == 7)),
                             reads=[W2, hTs], writes=[pP])
                b.op("act", lambda e: e.copy(out=XL[:, q4 * 4:(q4 + 1) * 4, :].rearrange("p a t -> p (a t)"), in_=pP[0:64, :]), reads=[pP], writes=[XL])
            for c_ in range(NCH):
                for c in range(8):
                    b.op("pe", lambda e: e.matmul(pP[0:64, :], lhsT=hTs[:, c, 1 + c_ * 64:1 + (c_ + 1) * 64], rhs=W1[:, c, 1024:1536], start=(c == 0), stop=False),
                         reads=[W1, hTs], writes=[pP])
                for c in range(8):
                    b.op("pe", lambda e: e.matmul(pP[0:64, :], lhsT=hTs[:, c, c_ * 64:(c_ + 1) * 64], rhs=W2[:, c, 1024:1536], start=False, stop=(c == 7)),
                         reads=[W2, hTs], writes=[pP])
                b.op("act", lambda e: e.copy(out=Vtm[:, c_, :], in_=pP[0:64, :]), reads=[pP], writes=[Vtm])
            R_ = XL[:, 0:8, :]
            K_ = XL[:, 8:16, :]
            b.op("act", lambda e: e.activation(out=XL[:, 16, :], in_=XL[:, 16, :], func=AF.Tanh), reads=[XL], writes=[XL])
            b.op("act", lambda e: e.activation(out=XL[:, 18:20, :], in_=XL[:, 18:20, :], func=AF.Sigmoid), reads=[XL], writes=[XL])
            for (ws_, src, bias_, dst) in [(w2s, 16, w0, "LW"), (a2s, 17, a0, "AS")]:
                for half in range(2):
                    for j in range(4):
                        h = half * 4 + j
                        b.op("pe", lambda e: e.matmul(pP[0:64, j * TG:(j + 1) * TG], lhsT=ws_[:, h * 64:(h + 1) * 64], rhs=XL[:, src, :], start=True, stop=True),
                             reads=[ws_, XL], writes=[pP])
                    for j in range(4):
                        h = half * 4 + j
                        b.op("act", lambda e: e.activation(out=T[dst][:, h, :], in_=pP[0:64, j * TG:(j + 1) * TG], func=AF.Sigmoid, bias=bias_[:, h:h + 1]),
                             reads=[pP, bias_], writes=[T[dst]])
            b.op("pool", lambda e: e.tensor_scalar_mul(out=f2(T["LW"]), in0=f2(T["LW"]), scalar1=-0.6065306597126334), reads=[T["LW"]], writes=[T["LW"]])
            tt("dve", T["KK"][:], K_, bc(k_k), ALU.mult, [XL, k_k], [T["KK"]])
            tt("pool", T["NR"][:], T["KK"][:], T["KK"][:], ALU.mult, [T["KK"]], [T["NR"]])
            for half in range(2):
                b.op("pe", lambda e: e.matmul(pP[0:64, :], lhsT=ones[:], rhs=T["NR"][:, half * 4:(half + 1) * 4, :].rearrange("p h t -> p (h t)"), start=True, stop=True),
                     reads=[ones, T["NR"]], writes=[pP])
                b.op("act", lambda e: e.activation(out=T["KKN"][:, half * 4:(half + 1) * 4, :].rearrange("p h t -> p (h t)"), in_=pP[0:64, :], func=AF.Sqrt),
                     reads=[pP], writes=[T["KKN"]])
            b.op("dve", lambda e: e.tensor_scalar_max(out=f2(T["KKN"]), in0=f2(T["KKN"]), scalar1=1e-12), reads=[T["KKN"]], writes=[T["KKN"]])
            b.op("dve", lambda e: e.reciprocal(out=f2(T["KKN"]), in_=f2(T["KKN"])), reads=[T["KKN"]], writes=[T["KKN"]])
            tt("dve", T["KKN"][:], T["KKN"][:], T["KK"][:], ALU.mult, [T["KKN"], T["KK"]], [T["KKN"]])
            tt("pool", T["BVc"][:], T["KKN"][:], T["AS"][:], ALU.mult, [T["KKN"], T["AS"]], [T["BVc"]])
            b.op("pool", lambda e: e.tensor_scalar_add(out=f2(T["T1"]), in0=f2(T["AS"]), scalar1=-1.0), reads=[T["AS"]], writes=[T["T1"]])
            tt("pool", T["T1"][:], T["T1"][:], bc(k_a), ALU.mult, [T["T1"], k_a], [T["T1"]])
            b.op("dve", lambda e: e.scalar_tensor_tensor(out=f2(T["KP"]), in0=f2(T["T1"]), scalar=1.0, in1=K_.rearrange("p h t -> p (h t)"), op0=ALU.add, op1=ALU.mult),
                 reads=[T["T1"], XL], writes=[T["KP"]])
            tt("pool", T["RK"][:], R_, T["KP"][:], ALU.mult, [XL, T["KP"]], [T["RK"]])
            tt("pool", T["RK"][:], T["RK"][:], bc(r_k), ALU.mult, [T["RK"], r_k], [T["RK"]])
            for c_ in range(NCH):
                for h in range(8):
                    b.op("pe", lambda e: e.matmul(pD[0:64, c_ * 8 + h:c_ * 8 + h + 1], lhsT=T["RK"][:, h, c_ * 64:(c_ + 1) * 64], rhs=ones[:, 0:1], start=True, stop=True),
                         reads=[T["RK"], ones], writes=[pD])
            b.op("act", lambda e: e.copy(out=BON[:], in_=pD[0:64, 0:NCH * 8]), reads=[pD], writes=[BON])
            b.op("dve", lambda e: e.tensor_tensor_scan(out=f2(T["L"]), data0=rstm[:], data1=f2(T["LW"]), initial=0.0, op0=ALU.mult, op1=ALU.add),
                 reads=[rstm, T["LW"]], writes=[T["L"]])
            b.op("act", lambda e: e.activation(out=f2(T["EP"]), in_=f2(T["L"]), func=AF.Exp), reads=[T["L"]], writes=[T["EP"]])
            b.op("act", lambda e: e.activation(out=f2(T["EM"]), in_=f2(T["L"]), func=AF.Exp, scale=-1.0), reads=[T["L"]], writes=[T["EM"]])
            tt("pool", T["L"][:], T["L"][:], T["LW"][:], ALU.subtract, [T["L"], T["LW"]], [T["L"]])
            b.op("act", lambda e: e.activation(out=f2(T["EX"]), in_=f2(T["L"]), func=AF.Exp), reads=[T["L"]], writes=[T["EX"]])
            ar0 = AR[:, :, :, 0, :].rearrange("p h c t -> p (h c) t")
            ar1 = AR[:, :, :, 1, :].rearrange("p h c t -> p (h c) t")
            b.op("dve", lambda e: e.scalar_tensor_tensor(out=ar0, in0=c16(T["KKN"]), scalar=-1.0, in1=c16(T["EX"]), op0=ALU.mult, op1=ALU.mult),
                 reads=[T["KKN"], T["EX"]], writes=[AR])
            tt("pool", ar1, R_.rearrange("p h (c t) -> p (h c) t", t=64), c16(T["EP"]), ALU.mult, [XL, T["EP"]], [AR])
            tt("dve", T["BT"][:], T["BVc"][:], T["EM"][:], ALU.mult, [T["BVc"], T["EM"]], [T["BT"]])
            tt("pool", T["KT"][:], T["KP"][:], T["EM"][:], ALU.mult, [T["KP"], T["EM"]], [T["KT"]])
            gC = c16(T["EP"])[:, :, 63:64].to_broadcast([64, 16, 64])
            tt("dve", c16(T["BG"]), c16(T["BT"]), gC, ALU.mult, [T["BT"], T["EP"]], [T["BG"]])
            tt("pool", c16(T["KG"]), c16(T["KT"]), gC, ALU.mult, [T["KT"], T["EP"]], [T["KG"]])
            for c_ in range(NCH):
                cs = slice(c_ * 64, (c_ + 1) * 64)
                cur = (gi * NCH + c_) % 2
                for hb_ in range(2):
                    heads = list(range(hb_ * 4, hb_ * 4 + 4))
                    for j, h in enumerate(heads):
                        b.op("pe", lambda e: e.transpose(out=pC[0:64, j * 128:j * 128 + 64], in_=T["BG"][:, h, cs], identity=idf[0:64, 0:64]), reads=[T["BG"], idf], writes=[pC])
                        b.op("pe", lambda e: e.transpose(out=pC[0:64, j * 128 + 64:(j + 1) * 128], in_=T["KG"][:, h, cs], identity=idf[0:64, 0:64]), reads=[T["KG"], idf], writes=[pC])
                    tm = nxt("tm", TM4)
                    b.op("act", lambda e: e.copy(out=tm[:].rearrange("p h a t -> p (h a t)"), in_=pC[0:64, 0:512]), reads=[pC], writes=[tm])
                    for j, h in enumerate(heads):
                        arc = AR[:, h, c_, :, :].rearrange("p a t -> p (a t)")
                        b.op("pe", lambda e: e.matmul(pA[0:64, j * 256:j * 256 + 128], lhsT=T["BT"][:, h, cs], rhs=arc, start=True, stop=True), reads=[T["BT"], AR], writes=[pA])
                        b.op("pe", lambda e: e.matmul(pA[0:64, j * 256 + 128:(j + 1) * 256], lhsT=T["KT"][:, h, cs], rhs=arc, start=True, stop=True), reads=[T["KT"], AR], writes=[pA])
                        b.op("pe", lambda e: e.matmul(pB[0:64, j * 64:(j + 1) * 64], lhsT=AR[:, h, c_, 0, :], rhs=T["BT"][:, h, cs], start=True, stop=True), reads=[T["BT"], AR], writes=[pB])
                    xm = nxt("xm", XM4)
                    tt("dve", xm[:].rearrange("p h (a m) t -> p (h a) m t", a=2), pA[0:64, :].rearrange("p (ha m t) -> p ha m t", m=2, t=64),
                       msk[:, None, 0:2, :].to_broadcast([64, 8, 2, 64]), ALU.mult, [pA, msk], [xm])
                    aa = nxt("aa", AA4)
                    b.op("pool", lambda e: e.tensor_copy(out=aa[:, :, 0, :], in_=xm[:, :, 0, :]), reads=[xm], writes=[aa])
                    tt("dve", aa[:, :, 1, :], pB[0:64, 0:256].rearrange("p (h t) -> p h t", t=64), msk[:, 2:3, :].to_broadcast([64, 4, 64]), ALU.mult, [pB, msk], [aa])
                    P_ = nxt("pp4", PP4)
                    tt("pool", P_[:], xm[:, :, 0, :], idf[0:64, None, 0:64].to_broadcast([64, 4, 64]), ALU.add, [xm, idf], [P_])
                    for step in range(5):
                        for j in range(4):
                            b.op("pe", lambda e: e.matmul(pD[0:64, j * 128:j * 128 + 64], lhsT=aa[:, j, 1, :], rhs=aa[:, j, 0, :], start=True, stop=True), reads=[aa], writes=[pD])
                            b.op("pe", lambda e: e.matmul(pD[0:64, j * 128 + 64:(j + 1) * 128], lhsT=aa[:, j, 0, :], rhs=aa[:, j, 1, :], start=True, stop=True), reads=[aa], writes=[pD])
                        aa2 = nxt("aa", AA4)
                        b.op("act", lambda e: e.copy(out=aa2[:].rearrange("p h a t -> p (h a t)"), in_=pD[0:64, :]), reads=[pD], writes=[aa2])
                        for j in range(4):
                            b.op("pe", lambda e: e.matmul(pB[0:64, 256 + j * 64:256 + (j + 1) * 64], lhsT=aa2[:, j, 1, :], rhs=P_[:, j, :], start=True, stop=True), reads=[aa2, P_], writes=[pB])
                        P2 = nxt("pp4", PP4)
                        tt("dve", P2[:], pB[0:64, 256:512].rearrange("p (h t) -> p h t", t=64), P_[:], ALU.add, [pB, P_], [P2])
                        aa, P_ = aa2, P2
                    for j, h in enumerate(heads):
                        b.op("pe", lambda e: e.matmul(pZ[0:64, j * 64:(j + 1) * 64], lhsT=xm[:, j, 2, :], rhs=Vtm[:, c_, h * 64:(h + 1) * 64], start=True, stop=False), reads=[xm, Vtm], writes=[pZ])
                        b.op("pe", lambda e: e.matmul(pZ[0:64, j * 64:(j + 1) * 64], lhsT=AR[:, h, c_, 0, :], rhs=Hst[:, cur, h, :], start=False, stop=True), reads=[AR, Hst], writes=[pZ])
                    b.op("act", lambda e: e.copy(out=Xs4[:].rearrange("p h t -> p (h t)"), in_=pZ[0:64, 0:256]), reads=[pZ], writes=[Xs4])
                    for j in range(4):
                        b.op("pe", lambda e: e.matmul(pZ[0:64, 256 + j * 64:256 + (j + 1) * 64], lhsT=P_[:, j, :], rhs=Xs4[:, j, :], start=True, stop=True), reads=[P_, Xs4], writes=[pZ])
                    b.op("act", lambda e: e.copy(out=Us4[:].rearrange("p h t -> p (h t)"), in_=pZ[0:64, 256:512]), reads=[pZ], writes=[Us4])
                    for j, h in enumerate(heads):
                        o = slice(j * 64, (j + 1) * 64)
                        vh = Vtm[:, c_, h * 64:(h + 1) * 64]
                        b.op("pe", lambda e: e.matmul(pZ[0:64, o], lhsT=AR[:, h, c_, 1, :], rhs=Hst[:, cur, h, :], start=True, stop=False), reads=[AR, Hst], writes=[pZ])
                        b.op("pe", lambda e: e.matmul(pZ[0:64, o], lhsT=xm[:, j, 1, :], rhs=Us4[:, j, :], start=False, stop=False), reads=[xm, Us4], writes=[pZ])
                        b.op("pe", lambda e: e.matmul(pZ[0:64, o], lhsT=xm[:, j, 3, :], rhs=vh, start=False, stop=True), reads=[xm, Vtm], writes=[pZ])
                    for j, h in enumerate(heads):
                        o = slice(256 + j * 64, 256 + (j + 1) * 64)
                        vh = Vtm[:, c_, h * 64:(h + 1) * 64]
                        b.op("pe", lambda e: e.matmul(pZ[0:64, o], lhsT=tm[:, j, 0, :], rhs=Us4[:, j, :], start=True, stop=False), reads=[tm, Us4], writes=[pZ])
                        b.op("pe", lambda e: e.matmul(pZ[0:64, o], lhsT=tm[:, j, 1, :], rhs=vh, start=False, stop=True), reads=[tm, Vtm], writes=[pZ])
                    b.op("act", lambda e: e.copy(out=Ytm[:, c_, hb_ * 4:(hb_ + 1) * 4, :].rearrange("p h t -> p (h t)"), in_=pZ[0:64, 0:256]), reads=[pZ], writes=[Ytm])
                    gH = T["EP"][:, hb_ * 4:(hb_ + 1) * 4, c_ * 64 + 63:c_ * 64 + 64].to_broadcast([64, 4, 64])
                    tt("pool", Ht4[:], Hst[:, cur, hb_ * 4:(hb_ + 1) * 4, :], gH, ALU.mult, [Hst, T["EP"]], [Ht4])
                    tt("dve", Hst[:, 1 - cur, hb_ * 4:(hb_ + 1) * 4, :], pZ[0:64, 256:512].rearrange("p (h t) -> p h t", t=64), Ht4[:], ALU.add, [pZ, Ht4], [Hst])
            Y3 = Ytm[:].rearrange("p c h i -> p (c h) i")
            S3 = sqv[:].rearrange("p c h i -> p (c h) i")
            b.op("dve", lambda e: e.tensor_reduce(out=st1[:], in_=Y3, axis=AX.X, op=ALU.add), reads=[Ytm], writes=[st1])
            b.op("pool", lambda e: e.tensor_scalar_mul(out=st1[:], in0=st1[:], scalar1=1.0 / 64), reads=[st1], writes=[st1])
            tt("dve", Y3, Y3, st1[:].unsqueeze(2).to_broadcast([64, NCH * 8, 64]), ALU.subtract, [Ytm, st1], [Ytm])
            tt("pool", S3, Y3, Y3, ALU.mult, [Ytm], [sqv])
            b.op("dve", lambda e: e.tensor_reduce(out=st2[:], in_=S3, axis=AX.X, op=ALU.add), reads=[sqv], writes=[st2])
            b.op("act", lambda e: e.activation(out=st2[:], in_=st2[:], func=AF.Sqrt, scale=1.0 / 64, bias=64e-5), reads=[st2], writes=[st2])
            b.op("dve", lambda e: e.reciprocal(out=st2[:], in_=st2[:]), reads=[st2], writes=[st2])
            tt("dve", Y3, Y3, st2[:].unsqueeze(2).to_broadcast([64, NCH * 8, 64]), ALU.mult, [Ytm, st2], [Ytm])
            lg = lng[:].rearrange("p (h i) -> p h i", i=64)[:, None, :, :].to_broadcast([64, NCH, 8, 64])
            lb = lnb[:].rearrange("p (h i) -> p h i", i=64)[:, None, :, :].to_broadcast([64, NCH, 8, 64])
            tt("pool", Ytm[:], Ytm[:], lg, ALU.mult, [Ytm, lng], [Ytm])
            tt("dve", Ytm[:], Ytm[:], lb, ALU.add, [Ytm, lnb], [Ytm])
            V3 = Vtm[:].rearrange("p c (h i) -> p (c h) i", i=64)
            tt("pool", S3, V3, BON[:].unsqueeze(2).to_broadcast([64, NCH * 8, 64]), ALU.mult, [Vtm, BON], [sqv])
            tt("dve", Y3, Y3, S3, ALU.add, [Ytm, sqv], [Ytm])
            for c_ in range(NCH):
                for two in range(2):
                    b.op("pe", lambda e: e.matmul(pP[0:64, :], lhsT=XL[:, 18 + two, c_ * 64:(c_ + 1) * 64], rhs=g2s[:, two, :], start=(two == 0), stop=(two == 1)),
                         reads=[XL, g2s], writes=[pP])
                tt("dve", OBb[:, c_, :], Ytm[:, c_, :, :].rearrange("p h i -> p (h i)"), pP[0:64, :], ALU.mult, [Ytm, pP], [OBb])
                for k4 in range(4):
                    b.op("pe", lambda e: e.transpose(out=pt[:, k4, c_ * 64:(c_ + 1) * 64], in_=OBb[:, c_, k4 * 128:(k4 + 1) * 128], identity=self.ident[0:64, 0:64]),
                         reads=[OBb, self.ident], writes=[pt])
            ot = obT[gi % 2]
            b.op("act", lambda e: e.copy(out=ot[:], in_=pt[:, 0:4, :]), reads=[pt], writes=[ot])
            b.dma("pool", self.obT_d[:, :, q0:q0 + TG].rearrange("c p t -> p c t"), ot[:], reads=[ot], writes=[self.obT_d])
        if "rwkv" in self.debug:
            d = self.dbg_out("obT", [4, 128, S], BF16)
            b.dma("pool", d, self.obT_d[:], reads=[self.obT_d])


Prog.phase_rwkv2 = _phase_rwkv2


def _phase_rwkv3(self):
    b = self.b
    I = self.inp
    TG = 128
    NCH = 2
    CHDT = mybir.dt.float32r if getattr(self, "use_f32r", True) else F32
    tt = lambda eng, out, in0, in1, op, rd, wr: b.op(eng, lambda e: e.tensor_tensor(out=out, in0=in0, in1=in1, op=op), reads=rd, writes=wr)
    with b.scope():
        W1 = b.sb("W1", [128, 8, 1792], BF16)
        with b.scope():
            gat = self.load_gain("gat3", I["attn_norm_g"][0])
            stage = [b.sb(f"rst{i}", [128, 1792], F32) for i in range(2)]
            self.load_weight(W1, I["w_in"][0][:, RW0:RW0 + 1792], 1792, gvec=gat, stage=stage)

        def colvec(name, src, n):
            t = b.sb(name, [64, n], F32)
            b.dma("sp", t[:], src.rearrange("(c p) -> p c", p=64), writes=[t], allow_slow_non_contiguous=True)
            return t
        mu = colvec("mu", I["rwkv_mu"][0], 28)
        w0 = colvec("w0", I["rwkv_w0"][0], 8)
        a0 = colvec("a0", I["rwkv_a0"][0], 8)
        k_k = colvec("k_k", I["rwkv_k_k"][0], 8)
        k_a = colvec("k_a", I["rwkv_k_a"][0], 8)
        r_k = colvec("r_k", I["rwkv_r_k"][0].rearrange("h d -> (h d)"), 8)
        w2s = b.sb("w2s", [64, 512], F32)
        a2s = b.sb("a2s", [64, 512], F32)
        g2s = b.sb("g2s", [64, 2, 512], F32)
        b.dma("sp", w2s[:], I["rwkv_w2"][0], writes=[w2s])
        b.dma("sp", a2s[:], I["rwkv_a2"][0], writes=[a2s])
        b.dma("sp", g2s[:], I["rwkv_g2"][0].rearrange("(two l) f -> l two f", two=2), writes=[g2s])
        lng = b.sb("lng", [64, 512], F32)
        lnb = b.sb("lnb", [64, 512], F32)
        b.dma("sp", lng[:], I["rwkv_ln_g"][0].partition_broadcast(64), writes=[lng])
        b.dma("sp", lnb[:], I["rwkv_ln_b"][0].partition_broadcast(64), writes=[lnb])
        msk = b.sb("rmsk", [64, 3, 64], F32)
        b.dma("sp", msk[:], I["rwmask"], writes=[msk])
        rstm = b.sb("rstm", [64, 8 * TG], F32)
        b.dma("sp", rstm[:], I["rwreset"], writes=[rstm])
        ones = b.sb("ones64", [64, 64], F32)
        b.op("pool", lambda e: e.memset(ones[:], 1.0), writes=[ones])
        idf = self.identf
        Hst = b.sb("rH", [64, 2, 8, 64], CHDT)
        b.op("pool", lambda e: e.memset(Hst[:].bitcast(F32), 0.0), writes=[Hst])
        xt = [b.sb(f"rxt{i}", [128, D], F32) for i in range(1)] * 2
        junk = b.sb("rjunk", [128, D], BF16)
        ss = [b.sb(f"rss{i}", [128, 1], F32) for i in range(1)] * 2
        hb = [b.sb(f"rhb{i}", [128, D], BF16) for i in range(1)] * 2
        hT1 = [b.sb(f"rhT{i}", [128, 8, 128], BF16) for i in range(1)] * 2
        PB = b.sb("rPB", [64, 28, TG + 1], F32)
        b.op("pool", lambda e: e.memset(PB[:], 0.0), writes=[PB])
        XL = b.sb("rXL", [64, 28, TG], F32)
        Vtm = b.sb("rVtm", [64, NCH, 512], CHDT)
        names = ["LW", "AS", "KKN", "BVc", "KP", "RK", "L", "EP", "EM", "BG", "KG"]
        T = {n: b.sb("r" + n, [64, 8, TG], F32) for n in names}
        T["NR"] = T["RK"]
        T["T1"] = T["BG"]
        T["KK"] = T["KG"]
        T["EX"] = T["L"]
        T["BT"] = b.sb("rBTr", [64, 8, TG], CHDT)
        T["KT"] = b.sb("rKTr", [64, 8, TG], CHDT)
        AR = b.sb("rAR", [64, 8, NCH, 2, 64], CHDT)
        BON = b.sb("rBON", [64, NCH * 8], F32)
        Ytm = b.sb("rYtm", [64, NCH, 8, 64], F32)
        sqv = b.sb("rsqv", [64, NCH, 8, 64], F32)
        st1 = b.sb("rst1", [64, NCH * 8], F32)
        st2 = b.sb("rst2", [64, NCH * 8], F32)
        TM4 = [b.sb(f"rTM{i}", [64, 4, 2, 64], CHDT) for i in range(2)]
        XM4 = [b.sb(f"rXM{i}", [64, 4, 4, 64], CHDT) for i in range(2)]
        AA4 = [[b.sb(f"rAA{u}_{i}", [64, 4, 2, 64], CHDT) for i in range(2)] for u in range(2)]
        PP4 = [[b.sb(f"rPP{u}_{i}", [64, 4, 64], CHDT) for i in range(2)] for u in range(2)]
        Xs8 = b.sb("rXs8", [64, 8, 64], CHDT)
        Us8 = b.sb("rUs8", [64, 8, 64], CHDT)
        Ht8 = b.sb("rHt8", [64, 8, 64], F32)
        OBb = b.sb("rOBb", [64, NCH, 512], BF16)
        obT = [b.sb(f"robT{i}", [128, 4, TG], BF16) for i in range(1)] * 2
        pt = b.ps("rpt", [128, 8, 128], BF16)
        pP = b.ps("rpP", [128, 512], F32)
        pA = b.ps("rpA", [128, 1024], F32)
        pB = b.ps("rpB", [128, 512], F32)
        pC = b.ps("rpC", [128, 512], F32)
        pD = b.ps("rpD", [128, 512], F32)
        pZ = b.ps("rpZ", [128, 512], F32)
        cnt = {}

        def nxt(k, lst):
            cnt[k] = cnt.get(k, 0) + 1
            return lst[cnt[k] % len(lst)]
        bc = lambda v: v[:].unsqueeze(2).to_broadcast([64, 8, TG])
        f2 = lambda t_: t_[:].rearrange("p h t -> p (h t)")
        c16 = lambda t_: t_[:].rearrange("p h (c t) -> p (h c) t", t=64)

        ngr = getattr(self, "nrg_limit", S // TG)
        RR = lambda ap: ap

        def emit_inproj(gi):
            i = gi % 2
            self.make_hT(I["x"], gi, xt[i], junk, ss[i], hb[i], pt, hT1[i], self.ident)
            b.op("dve", lambda e: e.tensor_copy(out=PB[:, :, 0:1], in_=PB[:, :, TG:TG + 1]), reads=[PB], writes=[PB])
            for r7 in range(7):
                for j in range(4):
                    fc = r7 * 4 + j
                    for c in range(8):
                        b.op("pe", lambda e: e.matmul(pP[0:64, j * TG:(j + 1) * TG], lhsT=W1[:, c, fc * 64:(fc + 1) * 64], rhs=hT1[i][:, c, :], start=(c == 0), stop=(c == 7)),
                             reads=[W1, hT1[i]], writes=[pP])
                b.op("act", lambda e: e.copy(out=PB[:, r7 * 4:(r7 + 1) * 4, 1:TG + 1], in_=pP[0:64, :].rearrange("p (a t) -> p a t", t=TG)), reads=[pP], writes=[PB])

        emit_inproj(0)
        for gi in range(ngr):
            q0 = gi * TG
            tt("dve", XL[:], PB[:, :, 0:TG], PB[:, :, 1:TG + 1], ALU.subtract, [PB], [XL])
            tt("dve", XL[:], XL[:], mu[:].unsqueeze(2).to_broadcast([64, 28, TG]), ALU.mult, [XL, mu], [XL])
            tt("dve", XL[:], XL[:], PB[:, :, 1:TG + 1], ALU.add, [XL, PB], [XL])
            for c_ in range(NCH):
                for h in range(8):
                    b.op("pe", lambda e: e.transpose(out=pC[0:64, h * 64:(h + 1) * 64], in_=XL[:, 16 + h, c_ * 64:(c_ + 1) * 64], identity=idf[0:64, 0:64]), reads=[XL, idf], writes=[pC])
                b.op("act", lambda e: e.copy(out=Vtm[:, c_, :], in_=pC[0:64, :]), reads=[pC], writes=[Vtm])
            R_ = XL[:, 0:8, :]
            K_ = XL[:, 8:16, :]
            b.op("act", lambda e: e.activation(out=XL[:, 24, :], in_=XL[:, 24, :], func=AF.Tanh), reads=[XL], writes=[XL])
            b.op("act", lambda e: e.activation(out=XL[:, 26:28, :], in_=XL[:, 26:28, :], func=AF.Sigmoid), reads=[XL], writes=[XL])
            for (ws_, src, bias_, dst) in [(w2s, 24, w0, "LW"), (a2s, 25, a0, "AS")]:
                for half in range(2):
                    for j in range(4):
                        h = half * 4 + j
                        b.op("pe", lambda e: e.matmul(pP[0:64, j * TG:(j + 1) * TG], lhsT=ws_[:, h * 64:(h + 1) * 64], rhs=XL[:, src, :], start=True, stop=True),
                             reads=[ws_, XL], writes=[pP])
                    for j in range(4):
                        h = half * 4 + j
                        b.op("act", lambda e: e.activation(out=T[dst][:, h, :], in_=pP[0:64, j * TG:(j + 1) * TG], func=AF.Sigmoid, bias=bias_[:, h:h + 1]),
                             reads=[pP, bias_], writes=[T[dst]])
            b.op("dve", lambda e: e.tensor_scalar_mul(out=f2(T["LW"]), in0=f2(T["LW"]), scalar1=-0.6065306597126334), reads=[T["LW"]], writes=[T["LW"]])
            tt("dve", T["KK"][:], K_, bc(k_k), ALU.mult, [XL, k_k], [T["KK"]])
            tt("dve", T["NR"][:], T["KK"][:], T["KK"][:], ALU.mult, [T["KK"]], [T["NR"]])
            for half in range(2):
                b.op("pe", lambda e: e.matmul(pP[0:64, :], lhsT=ones[:], rhs=T["NR"][:, half * 4:(half + 1) * 4, :].rearrange("p h t -> p (h t)"), start=True, stop=True),
                     reads=[ones, T["NR"]], writes=[pP])
                b.op("act", lambda e: e.activation(out=T["KKN"][:, half * 4:(half + 1) * 4, :].rearrange("p h t -> p (h t)"), in_=pP[0:64, :], func=AF.Sqrt),
                     reads=[pP], writes=[T["KKN"]])
            b.op("dve", lambda e: e.tensor_scalar_max(out=f2(T["KKN"]), in0=f2(T["KKN"]), scalar1=1e-12), reads=[T["KKN"]], writes=[T["KKN"]])
            b.op("dve", lambda e: e.reciprocal(out=f2(T["KKN"]), in_=f2(T["KKN"])), reads=[T["KKN"]], writes=[T["KKN"]])
            tt("dve", T["KKN"][:], T["KKN"][:], T["KK"][:], ALU.mult, [T["KKN"], T["KK"]], [T["KKN"]])
            tt("dve", T["BVc"][:], T["KKN"][:], T["AS"][:], ALU.mult, [T["KKN"], T["AS"]], [T["BVc"]])
            b.op("pool", lambda e: e.tensor_scalar_add(out=f2(T["T1"]), in0=f2(T["AS"]), scalar1=-1.0), reads=[T["AS"]], writes=[T["T1"]])
            tt("pool", T["T1"][:], T["T1"][:], bc(k_a), ALU.mult, [T["T1"], k_a], [T["T1"]])
            b.op("dve", lambda e: e.scalar_tensor_tensor(out=f2(T["KP"]), in0=f2(T["T1"]), scalar=1.0, in1=K_.rearrange("p h t -> p (h t)"), op0=ALU.add, op1=ALU.mult),
                 reads=[T["T1"], XL], writes=[T["KP"]])
            tt("pool", T["RK"][:], R_, T["KP"][:], ALU.mult, [XL, T["KP"]], [T["RK"]])
            tt("pool", T["RK"][:], T["RK"][:], bc(r_k), ALU.mult, [T["RK"], r_k], [T["RK"]])
            for c_ in range(NCH):
                for h in range(8):
                    b.op("pe", lambda e: e.matmul(pD[0:64, c_ * 8 + h:c_ * 8 + h + 1], lhsT=T["RK"][:, h, c_ * 64:(c_ + 1) * 64], rhs=ones[:, 0:1], start=True, stop=True),
                         reads=[T["RK"], ones], writes=[pD])
            b.op("act", lambda e: e.copy(out=BON[:], in_=pD[0:64, 0:NCH * 8]), reads=[pD], writes=[BON])
            b.op("dve", lambda e: e.tensor_tensor_scan(out=f2(T["L"]), data0=rstm[:], data1=f2(T["LW"]), initial=0.0, op0=ALU.mult, op1=ALU.add),
                 reads=[rstm, T["LW"]], writes=[T["L"]])
            b.op("act", lambda e: e.activation(out=f2(T["EP"]), in_=f2(T["L"]), func=AF.Exp), reads=[T["L"]], writes=[T["EP"]])
            b.op("act", lambda e: e.activation(out=f2(T["EM"]), in_=f2(T["L"]), func=AF.Exp, scale=-1.0), reads=[T["L"]], writes=[T["EM"]])
            tt("dve", T["L"][:], T["L"][:], T["LW"][:], ALU.subtract, [T["L"], T["LW"]], [T["L"]])
            b.op("act", lambda e: e.activation(out=f2(T["EX"]), in_=f2(T["L"]), func=AF.Exp), reads=[T["L"]], writes=[T["EX"]])
            ar0 = AR[:, :, :, 0, :].rearrange("p h c t -> p (h c) t")
            ar1 = AR[:, :, :, 1, :].rearrange("p h c t -> p (h c) t")
            b.op("dve", lambda e: e.scalar_tensor_tensor(out=ar0, in0=c16(T["KKN"]), scalar=-1.0, in1=c16(T["EX"]), op0=ALU.mult, op1=ALU.mult),
                 reads=[T["KKN"], T["EX"]], writes=[AR])
            tt("dve", ar1, R_.rearrange("p h (c t) -> p (h c) t", t=64), c16(T["EP"]), ALU.mult, [XL, T["EP"]], [AR])
            tt("dve", T["BT"][:], T["BVc"][:], T["EM"][:], ALU.mult, [T["BVc"], T["EM"]], [T["BT"]])
            tt("dve", T["KT"][:], T["KP"][:], T["EM"][:], ALU.mult, [T["KP"], T["EM"]], [T["KT"]])
            gC = c16(T["EP"])[:, :, 63:64].to_broadcast([64, 16, 64])
            tt("dve", c16(T["BG"]), c16(T["BT"]), gC, ALU.mult, [T["BT"], T["EP"]], [T["BG"]])
            tt("dve", c16(T["KG"]), c16(T["KT"]), gC, ALU.mult, [T["KT"], T["EP"]], [T["KG"]])
            if gi + 1 < ngr:
                emit_inproj(gi + 1)
            for c_ in range(NCH):
                cs = slice(c_ * 64, (c_ + 1) * 64)
                cur = (gi * NCH + c_) % 2
                U_ = []
                for u in range(2):
                    heads = list(range(u * 4, u * 4 + 4))
                    pBu = pB if u == 0 else pC
                    for j, h in enumerate(heads):
                        b.op("pe", lambda e: e.transpose(out=pZ[0:64, j * 128:j * 128 + 64], in_=T["BG"][:, h, cs], identity=idf[0:64, 0:64]), reads=[T["BG"], idf], writes=[pZ])
                        b.op("pe", lambda e: e.transpose(out=pZ[0:64, j * 128 + 64:(j + 1) * 128], in_=T["KG"][:, h, cs], identity=idf[0:64, 0:64]), reads=[T["KG"], idf], writes=[pZ])
                    tm = TM4[u]
                    b.op("act", lambda e: e.copy(out=tm[:].rearrange("p h a t -> p (h a t)"), in_=pZ[0:64, 0:512]), reads=[pZ], writes=[tm])
                    for j, h in enumerate(heads):
                        arc = AR[:, h, c_, :, :].rearrange("p a t -> p (a t)")
                        b.op("pe", lambda e: e.matmul(pA[0:64, j * 256:j * 256 + 128], lhsT=T["BT"][:, h, cs], rhs=arc, start=True, stop=True), reads=[T["BT"], AR], writes=[pA])
                        b.op("pe", lambda e: e.matmul(pA[0:64, j * 256 + 128:(j + 1) * 256], lhsT=T["KT"][:, h, cs], rhs=arc, start=True, stop=True), reads=[T["KT"], AR], writes=[pA])
                        b.op("pe", lambda e: e.matmul(pBu[0:64, j * 64:(j + 1) * 64], lhsT=AR[:, h, c_, 0, :], rhs=T["BT"][:, h, cs], start=True, stop=True), reads=[T["BT"], AR], writes=[pBu])
                    xm = XM4[u]
                    tt("dve", xm[:].rearrange("p h (a m) t -> p (h a) m t", a=2), pA[0:64, :].rearrange("p (ha m t) -> p ha m t", m=2, t=64),
                       msk[:, None, 0:2, :].to_broadcast([64, 8, 2, 64]), ALU.mult, [pA, msk], [xm])
                    aa = AA4[u][0]
                    b.op("dve", lambda e: e.tensor_copy(out=aa[:, :, 0, :], in_=xm[:, :, 0, :]), reads=[xm], writes=[aa])
                    tt("dve", aa[:, :, 1, :], pBu[0:64, 0:256].rearrange("p (h t) -> p h t", t=64), msk[:, 2:3, :].to_broadcast([64, 4, 64]), ALU.mult, [pBu, msk], [aa])
                    P_ = PP4[u][0]
                    tt("dve", P_[:], xm[:, :, 0, :], idf[0:64, None, 0:64].to_broadcast([64, 4, 64]), ALU.add, [xm, idf], [P_])
                    U_.append(dict(tm=tm, xm=xm, aa=aa, P=P_, pB=pBu, pD=(pD if u == 0 else pP), k=0))
                for step in range(5):
                    for u_ in U_:
                        aa, pDu = u_["aa"], u_["pD"]
                        for j in range(4):
                            b.op("pe", lambda e: e.matmul(pDu[0:64, j * 128:j * 128 + 64], lhsT=RR(aa[:, j, 1, :]), rhs=RR(aa[:, j, 0, :]), start=True, stop=True), reads=[aa], writes=[pDu])
                            b.op("pe", lambda e: e.matmul(pDu[0:64, j * 128 + 64:(j + 1) * 128], lhsT=RR(aa[:, j, 0, :]), rhs=RR(aa[:, j, 1, :]), start=True, stop=True), reads=[aa], writes=[pDu])
                    for ui, u_ in enumerate(U_):
                        u_["k"] += 1
                        aa2 = AA4[ui][u_["k"] % 2]
                        b.op("act", lambda e: e.copy(out=aa2[:].rearrange("p h a t -> p (h a t)"), in_=u_["pD"][0:64, :]), reads=[u_["pD"]], writes=[aa2])
                        u_["aa"] = aa2
                    for u_ in U_:
                        for j in range(4):
                            b.op("pe", lambda e: e.matmul(u_["pB"][0:64, 256 + j * 64:256 + (j + 1) * 64], lhsT=RR(u_["aa"][:, j, 1, :]), rhs=RR(u_["P"][:, j, :]), start=True, stop=True),
                                 reads=[u_["aa"], u_["P"]], writes=[u_["pB"]])
                    for ui, u_ in enumerate(U_):
                        P2 = PP4[ui][u_["k"] % 2]
                        tt("dve", P2[:], u_["pB"][0:64, 256:512].rearrange("p (h t) -> p h t", t=64), u_["P"][:], ALU.add, [u_["pB"], u_["P"]], [P2])
                        u_["P"] = P2
                for h in range(8):
                    u_, j = U_[h // 4], h % 4
                    o = slice(h * 64, (h + 1) * 64)
                    b.op("pe", lambda e: e.matmul(pA[0:64, o], lhsT=u_["xm"][:, j, 2, :], rhs=Vtm[:, c_, o], start=True, stop=False), reads=[u_["xm"], Vtm], writes=[pA])
                    b.op("pe", lambda e: e.matmul(pA[0:64, o], lhsT=AR[:, h, c_, 0, :], rhs=Hst[:, cur, h, :], start=False, stop=True), reads=[AR, Hst], writes=[pA])
                b.op("act", lambda e: e.copy(out=Xs8[:].rearrange("p h t -> p (h t)"), in_=pA[0:64, 0:512]), reads=[pA], writes=[Xs8])
                for h in range(8):
                    u_, j = U_[h // 4], h % 4
                    b.op("pe", lambda e: e.matmul(pA[0:64, 512 + h * 64:512 + (h + 1) * 64], lhsT=u_["P"][:, j, :], rhs=Xs8[:, h, :], start=True, stop=True), reads=[u_["P"], Xs8], writes=[pA])
                b.op("act", lambda e: e.copy(out=Us8[:].rearrange("p h t -> p (h t)"), in_=pA[0:64, 512:1024]), reads=[pA], writes=[Us8])
                for h in range(8):
                    u_, j = U_[h // 4], h % 4
                    o = slice(h * 64, (h + 1) * 64)
                    b.op("pe", lambda e: e.matmul(pD[0:64, o], lhsT=u_["tm"][:, j, 0, :], rhs=Us8[:, h, :], start=True, stop=False), reads=[u_["tm"], Us8], writes=[pD])
                    b.op("pe", lambda e: e.matmul(pD[0:64, o], lhsT=u_["tm"][:, j, 1, :], rhs=Vtm[:, c_, o], start=False, stop=True), reads=[u_["tm"], Vtm], writes=[pD])
                tt("dve", Ht8[:], Hst[:, cur, :, :], T["EP"][:, :, c_ * 64 + 63:c_ * 64 + 64].to_broadcast([64, 8, 64]), ALU.mult, [Hst, T["EP"]], [Ht8])
                tt("dve", Hst[:, 1 - cur, :, :], pD[0:64, :].rearrange("p (h t) -> p h t", t=64), Ht8[:], ALU.add, [pD, Ht8], [Hst])
                for h in range(8):
                    u_, j = U_[h // 4], h % 4
                    o = slice(h * 64, (h + 1) * 64)
                    b.op("pe", lambda e: e.matmul(pZ[0:64, o], lhsT=AR[:, h, c_, 1, :], rhs=Hst[:, cur, h, :], start=True, stop=False), reads=[AR, Hst], writes=[pZ])
                    b.op("pe", lambda e: e.matmul(pZ[0:64, o], lhsT=u_["xm"][:, j, 1, :], rhs=Us8[:, h, :], start=False, stop=False), reads=[u_["xm"], Us8], writes=[pZ])
                    b.op("pe", lambda e: e.matmul(pZ[0:64, o], lhsT=u_["xm"][:, j, 3, :], rhs=Vtm[:, c_, o], start=False, stop=True), reads=[u_["xm"], Vtm], writes=[pZ])
                b.op("act", lambda e: e.copy(out=Ytm[:, c_, :, :].rearrange("p h t -> p (h t)"), in_=pZ[0:64, :]), reads=[pZ], writes=[Ytm])
            Y3 = Ytm[:].rearrange("p c h i -> p (c h) i")
            S3 = sqv[:].rearrange("p c h i -> p (c h) i")
            b.op("dve", lambda e: e.tensor_reduce(out=st1[:], in_=Y3, axis=AX.X, op=ALU.add), reads=[Ytm], writes=[st1])
            b.op("dve", lambda e: e.tensor_scalar_mul(out=st1[:], in0=st1[:], scalar1=1.0 / 64), reads=[st1], writes=[st1])
            tt("dve", Y3, Y3, st1[:].unsqueeze(2).to_broadcast([64, NCH * 8, 64]), ALU.subtract, [Ytm, st1], [Ytm])
            tt("dve", S3, Y3, Y3, ALU.mult, [Ytm], [sqv])
            b.op("dve", lambda e: e.tensor_reduce(out=st2[:], in_=S3, axis=AX.X, op=ALU.add), reads=[sqv], writes=[st2])
            b.op("act", lambda e: e.activation(out=st2[:], in_=st2[:], func=AF.Sqrt, scale=1.0 / 64, bias=64e-5), reads=[st2], writes=[st2])
            b.op("dve", lambda e: e.reciprocal(out=st2[:], in_=st2[:]), reads=[st2], writes=[st2])
            tt("dve", Y3, Y3, st2[:].unsqueeze(2).to_broadcast([64, NCH * 8, 64]), ALU.mult, [Ytm, st2], [Ytm])
            lg = lng[:].rearrange("p (h i) -> p h i", i=64)[:, None, :, :].to_broadcast([64, NCH, 8, 64])
            lb = lnb[:].rearrange("p (h i) -> p h i", i=64)[:, None, :, :].to_broadcast([64, NCH, 8, 64])
            tt("dve", Ytm[:], Ytm[:], lg, ALU.mult, [Ytm, lng], [Ytm])
            tt("dve", Ytm[:], Ytm[:], lb, ALU.add, [Ytm, lnb], [Ytm])
            V3 = Vtm[:].rearrange("p c (h i) -> p (c h) i", i=64)
            tt("dve", S3, V3, BON[:].unsqueeze(2).to_broadcast([64, NCH * 8, 64]), ALU.mult, [Vtm, BON], [sqv])
            tt("dve", Y3, Y3, S3, ALU.add, [Ytm, sqv], [Ytm])
            for c_ in range(NCH):
                for two in range(2):
                    b.op("pe", lambda e: e.matmul(pP[0:64, :], lhsT=XL[:, 26 + two, c_ * 64:(c_ + 1) * 64], rhs=g2s[:, two, :], start=(two == 0), stop=(two == 1)),
                         reads=[XL, g2s], writes=[pP])
                tt("dve", OBb[:, c_, :], Ytm[:, c_, :, :].rearrange("p h i -> p (h i)"), pP[0:64, :], ALU.mult, [Ytm, pP], [OBb])
                for k4 in range(4):
                    b.op("pe", lambda e: e.transpose(out=pt[:, k4, c_ * 64:(c_ + 1) * 64], in_=OBb[:, c_, k4 * 128:(k4 + 1) * 128], identity=self.ident[0:64, 0:64]),
                         reads=[OBb, self.ident], writes=[pt])
            ot = obT[gi % 2]
            b.op("act", lambda e: e.copy(out=ot[:], in_=pt[:, 0:4, :]), reads=[pt], writes=[ot])
            b.dma("pool", self.obT_d[:, :, q0:q0 + TG].rearrange("c p t -> p c t"), ot[:], reads=[ot], writes=[self.obT_d])
        if "rwkv" in self.debug:
            d = self.dbg_out("obT", [4, 128, S], BF16)
            b.dma("pool", d, self.obT_d[:], reads=[self.obT_d])


Prog.phase_rwkv3 = _phase_rwkv3


def _phase_ffn2(self):
    b = self.b
    I = self.inp
    TG = 256
    NFT = 44
    with b.scope():
        gf = self.load_gain("gf", I["ffn_norm_g"][0])
        stage = [b.sb(f"fst{i}", [128, 1024], F32) for i in range(2)]
        wu = b.sb("wu", [128, 8, 2 * DFF], BF16)
        for n in range(8):
            for c in range(8):
                st = stage[c % 2]
                b.dma("sp", st[:, 0:704], I["w_up"][0][c * 128:(c + 1) * 128, n * 704:(n + 1) * 704], writes=[st])
                b.op("act", lambda e: e.activation(out=wu[:, c, n * 704:(n + 1) * 704], in_=st[:, 0:704], func=AF.Copy, scale=gf[:, c:c + 1]),
                     reads=[st, gf], writes=[wu])
        wd = b.sb("wd", [128, 22, D], BF16)
        self.load_weight(wd, I["w_down"][0], D, kch=22, stage=stage, eng="dve")
        cw = b.sb("cw", [128, 3, NFT], F32)
        for j in range(3):
            b.dma("sp", cw[:, j, :], I["conv_w"][0][j].rearrange("(c p) -> p c", p=128), writes=[cw], allow_slow_non_contiguous=True)
        cbias = self.load_gain("cbias", I["conv_b"][0], kch=NFT)
        xt = [b.sb(f"fxt{i}", [128, D], F32) for i in range(2)]
        junk = b.sb("fjunk", [128, D], BF16)
        ss = [b.sb(f"fss{i}", [128, 1], F32) for i in range(2)]
        hb = [b.sb(f"fhb{i}", [128, D], BF16) for i in range(2)]
        hTg = b.sb("fhTg", [128, 8, TG + 2], BF16)
        b.op("pool", lambda e: e.memset(hTg[:], 0.0), writes=[hTg])
        cv = [b.sb(f"cv{i}", [128, TG], F32) for i in range(3)]
        sgl = [b.sb(f"sgl{i}", [128, TG], BF16) for i in range(2)]
        actT = b.sb("actT", [128, 22, TG], BF16)
        val = b.sb("fval", [128, 22, TG], BF16)
        pt = b.ps("fpt", [128, 8, 128], BF16)
        pu = [b.ps(f"fpu{i}", [128, 512], F32) for i in range(4)]
        pd = [b.ps(f"fpd{i}", [128, 512], F32) for i in range(2)]
        ng = getattr(self, "nt_limit", NT) * 128 // TG
        for gi in range(ng):
            b.op("pool", lambda e: e.tensor_copy(out=hTg[:, :, 0:2], in_=hTg[:, :, TG:TG + 2]), reads=[hTg], writes=[hTg])
            for s_ in range(TG // 128):
                t = gi * (TG // 128) + s_
                self.make_hT(self.x1_d, t, xt[s_], junk, ss[s_], hb[s_], pt, hTg, self.ident, hT_ap=hTg[:, :, 2 + s_ * 128:2 + (s_ + 1) * 128])
            for ft in range(NFT):
                p = pu[ft % 4]
                c_ = cv[ft % 3]
                for c in range(8):
                    b.op("pe", lambda e: e.matmul(p[:, 0:TG + 2], lhsT=wu[:, c, ft * 128:(ft + 1) * 128], rhs=hTg[:, c, :], start=(c == 0), stop=(c == 7)),
                         reads=[wu, hTg], writes=[p])
                b.op("act", lambda e: e.activation(out=c_[:], in_=p[:, 0:TG], func=AF.Identity, scale=cw[:, 0, ft:ft + 1], bias=cbias[:, ft:ft + 1]),
                     reads=[p, cw, cbias], writes=[c_])
                b.op("dve", lambda e: e.scalar_tensor_tensor(out=c_[:], in0=p[:, 1:TG + 1], scalar=cw[:, 1, ft:ft + 1], in1=c_[:], op0=ALU.mult, op1=ALU.add),
                     reads=[p, cw, c_], writes=[c_])
                if ft < 22:
                    b.op("dve", lambda e: e.scalar_tensor_tensor(out=val[:, ft, :], in0=p[:, 2:TG + 2], scalar=cw[:, 2, ft:ft + 1], in1=c_[:], op0=ALU.mult, op1=ALU.add),
                         reads=[p, cw, c_], writes=[val])
                else:
                    sg_ = sgl[ft % 2]
                    b.op("dve", lambda e: e.scalar_tensor_tensor(out=c_[:], in0=p[:, 2:TG + 2], scalar=cw[:, 2, ft:ft + 1], in1=c_[:], op0=ALU.mult, op1=ALU.add),
                         reads=[p, cw, c_], writes=[c_])
                    b.op("act", lambda e: e.activation(out=sg_[:], in_=c_[:], func=AF.Silu), reads=[c_], writes=[sg_])
                    b.op("pool", lambda e: e.tensor_tensor(out=actT[:, ft - 22, :], in0=sg_[:], in1=val[:, ft - 22, :], op=ALU.mult),
                         reads=[sg_, val], writes=[actT])
            for s_ in range(TG // 128):
                t = gi * (TG // 128) + s_
                for n in range(2):
                    for f in range(22):
                        b.op("pe", lambda e: e.matmul(pd[n][:, :], lhsT=actT[:, f, s_ * 128:(s_ + 1) * 128], rhs=wd[:, f, n * 512:(n + 1) * 512], start=(f == 0), stop=(f == 21)),
                             reads=[actT, wd], writes=[pd[n]])
                    b.op("dve", lambda e: e.tensor_tensor(out=xt[s_][:, n * 512:(n + 1) * 512], in0=pd[n][:, :], in1=xt[s_][:, n * 512:(n + 1) * 512], op=ALU.add),
                         reads=[pd[n], xt[s_]], writes=[xt[s_]])
                b.dma("pool", self.out[t * 128:(t + 1) * 128, :], xt[s_][:], reads=[xt[s_]])


Prog.phase_ffn2 = _phase_ffn2
```

```python
import contextlib
import numpy as np
import ml_dtypes
import concourse.bass as bass
import concourse.mybir as mybir
from concourse.bass_utils import run_bass_kernel_spmd

F32 = mybir.dt.float32
BF16 = mybir.dt.bfloat16
AF = mybir.ActivationFunctionType
ALU = mybir.AluOpType
AX = mybir.AxisListType

S = 4096
D = 1024
NT = S // 128
IN_WIDTH = 5144
RW0 = 1304
GA0 = 3096
GB0 = 4120
DFF = 2816
RMS_EPS = 1e-6


class Buf:
    def __init__(self, t, name):
        self.t = t
        self.name = name
        self.w = None
        self.r = {}
        self.psum = False

    def __getitem__(self, idx):
        return self.t[idx]


class Builder:
    SEM_ROLL = 30000

    def __init__(self, nc):
        self.nc = nc
        self.stack = contextlib.ExitStack()
        self.root = self.stack
        self.eng = {"pe": nc.tensor, "act": nc.scalar, "dve": nc.vector,
                    "pool": nc.gpsimd, "sp": nc.sync}
        self.sem = {}
        self.cnt = {}
        self.seen = {e: {} for e in self.eng}
        self.nsem = 0
        self.lanes = {}
        self.lane_rr = {}
        self.last_tok = {}
        for e in self.eng:
            self._roll(e)

    def newsem(self, name):
        self.nsem += 1
        return self.root.enter_context(self.nc.semaphore(f"{name}_{self.nsem}"))

    def sb(self, name, shape, dt=F32):
        self.nsem += 1
        name = f"sb{self.nsem}_{name}"
        return Buf(self.stack.enter_context(self.nc.sbuf_tensor(name, list(shape), dt)), name)

    def ps(self, name, shape, dt=F32):
        self.nsem += 1
        name = f"ps{self.nsem}_{name}"
        bf = Buf(self.stack.enter_context(self.nc.psum_tensor(name, list(shape), dt)), name)
        bf.psum = True
        return bf

    def dram(self, name, shape, dt=F32, kind="Internal"):
        return Buf(self.nc.dram_tensor(name, list(shape), dt, kind=kind), name)

    def _roll(self, e):
        self.sem[e] = self.newsem("s" + e)
        self.cnt[e] = 0

    def _wait(self, e, tok):
        sem, val = tok
        k = id(sem)
        if self.seen[e].get(k, 0) < val:
            self.eng[e].wait_ge(sem, val)
            self.seen[e][k] = val

    def _deps(self, e, reads, writes):
        for b in reads:
            if b.w is not None:
                we, tok = b.w
                self._wait(e, tok)
            if b.psum:
                for re_, tok in b.r.items():
                    if re_ != e:
                        self._wait(e, tok)
        for b in writes:
            if b.w is not None:
                we, tok = b.w
                if we != e:
                    self._wait(e, tok)
            for re_, tok in b.r.items():
                if re_ != e:
                    self._wait(e, tok)

    def op(self, e, fn, reads=(), writes=()):
        if self.cnt[e] >= self.SEM_ROLL:
            self._roll(e)
        self._deps(e, reads, writes)
        ins = fn(self.eng[e])
        self.cnt[e] += 1
        tok = (self.sem[e], self.cnt[e])
        ins.then_inc(self.sem[e], 1)
        self.last_tok[e] = tok
        for b in reads:
            b.r[e] = tok
        for b in writes:
            b.w = (e, tok)
            b.r = {}
        return tok

    def dma(self, q, out, in_, reads=(), writes=(), nlanes=6, **kw):
        if q not in self.lanes:
            self.lanes[q] = [[self.newsem("l" + q), 0] for _ in range(nlanes)]
            self.lane_rr[q] = 0
        li = self.lane_rr[q]
        self.lane_rr[q] = (li + 1) % len(self.lanes[q])
        lane = self.lanes[q][li]
        if lane[1] >= 1800:
            self._wait(q, (lane[0], 16 * lane[1]))
            lane[0] = self.newsem("l" + q)
            lane[1] = 0
        if lane[1] > 0:
            self._wait(q, (lane[0], 16 * lane[1]))
        self._deps_dma(q, reads, writes)
        ins = self.eng[q].dma_start(out=out, in_=in_, **kw)
        lane[1] += 1
        tok = (lane[0], 16 * lane[1])
        ins.then_inc(lane[0], 16)
        key = "dma_" + q + str(li)
        for b in reads:
            b.r[key] = tok
        for b in writes:
            b.w = (key, tok)
            b.r = {}
        return tok

    def _deps_dma(self, q, reads, writes):
        for b in reads:
            if b.w is not None:
                self._wait(q, b.w[1])
        for b in writes:
            if b.w is not None:
                self._wait(q, b.w[1])
            for re_, tok in b.r.items():
                self._wait(q, tok)

    def barrier(self):
        toks = list(self.last_tok.values())
        for q, lanes in self.lanes.items():
            for lane in lanes:
                if lane[1] > 0:
                    toks.append((lane[0], 16 * lane[1]))
        for e in self.eng:
            for tok in toks:
                self._wait(e, tok)

    def wait_all_on(self, e):
        toks = list(self.last_tok.values())
        for q, lanes in self.lanes.items():
            for lane in lanes:
                if lane[1] > 0:
                    toks.append((lane[0], 16 * lane[1]))
        for tok in toks:
            self._wait(e, tok)

    @contextlib.contextmanager
    def scope(self):
        old = self.stack
        self.stack = contextlib.ExitStack()
        try:
            yield
            self.barrier()
        finally:
            self.stack.close()
            self.stack = old

    def close(self):
        self.stack.close()


NEG = -30000.0


def _bucket(dist):
    n = np.maximum(dist, 0)
    ratio = np.log(np.maximum(n, 1).astype(np.float32) / np.float32(16.0)) / np.float32(np.log(8.0))
    large = np.minimum(16 + (ratio * 16).astype(np.int32), 31)
    return np.where(n < 16, n, large)


def host_consts(rel_bias):
    rel = np.asarray(rel_bias, np.float32)
    c = {}
    c["ident"] = np.eye(128, dtype=np.float32).astype(ml_dtypes.bfloat16)
    c["identf"] = np.eye(128, dtype=np.float32)
    kp = np.arange(128)[:, None]
    cc = np.arange(640)[None, :]
    dist = cc - kp
    bt = rel[_bucket(dist)]
    tw = np.where(((dist >= 0) & (dist < 512))[..., None], bt, np.float32(NEG))
    ts = np.where((dist >= 0)[..., None], bt, np.float32(NEG))
    c["tw"] = np.ascontiguousarray(tw.transpose(0, 2, 1)).astype(np.float32)
    c["ts"] = np.ascontiguousarray(ts.transpose(0, 2, 1)).astype(np.float32)
    cidx = np.arange(256)[:, None]
    qidx = np.arange(S)[None, :]
    dc = qidx - 16 * cidx - 31
    bcg = rel[_bucket(dc)]
    ok = (dc >= 0) & (cidx < 255)
    bc = np.where(ok[..., None], bcg, np.float32(NEG))
    c["biasc"] = np.ascontiguousarray(bc.transpose(2, 0, 1)).reshape(8, 2, 128, S).astype(np.float32)
    A = np.zeros((256, 64), np.float32)
    Wt = (1, 2, 2, 2, 1)
    for ci in range(255):
        for j in range(64):
            o = ci + 1 - 4 * j
            if 0 <= o <= 4:
                A[ci, j] = Wt[o]
    c["amat"] = A.reshape(2, 128, 64)
    E = np.zeros((64, S), np.float32)
    E[np.arange(S) // 64, np.arange(S)] = 1.0
    c["emat"] = E.astype(ml_dtypes.bfloat16)
    qp = np.arange(128)[:, None, None]
    qt = np.arange(32)[None, :, None]
    j = np.arange(64)[None, None, :]
    cur = (128 * qt + qp) // 64
    cand = (j >= 1) & (j <= cur - 2)
    c["candneg"] = np.where(cand, 0.0, -1e9).astype(np.float32)
    c["fz"] = ((j == 0) | (j == cur) | (j == cur - 1)).astype(np.float32)
    tri = np.triu(np.ones((64, 64), np.float32))
    c["rwmask"] = np.ascontiguousarray(np.stack([np.triu(np.ones((64, 64), np.float32), 1), tri, np.tril(np.ones((64, 64), np.float32), -1)], axis=1))
    rr = np.ones((64, 1024), np.float32)
    rr[:, ::64] = 0.0
    c["rwreset"] = rr
    c["b31"] = np.ascontiguousarray(np.broadcast_to(rel[31][None, :], (128, 8))).astype(np.float32)
    return c


CONST_SPECS = {
    "ident": ([128, 128], BF16), "identf": ([128, 128], F32),
    "tw": ([128, 8, 640], F32), "ts": ([128, 8, 640], F32),
    "biasc": ([8, 2, 128, S], F32), "amat": ([2, 128, 64], F32),
    "emat": ([64, S], BF16), "candneg": ([128, 32, 64], F32), "fz": ([128, 32, 64], F32),
    "b31": ([128, 8], F32), "rwmask": ([64, 3, 64], F32), "rwreset": ([64, 1024], F32),
}

W_SPECS = {
    "x": [S, D], "attn_norm_g": [1, D], "w_in": [1, D, IN_WIDTH], "q_norm_g": [1, 64], "k_norm_g": [1, 64],
    "cmp_pe_k": [1, 32, 64], "cmp_w1_k": [1, 2048, 256], "cmp_w2_k": [1, 256, 64],
    "cmp_pe_v": [1, 32, 64], "cmp_w1_v": [1, 2048, 256], "cmp_w2_v": [1, 256, 64],
    "rwkv_mu": [1, 1792], "rwkv_w0": [1, 512], "rwkv_w2": [1, 64, 512], "rwkv_a0": [1, 512],
    "rwkv_a2": [1, 64, 512], "rwkv_g2": [1, 128, 512], "rwkv_k_k": [1, 512], "rwkv_k_a": [1, 512],
    "rwkv_r_k": [1, 8, 64], "rwkv_ln_g": [1, 512], "rwkv_ln_b": [1, 512],
    "w_proj_a": [1, 512, D], "w_proj_b": [1, 512, D], "w_out": [1, D, D], "ffn_norm_g": [1, D],
    "w_up": [1, D, 2 * DFF], "conv_w": [1, 3, 2 * DFF], "conv_b": [1, 2 * DFF], "w_down": [1, DFF, D],
}


class Prog:
    def __init__(self, debug=()):
        self.debug = set(debug)
        nc = bass.Bass("TRN2", target_bir_lowering=False)
        self.nc = nc
        self.inp = {}
        for k, shp in W_SPECS.items():
            self.inp[k] = nc.dram_tensor(k, list(shp), F32, kind="ExternalInput").ap()
        for k, (shp, dt) in CONST_SPECS.items():
            self.inp[k] = nc.dram_tensor(k, list(shp), dt, kind="ExternalInput").ap()
        self.out = nc.dram_tensor("out", [S, D], F32, kind="ExternalOutput").ap()
        self.dbg = {}
        self.b = Builder(nc)

    def dbg_out(self, name, shape, dt=F32):
        t = self.nc.dram_tensor("dbg_" + name, list(shape), dt, kind="ExternalOutput").ap()
        self.dbg[name] = t
        return t

    def load_weight(self, dst, src, ncols, gvec=None, kch=8, stage=None, eng="act"):
        b = self.b
        for c in range(kch):
            st = stage[c % len(stage)]
            b.dma("sp", st[:, :ncols], src[c * 128:(c + 1) * 128, :], writes=[st])
            if gvec is not None:
                b.op(eng, lambda e: e.activation(out=dst[:, c, :], in_=st[:, :ncols], func=AF.Copy, scale=gvec[:, c:c + 1])
                     if eng == "act" else e.tensor_scalar_mul(out=dst[:, c, :], in0=st[:, :ncols], scalar1=gvec[:, c:c + 1]),
                     reads=[st, gvec], writes=[dst])
            else:
                b.op(eng, lambda e: e.copy(out=dst[:, c, :], in_=st[:, :ncols]) if eng == "act"
                     else e.tensor_copy(out=dst[:, c, :], in_=st[:, :ncols]), reads=[st], writes=[dst])

    def load_gain(self, name, src_vec, kch=8):
        b = self.b
        g = b.sb(name, [128, kch], F32)
        b.dma("sp", g[:], src_vec.rearrange("(c p) -> p c", p=128), writes=[g], allow_slow_non_contiguous=True)
        return g

    def bcast_row(self, name, src_row, n):
        b = self.b
        t = b.sb(name, [128, n], F32)
        b.dma("sp", t[:], src_row.partition_broadcast(128), writes=[t])
        return t

    def make_hT(self, x_ap, t, xt, junk, ss, hb, pt, hT, ident, hT_ap=None):
        b = self.b
        b.dma("sp", xt[:], x_ap[t * 128:(t + 1) * 128, :], writes=[xt])
        b.op("act", lambda e: e.activation(out=junk[:], in_=xt[:], func=AF.Square, accum_out=ss[:]), reads=[xt], writes=[junk, ss])
        b.op("act", lambda e: e.activation(out=ss[:], in_=ss[:], func=AF.Sqrt, scale=1.0 / D, bias=RMS_EPS), reads=[ss], writes=[ss])
        b.op("dve", lambda e: e.reciprocal(out=ss[:], in_=ss[:]), reads=[ss], writes=[ss])
        b.op("dve", lambda e: e.tensor_scalar_mul(out=hb[:], in0=xt[:], scalar1=ss[:]), reads=[xt, ss], writes=[hb])
        for c in range(8):
            b.op("pe", lambda e: e.transpose(out=pt[:, c, :], in_=hb[:, c * 128:(c + 1) * 128], identity=ident[:]),
                 reads=[hb, ident], writes=[pt])
        b.op("act", lambda e: e.copy(out=(hT[:] if hT_ap is None else hT_ap), in_=pt[:]), reads=[pt], writes=[hT])

    def alloc_root(self):
        b = self.b
        I = self.inp
        self.ident = b.sb("ident", [128, 128], BF16)
        b.dma("sp", self.ident[:], I["ident"], writes=[self.ident])
        self.identf = b.sb("identf", [128, 128], F32)
        b.dma("sp", self.identf[:], I["identf"], writes=[self.identf])

    def alloc_persistent(self):
        b = self.b
        I = self.inp
        if not hasattr(self, "ident"):
            self.alloc_root()
        self.ksE = b.sb("ksE", [128, 2, S], BF16)
        self.kwT = b.sb("kwT", [64, 2, S], BF16)
        self.vaug_s = b.sb("vaug_s", [128, NT, 2, 65], BF16)
        self.vaug_w = b.sb("vaug_w", [128, NT, 2, 65], BF16)
        self.gts = b.sb("gts", [128, NT, 24], F32)
        self.kcT = b.sb("kcT", [64, 2, 256], BF16)
        self.vcA = b.sb("vcA", [128, 2, 2, 129], F32)
        self.qT_d = b.dram("qT_d", [8, 64, S], BF16)
        self.oaT_d = b.dram("oaT_d", [4, 128, S], BF16)
        self.obT_d = b.dram("obT_d", [4, 128, S], BF16)
        for g in range(2):
            b.dma("sp", self.ksE[64:128, g, :], I["emat"], writes=[self.ksE])
        b.op("pool", lambda e: e.memset(self.vaug_s[:, :, :, 64:65], 1.0), writes=[self.vaug_s])
        b.op("pool", lambda e: e.memset(self.vaug_w[:, :, :, 64:65], 1.0), writes=[self.vaug_w])
        b.op("pool", lambda e: e.memset(self.vcA[:, :, :, 64:65], 1.0), writes=[self.vcA])
        for g in range(2):
            for ct in range(2):
                b.dma("sp", self.vcA[:, g, ct, 65:129], I["amat"][ct], writes=[self.vcA])

    def phase_nsa_proj(self):
        b = self.b
        I = self.inp
        with b.scope():
            gat = self.load_gain("gat", I["attn_norm_g"][0])
            wn = b.sb("wn", [128, 8, RW0], BF16)
            stage = [b.sb(f"wst{i}", [128, RW0], F32) for i in range(2)]
            self.load_weight(wn, I["w_in"][0][:, 0:RW0], RW0, gvec=gat, stage=stage)
            gq = self.bcast_row("gq", I["q_norm_g"][0], 64)
            gk = self.bcast_row("gk", I["k_norm_g"][0], 64)
            gq_rep = b.sb("gq_rep", [128, 8, 64], F32)
            gk_rep = b.sb("gk_rep", [128, 2, 64], F32)
            b.op("act", lambda e: e.activation(out=gq_rep[:], in_=gq[:, None, :].to_broadcast([128, 8, 64]), func=AF.Copy, scale=0.125),
                 reads=[gq], writes=[gq_rep])
            b.op("act", lambda e: e.activation(out=gk_rep[:], in_=gk[:, None, :].to_broadcast([128, 2, 64]), func=AF.Copy, scale=1.0),
                 reads=[gk], writes=[gk_rep])
            if getattr(self, 'stop_at', 99) <= 0:
                return
            kcdup = b.sb("kcdup", [128, 2, S + 1], BF16)
            vcdup = b.sb("vcdup", [128, 2, S + 1], BF16)
            xt = [b.sb(f"xt{i}", [128, D], F32) for i in range(2)]
            junk = b.sb("junk", [128, D], BF16)
            ss = [b.sb(f"ss{i}", [128, 1], F32) for i in range(2)]
            hb = [b.sb(f"hb{i}", [128, D], BF16) for i in range(2)]
            hT = [b.sb(f"hT{i}", [128, 8, 128], BF16) for i in range(2)]
            sq_ = [b.sb(f"sq{i}", [128, 12, 64], F32) for i in range(2)]
            ssq_ = [b.sb(f"ssq{i}", [128, 12], F32) for i in range(2)]
            tmpq_ = [b.sb(f"tmpq{i}", [128, 8, 64], F32) for i in range(2)]
            tmpk_ = [b.sb(f"tmpk{i}", [128, 4, 64], F32) for i in range(2)]
            qb_ = [b.sb(f"qb{i}", [128, 512], BF16) for i in range(2)]
            kb_ = [b.sb(f"kb{i}", [128, 4, 64], BF16) for i in range(2)]
            cb_ = [b.sb(f"cb{i}", [128, 4, 2, 64], BF16) for i in range(2)]
            qst = [b.sb(f"qst{i}", [64, 8, 128], BF16) for i in range(2)]
            pt = b.ps("pt", [128, 8, 128], BF16)
            pm = [b.ps(f"pm{i}", [128, 512], F32) for i in range(3)]
            ptq_ = [b.ps(f"ptq{i}", [128, 8, 128], BF16) for i in range(2)]
            ptk_ = [b.ps(f"ptk{i}", [128, 8, 128], BF16) for i in range(2)]
            colgroups = [(0, 512), (512, 1024), (1024, RW0)]
            pmS = [[b.sb(f"pmS{i}_{n}", [128, 512], F32) for n in range(3)] for i in range(2)]
            ntl = getattr(self, 'nt_limit', NT)

            def stageA(t):
                    i = t % 2
                    self.make_hT(I["x"], t, xt[i], junk, ss[i], hb[i], pt, hT[i], self.ident)
                    sq, ssq, tmpq, tmpk, qb, kb, cb, ptq, ptk = sq_[i], ssq_[i], tmpq_[i], tmpk_[i], qb_[i], kb_[i], cb_[i], ptq_[i], ptk_[i]
                    for n, (c0, c1) in enumerate(colgroups):
                        for c in range(8):
                            b.op("pe", lambda e: e.matmul(pm[n][:, :c1 - c0], lhsT=hT[i][:, c, :], rhs=wn[:, c, c0:c1],
                                                          start=(c == 0), stop=(c == 7)), reads=[hT[i], wn], writes=[pm[n]])

            def stageA2(t):
                    i = t % 2
                    b.op("act", lambda e: e.copy(out=pmS[i][0][:], in_=pm[0][:]), reads=[pm[0]], writes=[pmS[i][0]])
                    b.op("dve", lambda e: e.tensor_copy(out=pmS[i][1][:], in_=pm[1][:]), reads=[pm[1]], writes=[pmS[i][1]])
                    b.op("act", lambda e: e.copy(out=pmS[i][2][:, 0:RW0 - 1024], in_=pm[2][:, 0:RW0 - 1024]), reads=[pm[2]], writes=[pmS[i][2]])

            def stageB(t):
                    i = t % 2
                    sq, ssq, tmpq, tmpk, qb, kb, cb, ptq, ptk = sq_[i], ssq_[i], tmpq_[i], tmpk_[i], qb_[i], kb_[i], cb_[i], ptq_[i], ptk_[i]
                    b.op("act", lambda e: e.activation(out=sq[:, 0:8, :], in_=pmS[i][0][:, 0:512].rearrange("p (h d) -> p h d", d=64), func=AF.Square),
                         reads=[pmS[i][0]], writes=[sq])
                    b.op("act", lambda e: e.activation(out=sq[:, 8:10, :], in_=pmS[i][1][:, 256:384].rearrange("p (h d) -> p h d", d=64), func=AF.Square),
                         reads=[pmS[i][1]], writes=[sq])
                    b.op("act", lambda e: e.activation(out=sq[:, 10:12, :], in_=pmS[i][2][:, 0:128].rearrange("p (h d) -> p h d", d=64), func=AF.Square),
                         reads=[pmS[i][2]], writes=[sq])
                    b.op("dve", lambda e: e.tensor_reduce(out=ssq[:], in_=sq[:], axis=AX.X, op=ALU.add), reads=[sq], writes=[ssq])
                    b.op("act", lambda e: e.activation(out=ssq[:], in_=ssq[:], func=AF.Sqrt, scale=1.0 / 64, bias=RMS_EPS), reads=[ssq], writes=[ssq])
                    b.op("dve", lambda e: e.reciprocal(out=ssq[:], in_=ssq[:]), reads=[ssq], writes=[ssq])
                    if getattr(self, 'stop_at', 99) <= 2:
                        return
                    b.op("dve", lambda e: e.tensor_tensor(out=tmpq[:], in0=pmS[i][0][:, 0:512].rearrange("p (h d) -> p h d", d=64),
                                                          in1=ssq[:, 0:8].unsqueeze(2).to_broadcast([128, 8, 64]), op=ALU.mult),
                         reads=[pmS[i][0], ssq], writes=[tmpq])
                    b.op("pool", lambda e: e.tensor_tensor(out=qb[:].rearrange("p (h d) -> p h d", d=64), in0=tmpq[:], in1=gq_rep[:], op=ALU.mult),
                         reads=[tmpq, gq_rep], writes=[qb])
                    for h in range(8):
                        b.op("pe", lambda e: e.transpose(out=ptq[0:64, h, :], in_=qb[:, h * 64:(h + 1) * 64], identity=self.ident[:]),
                             reads=[qb, self.ident], writes=[ptq])
                    b.op("act", lambda e: e.copy(out=qst[i][:], in_=ptq[0:64, :, :]), reads=[ptq], writes=[qst[i]])
                    b.dma("pool", self.qT_d[:, :, t * 128:(t + 1) * 128].rearrange("h d t -> d h t"), qst[i][:], reads=[qst[i]], writes=[self.qT_d])
                    if getattr(self, 'stop_at', 99) <= 3:
                        return
                    b.op("dve", lambda e: e.tensor_tensor(out=tmpk[:, 0:2, :], in0=pmS[i][1][:, 256:384].rearrange("p (h d) -> p h d", d=64),
                                                          in1=ssq[:, 8:10].unsqueeze(2).to_broadcast([128, 2, 64]), op=ALU.mult),
                         reads=[pmS[i][1], ssq], writes=[tmpk])
                    b.op("dve", lambda e: e.tensor_tensor(out=tmpk[:, 2:4, :], in0=pmS[i][2][:, 0:128].rearrange("p (h d) -> p h d", d=64),
                                                          in1=ssq[:, 10:12].unsqueeze(2).to_broadcast([128, 2, 64]), op=ALU.mult),
                         reads=[pmS[i][2], ssq], writes=[tmpk])
                    b.op("pool", lambda e: e.tensor_tensor(out=kb[:].rearrange("p (a g) d -> p a g d", a=2), in0=tmpk[:].rearrange("p (a g) d -> p a g d", a=2),
                                                           in1=gk_rep[:, None, :, :].to_broadcast([128, 2, 2, 64]), op=ALU.mult),
                         reads=[tmpk, gk_rep], writes=[kb])
                    for j in range(4):
                        b.op("pe", lambda e: e.transpose(out=ptk[0:64, j, :], in_=kb[:, j, :], identity=self.ident[:]),
                             reads=[kb, self.ident], writes=[ptk])
                    if getattr(self, 'stop_at', 99) <= 4:
                        return
                    for du in range(2):
                        b.op("act", lambda e: e.copy(out=cb[:, :, du, :], in_=pmS[i][1][:, 0:256].rearrange("p (a d) -> p a d", d=64)),
                             reads=[pmS[i][1]], writes=[cb])
                    for j in range(4):
                        b.op("pe", lambda e: e.transpose(out=ptk[:, 4 + j, :], in_=cb[:, j, :, :].rearrange("p a d -> p (a d)"), identity=self.ident[:]),
                             reads=[cb, self.ident], writes=[ptk])
                    c0 = t * 128
                    b.op("dve", lambda e: e.tensor_copy(out=self.ksE[0:64, :, c0:c0 + 128], in_=ptk[0:64, 0:2, :]), reads=[ptk], writes=[self.ksE])
                    b.op("dve", lambda e: e.tensor_copy(out=self.kwT[0:64, :, c0:c0 + 128], in_=ptk[0:64, 2:4, :]), reads=[ptk], writes=[self.kwT])
                    b.op("act", lambda e: e.copy(out=kcdup[0:64, :, 1 + c0:1 + c0 + 128], in_=ptk[0:64, 4:6, :]), reads=[ptk], writes=[kcdup])
                    b.op("act", lambda e: e.copy(out=kcdup[64:128, :, c0:c0 + 128], in_=ptk[64:128, 4:6, :]), reads=[ptk], writes=[kcdup])
                    b.op("dve", lambda e: e.tensor_copy(out=vcdup[0:64, :, 1 + c0:1 + c0 + 128], in_=ptk[0:64, 6:8, :]), reads=[ptk], writes=[vcdup])
                    b.op("dve", lambda e: e.tensor_copy(out=vcdup[64:128, :, c0:c0 + 128], in_=ptk[64:128, 6:8, :]), reads=[ptk], writes=[vcdup])
                    if getattr(self, 'stop_at', 99) <= 5:
                        return
                    b.op("act", lambda e: e.copy(out=self.vaug_s[:, t, :, 0:64], in_=pmS[i][1][:, 384:512].rearrange("p (g d) -> p g d", d=64)),
                         reads=[pmS[i][1]], writes=[self.vaug_s])
                    b.op("act", lambda e: e.copy(out=self.vaug_w[:, t, :, 0:64], in_=pmS[i][2][:, 128:256].rearrange("p (g d) -> p g d", d=64)),
                         reads=[pmS[i][2]], writes=[self.vaug_w])
                    b.op("act", lambda e: e.activation(out=self.gts[:, t, :], in_=pmS[i][2][:, 256:280], func=AF.Sigmoid), reads=[pmS[i][2]], writes=[self.gts])

            stageA(0)
            stageA2(0)
            for t in range(ntl):
                if t + 1 < ntl:
                    stageA(t + 1)
                stageB(t)
                if t + 1 < ntl:
                    stageA2(t + 1)
            if "nsa_proj" in self.debug:
                d = self.dbg_out("ksE", [128, 2, S], BF16)
                b.dma("pool", d, self.ksE[:], reads=[self.ksE])
                d = self.dbg_out("kcdup", [128, 2, S + 1], BF16)
                b.dma("pool", d, kcdup[:], reads=[kcdup])
                d = self.dbg_out("vaug_w", [128, NT, 2, 65], BF16)
                b.dma("pool", d, self.vaug_w[:], reads=[self.vaug_w])
                d = self.dbg_out("gts", [128, NT, 24], F32)
                b.dma("pool", d, self.gts[:], reads=[self.gts])
            if not getattr(self, 'skip_compress', False):
                self.compress(kcdup, vcdup, gk_rep, [pm[0], pm[1]], pm[2], ptk_[0])

    def compress(self, kcdup, vcdup, gk_rep, ph, po, ptc):
        b = self.b
        I = self.inp
        C2 = 2.0 * 0.7978845608028654
        w1 = b.sb("w1", [128, 16, 256], BF16)
        w2 = b.sb("w2", [128, 2, 64], BF16)
        w1st = [b.sb(f"w1st{i}", [128, 256], F32) for i in range(2)]
        peT = b.sb("peT", [128, 16], F32)
        peTb = b.sb("peTb", [128, 16], BF16)
        hTc = b.sb("hTc", [128, 2, 256], BF16)
        pbias = b.sb("pbias", [128, 2], F32)
        xh = b.sb("xh", [128, 255], F32)
        x2 = b.sb("x2", [128, 255], F32)
        sg = b.sb("sg", [128, 255], F32)
        ctmp = b.sb("ctmp", [128, 64], F32)
        csq = b.sb("csq", [128, 64], F32)
        cs1 = b.sb("cs1", [128, 1], F32)
        kcb = b.sb("kcb", [128, 64], BF16)
        b.op("pool", lambda e: e.memset(hTc[:], 0.0), writes=[hTc])
        for kv, (dup, pe_n, w1_n, w2_n) in enumerate([(kcdup, "cmp_pe_k", "cmp_w1_k", "cmp_w2_k"), (vcdup, "cmp_pe_v", "cmp_w1_v", "cmp_w2_v")]):
            self.load_weight(w1, I[w1_n][0], 256, kch=16, stage=w1st, eng="dve")
            self.load_weight(w2, I[w2_n][0], 64, kch=2, stage=w1st, eng="dve")
            for two in range(2):
                b.dma("sp", peT[two * 64:(two + 1) * 64, :], I[pe_n][0].rearrange("(pp two) d -> two d pp", two=2)[two],
                      writes=[peT], allow_slow_non_contiguous=True)
            b.op("dve", lambda e: e.tensor_copy(out=peTb[:], in_=peT[:]), reads=[peT], writes=[peTb])
            for ft in range(2):
                for pp in range(16):
                    b.op("pe", lambda e: e.matmul(po[:, ft:ft + 1], lhsT=w1[:, pp, ft * 128:(ft + 1) * 128], rhs=peTb[:, pp:pp + 1],
                                                  start=(pp == 0), stop=(pp == 15)), reads=[w1, peTb], writes=[po])
            b.op("dve", lambda e: e.tensor_copy(out=pbias[:], in_=po[:, 0:2]), reads=[po], writes=[pbias])
            for g in range(2):
                for ft in range(2):
                    p = ph[ft]
                    for pp in range(16):
                        b.op("pe", lambda e: e.matmul(p[:, 0:255], lhsT=w1[:, pp, ft * 128:(ft + 1) * 128],
                                                      rhs=dup[:, g, 1 + 2 * pp:1 + 2 * pp + 16 * 254 + 1:16],
                                                      start=(pp == 0), stop=(pp == 15)), reads=[w1, dup], writes=[p])
                    b.op("act", lambda e: e.activation(out=xh[:], in_=p[:, 0:255], func=AF.Identity, bias=pbias[:, ft:ft + 1]), reads=[p, pbias], writes=[xh])
                    b.op("dve", lambda e: e.tensor_tensor(out=x2[:], in0=xh[:], in1=xh[:], op=ALU.mult), reads=[xh], writes=[x2])
                    b.op("dve", lambda e: e.tensor_scalar(out=x2[:], in0=x2[:], scalar1=0.044715, scalar2=1.0, op0=ALU.mult, op1=ALU.add), reads=[x2], writes=[x2])
                    b.op("dve", lambda e: e.tensor_tensor(out=x2[:], in0=x2[:], in1=xh[:], op=ALU.mult), reads=[x2, xh], writes=[x2])
                    b.op("act", lambda e: e.activation(out=sg[:], in_=x2[:], func=AF.Sigmoid, scale=C2), reads=[x2], writes=[sg])
                    b.op("dve", lambda e: e.tensor_tensor(out=hTc[:, ft, 0:255], in0=xh[:], in1=sg[:], op=ALU.mult), reads=[xh, sg], writes=[hTc])
                for ct in range(2):
                    for ft in range(2):
                        b.op("pe", lambda e: e.matmul(po[:, 64:128], lhsT=hTc[:, ft, ct * 128:(ct + 1) * 128], rhs=w2[:, ft, :],
                                                      start=(ft == 0), stop=(ft == 1)), reads=[hTc, w2], writes=[po])
                    if kv == 0:
                        b.op("act", lambda e: e.activation(out=csq[:], in_=po[:, 64:128], func=AF.Square, accum_out=cs1[:]), reads=[po], writes=[csq, cs1])
                        b.op("act", lambda e: e.activation(out=cs1[:], in_=cs1[:], func=AF.Sqrt, scale=1.0 / 64, bias=RMS_EPS), reads=[cs1], writes=[cs1])
                        b.op("dve", lambda e: e.reciprocal(out=cs1[:], in_=cs1[:]), reads=[cs1], writes=[cs1])
                        b.op("dve", lambda e: e.tensor_scalar_mul(out=ctmp[:], in0=po[:, 64:128], scalar1=cs1[:]), reads=[po, cs1], writes=[ctmp])
                        b.op("dve", lambda e: e.tensor_tensor(out=kcb[:], in0=ctmp[:], in1=gk_rep[:, 0, :], op=ALU.mult), reads=[ctmp, gk_rep], writes=[kcb])
                        b.op("pe", lambda e: e.transpose(out=ptc[0:64, 0, :], in_=kcb[:], identity=self.ident[:]), reads=[kcb, self.ident], writes=[ptc])
                        b.op("dve", lambda e: e.tensor_copy(out=self.kcT[:, g, ct * 128:(ct + 1) * 128], in_=ptc[0:64, 0, :]), reads=[ptc], writes=[self.kcT])
                    else:
                        b.op("dve", lambda e: e.tensor_copy(out=self.vcA[:, g, ct, 0:64], in_=po[:, 64:128]), reads=[po], writes=[self.vcA])
        if "compress" in self.debug:
            d = self.dbg_out("kcT", [64, 2, 256], BF16)
            b.dma("pool", d, self.kcT[:], reads=[self.kcT])
            d = self.dbg_out("vcA", [128, 2, 2, 129], F32)
            b.dma("pool", d, self.vcA[:], reads=[self.vcA])

    def finish(self):
        b = self.b
        b.wait_all_on("pool")
        b.barrier()
        b.close()
        return self.nc


def _phase_attn(self):
    b = self.b
    I = self.inp
    with b.scope():
        tw = b.sb("tw", [128, 8, 640], F32)
        ts = b.sb("ts", [128, 8, 640], F32)
        b.dma("sp", tw[:], I["tw"], writes=[tw])
        b.dma("sp", ts[:], I["ts"], writes=[ts])
        candneg = b.sb("candneg", [128, 32, 64], F32)
        fz = b.sb("fz", [128, 32, 64], F32)
        b.dma("sp", candneg[:], I["candneg"], writes=[candneg])
        b.dma("sp", fz[:], I["fz"], writes=[fz])
        b31 = b.sb("b31", [128, 8], F32)
        b.dma("sp", b31[:], I["b31"], writes=[b31])
        kwp = b.sb("kwp", [128, 2, S], BF16)
        b.op("pool", lambda e: e.memset(kwp[64:128, :, :], 0.0), writes=[kwp])
        b.op("pool", lambda e: e.tensor_copy(out=kwp[0:64, :, :], in_=self.kwT[:]), reads=[self.kwT], writes=[kwp])
        kcp = b.sb("kcp", [128, 2, 256], BF16)
        b.op("pool", lambda e: e.memset(kcp[64:128, :, :], 0.0), writes=[kcp])
        b.op("pool", lambda e: e.tensor_copy(out=kcp[0:64, :, :], in_=self.kcT[:]), reads=[self.kcT], writes=[kcp])
        zer = b.sb("zer", [128, 512], BF16)
        b.op("pool", lambda e: e.memset(zer[:], 0.0), writes=[zer])
        qm = [b.sb(f"qm{i}", [128, 8, 512], BF16) for i in range(2)]
        bct = [b.sb(f"bct{i}", [128, 512], F32) for i in range(3)]
        scf = [b.sb(f"scf{i}", [128, 640], F32) for i in range(2)]
        pcT = [b.sb(f"pcT{i}", [128, 2, 512], F32) for i in range(2)]
        pT = [b.sb(f"pT{i}", [128, 640], BF16) for i in range(3)]
        oacc = b.sb("oacc", [128, 4, 512], F32)
        imp = b.sb("imp", [128, 4, 2, 64], F32)
        impm = b.sb("impm", [128, 64], F32)
        impm2 = b.sb("impm2", [128, 64], F32)
        m8a = b.sb("m8a", [128, 8], F32)
        m8b = b.sb("m8b", [128, 8], F32)
        msk = b.sb("msk", [128, 64], F32)
        mb = b.sb("mb", [128, 128], BF16)
        b.op("pool", lambda e: e.memset(mb[:], 0.0), writes=[mb])
        rs = b.sb("rs", [128, 4], F32)
        rg = b.sb("rg", [128, 4], F32)
        oab = b.sb("oab", [128, 512], BF16)
        oaT = [b.sb(f"oaT{i}", [128, 4, 128], BF16) for i in range(2)]
        pS = [b.ps(f"pS{i}", [128, 512], F32) for i in range(2)]
        pS2 = b.ps("pS2", [128, 512], F32)
        pO = [b.ps(f"pO{i}", [128, 512], F32) for i in range(3)]
        pTr = b.ps("pTr", [128, 8, 128], BF16)
        nrot = {"bct": 0, "scf": 0, "pT": 0, "pS": 0}

        def rot(name, lst):
            nrot[name] += 1
            return lst[nrot[name] % len(lst)]

        def finalize(po, ncol_off, h, qs, branch, first):
            qt = qs_base + qs
            o0 = ncol_off
            b.op("dve", lambda e: e.tensor_scalar_max(out=rs[:, 0:1], in0=po[:, o0 + 64:o0 + 65], scalar1=1e-30), reads=[po], writes=[rs])
            b.op("dve", lambda e: e.reciprocal(out=rs[:, 1:2], in_=rs[:, 0:1]), reads=[rs], writes=[rs])
            b.op("dve", lambda e: e.tensor_tensor(out=rg[:, 0:1], in0=rs[:, 1:2], in1=self.gts[:, qt, h * 3 + branch:h * 3 + branch + 1], op=ALU.mult),
                 reads=[rs, self.gts], writes=[rg])
            if first:
                b.op("dve", lambda e: e.tensor_scalar_mul(out=oacc[:, qs, h * 64:(h + 1) * 64], in0=po[:, o0:o0 + 64], scalar1=rg[:, 0:1]),
                     reads=[po, rg], writes=[oacc])
            else:
                b.op("dve", lambda e: e.scalar_tensor_tensor(out=oacc[:, qs, h * 64:(h + 1) * 64], in0=po[:, o0:o0 + 64], scalar=rg[:, 0:1],
                                                             in1=oacc[:, qs, h * 64:(h + 1) * 64], op0=ALU.mult, op1=ALU.add),
                     reads=[po, rg, oacc], writes=[oacc])

        nqg = getattr(self, "nqg_limit", 8)
        for qg in range(nqg):
            qs_base = 4 * qg
            q0 = 512 * qg
            Q = qm[qg % 2]
            b.dma("sp", Q[0:64, :, :], self.qT_d[:, :, q0:q0 + 512].rearrange("h d t -> d h t"), reads=[self.qT_d], writes=[Q])
            if qg < 2:
                b.op("pool", lambda e: e.memset(Q[64:128, :, :], 0.0), writes=[Q])
            for h in range(8):
                g = h // 4
                pc = pcT[h % 2]
                for ct in range(2):
                    p = rot("pS", pS)
                    b.op("pe", lambda e: e.matmul(p[:, :], lhsT=kcp[:, g, ct * 128:(ct + 1) * 128], rhs=Q[:, h, :], start=True, stop=True),
                         reads=[kcp, Q], writes=[p])
                    bt = rot("bct", bct)
                    b.dma("sp", bt[:], I["biasc"][h, ct, :, q0:q0 + 512], writes=[bt])
                    sc = rot("scf", scf)
                    b.op("dve", lambda e: e.tensor_tensor(out=sc[:, 0:512], in0=p[:, :], in1=bt[:], op=ALU.add), reads=[p, bt], writes=[sc])
                    b.op("act", lambda e: e.activation(out=pc[:, ct, :], in_=sc[:, 0:512], func=AF.Exp), reads=[sc], writes=[pc])
                po = pO[0]
                for qs in range(4):
                    for ct in range(2):
                        b.op("pe", lambda e: e.matmul(po[:, qs * 128:qs * 128 + 129] if False else po[:, 0:129], lhsT=pc[:, ct, qs * 128:(qs + 1) * 128],
                                                      rhs=self.vcA[:, g, ct, :], start=(ct == 0), stop=(ct == 1)), reads=[pc, self.vcA], writes=[po])
                    finalize(po, 0, h, qs, 0, True)
                    if h % 4 == 0:
                        b.op("dve", lambda e: e.tensor_scalar_mul(out=imp[:, qs, g, :], in0=po[:, 65:129], scalar1=rs[:, 1:2]), reads=[po, rs], writes=[imp])
                    else:
                        b.op("dve", lambda e: e.scalar_tensor_tensor(out=imp[:, qs, g, :], in0=po[:, 65:129], scalar=rs[:, 1:2], in1=imp[:, qs, g, :],
                                                                     op0=ALU.mult, op1=ALU.add), reads=[po, rs, imp], writes=[imp])
            if qg >= 2:
                for qs in range(4):
                    qt = qs_base + qs
                    for g in range(2):
                        b.op("dve", lambda e: e.tensor_tensor(out=impm[:], in0=imp[:, qs, g, :], in1=candneg[:, qt, :], op=ALU.add), reads=[imp, candneg], writes=[impm])
                        b.op("dve", lambda e: e.max(out=m8a[:], in_=impm[:]), reads=[impm], writes=[m8a])
                        b.op("dve", lambda e: e.match_replace(out=impm2[:], in_to_replace=m8a[:], in_values=impm[:], imm_value=-1e9), reads=[m8a, impm], writes=[impm2])
                        b.op("dve", lambda e: e.max(out=m8b[:], in_=impm2[:]), reads=[impm2], writes=[m8b])
                        b.op("dve", lambda e: e.tensor_scalar(out=msk[:], in0=impm[:], scalar1=m8b[:, 4:5], scalar2=None, op0=ALU.is_ge), reads=[impm, m8b], writes=[msk])
                        b.op("dve", lambda e: e.tensor_tensor(out=msk[:], in0=msk[:], in1=fz[:, qt, :], op=ALU.max), reads=[msk, fz], writes=[msk])
                        b.op("dve", lambda e: e.tensor_scalar(out=mb[:, 64:128], in0=msk[:], scalar1=-NEG, scalar2=NEG, op0=ALU.mult, op1=ALU.add), reads=[msk], writes=[mb])
                        b.op("pe", lambda e: e.transpose(out=pTr[:, 0, :], in_=mb[:], identity=self.ident[:]), reads=[mb, self.ident], writes=[pTr])
                        b.op("act", lambda e: e.copy(out=Q[64:128, 4 * g:4 * g + 4, qs * 128:(qs + 1) * 128],
                                                     in_=pTr[64:128, 0:1, :].to_broadcast([64, 4, 128])), reads=[pTr], writes=[Q])
            for h in range(8):
                g = h // 4
                po_s, po_w = pO[1], pO[2]
                for po in (po_s, po_w):
                    b.op("pe", lambda e: e.matmul(po[:, 0:260], lhsT=zer[:, 0:128], rhs=zer[:, 0:260], start=True, stop=True), reads=[zer], writes=[po])
                nkt = 4 * (qg + 1)
                for kt in range(nkt):
                    dlt = 4 * qg - kt
                    qstart = 0 if dlt >= 0 else -dlt * 128
                    N = 512 - qstart
                    p = rot("pS", pS)
                    b.op("pe", lambda e: e.matmul(p[:, 0:N], lhsT=self.ksE[:, g, kt * 128:(kt + 1) * 128], rhs=Q[:, h, qstart:512], start=True, stop=True),
                         reads=[self.ksE, Q], writes=[p])
                    pt_ = rot("pT", pT)
                    if dlt <= 1:
                        c0 = 128 if dlt == 1 else 0
                        sc = rot("scf", scf)
                        b.op("dve", lambda e: e.tensor_tensor(out=sc[:, 0:N], in0=p[:, 0:N], in1=ts[:, h, c0:c0 + N], op=ALU.add), reads=[p, ts], writes=[sc])
                        b.op("act", lambda e: e.activation(out=pt_[:, 0:N], in_=sc[:, 0:N], func=AF.Exp), reads=[sc], writes=[pt_])
                    else:
                        b.op("act", lambda e: e.activation(out=pt_[:, 0:N], in_=p[:, 0:N], func=AF.Exp, bias=b31[:, h:h + 1]), reads=[p, b31], writes=[pt_])
                    for qs in range(qstart // 128, 4):
                        o = qs * 128 - qstart
                        b.op("pe", lambda e: e.matmul(po_s[:, qs * 65:(qs + 1) * 65], lhsT=pt_[:, o:o + 128], rhs=self.vaug_s[:, kt, g, :],
                                                      start=False, stop=(kt == nkt - 1), skip_group_check=True), reads=[pt_, self.vaug_s], writes=[po_s])
                kts = [kt for kt in range(4 * qg - 4, 4 * qg + 4) if kt >= 0]
                for kt in kts:
                    qs_lo = max(0, kt - 4 * qg)
                    qs_hi = min(3, kt + 4 - 4 * qg)
                    N = (qs_hi - qs_lo + 1) * 128
                    c0 = 128 * (4 * qg + qs_lo - kt)
                    p = rot("pS", pS)
                    b.op("pe", lambda e: e.matmul(p[:, 0:N], lhsT=kwp[:, g, kt * 128:(kt + 1) * 128], rhs=Q[:, h, qs_lo * 128:(qs_hi + 1) * 128], start=True, stop=True),
                         reads=[kwp, Q], writes=[p])
                    sc = rot("scf", scf)
                    b.op("dve", lambda e: e.tensor_tensor(out=sc[:, 0:N], in0=p[:, 0:N], in1=tw[:, h, c0:c0 + N], op=ALU.add), reads=[p, tw], writes=[sc])
                    pt_ = rot("pT", pT)
                    b.op("act", lambda e: e.activation(out=pt_[:, 0:N], in_=sc[:, 0:N], func=AF.Exp), reads=[sc], writes=[pt_])
                    for qs in range(qs_lo, qs_hi + 1):
                        o = (qs - qs_lo) * 128
                        b.op("pe", lambda e: e.matmul(po_w[:, qs * 65:(qs + 1) * 65], lhsT=pt_[:, o:o + 128], rhs=self.vaug_w[:, kt, g, :],
                                                      start=False, stop=(kt == kts[-1]), skip_group_check=True), reads=[pt_, self.vaug_w], writes=[po_w])
                for qs in range(4):
                    finalize(po_s, qs * 65, h, qs, 1, False)
                    finalize(po_w, qs * 65, h, qs, 2, False)
            for qs in range(4):
                qt = qs_base + qs
                ot = oaT[qs % 2]
                b.op("act", lambda e: e.copy(out=oab[:], in_=oacc[:, qs, :]), reads=[oacc], writes=[oab])
                for c in range(4):
                    b.op("pe", lambda e: e.transpose(out=pTr[:, 4 + c, :], in_=oab[:, c * 128:(c + 1) * 128], identity=self.ident[:]), reads=[oab, self.ident], writes=[pTr])
                b.op("act", lambda e: e.copy(out=ot[:], in_=pTr[:, 4:8, :]), reads=[pTr], writes=[ot])
                b.dma("pool", self.oaT_d[:, :, qt * 128:(qt + 1) * 128].rearrange("c p t -> p c t"), ot[:], reads=[ot], writes=[self.oaT_d])
        if "attn" in self.debug:
            d = self.dbg_out("oaT", [4, 128, S], BF16)
            b.dma("pool", d, self.oaT_d[:], reads=[self.oaT_d])


Prog.phase_attn = _phase_attn


def _phase_attn2(self):
    b = self.b
    I = self.inp
    with b.scope():
        tw = b.sb("tw", [128, 8, 640], F32)
        ts = b.sb("ts", [128, 8, 640], F32)
        b.dma("sp", tw[:], I["tw"], writes=[tw])
        b.dma("sp", ts[:], I["ts"], writes=[ts])
        candneg = b.sb("candneg", [128, 32, 64], F32)
        fz = b.sb("fz", [128, 32, 64], F32)
        b.dma("sp", candneg[:], I["candneg"], writes=[candneg])
        b.dma("sp", fz[:], I["fz"], writes=[fz])
        b31 = b.sb("b31", [128, 8], F32)
        b.dma("sp", b31[:], I["b31"], writes=[b31])
        kwp = b.sb("kwp", [128, 2, S], BF16)
        b.op("pool", lambda e: e.memset(kwp[64:128, :, :], 0.0), writes=[kwp])
        b.op("pool", lambda e: e.tensor_copy(out=kwp[0:64, :, :], in_=self.kwT[:]), reads=[self.kwT], writes=[kwp])
        kcp = b.sb("kcp", [128, 2, 256], BF16)
        b.op("pool", lambda e: e.memset(kcp[64:128, :, :], 0.0), writes=[kcp])
        b.op("pool", lambda e: e.tensor_copy(out=kcp[0:64, :, :], in_=self.kcT[:]), reads=[self.kcT], writes=[kcp])
        zer = b.sb("zer", [128, 512], BF16)
        b.op("pool", lambda e: e.memset(zer[:], 0.0), writes=[zer])
        qm = [b.sb(f"qm{i}", [128, 8, 512], BF16) for i in range(2)]
        bct = [b.sb(f"bct{i}", [128, 512], F32) for i in range(3)]
        scf = [b.sb(f"scf{i}", [128, 640], F32) for i in range(3)]
        pcT = [b.sb(f"pcT{i}", [128, 2, 512], F32) for i in range(2)]
        pT = [b.sb(f"pT{i}", [128, 640], BF16) for i in range(4)]
        oacc = b.sb("oacc", [128, 4, 512], F32)
        imp = b.sb("imp", [128, 4, 2, 64], F32)
        impm = b.sb("impm", [128, 64], F32)
        impm2 = b.sb("impm2", [128, 64], F32)
        m8a = b.sb("m8a", [128, 8], F32)
        m8b = b.sb("m8b", [128, 8], F32)
        msk = b.sb("msk", [128, 64], F32)
        mb = b.sb("mb", [128, 128], BF16)
        b.op("pool", lambda e: e.memset(mb[:], 0.0), writes=[mb])
        rs = b.sb("rs", [128, 4], F32)
        rg = b.sb("rg", [128, 4], F32)
        oab = b.sb("oab", [128, 512], BF16)
        oaT = [b.sb(f"oaT{i}", [128, 4, 128], BF16) for i in range(2)]
        pS = [b.ps(f"pS{i}", [128, 512], F32) for i in range(3)]
        pOs = [b.ps(f"pOs{i}", [128, 512], F32) for i in range(2)]
        pOw = [b.ps(f"pOw{i}", [128, 512], F32) for i in range(2)]
        pTr = b.ps("pTr", [128, 8, 128], BF16)
        nrot = {"bct": 0, "scf": 0, "pT": 0, "pS": 0}

        def rot(name, lst):
            nrot[name] += 1
            return lst[nrot[name] % len(lst)]

        def finalize(po, ncol_off, h, qs, branch, first):
            qt = qs_base + qs
            o0 = ncol_off
            b.op("dve", lambda e: e.tensor_scalar_max(out=rs[:, 0:1], in0=po[:, o0 + 64:o0 + 65], scalar1=1e-30), reads=[po], writes=[rs])
            b.op("dve", lambda e: e.reciprocal(out=rs[:, 1:2], in_=rs[:, 0:1]), reads=[rs], writes=[rs])
            b.op("dve", lambda e: e.tensor_tensor(out=rg[:, 0:1], in0=rs[:, 1:2], in1=self.gts[:, qt, h * 3 + branch:h * 3 + branch + 1], op=ALU.mult),
                 reads=[rs, self.gts], writes=[rg])
            if first:
                b.op("dve", lambda e: e.tensor_scalar_mul(out=oacc[:, qs, h * 64:(h + 1) * 64], in0=po[:, o0:o0 + 64], scalar1=rg[:, 0:1]),
                     reads=[po, rg], writes=[oacc])
            else:
                b.op("dve", lambda e: e.scalar_tensor_tensor(out=oacc[:, qs, h * 64:(h + 1) * 64], in0=po[:, o0:o0 + 64], scalar=rg[:, 0:1],
                                                             in1=oacc[:, qs, h * 64:(h + 1) * 64], op0=ALU.mult, op1=ALU.add),
                     reads=[po, rg, oacc], writes=[oacc])

        nqg = getattr(self, "nqg_limit", 8)
        for qg in range(nqg):
            qs_base = 4 * qg
            q0 = 512 * qg
            Q = qm[qg % 2]
            b.dma("sp", Q[0:64, :, :], self.qT_d[:, :, q0:q0 + 512].rearrange("h d t -> d h t"), reads=[self.qT_d], writes=[Q])
            if qg < 2:
                b.op("pool", lambda e: e.memset(Q[64:128, :, :], 0.0), writes=[Q])
            for h in range(8):
                g = h // 4
                pc = pcT[h % 2]
                for ct in range(2):
                    p = rot("pS", pS)
                    b.op("pe", lambda e: e.matmul(p[:, :], lhsT=kcp[:, g, ct * 128:(ct + 1) * 128], rhs=Q[:, h, :], start=True, stop=True),
                         reads=[kcp, Q], writes=[p])
                    bt = rot("bct", bct)
                    b.dma("sp", bt[:], I["biasc"][h, ct, :, q0:q0 + 512], writes=[bt])
                    sc = rot("scf", scf)
                    b.op("dve", lambda e: e.tensor_tensor(out=sc[:, 0:512], in0=p[:, :], in1=bt[:], op=ALU.add), reads=[p, bt], writes=[sc])
                    b.op("act", lambda e: e.activation(out=pc[:, ct, :], in_=sc[:, 0:512], func=AF.Exp), reads=[sc], writes=[pc])
                po = pOs[h % 2]
                for qs in range(4):
                    for ct in range(2):
                        b.op("pe", lambda e: e.matmul(po[:, qs * 128:qs * 128 + 129] if False else po[:, 0:129], lhsT=pc[:, ct, qs * 128:(qs + 1) * 128],
                                                      rhs=self.vcA[:, g, ct, :], start=(ct == 0), stop=(ct == 1)), reads=[pc, self.vcA], writes=[po])
                    finalize(po, 0, h, qs, 0, True)
                    if h % 4 == 0:
                        b.op("dve", lambda e: e.tensor_scalar_mul(out=imp[:, qs, g, :], in0=po[:, 65:129], scalar1=rs[:, 1:2]), reads=[po, rs], writes=[imp])
                    else:
                        b.op("dve", lambda e: e.scalar_tensor_tensor(out=imp[:, qs, g, :], in0=po[:, 65:129], scalar=rs[:, 1:2], in1=imp[:, qs, g, :],
                                                                     op0=ALU.mult, op1=ALU.add), reads=[po, rs, imp], writes=[imp])
            if qg >= 2:
                for qs in range(4):
                    qt = qs_base + qs
                    for g in range(2):
                        b.op("dve", lambda e: e.tensor_tensor(out=impm[:], in0=imp[:, qs, g, :], in1=candneg[:, qt, :], op=ALU.add), reads=[imp, candneg], writes=[impm])
                        b.op("dve", lambda e: e.max(out=m8a[:], in_=impm[:]), reads=[impm], writes=[m8a])
                        b.op("dve", lambda e: e.match_replace(out=impm2[:], in_to_replace=m8a[:], in_values=impm[:], imm_value=-1e9), reads=[m8a, impm], writes=[impm2])
                        b.op("dve", lambda e: e.max(out=m8b[:], in_=impm2[:]), reads=[impm2], writes=[m8b])
                        b.op("dve", lambda e: e.tensor_scalar(out=msk[:], in0=impm[:], scalar1=m8b[:, 4:5], scalar2=None, op0=ALU.is_ge), reads=[impm, m8b], writes=[msk])
                        b.op("dve", lambda e: e.tensor_tensor(out=msk[:], in0=msk[:], in1=fz[:, qt, :], op=ALU.max), reads=[msk, fz], writes=[msk])
                        b.op("dve", lambda e: e.tensor_scalar(out=mb[:, 64:128], in0=msk[:], scalar1=-NEG, scalar2=NEG, op0=ALU.mult, op1=ALU.add), reads=[msk], writes=[mb])
                        b.op("pe", lambda e: e.transpose(out=pTr[:, 0, :], in_=mb[:], identity=self.ident[:]), reads=[mb, self.ident], writes=[pTr])
                        b.op("act", lambda e: e.copy(out=Q[64:128, 4 * g:4 * g + 4, qs * 128:(qs + 1) * 128],
                                                     in_=pTr[64:128, 0:1, :].to_broadcast([64, 4, 128])), reads=[pTr], writes=[Q])
            jobs = []
            for h in range(8):
                g = h // 4
                nkt = 4 * (qg + 1)
                for kt in range(nkt):
                    dlt = 4 * qg - kt
                    qstart = 0 if dlt >= 0 else -dlt * 128
                    jobs.append(dict(kind="s", h=h, g=g, kt=kt, qlo=qstart // 128, qhi=3, first=(kt == 0), last=False, lastkt=(kt == nkt - 1),
                                     tab=(ts, (128 if dlt == 1 else 0)) if dlt <= 1 else None))
                kts = [kt for kt in range(4 * qg - 4, 4 * qg + 4) if kt >= 0]
                for kt in kts:
                    qs_lo = max(0, kt - 4 * qg)
                    qs_hi = min(3, kt + 4 - 4 * qg)
                    jobs.append(dict(kind="w", h=h, g=g, kt=kt, qlo=qs_lo, qhi=qs_hi, first=False, last=(kt == kts[-1]), lastkt=(kt == kts[-1]),
                                     tab=(tw, 128 * (4 * qg + qs_lo - kt))))

            def emitS(j):
                h, g, kt = j["h"], j["g"], j["kt"]
                N = (j["qhi"] - j["qlo"] + 1) * 128
                p = rot("pS", pS)
                kmat = self.ksE if j["kind"] == "s" else kwp
                b.op("pe", lambda e: e.matmul(p[:, 0:N], lhsT=kmat[:, g, kt * 128:(kt + 1) * 128], rhs=Q[:, h, j["qlo"] * 128:(j["qhi"] + 1) * 128], start=True, stop=True),
                     reads=[kmat, Q], writes=[p])
                j["p"] = p
                j["N"] = N

            def emitE(j):
                h = j["h"]
                p, N = j["p"], j["N"]
                pt_ = rot("pT", pT)
                if j["tab"] is not None:
                    tab, c0 = j["tab"]
                    sc = rot("scf", scf)
                    b.op("dve", lambda e: e.tensor_tensor(out=sc[:, 0:N], in0=p[:, 0:N], in1=tab[:, h, c0:c0 + N], op=ALU.add), reads=[p, tab], writes=[sc])
                    b.op("act", lambda e: e.activation(out=pt_[:, 0:N], in_=sc[:, 0:N], func=AF.Exp), reads=[sc], writes=[pt_])
                else:
                    b.op("act", lambda e: e.activation(out=pt_[:, 0:N], in_=p[:, 0:N], func=AF.Exp, bias=b31[:, h:h + 1]), reads=[p, b31], writes=[pt_])
                j["pt"] = pt_

            def emitPV(j):
                h, g, kt = j["h"], j["g"], j["kt"]
                po_s, po_w = pOs[h % 2], pOw[h % 2]
                if j["first"]:
                    for po in (po_s, po_w):
                        b.op("pe", lambda e: e.matmul(po[:, 0:260], lhsT=zer[:, 0:128], rhs=zer[:, 0:260], start=True, stop=True), reads=[zer], writes=[po])
                po = po_s if j["kind"] == "s" else po_w
                va = self.vaug_s if j["kind"] == "s" else self.vaug_w
                for qs in range(j["qlo"], j["qhi"] + 1):
                    o = (qs - j["qlo"]) * 128
                    b.op("pe", lambda e: e.matmul(po[:, qs * 65:(qs + 1) * 65], lhsT=j["pt"][:, o:o + 128], rhs=va[:, kt, g, :],
                                                  start=False, stop=j["lastkt"], skip_group_check=True), reads=[j["pt"], va], writes=[po])
                if j["last"]:
                    for qs in range(4):
                        finalize(po_s, qs * 65, h, qs, 1, False)
                        finalize(po_w, qs * 65, h, qs, 2, False)

            LA = 2
            for i_ in range(len(jobs) + LA):
                if i_ < len(jobs):
                    emitS(jobs[i_])
                if i_ >= LA:
                    emitE(jobs[i_ - LA])
                    emitPV(jobs[i_ - LA])
            for qs in range(4):
                qt = qs_base + qs
                ot = oaT[qs % 2]
                b.op("act", lambda e: e.copy(out=oab[:], in_=oacc[:, qs, :]), reads=[oacc], writes=[oab])
                for c in range(4):
                    b.op("pe", lambda e: e.transpose(out=pTr[:, 4 + c, :], in_=oab[:, c * 128:(c + 1) * 128], identity=self.ident[:]), reads=[oab, self.ident], writes=[pTr])
                b.op("act", lambda e: e.copy(out=ot[:], in_=pTr[:, 4:8, :]), reads=[pTr], writes=[ot])
                b.dma("pool", self.oaT_d[:, :, qt * 128:(qt + 1) * 128].rearrange("c p t -> p c t"), ot[:], reads=[ot], writes=[self.oaT_d])
        if "attn" in self.debug:
            d = self.dbg_out("oaT", [4, 128, S], BF16)
            b.dma("pool", d, self.oaT_d[:], reads=[self.oaT_d])


Prog.phase_attn2 = _phase_attn2


def _phase_merge(self):
    b = self.b
    I = self.inp
    self.x1_d = b.dram("x1_d", [S, D], F32)
    with b.scope():
        gat = self.load_gain("gat2", I["attn_norm_g"][0])
        stage = [b.sb(f"mst{i}", [128, 1024], F32) for i in range(2)]
        wg = b.sb("wg", [128, 8, 2048], BF16)
        for n in range(2):
            for c in range(8):
                st = stage[c % 2]
                b.dma("sp", st[:], I["w_in"][0][c * 128:(c + 1) * 128, GA0 + n * 1024:GA0 + (n + 1) * 1024], writes=[st])
                b.op("act", lambda e: e.activation(out=wg[:, c, n * 1024:(n + 1) * 1024], in_=st[:], func=AF.Copy, scale=gat[:, c:c + 1]),
                     reads=[st, gat], writes=[wg])
        wa = b.sb("wa", [128, 4, 1024], BF16)
        wb = b.sb("wb", [128, 4, 1024], BF16)
        wo = b.sb("wo", [128, 8, 1024], BF16)
        self.load_weight(wa, I["w_proj_a"][0], 1024, kch=4, stage=stage, eng="dve")
        self.load_weight(wb, I["w_proj_b"][0], 1024, kch=4, stage=stage, eng="dve")
        self.load_weight(wo, I["w_out"][0], 1024, kch=8, stage=stage, eng="dve")
        xt = [b.sb(f"mxt{i}", [128, D], F32) for i in range(2)]
        junk = b.sb("mjunk", [128, D], BF16)
        ss = [b.sb(f"mss{i}", [128, 1], F32) for i in range(2)]
        hb = [b.sb(f"mhb{i}", [128, D], BF16) for i in range(2)]
        hT = [b.sb(f"mhT{i}", [128, 8, 128], BF16) for i in range(2)]
        oat = [b.sb(f"oat{i}", [128, 4, 128], BF16) for i in range(2)]
        obt = [b.sb(f"obt{i}", [128, 4, 128], BF16) for i in range(2)]
        sg = b.sb("msg", [128, 2048], F32)
        m1 = b.sb("m1", [128, 1024], F32)
        m2 = b.sb("m2", [128, 1024], F32)
        mgb = b.sb("mgb", [128, 1024], BF16)
        mT = b.sb("mT", [128, 8, 128], BF16)
        x1t = [b.sb(f"x1t{i}", [128, D], F32) for i in range(2)]
        pt = b.ps("mpt", [128, 8, 128], BF16)
        pg = [b.ps(f"mpg{i}", [128, 512], F32) for i in range(2)]
        pa = [b.ps(f"mpa{i}", [128, 512], F32) for i in range(2)]
        pb = [b.ps(f"mpb{i}", [128, 512], F32) for i in range(2)]
        for t in range(getattr(self, "nt_limit", NT)):
            i = t % 2
            self.make_hT(I["x"], t, xt[i], junk, ss[i], hb[i], pt, hT[i], self.ident)
            b.dma("sp", oat[i][:], self.oaT_d[:, :, t * 128:(t + 1) * 128].rearrange("c p t -> p c t"), reads=[self.oaT_d], writes=[oat[i]])
            b.dma("sp", obt[i][:], self.obT_d[:, :, t * 128:(t + 1) * 128].rearrange("c p t -> p c t"), reads=[self.obT_d], writes=[obt[i]])
            for n in range(4):
                p = pg[n % 2]
                for c in range(8):
                    b.op("pe", lambda e: e.matmul(p[:, :], lhsT=hT[i][:, c, :], rhs=wg[:, c, n * 512:(n + 1) * 512], start=(c == 0), stop=(c == 7)),
                         reads=[hT[i], wg], writes=[p])
                b.op("act", lambda e: e.activation(out=sg[:, n * 512:(n + 1) * 512], in_=p[:, :], func=AF.Sigmoid), reads=[p], writes=[sg])
            for n in range(2):
                for c in range(4):
                    b.op("pe", lambda e: e.matmul(pa[n][:, :], lhsT=oat[i][:, c, :], rhs=wa[:, c, n * 512:(n + 1) * 512], start=(c == 0), stop=(c == 3)),
                         reads=[oat[i], wa], writes=[pa[n]])
                for c in range(4):
                    b.op("pe", lambda e: e.matmul(pb[n][:, :], lhsT=obt[i][:, c, :], rhs=wb[:, c, n * 512:(n + 1) * 512], start=(c == 0), stop=(c == 3)),
                         reads=[obt[i], wb], writes=[pb[n]])
                b.op("dve", lambda e: e.tensor_tensor(out=m1[:, n * 512:(n + 1) * 512], in0=pa[n][:, :], in1=sg[:, n * 512:(n + 1) * 512], op=ALU.mult),
                     reads=[pa[n], sg], writes=[m1])
                b.op("dve", lambda e: e.tensor_tensor(out=m2[:, n * 512:(n + 1) * 512], in0=pb[n][:, :], in1=sg[:, 1024 + n * 512:1024 + (n + 1) * 512], op=ALU.mult),
                     reads=[pb[n], sg], writes=[m2])
            b.op("pool", lambda e: e.tensor_tensor(out=mgb[:], in0=m1[:], in1=m2[:], op=ALU.add), reads=[m1, m2], writes=[mgb])
            for c in range(8):
                b.op("pe", lambda e: e.transpose(out=pt[:, c, :], in_=mgb[:, c * 128:(c + 1) * 128], identity=self.ident[:]), reads=[mgb, self.ident], writes=[pt])
            b.op("act", lambda e: e.copy(out=mT[:], in_=pt[:]), reads=[pt], writes=[mT])
            for n in range(2):
                for c in range(8):
                    b.op("pe", lambda e: e.matmul(pa[n][:, :], lhsT=mT[:, c, :], rhs=wo[:, c, n * 512:(n + 1) * 512], start=(c == 0), stop=(c == 7)),
                         reads=[mT, wo], writes=[pa[n]])
                b.op("dve", lambda e: e.tensor_tensor(out=x1t[i][:, n * 512:(n + 1) * 512], in0=pa[n][:, :], in1=xt[i][:, n * 512:(n + 1) * 512], op=ALU.add),
                     reads=[pa[n], xt[i]], writes=[x1t[i]])
            b.dma("pool", self.x1_d[t * 128:(t + 1) * 128, :], x1t[i][:], reads=[x1t[i]], writes=[self.x1_d])
        if "merge" in self.debug:
            d = self.dbg_out("x1", [S, D], F32)
            b.dma("pool", d, self.x1_d[:], reads=[self.x1_d])


def _phase_ffn(self):
    b = self.b
    I = self.inp
    TG = 128
    NFT = 44
    with b.scope():
        gf = self.load_gain("gf", I["ffn_norm_g"][0])
        stage = [b.sb(f"fst{i}", [128, 1024], F32) for i in range(2)]
        wu = b.sb("wu", [128, 8, 2 * DFF], BF16)
        for n in range(8):
            for c in range(8):
                st = stage[c % 2]
                b.dma("sp", st[:, 0:704], I["w_up"][0][c * 128:(c + 1) * 128, n * 704:(n + 1) * 704], writes=[st])
                b.op("act", lambda e: e.activation(out=wu[:, c, n * 704:(n + 1) * 704], in_=st[:, 0:704], func=AF.Copy, scale=gf[:, c:c + 1]),
                     reads=[st, gf], writes=[wu])
        wd = b.sb("wd", [128, 22, D], BF16)
        self.load_weight(wd, I["w_down"][0], D, kch=22, stage=stage, eng="dve")
        cw = b.sb("cw", [128, 3, NFT], F32)
        for j in range(3):
            b.dma("sp", cw[:, j, :], I["conv_w"][0][j].rearrange("(c p) -> p c", p=128), writes=[cw], allow_slow_non_contiguous=True)
        cbias = self.load_gain("cbias", I["conv_b"][0], kch=NFT)
        carry = b.sb("carry", [128, NFT, 2], F32)
        b.op("pool", lambda e: e.memset(carry[:], 0.0), writes=[carry])
        xt = [b.sb(f"fxt{i}", [128, D], F32) for i in range(2)]
        junk = b.sb("fjunk", [128, D], BF16)
        ss = [b.sb(f"fss{i}", [128, 1], F32) for i in range(2)]
        hb = [b.sb(f"fhb{i}", [128, D], BF16) for i in range(2)]
        hT1 = [b.sb(f"fhT{i}", [128, 8, 128], BF16) for i in range(2)]
        hTg = b.sb("fhTg", [128, 8, TG], BF16)
        ub = [b.sb(f"ub{i}", [128, TG + 2], F32) for i in range(2)]
        cv = [b.sb(f"cv{i}", [128, TG], F32) for i in range(2)]
        sgl = b.sb("sgl", [128, TG], F32)
        actT = b.sb("actT", [128, 22, TG], BF16)
        self._val = b.sb("fval", [128, 22, TG], BF16)
        ot = xt
        pt = b.ps("fpt", [128, 8, 128], BF16)
        pu = [b.ps(f"fpu{i}", [128, 512], F32) for i in range(3)]
        pd = [b.ps(f"fpd{i}", [128, 512], F32) for i in range(2)]
        ng = getattr(self, "nt_limit", NT) * 128 // TG
        for gi in range(ng):
            for s_ in range(TG // 128):
                t = gi * (TG // 128) + s_
                self.make_hT(self.x1_d, t, xt[s_], junk, ss[s_], hb[s_], pt, hT1[s_], self.ident)
                b.op("pool", lambda e: e.tensor_copy(out=hTg[:, :, s_ * 128:(s_ + 1) * 128], in_=hT1[s_][:]), reads=[hT1[s_]], writes=[hTg])
            for ft in range(NFT):
                p = pu[ft % 3]
                u = ub[ft % 2]
                c_ = cv[(ft // 22) % 2] if False else cv[ft % 2]
                for c in range(8):
                    b.op("pe", lambda e: e.matmul(p[:, 0:TG], lhsT=wu[:, c, ft * 128:(ft + 1) * 128], rhs=hTg[:, c, :], start=(c == 0), stop=(c == 7)),
                         reads=[wu, hTg], writes=[p])
                b.op("act", lambda e: e.copy(out=u[:, 2:TG + 2], in_=p[:, 0:TG]), reads=[p], writes=[u])
                b.op("pool", lambda e: e.tensor_copy(out=u[:, 0:2], in_=carry[:, ft, :]), reads=[carry], writes=[u])
                b.op("pool", lambda e: e.tensor_copy(out=carry[:, ft, :], in_=u[:, TG:TG + 2]), reads=[u], writes=[carry])
                b.op("dve", lambda e: e.tensor_scalar(out=c_[:], in0=u[:, 0:TG], scalar1=cw[:, 0, ft:ft + 1], scalar2=cbias[:, ft:ft + 1], op0=ALU.mult, op1=ALU.add),
                     reads=[u, cw, cbias], writes=[c_])
                b.op("dve", lambda e: e.scalar_tensor_tensor(out=c_[:], in0=u[:, 1:TG + 1], scalar=cw[:, 1, ft:ft + 1], in1=c_[:], op0=ALU.mult, op1=ALU.add),
                     reads=[u, cw, c_], writes=[c_])
                if ft < 22:
                    b.op("dve", lambda e: e.scalar_tensor_tensor(out=self._val[:, ft, :], in0=u[:, 2:TG + 2], scalar=cw[:, 2, ft:ft + 1], in1=c_[:], op0=ALU.mult, op1=ALU.add),
                         reads=[u, cw, c_], writes=[self._val])
                else:
                    b.op("dve", lambda e: e.scalar_tensor_tensor(out=c_[:], in0=u[:, 2:TG + 2], scalar=cw[:, 2, ft:ft + 1], in1=c_[:], op0=ALU.mult, op1=ALU.add),
                         reads=[u, cw, c_], writes=[c_])
                    b.op("act", lambda e: e.activation(out=sgl[:], in_=c_[:], func=AF.Silu), reads=[c_], writes=[sgl])
                    b.op("dve", lambda e: e.tensor_tensor(out=actT[:, ft - 22, :], in0=sgl[:], in1=self._val[:, ft - 22, :], op=ALU.mult),
                         reads=[sgl, self._val], writes=[actT])
            for s_ in range(TG // 128):
                t = gi * (TG // 128) + s_
                for n in range(2):
                    for f in range(22):
                        b.op("pe", lambda e: e.matmul(pd[n][:, :], lhsT=actT[:, f, s_ * 128:(s_ + 1) * 128], rhs=wd[:, f, n * 512:(n + 1) * 512], start=(f == 0), stop=(f == 21)),
                             reads=[actT, wd], writes=[pd[n]])
                    b.op("dve", lambda e: e.tensor_tensor(out=ot[s_][:, n * 512:(n + 1) * 512], in0=pd[n][:, :], in1=xt[s_][:, n * 512:(n + 1) * 512], op=ALU.add),
                         reads=[pd[n], xt[s_]], writes=[ot[s_]])
                b.dma("pool", self.out[t * 128:(t + 1) * 128, :], ot[s_][:], reads=[ot[s_]])


Prog.phase_merge = _phase_merge
Prog.phase_ffn = _phase_ffn


def _phase_rwkv(self):
    b = self.b
    I = self.inp
    TG = 256
    NCH = TG // 64
    tt = lambda eng, out, in0, in1, op, rd, wr: b.op(eng, lambda e: e.tensor_tensor(out=out, in0=in0, in1=in1, op=op), reads=rd, writes=wr)
    with b.scope():
        gat = self.load_gain("gat3", I["attn_norm_g"][0])
        stage = [b.sb(f"rst{i}", [128, 1792], F32) for i in range(2)]
        wr = b.sb("wr", [128, 8, 1792], BF16)
        self.load_weight(wr, I["w_in"][0][:, RW0:RW0 + 1792], 1792, gvec=gat, stage=stage)

        def colvec(name, src, n):
            t = b.sb(name, [64, n], F32)
            b.dma("sp", t[:], src.rearrange("(c p) -> p c", p=64), writes=[t], allow_slow_non_contiguous=True)
            return t
        mu = colvec("mu", I["rwkv_mu"][0], 28)
        w0 = colvec("w0", I["rwkv_w0"][0], 8)
        a0 = colvec("a0", I["rwkv_a0"][0], 8)
        k_k = colvec("k_k", I["rwkv_k_k"][0], 8)
        k_a = colvec("k_a", I["rwkv_k_a"][0], 8)
        r_k = colvec("r_k", I["rwkv_r_k"][0].rearrange("h d -> (h d)"), 8)
        w2s = b.sb("w2s", [64, 512], F32)
        a2s = b.sb("a2s", [64, 512], F32)
        g2s = b.sb("g2s", [64, 2, 512], F32)
        b.dma("sp", w2s[:], I["rwkv_w2"][0], writes=[w2s])
        b.dma("sp", a2s[:], I["rwkv_a2"][0], writes=[a2s])
        b.dma("sp", g2s[:], I["rwkv_g2"][0].rearrange("(two l) f -> l two f", two=2), writes=[g2s])
        lng = b.sb("lng", [64, 512], F32)
        lnb = b.sb("lnb", [64, 512], F32)
        b.dma("sp", lng[:], I["rwkv_ln_g"][0].partition_broadcast(64), writes=[lng])
        b.dma("sp", lnb[:], I["rwkv_ln_b"][0].partition_broadcast(64), writes=[lnb])
        msk = b.sb("rmsk", [64, 3, 64], F32)
        b.dma("sp", msk[:], I["rwmask"], writes=[msk])
        rstm = b.sb("rstm", [64, TG], F32)
        b.dma("sp", rstm[:], I["rwreset"][:, 0:TG], writes=[rstm])
        ones = b.sb("ones64", [64, 64], F32)
        b.op("pool", lambda e: e.memset(ones[:], 1.0), writes=[ones])
        idf = self.identf
        carry = b.sb("rcarry", [64, 28], F32)
        b.op("pool", lambda e: e.memset(carry[:], 0.0), writes=[carry])
        Hs = [[b.sb(f"H{h}_{i}", [64, 64], F32) for i in range(2)] for h in range(8)]
        for h in range(8):
            b.op("pool", lambda e: e.memset(Hs[h][0][:], 0.0), writes=[Hs[h][0]])
        xt = [b.sb(f"rxt{i}", [128, D], F32) for i in range(2)]
        junk = b.sb("rjunk", [128, D], BF16)
        ss = [b.sb(f"rss{i}", [128, 1], F32) for i in range(2)]
        hb = [b.sb(f"rhb{i}", [128, D], BF16) for i in range(2)]
        hT1 = [b.sb(f"rhT{i}", [128, 8, 128], BF16) for i in range(2)]
        hTg = b.sb("rhTg", [128, 8, TG], BF16)
        pbuf = [b.sb(f"rpb{i}", [64, TG + 1], F32) for i in range(2)]
        dtmp = b.sb("rdtmp", [64, TG], F32)
        X = [b.sb(f"rX{w}", [64, 8, TG], F32) for w in range(3)]
        xs = b.sb("rxs", [64, 4, TG], F32)
        BV = b.sb("rBV", [64, 8, TG], F32)
        Ytm = b.sb("rYtm", [64, NCH, 8, 64], F32)
        sqv = b.sb("rsqv", [64, NCH, 8, 64], F32)
        st1 = b.sb("rst1", [64, NCH * 8], F32)
        st2 = b.sb("rst2", [64, NCH * 8], F32)
        T = {n: b.sb("r" + n, [64, TG], F32) for n in ["lw", "as", "kk", "sq", "kkn", "bv", "kp", "t1", "L", "Lx", "Ep", "Em", "Ex", "BT", "KT", "BG", "KG", "rk"]}
        AR = b.sb("rAR", [64, NCH, 2, 64], F32)
        TM = [b.sb(f"rTM{i}", [64, 3, 64], F32) for i in range(2)]
        XM = [b.sb(f"rXM{i}", [64, 4, 64], F32) for i in range(2)]
        AA = [b.sb(f"rAA{i}", [64, 2, 64], F32) for i in range(3)]
        PP = [b.sb(f"rPP{i}", [64, 64], F32) for i in range(3)]
        Xs = b.sb("rXs", [64, 64], F32)
        Us = b.sb("rUs", [64, 64], F32)
        obf = [b.sb(f"robf{i}", [64, TG], BF16) for i in range(2)]
        otmp = b.sb("rotmp", [64, TG], F32)
        pt = b.ps("rpt", [128, 8, 128], BF16)
        pp = [b.ps(f"rpp{i}", [128, 512], F32) for i in range(2)]
        pq = [b.ps(f"rpq{i}", [128, 512], F32) for i in range(2)]
        pd = [b.ps(f"rpd{i}", [128, 512], F32) for i in range(2)]
        pz = b.ps("rpz", [128, 512], F32)
        cnt = {"pp": 0, "pq": 0, "pd": 0, "aa": 0, "ppb": 0, "tm": 0, "xm": 0, "pb": 0}

        def nxt(k, lst):
            cnt[k] += 1
            return lst[cnt[k] % len(lst)]

        ngr = getattr(self, "nrg_limit", S // TG)
        for gi in range(ngr):
            q0 = gi * TG
            for s_ in range(TG // 128):
                t = gi * (TG // 128) + s_
                self.make_hT(I["x"], t, xt[s_], junk, ss[s_], hb[s_], pt, hT1[s_], self.ident)
                b.op("pool", lambda e: e.tensor_copy(out=hTg[:, :, s_ * 128:(s_ + 1) * 128], in_=hT1[s_][:]), reads=[hT1[s_]], writes=[hTg])

            def proj_lerp(fc, out_ap, out_buf, post=None):
                p = nxt("pp", pp)
                for c in range(8):
                    b.op("pe", lambda e: e.matmul(p[0:64, 0:TG], lhsT=wr[:, c, fc * 64:(fc + 1) * 64], rhs=hTg[:, c, :], start=(c == 0), stop=(c == 7)),
                         reads=[wr, hTg], writes=[p])
                pb_ = nxt("pb", pbuf)
                b.op("act", lambda e: e.copy(out=pb_[:, 1:TG + 1], in_=p[0:64, 0:TG]), reads=[p], writes=[pb_])
                b.op("pool", lambda e: e.tensor_copy(out=pb_[:, 0:1], in_=carry[:, fc:fc + 1]), reads=[carry], writes=[pb_])
                b.op("pool", lambda e: e.tensor_copy(out=carry[:, fc:fc + 1], in_=pb_[:, TG:TG + 1]), reads=[pb_], writes=[carry])
                tt("dve", dtmp[:], pb_[:, 0:TG], pb_[:, 1:TG + 1], ALU.subtract, [pb_], [dtmp])
                b.op("dve", lambda e: e.scalar_tensor_tensor(out=out_ap, in0=dtmp[:], scalar=mu[:, fc:fc + 1], in1=pb_[:, 1:TG + 1], op0=ALU.mult, op1=ALU.add),
                     reads=[dtmp, mu, pb_], writes=[out_buf])

            for w in range(3):
                for h in range(8):
                    proj_lerp(w * 8 + h, X[w][:, h, :], X[w])
            for j in range(4):
                proj_lerp(24 + j, xs[:, j, :], xs)
            b.op("act", lambda e: e.activation(out=xs[:, 0, :], in_=xs[:, 0, :], func=AF.Tanh), reads=[xs], writes=[xs])
            b.op("act", lambda e: e.activation(out=xs[:, 2:4, :], in_=xs[:, 2:4, :], func=AF.Sigmoid), reads=[xs], writes=[xs])

            for h in range(8):
                hs = slice(h * 64, (h + 1) * 64)
                R_, K_, V_ = X[0][:, h, :], X[1][:, h, :], X[2][:, h, :]
                p = nxt("pp", pp)
                b.op("pe", lambda e: e.matmul(p[0:64, 0:TG], lhsT=w2s[:, hs], rhs=xs[:, 0, :], start=True, stop=True), reads=[w2s, xs], writes=[p])
                b.op("act", lambda e: e.activation(out=T["lw"][:], in_=p[0:64, 0:TG], func=AF.Sigmoid, bias=w0[:, h:h + 1]), reads=[p, w0], writes=[T["lw"]])
                b.op("pool", lambda e: e.tensor_scalar_mul(out=T["lw"][:], in0=T["lw"][:], scalar1=-0.6065306597126334), reads=[T["lw"]], writes=[T["lw"]])
                p = nxt("pp", pp)
                b.op("pe", lambda e: e.matmul(p[0:64, 0:TG], lhsT=a2s[:, hs], rhs=xs[:, 1, :], start=True, stop=True), reads=[a2s, xs], writes=[p])
                b.op("act", lambda e: e.activation(out=T["as"][:], in_=p[0:64, 0:TG], func=AF.Sigmoid, bias=a0[:, h:h + 1]), reads=[p, a0], writes=[T["as"]])
                b.op("dve", lambda e: e.tensor_scalar_mul(out=T["kk"][:], in0=K_, scalar1=k_k[:, h:h + 1]), reads=[X[1], k_k], writes=[T["kk"]])
                tt("pool", T["sq"][:], T["kk"][:], T["kk"][:], ALU.mult, [T["kk"]], [T["sq"]])
                p = nxt("pp", pp)
                b.op("pe", lambda e: e.matmul(p[0:64, 0:TG], lhsT=ones[:], rhs=T["sq"][:], start=True, stop=True), reads=[ones, T["sq"]], writes=[p])
                b.op("act", lambda e: e.activation(out=T["sq"][:], in_=p[0:64, 0:TG], func=AF.Sqrt), reads=[p], writes=[T["sq"]])
                b.op("dve", lambda e: e.tensor_scalar_max(out=T["sq"][:], in0=T["sq"][:], scalar1=1e-12), reads=[T["sq"]], writes=[T["sq"]])
                b.op("dve", lambda e: e.reciprocal(out=T["sq"][:], in_=T["sq"][:]), reads=[T["sq"]], writes=[T["sq"]])
                tt("dve", T["kkn"][:], T["kk"][:], T["sq"][:], ALU.mult, [T["kk"], T["sq"]], [T["kkn"]])
                tt("pool", T["bv"][:], T["kkn"][:], T["as"][:], ALU.mult, [T["kkn"], T["as"]], [T["bv"]])
                b.op("dve", lambda e: e.tensor_scalar(out=T["t1"][:], in0=T["as"][:], scalar1=-1.0, scalar2=k_a[:, h:h + 1], op0=ALU.add, op1=ALU.mult),
                     reads=[T["as"], k_a], writes=[T["t1"]])
                b.op("dve", lambda e: e.scalar_tensor_tensor(out=T["kp"][:], in0=T["t1"][:], scalar=1.0, in1=K_, op0=ALU.add, op1=ALU.mult),
                     reads=[T["t1"], X[1]], writes=[T["kp"]])
                tt("pool", T["rk"][:], R_, T["kp"][:], ALU.mult, [X[0], T["kp"]], [T["rk"]])
                b.op("pool", lambda e: e.tensor_scalar_mul(out=T["rk"][:], in0=T["rk"][:], scalar1=r_k[:, h:h + 1]), reads=[T["rk"], r_k], writes=[T["rk"]])
                p = nxt("pp", pp)
                b.op("pe", lambda e: e.matmul(p[0:64, 0:TG], lhsT=ones[:], rhs=T["rk"][:], start=True, stop=True), reads=[ones, T["rk"]], writes=[p])
                tt("dve", BV[:, h, :], p[0:64, 0:TG], V_, ALU.mult, [p, X[2]], [BV])
                b.op("dve", lambda e: e.tensor_tensor_scan(out=T["L"][:], data0=rstm[:], data1=T["lw"][:], initial=0.0, op0=ALU.mult, op1=ALU.add),
                     reads=[rstm, T["lw"]], writes=[T["L"]])
                tt("pool", T["Lx"][:], T["L"][:], T["lw"][:], ALU.subtract, [T["L"], T["lw"]], [T["Lx"]])
                b.op("act", lambda e: e.activation(out=T["Ep"][:], in_=T["L"][:], func=AF.Exp), reads=[T["L"]], writes=[T["Ep"]])
                b.op("act", lambda e: e.activation(out=T["Em"][:], in_=T["L"][:], func=AF.Exp, scale=-1.0), reads=[T["L"]], writes=[T["Em"]])
                b.op("act", lambda e: e.activation(out=T["Ex"][:], in_=T["Lx"][:], func=AF.Exp), reads=[T["Lx"]], writes=[T["Ex"]])
                c3 = lambda ap: ap.rearrange("p (c t) -> p c t", t=64)
                b.op("dve", lambda e: e.scalar_tensor_tensor(out=AR[:, :, 0, :], in0=c3(T["kkn"][:]), scalar=-1.0, in1=c3(T["Ex"][:]), op0=ALU.mult, op1=ALU.mult),
                     reads=[T["kkn"], T["Ex"]], writes=[AR])
                tt("pool", AR[:, :, 1, :], c3(R_), c3(T["Ep"][:]), ALU.mult, [X[0], T["Ep"]], [AR])
                tt("dve", T["BT"][:], T["bv"][:], T["Em"][:], ALU.mult, [T["bv"], T["Em"]], [T["BT"]])
                tt("pool", T["KT"][:], T["kp"][:], T["Em"][:], ALU.mult, [T["kp"], T["Em"]], [T["KT"]])
                gC = c3(T["Ep"][:])[:, :, 63:64].to_broadcast([64, NCH, 64])
                tt("dve", c3(T["BG"][:]), c3(T["BT"][:]), gC, ALU.mult, [T["BT"], T["Ep"]], [T["BG"]])
                tt("pool", c3(T["KG"][:]), c3(T["KT"][:]), gC, ALU.mult, [T["KT"], T["Ep"]], [T["KG"]])
                for c in range(NCH):
                    cs = slice(c * 64, (c + 1) * 64)
                    Hc = Hs[h][(gi * NCH + c) % 2]
                    Hn = Hs[h][(gi * NCH + c + 1) % 2]
                    p = nxt("pq", pq)
                    for j, (src, sb_) in enumerate([(V_[:, cs], X[2]), (T["BG"][:, cs], T["BG"]), (T["KG"][:, cs], T["KG"])]):
                        b.op("pe", lambda e: e.transpose(out=p[0:64, j * 64:(j + 1) * 64], in_=src, identity=idf[0:64, 0:64]), reads=[sb_, idf], writes=[p])
                    tm = nxt("tm", TM)
                    b.op("act", lambda e: e.copy(out=tm[:].rearrange("p a b -> p (a b)"), in_=p[0:64, 0:192]), reads=[p], writes=[tm])
                    p = nxt("pq", pq)
                    arc = AR[:, c, :, :].rearrange("p a t -> p (a t)")
                    b.op("pe", lambda e: e.matmul(p[0:64, 0:128], lhsT=T["BT"][:, cs], rhs=arc, start=True, stop=True), reads=[T["BT"], AR], writes=[p])
                    b.op("pe", lambda e: e.matmul(p[0:64, 128:256], lhsT=T["KT"][:, cs], rhs=arc, start=True, stop=True), reads=[T["KT"], AR], writes=[p])
                    b.op("pe", lambda e: e.matmul(p[0:64, 256:320], lhsT=AR[:, c, 0, :], rhs=T["BT"][:, cs], start=True, stop=True), reads=[T["BT"], AR], writes=[p])
                    xm = nxt("xm", XM)
                    tt("dve", xm[:].rearrange("p (a m) t -> p a m t", a=2), p[0:64, 0:256].rearrange("p (a m t) -> p a m t", a=2, m=2),
                       msk[:, None, 0:2, :].to_broadcast([64, 2, 2, 64]), ALU.mult, [p, msk], [xm])
                    aa = nxt("aa", AA)
                    b.op("pool", lambda e: e.tensor_copy(out=aa[:, 0, :], in_=xm[:, 0, :]), reads=[xm], writes=[aa])
                    tt("dve", aa[:, 1, :], p[0:64, 256:320], msk[:, 2, :], ALU.mult, [p, msk], [aa])
                    P_ = nxt("ppb", PP)
                    tt("pool", P_[:], xm[:, 0, :], idf[0:64, 0:64], ALU.add, [xm, idf], [P_])
                    for step in range(5):
                        pdb = nxt("pd", pd)
                        b.op("pe", lambda e: e.matmul(pdb[0:64, 0:64], lhsT=aa[:, 1, :], rhs=aa[:, 0, :], start=True, stop=True), reads=[aa], writes=[pdb])
                        b.op("pe", lambda e: e.matmul(pdb[0:64, 64:128], lhsT=aa[:, 0, :], rhs=aa[:, 1, :], start=True, stop=True), reads=[aa], writes=[pdb])
                        aa2 = nxt("aa", AA)
                        b.op("act", lambda e: e.copy(out=aa2[:].rearrange("p a t -> p (a t)"), in_=pdb[0:64, 0:128]), reads=[pdb], writes=[aa2])
                        b.op("pe", lambda e: e.matmul(pdb[0:64, 128:192], lhsT=aa2[:, 1, :], rhs=P_[:], start=True, stop=True), reads=[aa2, P_], writes=[pdb])
                        P2 = nxt("ppb", PP)
                        tt("dve", P2[:], pdb[0:64, 128:192], P_[:], ALU.add, [pdb, P_], [P2])
                        aa, P_ = aa2, P2
                    b.op("pe", lambda e: e.matmul(pz[0:64, 0:64], lhsT=xm[:, 2, :], rhs=tm[:, 0, :], start=True, stop=False), reads=[xm, tm], writes=[pz])
                    b.op("pe", lambda e: e.matmul(pz[0:64, 0:64], lhsT=AR[:, c, 0, :], rhs=Hc[:], start=False, stop=True), reads=[AR, Hc], writes=[pz])
                    b.op("act", lambda e: e.copy(out=Xs[:], in_=pz[0:64, 0:64]), reads=[pz], writes=[Xs])
                    b.op("pe", lambda e: e.matmul(pz[0:64, 64:128], lhsT=P_[:], rhs=Xs[:], start=True, stop=True), reads=[P_, Xs], writes=[pz])
                    b.op("act", lambda e: e.copy(out=Us[:], in_=pz[0:64, 64:128]), reads=[pz], writes=[Us])
                    b.op("pe", lambda e: e.matmul(pz[0:64, 128:192], lhsT=AR[:, c, 1, :], rhs=Hc[:], start=True, stop=False), reads=[AR, Hc], writes=[pz])
                    b.op("pe", lambda e: e.matmul(pz[0:64, 128:192], lhsT=xm[:, 1, :], rhs=Us[:], start=False, stop=False), reads=[xm, Us], writes=[pz])
                    b.op("pe", lambda e: e.matmul(pz[0:64, 128:192], lhsT=xm[:, 3, :], rhs=tm[:, 0, :], start=False, stop=True), reads=[xm, tm], writes=[pz])
                    b.op("pe", lambda e: e.matmul(pz[0:64, 192:256], lhsT=tm[:, 1, :], rhs=Us[:], start=True, stop=False), reads=[tm, Us], writes=[pz])
                    b.op("pe", lambda e: e.matmul(pz[0:64, 192:256], lhsT=tm[:, 2, :], rhs=tm[:, 0, :], start=False, stop=True), reads=[tm], writes=[pz])
                    b.op("act", lambda e: e.copy(out=Ytm[:, c, h, :], in_=pz[0:64, 128:192]), reads=[pz], writes=[Ytm])
                    b.op("dve", lambda e: e.scalar_tensor_tensor(out=Hn[:], in0=Hc[:], scalar=T["Ep"][:, c * 64 + 63:c * 64 + 64], in1=pz[0:64, 192:256],
                                                                 op0=ALU.mult, op1=ALU.add), reads=[Hc, T["Ep"], pz], writes=[Hn])
            Y3 = Ytm[:].rearrange("p c h i -> p (c h) i")
            S3 = sqv[:].rearrange("p c h i -> p (c h) i")
            b.op("dve", lambda e: e.tensor_reduce(out=st1[:], in_=Y3, axis=AX.X, op=ALU.add), reads=[Ytm], writes=[st1])
            b.op("pool", lambda e: e.tensor_scalar_mul(out=st1[:], in0=st1[:], scalar1=1.0 / 64), reads=[st1], writes=[st1])
            tt("dve", Y3, Y3, st1[:].unsqueeze(2).to_broadcast([64, NCH * 8, 64]), ALU.subtract, [Ytm, st1], [Ytm])
            tt("pool", S3, Y3, Y3, ALU.mult, [Ytm], [sqv])
            b.op("dve", lambda e: e.tensor_reduce(out=st2[:], in_=S3, axis=AX.X, op=ALU.add), reads=[sqv], writes=[st2])
            b.op("act", lambda e: e.activation(out=st2[:], in_=st2[:], func=AF.Sqrt, scale=1.0 / 64, bias=64e-5), reads=[st2], writes=[st2])
            b.op("dve", lambda e: e.reciprocal(out=st2[:], in_=st2[:]), reads=[st2], writes=[st2])
            tt("dve", Y3, Y3, st2[:].unsqueeze(2).to_broadcast([64, NCH * 8, 64]), ALU.mult, [Ytm, st2], [Ytm])
            lg = lng[:].rearrange("p (h i) -> p h i", i=64)[:, None, :, :].to_broadcast([64, NCH, 8, 64])
            lb = lnb[:].rearrange("p (h i) -> p h i", i=64)[:, None, :, :].to_broadcast([64, NCH, 8, 64])
            tt("pool", Ytm[:], Ytm[:], lg, ALU.mult, [Ytm, lng], [Ytm])
            tt("dve", Ytm[:], Ytm[:], lb, ALU.add, [Ytm, lnb], [Ytm])
            for h in range(8):
                p = nxt("pq", pq)
                for c in range(NCH):
                    b.op("pe", lambda e: e.transpose(out=p[0:64, c * 64:(c + 1) * 64], in_=Ytm[:, c, h, :], identity=idf[0:64, 0:64]), reads=[Ytm, idf], writes=[p])
                tt("dve", otmp[:], p[0:64, 0:TG], BV[:, h, :], ALU.add, [p, BV], [otmp])
                pg_ = nxt("pp", pp)
                b.op("pe", lambda e: e.matmul(pg_[0:64, 0:TG], lhsT=g2s[:, 0, h * 64:(h + 1) * 64], rhs=xs[:, 2, :], start=True, stop=False), reads=[g2s, xs], writes=[pg_])
                b.op("pe", lambda e: e.matmul(pg_[0:64, 0:TG], lhsT=g2s[:, 1, h * 64:(h + 1) * 64], rhs=xs[:, 3, :], start=False, stop=True), reads=[g2s, xs], writes=[pg_])
                ob_ = obf[h % 2]
                tt("dve", ob_[:], otmp[:], pg_[0:64, 0:TG], ALU.mult, [otmp, pg_], [ob_])
                b.dma("pool", self.obT_d[h // 2, (h % 2) * 64:(h % 2) * 64 + 64, q0:q0 + TG], ob_[:], reads=[ob_], writes=[self.obT_d])
        if "rwkv" in self.debug:
            d = self.dbg_out("obT", [4, 128, S], BF16)
            b.dma("pool", d, self.obT_d[:], reads=[self.obT_d])


Prog.phase_rwkv = _phase_rwkv


def build_full():
    p = Prog()
    b = p.b
    p.alloc_root()
    with b.scope():
        p.alloc_persistent()
        p.phase_nsa_proj()
        p.phase_attn2()
    p.phase_rwkv3()
    p.phase_merge()
    p.phase_ffn2()
    p.finish()
    return p


def kernel(**inputs):
    p = build_full()
    consts = host_consts(inputs["rel_bias"])
    shared = {k: np.ascontiguousarray(np.asarray(inputs[k], np.float32)) for k in W_SPECS if k != "x"}
    shared.update(consts)
    x = np.asarray(inputs["x"], np.float32)
    in_maps = []
    for c in range(8):
        m = dict(shared)
        m["x"] = np.ascontiguousarray(x[c])
        in_maps.append(m)
    res = run_bass_kernel_spmd(p.nc, in_maps, core_ids=list(range(8)))
    return np.stack([np.asarray(r["out"], np.float32) for r in res.results], axis=0)


def _phase_rwkv2(self):
    b = self.b
    I = self.inp
    TG = 128
    NCH = 2
    tt = lambda eng, out, in0, in1, op, rd, wr: b.op(eng, lambda e: e.tensor_tensor(out=out, in0=in0, in1=in1, op=op), reads=rd, writes=wr)
    with b.scope():
        W1 = b.sb("W1", [128, 8, 1792], BF16)
        W2 = b.sb("W2", [128, 8, 1792], BF16)
        with b.scope():
            gat = self.load_gain("gat3", I["attn_norm_g"][0])
            stage = [b.sb(f"rst{i}", [128, 1792], F32) for i in range(2)]
            tmpw = [b.sb(f"rtw{i}", [128, 1792], F32) for i in range(2)]
            mur = self.bcast_row("mur", I["rwkv_mu"][0], 1792)
            for c in range(8):
                st = stage[c % 2]
                tw_ = tmpw[c % 2]
                b.dma("sp", st[:], I["w_in"][0][c * 128:(c + 1) * 128, RW0:RW0 + 1792], writes=[st])
                tt("dve", tw_[:], st[:], mur[:], ALU.mult, [st, mur], [tw_])
                b.op("act", lambda e: e.activation(out=W2[:, c, :], in_=tw_[:], func=AF.Copy, scale=gat[:, c:c + 1]), reads=[tw_, gat], writes=[W2])
                tt("pool", st[:], st[:], tw_[:], ALU.subtract, [st, tw_], [st])
                b.op("act", lambda e: e.activation(out=W1[:, c, :], in_=st[:], func=AF.Copy, scale=gat[:, c:c + 1]), reads=[st, gat], writes=[W1])

        def colvec(name, src, n):
            t = b.sb(name, [64, n], F32)
            b.dma("sp", t[:], src.rearrange("(c p) -> p c", p=64), writes=[t], allow_slow_non_contiguous=True)
            return t
        w0 = colvec("w0", I["rwkv_w0"][0], 8)
        a0 = colvec("a0", I["rwkv_a0"][0], 8)
        k_k = colvec("k_k", I["rwkv_k_k"][0], 8)
        k_a = colvec("k_a", I["rwkv_k_a"][0], 8)
        r_k = colvec("r_k", I["rwkv_r_k"][0].rearrange("h d -> (h d)"), 8)
        w2s = b.sb("w2s", [64, 512], F32)
        a2s = b.sb("a2s", [64, 512], F32)
        g2s = b.sb("g2s", [64, 2, 512], F32)
        b.dma("sp", w2s[:], I["rwkv_w2"][0], writes=[w2s])
        b.dma("sp", a2s[:], I["rwkv_a2"][0], writes=[a2s])
        b.dma("sp", g2s[:], I["rwkv_g2"][0].rearrange("(two l) f -> l two f", two=2), writes=[g2s])
        lng = b.sb("lng", [64, 512], F32)
        lnb = b.sb("lnb", [64, 512], F32)
        b.dma("sp", lng[:], I["rwkv_ln_g"][0].partition_broadcast(64), writes=[lng])
        b.dma("sp", lnb[:], I["rwkv_ln_b"][0].partition_broadcast(64), writes=[lnb])
        msk = b.sb("rmsk", [64, 3, 64], F32)
        b.dma("sp", msk[:], I["rwmask"], writes=[msk])
        rstm = b.sb("rstm", [64, 8 * TG], F32)
        b.dma("sp", rstm[:], I["rwreset"], writes=[rstm])
        ones = b.sb("ones64", [64, 64], F32)
        b.op("pool", lambda e: e.memset(ones[:], 1.0), writes=[ones])
        idf = self.identf
        Hst = b.sb("rH", [64, 2, 8, 64], F32)
        b.op("pool", lambda e: e.memset(Hst[:], 0.0), writes=[Hst])
        xt = [b.sb(f"rxt{i}", [128, D], F32) for i in range(1)] * 2
        junk = b.sb("rjunk", [128, D], BF16)
        ss = [b.sb(f"rss{i}", [128, 1], F32) for i in range(1)] * 2
        hb = [b.sb(f"rhb{i}", [128, D], BF16) for i in range(1)] * 2
        hT1 = [b.sb(f"rhT{i}", [128, 8, 128], BF16) for i in range(1)] * 2
        hTs = b.sb("rhTs", [128, 8, TG + 1], BF16)
        b.op("pool", lambda e: e.memset(hTs[:], 0.0), writes=[hTs])
        XL = b.sb("rXL", [64, 20, TG], F32)
        Vtm = b.sb("rVtm", [64, NCH, 512], F32)
        names = ["LW", "AS", "KKN", "BVc", "KP", "RK", "L", "EP", "EM", "BG", "KG"]
        T = {n: b.sb("r" + n, [64, 8, TG], F32) for n in names}
        T["NR"] = T["RK"]
        T["T1"] = T["BG"]
        T["KK"] = T["KG"]
        T["EX"] = T["L"]
        T["BT"] = T["LW"]
        T["KT"] = T["AS"]
        AR = b.sb("rAR", [64, 8, NCH, 2, 64], F32)
        BON = b.sb("rBON", [64, NCH * 8], F32)
        Ytm = b.sb("rYtm", [64, NCH, 8, 64], F32)
        sqv = b.sb("rsqv", [64, NCH, 8, 64], F32)
        st1 = b.sb("rst1", [64, NCH * 8], F32)
        st2 = b.sb("rst2", [64, NCH * 8], F32)
        TM4 = [b.sb(f"rTM{i}", [64, 4, 2, 64], F32) for i in range(2)]
        XM4 = [b.sb(f"rXM{i}", [64, 4, 4, 64], F32) for i in range(2)]
        AA4 = [b.sb(f"rAA{i}", [64, 4, 2, 64], F32) for i in range(2)]
        PP4 = [b.sb(f"rPP{i}", [64, 4, 64], F32) for i in range(2)]
        Xs4 = b.sb("rXs4", [64, 4, 64], F32)
        Us4 = b.sb("rUs4", [64, 4, 64], F32)
        Ht4 = b.sb("rHt4", [64, 4, 64], F32)
        OBb = b.sb("rOBb", [64, NCH, 512], BF16)
        obT = [b.sb(f"robT{i}", [128, 4, TG], BF16) for i in range(1)] * 2
        pt = b.ps("rpt", [128, 8, 128], BF16)
        pP = b.ps("rpP", [128, 512], F32)
        pA = b.ps("rpA", [128, 1024], F32)
        pB = b.ps("rpB", [128, 512], F32)
        pC = b.ps("rpC", [128, 512], F32)
        pD = b.ps("rpD", [128, 512], F32)
        pZ = b.ps("rpZ", [128, 512], F32)
        cnt = {}

        def nxt(k, lst):
            cnt[k] = cnt.get(k, 0) + 1
            return lst[cnt[k] % len(lst)]
        bc = lambda v: v[:].unsqueeze(2).to_broadcast([64, 8, TG])
        f2 = lambda t_: t_[:].rearrange("p h t -> p (h t)")
        c16 = lambda t_: t_[:].rearrange("p h (c t) -> p (h c) t", t=64)

        ngr = getattr(self, "nrg_limit", S // TG)
        for gi in range(ngr):
            q0 = gi * TG
            i = gi % 2
            self.make_hT(I["x"], gi, xt[i], junk, ss[i], hb[i], pt, hT1[i], self.ident)
            b.op("pool", lambda e: e.tensor_copy(out=hTs[:, :, 0:1], in_=hTs[:, :, TG:TG + 1]), reads=[hTs], writes=[hTs])
            b.op("pool", lambda e: e.tensor_copy(out=hTs[:, :, 1:TG + 1], in_=hT1[i][:]), reads=[hT1[i]], writes=[hTs])
            ftiles = list(range(0, 16)) + [24, 25, 26, 27]
            for q4 in range(5):
                for j in range(4):
                    fc = ftiles[q4 * 4 + j]
                    for c in range(8):
                        b.op("pe", lambda e: e.matmul(pP[0:64, j * TG:(j + 1) * TG], lhsT=W1[:, c, fc * 64:(fc + 1) * 64], rhs=hTs[:, c, 1:TG + 1], start=(c == 0), stop=False),
                             reads=[W1, hTs], writes=[pP])
                    for c in range(8):
                        b.op("pe", lambda e: e.matmul(pP[0:64, j * TG:(j + 1) * TG], lhsT=W2[:, c, fc * 64:(fc + 1) * 64], rhs=hTs[:, c, 0:TG], start=False, stop=(c == 7)),
                             reads=[W2, hTs], writes=[pP])
                b.op("act", lambda e: e.copy(out=XL[:, q4 * 4:(q4 + 1) * 4, :].rearrange("p a t -> p (a t)"), in_=pP[0:64, :]), reads=[pP], writes=[XL])
            for c_ in range(NCH):
                for c in range(8):
                    b.op("pe", lambda e: e.matmul(pP[0:64, :], lhsT=hTs[:, c, 1 + c_ * 64:1 + (c_ + 1) * 64], rhs=W1[:, c, 1024:1536], start=(c == 0), stop=False),
                         reads=[W1, hTs], writes=[pP])
                for c in range(8):
                    b.op("pe", lambda e: e.matmul(pP[0:64, :], lhsT=hTs[:, c, c_ * 64:(c_ + 1) * 64], rhs=W2[:, c, 1024:1536], start=False, stop=(c == 7)),
                         reads=[W2, hTs], writes=[pP])
                b.op("act", lambda e: e.copy(out=Vtm[:, c_, :], in_=pP[0:64, :]), reads=[pP], writes=[Vtm])
            R_ = XL[:, 0:8, :]
            K_ = XL[:, 8:16, :]
            b.op("act", lambda e: e.activation(out=XL[:, 16, :], in_=XL[:, 16, :], func=AF.Tanh), reads=[XL], writes=[XL])
            b.op("act", lambda e: e.activation(out=XL[:, 18:20, :], in_=XL[:, 18:20, :], func=AF.Sigmoid), reads=[XL], writes=[XL])
            for (ws_, src, bias_, dst) in [(w2s, 16, w0, "LW"), (a2s, 17, a0, "AS")]:
                for half in range(2):
                    for j in range(4):
                        h = half * 4 + j
                        b.op("pe", lambda e: e.matmul(pP[0:64, j * TG:(j + 1) * TG], lhsT=ws_[:, h * 64:(h + 1) * 64], rhs=XL[:, src, :], start=True, stop=True),
                             reads=[ws_, XL], writes=[pP])
                    for j in range(4):
                        h = half * 4 + j
                        b.op("act", lambda e: e.activation(out=T[dst][:, h, :], in_=pP[0:64, j * TG:(j + 1) * TG], func=AF.Sigmoid, bias=bias_[:, h:h + 1]),
                             reads=[pP, bias_], writes=[T[dst]])
            b.op("pool", lambda e: e.tensor_scalar_mul(out=f2(T["LW"]), in0=f2(T["LW"]), scalar1=-0.6065306597126334), reads=[T["LW"]], writes=[T["LW"]])
            tt("dve", T["KK"][:], K_, bc(k_k), ALU.mult, [XL, k_k], [T["KK"]])
            tt("pool", T["NR"][:], T["KK"][:], T["KK"][:], ALU.mult, [T["KK"]], [T["NR"]])
            for half in range(2):
                b.op("pe", lambda e: e.matmul(pP[0:64, :], lhsT=ones[:], rhs=T["NR"][:, half * 4:(half + 1) * 4, :].rearrange("p h t -> p (h t)"), start=True, stop=True),
                     reads=[ones, T["NR"]], writes=[pP])
                b.op("act", lambda e: e.activation(out=T["KKN"][:, half * 4:(half + 1) * 4, :].rearrange("p h t -> p (h t)"), in_=pP[0:64, :], func=AF.Sqrt),
                     reads=[pP], writes=[T["KKN"]])
            b.op("dve", lambda e: e.tensor_scalar_max(out=f2(T["KKN"]), in0=f2(T["KKN"]), scalar1=1e-12), reads=[T["KKN"]], writes=[T["KKN"]])
            b.op("dve", lambda e: e.reciprocal(out=f2(T["KKN"]), in_=f2(T["KKN"])), reads=[T["KKN"]], writes=[T["KKN"]])
            tt("dve", T["KKN"][:], T["KKN"][:], T["KK"][:], ALU.mult, [T["KKN"], T["KK"]], [T["KKN"]])
            tt("pool", T["BVc"][:], T["KKN"][:], T["AS"][:], ALU.mult, [T["KKN"], T["AS"]], [T["BVc"]])
            b.op("pool", lambda e: e.tensor_scalar_add(out=f2(T["T1"]), in0=f2(T["AS"]), scalar1=-1.0), reads=[T["AS"]], writes=[T["T1"]])
            tt("pool", T["T1"][:], T["T1"][:], bc(k_a), ALU.mult, [T["T1"], k_a], [T["T1"]])
            b.op("dve", lambda e: e.scalar_tensor_tensor(out=f2(T["KP"]), in0=f2(T["T1"]), scalar=1.0, in1=K_.rearrange("p h t -> p (h t)"), op0=ALU.add, op1=ALU.mult),
                 reads=[T["T1"], XL], writes=[T["KP"]])
            tt("pool", T["RK"][:], R_, T["KP"][:], ALU.mult, [XL, T["KP"]], [T["RK"]])
            tt("pool", T["RK"][:], T["RK"][:], bc(r_k), ALU.mult, [T["RK"], r_k], [T["RK"]])
            for c_ in range(NCH):
                for h in range(8):
                    b.op("pe", lambda e: e.matmul(pD[0:64, c_ * 8 + h:c_ * 8 + h + 1], lhsT=T["RK"][:, h, c_ * 64:(c_ + 1) * 64], rhs=ones[:, 0:1], start=True, stop=True),
                         reads=[T["RK"], ones], writes=[pD])
            b.op("act", lambda e: e.copy(out=BON[:], in_=pD[0:64, 0:NCH * 8]), reads=[pD], writes=[BON])
            b.op("dve", lambda e: e.tensor_tensor_scan(out=f2(T["L"]), data0=rstm[:], data1=f2(T["LW"]), initial=0.0, op0=ALU.mult, op1=ALU.add),
                 reads=[rstm, T["LW"]], writes=[T["L"]])
            b.op("act", lambda e: e.activation(out=f2(T["EP"]), in_=f2(T["L"]), func=AF.Exp), reads=[T["L"]], writes=[T["EP"]])
            b.op("act", lambda e: e.activation(out=f2(T["EM"]), in_=f2(T["L"]), func=AF.Exp, scale=-1.0), reads=[T["L"]], writes=[T["EM"]])
            tt("pool", T["L"][:], T["L"][:], T["LW"][:], ALU.subtract, [T["L"], T["LW"]], [T["L"]])
            b.op("act", lambda e: e.activation(out=f2(T["EX"]), in_=f2(T["L"]), func=AF.Exp), reads=[T["L"]], writes=[T["EX"]])
            ar0 = AR[:, :, :, 0, :].rearrange("p h c t -> p (h c) t")
            ar1 = AR[:, :, :, 1, :].rearrange("p h c t -> p (h c) t")
            b.op("dve", lambda e: e.scalar_tensor_tensor(out=ar0, in0=c16(T["KKN"]), scalar=-1.0, in1=c16(T["EX"]), op0=ALU.mult, op1=ALU.mult),
                 reads=[T["KKN"], T["EX"]], writes=[AR])
            tt("pool", ar1, R_.rearrange("p h (c t) -> p (h c) t", t=64), c16(T["EP"]), ALU.mult, [XL, T["EP"]], [AR])
            tt("dve", T["BT"][:], T["BVc"][:], T["EM"][:], ALU.mult, [T["BVc"], T["EM"]], [T["BT"]])
            tt("pool", T["KT"][:], T["KP"][:], T["EM"][:], ALU.mult, [T["KP"], T["EM"]], [T["KT"]])
            gC = c16(T["EP"])[:, :, 63:64].to_broadcast([64, 16, 64])
            tt("dve", c16(T["BG"]), c16(T["BT"]), gC, ALU.mult, [T["BT"], T["EP"]], [T["BG"]])
            tt("pool", c16(T["KG"]), c16(T["KT"]), gC, ALU.mult, [T["KT"], T["EP"]], [T["KG"]])
            for c_ in range(NCH):
                cs = slice(c_ * 64, (c_ + 1) * 64)
                cur = (gi * NCH + c_) % 2
                for hb_ in range(2):
                    heads = list(range(hb_ * 4, hb_ * 4 + 4))
                    for j, h in enumerate(heads):
                        b.op("pe", lambda e: e.transpose(out=pC[0:64, j * 128:j * 128 + 64], in_=T["BG"][:, h, cs], identity=idf[0:64, 0:64]), reads=[T["BG"], idf], writes=[pC])
                        b.op("pe", lambda e: e.transpose(out=pC[0:64, j * 128 + 64:(j + 1) * 128], in_=T["KG"][:, h, cs], identity=idf[0:64, 0:64]), reads=[T["KG"], idf], writes=[pC])
                    tm = nxt("tm", TM4)
                    b.op("act", lambda e: e.copy(out=tm[:].rearrange("p h a t -> p (h a t)"), in_=pC[0:64, 0:512]), reads=[pC], writes=[tm])
                    for j, h in enumerate(heads):
                        arc = AR[:, h, c_, :, :].rearrange("p a t -> p (a t)")
                        b.op("pe", lambda e: e.matmul(pA[0:64, j * 256:j * 256 + 128], lhsT=T["BT"][:, h, cs], rhs=arc, start=True, stop=True), reads=[T["BT"], AR], writes=[pA])
                        b.op("pe", lambda e: e.matmul(pA[0:64, j * 256 + 128:(j + 1) * 256], lhsT=T["KT"][:, h, cs], rhs=arc, start=True, stop=True), reads=[T["KT"], AR], writes=[pA])
                        b.op("pe", lambda e: e.matmul(pB[0:64, j * 64:(j + 1) * 64], lhsT=AR[:, h, c_, 0, :], rhs=T["BT"][:, h, cs], start=True, stop=True), reads=[T["BT"], AR], writes=[pB])
                    xm = nxt("xm", XM4)
                    tt("dve", xm[:].rearrange("p h (a m) t -> p (h a) m t", a=2), pA[0:64, :].rearrange("p (ha m t) -> p ha m t", m=2, t=64),
                       msk[:, None, 0:2, :].to_broadcast([64, 8, 2, 64]), ALU.mult, [pA, msk], [xm])
                    aa = nxt("aa", AA4)
                    b.op("pool", lambda e: e.tensor_copy(out=aa[:, :, 0, :], in_=xm[:, :, 0, :]), reads=[xm], writes=[aa])
                    tt("dve", aa[:, :, 1, :], pB[0:64, 0:256].rearrange("p (h t) -> p h t", t=64), msk[:, 2:3, :].to_broadcast([64, 4, 64]), ALU.mult, [pB, msk], [aa])
                    P_ = nxt("pp4", PP4)
                    tt("pool", P_[:], xm[:, :, 0, :], idf[0:64, None, 0:64].to_broadcast([64, 4, 64]), ALU.add, [xm, idf], [P_])
                    for step in range(5):
                        for j in range(4):
                            b.op("pe", lambda e: e.matmul(pD[0:64, j * 128:j * 128 + 64], lhsT=aa[:, j, 1, :], rhs=aa[:, j, 0, :], start=True, stop=True), reads=[aa], writes=[pD])
                            b.op("pe", lambda e: e.matmul(pD[0:64, j * 128 + 64:(j + 1) * 128], lhsT=aa[:, j, 0, :], rhs=aa[:, j, 1, :], start=True, stop=True), reads=[aa], writes=[pD])
                        aa2 = nxt("aa", AA4)
                        b.op("act", lambda e: e.copy(out=aa2[:].rearrange("p h a t -> p (h a t)"), in_=pD[0:64, :]), reads=[pD], writes=[aa2])
                        for j in range(4):
                            b.op("pe", lambda e: e.matmul(pB[0:64, 256 + j * 64:256 + (j + 1) * 64], lhsT=aa2[:, j, 1, :], rhs=P_[:, j, :], start=True, stop=True), reads=[aa2, P_], writes=[pB])
                        P2 = nxt("pp4", PP4)
                        tt("dve", P2[:], pB[0:64, 256:512].rearrange("p (h t) -> p h t", t=64), P_[:], ALU.add, [pB, P_], [P2])
                        aa, P_ = aa2, P2
                    for j, h in enumerate(heads):
                        b.op("pe", lambda e: e.matmul(pZ[0:64, j * 64:(j + 1) * 64], lhsT=xm[:, j, 2, :], rhs=Vtm[:, c_, h * 64:(h + 1) * 64], start=True, stop=False), reads=[xm, Vtm], writes=[pZ])
                        b.op("pe", lambda e: e.matmul(pZ[0:64, j * 64:(j + 1) * 64], lhsT=AR[:, h, c_, 0, :], rhs=Hst[:, cur, h, :], start=False, stop=True), reads=[AR, Hst], writes=[pZ])
                    b.op("act", lambda e: e.copy(out=Xs4[:].rearrange("p h t -> p (h t)"), in_=pZ[0:64, 0:256]), reads=[pZ], writes=[Xs4])
                    for j in range(4):
                        b.op("pe", lambda e: e.matmul(pZ[0:64, 256 + j * 64:256 + (j + 1) * 64], lhsT=P_[:, j, :], rhs=Xs4[:, j, :], start=True, stop=True), reads=[P_, Xs4], writes=[pZ])
                    b.op("act", lambda e: e.copy(out=Us4[:].rearrange("p h t -> p (h t)"), in_=pZ[0:64, 256:512]), reads=[pZ], writes=[Us4])
                    for j, h in enumerate(heads):
                        o = slice(j * 64, (j + 1) * 64)
                        vh = Vtm[:, c_, h * 64:(h + 1) * 64]
                        b.op("pe", lambda e: e.matmul(pZ[0:64, o], lhsT=AR[:, h, c_, 1, :], rhs=Hst[:, cur, h, :], start=True, stop=False), reads=[AR, Hst], writes=[pZ])
                        b.op("pe", lambda e: e.matmul(pZ[0:64, o], lhsT=xm[:, j, 1, :], rhs=Us4[:, j, :], start=False, stop=False), reads=[xm, Us4], writes=[pZ])
                        b.op("pe", lambda e: e.matmul(pZ[0:64, o], lhsT=xm[:, j, 3, :], rhs=vh, start=False, stop=True), reads=[xm, Vtm], writes=[pZ])
                    for j, h in enumerate(heads):
                        o = slice(256 + j * 64, 256 + (j + 1) * 64)
                        vh = Vtm[:, c_, h * 64:(h + 1) * 64]
                        b.op("pe", lambda e: e.matmul(pZ[0:64, o], lhsT=tm[:, j, 0, :], rhs=Us4[:, j, :], start=True, stop=False), reads=[tm, Us4], writes=[pZ])
                        b.op("pe", lambda e: e.matmul(pZ[0:64, o], lhsT=tm[:, j, 1, :], rhs=vh, start=False, stop=True), reads=[tm, Vtm], writes=[pZ])
                    b.op("act", lambda e: e.copy(out=Ytm[:, c_, hb_ * 4:(hb_ + 1) * 4, :].rearrange("p h t -> p (h t)"), in_=pZ[0:64, 0:256]), reads=[pZ], writes=[Ytm])
                    gH = T["EP"][:, hb_ * 4:(hb_ + 1) * 4, c_ * 64 + 63:c_ * 64 + 64].to_broadcast([64, 4, 64])
                    tt("pool", Ht4[:], Hst[:, cur, hb_ * 4:(hb_ + 1) * 4, :], gH, ALU.mult, [Hst, T["EP"]], [Ht4])
                    tt("dve", Hst[:, 1 - cur, hb_ * 4:(hb_ + 1) * 4, :], pZ[0:64, 256:512].rearrange("p (h t) -> p h t", t=64), Ht4[:], ALU.add, [pZ, Ht4], [Hst])
            Y3 = Ytm[:].rearrange("p c h i -> p (c h) i")
            S3 = sqv[:].rearrange("p c h i -> p (c h) i")
            b.op("dve", lambda e: e.tensor_reduce(out=st1[:], in_=Y3, axis=AX.X, op=ALU.add), reads=[Ytm], writes=[st1])
            b.op("pool", lambda e: e.tensor_scalar_mul(out=st1[:], in0=st1[:], scalar1=1.0 / 64), reads=[st1], writes=[st1])
            tt("dve", Y3, Y3, st1[:].unsqueeze(2).to_broadcast([64, NCH * 8, 64]), ALU.subtract, [Ytm, st1], [Ytm])
            tt("pool", S3, Y3, Y3, ALU.mult, [Ytm], [sqv])
            b.op("dve", lambda e: e.tensor_reduce(out=st2[:], in_=S3, axis=AX.X, op=ALU.add), reads=[sqv], writes=[st2])
            b.op("act", lambda e: e.activation(out=st2[:], in_=st2[:], func=AF.Sqrt, scale=1.0 / 64, bias=64e-5), reads=[st2], writes=[st2])
            b.op("dve", lambda e: e.reciprocal(out=st2[:], in_=st2[:]), reads=[st2], writes=[st2])
            tt("dve", Y3, Y3, st2[:].unsqueeze(2).to_broadcast([64, NCH * 8, 64]), ALU.mult, [Ytm, st2], [Ytm])
            lg = lng[:].rearrange("p (h i) -> p h i", i=64)[:, None, :, :].to_broadcast([64, NCH, 8, 64])
            lb = lnb[:].rearrange("p (h i) -> p h i", i=64)[:, None, :, :].to_broadcast([64, NCH, 8, 64])
            tt("pool", Ytm[:], Ytm[:], lg, ALU.mult, [Ytm, lng], [Ytm])
            tt("dve", Ytm[:], Ytm[:], lb, ALU.add, [Ytm, lnb], [Ytm])
            V3 = Vtm[:].rearrange("p c (h i) -> p (c h) i", i=64)
            tt("pool", S3, V3, BON[:].unsqueeze(2).to_broadcast([64, NCH * 8, 64]), ALU.mult, [Vtm, BON], [sqv])
            tt("dve", Y3, Y3, S3, ALU.add, [Ytm, sqv], [Ytm])
            for c_ in range(NCH):
                for two in range(2):
                    b.op("pe", lambda e: e.matmul(pP[0:64, :], lhsT=XL[:, 18 + two, c_ * 64:(c_ + 1) * 64], rhs=g2s[:, two, :], start=(two == 0), stop=(two == 1)),
                         reads=[XL, g2s], writes=[pP])
                tt("dve", OBb[:, c_, :], Ytm[:, c_, :, :].rearrange("p h i -> p (h i)"), pP[0:64, :], ALU.mult, [Ytm, pP], [OBb])
                for k4 in range(4):
                    b.op("pe", lambda e: e.transpose(out=pt[:, k4, c_ * 64:(c_ + 1) * 64], in_=OBb[:, c_, k4 * 128:(k4 + 1) * 128], identity=self.ident[0:64, 0:64]),
                         reads=[OBb, self.ident], writes=[pt])
            ot = obT[gi % 2]
            b.op("act", lambda e: e.copy(out=ot[:], in_=pt[:, 0:4, :]), reads=[pt], writes=[ot])
            b.dma("pool", self.obT_d[:, :, q0:q0 + TG].rearrange("c p t -> p c t"), ot[:], reads=[ot], writes=[self.obT_d])
        if "rwkv" in self.debug:
            d = self.dbg_out("obT", [4, 128, S], BF16)
            b.dma("pool", d, self.obT_d[:], reads=[self.obT_d])


Prog.phase_rwkv2 = _phase_rwkv2


def _phase_rwkv3(self):
    b = self.b
    I = self.inp
    TG = 128
    NCH = 2
    CHDT = mybir.dt.float32r if getattr(self, "use_f32r", True) else F32
    tt = lambda eng, out, in0, in1, op, rd, wr: b.op(eng, lambda e: e.tensor_tensor(out=out, in0=in0, in1=in1, op=op), reads=rd, writes=wr)
    with b.scope():
        W1 = b.sb("W1", [128, 8, 1792], BF16)
        with b.scope():
            gat = self.load_gain("gat3", I["attn_norm_g"][0])
            stage = [b.sb(f"rst{i}", [128, 1792], F32) for i in range(2)]
            self.load_weight(W1, I["w_in"][0][:, RW0:RW0 + 1792], 1792, gvec=gat, stage=stage)

        def colvec(name, src, n):
            t = b.sb(name, [64, n], F32)
            b.dma("sp", t[:], src.rearrange("(c p) -> p c", p=64), writes=[t], allow_slow_non_contiguous=True)
            return t
        mu = colvec("mu", I["rwkv_mu"][0], 28)
        w0 = colvec("w0", I["rwkv_w0"][0], 8)
        a0 = colvec("a0", I["rwkv_a0"][0], 8)
        k_k = colvec("k_k", I["rwkv_k_k"][0], 8)
        k_a = colvec("k_a", I["rwkv_k_a"][0], 8)
        r_k = colvec("r_k", I["rwkv_r_k"][0].rearrange("h d -> (h d)"), 8)
        w2s = b.sb("w2s", [64, 512], F32)
        a2s = b.sb("a2s", [64, 512], F32)
        g2s = b.sb("g2s", [64, 2, 512], F32)
        b.dma("sp", w2s[:], I["rwkv_w2"][0], writes=[w2s])
        b.dma("sp", a2s[:], I["rwkv_a2"][0], writes=[a2s])
        b.dma("sp", g2s[:], I["rwkv_g2"][0].rearrange("(two l) f -> l two f", two=2), writes=[g2s])
        lng = b.sb("lng", [64, 512], F32)
        lnb = b.sb("lnb", [64, 512], F32)
        b.dma("sp", lng[:], I["rwkv_ln_g"][0].partition_broadcast(64), writes=[lng])
        b.dma("sp", lnb[:], I["rwkv_ln_b"][0].partition_broadcast(64), writes=[lnb])
        msk = b.sb("rmsk", [64, 3, 64], F32)
        b.dma("sp", msk[:], I["rwmask"], writes=[msk])
        rstm = b.sb("rstm", [64, 8 * TG], F32)
        b.dma("sp", rstm[:], I["rwreset"], writes=[rstm])
        ones = b.sb("ones64", [64, 64], F32)
        b.op("pool", lambda e: e.memset(ones[:], 1.0), writes=[ones])
        idf = self.identf
        Hst = b.sb("rH", [64, 2, 8, 64], CHDT)
        b.op("pool", lambda e: e.memset(Hst[:].bitcast(F32), 0.0), writes=[Hst])
        xt = [b.sb(f"rxt{i}", [128, D], F32) for i in range(1)] * 2
        junk = b.sb("rjunk", [128, D], BF16)
        ss = [b.sb(f"rss{i}", [128, 1], F32) for i in range(1)] * 2
        hb = [b.sb(f"rhb{i}", [128, D], BF16) for i in range(1)] * 2
        hT1 = [b.sb(f"rhT{i}", [128, 8, 128], BF16) for i in range(1)] * 2
        PB = b.sb("rPB", [64, 28, TG + 1], F32)
        b.op("pool", lambda e: e.memset(PB[:], 0.0), writes=[PB])
        XL = b.sb("rXL", [64, 28, TG], F32)
        Vtm = b.sb("rVtm", [64, NCH, 512], CHDT)
        names = ["LW", "AS", "KKN", "BVc", "KP", "RK", "L", "EP", "EM", "BG", "KG"]
        T = {n: b.sb("r" + n, [64, 8, TG], F32) for n in names}
        T["NR"] = T["RK"]
        T["T1"] = T["BG"]
        T["KK"] = T["KG"]
        T["EX"] = T["L"]
        T["BT"] = b.sb("rBTr", [64, 8, TG], CHDT)
        T["KT"] = b.sb("rKTr", [64, 8, TG], CHDT)
        AR = b.sb("rAR", [64, 8, NCH, 2, 64], CHDT)
        BON = b.sb("rBON", [64, NCH * 8], F32)
        Ytm = b.sb("rYtm", [64, NCH, 8, 64], F32)
        sqv = b.sb("rsqv", [64, NCH, 8, 64], F32)
        st1 = b.sb("rst1", [64, NCH * 8], F32)
        st2 = b.sb("rst2", [64, NCH * 8], F32)
        TM4 = [b.sb(f"rTM{i}", [64, 4, 2, 64], CHDT) for i in range(2)]
        XM4 = [b.sb(f"rXM{i}", [64, 4, 4, 64], CHDT) for i in range(2)]
        AA4 = [[b.sb(f"rAA{u}_{i}", [64, 4, 2, 64], CHDT) for i in range(2)] for u in range(2)]
        PP4 = [[b.sb(f"rPP{u}_{i}", [64, 4, 64], CHDT) for i in range(2)] for u in range(2)]
        Xs8 = b.sb("rXs8", [64, 8, 64], CHDT)
        Us8 = b.sb("rUs8", [64, 8, 64], CHDT)
        Ht8 = b.sb("rHt8", [64, 8, 64], F32)
        OBb = b.sb("rOBb", [64, NCH, 512], BF16)
        obT = [b.sb(f"robT{i}", [128, 4, TG], BF16) for i in range(1)] * 2
        pt = b.ps("rpt", [128, 8, 128], BF16)
        pP = b.ps("rpP", [128, 512], F32)
        pA = b.ps("rpA", [128, 1024], F32)
        pB = b.ps("rpB", [128, 512], F32)
        pC = b.ps("rpC", [128, 512], F32)
        pD = b.ps("rpD", [128, 512], F32)
        pZ = b.ps("rpZ", [128, 512], F32)
        cnt = {}

        def nxt(k, lst):
            cnt[k] = cnt.get(k, 0) + 1
            return lst[cnt[k] % len(lst)]
        bc = lambda v: v[:].unsqueeze(2).to_broadcast([64, 8, TG])
        f2 = lambda t_: t_[:].rearrange("p h t -> p (h t)")
        c16 = lambda t_: t_[:].rearrange("p h (c t) -> p (h c) t", t=64)

        ngr = getattr(self, "nrg_limit", S // TG)
        RR = lambda ap: ap

        def emit_inproj_head(gi):
            i = gi % 2
            self.make_hT(I["x"], gi, xt[i], junk, ss[i], hb[i], pt, hT1[i], self.ident)
            b.op("dve", lambda e: e.tensor_copy(out=PB[:, :, 0:1], in_=PB[:, :, TG:TG + 1]), reads=[PB], writes=[PB])

        def emit_inproj_rounds(gi, rounds):
            i = gi % 2
            for r7 in rounds:
                for j in range(4):
                    fc = r7 * 4 + j
                    for c in range(8):
                        b.op("pe", lambda e: e.matmul(pP[0:64, j * TG:(j + 1) * TG], lhsT=W1[:, c, fc * 64:(fc + 1) * 64], rhs=hT1[i][:, c, :], start=(c == 0), stop=(c == 7)),
                             reads=[W1, hT1[i]], writes=[pP])
                b.op("act", lambda e: e.copy(out=PB[:, r7 * 4:(r7 + 1) * 4, 1:TG + 1], in_=pP[0:64, :].rearrange("p (a t) -> p a t", t=TG)), reads=[pP], writes=[PB])

        emit_inproj_head(0)
        emit_inproj_rounds(0, range(7))
        for gi in range(ngr):
            q0 = gi * TG
            tt("dve", XL[:], PB[:, :, 0:TG], PB[:, :, 1:TG + 1], ALU.subtract, [PB], [XL])
            tt("dve", XL[:], XL[:], mu[:].unsqueeze(2).to_broadcast([64, 28, TG]), ALU.mult, [XL, mu], [XL])
            tt("dve", XL[:], XL[:], PB[:, :, 1:TG + 1], ALU.add, [XL, PB], [XL])
            if gi + 1 < ngr:
                emit_inproj_head(gi + 1)
            for c_ in range(NCH):
                for h in range(8):
                    b.op("pe", lambda e: e.transpose(out=pC[0:64, h * 64:(h + 1) * 64], in_=XL[:, 16 + h, c_ * 64:(c_ + 1) * 64], identity=idf[0:64, 0:64]), reads=[XL, idf], writes=[pC])
                b.op("act", lambda e: e.copy(out=Vtm[:, c_, :], in_=pC[0:64, :]), reads=[pC], writes=[Vtm])
            R_ = XL[:, 0:8, :]
            K_ = XL[:, 8:16, :]
            b.op("act", lambda e: e.activation(out=XL[:, 24, :], in_=XL[:, 24, :], func=AF.Tanh), reads=[XL], writes=[XL])
            b.op("act", lambda e: e.activation(out=XL[:, 26:28, :], in_=XL[:, 26:28, :], func=AF.Sigmoid), reads=[XL], writes=[XL])
            for (ws_, src, bias_, dst) in [(w2s, 24, w0, "LW"), (a2s, 25, a0, "AS")]:
                for half in range(2):
                    for j in range(4):
                        h = half * 4 + j
                        b.op("pe", lambda e: e.matmul(pP[0:64, j * TG:(j + 1) * TG], lhsT=ws_[:, h * 64:(h + 1) * 64], rhs=XL[:, src, :], start=True, stop=True),
                             reads=[ws_, XL], writes=[pP])
                    for j in range(4):
                        h = half * 4 + j
                        b.op("act", lambda e: e.activation(out=T[dst][:, h, :], in_=pP[0:64, j * TG:(j + 1) * TG], func=AF.Sigmoid, bias=bias_[:, h:h + 1]),
                             reads=[pP, bias_], writes=[T[dst]])
            b.op("dve", lambda e: e.tensor_scalar_mul(out=f2(T["LW"]), in0=f2(T["LW"]), scalar1=-0.6065306597126334), reads=[T["LW"]], writes=[T["LW"]])
            tt("dve", T["KK"][:], K_, bc(k_k), ALU.mult, [XL, k_k], [T["KK"]])
            tt("dve", T["NR"][:], T["KK"][:], T["KK"][:], ALU.mult, [T["KK"]], [T["NR"]])
            for half in range(2):
                b.op("pe", lambda e: e.matmul(pP[0:64, :], lhsT=ones[:], rhs=T["NR"][:, half * 4:(half + 1) * 4, :].rearrange("p h t -> p (h t)"), start=True, stop=True),
                     reads=[ones, T["NR"]], writes=[pP])
                b.op("act", lambda e: e.activation(out=T["KKN"][:, half * 4:(half + 1) * 4, :].rearrange("p h t -> p (h t)"), in_=pP[0:64, :], func=AF.Sqrt),
                     reads=[pP], writes=[T["KKN"]])
            if gi + 1 < ngr:
                emit_inproj_rounds(gi + 1, range(0, 4))
            b.op("dve", lambda e: e.tensor_scalar_max(out=f2(T["KKN"]), in0=f2(T["KKN"]), scalar1=1e-12), reads=[T["KKN"]], writes=[T["KKN"]])
            b.op("dve", lambda e: e.reciprocal(out=f2(T["KKN"]), in_=f2(T["KKN"])), reads=[T["KKN"]], writes=[T["KKN"]])
            tt("dve", T["KKN"][:], T["KKN"][:], T["KK"][:], ALU.mult, [T["KKN"], T["KK"]], [T["KKN"]])
            tt("dve", T["BVc"][:], T["KKN"][:], T["AS"][:], ALU.mult, [T["KKN"], T["AS"]], [T["BVc"]])
            b.op("dve", lambda e: e.tensor_scalar_add(out=f2(T["T1"]), in0=f2(T["AS"]), scalar1=-1.0), reads=[T["AS"]], writes=[T["T1"]])
            tt("dve", T["T1"][:], T["T1"][:], bc(k_a), ALU.mult, [T["T1"], k_a], [T["T1"]])
            b.op("dve", lambda e: e.scalar_tensor_tensor(out=f2(T["KP"]), in0=f2(T["T1"]), scalar=1.0, in1=K_.rearrange("p h t -> p (h t)"), op0=ALU.add, op1=ALU.mult),
                 reads=[T["T1"], XL], writes=[T["KP"]])
            tt("dve", T["RK"][:], R_, T["KP"][:], ALU.mult, [XL, T["KP"]], [T["RK"]])
            tt("dve", T["RK"][:], T["RK"][:], bc(r_k), ALU.mult, [T["RK"], r_k], [T["RK"]])
            for c_ in range(NCH):
                for h in range(8):
                    b.op("pe", lambda e: e.matmul(pD[0:64, c_ * 8 + h:c_ * 8 + h + 1], lhsT=T["RK"][:, h, c_ * 64:(c_ + 1) * 64], rhs=ones[:, 0:1], start=True, stop=True),
                         reads=[T["RK"], ones], writes=[pD])
            b.op("act", lambda e: e.copy(out=BON[:], in_=pD[0:64, 0:NCH * 8]), reads=[pD], writes=[BON])
            if gi + 1 < ngr:
                emit_inproj_rounds(gi + 1, range(4, 7))
            b.op("dve", lambda e: e.tensor_tensor_scan(out=f2(T["L"]), data0=rstm[:], data1=f2(T["LW"]), initial=0.0, op0=ALU.mult, op1=ALU.add),
                 reads=[rstm, T["LW"]], writes=[T["L"]])
            b.op("act", lambda e: e.activation(out=f2(T["EP"]), in_=f2(T["L"]), func=AF.Exp), reads=[T["L"]], writes=[T["EP"]])
            b.op("act", lambda e: e.activation(out=f2(T["EM"]), in_=f2(T["L"]), func=AF.Exp, scale=-1.0), reads=[T["L"]], writes=[T["EM"]])
            tt("dve", T["L"][:], T["L"][:], T["LW"][:], ALU.subtract, [T["L"], T["LW"]], [T["L"]])
            b.op("act", lambda e: e.activation(out=f2(T["EX"]), in_=f2(T["L"]), func=AF.Exp), reads=[T["L"]], writes=[T["EX"]])
            ar0 = AR[:, :, :, 0, :].rearrange("p h c t -> p (h c) t")
            ar1 = AR[:, :, :, 1, :].rearrange("p h c t -> p (h c) t")
            b.op("dve", lambda e: e.scalar_tensor_tensor(out=ar0, in0=c16(T["KKN"]), scalar=-1.0, in1=c16(T["EX"]), op0=ALU.mult, op1=ALU.mult),
                 reads=[T["KKN"], T["EX"]], writes=[AR])
            tt("dve", ar1, R_.rearrange("p h (c t) -> p (h c) t", t=64), c16(T["EP"]), ALU.mult, [XL, T["EP"]], [AR])
            tt("dve", T["BT"][:], T["BVc"][:], T["EM"][:], ALU.mult, [T["BVc"], T["EM"]], [T["BT"]])
            tt("dve", T["KT"][:], T["KP"][:], T["EM"][:], ALU.mult, [T["KP"], T["EM"]], [T["KT"]])
            gC = c16(T["EP"])[:, :, 63:64].to_broadcast([64, 16, 64])
            tt("dve", c16(T["BG"]), c16(T["BT"]), gC, ALU.mult, [T["BT"], T["EP"]], [T["BG"]])
            tt("dve", c16(T["KG"]), c16(T["KT"]), gC, ALU.mult, [T["KT"], T["EP"]], [T["KG"]])
            for c_ in range(NCH):
                cs = slice(c_ * 64, (c_ + 1) * 64)
                cur = (gi * NCH + c_) % 2
                U_ = []
                for u in range(2):
                    heads = list(range(u * 4, u * 4 + 4))
                    pBu = pB if u == 0 else pC
                    for j, h in enumerate(heads):
                        b.op("pe", lambda e: e.transpose(out=pZ[0:64, j * 128:j * 128 + 64], in_=T["BG"][:, h, cs], identity=idf[0:64, 0:64]), reads=[T["BG"], idf], writes=[pZ])
                        b.op("pe", lambda e: e.transpose(out=pZ[0:64, j * 128 + 64:(j + 1) * 128], in_=T["KG"][:, h, cs], identity=idf[0:64, 0:64]), reads=[T["KG"], idf], writes=[pZ])
                    tm = TM4[u]
                    b.op("act", lambda e: e.copy(out=tm[:].rearrange("p h a t -> p (h a t)"), in_=pZ[0:64, 0:512]), reads=[pZ], writes=[tm])
                    for j, h in enumerate(heads):
                        arc = AR[:, h, c_, :, :].rearrange("p a t -> p (a t)")
                        b.op("pe", lambda e: e.matmul(pA[0:64, j * 256:j * 256 + 128], lhsT=T["BT"][:, h, cs], rhs=arc, start=True, stop=True), reads=[T["BT"], AR], writes=[pA])
                        b.op("pe", lambda e: e.matmul(pA[0:64, j * 256 + 128:(j + 1) * 256], lhsT=T["KT"][:, h, cs], rhs=arc, start=True, stop=True), reads=[T["KT"], AR], writes=[pA])
                        b.op("pe", lambda e: e.matmul(pBu[0:64, j * 64:(j + 1) * 64], lhsT=AR[:, h, c_, 0, :], rhs=T["BT"][:, h, cs], start=True, stop=True), reads=[T["BT"], AR], writes=[pBu])
                    xm = XM4[u]
                    tt("dve", xm[:].rearrange("p h (a m) t -> p (h a) m t", a=2), pA[0:64, :].rearrange("p (ha m t) -> p ha m t", m=2, t=64),
                       msk[:, None, 0:2, :].to_broadcast([64, 8, 2, 64]), ALU.mult, [pA, msk], [xm])
                    aa = AA4[u][0]
                    b.op("dve", lambda e: e.tensor_copy(out=aa[:, :, 0, :], in_=xm[:, :, 0, :]), reads=[xm], writes=[aa])
                    tt("dve", aa[:, :, 1, :], pBu[0:64, 0:256].rearrange("p (h t) -> p h t", t=64), msk[:, 2:3, :].to_broadcast([64, 4, 64]), ALU.mult, [pBu, msk], [aa])
                    P_ = PP4[u][0]
                    tt("dve", P_[:], xm[:, :, 0, :], idf[0:64, None, 0:64].to_broadcast([64, 4, 64]), ALU.add, [xm, idf], [P_])
                    U_.append(dict(tm=tm, xm=xm, aa=aa, P=P_, pB=pBu, pD=(pD if u == 0 else pP), k=0))
                for step in range(5):
                    for u_ in U_:
                        aa, pDu = u_["aa"], u_["pD"]
                        for j in range(4):
                            b.op("pe", lambda e: e.matmul(pDu[0:64, j * 128:j * 128 + 64], lhsT=RR(aa[:, j, 1, :]), rhs=RR(aa[:, j, 0, :]), start=True, stop=True), reads=[aa], writes=[pDu])
                            b.op("pe", lambda e: e.matmul(pDu[0:64, j * 128 + 64:(j + 1) * 128], lhsT=RR(aa[:, j, 0, :]), rhs=RR(aa[:, j, 1, :]), start=True, stop=True), reads=[aa], writes=[pDu])
                    for ui, u_ in enumerate(U_):
                        u_["k"] += 1
                        aa2 = AA4[ui][u_["k"] % 2]
                        b.op("act", lambda e: e.copy(out=aa2[:].rearrange("p h a t -> p (h a t)"), in_=u_["pD"][0:64, :]), reads=[u_["pD"]], writes=[aa2])
                        u_["aa"] = aa2
                    for u_ in U_:
                        for j in range(4):
                            b.op("pe", lambda e: e.matmul(u_["pB"][0:64, 256 + j * 64:256 + (j + 1) * 64], lhsT=RR(u_["aa"][:, j, 1, :]), rhs=RR(u_["P"][:, j, :]), start=True, stop=True),
                                 reads=[u_["aa"], u_["P"]], writes=[u_["pB"]])
                    for ui, u_ in enumerate(U_):
                        P2 = PP4[ui][u_["k"] % 2]
                        tt("dve", P2[:], u_["pB"][0:64, 256:512].rearrange("p (h t) -> p h t", t=64), u_["P"][:], ALU.add, [u_["pB"], u_["P"]], [P2])
                        u_["P"] = P2
                for h in range(8):
                    u_, j = U_[h // 4], h % 4
                    o = slice(h * 64, (h + 1) * 64)
                    b.op("pe", lambda e: e.matmul(pA[0:64, o], lhsT=u_["xm"][:, j, 2, :], rhs=Vtm[:, c_, o], start=True, stop=False), reads=[u_["xm"], Vtm], writes=[pA])
                    b.op("pe", lambda e: e.matmul(pA[0:64, o], lhsT=AR[:, h, c_, 0, :], rhs=Hst[:, cur, h, :], start=False, stop=True), reads=[AR, Hst], writes=[pA])
                b.op("act", lambda e: e.copy(out=Xs8[:].rearrange("p h t -> p (h t)"), in_=pA[0:64, 0:512]), reads=[pA], writes=[Xs8])
                for h in range(8):
                    u_, j = U_[h // 4], h % 4
                    b.op("pe", lambda e: e.matmul(pA[0:64, 512 + h * 64:512 + (h + 1) * 64], lhsT=u_["P"][:, j, :], rhs=Xs8[:, h, :], start=True, stop=True), reads=[u_["P"], Xs8], writes=[pA])
                b.op("act", lambda e: e.copy(out=Us8[:].rearrange("p h t -> p (h t)"), in_=pA[0:64, 512:1024]), reads=[pA], writes=[Us8])
                for h in range(8):
                    u_, j = U_[h // 4], h % 4
                    o = slice(h * 64, (h + 1) * 64)
                    b.op("pe", lambda e: e.matmul(pD[0:64, o], lhsT=u_["tm"][:, j, 0, :], rhs=Us8[:, h, :], start=True, stop=False), reads=[u_["tm"], Us8], writes=[pD])
                    b.op("pe", lambda e: e.matmul(pD[0:64, o], lhsT=u_["tm"][:, j, 1, :], rhs=Vtm[:, c_, o], start=False, stop=True), reads=[u_["tm"], Vtm], writes=[pD])
                tt("dve", Ht8[:], Hst[:, cur, :, :], T["EP"][:, :, c_ * 64 + 63:c_ * 64 + 64].to_broadcast([64, 8, 64]), ALU.mult, [Hst, T["EP"]], [Ht8])
                tt("dve", Hst[:, 1 - cur, :, :], pD[0:64, :].rearrange("p (h t) -> p h t", t=64), Ht8[:], ALU.add, [pD, Ht8], [Hst])
                for h in range(8):
                    u_, j = U_[h // 4], h % 4
                    o = slice(h * 64, (h + 1) * 64)
                    b.op("pe", lambda e: e.matmul(pZ[0:64, o], lhsT=AR[:, h, c_, 1, :], rhs=Hst[:, cur, h, :], start=True, stop=False), reads=[AR, Hst], writes=[pZ])
                    b.op("pe", lambda e: e.matmul(pZ[0:64, o], lhsT=u_["xm"][:, j, 1, :], rhs=Us8[:, h, :], start=False, stop=False), reads=[u_["xm"], Us8], writes=[pZ])
                    b.op("pe", lambda e: e.matmul(pZ[0:64, o], lhsT=u_["xm"][:, j, 3, :], rhs=Vtm[:, c_, o], start=False, stop=True), reads=[u_["xm"], Vtm], writes=[pZ])
                b.op("act", lambda e: e.copy(out=Ytm[:, c_, :, :].rearrange("p h t -> p (h t)"), in_=pZ[0:64, :]), reads=[pZ], writes=[Ytm])
            Y3 = Ytm[:].rearrange("p c h i -> p (c h) i")
            S3 = sqv[:].rearrange("p c h i -> p (c h) i")
            b.op("dve", lambda e: e.tensor_reduce(out=st1[:], in_=Y3, axis=AX.X, op=ALU.add), reads=[Ytm], writes=[st1])
            b.op("dve", lambda e: e.tensor_scalar_mul(out=st1[:], in0=st1[:], scalar1=1.0 / 64), reads=[st1], writes=[st1])
            tt("dve", Y3, Y3, st1[:].unsqueeze(2).to_broadcast([64, NCH * 8, 64]), ALU.subtract, [Ytm, st1], [Ytm])
            tt("dve", S3, Y3, Y3, ALU.mult, [Ytm], [sqv])
            b.op("dve", lambda e: e.tensor_reduce(out=st2[:], in_=S3, axis=AX.X, op=ALU.add), reads=[sqv], writes=[st2])
            b.op("act", lambda e: e.activation(out=st2[:], in_=st2[:], func=AF.Sqrt, scale=1.0 / 64, bias=64e-5), reads=[st2], writes=[st2])
            b.op("dve", lambda e: e.reciprocal(out=st2[:], in_=st2[:]), reads=[st2], writes=[st2])
            tt("dve", Y3, Y3, st2[:].unsqueeze(2).to_broadcast([64, NCH * 8, 64]), ALU.mult, [Ytm, st2], [Ytm])
            lg = lng[:].rearrange("p (h i) -> p h i", i=64)[:, None, :, :].to_broadcast([64, NCH, 8, 64])
            lb = lnb[:].rearrange("p (h i) -> p h i", i=64)[:, None, :, :].to_broadcast([64, NCH, 8, 64])
            tt("dve", Ytm[:], Ytm[:], lg, ALU.mult, [Ytm, lng], [Ytm])
            tt("dve", Ytm[:], Ytm[:], lb, ALU.add, [Ytm, lnb], [Ytm])
            V3 = Vtm[:].rearrange("p c (h i) -> p (c h) i", i=64)
            tt("dve", S3, V3, BON[:].unsqueeze(2).to_broadcast([64, NCH * 8, 64]), ALU.mult, [Vtm, BON], [sqv])
            tt("dve", Y3, Y3, S3, ALU.add, [Ytm, sqv], [Ytm])
            for c_ in range(NCH):
                for two in range(2):
                    b.op("pe", lambda e: e.matmul(pP[0:64, :], lhsT=XL[:, 26 + two, c_ * 64:(c_ + 1) * 64], rhs=g2s[:, two, :], start=(two == 0), stop=(two == 1)),
                         reads=[XL, g2s], writes=[pP])
                tt("dve", OBb[:, c_, :], Ytm[:, c_, :, :].rearrange("p h i -> p (h i)"), pP[0:64, :], ALU.mult, [Ytm, pP], [OBb])
                for k4 in range(4):
                    b.op("pe", lambda e: e.transpose(out=pt[:, k4, c_ * 64:(c_ + 1) * 64], in_=OBb[:, c_, k4 * 128:(k4 + 1) * 128], identity=self.ident[0:64, 0:64]),
                         reads=[OBb, self.ident], writes=[pt])
            ot = obT[gi % 2]
            b.op("act", lambda e: e.copy(out=ot[:], in_=pt[:, 0:4, :]), reads=[pt], writes=[ot])
            b.dma("pool", self.obT_d[:, :, q0:q0 + TG].rearrange("c p t -> p c t"), ot[:], reads=[ot], writes=[self.obT_d])
        if "rwkv" in self.debug:
            d = self.dbg_out("obT", [4, 128, S], BF16)
            b.dma("pool", d, self.obT_d[:], reads=[self.obT_d])


Prog.phase_rwkv3 = _phase_rwkv3


def _phase_ffn2(self):
    b = self.b
    I = self.inp
    TG = 256
    NFT = 44
    with b.scope():
        gf = self.load_gain("gf", I["ffn_norm_g"][0])
        stage = [b.sb(f"fst{i}", [128, 1024], F32) for i in range(2)]
        wu = b.sb("wu", [128, 8, 2 * DFF], BF16)
        for n in range(8):
            for c in range(8):
                st = stage[c % 2]
                b.dma("sp", st[:, 0:704], I["w_up"][0][c * 128:(c + 1) * 128, n * 704:(n + 1) * 704], writes=[st])
                b.op("act", lambda e: e.activation(out=wu[:, c, n * 704:(n + 1) * 704], in_=st[:, 0:704], func=AF.Copy, scale=gf[:, c:c + 1]),
                     reads=[st, gf], writes=[wu])
        wd = b.sb("wd", [128, 22, D], BF16)
        self.load_weight(wd, I["w_down"][0], D, kch=22, stage=stage, eng="dve")
        cw = b.sb("cw", [128, 3, NFT], F32)
        for j in range(3):
            b.dma("sp", cw[:, j, :], I["conv_w"][0][j].rearrange("(c p) -> p c", p=128), writes=[cw], allow_slow_non_contiguous=True)
        cbias = self.load_gain("cbias", I["conv_b"][0], kch=NFT)
        xt = [b.sb(f"fxt{i}", [128, D], F32) for i in range(2)]
        junk = b.sb("fjunk", [128, D], BF16)
        ss = [b.sb(f"fss{i}", [128, 1], F32) for i in range(2)]
        hb = [b.sb(f"fhb{i}", [128, D], BF16) for i in range(2)]
        hTg = b.sb("fhTg", [128, 8, TG + 2], BF16)
        b.op("pool", lambda e: e.memset(hTg[:], 0.0), writes=[hTg])
        cv = [b.sb(f"cv{i}", [128, TG], F32) for i in range(3)]
        sgl = [b.sb(f"sgl{i}", [128, TG], BF16) for i in range(2)]
        actT = b.sb("actT", [128, 22, TG], BF16)
        val = b.sb("fval", [128, 22, TG], BF16)
        pt = b.ps("fpt", [128, 8, 128], BF16)
        pu = [b.ps(f"fpu{i}", [128, 512], F32) for i in range(4)]
        pd = [b.ps(f"fpd{i}", [128, 512], F32) for i in range(2)]
        ng = getattr(self, "nt_limit", NT) * 128 // TG
        for gi in range(ng):
            b.op("pool", lambda e: e.tensor_copy(out=hTg[:, :, 0:2], in_=hTg[:, :, TG:TG + 2]), reads=[hTg], writes=[hTg])
            for s_ in range(TG // 128):
                t = gi * (TG // 128) + s_
                self.make_hT(self.x1_d, t, xt[s_], junk, ss[s_], hb[s_], pt, hTg, self.ident, hT_ap=hTg[:, :, 2 + s_ * 128:2 + (s_ + 1) * 128])
            for ft in range(NFT):
                p = pu[ft % 4]
                c_ = cv[ft % 3]
                for c in range(8):
                    b.op("pe", lambda e: e.matmul(p[:, 0:TG + 2], lhsT=wu[:, c, ft * 128:(ft + 1) * 128], rhs=hTg[:, c, :], start=(c == 0), stop=(c == 7)),
                         reads=[wu, hTg], writes=[p])
                b.op("act", lambda e: e.activation(out=c_[:], in_=p[:, 0:TG], func=AF.Identity, scale=cw[:, 0, ft:ft + 1], bias=cbias[:, ft:ft + 1]),
                     reads=[p, cw, cbias], writes=[c_])
                b.op("dve", lambda e: e.scalar_tensor_tensor(out=c_[:], in0=p[:, 1:TG + 1], scalar=cw[:, 1, ft:ft + 1], in1=c_[:], op0=ALU.mult, op1=ALU.add),
                     reads=[p, cw, c_], writes=[c_])
                if ft < 22:
                    b.op("dve", lambda e: e.scalar_tensor_tensor(out=val[:, ft, :], in0=p[:, 2:TG + 2], scalar=cw[:, 2, ft:ft + 1], in1=c_[:], op0=ALU.mult, op1=ALU.add),
                         reads=[p, cw, c_], writes=[val])
                else:
                    sg_ = sgl[ft % 2]
                    b.op("dve", lambda e: e.scalar_tensor_tensor(out=c_[:], in0=p[:, 2:TG + 2], scalar=cw[:, 2, ft:ft + 1], in1=c_[:], op0=ALU.mult, op1=ALU.add),
                         reads=[p, cw, c_], writes=[c_])
                    b.op("act", lambda e: e.activation(out=sg_[:], in_=c_[:], func=AF.Silu), reads=[c_], writes=[sg_])
                    b.op("pool", lambda e: e.tensor_tensor(out=actT[:, ft - 22, :], in0=sg_[:], in1=val[:, ft - 22, :], op=ALU.mult),
                         reads=[sg_, val], writes=[actT])
            for s_ in range(TG // 128):
                t = gi * (TG // 128) + s_
                for n in range(2):
                    for f in range(22):
                        b.op("pe", lambda e: e.matmul(pd[n][:, :], lhsT=actT[:, f, s_ * 128:(s_ + 1) * 128], rhs=wd[:, f, n * 512:(n + 1) * 512], start=(f == 0), stop=(f == 21)),
                             reads=[actT, wd], writes=[pd[n]])
                    b.op("dve", lambda e: e.tensor_tensor(out=xt[s_][:, n * 512:(n + 1) * 512], in0=pd[n][:, :], in1=xt[s_][:, n * 512:(n + 1) * 512], op=ALU.add),
                         reads=[pd[n], xt[s_]], writes=[xt[s_]])
                b.dma("pool", self.out[t * 128:(t + 1) * 128, :], xt[s_][:], reads=[xt[s_]])


Prog.phase_ffn2 = _phase_ffn2
```

```python
import contextlib
import numpy as np
import ml_dtypes
import concourse.bass as bass
import concourse.mybir as mybir
from concourse.bass_utils import run_bass_kernel_spmd

F32 = mybir.dt.float32
BF16 = mybir.dt.bfloat16
AF = mybir.ActivationFunctionType
ALU = mybir.AluOpType
AX = mybir.AxisListType

S = 4096
D = 1024
NT = S // 128
IN_WIDTH = 5144
RW0 = 1304
GA0 = 3096
GB0 = 4120
DFF = 2816
RMS_EPS = 1e-6


class Buf:
    def __init__(self, t, name):
        self.t = t
        self.name = name
        self.w = None
        self.r = {}
        self.psum = False

    def __getitem__(self, idx):
        return self.t[idx]


class Builder:
    SEM_ROLL = 30000

    def __init__(self, nc):
        self.nc = nc
        self.stack = contextlib.ExitStack()
        self.root = self.stack
        self.eng = {"pe": nc.tensor, "act": nc.scalar, "dve": nc.vector,
                    "pool": nc.gpsimd, "sp": nc.sync}
        self.sem = {}
        self.cnt = {}
        self.seen = {e: {} for e in self.eng}
        self.nsem = 0
        self.lanes = {}
        self.lane_rr = {}
        self.last_tok = {}
        for e in self.eng:
            self._roll(e)

    def newsem(self, name):
        self.nsem += 1
        return self.root.enter_context(self.nc.semaphore(f"{name}_{self.nsem}"))

    def sb(self, name, shape, dt=F32):
        self.nsem += 1
        name = f"sb{self.nsem}_{name}"
        return Buf(self.stack.enter_context(self.nc.sbuf_tensor(name, list(shape), dt)), name)

    def ps(self, name, shape, dt=F32):
        self.nsem += 1
        name = f"ps{self.nsem}_{name}"
        bf = Buf(self.stack.enter_context(self.nc.psum_tensor(name, list(shape), dt)), name)
        bf.psum = True
        return bf

    def dram(self, name, shape, dt=F32, kind="Internal"):
        return Buf(self.nc.dram_tensor(name, list(shape), dt, kind=kind), name)

    def _roll(self, e):
        self.sem[e] = self.newsem("s" + e)
        self.cnt[e] = 0

    def _wait(self, e, tok):
        sem, val = tok
        k = id(sem)
        if self.seen[e].get(k, 0) < val:
            self.eng[e].wait_ge(sem, val)
            self.seen[e][k] = val

    def _deps(self, e, reads, writes):
        for b in reads:
            if b.w is not None:
                we, tok = b.w
                self._wait(e, tok)
            if b.psum:
                for re_, tok in b.r.items():
                    if re_ != e:
                        self._wait(e, tok)
        for b in writes:
            if b.w is not None:
                we, tok = b.w
                if we != e:
                    self._wait(e, tok)
            for re_, tok in b.r.items():
                if re_ != e:
                    self._wait(e, tok)

    def op(self, e, fn, reads=(), writes=()):
        if self.cnt[e] >= self.SEM_ROLL:
            self._roll(e)
        self._deps(e, reads, writes)
        ins = fn(self.eng[e])
        self.cnt[e] += 1
        tok = (self.sem[e], self.cnt[e])
        ins.then_inc(self.sem[e], 1)
        self.last_tok[e] = tok
        for b in reads:
            b.r[e] = tok
        for b in writes:
            b.w = (e, tok)
            b.r = {}
        return tok

    def dma(self, q, out, in_, reads=(), writes=(), nlanes=6, **kw):
        if q not in self.lanes:
            self.lanes[q] = [[self.newsem("l" + q), 0] for _ in range(nlanes)]
            self.lane_rr[q] = 0
        li = self.lane_rr[q]
        self.lane_rr[q] = (li + 1) % len(self.lanes[q])
        lane = self.lanes[q][li]
        if lane[1] >= 1800:
            self._wait(q, (lane[0], 16 * lane[1]))
            lane[0] = self.newsem("l" + q)
            lane[1] = 0
        if lane[1] > 0:
            self._wait(q, (lane[0], 16 * lane[1]))
        self._deps_dma(q, reads, writes)
        ins = self.eng[q].dma_start(out=out, in_=in_, **kw)
        lane[1] += 1
        tok = (lane[0], 16 * lane[1])
        ins.then_inc(lane[0], 16)
        key = "dma_" + q + str(li)
        for b in reads:
            b.r[key] = tok
        for b in writes:
            b.w = (key, tok)
            b.r = {}
        return tok

    def _deps_dma(self, q, reads, writes):
        for b in reads:
            if b.w is not None:
                self._wait(q, b.w[1])
        for b in writes:
            if b.w is not None:
                self._wait(q, b.w[1])
            for re_, tok in b.r.items():
                self._wait(q, tok)

    def barrier(self):
        toks = list(self.last_tok.values())
        for q, lanes in self.lanes.items():
            for lane in lanes:
                if lane[1] > 0:
                    toks.append((lane[0], 16 * lane[1]))
        for e in self.eng:
            for tok in toks:
                self._wait(e, tok)

    def wait_all_on(self, e):
        toks = list(self.last_tok.values())
        for q, lanes in self.lanes.items():
            for lane in lanes:
                if lane[1] > 0:
                    toks.append((lane[0], 16 * lane[1]))
        for tok in toks:
            self._wait(e, tok)

    @contextlib.contextmanager
    def scope(self):
        old = self.stack
        self.stack = contextlib.ExitStack()
        try:
            yield
            self.barrier()
        finally:
            self.stack.close()
            self.stack = old

    def close(self):
        self.stack.close()


NEG = -30000.0


def _bucket(dist):
    n = np.maximum(dist, 0)
    ratio = np.log(np.maximum(n, 1).astype(np.float32) / np.float32(16.0)) / np.float32(np.log(8.0))
    large = np.minimum(16 + (ratio * 16).astype(np.int32), 31)
    return np.where(n < 16, n, large)


def host_consts(rel_bias):
    rel = np.asarray(rel_bias, np.float32)
    c = {}
    c["ident"] = np.eye(128, dtype=np.float32).astype(ml_dtypes.bfloat16)
    c["identf"] = np.eye(128, dtype=np.float32)
    kp = np.arange(128)[:, None]
    cc = np.arange(640)[None, :]
    dist = cc - kp
    bt = rel[_bucket(dist)]
    tw = np.where(((dist >= 0) & (dist < 512))[..., None], bt, np.float32(NEG))
    ts = np.where((dist >= 0)[..., None], bt, np.float32(NEG))
    c["tw"] = np.ascontiguousarray(tw.transpose(0, 2, 1)).astype(np.float32)
    c["ts"] = np.ascontiguousarray(ts.transpose(0, 2, 1)).astype(np.float32)
    cidx = np.arange(256)[:, None]
    qidx = np.arange(S)[None, :]
    dc = qidx - 16 * cidx - 31
    bcg = rel[_bucket(dc)]
    ok = (dc >= 0) & (cidx < 255)
    bc = np.where(ok[..., None], bcg, np.float32(NEG))
    c["biasc"] = np.ascontiguousarray(bc.transpose(2, 0, 1)).reshape(8, 2, 128, S).astype(np.float32)
    A = np.zeros((256, 64), np.float32)
    Wt = (1, 2, 2, 2, 1)
    for ci in range(255):
        for j in range(64):
            o = ci + 1 - 4 * j
            if 0 <= o <= 4:
                A[ci, j] = Wt[o]
    c["amat"] = A.reshape(2, 128, 64)
    E = np.zeros((64, S), np.float32)
    E[np.arange(S) // 64, np.arange(S)] = 1.0
    c["emat"] = E.astype(ml_dtypes.bfloat16)
    qp = np.arange(128)[:, None, None]
    qt = np.arange(32)[None, :, None]
    j = np.arange(64)[None, None, :]
    cur = (128 * qt + qp) // 64
    cand = (j >= 1) & (j <= cur - 2)
    c["candneg"] = np.where(cand, 0.0, -1e9).astype(np.float32)
    c["fz"] = ((j == 0) | (j == cur) | (j == cur - 1)).astype(np.float32)
    tri = np.triu(np.ones((64, 64), np.float32))
    c["rwmask"] = np.ascontiguousarray(np.stack([np.triu(np.ones((64, 64), np.float32), 1), tri, np.tril(np.ones((64, 64), np.float32), -1)], axis=1))
    rr = np.ones((64, 1024), np.float32)
    rr[:, ::64] = 0.0
    c["rwreset"] = rr
    c["b31"] = np.ascontiguousarray(np.broadcast_to(rel[31][None, :], (128, 8))).astype(np.float32)
    return c


CONST_SPECS = {
    "ident": ([128, 128], BF16), "identf": ([128, 128], F32),
    "tw": ([128, 8, 640], F32), "ts": ([128, 8, 640], F32),
    "biasc": ([8, 2, 128, S], F32), "amat": ([2, 128, 64], F32),
    "emat": ([64, S], BF16), "candneg": ([128, 32, 64], F32), "fz": ([128, 32, 64], F32),
    "b31": ([128, 8], F32), "rwmask": ([64, 3, 64], F32), "rwreset": ([64, 1024], F32),
}

W_SPECS = {
    "x": [S, D], "attn_norm_g": [1, D], "w_in": [1, D, IN_WIDTH], "q_norm_g": [1, 64], "k_norm_g": [1, 64],
    "cmp_pe_k": [1, 32, 64], "cmp_w1_k": [1, 2048, 256], "cmp_w2_k": [1, 256, 64],
    "cmp_pe_v": [1, 32, 64], "cmp_w1_v": [1, 2048, 256], "cmp_w2_v": [1, 256, 64],
    "rwkv_mu": [1, 1792], "rwkv_w0": [1, 512], "rwkv_w2": [1, 64, 512], "rwkv_a0": [1, 512],
    "rwkv_a2": [1, 64, 512], "rwkv_g2": [1, 128, 512], "rwkv_k_k": [1, 512], "rwkv_k_a": [1, 512],
    "rwkv_r_k": [1, 8, 64], "rwkv_ln_g": [1, 512], "rwkv_ln_b": [1, 512],
    "w_proj_a": [1, 512, D], "w_proj_b": [1, 512, D], "w_out": [1, D, D], "ffn_norm_g": [1, D],
    "w_up": [1, D, 2 * DFF], "conv_w": [1, 3, 2 * DFF], "conv_b": [1, 2 * DFF], "w_down": [1, DFF, D],
}


class Prog:
    def __init__(self, debug=()):
        self.debug = set(debug)
        nc = bass.Bass("TRN2", target_bir_lowering=False)
        self.nc = nc
        self.inp = {}
        for k, shp in W_SPECS.items():
            self.inp[k] = nc.dram_tensor(k, list(shp), F32, kind="ExternalInput").ap()
        for k, (shp, dt) in CONST_SPECS.items():
            self.inp[k] = nc.dram_tensor(k, list(shp), dt, kind="ExternalInput").ap()
        self.out = nc.dram_tensor("out", [S, D], F32, kind="ExternalOutput").ap()
        self.dbg = {}
        self.b = Builder(nc)

    def dbg_out(self, name, shape, dt=F32):
        t = self.nc.dram_tensor("dbg_" + name, list(shape), dt, kind="ExternalOutput").ap()
        self.dbg[name] = t
        return t

    def load_weight(self, dst, src, ncols, gvec=None, kch=8, stage=None, eng="act"):
        b = self.b
        for c in range(kch):
            st = stage[c % len(stage)]
            b.dma("sp", st[:, :ncols], src[c * 128:(c + 1) * 128, :], writes=[st])
            if gvec is not None:
                b.op(eng, lambda e: e.activation(out=dst[:, c, :], in_=st[:, :ncols], func=AF.Copy, scale=gvec[:, c:c + 1])
                     if eng == "act" else e.tensor_scalar_mul(out=dst[:, c, :], in0=st[:, :ncols], scalar1=gvec[:, c:c + 1]),
                     reads=[st, gvec], writes=[dst])
            else:
                b.op(eng, lambda e: e.copy(out=dst[:, c, :], in_=st[:, :ncols]) if eng == "act"
                     else e.tensor_copy(out=dst[:, c, :], in_=st[:, :ncols]), reads=[st], writes=[dst])

    def load_gain(self, name, src_vec, kch=8):
        b = self.b
        g = b.sb(name, [128, kch], F32)
        b.dma("sp", g[:], src_vec.rearrange("(c p) -> p c", p=128), writes=[g], allow_slow_non_contiguous=True)
        return g

    def bcast_row(self, name, src_row, n):
        b = self.b
        t = b.sb(name, [128, n], F32)
        b.dma("sp", t[:], src_row.partition_broadcast(128), writes=[t])
        return t

    def make_hT(self, x_ap, t, xt, junk, ss, hb, pt, hT, ident, hT_ap=None):
        b = self.b
        b.dma("sp", xt[:], x_ap[t * 128:(t + 1) * 128, :], writes=[xt])
        b.op("act", lambda e: e.activation(out=junk[:], in_=xt[:], func=AF.Square, accum_out=ss[:]), reads=[xt], writes=[junk, ss])
        b.op("act", lambda e: e.activation(out=ss[:], in_=ss[:], func=AF.Sqrt, scale=1.0 / D, bias=RMS_EPS), reads=[ss], writes=[ss])
        b.op("dve", lambda e: e.reciprocal(out=ss[:], in_=ss[:]), reads=[ss], writes=[ss])
        b.op("dve", lambda e: e.tensor_scalar_mul(out=hb[:], in0=xt[:], scalar1=ss[:]), reads=[xt, ss], writes=[hb])
        for c in range(8):
            b.op("pe", lambda e: e.transpose(out=pt[:, c, :], in_=hb[:, c * 128:(c + 1) * 128], identity=ident[:]),
                 reads=[hb, ident], writes=[pt])
        b.op("act", lambda e: e.copy(out=(hT[:] if hT_ap is None else hT_ap), in_=pt[:]), reads=[pt], writes=[hT])

    def alloc_root(self):
        b = self.b
        I = self.inp
        self.ident = b.sb("ident", [128, 128], BF16)
        b.dma("sp", self.ident[:], I["ident"], writes=[self.ident])
        self.identf = b.sb("identf", [128, 128], F32)
        b.dma("sp", self.identf[:], I["identf"], writes=[self.identf])

    def alloc_persistent(self):
        b = self.b
        I = self.inp
        if not hasattr(self, "ident"):
            self.alloc_root()
        self.ksE = b.sb("ksE", [128, 2, S], BF16)
        self.kwT = b.sb("kwT", [64, 2, S], BF16)
        self.vaug_s = b.sb("vaug_s", [128, NT, 2, 65], BF16)
        self.vaug_w = b.sb("vaug_w", [128, NT, 2, 65], BF16)
        self.gts = b.sb("gts", [128, NT, 24], F32)
        self.kcT = b.sb("kcT", [64, 2, 256], BF16)
        self.vcA = b.sb("vcA", [128, 2, 2, 129], F32)
        self.qT_d = b.dram("qT_d", [8, 64, S], BF16)
        self.oaT_d = b.dram("oaT_d", [4, 128, S], BF16)
        self.obT_d = b.dram("obT_d", [4, 128, S], BF16)
        for g in range(2):
            b.dma("sp", self.ksE[64:128, g, :], I["emat"], writes=[self.ksE])
        b.op("pool", lambda e: e.memset(self.vaug_s[:, :, :, 64:65], 1.0), writes=[self.vaug_s])
        b.op("pool", lambda e: e.memset(self.vaug_w[:, :, :, 64:65], 1.0), writes=[self.vaug_w])
        b.op("pool", lambda e: e.memset(self.vcA[:, :, :, 64:65], 1.0), writes=[self.vcA])
        for g in range(2):
            for ct in range(2):
                b.dma("sp", self.vcA[:, g, ct, 65:129], I["amat"][ct], writes=[self.vcA])

    def phase_nsa_proj(self):
        b = self.b
        I = self.inp
        with b.scope():
            gat = self.load_gain("gat", I["attn_norm_g"][0])
            wn = b.sb("wn", [128, 8, RW0], BF16)
            stage = [b.sb(f"wst{i}", [128, RW0], F32) for i in range(2)]
            self.load_weight(wn, I["w_in"][0][:, 0:RW0], RW0, gvec=gat, stage=stage)
            gq = self.bcast_row("gq", I["q_norm_g"][0], 64)
            gk = self.bcast_row("gk", I["k_norm_g"][0], 64)
            gq_rep = b.sb("gq_rep", [128, 8, 64], F32)
            gk_rep = b.sb("gk_rep", [128, 2, 64], F32)
            b.op("act", lambda e: e.activation(out=gq_rep[:], in_=gq[:, None, :].to_broadcast([128, 8, 64]), func=AF.Copy, scale=0.125),
                 reads=[gq], writes=[gq_rep])
            b.op("act", lambda e: e.activation(out=gk_rep[:], in_=gk[:, None, :].to_broadcast([128, 2, 64]), func=AF.Copy, scale=1.0),
                 reads=[gk], writes=[gk_rep])
            if getattr(self, 'stop_at', 99) <= 0:
                return
            kcdup = b.sb("kcdup", [128, 2, S + 1], BF16)
            vcdup = b.sb("vcdup", [128, 2, S + 1], BF16)
            xt = [b.sb(f"xt{i}", [128, D], F32) for i in range(2)]
            junk = b.sb("junk", [128, D], BF16)
            ss = [b.sb(f"ss{i}", [128, 1], F32) for i in range(2)]
            hb = [b.sb(f"hb{i}", [128, D], BF16) for i in range(2)]
            hT = [b.sb(f"hT{i}", [128, 8, 128], BF16) for i in range(2)]
            sq_ = [b.sb(f"sq{i}", [128, 12, 64], F32) for i in range(2)]
            ssq_ = [b.sb(f"ssq{i}", [128, 12], F32) for i in range(2)]
            tmpq_ = [b.sb(f"tmpq{i}", [128, 8, 64], F32) for i in range(2)]
            tmpk_ = [b.sb(f"tmpk{i}", [128, 4, 64], F32) for i in range(2)]
            qb_ = [b.sb(f"qb{i}", [128, 512], BF16) for i in range(2)]
            kb_ = [b.sb(f"kb{i}", [128, 4, 64], BF16) for i in range(2)]
            cb_ = [b.sb(f"cb{i}", [128, 4, 2, 64], BF16) for i in range(2)]
            qst = [b.sb(f"qst{i}", [64, 8, 128], BF16) for i in range(2)]
            pt = b.ps("pt", [128, 8, 128], BF16)
            pm = [b.ps(f"pm{i}", [128, 512], F32) for i in range(3)]
            ptq_ = [b.ps(f"ptq{i}", [128, 8, 128], BF16) for i in range(2)]
            ptk_ = [b.ps(f"ptk{i}", [128, 8, 128], BF16) for i in range(2)]
            colgroups = [(0, 512), (512, 1024), (1024, RW0)]
            pmS = [[b.sb(f"pmS{i}_{n}", [128, 512], F32) for n in range(3)] for i in range(2)]
            ntl = getattr(self, 'nt_limit', NT)

            def stageA(t):
                    i = t % 2
                    self.make_hT(I["x"], t, xt[i], junk, ss[i], hb[i], pt, hT[i], self.ident)
                    sq, ssq, tmpq, tmpk, qb, kb, cb, ptq, ptk = sq_[i], ssq_[i], tmpq_[i], tmpk_[i], qb_[i], kb_[i], cb_[i], ptq_[i], ptk_[i]
                    for n, (c0, c1) in enumerate(colgroups):
                        for c in range(8):
                            b.op("pe", lambda e: e.matmul(pm[n][:, :c1 - c0], lhsT=hT[i][:, c, :], rhs=wn[:, c, c0:c1],
                                                          start=(c == 0), stop=(c == 7)), reads=[hT[i], wn], writes=[pm[n]])

            def stageA2(t):
                    i = t % 2
                    b.op("act", lambda e: e.copy(out=pmS[i][0][:], in_=pm[0][:]), reads=[pm[0]], writes=[pmS[i][0]])
                    b.op("dve", lambda e: e.tensor_copy(out=pmS[i][1][:], in_=pm[1][:]), reads=[pm[1]], writes=[pmS[i][1]])
                    b.op("act", lambda e: e.copy(out=pmS[i][2][:, 0:RW0 - 1024], in_=pm[2][:, 0:RW0 - 1024]), reads=[pm[2]], writes=[pmS[i][2]])

            def stageB(t):
                    i = t % 2
                    sq, ssq, tmpq, tmpk, qb, kb, cb, ptq, ptk = sq_[i], ssq_[i], tmpq_[i], tmpk_[i], qb_[i], kb_[i], cb_[i], ptq_[i], ptk_[i]
                    b.op("act", lambda e: e.activation(out=sq[:, 0:8, :], in_=pmS[i][0][:, 0:512].rearrange("p (h d) -> p h d", d=64), func=AF.Square),
                         reads=[pmS[i][0]], writes=[sq])
                    b.op("act", lambda e: e.activation(out=sq[:, 8:10, :], in_=pmS[i][1][:, 256:384].rearrange("p (h d) -> p h d", d=64), func=AF.Square),
                         reads=[pmS[i][1]], writes=[sq])
                    b.op("act", lambda e: e.activation(out=sq[:, 10:12, :], in_=pmS[i][2][:, 0:128].rearrange("p (h d) -> p h d", d=64), func=AF.Square),
                         reads=[pmS[i][2]], writes=[sq])
                    b.op("dve", lambda e: e.tensor_reduce(out=ssq[:], in_=sq[:], axis=AX.X, op=ALU.add), reads=[sq], writes=[ssq])
                    b.op("act", lambda e: e.activation(out=ssq[:], in_=ssq[:], func=AF.Sqrt, scale=1.0 / 64, bias=RMS_EPS), reads=[ssq], writes=[ssq])
                    b.op("dve", lambda e: e.reciprocal(out=ssq[:], in_=ssq[:]), reads=[ssq], writes=[ssq])
                    if getattr(self, 'stop_at', 99) <= 2:
                        return
                    b.op("dve", lambda e: e.tensor_tensor(out=tmpq[:], in0=pmS[i][0][:, 0:512].rearrange("p (h d) -> p h d", d=64),
                                                          in1=ssq[:, 0:8].unsqueeze(2).to_broadcast([128, 8, 64]), op=ALU.mult),
                         reads=[pmS[i][0], ssq], writes=[tmpq])
                    b.op("pool", lambda e: e.tensor_tensor(out=qb[:].rearrange("p (h d) -> p h d", d=64), in0=tmpq[:], in1=gq_rep[:], op=ALU.mult),
                         reads=[tmpq, gq_rep], writes=[qb])
                    for h in range(8):
                        b.op("pe", lambda e: e.transpose(out=ptq[0:64, h, :], in_=qb[:, h * 64:(h + 1) * 64], identity=self.ident[:]),
                             reads=[qb, self.ident], writes=[ptq])
                    b.op("act", lambda e: e.copy(out=qst[i][:], in_=ptq[0:64, :, :]), reads=[ptq], writes=[qst[i]])
                    b.dma("pool", self.qT_d[:, :, t * 128:(t + 1) * 128].rearrange("h d t -> d h t"), qst[i][:], reads=[qst[i]], writes=[self.qT_d])
                    if getattr(self, 'stop_at', 99) <= 3:
                        return
                    b.op("dve", lambda e: e.tensor_tensor(out=tmpk[:, 0:2, :], in0=pmS[i][1][:, 256:384].rearrange("p (h d) -> p h d", d=64),
                                                          in1=ssq[:, 8:10].unsqueeze(2).to_broadcast([128, 2, 64]), op=ALU.mult),
                         reads=[pmS[i][1], ssq], writes=[tmpk])
                    b.op("dve", lambda e: e.tensor_tensor(out=tmpk[:, 2:4, :], in0=pmS[i][2][:, 0:128].rearrange("p (h d) -> p h d", d=64),
                                                          in1=ssq[:, 10:12].unsqueeze(2).to_broadcast([128, 2, 64]), op=ALU.mult),
                         reads=[pmS[i][2], ssq], writes=[tmpk])
                    b.op("pool", lambda e: e.tensor_tensor(out=kb[:].rearrange("p (a g) d -> p a g d", a=2), in0=tmpk[:].rearrange("p (a g) d -> p a g d", a=2),
                                                           in1=gk_rep[:, None, :, :].to_broadcast([128, 2, 2, 64]), op=ALU.mult),
                         reads=[tmpk, gk_rep], writes=[kb])
                    for j in range(4):
                        b.op("pe", lambda e: e.transpose(out=ptk[0:64, j, :], in_=kb[:, j, :], identity=self.ident[:]),
                             reads=[kb, self.ident], writes=[ptk])
                    if getattr(self, 'stop_at', 99) <= 4:
                        return
                    for du in range(2):
                        b.op("act", lambda e: e.copy(out=cb[:, :, du, :], in_=pmS[i][1][:, 0:256].rearrange("p (a d) -> p a d", d=64)),
                             reads=[pmS[i][1]], writes=[cb])
                    for j in range(4):
                        b.op("pe", lambda e: e.transpose(out=ptk[:, 4 + j, :], in_=cb[:, j, :, :].rearrange("p a d -> p (a d)"), identity=self.ident[:]),
                             reads=[cb, self.ident], writes=[ptk])
                    c0 = t * 128
                    b.op("dve", lambda e: e.tensor_copy(out=self.ksE[0:64, :, c0:c0 + 128], in_=ptk[0:64, 0:2, :]), reads=[ptk], writes=[self.ksE])
                    b.op("dve", lambda e: e.tensor_copy(out=self.kwT[0:64, :, c0:c0 + 128], in_=ptk[0:64, 2:4, :]), reads=[ptk], writes=[self.kwT])
                    b.op("act", lambda e: e.copy(out=kcdup[0:64, :, 1 + c0:1 + c0 + 128], in_=ptk[0:64, 4:6, :]), reads=[ptk], writes=[kcdup])
                    b.op("act", lambda e: e.copy(out=kcdup[64:128, :, c0:c0 + 128], in_=ptk[64:128, 4:6, :]), reads=[ptk], writes=[kcdup])
                    b.op("dve", lambda e: e.tensor_copy(out=vcdup[0:64, :, 1 + c0:1 + c0 + 128], in_=ptk[0:64, 6:8, :]), reads=[ptk], writes=[vcdup])
                    b.op("dve", lambda e: e.tensor_copy(out=vcdup[64:128, :, c0:c0 + 128], in_=ptk[64:128, 6:8, :]), reads=[ptk], writes=[vcdup])
                    if getattr(self, 'stop_at', 99) <= 5:
                        return
                    b.op("act", lambda e: e.copy(out=self.vaug_s[:, t, :, 0:64], in_=pmS[i][1][:, 384:512].rearrange("p (g d) -> p g d", d=64)),
                         reads=[pmS[i][1]], writes=[self.vaug_s])
                    b.op("act", lambda e: e.copy(out=self.vaug_w[:, t, :, 0:64], in_=pmS[i][2][:, 128:256].rearrange("p (g d) -> p g d", d=64)),
                         reads=[pmS[i][2]], writes=[self.vaug_w])
                    b.op("act", lambda e: e.activation(out=self.gts[:, t, :], in_=pmS[i][2][:, 256:280], func=AF.Sigmoid), reads=[pmS[i][2]], writes=[self.gts])

            stageA(0)
            stageA2(0)
            for t in range(ntl):
                if t + 1 < ntl:
                    stageA(t + 1)
                stageB(t)
                if t + 1 < ntl:
                    stageA2(t + 1)
            if "nsa_proj" in self.debug:
                d = self.dbg_out("ksE", [128, 2, S], BF16)
                b.dma("pool", d, self.ksE[:], reads=[self.ksE])
                d = self.dbg_out("kcdup", [128, 2, S + 1], BF16)
                b.dma("pool", d, kcdup[:], reads=[kcdup])
                d = self.dbg_out("vaug_w", [128, NT, 2, 65], BF16)
                b.dma("pool", d, self.vaug_w[:], reads=[self.vaug_w])
                d = self.dbg_out("gts", [128, NT, 24], F32)
                b.dma("pool", d, self.gts[:], reads=[self.gts])
            if not getattr(self, 'skip_compress', False):
                self.compress(kcdup, vcdup, gk_rep, [pm[0], pm[1]], pm[2], ptk_[0])

    def compress(self, kcdup, vcdup, gk_rep, ph, po, ptc):
        b = self.b
        I = self.inp
        C2 = 2.0 * 0.7978845608028654
        w1 = b.sb("w1", [128, 16, 256], BF16)
        w2 = b.sb("w2", [128, 2, 64], BF16)
        w1st = [b.sb(f"w1st{i}", [128, 256], F32) for i in range(2)]
        peT = b.sb("peT", [128, 16], F32)
        peTb = b.sb("peTb", [128, 16], BF16)
        hTc = b.sb("hTc", [128, 2, 256], BF16)
        pbias = b.sb("pbias", [128, 2], F32)
        xh = b.sb("xh", [128, 255], F32)
        x2 = b.sb("x2", [128, 255], F32)
        sg = b.sb("sg", [128, 255], F32)
        ctmp = b.sb("ctmp", [128, 64], F32)
        csq = b.sb("csq", [128, 64], F32)
        cs1 = b.sb("cs1", [128, 1], F32)
        kcb = b.sb("kcb", [128, 64], BF16)
        b.op("pool", lambda e: e.memset(hTc[:], 0.0), writes=[hTc])
        for kv, (dup, pe_n, w1_n, w2_n) in enumerate([(kcdup, "cmp_pe_k", "cmp_w1_k", "cmp_w2_k"), (vcdup, "cmp_pe_v", "cmp_w1_v", "cmp_w2_v")]):
            self.load_weight(w1, I[w1_n][0], 256, kch=16, stage=w1st, eng="dve")
            self.load_weight(w2, I[w2_n][0], 64, kch=2, stage=w1st, eng="dve")
            for two in range(2):
                b.dma("sp", peT[two * 64:(two + 1) * 64, :], I[pe_n][0].rearrange("(pp two) d -> two d pp", two=2)[two],
                      writes=[peT], allow_slow_non_contiguous=True)
            b.op("dve", lambda e: e.tensor_copy(out=peTb[:], in_=peT[:]), reads=[peT], writes=[peTb])
            for ft in range(2):
                for pp in range(16):
                    b.op("pe", lambda e: e.matmul(po[:, ft:ft + 1], lhsT=w1[:, pp, ft * 128:(ft + 1) * 128], rhs=peTb[:, pp:pp + 1],
                                                  start=(pp == 0), stop=(pp == 15)), reads=[w1, peTb], writes=[po])
            b.op("dve", lambda e: e.tensor_copy(out=pbias[:], in_=po[:, 0:2]), reads=[po], writes=[pbias])
            for g in range(2):
                for ft in range(2):
                    p = ph[ft]
                    for pp in range(16):
                        b.op("pe", lambda e: e.matmul(p[:, 0:255], lhsT=w1[:, pp, ft * 128:(ft + 1) * 128],
                                                      rhs=dup[:, g, 1 + 2 * pp:1 + 2 * pp + 16 * 254 + 1:16],
                                                      start=(pp == 0), stop=(pp == 15)), reads=[w1, dup], writes=[p])
                    b.op("act", lambda e: e.activation(out=xh[:], in_=p[:, 0:255], func=AF.Identity, bias=pbias[:, ft:ft + 1]), reads=[p, pbias], writes=[xh])
                    b.op("dve", lambda e: e.tensor_tensor(out=x2[:], in0=xh[:], in1=xh[:], op=ALU.mult), reads=[xh], writes=[x2])
                    b.op("dve", lambda e: e.tensor_scalar(out=x2[:], in0=x2[:], scalar1=0.044715, scalar2=1.0, op0=ALU.mult, op1=ALU.add), reads=[x2], writes=[x2])
                    b.op("dve", lambda e: e.tensor_tensor(out=x2[:], in0=x2[:], in1=xh[:], op=ALU.mult), reads=[x2, xh], writes=[x2])
                    b.op("act", lambda e: e.activation(out=sg[:], in_=x2[:], func=AF.Sigmoid, scale=C2), reads=[x2], writes=[sg])
                    b.op("dve", lambda e: e.tensor_tensor(out=hTc[:, ft, 0:255], in0=xh[:], in1=sg[:], op=ALU.mult), reads=[xh, sg], writes=[hTc])
                for ct in range(2):
                    for ft in range(2):
                        b.op("pe", lambda e: e.matmul(po[:, 64:128], lhsT=hTc[:, ft, ct * 128:(ct + 1) * 128], rhs=w2[:, ft, :],
                                                      start=(ft == 0), stop=(ft == 1)), reads=[hTc, w2], writes=[po])
                    if kv == 0:
                        b.op("act", lambda e: e.activation(out=csq[:], in_=po[:, 64:128], func=AF.Square, accum_out=cs1[:]), reads=[po], writes=[csq, cs1])
                        b.op("act", lambda e: e.activation(out=cs1[:], in_=cs1[:], func=AF.Sqrt, scale=1.0 / 64, bias=RMS_EPS), reads=[cs1], writes=[cs1])
                        b.op("dve", lambda e: e.reciprocal(out=cs1[:], in_=cs1[:]), reads=[cs1], writes=[cs1])
                        b.op("dve", lambda e: e.tensor_scalar_mul(out=ctmp[:], in0=po[:, 64:128], scalar1=cs1[:]), reads=[po, cs1], writes=[ctmp])
                        b.op("dve", lambda e: e.tensor_tensor(out=kcb[:], in0=ctmp[:], in1=gk_rep[:, 0, :], op=ALU.mult), reads=[ctmp, gk_rep], writes=[kcb])
                        b.op("pe", lambda e: e.transpose(out=ptc[0:64, 0, :], in_=kcb[:], identity=self.ident[:]), reads=[kcb, self.ident], writes=[ptc])
                        b.op("dve", lambda e: e.tensor_copy(out=self.kcT[:, g, ct * 128:(ct + 1) * 128], in_=ptc[0:64, 0, :]), reads=[ptc], writes=[self.kcT])
                    else:
                        b.op("dve", lambda e: e.tensor_copy(out=self.vcA[:, g, ct, 0:64], in_=po[:, 64:128]), reads=[po], writes=[self.vcA])
        if "compress" in self.debug:
            d = self.dbg_out("kcT", [64, 2, 256], BF16)
            b.dma("pool", d, self.kcT[:], reads=[self.kcT])
            d = self.dbg_out("vcA", [128, 2, 2, 129], F32)
            b.dma("pool", d, self.vcA[:], reads=[self.vcA])

    def finish(self):
        b = self.b
        b.wait_all_on("pool")
        b.barrier()
        b.close()
        return self.nc


def _phase_attn(self):
    b = self.b
    I = self.inp
    with b.scope():
        tw = b.sb("tw", [128, 8, 640], F32)
        ts = b.sb("ts", [128, 8, 640], F32)
        b.dma("sp", tw[:], I["tw"], writes=[tw])
        b.dma("sp", ts[:], I["ts"], writes=[ts])
        candneg = b.sb("candneg", [128, 32, 64], F32)
        fz = b.sb("fz", [128, 32, 64], F32)
        b.dma("sp", candneg[:], I["candneg"], writes=[candneg])
        b.dma("sp", fz[:], I["fz"], writes=[fz])
        b31 = b.sb("b31", [128, 8], F32)
        b.dma("sp", b31[:], I["b31"], writes=[b31])
        kwp = b.sb("kwp", [128, 2, S], BF16)
        b.op("pool", lambda e: e.memset(kwp[64:128, :, :], 0.0), writes=[kwp])
        b.op("pool", lambda e: e.tensor_copy(out=kwp[0:64, :, :], in_=self.kwT[:]), reads=[self.kwT], writes=[kwp])
        kcp = b.sb("kcp", [128, 2, 256], BF16)
        b.op("pool", lambda e: e.memset(kcp[64:128, :, :], 0.0), writes=[kcp])
        b.op("pool", lambda e: e.tensor_copy(out=kcp[0:64, :, :], in_=self.kcT[:]), reads=[self.kcT], writes=[kcp])
        zer = b.sb("zer", [128, 512], BF16)
        b.op("pool", lambda e: e.memset(zer[:], 0.0), writes=[zer])
        qm = [b.sb(f"qm{i}", [128, 8, 512], BF16) for i in range(2)]
        bct = [b.sb(f"bct{i}", [128, 512], F32) for i in range(3)]
        scf = [b.sb(f"scf{i}", [128, 640], F32) for i in range(2)]
        pcT = [b.sb(f"pcT{i}", [128, 2, 512], F32) for i in range(2)]
        pT = [b.sb(f"pT{i}", [128, 640], BF16) for i in range(3)]
        oacc = b.sb("oacc", [128, 4, 512], F32)
        imp = b.sb("imp", [128, 4, 2, 64], F32)
        impm = b.sb("impm", [128, 64], F32)
        impm2 = b.sb("impm2", [128, 64], F32)
        m8a = b.sb("m8a", [128, 8], F32)
        m8b = b.sb("m8b", [128, 8], F32)
        msk = b.sb("msk", [128, 64], F32)
        mb = b.sb("mb", [128, 128], BF16)
        b.op("pool", lambda e: e.memset(mb[:], 0.0), writes=[mb])
        rs = b.sb("rs", [128, 4], F32)
        rg = b.sb("rg", [128, 4], F32)
        oab = b.sb("oab", [128, 512], BF16)
        oaT = [b.sb(f"oaT{i}", [128, 4, 128], BF16) for i in range(2)]
        pS = [b.ps(f"pS{i}", [128, 512], F32) for i in range(2)]
        pS2 = b.ps("pS2", [128, 512], F32)
        pO = [b.ps(f"pO{i}", [128, 512], F32) for i in range(3)]
        pTr = b.ps("pTr", [128, 8, 128], BF16)
        nrot = {"bct": 0, "scf": 0, "pT": 0, "pS": 0}

        def rot(name, lst):
            nrot[name] += 1
            return lst[nrot[name] % len(lst)]

        def finalize(po, ncol_off, h, qs, branch, first):
            qt = qs_base + qs
            o0 = ncol_off
            b.op("dve", lambda e: e.tensor_scalar_max(out=rs[:, 0:1], in0=po[:, o0 + 64:o0 + 65], scalar1=1e-30), reads=[po], writes=[rs])
            b.op("dve", lambda e: e.reciprocal(out=rs[:, 1:2], in_=rs[:, 0:1]), reads=[rs], writes=[rs])
            b.op("dve", lambda e: e.tensor_tensor(out=rg[:, 0:1], in0=rs[:, 1:2], in1=self.gts[:, qt, h * 3 + branch:h * 3 + branch + 1], op=ALU.mult),
                 reads=[rs, self.gts], writes=[rg])
            if first:
                b.op("dve", lambda e: e.tensor_scalar_mul(out=oacc[:, qs, h * 64:(h + 1) * 64], in0=po[:, o0:o0 + 64], scalar1=rg[:, 0:1]),
                     reads=[po, rg], writes=[oacc])
            else:
                b.op("dve", lambda e: e.scalar_tensor_tensor(out=oacc[:, qs, h * 64:(h + 1) * 64], in0=po[:, o0:o0 + 64], scalar=rg[:, 0:1],
                                                             in1=oacc[:, qs, h * 64:(h + 1) * 64], op0=ALU.mult, op1=ALU.add),
                     reads=[po, rg, oacc], writes=[oacc])

        nqg = getattr(self, "nqg_limit", 8)
        for qg in range(nqg):
            qs_base = 4 * qg
            q0 = 512 * qg
            Q = qm[qg % 2]
            b.dma("sp", Q[0:64, :, :], self.qT_d[:, :, q0:q0 + 512].rearrange("h d t -> d h t"), reads=[self.qT_d], writes=[Q])
            if qg < 2:
                b.op("pool", lambda e: e.memset(Q[64:128, :, :], 0.0), writes=[Q])
            for h in range(8):
                g = h // 4
                pc = pcT[h % 2]
                for ct in range(2):
                    p = rot("pS", pS)
                    b.op("pe", lambda e: e.matmul(p[:, :], lhsT=kcp[:, g, ct * 128:(ct + 1) * 128], rhs=Q[:, h, :], start=True, stop=True),
                         reads=[kcp, Q], writes=[p])
                    bt = rot("bct", bct)
                    b.dma("sp", bt[:], I["biasc"][h, ct, :, q0:q0 + 512], writes=[bt])
                    sc = rot("scf", scf)
                    b.op("dve", lambda e: e.tensor_tensor(out=sc[:, 0:512], in0=p[:, :], in1=bt[:], op=ALU.add), reads=[p, bt], writes=[sc])
                    b.op("act", lambda e: e.activation(out=pc[:, ct, :], in_=sc[:, 0:512], func=AF.Exp), reads=[sc], writes=[pc])
                po = pO[0]
                for qs in range(4):
                    for ct in range(2):
                        b.op("pe", lambda e: e.matmul(po[:, qs * 128:qs * 128 + 129] if False else po[:, 0:129], lhsT=pc[:, ct, qs * 128:(qs + 1) * 128],
                                                      rhs=self.vcA[:, g, ct, :], start=(ct == 0), stop=(ct == 1)), reads=[pc, self.vcA], writes=[po])
                    finalize(po, 0, h, qs, 0, True)
                    if h % 4 == 0:
                        b.op("dve", lambda e: e.tensor_scalar_mul(out=imp[:, qs, g, :], in0=po[:, 65:129], scalar1=rs[:, 1:2]), reads=[po, rs], writes=[imp])
                    else:
                        b.op("dve", lambda e: e.scalar_tensor_tensor(out=imp[:, qs, g, :], in0=po[:, 65:129], scalar=rs[:, 1:2], in1=imp[:, qs, g, :],
                                                                     op0=ALU.mult, op1=ALU.add), reads=[po, rs, imp], writes=[imp])
            if qg >= 2:
                for qs in range(4):
                    qt = qs_base + qs
                    for g in range(2):
                        b.op("dve", lambda e: e.tensor_tensor(out=impm[:], in0=imp[:, qs, g, :], in1=candneg[:, qt, :], op=ALU.add), reads=[imp, candneg], writes=[impm])
                        b.op("dve", lambda e: e.max(out=m8a[:], in_=impm[:]), reads=[impm], writes=[m8a])
                        b.op("dve", lambda e: e.match_replace(out=impm2[:], in_to_replace=m8a[:], in_values=impm[:], imm_value=-1e9), reads=[m8a, impm], writes=[impm2])
                        b.op("dve", lambda e: e.max(out=m8b[:], in_=impm2[:]), reads=[impm2], writes=[m8b])
                        b.op("dve", lambda e: e.tensor_scalar(out=msk[:], in0=impm[:], scalar1=m8b[:, 4:5], scalar2=None, op0=ALU.is_ge), reads=[impm, m8b], writes=[msk])
                        b.op("dve", lambda e: e.tensor_tensor(out=msk[:], in0=msk[:], in1=fz[:, qt, :], op=ALU.max), reads=[msk, fz], writes=[msk])
                        b.op("dve", lambda e: e.tensor_scalar(out=mb[:, 64:128], in0=msk[:], scalar1=-NEG, scalar2=NEG, op0=ALU.mult, op1=ALU.add), reads=[msk], writes=[mb])
                        b.op("pe", lambda e: e.transpose(out=pTr[:, 0, :], in_=mb[:], identity=self.ident[:]), reads=[mb, self.ident], writes=[pTr])
                        b.op("act", lambda e: e.copy(out=Q[64:128, 4 * g:4 * g + 4, qs * 128:(qs + 1) * 128],
                                                     in_=pTr[64:128, 0:1, :].to_broadcast([64, 4, 128])), reads=[pTr], writes=[Q])
            for h in range(8):
                g = h // 4
                po_s, po_w = pO[1], pO[2]
                for po in (po_s, po_w):
                    b.op("pe", lambda e: e.matmul(po[:, 0:260], lhsT=zer[:, 0:128], rhs=zer[:, 0:260], start=True, stop=True), reads=[zer], writes=[po])
                nkt = 4 * (qg + 1)
                for kt in range(nkt):
                    dlt = 4 * qg - kt
                    qstart = 0 if dlt >= 0 else -dlt * 128
                    N = 512 - qstart
                    p = rot("pS", pS)
                    b.op("pe", lambda e: e.matmul(p[:, 0:N], lhsT=self.ksE[:, g, kt * 128:(kt + 1) * 128], rhs=Q[:, h, qstart:512], start=True, stop=True),
                         reads=[self.ksE, Q], writes=[p])
                    pt_ = rot("pT", pT)
                    if dlt <= 1:
                        c0 = 128 if dlt == 1 else 0
                        sc = rot("scf", scf)
                        b.op("dve", lambda e: e.tensor_tensor(out=sc[:, 0:N], in0=p[:, 0:N], in1=ts[:, h, c0:c0 + N], op=ALU.add), reads=[p, ts], writes=[sc])
                        b.op("act", lambda e: e.activation(out=pt_[:, 0:N], in_=sc[:, 0:N], func=AF.Exp), reads=[sc], writes=[pt_])
                    else:
                        b.op("act", lambda e: e.activation(out=pt_[:, 0:N], in_=p[:, 0:N], func=AF.Exp, bias=b31[:, h:h + 1]), reads=[p, b31], writes=[pt_])
                    for qs in range(qstart // 128, 4):
                        o = qs * 128 - qstart
                        b.op("pe", lambda e: e.matmul(po_s[:, qs * 65:(qs + 1) * 65], lhsT=pt_[:, o:o + 128], rhs=self.vaug_s[:, kt, g, :],
                                                      start=False, stop=(kt == nkt - 1), skip_group_check=True), reads=[pt_, self.vaug_s], writes=[po_s])
                kts = [kt for kt in range(4 * qg - 4, 4 * qg + 4) if kt >= 0]
                for kt in kts:
                    qs_lo = max(0, kt - 4 * qg)
                    qs_hi = min(3, kt + 4 - 4 * qg)
                    N = (qs_hi - qs_lo + 1) * 128
                    c0 = 128 * (4 * qg + qs_lo - kt)
                    p = rot("pS", pS)
                    b.op("pe", lambda e: e.matmul(p[:, 0:N], lhsT=kwp[:, g, kt * 128:(kt + 1) * 128], rhs=Q[:, h, qs_lo * 128:(qs_hi + 1) * 128], start=True, stop=True),
                         reads=[kwp, Q], writes=[p])
                    sc = rot("scf", scf)
                    b.op("dve", lambda e: e.tensor_tensor(out=sc[:, 0:N], in0=p[:, 0:N], in1=tw[:, h, c0:c0 + N], op=ALU.add), reads=[p, tw], writes=[sc])
                    pt_ = rot("pT", pT)
                    b.op("act", lambda e: e.activation(out=pt_[:, 0:N], in_=sc[:, 0:N], func=AF.Exp), reads=[sc], writes=[pt_])
                    for qs in range(qs_lo, qs_hi + 1):
                        o = (qs - qs_lo) * 128
                        b.op("pe", lambda e: e.matmul(po_w[:, qs * 65:(qs + 1) * 65], lhsT=pt_[:, o:o + 128], rhs=self.vaug_w[:, kt, g, :],
                                                      start=False, stop=(kt == kts[-1]), skip_group_check=True), reads=[pt_, self.vaug_w], writes=[po_w])
                for qs in range(4):
                    finalize(po_s, qs * 65, h, qs, 1, False)
                    finalize(po_w, qs * 65, h, qs, 2, False)
            for qs in range(4):
                qt = qs_base + qs
                ot = oaT[qs % 2]
                b.op("act", lambda e: e.copy(out=oab[:], in_=oacc[:, qs, :]), reads=[oacc], writes=[oab])
                for c in range(4):
                    b.op("pe", lambda e: e.transpose(out=pTr[:, 4 + c, :], in_=oab[:, c * 128:(c + 1) * 128], identity=self.ident[:]), reads=[oab, self.ident], writes=[pTr])
                b.op("act", lambda e: e.copy(out=ot[:], in_=pTr[:, 4:8, :]), reads=[pTr], writes=[ot])
                b.dma("pool", self.oaT_d[:, :, qt * 128:(qt + 1) * 128].rearrange("c p t -> p c t"), ot[:], reads=[ot], writes=[self.oaT_d])
        if "attn" in self.debug:
            d = self.dbg_out("oaT", [4, 128, S], BF16)
            b.dma("pool", d, self.oaT_d[:], reads=[self.oaT_d])


Prog.phase_attn = _phase_attn


def _phase_attn2(self):
    b = self.b
    I = self.inp
    with b.scope():
        tw = b.sb("tw", [128, 8, 640], F32)
        ts = b.sb("ts", [128, 8, 640], F32)
        b.dma("sp", tw[:], I["tw"], writes=[tw])
        b.dma("sp", ts[:], I["ts"], writes=[ts])
        candneg = b.sb("candneg", [128, 32, 64], F32)
        fz = b.sb("fz", [128, 32, 64], F32)
        b.dma("sp", candneg[:], I["candneg"], writes=[candneg])
        b.dma("sp", fz[:], I["fz"], writes=[fz])
        b31 = b.sb("b31", [128, 8], F32)
        b.dma("sp", b31[:], I["b31"], writes=[b31])
        kwp = b.sb("kwp", [128, 2, S], BF16)
        b.op("pool", lambda e: e.memset(kwp[64:128, :, :], 0.0), writes=[kwp])
        b.op("pool", lambda e: e.tensor_copy(out=kwp[0:64, :, :], in_=self.kwT[:]), reads=[self.kwT], writes=[kwp])
        kcp = b.sb("kcp", [128, 2, 256], BF16)
        b.op("pool", lambda e: e.memset(kcp[64:128, :, :], 0.0), writes=[kcp])
        b.op("pool", lambda e: e.tensor_copy(out=kcp[0:64, :, :], in_=self.kcT[:]), reads=[self.kcT], writes=[kcp])
        zer = b.sb("zer", [128, 512], BF16)
        b.op("pool", lambda e: e.memset(zer[:], 0.0), writes=[zer])
        qm = [b.sb(f"qm{i}", [128, 8, 512], BF16) for i in range(2)]
        bct = [b.sb(f"bct{i}", [128, 512], F32) for i in range(3)]
        scf = [b.sb(f"scf{i}", [128, 640], F32) for i in range(3)]
        pcT = [b.sb(f"pcT{i}", [128, 2, 512], F32) for i in range(2)]
        pT = [b.sb(f"pT{i}", [128, 640], BF16) for i in range(4)]
        oacc = b.sb("oacc", [128, 4, 512], F32)
        imp = b.sb("imp", [128, 4, 2, 64], F32)
        impm = b.sb("impm", [128, 64], F32)
        impm2 = b.sb("impm2", [128, 64], F32)
        m8a = b.sb("m8a", [128, 8], F32)
        m8b = b.sb("m8b", [128, 8], F32)
        msk = b.sb("msk", [128, 64], F32)
        mb = b.sb("mb", [128, 128], BF16)
        b.op("pool", lambda e: e.memset(mb[:], 0.0), writes=[mb])
        rs = b.sb("rs", [128, 4], F32)
        rg = b.sb("rg", [128, 4], F32)
        oab = b.sb("oab", [128, 512], BF16)
        oaT = [b.sb(f"oaT{i}", [128, 4, 128], BF16) for i in range(2)]
        pS = [b.ps(f"pS{i}", [128, 512], F32) for i in range(3)]
        pOs = [b.ps(f"pOs{i}", [128, 512], F32) for i in range(2)]
        pOw = [b.ps(f"pOw{i}", [128, 512], F32) for i in range(2)]
        pTr = b.ps("pTr", [128, 8, 128], BF16)
        nrot = {"bct": 0, "scf": 0, "pT": 0, "pS": 0}

        def rot(name, lst):
            nrot[name] += 1
            return lst[nrot[name] % len(lst)]

        def finalize(po, ncol_off, h, qs, branch, first):
            qt = qs_base + qs
            o0 = ncol_off
            b.op("dve", lambda e: e.tensor_scalar_max(out=rs[:, 0:1], in0=po[:, o0 + 64:o0 + 65], scalar1=1e-30), reads=[po], writes=[rs])
            b.op("dve", lambda e: e.reciprocal(out=rs[:, 1:2], in_=rs[:, 0:1]), reads=[rs], writes=[rs])
            b.op("dve", lambda e: e.tensor_tensor(out=rg[:, 0:1], in0=rs[:, 1:2], in1=self.gts[:, qt, h * 3 + branch:h * 3 + branch + 1], op=ALU.mult),
                 reads=[rs, self.gts], writes=[rg])
            if first:
                b.op("dve", lambda e: e.tensor_scalar_mul(out=oacc[:, qs, h * 64:(h + 1) * 64], in0=po[:, o0:o0 + 64], scalar1=rg[:, 0:1]),
                     reads=[po, rg], writes=[oacc])
            else:
                b.op("dve", lambda e: e.scalar_tensor_tensor(out=oacc[:, qs, h * 64:(h + 1) * 64], in0=po[:, o0:o0 + 64], scalar=rg[:, 0:1],
                                                             in1=oacc[:, qs, h * 64:(h + 1) * 64], op0=ALU.mult, op1=ALU.add),
                     reads=[po, rg, oacc], writes=[oacc])

        nqg = getattr(self, "nqg_limit", 8)
        for qg in range(nqg):
            qs_base = 4 * qg
            q0 = 512 * qg
            Q = qm[qg % 2]
            b.dma("sp", Q[0:64, :, :], self.qT_d[:, :, q0:q0 + 512].rearrange("h d t -> d h t"), reads=[self.qT_d], writes=[Q])
            if qg < 2:
                b.op("pool", lambda e: e.memset(Q[64:128, :, :], 0.0), writes=[Q])
            for h in range(8):
                g = h // 4
                pc = pcT[h % 2]
                for ct in range(2):
                    p = rot("pS", pS)
                    b.op("pe", lambda e: e.matmul(p[:, :], lhsT=kcp[:, g, ct * 128:(ct + 1) * 128], rhs=Q[:, h, :], start=True, stop=True),
                         reads=[kcp, Q], writes=[p])
                    bt = rot("bct", bct)
                    b.dma("sp", bt[:], I["biasc"][h, ct, :, q0:q0 + 512], writes=[bt])
                    sc = rot("scf", scf)
                    b.op("dve", lambda e: e.tensor_tensor(out=sc[:, 0:512], in0=p[:, :], in1=bt[:], op=ALU.add), reads=[p, bt], writes=[sc])
                    b.op("act", lambda e: e.activation(out=pc[:, ct, :], in_=sc[:, 0:512], func=AF.Exp), reads=[sc], writes=[pc])
                po = pOs[h % 2]
                for qs in range(4):
                    for ct in range(2):
                        b.op("pe", lambda e: e.matmul(po[:, qs * 128:qs * 128 + 129] if False else po[:, 0:129], lhsT=pc[:, ct, qs * 128:(qs + 1) * 128],
                                                      rhs=self.vcA[:, g, ct, :], start=(ct == 0), stop=(ct == 1)), reads=[pc, self.vcA], writes=[po])
                    finalize(po, 0, h, qs, 0, True)
                    if h % 4 == 0:
                        b.op("dve", lambda e: e.tensor_scalar_mul(out=imp[:, qs, g, :], in0=po[:, 65:129], scalar1=rs[:, 1:2]), reads=[po, rs], writes=[imp])
                    else:
                        b.op("dve", lambda e: e.scalar_tensor_tensor(out=imp[:, qs, g, :], in0=po[:, 65:129], scalar=rs[:, 1:2], in1=imp[:, qs, g, :],
                                                                     op0=ALU.mult, op1=ALU.add), reads=[po, rs, imp], writes=[imp])
            if qg >= 2:
                for qs in range(4):
                    qt = qs_base + qs
                    for g in range(2):
                        b.op("dve", lambda e: e.tensor_tensor(out=impm[:], in0=imp[:, qs, g, :], in1=candneg[:, qt, :], op=ALU.add), reads=[imp, candneg], writes=[impm])
                        b.op("dve", lambda e: e.max(out=m8a[:], in_=impm[:]), reads=[impm], writes=[m8a])
                        b.op("dve", lambda e: e.match_replace(out=impm2[:], in_to_replace=m8a[:], in_values=impm[:], imm_value=-1e9), reads=[m8a, impm], writes=[impm2])
                        b.op("dve", lambda e: e.max(out=m8b[:], in_=impm2[:]), reads=[impm2], writes=[m8b])
                        b.op("dve", lambda e: e.tensor_scalar(out=msk[:], in0=impm[:], scalar1=m8b[:, 4:5], scalar2=None, op0=ALU.is_ge), reads=[impm, m8b], writes=[msk])
                        b.op("dve", lambda e: e.tensor_tensor(out=msk[:], in0=msk[:], in1=fz[:, qt, :], op=ALU.max), reads=[msk, fz], writes=[msk])
                        b.op("dve", lambda e: e.tensor_scalar(out=mb[:, 64:128], in0=msk[:], scalar1=-NEG, scalar2=NEG, op0=ALU.mult, op1=ALU.add), reads=[msk], writes=[mb])
                        b.op("pe", lambda e: e.transpose(out=pTr[:, 0, :], in_=mb[:], identity=self.ident[:]), reads=[mb, self.ident], writes=[pTr])
                        b.op("act", lambda e: e.copy(out=Q[64:128, 4 * g:4 * g + 4, qs * 128:(qs + 1) * 128],
                                                     in_=pTr[64:128, 0:1, :].to_broadcast([64, 4, 128])), reads=[pTr], writes=[Q])
            jobs = []
            for h in range(8):
                g = h // 4
                nkt = 4 * (qg + 1)
                for kt in range(nkt):
                    dlt = 4 * qg - kt
                    qstart = 0 if dlt >= 0 else -dlt * 128
                    jobs.append(dict(kind="s", h=h, g=g, kt=kt, qlo=qstart // 128, qhi=3, first=(kt == 0), last=False, lastkt=(kt == nkt - 1),
                                     tab=(ts, (128 if dlt == 1 else 0)) if dlt <= 1 else None))
                kts = [kt for kt in range(4 * qg - 4, 4 * qg + 4) if kt >= 0]
                for kt in kts:
                    qs_lo = max(0, kt - 4 * qg)
                    qs_hi = min(3, kt + 4 - 4 * qg)
                    jobs.append(dict(kind="w", h=h, g=g, kt=kt, qlo=qs_lo, qhi=qs_hi, first=False, last=(kt == kts[-1]), lastkt=(kt == kts[-1]),
                                     tab=(tw, 128 * (4 * qg + qs_lo - kt))))

            def emitS(j):
                h, g, kt = j["h"], j["g"], j["kt"]
                N = (j["qhi"] - j["qlo"] + 1) * 128
                p = rot("pS", pS)
                kmat = self.ksE if j["kind"] == "s" else kwp
                b.op("pe", lambda e: e.matmul(p[:, 0:N], lhsT=kmat[:, g, kt * 128:(kt + 1) * 128], rhs=Q[:, h, j["qlo"] * 128:(j["qhi"] + 1) * 128], start=True, stop=True),
                     reads=[kmat, Q], writes=[p])
                j["p"] = p
                j["N"] = N

            def emitE(j):
                h = j["h"]
                p, N = j["p"], j["N"]
                pt_ = rot("pT", pT)
                if j["tab"] is not None:
                    tab, c0 = j["tab"]
                    sc = rot("scf", scf)
                    b.op("dve", lambda e: e.tensor_tensor(out=sc[:, 0:N], in0=p[:, 0:N], in1=tab[:, h, c0:c0 + N], op=ALU.add), reads=[p, tab], writes=[sc])
                    b.op("act", lambda e: e.activation(out=pt_[:, 0:N], in_=sc[:, 0:N], func=AF.Exp), reads=[sc], writes=[pt_])
                else:
                    b.op("act", lambda e: e.activation(out=pt_[:, 0:N], in_=p[:, 0:N], func=AF.Exp, bias=b31[:, h:h + 1]), reads=[p, b31], writes=[pt_])
                j["pt"] = pt_

            def emitPV(j):
                h, g, kt = j["h"], j["g"], j["kt"]
                po_s, po_w = pOs[h % 2], pOw[h % 2]
                if j["first"]:
                    for po in (po_s, po_w):
                        b.op("pe", lambda e: e.matmul(po[:, 0:260], lhsT=zer[:, 0:128], rhs=zer[:, 0:260], start=True, stop=True), reads=[zer], writes=[po])
                po = po_s if j["kind"] == "s" else po_w
                va = self.vaug_s if j["kind"] == "s" else self.vaug_w
                for qs in range(j["qlo"], j["qhi"] + 1):
                    o = (qs - j["qlo"]) * 128
                    b.op("pe", lambda e: e.matmul(po[:, qs * 65:(qs + 1) * 65], lhsT=j["pt"][:, o:o + 128], rhs=va[:, kt, g, :],
                                                  start=False, stop=j["lastkt"], skip_group_check=True), reads=[j["pt"], va], writes=[po])
                if j["last"]:
                    for qs in range(4):
                        finalize(po_s, qs * 65, h, qs, 1, False)
                        finalize(po_w, qs * 65, h, qs, 2, False)

            LA = 2
            for i_ in range(len(jobs) + LA):
                if i_ < len(jobs):
                    emitS(jobs[i_])
                if i_ >= LA:
                    emitE(jobs[i_ - LA])
                    emitPV(jobs[i_ - LA])
            for qs in range(4):
                qt = qs_base + qs
                ot = oaT[qs % 2]
                b.op("act", lambda e: e.copy(out=oab[:], in_=oacc[:, qs, :]), reads=[oacc], writes=[oab])
                for c in range(4):
                    b.op("pe", lambda e: e.transpose(out=pTr[:, 4 + c, :], in_=oab[:, c * 128:(c + 1) * 128], identity=self.ident[:]), reads=[oab, self.ident], writes=[pTr])
                b.op("act", lambda e: e.copy(out=ot[:], in_=pTr[:, 4:8, :]), reads=[pTr], writes=[ot])
                b.dma("pool", self.oaT_d[:, :, qt * 128:(qt + 1) * 128].rearrange("c p t -> p c t"), ot[:], reads=[ot], writes=[self.oaT_d])
        if "attn" in self.debug:
            d = self.dbg_out("oaT", [4, 128, S], BF16)
            b.dma("pool", d, self.oaT_d[:], reads=[self.oaT_d])


Prog.phase_attn2 = _phase_attn2


def _phase_merge(self):
    b = self.b
    I = self.inp
    self.x1_d = b.dram("x1_d", [S, D], F32)
    with b.scope():
        gat = self.load_gain("gat2", I["attn_norm_g"][0])
        stage = [b.sb(f"mst{i}", [128, 1024], F32) for i in range(2)]
        wg = b.sb("wg", [128, 8, 2048], BF16)
        for n in range(2):
            for c in range(8):
                st = stage[c % 2]
                b.dma("sp", st[:], I["w_in"][0][c * 128:(c + 1) * 128, GA0 + n * 1024:GA0 + (n + 1) * 1024], writes=[st])
                b.op("act", lambda e: e.activation(out=wg[:, c, n * 1024:(n + 1) * 1024], in_=st[:], func=AF.Copy, scale=gat[:, c:c + 1]),
                     reads=[st, gat], writes=[wg])
        wa = b.sb("wa", [128, 4, 1024], BF16)
        wb = b.sb("wb", [128, 4, 1024], BF16)
        wo = b.sb("wo", [128, 8, 1024], BF16)
        self.load_weight(wa, I["w_proj_a"][0], 1024, kch=4, stage=stage, eng="dve")
        self.load_weight(wb, I["w_proj_b"][0], 1024, kch=4, stage=stage, eng="dve")
        self.load_weight(wo, I["w_out"][0], 1024, kch=8, stage=stage, eng="dve")
        xt = [b.sb(f"mxt{i}", [128, D], F32) for i in range(2)]
        junk = b.sb("mjunk", [128, D], BF16)
        ss = [b.sb(f"mss{i}", [128, 1], F32) for i in range(2)]
        hb = [b.sb(f"mhb{i}", [128, D], BF16) for i in range(2)]
        hT = [b.sb(f"mhT{i}", [128, 8, 128], BF16) for i in range(2)]
        oat = [b.sb(f"oat{i}", [128, 4, 128], BF16) for i in range(2)]
        obt = [b.sb(f"obt{i}", [128, 4, 128], BF16) for i in range(2)]
        sg = b.sb("msg", [128, 2048], F32)
        m1 = b.sb("m1", [128, 1024], F32)
        m2 = b.sb("m2", [128, 1024], F32)
        mgb = b.sb("mgb", [128, 1024], BF16)
        mT = b.sb("mT", [128, 8, 128], BF16)
        x1t = [b.sb(f"x1t{i}", [128, D], F32) for i in range(2)]
        pt = b.ps("mpt", [128, 8, 128], BF16)
        pg = [b.ps(f"mpg{i}", [128, 512], F32) for i in range(2)]
        pa = [b.ps(f"mpa{i}", [128, 512], F32) for i in range(2)]
        pb = [b.ps(f"mpb{i}", [128, 512], F32) for i in range(2)]
        for t in range(getattr(self, "nt_limit", NT)):
            i = t % 2
            self.make_hT(I["x"], t, xt[i], junk, ss[i], hb[i], pt, hT[i], self.ident)
            b.dma("sp", oat[i][:], self.oaT_d[:, :, t * 128:(t + 1) * 128].rearrange("c p t -> p c t"), reads=[self.oaT_d], writes=[oat[i]])
            b.dma("sp", obt[i][:], self.obT_d[:, :, t * 128:(t + 1) * 128].rearrange("c p t -> p c t"), reads=[self.obT_d], writes=[obt[i]])
            for n in range(4):
                p = pg[n % 2]
                for c in range(8):
                    b.op("pe", lambda e: e.matmul(p[:, :], lhsT=hT[i][:, c, :], rhs=wg[:, c, n * 512:(n + 1) * 512], start=(c == 0), stop=(c == 7)),
                         reads=[hT[i], wg], writes=[p])
                b.op("act", lambda e: e.activation(out=sg[:, n * 512:(n + 1) * 512], in_=p[:, :], func=AF.Sigmoid), reads=[p], writes=[sg])
            for n in range(2):
                for c in range(4):
                    b.op("pe", lambda e: e.matmul(pa[n][:, :], lhsT=oat[i][:, c, :], rhs=wa[:, c, n * 512:(n + 1) * 512], start=(c == 0), stop=(c == 3)),
                         reads=[oat[i], wa], writes=[pa[n]])
                for c in range(4):
                    b.op("pe", lambda e: e.matmul(pb[n][:, :], lhsT=obt[i][:, c, :], rhs=wb[:, c, n * 512:(n + 1) * 512], start=(c == 0), stop=(c == 3)),
                         reads=[obt[i], wb], writes=[pb[n]])
                b.op("dve", lambda e: e.tensor_tensor(out=m1[:, n * 512:(n + 1) * 512], in0=pa[n][:, :], in1=sg[:, n * 512:(n + 1) * 512], op=ALU.mult),
                     reads=[pa[n], sg], writes=[m1])
                b.op("dve", lambda e: e.tensor_tensor(out=m2[:, n * 512:(n + 1) * 512], in0=pb[n][:, :], in1=sg[:, 1024 + n * 512:1024 + (n + 1) * 512], op=ALU.mult),
                     reads=[pb[n], sg], writes=[m2])
            b.op("pool", lambda e: e.tensor_tensor(out=mgb[:], in0=m1[:], in1=m2[:], op=ALU.add), reads=[m1, m2], writes=[mgb])
            for c in range(8):
                b.op("pe", lambda e: e.transpose(out=pt[:, c, :], in_=mgb[:, c * 128:(c + 1) * 128], identity=self.ident[:]), reads=[mgb, self.ident], writes=[pt])
            b.op("act", lambda e: e.copy(out=mT[:], in_=pt[:]), reads=[pt], writes=[mT])
            for n in range(2):
                for c in range(8):
                    b.op("pe", lambda e: e.matmul(pa[n][:, :], lhsT=mT[:, c, :], rhs=wo[:, c, n * 512:(n + 1) * 512], start=(c == 0), stop=(c == 7)),
                         reads=[mT, wo], writes=[pa[n]])
                b.op("dve", lambda e: e.tensor_tensor(out=x1t[i][:, n * 512:(n + 1) * 512], in0=pa[n][:, :], in1=xt[i][:, n * 512:(n + 1) * 512], op=ALU.add),
                     reads=[pa[n], xt[i]], writes=[x1t[i]])
            b.dma("pool", self.x1_d[t * 128:(t + 1) * 128, :], x1t[i][:], reads=[x1t[i]], writes=[self.x1_d])
        if "merge" in self.debug:
            d = self.dbg_out("x1", [S, D], F32)
            b.dma("pool", d, self.x1_d[:], reads=[self.x1_d])


def _phase_ffn(self):
    b = self.b
    I = self.inp
    TG = 128
    NFT = 44
    with b.scope():
        gf = self.load_gain("gf", I["ffn_norm_g"][0])
        stage = [b.sb(f"fst{i}", [128, 1024], F32) for i in range(2)]
        wu = b.sb("wu", [128, 8, 2 * DFF], BF16)
        for n in range(8):
            for c in range(8):
                st = stage[c % 2]
                b.dma("sp", st[:, 0:704], I["w_up"][0][c * 128:(c + 1) * 128, n * 704:(n + 1) * 704], writes=[st])
                b.op("act", lambda e: e.activation(out=wu[:, c, n * 704:(n + 1) * 704], in_=st[:, 0:704], func=AF.Copy, scale=gf[:, c:c + 1]),
                     reads=[st, gf], writes=[wu])
        wd = b.sb("wd", [128, 22, D], BF16)
        self.load_weight(wd, I["w_down"][0], D, kch=22, stage=stage, eng="dve")
        cw = b.sb("cw", [128, 3, NFT], F32)
        for j in range(3):
            b.dma("sp", cw[:, j, :], I["conv_w"][0][j].rearrange("(c p) -> p c", p=128), writes=[cw], allow_slow_non_contiguous=True)
        cbias = self.load_gain("cbias", I["conv_b"][0], kch=NFT)
        carry = b.sb("carry", [128, NFT, 2], F32)
        b.op("pool", lambda e: e.memset(carry[:], 0.0), writes=[carry])
        xt = [b.sb(f"fxt{i}", [128, D], F32) for i in range(2)]
        junk = b.sb("fjunk", [128, D], BF16)
        ss = [b.sb(f"fss{i}", [128, 1], F32) for i in range(2)]
        hb = [b.sb(f"fhb{i}", [128, D], BF16) for i in range(2)]
        hT1 = [b.sb(f"fhT{i}", [128, 8, 128], BF16) for i in range(2)]
        hTg = b.sb("fhTg", [128, 8, TG], BF16)
        ub = [b.sb(f"ub{i}", [128, TG + 2], F32) for i in range(2)]
        cv = [b.sb(f"cv{i}", [128, TG], F32) for i in range(2)]
        sgl = b.sb("sgl", [128, TG], F32)
        actT = b.sb("actT", [128, 22, TG], BF16)
        self._val = b.sb("fval", [128, 22, TG], BF16)
        ot = xt
        pt = b.ps("fpt", [128, 8, 128], BF16)
        pu = [b.ps(f"fpu{i}", [128, 512], F32) for i in range(3)]
        pd = [b.ps(f"fpd{i}", [128, 512], F32) for i in range(2)]
        ng = getattr(self, "nt_limit", NT) * 128 // TG
        for gi in range(ng):
            for s_ in range(TG // 128):
                t = gi * (TG // 128) + s_
                self.make_hT(self.x1_d, t, xt[s_], junk, ss[s_], hb[s_], pt, hT1[s_], self.ident)
                b.op("pool", lambda e: e.tensor_copy(out=hTg[:, :, s_ * 128:(s_ + 1) * 128], in_=hT1[s_][:]), reads=[hT1[s_]], writes=[hTg])
            for ft in range(NFT):
                p = pu[ft % 3]
                u = ub[ft % 2]
                c_ = cv[(ft // 22) % 2] if False else cv[ft % 2]
                for c in range(8):
                    b.op("pe", lambda e: e.matmul(p[:, 0:TG], lhsT=wu[:, c, ft * 128:(ft + 1) * 128], rhs=hTg[:, c, :], start=(c == 0), stop=(c == 7)),
                         reads=[wu, hTg], writes=[p])
                b.op("act", lambda e: e.copy(out=u[:, 2:TG + 2], in_=p[:, 0:TG]), reads=[p], writes=[u])
                b.op("pool", lambda e: e.tensor_copy(out=u[:, 0:2], in_=carry[:, ft, :]), reads=[carry], writes=[u])
                b.op("pool", lambda e: e.tensor_copy(out=carry[:, ft, :], in_=u[:, TG:TG + 2]), reads=[u], writes=[carry])
                b.op("dve", lambda e: e.tensor_scalar(out=c_[:], in0=u[:, 0:TG], scalar1=cw[:, 0, ft:ft + 1], scalar2=cbias[:, ft:ft + 1], op0=ALU.mult, op1=ALU.add),
                     reads=[u, cw, cbias], writes=[c_])
                b.op("dve", lambda e: e.scalar_tensor_tensor(out=c_[:], in0=u[:, 1:TG + 1], scalar=cw[:, 1, ft:ft + 1], in1=c_[:], op0=ALU.mult, op1=ALU.add),
                     reads=[u, cw, c_], writes=[c_])
                if ft < 22:
                    b.op("dve", lambda e: e.scalar_tensor_tensor(out=self._val[:, ft, :], in0=u[:, 2:TG + 2], scalar=cw[:, 2, ft:ft + 1], in1=c_[:], op0=ALU.mult, op1=ALU.add),
                         reads=[u, cw, c_], writes=[self._val])
                else:
                    b.op("dve", lambda e: e.scalar_tensor_tensor(out=c_[:], in0=u[:, 2:TG + 2], scalar=cw[:, 2, ft:ft + 1], in1=c_[:], op0=ALU.mult, op1=ALU.add),
                         reads=[u, cw, c_], writes=[c_])
                    b.op("act", lambda e: e.activation(out=sgl[:], in_=c_[:], func=AF.Silu), reads=[c_], writes=[sgl])
                    b.op("dve", lambda e: e.tensor_tensor(out=actT[:, ft - 22, :], in0=sgl[:], in1=self._val[:, ft - 22, :], op=ALU.mult),
                         reads=[sgl, self._val], writes=[actT])
            for s_ in range(TG // 128):
                t = gi * (TG // 128) + s_
                for n in range(2):
                    for f in range(22):
                        b.op("pe", lambda e: e.matmul(pd[n][:, :], lhsT=actT[:, f, s_ * 128:(s_ + 1) * 128], rhs=wd[:, f, n * 512:(n + 1) * 512], start=(f == 0), stop=(f == 21)),
                             reads=[actT, wd], writes=[pd[n]])
                    b.op("dve", lambda e: e.tensor_tensor(out=ot[s_][:, n * 512:(n + 1) * 512], in0=pd[n][:, :], in1=xt[s_][:, n * 512:(n + 1) * 512], op=ALU.add),
                         reads=[pd[n], xt[s_]], writes=[ot[s_]])
                b.dma("pool", self.out[t * 128:(t + 1) * 128, :], ot[s_][:], reads=[ot[s_]])


Prog.phase_merge = _phase_merge
Prog.phase_ffn = _phase_ffn


def _phase_rwkv(self):
    b = self.b
    I = self.inp
    TG = 256
    NCH = TG // 64
    tt = lambda eng, out, in0, in1, op, rd, wr: b.op(eng, lambda e: e.tensor_tensor(out=out, in0=in0, in1=in1, op=op), reads=rd, writes=wr)
    with b.scope():
        gat = self.load_gain("gat3", I["attn_norm_g"][0])
        stage = [b.sb(f"rst{i}", [128, 1792], F32) for i in range(2)]
        wr = b.sb("wr", [128, 8, 1792], BF16)
        self.load_weight(wr, I["w_in"][0][:, RW0:RW0 + 1792], 1792, gvec=gat, stage=stage)

        def colvec(name, src, n):
            t = b.sb(name, [64, n], F32)
            b.dma("sp", t[:], src.rearrange("(c p) -> p c", p=64), writes=[t], allow_slow_non_contiguous=True)
            return t
        mu = colvec("mu", I["rwkv_mu"][0], 28)
        w0 = colvec("w0", I["rwkv_w0"][0], 8)
        a0 = colvec("a0", I["rwkv_a0"][0], 8)
        k_k = colvec("k_k", I["rwkv_k_k"][0], 8)
        k_a = colvec("k_a", I["rwkv_k_a"][0], 8)
        r_k = colvec("r_k", I["rwkv_r_k"][0].rearrange("h d -> (h d)"), 8)
        w2s = b.sb("w2s", [64, 512], F32)
        a2s = b.sb("a2s", [64, 512], F32)
        g2s = b.sb("g2s", [64, 2, 512], F32)
        b.dma("sp", w2s[:], I["rwkv_w2"][0], writes=[w2s])
        b.dma("sp", a2s[:], I["rwkv_a2"][0], writes=[a2s])
        b.dma("sp", g2s[:], I["rwkv_g2"][0].rearrange("(two l) f -> l two f", two=2), writes=[g2s])
        lng = b.sb("lng", [64, 512], F32)
        lnb = b.sb("lnb", [64, 512], F32)
        b.dma("sp", lng[:], I["rwkv_ln_g"][0].partition_broadcast(64), writes=[lng])
        b.dma("sp", lnb[:], I["rwkv_ln_b"][0].partition_broadcast(64), writes=[lnb])
        msk = b.sb("rmsk", [64, 3, 64], F32)
        b.dma("sp", msk[:], I["rwmask"], writes=[msk])
        rstm = b.sb("rstm", [64, TG], F32)
        b.dma("sp", rstm[:], I["rwreset"][:, 0:TG], writes=[rstm])
        ones = b.sb("ones64", [64, 64], F32)
        b.op("pool", lambda e: e.memset(ones[:], 1.0), writes=[ones])
        idf = self.identf
        carry = b.sb("rcarry", [64, 28], F32)
        b.op("pool", lambda e: e.memset(carry[:], 0.0), writes=[carry])
        Hs = [[b.sb(f"H{h}_{i}", [64, 64], F32) for i in range(2)] for h in range(8)]
        for h in range(8):
            b.op("pool", lambda e: e.memset(Hs[h][0][:], 0.0), writes=[Hs[h][0]])
        xt = [b.sb(f"rxt{i}", [128, D], F32) for i in range(2)]
        junk = b.sb("rjunk", [128, D], BF16)
        ss = [b.sb(f"rss{i}", [128, 1], F32) for i in range(2)]
        hb = [b.sb(f"rhb{i}", [128, D], BF16) for i in range(2)]
        hT1 = [b.sb(f"rhT{i}", [128, 8, 128], BF16) for i in range(2)]
        hTg = b.sb("rhTg", [128, 8, TG], BF16)
        pbuf = [b.sb(f"rpb{i}", [64, TG + 1], F32) for i in range(2)]
        dtmp = b.sb("rdtmp", [64, TG], F32)
        X = [b.sb(f"rX{w}", [64, 8, TG], F32) for w in range(3)]
        xs = b.sb("rxs", [64, 4, TG], F32)
        BV = b.sb("rBV", [64, 8, TG], F32)
        Ytm = b.sb("rYtm", [64, NCH, 8, 64], F32)
        sqv = b.sb("rsqv", [64, NCH, 8, 64], F32)
        st1 = b.sb("rst1", [64, NCH * 8], F32)
        st2 = b.sb("rst2", [64, NCH * 8], F32)
        T = {n: b.sb("r" + n, [64, TG], F32) for n in ["lw", "as", "kk", "sq", "kkn", "bv", "kp", "t1", "L", "Lx", "Ep", "Em", "Ex", "BT", "KT", "BG", "KG", "rk"]}
        AR = b.sb("rAR", [64, NCH, 2, 64], F32)
        TM = [b.sb(f"rTM{i}", [64, 3, 64], F32) for i in range(2)]
        XM = [b.sb(f"rXM{i}", [64, 4, 64], F32) for i in range(2)]
        AA = [b.sb(f"rAA{i}", [64, 2, 64], F32) for i in range(3)]
        PP = [b.sb(f"rPP{i}", [64, 64], F32) for i in range(3)]
        Xs = b.sb("rXs", [64, 64], F32)
        Us = b.sb("rUs", [64, 64], F32)
        obf = [b.sb(f"robf{i}", [64, TG], BF16) for i in range(2)]
        otmp = b.sb("rotmp", [64, TG], F32)
        pt = b.ps("rpt", [128, 8, 128], BF16)
        pp = [b.ps(f"rpp{i}", [128, 512], F32) for i in range(2)]
        pq = [b.ps(f"rpq{i}", [128, 512], F32) for i in range(2)]
        pd = [b.ps(f"rpd{i}", [128, 512], F32) for i in range(2)]
        pz = b.ps("rpz", [128, 512], F32)
        cnt = {"pp": 0, "pq": 0, "pd": 0, "aa": 0, "ppb": 0, "tm": 0, "xm": 0, "pb": 0}

        def nxt(k, lst):
            cnt[k] += 1
            return lst[cnt[k] % len(lst)]

        ngr = getattr(self, "nrg_limit", S // TG)
        for gi in range(ngr):
            q0 = gi * TG
            for s_ in range(TG // 128):
                t = gi * (TG // 128) + s_
                self.make_hT(I["x"], t, xt[s_], junk, ss[s_], hb[s_], pt, hT1[s_], self.ident)
                b.op("pool", lambda e: e.tensor_copy(out=hTg[:, :, s_ * 128:(s_ + 1) * 128], in_=hT1[s_][:]), reads=[hT1[s_]], writes=[hTg])

            def proj_lerp(fc, out_ap, out_buf, post=None):
                p = nxt("pp", pp)
                for c in range(8):
                    b.op("pe", lambda e: e.matmul(p[0:64, 0:TG], lhsT=wr[:, c, fc * 64:(fc + 1) * 64], rhs=hTg[:, c, :], start=(c == 0), stop=(c == 7)),
                         reads=[wr, hTg], writes=[p])
                pb_ = nxt("pb", pbuf)
                b.op("act", lambda e: e.copy(out=pb_[:, 1:TG + 1], in_=p[0:64, 0:TG]), reads=[p], writes=[pb_])
                b.op("pool", lambda e: e.tensor_copy(out=pb_[:, 0:1], in_=carry[:, fc:fc + 1]), reads=[carry], writes=[pb_])
                b.op("pool", lambda e: e.tensor_copy(out=carry[:, fc:fc + 1], in_=pb_[:, TG:TG + 1]), reads=[pb_], writes=[carry])
                tt("dve", dtmp[:], pb_[:, 0:TG], pb_[:, 1:TG + 1], ALU.subtract, [pb_], [dtmp])
                b.op("dve", lambda e: e.scalar_tensor_tensor(out=out_ap, in0=dtmp[:], scalar=mu[:, fc:fc + 1], in1=pb_[:, 1:TG + 1], op0=ALU.mult, op1=ALU.add),
                     reads=[dtmp, mu, pb_], writes=[out_buf])

            for w in range(3):
                for h in range(8):
                    proj_lerp(w * 8 + h, X[w][:, h, :], X[w])
            for j in range(4):
                proj_lerp(24 + j, xs[:, j, :], xs)
            b.op("act", lambda e: e.activation(out=xs[:, 0, :], in_=xs[:, 0, :], func=AF.Tanh), reads=[xs], writes=[xs])
            b.op("act", lambda e: e.activation(out=xs[:, 2:4, :], in_=xs[:, 2:4, :], func=AF.Sigmoid), reads=[xs], writes=[xs])

            for h in range(8):
                hs = slice(h * 64, (h + 1) * 64)
                R_, K_, V_ = X[0][:, h, :], X[1][:, h, :], X[2][:, h, :]
                p = nxt("pp", pp)
                b.op("pe", lambda e: e.matmul(p[0:64, 0:TG], lhsT=w2s[:, hs], rhs=xs[:, 0, :], start=True, stop=True), reads=[w2s, xs], writes=[p])
                b.op("act", lambda e: e.activation(out=T["lw"][:], in_=p[0:64, 0:TG], func=AF.Sigmoid, bias=w0[:, h:h + 1]), reads=[p, w0], writes=[T["lw"]])
                b.op("pool", lambda e: e.tensor_scalar_mul(out=T["lw"][:], in0=T["lw"][:], scalar1=-0.6065306597126334), reads=[T["lw"]], writes=[T["lw"]])
                p = nxt("pp", pp)
                b.op("pe", lambda e: e.matmul(p[0:64, 0:TG], lhsT=a2s[:, hs], rhs=xs[:, 1, :], start=True, stop=True), reads=[a2s, xs], writes=[p])
                b.op("act", lambda e: e.activation(out=T["as"][:], in_=p[0:64, 0:TG], func=AF.Sigmoid, bias=a0[:, h:h + 1]), reads=[p, a0], writes=[T["as"]])
                b.op("dve", lambda e: e.tensor_scalar_mul(out=T["kk"][:], in0=K_, scalar1=k_k[:, h:h + 1]), reads=[X[1], k_k], writes=[T["kk"]])
                tt("pool", T["sq"][:], T["kk"][:], T["kk"][:], ALU.mult, [T["kk"]], [T["sq"]])
                p = nxt("pp", pp)
                b.op("pe", lambda e: e.matmul(p[0:64, 0:TG], lhsT=ones[:], rhs=T["sq"][:], start=True, stop=True), reads=[ones, T["sq"]], writes=[p])
                b.op("act", lambda e: e.activation(out=T["sq"][:], in_=p[0:64, 0:TG], func=AF.Sqrt), reads=[p], writes=[T["sq"]])
                b.op("dve", lambda e: e.tensor_scalar_max(out=T["sq"][:], in0=T["sq"][:], scalar1=1e-12), reads=[T["sq"]], writes=[T["sq"]])
                b.op("dve", lambda e: e.reciprocal(out=T["sq"][:], in_=T["sq"][:]), reads=[T["sq"]], writes=[T["sq"]])
                tt("dve", T["kkn"][:], T["kk"][:], T["sq"][:], ALU.mult, [T["kk"], T["sq"]], [T["kkn"]])
                tt("pool", T["bv"][:], T["kkn"][:], T["as"][:], ALU.mult, [T["kkn"], T["as"]], [T["bv"]])
                b.op("dve", lambda e: e.tensor_scalar(out=T["t1"][:], in0=T["as"][:], scalar1=-1.0, scalar2=k_a[:, h:h + 1], op0=ALU.add, op1=ALU.mult),
                     reads=[T["as"], k_a], writes=[T["t1"]])
                b.op("dve", lambda e: e.scalar_tensor_tensor(out=T["kp"][:], in0=T["t1"][:], scalar=1.0, in1=K_, op0=ALU.add, op1=ALU.mult),
                     reads=[T["t1"], X[1]], writes=[T["kp"]])
                tt("pool", T["rk"][:], R_, T["kp"][:], ALU.mult, [X[0], T["kp"]], [T["rk"]])
                b.op("pool", lambda e: e.tensor_scalar_mul(out=T["rk"][:], in0=T["rk"][:], scalar1=r_k[:, h:h + 1]), reads=[T["rk"], r_k], writes=[T["rk"]])
                p = nxt("pp", pp)
                b.op("pe", lambda e: e.matmul(p[0:64, 0:TG], lhsT=ones[:], rhs=T["rk"][:], start=True, stop=True), reads=[ones, T["rk"]], writes=[p])
                tt("dve", BV[:, h, :], p[0:64, 0:TG], V_, ALU.mult, [p, X[2]], [BV])
                b.op("dve", lambda e: e.tensor_tensor_scan(out=T["L"][:], data0=rstm[:], data1=T["lw"][:], initial=0.0, op0=ALU.mult, op1=ALU.add),
                     reads=[rstm, T["lw"]], writes=[T["L"]])
                tt("pool", T["Lx"][:], T["L"][:], T["lw"][:], ALU.subtract, [T["L"], T["lw"]], [T["Lx"]])
                b.op("act", lambda e: e.activation(out=T["Ep"][:], in_=T["L"][:], func=AF.Exp), reads=[T["L"]], writes=[T["Ep"]])
                b.op("act", lambda e: e.activation(out=T["Em"][:], in_=T["L"][:], func=AF.Exp, scale=-1.0), reads=[T["L"]], writes=[T["Em"]])
                b.op("act", lambda e: e.activation(out=T["Ex"][:], in_=T["Lx"][:], func=AF.Exp), reads=[T["Lx"]], writes=[T["Ex"]])
                c3 = lambda ap: ap.rearrange("p (c t) -> p c t", t=64)
                b.op("dve", lambda e: e.scalar_tensor_tensor(out=AR[:, :, 0, :], in0=c3(T["kkn"][:]), scalar=-1.0, in1=c3(T["Ex"][:]), op0=ALU.mult, op1=ALU.mult),
                     reads=[T["kkn"], T["Ex"]], writes=[AR])
                tt("pool", AR[:, :, 1, :], c3(R_), c3(T["Ep"][:]), ALU.mult, [X[0], T["Ep"]], [AR])
                tt("dve", T["BT"][:], T["bv"][:], T["Em"][:], ALU.mult, [T["bv"], T["Em"]], [T["BT"]])
                tt("pool", T["KT"][:], T["kp"][:], T["Em"][:], ALU.mult, [T["kp"], T["Em"]], [T["KT"]])
                gC = c3(T["Ep"][:])[:, :, 63:64].to_broadcast([64, NCH, 64])
                tt("dve", c3(T["BG"][:]), c3(T["BT"][:]), gC, ALU.mult, [T["BT"], T["Ep"]], [T["BG"]])
                tt("pool", c3(T["KG"][:]), c3(T["KT"][:]), gC, ALU.mult, [T["KT"], T["Ep"]], [T["KG"]])
                for c in range(NCH):
                    cs = slice(c * 64, (c + 1) * 64)
                    Hc = Hs[h][(gi * NCH + c) % 2]
                    Hn = Hs[h][(gi * NCH + c + 1) % 2]
                    p = nxt("pq", pq)
                    for j, (src, sb_) in enumerate([(V_[:, cs], X[2]), (T["BG"][:, cs], T["BG"]), (T["KG"][:, cs], T["KG"])]):
                        b.op("pe", lambda e: e.transpose(out=p[0:64, j * 64:(j + 1) * 64], in_=src, identity=idf[0:64, 0:64]), reads=[sb_, idf], writes=[p])
                    tm = nxt("tm", TM)
                    b.op("act", lambda e: e.copy(out=tm[:].rearrange("p a b -> p (a b)"), in_=p[0:64, 0:192]), reads=[p], writes=[tm])
                    p = nxt("pq", pq)
                    arc = AR[:, c, :, :].rearrange("p a t -> p (a t)")
                    b.op("pe", lambda e: e.matmul(p[0:64, 0:128], lhsT=T["BT"][:, cs], rhs=arc, start=True, stop=True), reads=[T["BT"], AR], writes=[p])
                    b.op("pe", lambda e: e.matmul(p[0:64, 128:256], lhsT=T["KT"][:, cs], rhs=arc, start=True, stop=True), reads=[T["KT"], AR], writes=[p])
                    b.op("pe", lambda e: e.matmul(p[0:64, 256:320], lhsT=AR[:, c, 0, :], rhs=T["BT"][:, cs], start=True, stop=True), reads=[T["BT"], AR], writes=[p])
                    xm = nxt("xm", XM)
                    tt("dve", xm[:].rearrange("p (a m) t -> p a m t", a=2), p[0:64, 0:256].rearrange("p (a m t) -> p a m t", a=2, m=2),
                       msk[:, None, 0:2, :].to_broadcast([64, 2, 2, 64]), ALU.mult, [p, msk], [xm])
                    aa = nxt("aa", AA)
                    b.op("pool", lambda e: e.tensor_copy(out=aa[:, 0, :], in_=xm[:, 0, :]), reads=[xm], writes=[aa])
                    tt("dve", aa[:, 1, :], p[0:64, 256:320], msk[:, 2, :], ALU.mult, [p, msk], [aa])
                    P_ = nxt("ppb", PP)
                    tt("pool", P_[:], xm[:, 0, :], idf[0:64, 0:64], ALU.add, [xm, idf], [P_])
                    for step in range(5):
                        pdb = nxt("pd", pd)
                        b.op("pe", lambda e: e.matmul(pdb[0:64, 0:64], lhsT=aa[:, 1, :], rhs=aa[:, 0, :], start=True, stop=True), reads=[aa], writes=[pdb])
                        b.op("pe", lambda e: e.matmul(pdb[0:64, 64:128], lhsT=aa[:, 0, :], rhs=aa[:, 1, :], start=True, stop=True), reads=[aa], writes=[pdb])
                        aa2 = nxt("aa", AA)
                        b.op("act", lambda e: e.copy(out=aa2[:].rearrange("p a t -> p (a t)"), in_=pdb[0:64, 0:128]), reads=[pdb], writes=[aa2])
                        b.op("pe", lambda e: e.matmul(pdb[0:64, 128:192], lhsT=aa2[:, 1, :], rhs=P_[:], start=True, stop=True), reads=[aa2, P_], writes=[pdb])
                        P2 = nxt("ppb", PP)
                        tt("dve", P2[:], pdb[0:64, 128:192], P_[:], ALU.add, [pdb, P_], [P2])
                        aa, P_ = aa2, P2
                    b.op("pe", lambda e: e.matmul(pz[0:64, 0:64], lhsT=xm[:, 2, :], rhs=tm[:, 0, :], start=True, stop=False), reads=[xm, tm], writes=[pz])
                    b.op("pe", lambda e: e.matmul(pz[0:64, 0:64], lhsT=AR[:, c, 0, :], rhs=Hc[:], start=False, stop=True), reads=[AR, Hc], writes=[pz])
                    b.op("act", lambda e: e.copy(out=Xs[:], in_=pz[0:64, 0:64]), reads=[pz], writes=[Xs])
                    b.op("pe", lambda e: e.matmul(pz[0:64, 64:128], lhsT=P_[:], rhs=Xs[:], start=True, stop=True), reads=[P_, Xs], writes=[pz])
                    b.op("act", lambda e: e.copy(out=Us[:], in_=pz[0:64, 64:128]), reads=[pz], writes=[Us])
                    b.op("pe", lambda e: e.matmul(pz[0:64, 128:192], lhsT=AR[:, c, 1, :], rhs=Hc[:], start=True, stop=False), reads=[AR, Hc], writes=[pz])
                    b.op("pe", lambda e: e.matmul(pz[0:64, 128:192], lhsT=xm[:, 1, :], rhs=Us[:], start=False, stop=False), reads=[xm, Us], writes=[pz])
                    b.op("pe", lambda e: e.matmul(pz[0:64, 128:192], lhsT=xm[:, 3, :], rhs=tm[:, 0, :], start=False, stop=True), reads=[xm, tm], writes=[pz])
                    b.op("pe", lambda e: e.matmul(pz[0:64, 192:256], lhsT=tm[:, 1, :], rhs=Us[:], start=True, stop=False), reads=[tm, Us], writes=[pz])
                    b.op("pe", lambda e: e.matmul(pz[0:64, 192:256], lhsT=tm[:, 2, :], rhs=tm[:, 0, :], start=False, stop=True), reads=[tm], writes=[pz])
                    b.op("act", lambda e: e.copy(out=Ytm[:, c, h, :], in_=pz[0:64, 128:192]), reads=[pz], writes=[Ytm])
                    b.op("dve", lambda e: e.scalar_tensor_tensor(out=Hn[:], in0=Hc[:], scalar=T["Ep"][:, c * 64 + 63:c * 64 + 64], in1=pz[0:64, 192:256],
                                                                 op0=ALU.mult, op1=ALU.add), reads=[Hc, T["Ep"], pz], writes=[Hn])
            Y3 = Ytm[:].rearrange("p c h i -> p (c h) i")
            S3 = sqv[:].rearrange("p c h i -> p (c h) i")
            b.op("dve", lambda e: e.tensor_reduce(out=st1[:], in_=Y3, axis=AX.X, op=ALU.add), reads=[Ytm], writes=[st1])
            b.op("pool", lambda e: e.tensor_scalar_mul(out=st1[:], in0=st1[:], scalar1=1.0 / 64), reads=[st1], writes=[st1])
            tt("dve", Y3, Y3, st1[:].unsqueeze(2).to_broadcast([64, NCH * 8, 64]), ALU.subtract, [Ytm, st1], [Ytm])
            tt("pool", S3, Y3, Y3, ALU.mult, [Ytm], [sqv])
            b.op("dve", lambda e: e.tensor_reduce(out=st2[:], in_=S3, axis=AX.X, op=ALU.add), reads=[sqv], writes=[st2])
            b.op("act", lambda e: e.activation(out=st2[:], in_=st2[:], func=AF.Sqrt, scale=1.0 / 64, bias=64e-5), reads=[st2], writes=[st2])
            b.op("dve", lambda e: e.reciprocal(out=st2[:], in_=st2[:]), reads=[st2], writes=[st2])
            tt("dve", Y3, Y3, st2[:].unsqueeze(2).to_broadcast([64, NCH * 8, 64]), ALU.mult, [Ytm, st2], [Ytm])
            lg = lng[:].rearrange("p (h i) -> p h i", i=64)[:, None, :, :].to_broadcast([64, NCH, 8, 64])
            lb = lnb[:].rearrange("p (h i) -> p h i", i=64)[:, None, :, :].to_broadcast([64, NCH, 8, 64])
            tt("pool", Ytm[:], Ytm[:], lg, ALU.mult, [Ytm, lng], [Ytm])
            tt("dve", Ytm[:], Ytm[:], lb, ALU.add, [Ytm, lnb], [Ytm])
            for h in range(8):
                p = nxt("pq", pq)
                for c in range(NCH):
                    b.op("pe", lambda e: e.transpose(out=p[0:64, c * 64:(c + 1) * 64], in_=Ytm[:, c, h, :], identity=idf[0:64, 0:64]), reads=[Ytm, idf], writes=[p])
                tt("dve", otmp[:], p[0:64, 0:TG], BV[:, h, :], ALU.add, [p, BV], [otmp])
                pg_ = nxt("pp", pp)
                b.op("pe", lambda e: e.matmul(pg_[0:64, 0:TG], lhsT=g2s[:, 0, h * 64:(h + 1) * 64], rhs=xs[:, 2, :], start=True, stop=False), reads=[g2s, xs], writes=[pg_])
                b.op("pe", lambda e: e.matmul(pg_[0:64, 0:TG], lhsT=g2s[:, 1, h * 64:(h + 1) * 64], rhs=xs[:, 3, :], start=False, stop=True), reads=[g2s, xs], writes=[pg_])
                ob_ = obf[h % 2]
                tt("dve", ob_[:], otmp[:], pg_[0:64, 0:TG], ALU.mult, [otmp, pg_], [ob_])
                b.dma("pool", self.obT_d[h // 2, (h % 2) * 64:(h % 2) * 64 + 64, q0:q0 + TG], ob_[:], reads=[ob_], writes=[self.obT_d])
        if "rwkv" in self.debug:
            d = self.dbg_out("obT", [4, 128, S], BF16)
            b.dma("pool", d, self.obT_d[:], reads=[self.obT_d])


Prog.phase_rwkv = _phase_rwkv


def build_full():
    p = Prog()
    b = p.b
    p.alloc_root()
    with b.scope():
        p.alloc_persistent()
        p.phase_nsa_proj()
        p.phase_attn2()
    p.phase_rwkv3()
    p.phase_merge()
    p.phase_ffn2()
    p.finish()
    return p


def kernel(**inputs):
    p = build_full()
    consts = host_consts(inputs["rel_bias"])
    shared = {k: np.ascontiguousarray(np.asarray(inputs[k], np.float32)) for k in W_SPECS if k != "x"}
    shared.update(consts)
    x = np.asarray(inputs["x"], np.float32)
    in_maps = []
    for c in range(8):
        m = dict(shared)
        m["x"] = np.ascontiguousarray(x[c])
        in_maps.append(m)
    res = run_bass_kernel_spmd(p.nc, in_maps, core_ids=list(range(8)))
    return np.stack([np.asarray(r["out"], np.float32) for r in res.results], axis=0)


def _phase_rwkv2(self):
    b = self.b
    I = self.inp
    TG = 128
    NCH = 2
    tt = lambda eng, out, in0, in1, op, rd, wr: b.op(eng, lambda e: e.tensor_tensor(out=out, in0=in0, in1=in1, op=op), reads=rd, writes=wr)
    with b.scope():
        W1 = b.sb("W1", [128, 8, 1792], BF16)
        W2 = b.sb("W2", [128, 8, 1792], BF16)
        with b.scope():
            gat = self.load_gain("gat3", I["attn_norm_g"][0])
            stage = [b.sb(f"rst{i}", [128, 1792], F32) for i in range(2)]
            tmpw = [b.sb(f"rtw{i}", [128, 1792], F32) for i in range(2)]
            mur = self.bcast_row("mur", I["rwkv_mu"][0], 1792)
            for c in range(8):
                st = stage[c % 2]
                tw_ = tmpw[c % 2]
                b.dma("sp", st[:], I["w_in"][0][c * 128:(c + 1) * 128, RW0:RW0 + 1792], writes=[st])
                tt("dve", tw_[:], st[:], mur[:], ALU.mult, [st, mur], [tw_])
                b.op("act", lambda e: e.activation(out=W2[:, c, :], in_=tw_[:], func=AF.Copy, scale=gat[:, c:c + 1]), reads=[tw_, gat], writes=[W2])
                tt("pool", st[:], st[:], tw_[:], ALU.subtract, [st, tw_], [st])
                b.op("act", lambda e: e.activation(out=W1[:, c, :], in_=st[:], func=AF.Copy, scale=gat[:, c:c + 1]), reads=[st, gat], writes=[W1])

        def colvec(name, src, n):
            t = b.sb(name, [64, n], F32)
            b.dma("sp", t[:], src.rearrange("(c p) -> p c", p=64), writes=[t], allow_slow_non_contiguous=True)
            return t
        w0 = colvec("w0", I["rwkv_w0"][0], 8)
        a0 = colvec("a0", I["rwkv_a0"][0], 8)
        k_k = colvec("k_k", I["rwkv_k_k"][0], 8)
        k_a = colvec("k_a", I["rwkv_k_a"][0], 8)
        r_k = colvec("r_k", I["rwkv_r_k"][0].rearrange("h d -> (h d)"), 8)
        w2s = b.sb("w2s", [64, 512], F32)
        a2s = b.sb("a2s", [64, 512], F32)
        g2s = b.sb("g2s", [64, 2, 512], F32)
        b.dma("sp", w2s[:], I["rwkv_w2"][0], writes=[w2s])
        b.dma("sp", a2s[:], I["rwkv_a2"][0], writes=[a2s])
        b.dma("sp", g2s[:], I["rwkv_g2"][0].rearrange("(two l) f -> l two f", two=2), writes=[g2s])
        lng = b.sb("lng", [64, 512], F32)
        lnb = b.sb("lnb", [64, 512], F32)
        b.dma("sp", lng[:], I["rwkv_ln_g"][0].partition_broadcast(64), writes=[lng])
        b.dma("sp", lnb[:], I["rwkv_ln_b"][0].partition_broadcast(64), writes=[lnb])
        msk = b.sb("rmsk", [64, 3, 64], F32)
        b.dma("sp", msk[:], I["rwmask"], writes=[msk])
        rstm = b.sb("rstm", [64, 8 * TG], F32)
        b.dma("sp", rstm[:], I["rwreset"], writes=[rstm])
        ones = b.sb("ones64", [64, 64], F32)
        b.op("pool", lambda e: e.memset(ones[:], 1.0), writes=[ones])
        idf = self.identf
        Hst = b.sb("rH", [64, 2, 8, 64], F32)
        b.op("pool", lambda e: e.memset(Hst[:], 0.0), writes=[Hst])
        xt = [b.sb(f"rxt{i}", [128, D], F32) for i in range(1)] * 2
        junk = b.sb("rjunk", [128, D], BF16)
        ss = [b.sb(f"rss{i}", [128, 1], F32) for i in range(1)] * 2
        hb = [b.sb(f"rhb{i}", [128, D], BF16) for i in range(1)] * 2
        hT1 = [b.sb(f"rhT{i}", [128, 8, 128], BF16) for i in range(1)] * 2
        hTs = b.sb("rhTs", [128, 8, TG + 1], BF16)
        b.op("pool", lambda e: e.memset(hTs[:], 0.0), writes=[hTs])
        XL = b.sb("rXL", [64, 20, TG], F32)
        Vtm = b.sb("rVtm", [64, NCH, 512], F32)
        names = ["LW", "AS", "KKN", "BVc", "KP", "RK", "L", "EP", "EM", "BG", "KG"]
        T = {n: b.sb("r" + n, [64, 8, TG], F32) for n in names}
        T["NR"] = T["RK"]
        T["T1"] = T["BG"]
        T["KK"] = T["KG"]
        T["EX"] = T["L"]
        T["BT"] = T["LW"]
        T["KT"] = T["AS"]
        AR = b.sb("rAR", [64, 8, NCH, 2, 64], F32)
        BON = b.sb("rBON", [64, NCH * 8], F32)
        Ytm = b.sb("rYtm", [64, NCH, 8, 64], F32)
        sqv = b.sb("rsqv", [64, NCH, 8, 64], F32)
        st1 = b.sb("rst1", [64, NCH * 8], F32)
        st2 = b.sb("rst2", [64, NCH * 8], F32)
        TM4 = [b.sb(f"rTM{i}", [64, 4, 2, 64], F32) for i in range(2)]
        XM4 = [b.sb(f"rXM{i}", [64, 4, 4, 64], F32) for i in range(2)]
        AA4 = [b.sb(f"rAA{i}", [64, 4, 2, 64], F32) for i in range(2)]
        PP4 = [b.sb(f"rPP{i}", [64, 4, 64], F32) for i in range(2)]
        Xs4 = b.sb("rXs4", [64, 4, 64], F32)
        Us4 = b.sb("rUs4", [64, 4, 64], F32)
        Ht4 = b.sb("rHt4", [64, 4, 64], F32)
        OBb = b.sb("rOBb", [64, NCH, 512], BF16)
        obT = [b.sb(f"robT{i}", [128, 4, TG], BF16) for i in range(1)] * 2
        pt = b.ps("rpt", [128, 8, 128], BF16)
        pP = b.ps("rpP", [128, 512], F32)
        pA = b.ps("rpA", [128, 1024], F32)
        pB = b.ps("rpB", [128, 512], F32)
        pC = b.ps("rpC", [128, 512], F32)
        pD = b.ps("rpD", [128, 512], F32)
        pZ = b.ps("rpZ", [128, 512], F32)
        cnt = {}

        def nxt(k, lst):
            cnt[k] = cnt.get(k, 0) + 1
            return lst[cnt[k] % len(lst)]
        bc = lambda v: v[:].unsqueeze(2).to_broadcast([64, 8, TG])
        f2 = lambda t_: t_[:].rearrange("p h t -> p (h t)")
        c16 = lambda t_: t_[:].rearrange("p h (c t) -> p (h c) t", t=64)

        ngr = getattr(self, "nrg_limit", S // TG)
        for gi in range(ngr):
            q0 = gi * TG
            i = gi % 2
            self.make_hT(I["x"], gi, xt[i], junk, ss[i], hb[i], pt, hT1[i], self.ident)
            b.op("pool", lambda e: e.tensor_copy(out=hTs[:, :, 0:1], in_=hTs[:, :, TG:TG + 1]), reads=[hTs], writes=[hTs])
            b.op("pool", lambda e: e.tensor_copy(out=hTs[:, :, 1:TG + 1], in_=hT1[i][:]), reads=[hT1[i]], writes=[hTs])
            ftiles = list(range(0, 16)) + [24, 25, 26, 27]
            for q4 in range(5):
                for j in range(4):
                    fc = ftiles[q4 * 4 + j]
                    for c in range(8):
                        b.op("pe", lambda e: e.matmul(pP[0:64, j * TG:(j + 1) * TG], lhsT=W1[:, c, fc * 64:(fc + 1) * 64], rhs=hTs[:, c, 1:TG + 1], start=(c == 0), stop=False),
                             reads=[W1, hTs], writes=[pP])
                    for c in range(8):
                        b.op("pe", lambda e: e.matmul(pP[0:64, j * TG:(j + 1) * TG], lhsT=W2[:, c, fc * 64:(fc + 1) * 64], rhs=hTs[:, c, 0:TG], start=False, stop=(c == 7)),
                             reads=[W2, hTs], writes=[pP])
                b.op("act", lambda e: e.copy(out=XL[:, q4 * 4:(q4 + 1) * 4, :].rearrange("p a t -> p (a t)"), in_=pP[0:64, :]), reads=[pP], writes=[XL])
            for c_ in range(NCH):
                for c in range(8):
                    b.op("pe", lambda e: e.matmul(pP[0:64, :], lhsT=hTs[:, c, 1 + c_ * 64:1 + (c_ + 1) * 64], rhs=W1[:, c, 1024:1536], start=(c == 0), stop=False),
                         reads=[W1, hTs], writes=[pP])
                for c in range(8):
                    b.op("pe", lambda e: e.matmul(pP[0:64, :], lhsT=hTs[:, c, c_ * 64:(c_ + 1) * 64], rhs=W2[:, c, 1024:1536], start=False, stop=(c == 7)),
                         reads=[W2, hTs], writes=[pP])
                b.op("act", lambda e: e.copy(out=Vtm[:, c_, :], in_=pP[0:64, :]), reads=[pP], writes=[Vtm])
            R_ = XL[:, 0:8, :]
            K_ = XL[:, 8:16, :]
            b.op("act", lambda e: e.activation(out=XL[:, 16, :], in_=XL[:, 16, :], func=AF.Tanh), reads=[XL], writes=[XL])
            b.op("act", lambda e: e.activation(out=XL[:, 18:20, :], in_=XL[:, 18:20, :], func=AF.Sigmoid), reads=[XL], writes=[XL])
            for (ws_, src, bias_, dst) in [(w2s, 16, w0, "LW"), (a2s, 17, a0, "AS")]:
                for half in range(2):
                    for j in range(4):
                        h = half * 4 + j
                        b.op("pe", lambda e: e.matmul(pP[0:64, j * TG:(j + 1) * TG], lhsT=ws_[:, h * 64:(h + 1) * 64], rhs=XL[:, src, :], start=True, stop=True),
                             reads=[ws_, XL], writes=[pP])
                    for j in range(4):
                        h = half * 4 + j
                        b.op("act", lambda e: e.activation(out=T[dst][:, h, :], in_=pP[0:64, j * TG:(j + 1) * TG], func=AF.Sigmoid, bias=bias_[:, h:h + 1]),
                             reads=[pP, bias_], writes=[T[dst]])
            b.op("pool", lambda e: e.tensor_scalar_mul(out=f2(T["LW"]), in0=f2(T["LW"]), scalar1=-0.6065306597126334), reads=[T["LW"]], writes=[T["LW"]])
            tt("dve", T["KK"][:], K_, bc(k_k), ALU.mult, [XL, k_k], [T["KK"]])
            tt("pool", T["NR"][:], T["KK"][:], T["KK"][:], ALU.mult, [T["KK"]], [T["NR"]])
            for half in range(2):
                b.op("pe", lambda e: e.matmul(pP[0:64, :], lhsT=ones[:], rhs=T["NR"][:, half * 4:(half + 1) * 4, :].rearrange("p h t -> p (h t)"), start=True, stop=True),
                     reads=[ones, T["NR"]], writes=[pP])
                b.op("act", lambda e: e.activation(out=T["KKN"][:, half * 4:(half + 1) * 4, :].rearrange("p h t -> p (h t)"), in_=pP[0:64, :], func=AF.Sqrt),
                     reads=[pP], writes=[T["KKN"]])
            b.op("dve", lambda e: e.tensor_scalar_max(out=f2(T["KKN"]), in0=f2(T["KKN"]), scalar1=1e-12), reads=[T["KKN"]], writes=[T["KKN"]])
            b.op("dve", lambda e: e.reciprocal(out=f2(T["KKN"]), in_=f2(T["KKN"])), reads=[T["KKN"]], writes=[T["KKN"]])
            tt("dve", T["KKN"][:], T["KKN"][:], T["KK"][:], ALU.mult, [T["KKN"], T["KK"]], [T["KKN"]])
            tt("pool", T["BVc"][:], T["KKN"][:], T["AS"][:], ALU.mult, [T["KKN"], T["AS"]], [T["BVc"]])
            b.op("pool", lambda e: e.tensor_scalar_add(out=f2(T["T1"]), in0=f2(T["AS"]), scalar1=-1.0), reads=[T["AS"]], writes=[T["T1"]])
            tt("pool", T["T1"][:], T["T1"][:], bc(k_a), ALU.mult, [T["T1"], k_a], [T["T1"]])
            b.op("dve", lambda e: e.scalar_tensor_tensor(out=f2(T["KP"]), in0=f2(T["T1"]), scalar=1.0, in1=K_.rearrange("p h t -> p (h t)"), op0=ALU.add, op1=ALU.mult),
                 reads=[T["T1"], XL], writes=[T["KP"]])
            tt("pool", T["RK"][:], R_, T["KP"][:], ALU.mult, [XL, T["KP"]], [T["RK"]])
            tt("pool", T["RK"][:], T["RK"][:], bc(r_k), ALU.mult, [T["RK"], r_k], [T["RK"]])
            for c_ in range(NCH):
                for h in range(8):
                    b.op("pe", lambda e: e.matmul(pD[0:64, c_ * 8 + h:c_ * 8 + h + 1], lhsT=T["RK"][:, h, c_ * 64:(c_ + 1) * 64], rhs=ones[:, 0:1], start=True, stop=True),
                         reads=[T["RK"], ones], writes=[pD])
            b.op("act", lambda e: e.copy(out=BON[:], in_=pD[0:64, 0:NCH * 8]), reads=[pD], writes=[BON])
            b.op("dve", lambda e: e.tensor_tensor_scan(out=f2(T["L"]), data0=rstm[:], data1=f2(T["LW"]), initial=0.0, op0=ALU.mult, op1=ALU.add),
                 reads=[rstm, T["LW"]], writes=[T["L"]])
            b.op("act", lambda e: e.activation(out=f2(T["EP"]), in_=f2(T["L"]), func=AF.Exp), reads=[T["L"]], writes=[T["EP"]])
            b.op("act", lambda e: e.activation(out=f2(T["EM"]), in_=f2(T["L"]), func=AF.Exp, scale=-1.0), reads=[T["L"]], writes=[T["EM"]])
            tt("pool", T["L"][:], T["L"][:], T["LW"][:], ALU.subtract, [T["L"], T["LW"]], [T["L"]])
            b.op("act", lambda e: e.activation(out=f2(T["EX"]), in_=f2(T["L"]), func=AF.Exp), reads=[T["L"]], writes=[T["EX"]])
            ar0 = AR[:, :, :, 0, :].rearrange("p h c t -> p (h c) t")
            ar1 = AR[:, :, :, 1, :].rearrange("p h c t -> p (h c) t")
            b.op("dve", lambda e: e.scalar_tensor_tensor(out=ar0, in0=c16(T["KKN"]), scalar=-1.0, in1=c16(T["EX"]), op0=ALU.mult, op1=ALU.mult),
                 reads=[T["KKN"], T["EX"]], writes=[AR])
            tt("pool", ar1, R_.rearrange("p h (c t) -> p (h c) t", t=64), c16(T["EP"]), ALU.mult, [XL, T["EP"]], [AR])
            tt("dve", T["BT"][:], T["BVc"][:], T["EM"][:], ALU.mult, [T["BVc"], T["EM"]], [T["BT"]])
            tt("pool", T["KT"][:], T["KP"][:], T["EM"][:], ALU.mult, [T["KP"], T["EM"]], [T["KT"]])
            gC = c16(T["EP"])[:, :, 63:64].to_broadcast([64, 16, 64])
            tt("dve", c16(T["BG"]), c16(T["BT"]), gC, ALU.mult, [T["BT"], T["EP"]], [T["BG"]])
            tt("pool", c16(T["KG"]), c16(T["KT"]), gC, ALU.mult, [T["KT"], T["EP"]], [T["KG"]])
            for c_ in range(NCH):
                cs = slice(c_ * 64, (c_ + 1) * 64)
                cur = (gi * NCH + c_) % 2
                for hb_ in range(2):
                    heads = list(range(hb_ * 4, hb_ * 4 + 4))
                    for j, h in enumerate(heads):
                        b.op("pe", lambda e: e.transpose(out=pC[0:64, j * 128:j * 128 + 64], in_=T["BG"][:, h, cs], identity=idf[0:64, 0:64]), reads=[T["BG"], idf], writes=[pC])
                        b.op("pe", lambda e: e.transpose(out=pC[0:64, j * 128 + 64:(j + 1) * 128], in_=T["KG"][:, h, cs], identity=idf[0:64, 0:64]), reads=[T["KG"], idf], writes=[pC])
                    tm = nxt("tm", TM4)
                    b.op("act", lambda e: e.copy(out=tm[:].rearrange("p h a t -> p (h a t)"), in_=pC[0:64, 0:512]), reads=[pC], writes=[tm])
                    for j, h in enumerate(heads):
                        arc = AR[:, h, c_, :, :].rearrange("p a t -> p (a t)")
                        b.op("pe", lambda e: e.matmul(pA[0:64, j * 256:j * 256 + 128], lhsT=T["BT"][:, h, cs], rhs=arc, start=True, stop=True), reads=[T["BT"], AR], writes=[pA])
                        b.op("pe", lambda e: e.matmul(pA[0:64, j * 256 + 128:(j + 1) * 256], lhsT=T["KT"][:, h, cs], rhs=arc, start=True, stop=True), reads=[T["KT"], AR], writes=[pA])
                        b.op("pe", lambda e: e.matmul(pB[0:64, j * 64:(j + 1) * 64], lhsT=AR[:, h, c_, 0, :], rhs=T["BT"][:, h, cs], start=True, stop=True), reads=[T["BT"], AR], writes=[pB])
                    xm = nxt("xm", XM4)
                    tt("dve", xm[:].rearrange("p h (a m) t -> p (h a) m t", a=2), pA[0:64, :].rearrange("p (ha m t) -> p ha m t", m=2, t=64),
                       msk[:, None, 0:2, :].to_broadcast([64, 8, 2, 64]), ALU.mult, [pA, msk], [xm])
                    aa = nxt("aa", AA4)
                    b.op("pool", lambda e: e.tensor_copy(out=aa[:, :, 0, :], in_=xm[:, :, 0, :]), reads=[xm], writes=[aa])
                    tt("dve", aa[:, :, 1, :], pB[0:64, 0:256].rearrange("p (h t) -> p h t", t=64), msk[:, 2:3, :].to_broadcast([64, 4, 64]), ALU.mult, [pB, msk], [aa])
                    P_ = nxt("pp4", PP4)
                    tt("pool", P_[:], xm[:, :, 0, :], idf[0:64, None, 0:64].to_broadcast([64, 4, 64]), ALU.add, [xm, idf], [P_])
                    for step in range(5):
                        for j in range(4):
                            b.op("pe", lambda e: e.matmul(pD[0:64, j * 128:j * 128 + 64], lhsT=aa[:, j, 1, :], rhs=aa[:, j, 0, :], start=True, stop=True), reads=[aa], writes=[pD])
                            b.op("pe", lambda e: e.matmul(pD[0:64, j * 128 + 64:(j + 1) * 128], lhsT=aa[:, j, 0, :], rhs=aa[:, j, 1, :], start=True, stop=True), reads=[aa], writes=[pD])
                        aa2 = nxt("aa", AA4)
                        b.op("act", lambda e: e.copy(out=aa2[:].rearrange("p h a t -> p (h a t)"), in_=pD[0:64, :]), reads=[pD], writes=[aa2])
                        for j in range(4):
                            b.op("pe", lambda e: e.matmul(pB[0:64, 256 + j * 64:256 + (j + 1) * 64], lhsT=aa2[:, j, 1, :], rhs=P_[:, j, :], start=True, stop=True), reads=[aa2, P_], writes=[pB])
                        P2 = nxt("pp4", PP4)
                        tt("dve", P2[:], pB[0:64, 256:512].rearrange("p (h t) -> p h t", t=64), P_[:], ALU.add, [pB, P_], [P2])
                        aa, P_ = aa2, P2
                    for j, h in enumerate(heads):
                        b.op("pe", lambda e: e.matmul(pZ[0:64, j * 64:(j + 1) * 64], lhsT=xm[:, j, 2, :], rhs=Vtm[:, c_, h * 64:(h + 1) * 64], start=True, stop=False), reads=[xm, Vtm], writes=[pZ])
                        b.op("pe", lambda e: e.matmul(pZ[0:64, j * 64:(j + 1) * 64], lhsT=AR[:, h, c_, 0, :], rhs=Hst[:, cur, h, :], start=False, stop=True), reads=[AR, Hst], writes=[pZ])
                    b.op("act", lambda e: e.copy(out=Xs4[:].rearrange("p h t -> p (h t)"), in_=pZ[0:64, 0:256]), reads=[pZ], writes=[Xs4])
                    for j in range(4):
                        b.op("pe", lambda e: e.matmul(pZ[0:64, 256 + j * 64:256 + (j + 1) * 64], lhsT=P_[:, j, :], rhs=Xs4[:, j, :], start=True, stop=True), reads=[P_, Xs4], writes=[pZ])
                    b.op("act", lambda e: e.copy(out=Us4[:].rearrange("p h t -> p (h t)"), in_=pZ[0:64, 256:512]), reads=[pZ], writes=[Us4])
                    for j, h in enumerate(heads):
                        o = slice(j * 64, (j + 1) * 64)
                        vh = Vtm[:, c_, h * 64:(h + 1) * 64]
                        b.op("pe", lambda e: e.matmul(pZ[0:64, o], lhsT=AR[:, h, c_, 1, :], rhs=Hst[:, cur, h, :], start=True, stop=False), reads=[AR, Hst], writes=[pZ])
                        b.op("pe", lambda e: e.matmul(pZ[0:64, o], lhsT=xm[:, j, 1, :], rhs=Us4[:, j, :], start=False, stop=False), reads=[xm, Us4], writes=[pZ])
                        b.op("pe", lambda e: e.matmul(pZ[0:64, o], lhsT=xm[:, j, 3, :], rhs=vh, start=False, stop=True), reads=[xm, Vtm], writes=[pZ])
                    for j, h in enumerate(heads):
                        o = slice(256 + j * 64, 256 + (j + 1) * 64)
                        vh = Vtm[:, c_, h * 64:(h + 1) * 64]
                        b.op("pe", lambda e: e.matmul(pZ[0:64, o], lhsT=tm[:, j, 0, :], rhs=Us4[:, j, :], start=True, stop=False), reads=[tm, Us4], writes=[pZ])
                        b.op("pe", lambda e: e.matmul(pZ[0:64, o], lhsT=tm[:, j, 1, :], rhs=vh, start=False, stop=True), reads=[tm, Vtm], writes=[pZ])
                    b.op("act", lambda e: e.copy(out=Ytm[:, c_, hb_ * 4:(hb_ + 1) * 4, :].rearrange("p h t -> p (h t)"), in_=pZ[0:64, 0:256]), reads=[pZ], writes=[Ytm])
                    gH = T["EP"][:, hb_ * 4:(hb_ + 1) * 4, c_ * 64 + 63:c_ * 64 + 64].to_broadcast([64, 4, 64])
                    tt("pool", Ht4[:], Hst[:, cur, hb_ * 4:(hb_ + 1) * 4, :], gH, ALU.mult, [Hst, T["EP"]], [Ht4])
                    tt("dve", Hst[:, 1 - cur, hb_ * 4:(hb_ + 1) * 4, :], pZ[0:64, 256:512].rearrange("p (h t) -> p h t", t=64), Ht4[:], ALU.add, [pZ, Ht4], [Hst])
            Y3 = Ytm[:].rearrange("p c h i -> p (c h) i")
            S3 = sqv[:].rearrange("p c h i -> p (c h) i")
            b.op("dve", lambda e: e.tensor_reduce(out=st1[:], in_=Y3, axis=AX.X, op=ALU.add), reads=[Ytm], writes=[st1])
            b.op("pool", lambda e: e.tensor_scalar_mul(out=st1[:], in0=st1[:], scalar1=1.0 / 64), reads=[st1], writes=[st1])
            tt("dve", Y3, Y3, st1[:].unsqueeze(2).to_broadcast([64, NCH * 8, 64]), ALU.subtract, [Ytm, st1], [Ytm])
            tt("pool", S3, Y3, Y3, ALU.mult, [Ytm], [sqv])
            b.op("dve", lambda e: e.tensor_reduce(out=st2[:], in_=S3, axis=AX.X, op=ALU.add), reads=[sqv], writes=[st2])
            b.op("act", lambda e: e.activation(out=st2[:], in_=st2[:], func=AF.Sqrt, scale=1.0 / 64, bias=64e-5), reads=[st2], writes=[st2])
            b.op("dve", lambda e: e.reciprocal(out=st2[:], in_=st2[:]), reads=[st2], writes=[st2])
            tt("dve", Y3, Y3, st2[:].unsqueeze(2).to_broadcast([64, NCH * 8, 64]), ALU.mult, [Ytm, st2], [Ytm])
            lg = lng[:].rearrange("p (h i) -> p h i", i=64)[:, None, :, :].to_broadcast([64, NCH, 8, 64])
            lb = lnb[:].rearrange("p (h i) -> p h i", i=64)[:, None, :, :].to_broadcast([64, NCH, 8, 64])
            tt("pool", Ytm[:], Ytm[:], lg, ALU.mult, [Ytm, lng], [Ytm])
            tt("dve", Ytm[:], Ytm[:], lb, ALU.add, [Ytm, lnb], [Ytm])
            V3 = Vtm[:].rearrange("p c (h i) -> p (c h) i", i=64)
            tt("pool", S3, V3, BON[:].unsqueeze(2).to_broadcast([64, NCH * 8, 64]), ALU.mult, [Vtm, BON], [sqv])
            tt("dve", Y3, Y3, S3, ALU.add, [Ytm, sqv], [Ytm])
            for c_ in range(NCH):
                for two in range(2):
                    b.op("pe", lambda e: e.matmul(pP[0:64, :], lhsT=XL[:, 18 + two, c_ * 64:(c_ + 1) * 64], rhs=g2s[:, two, :], start=(two == 0), stop=(two == 1)),
                         reads=[XL, g2s], writes=[pP])
                tt("dve", OBb[:, c_, :], Ytm[:, c_, :, :].rearrange("p h i -> p (h i)"), pP[0:64, :], ALU.mult, [Ytm, pP], [OBb])
                for k4 in range(4):
                    b.op("pe", lambda e: e.transpose(out=pt[:, k4, c_ * 64:(c_ + 1) * 64], in_=OBb[:, c_, k4 * 128:(k4 + 1) * 128], identity=self.ident[0:64, 0:64]),
                         reads=[OBb, self.ident], writes=[pt])
            ot = obT[gi % 2]
            b.op("act", lambda e: e.copy(out=ot[:], in_=pt[:, 0:4, :]), reads=[pt], writes=[ot])
            b.dma("pool", self.obT_d[:, :, q0:q0 + TG].rearrange("c p t -> p c t"), ot[:], reads=[ot], writes=[self.obT_d])
        if "rwkv" in self.debug:
            d = self.dbg_out("obT", [4, 128, S], BF16)
            b.dma("pool", d, self.obT_d[:], reads=[self.obT_d])


Prog.phase_rwkv2 = _phase_rwkv2


def _phase_rwkv3(self):
    b = self.b
    I = self.inp
    TG = 128
    NCH = 2
    CHDT = mybir.dt.float32r if getattr(self, "use_f32r", True) else F32
    tt = lambda eng, out, in0, in1, op, rd, wr: b.op(eng, lambda e: e.tensor_tensor(out=out, in0=in0, in1=in1, op=op), reads=rd, writes=wr)
    with b.scope():
        W1 = b.sb("W1", [128, 8, 1792], BF16)
        with b.scope():
            gat = self.load_gain("gat3", I["attn_norm_g"][0])
            stage = [b.sb(f"rst{i}", [128, 1792], F32) for i in range(2)]
            self.load_weight(W1, I["w_in"][0][:, RW0:RW0 + 1792], 1792, gvec=gat, stage=stage)

        def colvec(name, src, n):
            t = b.sb(name, [64, n], F32)
            b.dma("sp", t[:], src.rearrange("(c p) -> p c", p=64), writes=[t], allow_slow_non_contiguous=True)
            return t
        mu = colvec("mu", I["rwkv_mu"][0], 28)
        w0 = colvec("w0", I["rwkv_w0"][0], 8)
        a0 = colvec("a0", I["rwkv_a0"][0], 8)
        k_k = colvec("k_k", I["rwkv_k_k"][0], 8)
        k_a = colvec("k_a", I["rwkv_k_a"][0], 8)
        r_k = colvec("r_k", I["rwkv_r_k"][0].rearrange("h d -> (h d)"), 8)
        w2s = b.sb("w2s", [64, 512], F32)
        a2s = b.sb("a2s", [64, 512], F32)
        g2s = b.sb("g2s", [64, 2, 512], F32)
        b.dma("sp", w2s[:], I["rwkv_w2"][0], writes=[w2s])
        b.dma("sp", a2s[:], I["rwkv_a2"][0], writes=[a2s])
        b.dma("sp", g2s[:], I["rwkv_g2"][0].rearrange("(two l) f -> l two f", two=2), writes=[g2s])
        lng = b.sb("lng", [64, 512], F32)
        lnb = b.sb("lnb", [64, 512], F32)
        b.dma("sp", lng[:], I["rwkv_ln_g"][0].partition_broadcast(64), writes=[lng])
        b.dma("sp", lnb[:], I["rwkv_ln_b"][0].partition_broadcast(64), writes=[lnb])
        msk = b.sb("rmsk", [64, 3, 64], F32)
        b.dma("sp", msk[:], I["rwmask"], writes=[msk])
        rstm = b.sb("rstm", [64, 8 * TG], F32)
        b.dma("sp", rstm[:], I["rwreset"], writes=[rstm])
        ones = b.sb("ones64", [64, 64], F32)
        b.op("pool", lambda e: e.memset(ones[:], 1.0), writes=[ones])
        idf = self.identf
        Hst = b.sb("rH", [64, 2, 8, 64], CHDT)
        b.op("pool", lambda e: e.memset(Hst[:].bitcast(F32), 0.0), writes=[Hst])
        xt = [b.sb(f"rxt{i}", [128, D], F32) for i in range(1)] * 2
        junk = b.sb("rjunk", [128, D], BF16)
        ss = [b.sb(f"rss{i}", [128, 1], F32) for i in range(1)] * 2
        hb = [b.sb(f"rhb{i}", [128, D], BF16) for i in range(1)] * 2
        hT1 = [b.sb(f"rhT{i}", [128, 8, 128], BF16) for i in range(1)] * 2
        PB = b.sb("rPB", [64, 28, TG + 1], F32)
        b.op("pool", lambda e: e.memset(PB[:], 0.0), writes=[PB])
        XL = b.sb("rXL", [64, 28, TG], F32)
        VT2 = [b.sb(f"rVtm{i}", [64, NCH, 512], CHDT) for i in range(2)]
        SXG2 = [b.sb(f"rSXG{i}", [64, 2, TG], F32) for i in range(2)]
        names = ["LW", "AS", "KKN", "BVc", "KP", "RK", "L", "EP", "EM", "BG", "KG"]
        T = {n: b.sb("r" + n, [64, 8, TG], F32) for n in names}
        T["NR"] = T["RK"]
        T["T1"] = T["BG"]
        T["KK"] = T["KG"]
        T["EX"] = T["L"]
        T["BT"] = b.sb("rBTr", [64, 8, TG], CHDT)
        T["KT"] = b.sb("rKTr", [64, 8, TG], CHDT)
        AR = b.sb("rAR", [64, 8, NCH, 2, 64], CHDT)
        BON2 = [b.sb(f"rBON{i}", [64, NCH * 8], F32) for i in range(2)]
        Ytm = b.sb("rYtm", [64, NCH, 8, 64], F32)
        sqv = b.sb("rsqv", [64, NCH, 8, 64], F32)
        st1 = b.sb("rst1", [64, NCH * 8], F32)
        st2 = b.sb("rst2", [64, NCH * 8], F32)
        TM4 = [b.sb(f"rTM{i}", [64, 4, 2, 64], CHDT) for i in range(2)]
        XM4 = [b.sb(f"rXM{i}", [64, 4, 4, 64], CHDT) for i in range(2)]
        AA4 = [[b.sb(f"rAA{u}_{i}", [64, 4, 2, 64], CHDT) for i in range(2)] for u in range(2)]
        PP4 = [[b.sb(f"rPP{u}_{i}", [64, 4, 64], CHDT) for i in range(2)] for u in range(2)]
        Xs8 = b.sb("rXs8", [64, 8, 64], CHDT)
        Us8 = b.sb("rUs8", [64, 8, 64], CHDT)
        Ht8 = b.sb("rHt8", [64, 8, 64], F32)
        OBb = b.sb("rOBb", [64, NCH, 512], BF16)
        obT = [b.sb(f"robT{i}", [128, 4, TG], BF16) for i in range(1)] * 2
        pt = b.ps("rpt", [128, 8, 128], BF16)
        pP = b.ps("rpP", [128, 512], F32)
        pA = b.ps("rpA", [128, 1024], F32)
        pB = b.ps("rpB", [128, 512], F32)
        pC = b.ps("rpC", [128, 512], F32)
        pD = b.ps("rpD", [128, 512], F32)
        pZ = b.ps("rpZ", [128, 512], F32)
        cnt = {}

        def nxt(k, lst):
            cnt[k] = cnt.get(k, 0) + 1
            return lst[cnt[k] % len(lst)]
        bc = lambda v: v[:].unsqueeze(2).to_broadcast([64, 8, TG])
        f2 = lambda t_: t_[:].rearrange("p h t -> p (h t)")
        c16 = lambda t_: t_[:].rearrange("p h (c t) -> p (h c) t", t=64)

        ngr = getattr(self, "nrg_limit", S // TG)
        RR = lambda ap: ap

        def emit_inproj_head(gi):
            i = gi % 2
            self.make_hT(I["x"], gi, xt[i], junk, ss[i], hb[i], pt, hT1[i], self.ident)
            b.op("dve", lambda e: e.tensor_copy(out=PB[:, :, 0:1], in_=PB[:, :, TG:TG + 1]), reads=[PB], writes=[PB])

        def emit_inproj_rounds(gi, rounds):
            i = gi % 2
            for r7 in rounds:
                for j in range(4):
                    fc = r7 * 4 + j
                    for c in range(8):
                        b.op("pe", lambda e: e.matmul(pP[0:64, j * TG:(j + 1) * TG], lhsT=W1[:, c, fc * 64:(fc + 1) * 64], rhs=hT1[i][:, c, :], start=(c == 0), stop=(c == 7)),
                             reads=[W1, hT1[i]], writes=[pP])
                b.op("act", lambda e: e.copy(out=PB[:, r7 * 4:(r7 + 1) * 4, 1:TG + 1], in_=pP[0:64, :].rearrange("p (a t) -> p a t", t=TG)), reads=[pP], writes=[PB])

        emit_inproj_head(0)
        emit_inproj_rounds(0, range(7))
        def prep(gi, hook=None):
            Vtm, BON, SXG = VT2[gi % 2], BON2[gi % 2], SXG2[gi % 2]
            tt("dve", XL[:], PB[:, :, 0:TG], PB[:, :, 1:TG + 1], ALU.subtract, [PB], [XL])
            tt("dve", XL[:], XL[:], mu[:].unsqueeze(2).to_broadcast([64, 28, TG]), ALU.mult, [XL, mu], [XL])
            tt("dve", XL[:], XL[:], PB[:, :, 1:TG + 1], ALU.add, [XL, PB], [XL])
            if gi + 1 < ngr:
                emit_inproj_head(gi + 1)
            for c_ in range(NCH):
                for h in range(8):
                    b.op("pe", lambda e: e.transpose(out=pC[0:64, h * 64:(h + 1) * 64], in_=XL[:, 16 + h, c_ * 64:(c_ + 1) * 64], identity=idf[0:64, 0:64]), reads=[XL, idf], writes=[pC])
                b.op("act", lambda e: e.copy(out=Vtm[:, c_, :], in_=pC[0:64, :]), reads=[pC], writes=[Vtm])
            R_ = XL[:, 0:8, :]
            K_ = XL[:, 8:16, :]
            b.op("act", lambda e: e.activation(out=XL[:, 24, :], in_=XL[:, 24, :], func=AF.Tanh), reads=[XL], writes=[XL])
            b.op("act", lambda e: e.activation(out=SXG[:], in_=XL[:, 26:28, :], func=AF.Sigmoid), reads=[XL], writes=[SXG])
            for (ws_, src, bias_, dst) in [(w2s, 24, w0, "LW"), (a2s, 25, a0, "AS")]:
                for half in range(2):
                    for j in range(4):
                        h = half * 4 + j
                        b.op("pe", lambda e: e.matmul(pP[0:64, j * TG:(j + 1) * TG], lhsT=ws_[:, h * 64:(h + 1) * 64], rhs=XL[:, src, :], start=True, stop=True),
                             reads=[ws_, XL], writes=[pP])
                    for j in range(4):
                        h = half * 4 + j
                        b.op("act", lambda e: e.activation(out=T[dst][:, h, :], in_=pP[0:64, j * TG:(j + 1) * TG], func=AF.Sigmoid, bias=bias_[:, h:h + 1]),
                             reads=[pP, bias_], writes=[T[dst]])
            b.op("dve", lambda e: e.tensor_scalar_mul(out=f2(T["LW"]), in0=f2(T["LW"]), scalar1=-0.6065306597126334), reads=[T["LW"]], writes=[T["LW"]])
            tt("dve", T["KK"][:], K_, bc(k_k), ALU.mult, [XL, k_k], [T["KK"]])
            tt("dve", T["NR"][:], T["KK"][:], T["KK"][:], ALU.mult, [T["KK"]], [T["NR"]])
            for half in range(2):
                b.op("pe", lambda e: e.matmul(pP[0:64, :], lhsT=ones[:], rhs=T["NR"][:, half * 4:(half + 1) * 4, :].rearrange("p h t -> p (h t)"), start=True, stop=True),
                     reads=[ones, T["NR"]], writes=[pP])
                b.op("act", lambda e: e.activation(out=T["KKN"][:, half * 4:(half + 1) * 4, :].rearrange("p h t -> p (h t)"), in_=pP[0:64, :], func=AF.Sqrt),
                     reads=[pP], writes=[T["KKN"]])
            if gi + 1 < ngr:
                emit_inproj_rounds(gi + 1, range(0, 4))
            b.op("dve", lambda e: e.tensor_scalar_max(out=f2(T["KKN"]), in0=f2(T["KKN"]), scalar1=1e-12), reads=[T["KKN"]], writes=[T["KKN"]])
            b.op("dve", lambda e: e.reciprocal(out=f2(T["KKN"]), in_=f2(T["KKN"])), reads=[T["KKN"]], writes=[T["KKN"]])
            tt("dve", T["KKN"][:], T["KKN"][:], T["KK"][:], ALU.mult, [T["KKN"], T["KK"]], [T["KKN"]])
            tt("dve", T["BVc"][:], T["KKN"][:], T["AS"][:], ALU.mult, [T["KKN"], T["AS"]], [T["BVc"]])
            b.op("dve", lambda e: e.tensor_scalar_add(out=f2(T["T1"]), in0=f2(T["AS"]), scalar1=-1.0), reads=[T["AS"]], writes=[T["T1"]])
            tt("dve", T["T1"][:], T["T1"][:], bc(k_a), ALU.mult, [T["T1"], k_a], [T["T1"]])
            b.op("dve", lambda e: e.scalar_tensor_tensor(out=f2(T["KP"]), in0=f2(T["T1"]), scalar=1.0, in1=K_.rearrange("p h t -> p (h t)"), op0=ALU.add, op1=ALU.mult),
                 reads=[T["T1"], XL], writes=[T["KP"]])
            tt("dve", T["RK"][:], R_, T["KP"][:], ALU.mult, [XL, T["KP"]], [T["RK"]])
            tt("dve", T["RK"][:], T["RK"][:], bc(r_k), ALU.mult, [T["RK"], r_k], [T["RK"]])
            for c_ in range(NCH):
                for h in range(8):
                    b.op("pe", lambda e: e.matmul(pD[0:64, c_ * 8 + h:c_ * 8 + h + 1], lhsT=T["RK"][:, h, c_ * 64:(c_ + 1) * 64], rhs=ones[:, 0:1], start=True, stop=True),
                         reads=[T["RK"], ones], writes=[pD])
            b.op("act", lambda e: e.copy(out=BON[:], in_=pD[0:64, 0:NCH * 8]), reads=[pD], writes=[BON])
            if gi + 1 < ngr:
                emit_inproj_rounds(gi + 1, range(4, 7))
            b.op("dve", lambda e: e.tensor_tensor_scan(out=f2(T["L"]), data0=rstm[:], data1=f2(T["LW"]), initial=0.0, op0=ALU.mult, op1=ALU.add),
                 reads=[rstm, T["LW"]], writes=[T["L"]])
            b.op("act", lambda e: e.activation(out=f2(T["EP"]), in_=f2(T["L"]), func=AF.Exp), reads=[T["L"]], writes=[T["EP"]])
            b.op("act", lambda e: e.activation(out=f2(T["EM"]), in_=f2(T["L"]), func=AF.Exp, scale=-1.0), reads=[T["L"]], writes=[T["EM"]])
            tt("dve", T["L"][:], T["L"][:], T["LW"][:], ALU.subtract, [T["L"], T["LW"]], [T["L"]])
            b.op("act", lambda e: e.activation(out=f2(T["EX"]), in_=f2(T["L"]), func=AF.Exp), reads=[T["L"]], writes=[T["EX"]])
            ar0 = AR[:, :, :, 0, :].rearrange("p h c t -> p (h c) t")
            ar1 = AR[:, :, :, 1, :].rearrange("p h c t -> p (h c) t")
            b.op("dve", lambda e: e.scalar_tensor_tensor(out=ar0, in0=c16(T["KKN"]), scalar=-1.0, in1=c16(T["EX"]), op0=ALU.mult, op1=ALU.mult),
                 reads=[T["KKN"], T["EX"]], writes=[AR])
            tt("dve", ar1, R_.rearrange("p h (c t) -> p (h c) t", t=64), c16(T["EP"]), ALU.mult, [XL, T["EP"]], [AR])
            tt("dve", T["BT"][:], T["BVc"][:], T["EM"][:], ALU.mult, [T["BVc"], T["EM"]], [T["BT"]])
            tt("dve", T["KT"][:], T["KP"][:], T["EM"][:], ALU.mult, [T["KP"], T["EM"]], [T["KT"]])
            gC = c16(T["EP"])[:, :, 63:64].to_broadcast([64, 16, 64])
            tt("dve", c16(T["BG"]), c16(T["BT"]), gC, ALU.mult, [T["BT"], T["EP"]], [T["BG"]])
            tt("dve", c16(T["KG"]), c16(T["KT"]), gC, ALU.mult, [T["KT"], T["EP"]], [T["KG"]])

        def chains(gi, hook=None):
            Vtm, BON, SXG = VT2[gi % 2], BON2[gi % 2], SXG2[gi % 2]
            for c_ in range(NCH):
                cs = slice(c_ * 64, (c_ + 1) * 64)
                cur = (gi * NCH + c_) % 2
                U_ = []
                for u in range(2):
                    heads = list(range(u * 4, u * 4 + 4))
                    pBu = pB if u == 0 else pC
                    for j, h in enumerate(heads):
                        b.op("pe", lambda e: e.transpose(out=pZ[0:64, j * 128:j * 128 + 64], in_=T["BG"][:, h, cs], identity=idf[0:64, 0:64]), reads=[T["BG"], idf], writes=[pZ])
                        b.op("pe", lambda e: e.transpose(out=pZ[0:64, j * 128 + 64:(j + 1) * 128], in_=T["KG"][:, h, cs], identity=idf[0:64, 0:64]), reads=[T["KG"], idf], writes=[pZ])
                    tm = TM4[u]
                    b.op("act", lambda e: e.copy(out=tm[:].rearrange("p h a t -> p (h a t)"), in_=pZ[0:64, 0:512]), reads=[pZ], writes=[tm])
                    for j, h in enumerate(heads):
                        arc = AR[:, h, c_, :, :].rearrange("p a t -> p (a t)")
                        b.op("pe", lambda e: e.matmul(pA[0:64, j * 256:j * 256 + 128], lhsT=T["BT"][:, h, cs], rhs=arc, start=True, stop=True), reads=[T["BT"], AR], writes=[pA])
                        b.op("pe", lambda e: e.matmul(pA[0:64, j * 256 + 128:(j + 1) * 256], lhsT=T["KT"][:, h, cs], rhs=arc, start=True, stop=True), reads=[T["KT"], AR], writes=[pA])
                        b.op("pe", lambda e: e.matmul(pBu[0:64, j * 64:(j + 1) * 64], lhsT=AR[:, h, c_, 0, :], rhs=T["BT"][:, h, cs], start=True, stop=True), reads=[T["BT"], AR], writes=[pBu])
                    xm = XM4[u]
                    tt("dve", xm[:].rearrange("p h (a m) t -> p (h a) m t", a=2), pA[0:64, :].rearrange("p (ha m t) -> p ha m t", m=2, t=64),
                       msk[:, None, 0:2, :].to_broadcast([64, 8, 2, 64]), ALU.mult, [pA, msk], [xm])
                    aa = AA4[u][0]
                    b.op("dve", lambda e: e.tensor_copy(out=aa[:, :, 0, :], in_=xm[:, :, 0, :]), reads=[xm], writes=[aa])
                    tt("dve", aa[:, :, 1, :], pBu[0:64, 0:256].rearrange("p (h t) -> p h t", t=64), msk[:, 2:3, :].to_broadcast([64, 4, 64]), ALU.mult, [pBu, msk], [aa])
                    P_ = PP4[u][0]
                    tt("dve", P_[:], xm[:, :, 0, :], idf[0:64, None, 0:64].to_broadcast([64, 4, 64]), ALU.add, [xm, idf], [P_])
                    U_.append(dict(tm=tm, xm=xm, aa=aa, P=P_, pB=pBu, pD=(pD if u == 0 else pP), k=0))
                for step in range(5):
                    if hook is not None and c_ == 0:
                        next(hook, None)
                        next(hook, None)
                    for u_ in U_:
                        aa, pDu = u_["aa"], u_["pD"]
                        for j in range(4):
                            b.op("pe", lambda e: e.matmul(pDu[0:64, j * 128:j * 128 + 64], lhsT=RR(aa[:, j, 1, :]), rhs=RR(aa[:, j, 0, :]), start=True, stop=True), reads=[aa], writes=[pDu])
                            b.op("pe", lambda e: e.matmul(pDu[0:64, j * 128 + 64:(j + 1) * 128], lhsT=RR(aa[:, j, 0, :]), rhs=RR(aa[:, j, 1, :]), start=True, stop=True), reads=[aa], writes=[pDu])
                    for ui, u_ in enumerate(U_):
                        u_["k"] += 1
                        aa2 = AA4[ui][u_["k"] % 2]
                        b.op("act", lambda e: e.copy(out=aa2[:].rearrange("p h a t -> p (h a t)"), in_=u_["pD"][0:64, :]), reads=[u_["pD"]], writes=[aa2])
                        u_["aa"] = aa2
                    for u_ in U_:
                        for j in range(4):
                            b.op("pe", lambda e: e.matmul(u_["pB"][0:64, 256 + j * 64:256 + (j + 1) * 64], lhsT=RR(u_["aa"][:, j, 1, :]), rhs=RR(u_["P"][:, j, :]), start=True, stop=True),
                                 reads=[u_["aa"], u_["P"]], writes=[u_["pB"]])
                    for ui, u_ in enumerate(U_):
                        P2 = PP4[ui][u_["k"] % 2]
                        tt("dve", P2[:], u_["pB"][0:64, 256:512].rearrange("p (h t) -> p h t", t=64), u_["P"][:], ALU.add, [u_["pB"], u_["P"]], [P2])
                        u_["P"] = P2
                if hook is not None and c_ == 0:
                    for _ in hook:
                        pass
                for h in range(8):
                    u_, j = U_[h // 4], h % 4
                    o = slice(h * 64, (h + 1) * 64)
                    b.op("pe", lambda e: e.matmul(pA[0:64, o], lhsT=u_["xm"][:, j, 2, :], rhs=Vtm[:, c_, o], start=True, stop=False), reads=[u_["xm"], Vtm], writes=[pA])
                    b.op("pe", lambda e: e.matmul(pA[0:64, o], lhsT=AR[:, h, c_, 0, :], rhs=Hst[:, cur, h, :], start=False, stop=True), reads=[AR, Hst], writes=[pA])
                b.op("act", lambda e: e.copy(out=Xs8[:].rearrange("p h t -> p (h t)"), in_=pA[0:64, 0:512]), reads=[pA], writes=[Xs8])
                for h in range(8):
                    u_, j = U_[h // 4], h % 4
                    b.op("pe", lambda e: e.matmul(pA[0:64, 512 + h * 64:512 + (h + 1) * 64], lhsT=u_["P"][:, j, :], rhs=Xs8[:, h, :], start=True, stop=True), reads=[u_["P"], Xs8], writes=[pA])
                b.op("act", lambda e: e.copy(out=Us8[:].rearrange("p h t -> p (h t)"), in_=pA[0:64, 512:1024]), reads=[pA], writes=[Us8])
                for h in range(8):
                    u_, j = U_[h // 4], h % 4
                    o = slice(h * 64, (h + 1) * 64)
                    b.op("pe", lambda e: e.matmul(pD[0:64, o], lhsT=u_["tm"][:, j, 0, :], rhs=Us8[:, h, :], start=True, stop=False), reads=[u_["tm"], Us8], writes=[pD])
                    b.op("pe", lambda e: e.matmul(pD[0:64, o], lhsT=u_["tm"][:, j, 1, :], rhs=Vtm[:, c_, o], start=False, stop=True), reads=[u_["tm"], Vtm], writes=[pD])
                tt("dve", Ht8[:], Hst[:, cur, :, :], T["EP"][:, :, c_ * 64 + 63:c_ * 64 + 64].to_broadcast([64, 8, 64]), ALU.mult, [Hst, T["EP"]], [Ht8])
                tt("dve", Hst[:, 1 - cur, :, :], pD[0:64, :].rearrange("p (h t) -> p h t", t=64), Ht8[:], ALU.add, [pD, Ht8], [Hst])
                for h in range(8):
                    u_, j = U_[h // 4], h % 4
                    o = slice(h * 64, (h + 1) * 64)
                    b.op("pe", lambda e: e.matmul(pZ[0:64, o], lhsT=AR[:, h, c_, 1, :], rhs=Hst[:, cur, h, :], start=True, stop=False), reads=[AR, Hst], writes=[pZ])
                    b.op("pe", lambda e: e.matmul(pZ[0:64, o], lhsT=u_["xm"][:, j, 1, :], rhs=Us8[:, h, :], start=False, stop=False), reads=[u_["xm"], Us8], writes=[pZ])
                    b.op("pe", lambda e: e.matmul(pZ[0:64, o], lhsT=u_["xm"][:, j, 3, :], rhs=Vtm[:, c_, o], start=False, stop=True), reads=[u_["xm"], Vtm], writes=[pZ])
                b.op("act", lambda e: e.copy(out=Ytm[:, c_, :, :].rearrange("p h t -> p (h t)"), in_=pZ[0:64, :]), reads=[pZ], writes=[Ytm])

        def post(gi):
            q0 = gi * TG
            Vtm, BON, SXG = VT2[gi % 2], BON2[gi % 2], SXG2[gi % 2]
            Y3 = Ytm[:].rearrange("p c h i -> p (c h) i")
            S3 = sqv[:].rearrange("p c h i -> p (c h) i")
            b.op("dve", lambda e: e.tensor_reduce(out=st1[:], in_=Y3, axis=AX.X, op=ALU.add), reads=[Ytm], writes=[st1])
            b.op("dve", lambda e: e.tensor_scalar_mul(out=st1[:], in0=st1[:], scalar1=1.0 / 64), reads=[st1], writes=[st1])
            yield
            tt("dve", Y3, Y3, st1[:].unsqueeze(2).to_broadcast([64, NCH * 8, 64]), ALU.subtract, [Ytm, st1], [Ytm])
            tt("dve", S3, Y3, Y3, ALU.mult, [Ytm], [sqv])
            yield
            b.op("dve", lambda e: e.tensor_reduce(out=st2[:], in_=S3, axis=AX.X, op=ALU.add), reads=[sqv], writes=[st2])
            b.op("act", lambda e: e.activation(out=st2[:], in_=st2[:], func=AF.Sqrt, scale=1.0 / 64, bias=64e-5), reads=[st2], writes=[st2])
            yield
            b.op("dve", lambda e: e.reciprocal(out=st2[:], in_=st2[:]), reads=[st2], writes=[st2])
            tt("dve", Y3, Y3, st2[:].unsqueeze(2).to_broadcast([64, NCH * 8, 64]), ALU.mult, [Ytm, st2], [Ytm])
            yield
            lg = lng[:].rearrange("p (h i) -> p h i", i=64)[:, None, :, :].to_broadcast([64, NCH, 8, 64])
            lb = lnb[:].rearrange("p (h i) -> p h i", i=64)[:, None, :, :].to_broadcast([64, NCH, 8, 64])
            tt("dve", Ytm[:], Ytm[:], lg, ALU.mult, [Ytm, lng], [Ytm])
            tt("dve", Ytm[:], Ytm[:], lb, ALU.add, [Ytm, lnb], [Ytm])
            yield
            V3 = Vtm[:].rearrange("p c (h i) -> p (c h) i", i=64)
            tt("dve", S3, V3, BON[:].unsqueeze(2).to_broadcast([64, NCH * 8, 64]), ALU.mult, [Vtm, BON], [sqv])
            tt("dve", Y3, Y3, S3, ALU.add, [Ytm, sqv], [Ytm])
            yield
            for c_ in range(NCH):
                for two in range(2):
                    b.op("pe", lambda e: e.matmul(pP[0:64, :], lhsT=SXG[:, two, c_ * 64:(c_ + 1) * 64], rhs=g2s[:, two, :], start=(two == 0), stop=(two == 1)),
                         reads=[SXG, g2s], writes=[pP])
                tt("dve", OBb[:, c_, :], Ytm[:, c_, :, :].rearrange("p h i -> p (h i)"), pP[0:64, :], ALU.mult, [Ytm, pP], [OBb])
                for k4 in range(4):
                    b.op("pe", lambda e: e.transpose(out=pt[:, k4, c_ * 64:(c_ + 1) * 64], in_=OBb[:, c_, k4 * 128:(k4 + 1) * 128], identity=self.ident[0:64, 0:64]),
                         reads=[OBb, self.ident], writes=[pt])
                yield
            ot = obT[gi % 2]
            b.op("act", lambda e: e.copy(out=ot[:], in_=pt[:, 0:4, :]), reads=[pt], writes=[ot])
            b.dma("pool", self.obT_d[:, :, q0:q0 + TG].rearrange("c p t -> p c t"), ot[:], reads=[ot], writes=[self.obT_d])

        prep(0)
        for gi in range(ngr):
            pg = post(gi - 1) if gi > 0 else None
            chains(gi, hook=pg)
            if pg is not None:
                for _ in pg:
                    pass
            if gi + 1 < ngr:
                prep(gi + 1)
        for _ in post(ngr - 1):
            pass
        if "rwkv" in self.debug:
            d = self.dbg_out("obT", [4, 128, S], BF16)
            b.dma("pool", d, self.obT_d[:], reads=[self.obT_d])


Prog.phase_rwkv3 = _phase_rwkv3


def _phase_ffn2(self):
    b = self.b
    I = self.inp
    TG = 256
    NFT = 44
    with b.scope():
        gf = self.load_gain("gf", I["ffn_norm_g"][0])
        stage = [b.sb(f"fst{i}", [128, 1024], F32) for i in range(2)]
        wu = b.sb("wu", [128, 8, 2 * DFF], BF16)
        for n in range(8):
            for c in range(8):
                st = stage[c % 2]
                b.dma("sp", st[:, 0:704], I["w_up"][0][c * 128:(c + 1) * 128, n * 704:(n + 1) * 704], writes=[st])
                b.op("act", lambda e: e.activation(out=wu[:, c, n * 704:(n + 1) * 704], in_=st[:, 0:704], func=AF.Copy, scale=gf[:, c:c + 1]),
                     reads=[st, gf], writes=[wu])
        wd = b.sb("wd", [128, 22, D], BF16)
        self.load_weight(wd, I["w_down"][0], D, kch=22, stage=stage, eng="dve")
        cw = b.sb("cw", [128, 3, NFT], F32)
        for j in range(3):
            b.dma("sp", cw[:, j, :], I["conv_w"][0][j].rearrange("(c p) -> p c", p=128), writes=[cw], allow_slow_non_contiguous=True)
        cbias = self.load_gain("cbias", I["conv_b"][0], kch=NFT)
        xt = [b.sb(f"fxt{i}", [128, D], F32) for i in range(2)]
        junk = b.sb("fjunk", [128, D], BF16)
        ss = [b.sb(f"fss{i}", [128, 1], F32) for i in range(2)]
        hb = [b.sb(f"fhb{i}", [128, D], BF16) for i in range(2)]
        hTg = b.sb("fhTg", [128, 8, TG + 2], BF16)
        b.op("pool", lambda e: e.memset(hTg[:], 0.0), writes=[hTg])
        cv = [b.sb(f"cv{i}", [128, TG], F32) for i in range(3)]
        sgl = [b.sb(f"sgl{i}", [128, TG], BF16) for i in range(2)]
        actT = b.sb("actT", [128, 22, TG], BF16)
        val = b.sb("fval", [128, 22, TG], BF16)
        pt = b.ps("fpt", [128, 8, 128], BF16)
        pu = [b.ps(f"fpu{i}", [128, 512], F32) for i in range(4)]
        pd = [b.ps(f"fpd{i}", [128, 512], F32) for i in range(2)]
        ng = getattr(self, "nt_limit", NT) * 128 // TG
        for gi in range(ng):
            b.op("pool", lambda e: e.tensor_copy(out=hTg[:, :, 0:2], in_=hTg[:, :, TG:TG + 2]), reads=[hTg], writes=[hTg])
            for s_ in range(TG // 128):
                t = gi * (TG // 128) + s_
                self.make_hT(self.x1_d, t, xt[s_], junk, ss[s_], hb[s_], pt, hTg, self.ident, hT_ap=hTg[:, :, 2 + s_ * 128:2 + (s_ + 1) * 128])
            for ft in range(NFT):
                p = pu[ft % 4]
                c_ = cv[ft % 3]
                for c in range(8):
                    b.op("pe", lambda e: e.matmul(p[:, 0:TG + 2], lhsT=wu[:, c, ft * 128:(ft + 1) * 128], rhs=hTg[:, c, :], start=(c == 0), stop=(c == 7)),
                         reads=[wu, hTg], writes=[p])
                b.op("act", lambda e: e.activation(out=c_[:], in_=p[:, 0:TG], func=AF.Identity, scale=cw[:, 0, ft:ft + 1], bias=cbias[:, ft:ft + 1]),
                     reads=[p, cw, cbias], writes=[c_])
                b.op("dve", lambda e: e.scalar_tensor_tensor(out=c_[:], in0=p[:, 1:TG + 1], scalar=cw[:, 1, ft:ft + 1], in1=c_[:], op0=ALU.mult, op1=ALU.add),
                     reads=[p, cw, c_], writes=[c_])
                if ft < 22:
                    b.op("dve", lambda e: e.scalar_tensor_tensor(out=val[:, ft, :], in0=p[:, 2:TG + 2], scalar=cw[:, 2, ft:ft + 1], in1=c_[:], op0=ALU.mult, op1=ALU.add),
                         reads=[p, cw, c_], writes=[val])
                else:
                    sg_ = sgl[ft % 2]
                    b.op("dve", lambda e: e.scalar_tensor_tensor(out=c_[:], in0=p[:, 2:TG + 2], scalar=cw[:, 2, ft:ft + 1], in1=c_[:], op0=ALU.mult, op1=ALU.add),
                         reads=[p, cw, c_], writes=[c_])
                    b.op("act", lambda e: e.activation(out=sg_[:], in_=c_[:], func=AF.Silu), reads=[c_], writes=[sg_])
                    b.op("pool", lambda e: e.tensor_tensor(out=actT[:, ft - 22, :], in0=sg_[:], in1=val[:, ft - 22, :], op=ALU.mult),
                         reads=[sg_, val], writes=[actT])
            for s_ in range(TG // 128):
                t = gi * (TG // 128) + s_
                for n in range(2):
                    for f in range(22):
                        b.op("pe", lambda e: e.matmul(pd[n][:, :], lhsT=actT[:, f, s_ * 128:(s_ + 1) * 128], rhs=wd[:, f, n * 512:(n + 1) * 512], start=(f == 0), stop=(f == 21)),
                             reads=[actT, wd], writes=[pd[n]])
                    b.op("dve", lambda e: e.tensor_tensor(out=xt[s_][:, n * 512:(n + 1) * 512], in0=pd[n][:, :], in1=xt[s_][:, n * 512:(n + 1) * 512], op=ALU.add),
                         reads=[pd[n], xt[s_]], writes=[xt[s_]])
                b.dma("pool", self.out[t * 128:(t + 1) * 128, :], xt[s_][:], reads=[xt[s_]])


Prog.phase_ffn2 = _phase_ffn2
```

```python
import contextlib
import numpy as np
import ml_dtypes
import concourse.bass as bass
import concourse.mybir as mybir
from concourse.bass_utils import run_bass_kernel_spmd

F32 = mybir.dt.float32
BF16 = mybir.dt.bfloat16
AF = mybir.ActivationFunctionType
ALU = mybir.AluOpType
AX = mybir.AxisListType

S = 4096
D = 1024
NT = S // 128
IN_WIDTH = 5144
RW0 = 1304
GA0 = 3096
GB0 = 4120
DFF = 2816
RMS_EPS = 1e-6


class Buf:
    def __init__(self, t, name):
        self.t = t
        self.name = name
        self.w = None
        self.r = {}
        self.psum = False

    def __getitem__(self, idx):
        return self.t[idx]


class Builder:
    SEM_ROLL = 30000

    def __init__(self, nc):
        self.nc = nc
        self.stack = contextlib.ExitStack()
        self.root = self.stack
        self.eng = {"pe": nc.tensor, "act": nc.scalar, "dve": nc.vector,
                    "pool": nc.gpsimd, "sp": nc.sync}
        self.sem = {}
        self.cnt = {}
        self.seen = {e: {} for e in self.eng}
        self.nsem = 0
        self.lanes = {}
        self.lane_rr = {}
        self.last_tok = {}
        for e in self.eng:
            self._roll(e)

    def newsem(self, name):
        self.nsem += 1
        return self.root.enter_context(self.nc.semaphore(f"{name}_{self.nsem}"))

    def sb(self, name, shape, dt=F32):
        self.nsem += 1
        name = f"sb{self.nsem}_{name}"
        return Buf(self.stack.enter_context(self.nc.sbuf_tensor(name, list(shape), dt)), name)

    def ps(self, name, shape, dt=F32):
        self.nsem += 1
        name = f"ps{self.nsem}_{name}"
        bf = Buf(self.stack.enter_context(self.nc.psum_tensor(name, list(shape), dt)), name)
        bf.psum = True
        return bf

    def dram(self, name, shape, dt=F32, kind="Internal"):
        return Buf(self.nc.dram_tensor(name, list(shape), dt, kind=kind), name)

    def _roll(self, e):
        self.sem[e] = self.newsem("s" + e)
        self.cnt[e] = 0

    def _wait(self, e, tok):
        sem, val = tok
        k = id(sem)
        if self.seen[e].get(k, 0) < val:
            self.eng[e].wait_ge(sem, val)
            self.seen[e][k] = val

    def _deps(self, e, reads, writes):
        for b in reads:
            if b.w is not None:
                we, tok = b.w
                self._wait(e, tok)
            if b.psum:
                for re_, tok in b.r.items():
                    if re_ != e:
                        self._wait(e, tok)
        for b in writes:
            if b.w is not None:
                we, tok = b.w
                if we != e:
                    self._wait(e, tok)
            for re_, tok in b.r.items():
                if re_ != e:
                    self._wait(e, tok)

    def op(self, e, fn, reads=(), writes=()):
        if self.cnt[e] >= self.SEM_ROLL:
            self._roll(e)
        self._deps(e, reads, writes)
        ins = fn(self.eng[e])
        self.cnt[e] += 1
        tok = (self.sem[e], self.cnt[e])
        ins.then_inc(self.sem[e], 1)
        self.last_tok[e] = tok
        for b in reads:
            b.r[e] = tok
        for b in writes:
            b.w = (e, tok)
            b.r = {}
        return tok

    def dma(self, q, out, in_, reads=(), writes=(), nlanes=6, **kw):
        if q not in self.lanes:
            self.lanes[q] = [[self.newsem("l" + q), 0] for _ in range(nlanes)]
            self.lane_rr[q] = 0
        li = self.lane_rr[q]
        self.lane_rr[q] = (li + 1) % len(self.lanes[q])
        lane = self.lanes[q][li]
        if lane[1] >= 1800:
            self._wait(q, (lane[0], 16 * lane[1]))
            lane[0] = self.newsem("l" + q)
            lane[1] = 0
        if lane[1] > 0:
            self._wait(q, (lane[0], 16 * lane[1]))
        self._deps_dma(q, reads, writes)
        ins = self.eng[q].dma_start(out=out, in_=in_, **kw)
        lane[1] += 1
        tok = (lane[0], 16 * lane[1])
        ins.then_inc(lane[0], 16)
        key = "dma_" + q + str(li)
        for b in reads:
            b.r[key] = tok
        for b in writes:
            b.w = (key, tok)
            b.r = {}
        return tok

    def _deps_dma(self, q, reads, writes):
        for b in reads:
            if b.w is not None:
                self._wait(q, b.w[1])
        for b in writes:
            if b.w is not None:
                self._wait(q, b.w[1])
            for re_, tok in b.r.items():
                self._wait(q, tok)

    def barrier(self):
        toks = list(self.last_tok.values())
        for q, lanes in self.lanes.items():
            for lane in lanes:
                if lane[1] > 0:
                    toks.append((lane[0], 16 * lane[1]))
        for e in self.eng:
            for tok in toks:
                self._wait(e, tok)

    def wait_all_on(self, e):
        toks = list(self.last_tok.values())
        for q, lanes in self.lanes.items():
            for lane in lanes:
                if lane[1] > 0:
                    toks.append((lane[0], 16 * lane[1]))
        for tok in toks:
            self._wait(e, tok)

    @contextlib.contextmanager
    def scope(self):
        old = self.stack
        self.stack = contextlib.ExitStack()
        try:
            yield
            self.barrier()
        finally:
            self.stack.close()
            self.stack = old

    def close(self):
        self.stack.close()


NEG = -30000.0


def _bucket(dist):
    n = np.maximum(dist, 0)
    ratio = np.log(np.maximum(n, 1).astype(np.float32) / np.float32(16.0)) / np.float32(np.log(8.0))
    large = np.minimum(16 + (ratio * 16).astype(np.int32), 31)
    return np.where(n < 16, n, large)


def host_consts(rel_bias):
    rel = np.asarray(rel_bias, np.float32)
    c = {}
    c["ident"] = np.eye(128, dtype=np.float32).astype(ml_dtypes.bfloat16)
    c["identf"] = np.eye(128, dtype=np.float32)
    kp = np.arange(128)[:, None]
    cc = np.arange(640)[None, :]
    dist = cc - kp
    bt = rel[_bucket(dist)]
    tw = np.where(((dist >= 0) & (dist < 512))[..., None], bt, np.float32(NEG))
    ts = np.where((dist >= 0)[..., None], bt, np.float32(NEG))
    c["tw"] = np.ascontiguousarray(tw.transpose(0, 2, 1)).astype(np.float32)
    c["ts"] = np.ascontiguousarray(ts.transpose(0, 2, 1)).astype(np.float32)
    cidx = np.arange(256)[:, None]
    qidx = np.arange(S)[None, :]
    dc = qidx - 16 * cidx - 31
    bcg = rel[_bucket(dc)]
    ok = (dc >= 0) & (cidx < 255)
    bc = np.where(ok[..., None], bcg, np.float32(NEG))
    c["biasc"] = np.ascontiguousarray(bc.transpose(2, 0, 1)).reshape(8, 2, 128, S).astype(np.float32)
    A = np.zeros((256, 64), np.float32)
    Wt = (1, 2, 2, 2, 1)
    for ci in range(255):
        for j in range(64):
            o = ci + 1 - 4 * j
            if 0 <= o <= 4:
                A[ci, j] = Wt[o]
    c["amat"] = A.reshape(2, 128, 64)
    E = np.zeros((64, S), np.float32)
    E[np.arange(S) // 64, np.arange(S)] = 1.0
    c["emat"] = E.astype(ml_dtypes.bfloat16)
    qp = np.arange(128)[:, None, None]
    qt = np.arange(32)[None, :, None]
    j = np.arange(64)[None, None, :]
    cur = (128 * qt + qp) // 64
    cand = (j >= 1) & (j <= cur - 2)
    c["candneg"] = np.where(cand, 0.0, -1e9).astype(np.float32)
    c["fz"] = ((j == 0) | (j == cur) | (j == cur - 1)).astype(np.float32)
    tri = np.triu(np.ones((64, 64), np.float32))
    c["rwmask"] = np.ascontiguousarray(np.stack([np.triu(np.ones((64, 64), np.float32), 1), tri, np.tril(np.ones((64, 64), np.float32), -1)], axis=1))
    rr = np.ones((64, 1024), np.float32)
    rr[:, ::64] = 0.0
    c["rwreset"] = rr
    c["b31"] = np.ascontiguousarray(np.broadcast_to(rel[31][None, :], (128, 8))).astype(np.float32)
    return c


CONST_SPECS = {
    "ident": ([128, 128], BF16), "identf": ([128, 128], F32),
    "tw": ([128, 8, 640], F32), "ts": ([128, 8, 640], F32),
    "biasc": ([8, 2, 128, S], F32), "amat": ([2, 128, 64], F32),
    "emat": ([64, S], BF16), "candneg": ([128, 32, 64], F32), "fz": ([128, 32, 64], F32),
    "b31": ([128, 8], F32), "rwmask": ([64, 3, 64], F32), "rwreset": ([64, 1024], F32),
}

W_SPECS = {
    "x": [S, D], "attn_norm_g": [1, D], "w_in": [1, D, IN_WIDTH], "q_norm_g": [1, 64], "k_norm_g": [1, 64],
    "cmp_pe_k": [1, 32, 64], "cmp_w1_k": [1, 2048, 256], "cmp_w2_k": [1, 256, 64],
    "cmp_pe_v": [1, 32, 64], "cmp_w1_v": [1, 2048, 256], "cmp_w2_v": [1, 256, 64],
    "rwkv_mu": [1, 1792], "rwkv_w0": [1, 512], "rwkv_w2": [1, 64, 512], "rwkv_a0": [1, 512],
    "rwkv_a2": [1, 64, 512], "rwkv_g2": [1, 128, 512], "rwkv_k_k": [1, 512], "rwkv_k_a": [1, 512],
    "rwkv_r_k": [1, 8, 64], "rwkv_ln_g": [1, 512], "rwkv_ln_b": [1, 512],
    "w_proj_a": [1, 512, D], "w_proj_b": [1, 512, D], "w_out": [1, D, D], "ffn_norm_g": [1, D],
    "w_up": [1, D, 2 * DFF], "conv_w": [1, 3, 2 * DFF], "conv_b": [1, 2 * DFF], "w_down": [1, DFF, D],
}


class Prog:
    def __init__(self, debug=()):
        self.debug = set(debug)
        nc = bass.Bass("TRN2", target_bir_lowering=False)
        self.nc = nc
        self.inp = {}
        for k, shp in W_SPECS.items():
            self.inp[k] = nc.dram_tensor(k, list(shp), F32, kind="ExternalInput").ap()
        for k, (shp, dt) in CONST_SPECS.items():
            self.inp[k] = nc.dram_tensor(k, list(shp), dt, kind="ExternalInput").ap()
        self.out = nc.dram_tensor("out", [S, D], F32, kind="ExternalOutput").ap()
        self.dbg = {}
        self.b = Builder(nc)

    def dbg_out(self, name, shape, dt=F32):
        t = self.nc.dram_tensor("dbg_" + name, list(shape), dt, kind="ExternalOutput").ap()
        self.dbg[name] = t
        return t

    def load_weight(self, dst, src, ncols, gvec=None, kch=8, stage=None, eng="act"):
        b = self.b
        for c in range(kch):
            st = stage[c % len(stage)]
            b.dma("sp", st[:, :ncols], src[c * 128:(c + 1) * 128, :], writes=[st])
            if gvec is not None:
                b.op(eng, lambda e: e.activation(out=dst[:, c, :], in_=st[:, :ncols], func=AF.Copy, scale=gvec[:, c:c + 1])
                     if eng == "act" else e.tensor_scalar_mul(out=dst[:, c, :], in0=st[:, :ncols], scalar1=gvec[:, c:c + 1]),
                     reads=[st, gvec], writes=[dst])
            else:
                b.op(eng, lambda e: e.copy(out=dst[:, c, :], in_=st[:, :ncols]) if eng == "act"
                     else e.tensor_copy(out=dst[:, c, :], in_=st[:, :ncols]), reads=[st], writes=[dst])

    def load_gain(self, name, src_vec, kch=8):
        b = self.b
        g = b.sb(name, [128, kch], F32)
        b.dma("sp", g[:], src_vec.rearrange("(c p) -> p c", p=128), writes=[g], allow_slow_non_contiguous=True)
        return g

    def bcast_row(self, name, src_row, n):
        b = self.b
        t = b.sb(name, [128, n], F32)
        b.dma("sp", t[:], src_row.partition_broadcast(128), writes=[t])
        return t

    def make_hT(self, x_ap, t, xt, junk, ss, hb, pt, hT, ident, hT_ap=None):
        b = self.b
        b.dma("sp", xt[:], x_ap[t * 128:(t + 1) * 128, :], writes=[xt])
        b.op("act", lambda e: e.activation(out=junk[:], in_=xt[:], func=AF.Square, accum_out=ss[:]), reads=[xt], writes=[junk, ss])
        b.op("act", lambda e: e.activation(out=ss[:], in_=ss[:], func=AF.Sqrt, scale=1.0 / D, bias=RMS_EPS), reads=[ss], writes=[ss])
        b.op("dve", lambda e: e.reciprocal(out=ss[:], in_=ss[:]), reads=[ss], writes=[ss])
        b.op("dve", lambda e: e.tensor_scalar_mul(out=hb[:], in0=xt[:], scalar1=ss[:]), reads=[xt, ss], writes=[hb])
        for c in range(8):
            b.op("pe", lambda e: e.transpose(out=pt[:, c, :], in_=hb[:, c * 128:(c + 1) * 128], identity=ident[:]),
                 reads=[hb, ident], writes=[pt])
        b.op("act", lambda e: e.copy(out=(hT[:] if hT_ap is None else hT_ap), in_=pt[:]), reads=[pt], writes=[hT])

    def alloc_root(self):
        b = self.b
        I = self.inp
        self.ident = b.sb("ident", [128, 128], BF16)
        b.dma("sp", self.ident[:], I["ident"], writes=[self.ident])
        self.identf = b.sb("identf", [128, 128], F32)
        b.dma("sp", self.identf[:], I["identf"], writes=[self.identf])

    def alloc_persistent(self):
        b = self.b
        I = self.inp
        if not hasattr(self, "ident"):
            self.alloc_root()
        self.ksE = b.sb("ksE", [128, 2, S], BF16)
        self.kwT = b.sb("kwT", [64, 2, S], BF16)
        self.vaug_s = b.sb("vaug_s", [128, NT, 2, 65], BF16)
        self.vaug_w = b.sb("vaug_w", [128, NT, 2, 65], BF16)
        self.gts = b.sb("gts", [128, NT, 24], F32)
        self.kcT = b.sb("kcT", [64, 2, 256], BF16)
        self.vcA = b.sb("vcA", [128, 2, 2, 129], F32)
        self.qT_d = b.dram("qT_d", [8, 64, S], BF16)
        self.oaT_d = b.dram("oaT_d", [4, 128, S], BF16)
        self.obT_d = b.dram("obT_d", [4, 128, S], BF16)
        for g in range(2):
            b.dma("sp", self.ksE[64:128, g, :], I["emat"], writes=[self.ksE])
        b.op("pool", lambda e: e.memset(self.vaug_s[:, :, :, 64:65], 1.0), writes=[self.vaug_s])
        b.op("pool", lambda e: e.memset(self.vaug_w[:, :, :, 64:65], 1.0), writes=[self.vaug_w])
        b.op("pool", lambda e: e.memset(self.vcA[:, :, :, 64:65], 1.0), writes=[self.vcA])
        for g in range(2):
            for ct in range(2):
                b.dma("sp", self.vcA[:, g, ct, 65:129], I["amat"][ct], writes=[self.vcA])

    def phase_nsa_proj(self):
        b = self.b
        I = self.inp
        with b.scope():
            gat = self.load_gain("gat", I["attn_norm_g"][0])
            wn = b.sb("wn", [128, 8, RW0], BF16)
            stage = [b.sb(f"wst{i}", [128, RW0], F32) for i in range(2)]
            self.load_weight(wn, I["w_in"][0][:, 0:RW0], RW0, gvec=gat, stage=stage)
            gq = self.bcast_row("gq", I["q_norm_g"][0], 64)
            gk = self.bcast_row("gk", I["k_norm_g"][0], 64)
            gq_rep = b.sb("gq_rep", [128, 8, 64], F32)
            gk_rep = b.sb("gk_rep", [128, 2, 64], F32)
            b.op("act", lambda e: e.activation(out=gq_rep[:], in_=gq[:, None, :].to_broadcast([128, 8, 64]), func=AF.Copy, scale=0.125),
                 reads=[gq], writes=[gq_rep])
            b.op("act", lambda e: e.activation(out=gk_rep[:], in_=gk[:, None, :].to_broadcast([128, 2, 64]), func=AF.Copy, scale=1.0),
                 reads=[gk], writes=[gk_rep])
            if getattr(self, 'stop_at', 99) <= 0:
                return
            kcdup = b.sb("kcdup", [128, 2, S + 1], BF16)
            vcdup = b.sb("vcdup", [128, 2, S + 1], BF16)
            xt = [b.sb(f"xt{i}", [128, D], F32) for i in range(2)]
            junk = b.sb("junk", [128, D], BF16)
            ss = [b.sb(f"ss{i}", [128, 1], F32) for i in range(2)]
            hb = [b.sb(f"hb{i}", [128, D], BF16) for i in range(2)]
            hT = [b.sb(f"hT{i}", [128, 8, 128], BF16) for i in range(2)]
            sq_ = [b.sb(f"sq{i}", [128, 12, 64], F32) for i in range(2)]
            ssq_ = [b.sb(f"ssq{i}", [128, 12], F32) for i in range(2)]
            tmpq_ = [b.sb(f"tmpq{i}", [128, 8, 64], F32) for i in range(2)]
            tmpk_ = [b.sb(f"tmpk{i}", [128, 4, 64], F32) for i in range(2)]
            qb_ = [b.sb(f"qb{i}", [128, 512], BF16) for i in range(2)]
            kb_ = [b.sb(f"kb{i}", [128, 4, 64], BF16) for i in range(2)]
            cb_ = [b.sb(f"cb{i}", [128, 4, 2, 64], BF16) for i in range(2)]
            qst = [b.sb(f"qst{i}", [64, 8, 128], BF16) for i in range(2)]
            pt = b.ps("pt", [128, 8, 128], BF16)
            pm = [b.ps(f"pm{i}", [128, 512], F32) for i in range(3)]
            ptq_ = [b.ps(f"ptq{i}", [128, 8, 128], BF16) for i in range(2)]
            ptk_ = [b.ps(f"ptk{i}", [128, 8, 128], BF16) for i in range(2)]
            colgroups = [(0, 512), (512, 1024), (1024, RW0)]
            pmS = [[b.sb(f"pmS{i}_{n}", [128, 512], F32) for n in range(3)] for i in range(2)]
            ntl = getattr(self, 'nt_limit', NT)

            def stageA(t):
                    i = t % 2
                    self.make_hT(I["x"], t, xt[i], junk, ss[i], hb[i], pt, hT[i], self.ident)
                    sq, ssq, tmpq, tmpk, qb, kb, cb, ptq, ptk = sq_[i], ssq_[i], tmpq_[i], tmpk_[i], qb_[i], kb_[i], cb_[i], ptq_[i], ptk_[i]
                    for n, (c0, c1) in enumerate(colgroups):
                        for c in range(8):
                            b.op("pe", lambda e: e.matmul(pm[n][:, :c1 - c0], lhsT=hT[i][:, c, :], rhs=wn[:, c, c0:c1],
                                                          start=(c == 0), stop=(c == 7)), reads=[hT[i], wn], writes=[pm[n]])

            def stageA2(t):
                    i = t % 2
                    b.op("act", lambda e: e.copy(out=pmS[i][0][:], in_=pm[0][:]), reads=[pm[0]], writes=[pmS[i][0]])
                    b.op("dve", lambda e: e.tensor_copy(out=pmS[i][1][:], in_=pm[1][:]), reads=[pm[1]], writes=[pmS[i][1]])
                    b.op("act", lambda e: e.copy(out=pmS[i][2][:, 0:RW0 - 1024], in_=pm[2][:, 0:RW0 - 1024]), reads=[pm[2]], writes=[pmS[i][2]])

            def stageB(t):
                    i = t % 2
                    sq, ssq, tmpq, tmpk, qb, kb, cb, ptq, ptk = sq_[i], ssq_[i], tmpq_[i], tmpk_[i], qb_[i], kb_[i], cb_[i], ptq_[i], ptk_[i]
                    b.op("act", lambda e: e.activation(out=sq[:, 0:8, :], in_=pmS[i][0][:, 0:512].rearrange("p (h d) -> p h d", d=64), func=AF.Square),
                         reads=[pmS[i][0]], writes=[sq])
                    b.op("act", lambda e: e.activation(out=sq[:, 8:10, :], in_=pmS[i][1][:, 256:384].rearrange("p (h d) -> p h d", d=64), func=AF.Square),
                         reads=[pmS[i][1]], writes=[sq])
                    b.op("act", lambda e: e.activation(out=sq[:, 10:12, :], in_=pmS[i][2][:, 0:128].rearrange("p (h d) -> p h d", d=64), func=AF.Square),
                         reads=[pmS[i][2]], writes=[sq])
                    b.op("dve", lambda e: e.tensor_reduce(out=ssq[:], in_=sq[:], axis=AX.X, op=ALU.add), reads=[sq], writes=[ssq])
                    b.op("act", lambda e: e.activation(out=ssq[:], in_=ssq[:], func=AF.Sqrt, scale=1.0 / 64, bias=RMS_EPS), reads=[ssq], writes=[ssq])
                    b.op("dve", lambda e: e.reciprocal(out=ssq[:], in_=ssq[:]), reads=[ssq], writes=[ssq])
                    if getattr(self, 'stop_at', 99) <= 2:
                        return
                    b.op("dve", lambda e: e.tensor_tensor(out=tmpq[:], in0=pmS[i][0][:, 0:512].rearrange("p (h d) -> p h d", d=64),
                                                          in1=ssq[:, 0:8].unsqueeze(2).to_broadcast([128, 8, 64]), op=ALU.mult),
                         reads=[pmS[i][0], ssq], writes=[tmpq])
                    b.op("pool", lambda e: e.tensor_tensor(out=qb[:].rearrange("p (h d) -> p h d", d=64), in0=tmpq[:], in1=gq_rep[:], op=ALU.mult),
                         reads=[tmpq, gq_rep], writes=[qb])
                    for h in range(8):
                        b.op("pe", lambda e: e.transpose(out=ptq[0:64, h, :], in_=qb[:, h * 64:(h + 1) * 64], identity=self.ident[:]),
                             reads=[qb, self.ident], writes=[ptq])
                    b.op("act", lambda e: e.copy(out=qst[i][:], in_=ptq[0:64, :, :]), reads=[ptq], writes=[qst[i]])
                    b.dma("pool", self.qT_d[:, :, t * 128:(t + 1) * 128].rearrange("h d t -> d h t"), qst[i][:], reads=[qst[i]], writes=[self.qT_d])
                    if getattr(self, 'stop_at', 99) <= 3:
                        return
                    b.op("dve", lambda e: e.tensor_tensor(out=tmpk[:, 0:2, :], in0=pmS[i][1][:, 256:384].rearrange("p (h d) -> p h d", d=64),
                                                          in1=ssq[:, 8:10].unsqueeze(2).to_broadcast([128, 2, 64]), op=ALU.mult),
                         reads=[pmS[i][1], ssq], writes=[tmpk])
                    b.op("dve", lambda e: e.tensor_tensor(out=tmpk[:, 2:4, :], in0=pmS[i][2][:, 0:128].rearrange("p (h d) -> p h d", d=64),
                                                          in1=ssq[:, 10:12].unsqueeze(2).to_broadcast([128, 2, 64]), op=ALU.mult),
                         reads=[pmS[i][2], ssq], writes=[tmpk])
                    b.op("pool", lambda e: e.tensor_tensor(out=kb[:].rearrange("p (a g) d -> p a g d", a=2), in0=tmpk[:].rearrange("p (a g) d -> p a g d", a=2),
                                                           in1=gk_rep[:, None, :, :].to_broadcast([128, 2, 2, 64]), op=ALU.mult),
                         reads=[tmpk, gk_rep], writes=[kb])
                    for j in range(4):
                        b.op("pe", lambda e: e.transpose(out=ptk[0:64, j, :], in_=kb[:, j, :], identity=self.ident[:]),
                             reads=[kb, self.ident], writes=[ptk])
                    if getattr(self, 'stop_at', 99) <= 4:
                        return
                    for du in range(2):
                        b.op("act", lambda e: e.copy(out=cb[:, :, du, :], in_=pmS[i][1][:, 0:256].rearrange("p (a d) -> p a d", d=64)),
                             reads=[pmS[i][1]], writes=[cb])
                    for j in range(4):
                        b.op("pe", lambda e: e.transpose(out=ptk[:, 4 + j, :], in_=cb[:, j, :, :].rearrange("p a d -> p (a d)"), identity=self.ident[:]),
                             reads=[cb, self.ident], writes=[ptk])
                    c0 = t * 128
                    b.op("dve", lambda e: e.tensor_copy(out=self.ksE[0:64, :, c0:c0 + 128], in_=ptk[0:64, 0:2, :]), reads=[ptk], writes=[self.ksE])
                    b.op("dve", lambda e: e.tensor_copy(out=self.kwT[0:64, :, c0:c0 + 128], in_=ptk[0:64, 2:4, :]), reads=[ptk], writes=[self.kwT])
                    b.op("act", lambda e: e.copy(out=kcdup[0:64, :, 1 + c0:1 + c0 + 128], in_=ptk[0:64, 4:6, :]), reads=[ptk], writes=[kcdup])
                    b.op("act", lambda e: e.copy(out=kcdup[64:128, :, c0:c0 + 128], in_=ptk[64:128, 4:6, :]), reads=[ptk], writes=[kcdup])
                    b.op("dve", lambda e: e.tensor_copy(out=vcdup[0:64, :, 1 + c0:1 + c0 + 128], in_=ptk[0:64, 6:8, :]), reads=[ptk], writes=[vcdup])
                    b.op("dve", lambda e: e.tensor_copy(out=vcdup[64:128, :, c0:c0 + 128], in_=ptk[64:128, 6:8, :]), reads=[ptk], writes=[vcdup])
                    if getattr(self, 'stop_at', 99) <= 5:
                        return
                    b.op("act", lambda e: e.copy(out=self.vaug_s[:, t, :, 0:64], in_=pmS[i][1][:, 384:512].rearrange("p (g d) -> p g d", d=64)),
                         reads=[pmS[i][1]], writes=[self.vaug_s])
                    b.op("act", lambda e: e.copy(out=self.vaug_w[:, t, :, 0:64], in_=pmS[i][2][:, 128:256].rearrange("p (g d) -> p g d", d=64)),
                         reads=[pmS[i][2]], writes=[self.vaug_w])
                    b.op("act", lambda e: e.activation(out=self.gts[:, t, :], in_=pmS[i][2][:, 256:280], func=AF.Sigmoid), reads=[pmS[i][2]], writes=[self.gts])

            stageA(0)
            stageA2(0)
            for t in range(ntl):
                if t + 1 < ntl:
                    stageA(t + 1)
                stageB(t)
                if t + 1 < ntl:
                    stageA2(t + 1)
            if "nsa_proj" in self.debug:
                d = self.dbg_out("ksE", [128, 2, S], BF16)
                b.dma("pool", d, self.ksE[:], reads=[self.ksE])
                d = self.dbg_out("kcdup", [128, 2, S + 1], BF16)
                b.dma("pool", d, kcdup[:], reads=[kcdup])
                d = self.dbg_out("vaug_w", [128, NT, 2, 65], BF16)
                b.dma("pool", d, self.vaug_w[:], reads=[self.vaug_w])
                d = self.dbg_out("gts", [128, NT, 24], F32)
                b.dma("pool", d, self.gts[:], reads=[self.gts])
            if not getattr(self, 'skip_compress', False):
                self.compress(kcdup, vcdup, gk_rep, [pm[0], pm[1]], pm[2], ptk_[0])

    def compress(self, kcdup, vcdup, gk_rep, ph, po, ptc):
        b = self.b
        I = self.inp
        C2 = 2.0 * 0.7978845608028654
        w1 = b.sb("w1", [128, 16, 256], BF16)
        w2 = b.sb("w2", [128, 2, 64], BF16)
        w1st = [b.sb(f"w1st{i}", [128, 256], F32) for i in range(2)]
        peT = b.sb("peT", [128, 16], F32)
        peTb = b.sb("peTb", [128, 16], BF16)
        hTc = b.sb("hTc", [128, 2, 256], BF16)
        pbias = b.sb("pbias", [128, 2], F32)
        xh = b.sb("xh", [128, 255], F32)
        x2 = b.sb("x2", [128, 255], F32)
        sg = b.sb("sg", [128, 255], F32)
        ctmp = b.sb("ctmp", [128, 64], F32)
        csq = b.sb("csq", [128, 64], F32)
        cs1 = b.sb("cs1", [128, 1], F32)
        kcb = b.sb("kcb", [128, 64], BF16)
        b.op("pool", lambda e: e.memset(hTc[:], 0.0), writes=[hTc])
        for kv, (dup, pe_n, w1_n, w2_n) in enumerate([(kcdup, "cmp_pe_k", "cmp_w1_k", "cmp_w2_k"), (vcdup, "cmp_pe_v", "cmp_w1_v", "cmp_w2_v")]):
            self.load_weight(w1, I[w1_n][0], 256, kch=16, stage=w1st, eng="dve")
            self.load_weight(w2, I[w2_n][0], 64, kch=2, stage=w1st, eng="dve")
            for two in range(2):
                b.dma("sp", peT[two * 64:(two + 1) * 64, :], I[pe_n][0].rearrange("(pp two) d -> two d pp", two=2)[two],
                      writes=[peT], allow_slow_non_contiguous=True)
            b.op("dve", lambda e: e.tensor_copy(out=peTb[:], in_=peT[:]), reads=[peT], writes=[peTb])
            for ft in range(2):
                for pp in range(16):
                    b.op("pe", lambda e: e.matmul(po[:, ft:ft + 1], lhsT=w1[:, pp, ft * 128:(ft + 1) * 128], rhs=peTb[:, pp:pp + 1],
                                                  start=(pp == 0), stop=(pp == 15)), reads=[w1, peTb], writes=[po])
            b.op("dve", lambda e: e.tensor_copy(out=pbias[:], in_=po[:, 0:2]), reads=[po], writes=[pbias])
            for g in range(2):
                for ft in range(2):
                    p = ph[ft]
                    for pp in range(16):
                        b.op("pe", lambda e: e.matmul(p[:, 0:255], lhsT=w1[:, pp, ft * 128:(ft + 1) * 128],
                                                      rhs=dup[:, g, 1 + 2 * pp:1 + 2 * pp + 16 * 254 + 1:16],
                                                      start=(pp == 0), stop=(pp == 15)), reads=[w1, dup], writes=[p])
                    b.op("act", lambda e: e.activation(out=xh[:], in_=p[:, 0:255], func=AF.Identity, bias=pbias[:, ft:ft + 1]), reads=[p, pbias], writes=[xh])
                    b.op("dve", lambda e: e.tensor_tensor(out=x2[:], in0=xh[:], in1=xh[:], op=ALU.mult), reads=[xh], writes=[x2])
                    b.op("dve", lambda e: e.tensor_scalar(out=x2[:], in0=x2[:], scalar1=0.044715, scalar2=1.0, op0=ALU.mult, op1=ALU.add), reads=[x2], writes=[x2])
                    b.op("dve", lambda e: e.tensor_tensor(out=x2[:], in0=x2[:], in1=xh[:], op=ALU.mult), reads=[x2, xh], writes=[x2])
                    b.op("act", lambda e: e.activation(out=sg[:], in_=x2[:], func=AF.Sigmoid, scale=C2), reads=[x2], writes=[sg])
                    b.op("dve", lambda e: e.tensor_tensor(out=hTc[:, ft, 0:255], in0=xh[:], in1=sg[:], op=ALU.mult), reads=[xh, sg], writes=[hTc])
                for ct in range(2):
                    for ft in range(2):
                        b.op("pe", lambda e: e.matmul(po[:, 64:128], lhsT=hTc[:, ft, ct * 128:(ct + 1) * 128], rhs=w2[:, ft, :],
                                                      start=(ft == 0), stop=(ft == 1)), reads=[hTc, w2], writes=[po])
                    if kv == 0:
                        b.op("act", lambda e: e.activation(out=csq[:], in_=po[:, 64:128], func=AF.Square, accum_out=cs1[:]), reads=[po], writes=[csq, cs1])
                        b.op("act", lambda e: e.activation(out=cs1[:], in_=cs1[:], func=AF.Sqrt, scale=1.0 / 64, bias=RMS_EPS), reads=[cs1], writes=[cs1])
                        b.op("dve", lambda e: e.reciprocal(out=cs1[:], in_=cs1[:]), reads=[cs1], writes=[cs1])
                        b.op("dve", lambda e: e.tensor_scalar_mul(out=ctmp[:], in0=po[:, 64:128], scalar1=cs1[:]), reads=[po, cs1], writes=[ctmp])
                        b.op("dve", lambda e: e.tensor_tensor(out=kcb[:], in0=ctmp[:], in1=gk_rep[:, 0, :], op=ALU.mult), reads=[ctmp, gk_rep], writes=[kcb])
                        b.op("pe", lambda e: e.transpose(out=ptc[0:64, 0, :], in_=kcb[:], identity=self.ident[:]), reads=[kcb, self.ident], writes=[ptc])
                        b.op("dve", lambda e: e.tensor_copy(out=self.kcT[:, g, ct * 128:(ct + 1) * 128], in_=ptc[0:64, 0, :]), reads=[ptc], writes=[self.kcT])
                    else:
                        b.op("dve", lambda e: e.tensor_copy(out=self.vcA[:, g, ct, 0:64], in_=po[:, 64:128]), reads=[po], writes=[self.vcA])
        if "compress" in self.debug:
            d = self.dbg_out("kcT", [64, 2, 256], BF16)
            b.dma("pool", d, self.kcT[:], reads=[self.kcT])
            d = self.dbg_out("vcA", [128, 2, 2, 129], F32)
            b.dma("pool", d, self.vcA[:], reads=[self.vcA])

    def finish(self):
        b = self.b
        b.wait_all_on("pool")
        b.barrier()
        b.close()
        return self.nc


def _phase_attn(self):
    b = self.b
    I = self.inp
    with b.scope():
        tw = b.sb("tw", [128, 8, 640], F32)
        ts = b.sb("ts", [128, 8, 640], F32)
        b.dma("sp", tw[:], I["tw"], writes=[tw])
        b.dma("sp", ts[:], I["ts"], writes=[ts])
        candneg = b.sb("candneg", [128, 32, 64], F32)
        fz = b.sb("fz", [128, 32, 64], F32)
        b.dma("sp", candneg[:], I["candneg"], writes=[candneg])
        b.dma("sp", fz[:], I["fz"], writes=[fz])
        b31 = b.sb("b31", [128, 8], F32)
        b.dma("sp", b31[:], I["b31"], writes=[b31])
        kwp = b.sb("kwp", [128, 2, S], BF16)
        b.op("pool", lambda e: e.memset(kwp[64:128, :, :], 0.0), writes=[kwp])
        b.op("pool", lambda e: e.tensor_copy(out=kwp[0:64, :, :], in_=self.kwT[:]), reads=[self.kwT], writes=[kwp])
        kcp = b.sb("kcp", [128, 2, 256], BF16)
        b.op("pool", lambda e: e.memset(kcp[64:128, :, :], 0.0), writes=[kcp])
        b.op("pool", lambda e: e.tensor_copy(out=kcp[0:64, :, :], in_=self.kcT[:]), reads=[self.kcT], writes=[kcp])
        zer = b.sb("zer", [128, 512], BF16)
        b.op("pool", lambda e: e.memset(zer[:], 0.0), writes=[zer])
        qm = [b.sb(f"qm{i}", [128, 8, 512], BF16) for i in range(2)]
        bct = [b.sb(f"bct{i}", [128, 512], F32) for i in range(3)]
        scf = [b.sb(f"scf{i}", [128, 640], F32) for i in range(2)]
        pcT = [b.sb(f"pcT{i}", [128, 2, 512], F32) for i in range(2)]
        pT = [b.sb(f"pT{i}", [128, 640], BF16) for i in range(3)]
        oacc = b.sb("oacc", [128, 4, 512], F32)
        imp = b.sb("imp", [128, 4, 2, 64], F32)
        impm = b.sb("impm", [128, 64], F32)
        impm2 = b.sb("impm2", [128, 64], F32)
        m8a = b.sb("m8a", [128, 8], F32)
        m8b = b.sb("m8b", [128, 8], F32)
        msk = b.sb("msk", [128, 64], F32)
        mb = b.sb("mb", [128, 128], BF16)
        b.op("pool", lambda e: e.memset(mb[:], 0.0), writes=[mb])
        rs = b.sb("rs", [128, 4], F32)
        rg = b.sb("rg", [128, 4], F32)
        oab = b.sb("oab", [128, 512], BF16)
        oaT = [b.sb(f"oaT{i}", [128, 4, 128], BF16) for i in range(2)]
        pS = [b.ps(f"pS{i}", [128, 512], F32) for i in range(2)]
        pS2 = b.ps("pS2", [128, 512], F32)
        pO = [b.ps(f"pO{i}", [128, 512], F32) for i in range(3)]
        pTr = b.ps("pTr", [128, 8, 128], BF16)
        nrot = {"bct": 0, "scf": 0, "pT": 0, "pS": 0}

        def rot(name, lst):
            nrot[name] += 1
            return lst[nrot[name] % len(lst)]

        def finalize(po, ncol_off, h, qs, branch, first):
            qt = qs_base + qs
            o0 = ncol_off
            b.op("dve", lambda e: e.tensor_scalar_max(out=rs[:, 0:1], in0=po[:, o0 + 64:o0 + 65], scalar1=1e-30), reads=[po], writes=[rs])
            b.op("dve", lambda e: e.reciprocal(out=rs[:, 1:2], in_=rs[:, 0:1]), reads=[rs], writes=[rs])
            b.op("dve", lambda e: e.tensor_tensor(out=rg[:, 0:1], in0=rs[:, 1:2], in1=self.gts[:, qt, h * 3 + branch:h * 3 + branch + 1], op=ALU.mult),
                 reads=[rs, self.gts], writes=[rg])
            if first:
                b.op("dve", lambda e: e.tensor_scalar_mul(out=oacc[:, qs, h * 64:(h + 1) * 64], in0=po[:, o0:o0 + 64], scalar1=rg[:, 0:1]),
                     reads=[po, rg], writes=[oacc])
            else:
                b.op("dve", lambda e: e.scalar_tensor_tensor(out=oacc[:, qs, h * 64:(h + 1) * 64], in0=po[:, o0:o0 + 64], scalar=rg[:, 0:1],
                                                             in1=oacc[:, qs, h * 64:(h + 1) * 64], op0=ALU.mult, op1=ALU.add),
                     reads=[po, rg, oacc], writes=[oacc])

        nqg = getattr(self, "nqg_limit", 8)
        for qg in range(nqg):
            qs_base = 4 * qg
            q0 = 512 * qg
            Q = qm[qg % 2]
            b.dma("sp", Q[0:64, :, :], self.qT_d[:, :, q0:q0 + 512].rearrange("h d t -> d h t"), reads=[self.qT_d], writes=[Q])
            if qg < 2:
                b.op("pool", lambda e: e.memset(Q[64:128, :, :], 0.0), writes=[Q])
            for h in range(8):
                g = h // 4
                pc = pcT[h % 2]
                for ct in range(2):
                    p = rot("pS", pS)
                    b.op("pe", lambda e: e.matmul(p[:, :], lhsT=kcp[:, g, ct * 128:(ct + 1) * 128], rhs=Q[:, h, :], start=True, stop=True),
                         reads=[kcp, Q], writes=[p])
                    bt = rot("bct", bct)
                    b.dma("sp", bt[:], I["biasc"][h, ct, :, q0:q0 + 512], writes=[bt])
                    sc = rot("scf", scf)
                    b.op("dve", lambda e: e.tensor_tensor(out=sc[:, 0:512], in0=p[:, :], in1=bt[:], op=ALU.add), reads=[p, bt], writes=[sc])
                    b.op("act", lambda e: e.activation(out=pc[:, ct, :], in_=sc[:, 0:512], func=AF.Exp), reads=[sc], writes=[pc])
                po = pO[0]
                for qs in range(4):
                    for ct in range(2):
                        b.op("pe", lambda e: e.matmul(po[:, qs * 128:qs * 128 + 129] if False else po[:, 0:129], lhsT=pc[:, ct, qs * 128:(qs + 1) * 128],
                                                      rhs=self.vcA[:, g, ct, :], start=(ct == 0), stop=(ct == 1)), reads=[pc, self.vcA], writes=[po])
                    finalize(po, 0, h, qs, 0, True)
                    if h % 4 == 0:
                        b.op("dve", lambda e: e.tensor_scalar_mul(out=imp[:, qs, g, :], in0=po[:, 65:129], scalar1=rs[:, 1:2]), reads=[po, rs], writes=[imp])
                    else:
                        b.op("dve", lambda e: e.scalar_tensor_tensor(out=imp[:, qs, g, :], in0=po[:, 65:129], scalar=rs[:, 1:2], in1=imp[:, qs, g, :],
                                                                     op0=ALU.mult, op1=ALU.add), reads=[po, rs, imp], writes=[imp])
            if qg >= 2:
                for qs in range(4):
                    qt = qs_base + qs
                    for g in range(2):
                        b.op("dve", lambda e: e.tensor_tensor(out=impm[:], in0=imp[:, qs, g, :], in1=candneg[:, qt, :], op=ALU.add), reads=[imp, candneg], writes=[impm])
                        b.op("dve", lambda e: e.max(out=m8a[:], in_=impm[:]), reads=[impm], writes=[m8a])
                        b.op("dve", lambda e: e.match_replace(out=impm2[:], in_to_replace=m8a[:], in_values=impm[:], imm_value=-1e9), reads=[m8a, impm], writes=[impm2])
                        b.op("dve", lambda e: e.max(out=m8b[:], in_=impm2[:]), reads=[impm2], writes=[m8b])
                        b.op("dve", lambda e: e.tensor_scalar(out=msk[:], in0=impm[:], scalar1=m8b[:, 4:5], scalar2=None, op0=ALU.is_ge), reads=[impm, m8b], writes=[msk])
                        b.op("dve", lambda e: e.tensor_tensor(out=msk[:], in0=msk[:], in1=fz[:, qt, :], op=ALU.max), reads=[msk, fz], writes=[msk])
                        b.op("dve", lambda e: e.tensor_scalar(out=mb[:, 64:128], in0=msk[:], scalar1=-NEG, scalar2=NEG, op0=ALU.mult, op1=ALU.add), reads=[msk], writes=[mb])
                        b.op("pe", lambda e: e.transpose(out=pTr[:, 0, :], in_=mb[:], identity=self.ident[:]), reads=[mb, self.ident], writes=[pTr])
                        b.op("act", lambda e: e.copy(out=Q[64:128, 4 * g:4 * g + 4, qs * 128:(qs + 1) * 128],
                                                     in_=pTr[64:128, 0:1, :].to_broadcast([64, 4, 128])), reads=[pTr], writes=[Q])
            for h in range(8):
                g = h // 4
                po_s, po_w = pO[1], pO[2]
                for po in (po_s, po_w):
                    b.op("pe", lambda e: e.matmul(po[:, 0:260], lhsT=zer[:, 0:128], rhs=zer[:, 0:260], start=True, stop=True), reads=[zer], writes=[po])
                nkt = 4 * (qg + 1)
                for kt in range(nkt):
                    dlt = 4 * qg - kt
                    qstart = 0 if dlt >= 0 else -dlt * 128
                    N = 512 - qstart
                    p = rot("pS", pS)
                    b.op("pe", lambda e: e.matmul(p[:, 0:N], lhsT=self.ksE[:, g, kt * 128:(kt + 1) * 128], rhs=Q[:, h, qstart:512], start=True, stop=True),
                         reads=[self.ksE, Q], writes=[p])
                    pt_ = rot("pT", pT)
                    if dlt <= 1:
                        c0 = 128 if dlt == 1 else 0
                        sc = rot("scf", scf)
                        b.op("dve", lambda e: e.tensor_tensor(out=sc[:, 0:N], in0=p[:, 0:N], in1=ts[:, h, c0:c0 + N], op=ALU.add), reads=[p, ts], writes=[sc])
                        b.op("act", lambda e: e.activation(out=pt_[:, 0:N], in_=sc[:, 0:N], func=AF.Exp), reads=[sc], writes=[pt_])
                    else:
                        b.op("act", lambda e: e.activation(out=pt_[:, 0:N], in_=p[:, 0:N], func=AF.Exp, bias=b31[:, h:h + 1]), reads=[p, b31], writes=[pt_])
                    for qs in range(qstart // 128, 4):
                        o = qs * 128 - qstart
                        b.op("pe", lambda e: e.matmul(po_s[:, qs * 65:(qs + 1) * 65], lhsT=pt_[:, o:o + 128], rhs=self.vaug_s[:, kt, g, :],
                                                      start=False, stop=(kt == nkt - 1), skip_group_check=True), reads=[pt_, self.vaug_s], writes=[po_s])
                kts = [kt for kt in range(4 * qg - 4, 4 * qg + 4) if kt >= 0]
                for kt in kts:
                    qs_lo = max(0, kt - 4 * qg)
                    qs_hi = min(3, kt + 4 - 4 * qg)
                    N = (qs_hi - qs_lo + 1) * 128
                    c0 = 128 * (4 * qg + qs_lo - kt)
                    p = rot("pS", pS)
                    b.op("pe", lambda e: e.matmul(p[:, 0:N], lhsT=kwp[:, g, kt * 128:(kt + 1) * 128], rhs=Q[:, h, qs_lo * 128:(qs_hi + 1) * 128], start=True, stop=True),
                         reads=[kwp, Q], writes=[p])
                    sc = rot("scf", scf)
                    b.op("dve", lambda e: e.tensor_tensor(out=sc[:, 0:N], in0=p[:, 0:N], in1=tw[:, h, c0:c0 + N], op=ALU.add), reads=[p, tw], writes=[sc])
                    pt_ = rot("pT", pT)
                    b.op("act", lambda e: e.activation(out=pt_[:, 0:N], in_=sc[:, 0:N], func=AF.Exp), reads=[sc], writes=[pt_])
                    for qs in range(qs_lo, qs_hi + 1):
                        o = (qs - qs_lo) * 128
                        b.op("pe", lambda e: e.matmul(po_w[:, qs * 65:(qs + 1) * 65], lhsT=pt_[:, o:o + 128], rhs=self.vaug_w[:, kt, g, :],
                                                      start=False, stop=(kt == kts[-1]), skip_group_check=True), reads=[pt_, self.vaug_w], writes=[po_w])
                for qs in range(4):
                    finalize(po_s, qs * 65, h, qs, 1, False)
                    finalize(po_w, qs * 65, h, qs, 2, False)
            for qs in range(4):
                qt = qs_base + qs
                ot = oaT[qs % 2]
                b.op("act", lambda e: e.copy(out=oab[:], in_=oacc[:, qs, :]), reads=[oacc], writes=[oab])
                for c in range(4):
                    b.op("pe", lambda e: e.transpose(out=pTr[:, 4 + c, :], in_=oab[:, c * 128:(c + 1) * 128], identity=self.ident[:]), reads=[oab, self.ident], writes=[pTr])
                b.op("act", lambda e: e.copy(out=ot[:], in_=pTr[:, 4:8, :]), reads=[pTr], writes=[ot])
                b.dma("pool", self.oaT_d[:, :, qt * 128:(qt + 1) * 128].rearrange("c p t -> p c t"), ot[:], reads=[ot], writes=[self.oaT_d])
        if "attn" in self.debug:
            d = self.dbg_out("oaT", [4, 128, S], BF16)
            b.dma("pool", d, self.oaT_d[:], reads=[self.oaT_d])


Prog.phase_attn = _phase_attn


def _phase_attn2(self):
    b = self.b
    I = self.inp
    with b.scope():
        tw = b.sb("tw", [128, 8, 640], F32)
        ts = b.sb("ts", [128, 8, 640], F32)
        b.dma("sp", tw[:], I["tw"], writes=[tw])
        b.dma("sp", ts[:], I["ts"], writes=[ts])
        candneg = b.sb("candneg", [128, 32, 64], F32)
        fz = b.sb("fz", [128, 32, 64], F32)
        b.dma("sp", candneg[:], I["candneg"], writes=[candneg])
        b.dma("sp", fz[:], I["fz"], writes=[fz])
        b31 = b.sb("b31", [128, 8], F32)
        b.dma("sp", b31[:], I["b31"], writes=[b31])
        kwp = b.sb("kwp", [128, 2, S], BF16)
        b.op("pool", lambda e: e.memset(kwp[64:128, :, :], 0.0), writes=[kwp])
        b.op("pool", lambda e: e.tensor_copy(out=kwp[0:64, :, :], in_=self.kwT[:]), reads=[self.kwT], writes=[kwp])
        kcp = b.sb("kcp", [128, 2, 256], BF16)
        b.op("pool", lambda e: e.memset(kcp[64:128, :, :], 0.0), writes=[kcp])
        b.op("pool", lambda e: e.tensor_copy(out=kcp[0:64, :, :], in_=self.kcT[:]), reads=[self.kcT], writes=[kcp])
        zer = b.sb("zer", [128, 512], BF16)
        b.op("pool", lambda e: e.memset(zer[:], 0.0), writes=[zer])
        qm = [b.sb(f"qm{i}", [128, 8, 512], BF16) for i in range(2)]
        bct = [b.sb(f"bct{i}", [128, 512], F32) for i in range(3)]
        scf = [b.sb(f"scf{i}", [128, 640], F32) for i in range(3)]
        pcT = [b.sb(f"pcT{i}", [128, 2, 512], F32) for i in range(2)]
        pT = [b.sb(f"pT{i}", [128, 640], BF16) for i in range(4)]
        oacc = b.sb("oacc", [128, 4, 512], F32)
        imp = b.sb("imp", [128, 4, 2, 64], F32)
        impm = b.sb("impm", [128, 64], F32)
        impm2 = b.sb("impm2", [128, 64], F32)
        m8a = b.sb("m8a", [128, 8], F32)
        m8b = b.sb("m8b", [128, 8], F32)
        msk = b.sb("msk", [128, 64], F32)
        mb = b.sb("mb", [128, 128], BF16)
        b.op("pool", lambda e: e.memset(mb[:], 0.0), writes=[mb])
        rs = b.sb("rs", [128, 4], F32)
        rg = b.sb("rg", [128, 4], F32)
        oab = b.sb("oab", [128, 512], BF16)
        oaT = [b.sb(f"oaT{i}", [128, 4, 128], BF16) for i in range(2)]
        pS = [b.ps(f"pS{i}", [128, 512], F32) for i in range(3)]
        pOs = [b.ps(f"pOs{i}", [128, 512], F32) for i in range(2)]
        pOw = [b.ps(f"pOw{i}", [128, 512], F32) for i in range(2)]
        pTr = b.ps("pTr", [128, 8, 128], BF16)
        nrot = {"bct": 0, "scf": 0, "pT": 0, "pS": 0}

        def rot(name, lst):
            nrot[name] += 1
            return lst[nrot[name] % len(lst)]

        def finalize(po, ncol_off, h, qs, branch, first):
            qt = qs_base + qs
            o0 = ncol_off
            b.op("dve", lambda e: e.tensor_scalar_max(out=rs[:, 0:1], in0=po[:, o0 + 64:o0 + 65], scalar1=1e-30), reads=[po], writes=[rs])
            b.op("dve", lambda e: e.reciprocal(out=rs[:, 1:2], in_=rs[:, 0:1]), reads=[rs], writes=[rs])
            b.op("dve", lambda e: e.tensor_tensor(out=rg[:, 0:1], in0=rs[:, 1:2], in1=self.gts[:, qt, h * 3 + branch:h * 3 + branch + 1], op=ALU.mult),
                 reads=[rs, self.gts], writes=[rg])
            if first:
                b.op("dve", lambda e: e.tensor_scalar_mul(out=oacc[:, qs, h * 64:(h + 1) * 64], in0=po[:, o0:o0 + 64], scalar1=rg[:, 0:1]),
                     reads=[po, rg], writes=[oacc])
            else:
                b.op("dve", lambda e: e.scalar_tensor_tensor(out=oacc[:, qs, h * 64:(h + 1) * 64], in0=po[:, o0:o0 + 64], scalar=rg[:, 0:1],
                                                             in1=oacc[:, qs, h * 64:(h + 1) * 64], op0=ALU.mult, op1=ALU.add),
                     reads=[po, rg, oacc], writes=[oacc])

        nqg = getattr(self, "nqg_limit", 8)
        for qg in range(nqg):
            qs_base = 4 * qg
            q0 = 512 * qg
            Q = qm[qg % 2]
            b.dma("sp", Q[0:64, :, :], self.qT_d[:, :, q0:q0 + 512].rearrange("h d t -> d h t"), reads=[self.qT_d], writes=[Q])
            if qg < 2:
                b.op("pool", lambda e: e.memset(Q[64:128, :, :], 0.0), writes=[Q])
            for h in range(8):
                g = h // 4
                pc = pcT[h % 2]
                for ct in range(2):
                    p = rot("pS", pS)
                    b.op("pe", lambda e: e.matmul(p[:, :], lhsT=kcp[:, g, ct * 128:(ct + 1) * 128], rhs=Q[:, h, :], start=True, stop=True),
                         reads=[kcp, Q], writes=[p])
                    bt = rot("bct", bct)
                    b.dma("sp", bt[:], I["biasc"][h, ct, :, q0:q0 + 512], writes=[bt])
                    sc = rot("scf", scf)
                    b.op("dve", lambda e: e.tensor_tensor(out=sc[:, 0:512], in0=p[:, :], in1=bt[:], op=ALU.add), reads=[p, bt], writes=[sc])
                    b.op("act", lambda e: e.activation(out=pc[:, ct, :], in_=sc[:, 0:512], func=AF.Exp), reads=[sc], writes=[pc])
                po = pOs[h % 2]
                for qs in range(4):
                    for ct in range(2):
                        b.op("pe", lambda e: e.matmul(po[:, qs * 128:qs * 128 + 129] if False else po[:, 0:129], lhsT=pc[:, ct, qs * 128:(qs + 1) * 128],
                                                      rhs=self.vcA[:, g, ct, :], start=(ct == 0), stop=(ct == 1)), reads=[pc, self.vcA], writes=[po])
                    finalize(po, 0, h, qs, 0, True)
                    if h % 4 == 0:
                        b.op("dve", lambda e: e.tensor_scalar_mul(out=imp[:, qs, g, :], in0=po[:, 65:129], scalar1=rs[:, 1:2]), reads=[po, rs], writes=[imp])
                    else:
                        b.op("dve", lambda e: e.scalar_tensor_tensor(out=imp[:, qs, g, :], in0=po[:, 65:129], scalar=rs[:, 1:2], in1=imp[:, qs, g, :],
                                                                     op0=ALU.mult, op1=ALU.add), reads=[po, rs, imp], writes=[imp])
            if qg >= 2:
                for qs in range(4):
                    qt = qs_base + qs
                    for g in range(2):
                        b.op("dve", lambda e: e.tensor_tensor(out=impm[:], in0=imp[:, qs, g, :], in1=candneg[:, qt, :], op=ALU.add), reads=[imp, candneg], writes=[impm])
                        b.op("dve", lambda e: e.max(out=m8a[:], in_=impm[:]), reads=[impm], writes=[m8a])
                        b.op("dve", lambda e: e.match_replace(out=impm2[:], in_to_replace=m8a[:], in_values=impm[:], imm_value=-1e9), reads=[m8a, impm], writes=[impm2])
                        b.op("dve", lambda e: e.max(out=m8b[:], in_=impm2[:]), reads=[impm2], writes=[m8b])
                        b.op("dve", lambda e: e.tensor_scalar(out=msk[:], in0=impm[:], scalar1=m8b[:, 4:5], scalar2=None, op0=ALU.is_ge), reads=[impm, m8b], writes=[msk])
                        b.op("dve", lambda e: e.tensor_tensor(out=msk[:], in0=msk[:], in1=fz[:, qt, :], op=ALU.max), reads=[msk, fz], writes=[msk])
                        b.op("dve", lambda e: e.tensor_scalar(out=mb[:, 64:128], in0=msk[:], scalar1=-NEG, scalar2=NEG, op0=ALU.mult, op1=ALU.add), reads=[msk], writes=[mb])
                        b.op("pe", lambda e: e.transpose(out=pTr[:, 0, :], in_=mb[:], identity=self.ident[:]), reads=[mb, self.ident], writes=[pTr])
                        b.op("act", lambda e: e.copy(out=Q[64:128, 4 * g:4 * g + 4, qs * 128:(qs + 1) * 128],
                                                     in_=pTr[64:128, 0:1, :].to_broadcast([64, 4, 128])), reads=[pTr], writes=[Q])
            jobs = []
            for h in range(8):
                g = h // 4
                nkt = 4 * (qg + 1)
                for kt in range(nkt):
                    dlt = 4 * qg - kt
                    qstart = 0 if dlt >= 0 else -dlt * 128
                    jobs.append(dict(kind="s", h=h, g=g, kt=kt, qlo=qstart // 128, qhi=3, first=(kt == 0), last=False, lastkt=(kt == nkt - 1),
                                     tab=(ts, (128 if dlt == 1 else 0)) if dlt <= 1 else None))
                kts = [kt for kt in range(4 * qg - 4, 4 * qg + 4) if kt >= 0]
                for kt in kts:
                    qs_lo = max(0, kt - 4 * qg)
                    qs_hi = min(3, kt + 4 - 4 * qg)
                    jobs.append(dict(kind="w", h=h, g=g, kt=kt, qlo=qs_lo, qhi=qs_hi, first=False, last=(kt == kts[-1]), lastkt=(kt == kts[-1]),
                                     tab=(tw, 128 * (4 * qg + qs_lo - kt))))

            def emitS(j):
                h, g, kt = j["h"], j["g"], j["kt"]
                N = (j["qhi"] - j["qlo"] + 1) * 128
                p = rot("pS", pS)
                kmat = self.ksE if j["kind"] == "s" else kwp
                b.op("pe", lambda e: e.matmul(p[:, 0:N], lhsT=kmat[:, g, kt * 128:(kt + 1) * 128], rhs=Q[:, h, j["qlo"] * 128:(j["qhi"] + 1) * 128], start=True, stop=True),
                     reads=[kmat, Q], writes=[p])
                j["p"] = p
                j["N"] = N

            def emitE(j):
                h = j["h"]
                p, N = j["p"], j["N"]
                pt_ = rot("pT", pT)
                if j["tab"] is not None:
                    tab, c0 = j["tab"]
                    sc = rot("scf", scf)
                    b.op("dve", lambda e: e.tensor_tensor(out=sc[:, 0:N], in0=p[:, 0:N], in1=tab[:, h, c0:c0 + N], op=ALU.add), reads=[p, tab], writes=[sc])
                    b.op("act", lambda e: e.activation(out=pt_[:, 0:N], in_=sc[:, 0:N], func=AF.Exp), reads=[sc], writes=[pt_])
                else:
                    b.op("act", lambda e: e.activation(out=pt_[:, 0:N], in_=p[:, 0:N], func=AF.Exp, bias=b31[:, h:h + 1]), reads=[p, b31], writes=[pt_])
                j["pt"] = pt_

            def emitPV(j):
                h, g, kt = j["h"], j["g"], j["kt"]
                po_s, po_w = pOs[h % 2], pOw[h % 2]
                if j["first"]:
                    for po in (po_s, po_w):
                        b.op("pe", lambda e: e.matmul(po[:, 0:260], lhsT=zer[:, 0:128], rhs=zer[:, 0:260], start=True, stop=True), reads=[zer], writes=[po])
                po = po_s if j["kind"] == "s" else po_w
                va = self.vaug_s if j["kind"] == "s" else self.vaug_w
                for qs in range(j["qlo"], j["qhi"] + 1):
                    o = (qs - j["qlo"]) * 128
                    b.op("pe", lambda e: e.matmul(po[:, qs * 65:(qs + 1) * 65], lhsT=j["pt"][:, o:o + 128], rhs=va[:, kt, g, :],
                                                  start=False, stop=j["lastkt"], skip_group_check=True), reads=[j["pt"], va], writes=[po])
                if j["last"]:
                    for qs in range(4):
                        finalize(po_s, qs * 65, h, qs, 1, False)
                        finalize(po_w, qs * 65, h, qs, 2, False)

            LA = 2
            for i_ in range(len(jobs) + LA):
                if i_ < len(jobs):
                    emitS(jobs[i_])
                if i_ >= LA:
                    emitE(jobs[i_ - LA])
                    emitPV(jobs[i_ - LA])
            for qs in range(4):
                qt = qs_base + qs
                ot = oaT[qs % 2]
                b.op("act", lambda e: e.copy(out=oab[:], in_=oacc[:, qs, :]), reads=[oacc], writes=[oab])
                for c in range(4):
                    b.op("pe", lambda e: e.transpose(out=pTr[:, 4 + c, :], in_=oab[:, c * 128:(c + 1) * 128], identity=self.ident[:]), reads=[oab, self.ident], writes=[pTr])
                b.op("act", lambda e: e.copy(out=ot[:], in_=pTr[:, 4:8, :]), reads=[pTr], writes=[ot])
                b.dma("pool", self.oaT_d[:, :, qt * 128:(qt + 1) * 128].rearrange("c p t -> p c t"), ot[:], reads=[ot], writes=[self.oaT_d])
        if "attn" in self.debug:
            d = self.dbg_out("oaT", [4, 128, S], BF16)
            b.dma("pool", d, self.oaT_d[:], reads=[self.oaT_d])


Prog.phase_attn2 = _phase_attn2


def _phase_merge(self):
    b = self.b
    I = self.inp
    self.x1_d = b.dram("x1_d", [S, D], F32)
    with b.scope():
        gat = self.load_gain("gat2", I["attn_norm_g"][0])
        stage = [b.sb(f"mst{i}", [128, 1024], F32) for i in range(2)]
        wg = b.sb("wg", [128, 8, 2048], BF16)
        for n in range(2):
            for c in range(8):
                st = stage[c % 2]
                b.dma("sp", st[:], I["w_in"][0][c * 128:(c + 1) * 128, GA0 + n * 1024:GA0 + (n + 1) * 1024], writes=[st])
                b.op("act", lambda e: e.activation(out=wg[:, c, n * 1024:(n + 1) * 1024], in_=st[:], func=AF.Copy, scale=gat[:, c:c + 1]),
                     reads=[st, gat], writes=[wg])
        wa = b.sb("wa", [128, 4, 1024], BF16)
        wb = b.sb("wb", [128, 4, 1024], BF16)
        wo = b.sb("wo", [128, 8, 1024], BF16)
        self.load_weight(wa, I["w_proj_a"][0], 1024, kch=4, stage=stage, eng="dve")
        self.load_weight(wb, I["w_proj_b"][0], 1024, kch=4, stage=stage, eng="dve")
        self.load_weight(wo, I["w_out"][0], 1024, kch=8, stage=stage, eng="dve")
        xt = [b.sb(f"mxt{i}", [128, D], F32) for i in range(2)]
        junk = b.sb("mjunk", [128, D], BF16)
        ss = [b.sb(f"mss{i}", [128, 1], F32) for i in range(2)]
        hb = [b.sb(f"mhb{i}", [128, D], BF16) for i in range(2)]
        hT = [b.sb(f"mhT{i}", [128, 8, 128], BF16) for i in range(2)]
        oat = [b.sb(f"oat{i}", [128, 4, 128], BF16) for i in range(2)]
        obt = [b.sb(f"obt{i}", [128, 4, 128], BF16) for i in range(2)]
        sg = b.sb("msg", [128, 2048], F32)
        m1 = b.sb("m1", [128, 1024], F32)
        m2 = b.sb("m2", [128, 1024], F32)
        mgb = b.sb("mgb", [128, 1024], BF16)
        mT = b.sb("mT", [128, 8, 128], BF16)
        x1t = [b.sb(f"x1t{i}", [128, D], F32) for i in range(2)]
        pt = b.ps("mpt", [128, 8, 128], BF16)
        pg = [b.ps(f"mpg{i}", [128, 512], F32) for i in range(2)]
        pa = [b.ps(f"mpa{i}", [128, 512], F32) for i in range(2)]
        pb = [b.ps(f"mpb{i}", [128, 512], F32) for i in range(2)]
        for t in range(getattr(self, "nt_limit", NT)):
            i = t % 2
            self.make_hT(I["x"], t, xt[i], junk, ss[i], hb[i], pt, hT[i], self.ident)
            b.dma("sp", oat[i][:], self.oaT_d[:, :, t * 128:(t + 1) * 128].rearrange("c p t -> p c t"), reads=[self.oaT_d], writes=[oat[i]])
            b.dma("sp", obt[i][:], self.obT_d[:, :, t * 128:(t + 1) * 128].rearrange("c p t -> p c t"), reads=[self.obT_d], writes=[obt[i]])
            for n in range(4):
                p = pg[n % 2]
                for c in range(8):
                    b.op("pe", lambda e: e.matmul(p[:, :], lhsT=hT[i][:, c, :], rhs=wg[:, c, n * 512:(n + 1) * 512], start=(c == 0), stop=(c == 7)),
                         reads=[hT[i], wg], writes=[p])
                b.op("act", lambda e: e.activation(out=sg[:, n * 512:(n + 1) * 512], in_=p[:, :], func=AF.Sigmoid), reads=[p], writes=[sg])
            for n in range(2):
                for c in range(4):
                    b.op("pe", lambda e: e.matmul(pa[n][:, :], lhsT=oat[i][:, c, :], rhs=wa[:, c, n * 512:(n + 1) * 512], start=(c == 0), stop=(c == 3)),
                         reads=[oat[i], wa], writes=[pa[n]])
                for c in range(4):
                    b.op("pe", lambda e: e.matmul(pb[n][:, :], lhsT=obt[i][:, c, :], rhs=wb[:, c, n * 512:(n + 1) * 512], start=(c == 0), stop=(c == 3)),
                         reads=[obt[i], wb], writes=[pb[n]])
                b.op("dve", lambda e: e.tensor_tensor(out=m1[:, n * 512:(n + 1) * 512], in0=pa[n][:, :], in1=sg[:, n * 512:(n + 1) * 512], op=ALU.mult),
                     reads=[pa[n], sg], writes=[m1])
                b.op("dve", lambda e: e.tensor_tensor(out=m2[:, n * 512:(n + 1) * 512], in0=pb[n][:, :], in1=sg[:, 1024 + n * 512:1024 + (n + 1) * 512], op=ALU.mult),
                     reads=[pb[n], sg], writes=[m2])
            b.op("pool", lambda e: e.tensor_tensor(out=mgb[:], in0=m1[:], in1=m2[:], op=ALU.add), reads=[m1, m2], writes=[mgb])
            for c in range(8):
                b.op("pe", lambda e: e.transpose(out=pt[:, c, :], in_=mgb[:, c * 128:(c + 1) * 128], identity=self.ident[:]), reads=[mgb, self.ident], writes=[pt])
            b.op("act", lambda e: e.copy(out=mT[:], in_=pt[:]), reads=[pt], writes=[mT])
            for n in range(2):
                for c in range(8):
                    b.op("pe", lambda e: e.matmul(pa[n][:, :], lhsT=mT[:, c, :], rhs=wo[:, c, n * 512:(n + 1) * 512], start=(c == 0), stop=(c == 7)),
                         reads=[mT, wo], writes=[pa[n]])
                b.op("dve", lambda e: e.tensor_tensor(out=x1t[i][:, n * 512:(n + 1) * 512], in0=pa[n][:, :], in1=xt[i][:, n * 512:(n + 1) * 512], op=ALU.add),
                     reads=[pa[n], xt[i]], writes=[x1t[i]])
            b.dma("pool", self.x1_d[t * 128:(t + 1) * 128, :], x1t[i][:], reads=[x1t[i]], writes=[self.x1_d])
        if "merge" in self.debug:
            d = self.dbg_out("x1", [S, D], F32)
            b.dma("pool", d, self.x1_d[:], reads=[self.x1_d])


def _phase_ffn(self):
    b = self.b
    I = self.inp
    TG = 128
    NFT = 44
    with b.scope():
        gf = self.load_gain("gf", I["ffn_norm_g"][0])
        stage = [b.sb(f"fst{i}", [128, 1024], F32) for i in range(2)]
        wu = b.sb("wu", [128, 8, 2 * DFF], BF16)
        for n in range(8):
            for c in range(8):
                st = stage[c % 2]
                b.dma("sp", st[:, 0:704], I["w_up"][0][c * 128:(c + 1) * 128, n * 704:(n + 1) * 704], writes=[st])
                b.op("act", lambda e: e.activation(out=wu[:, c, n * 704:(n + 1) * 704], in_=st[:, 0:704], func=AF.Copy, scale=gf[:, c:c + 1]),
                     reads=[st, gf], writes=[wu])
        wd = b.sb("wd", [128, 22, D], BF16)
        self.load_weight(wd, I["w_down"][0], D, kch=22, stage=stage, eng="dve")
        cw = b.sb("cw", [128, 3, NFT], F32)
        for j in range(3):
            b.dma("sp", cw[:, j, :], I["conv_w"][0][j].rearrange("(c p) -> p c", p=128), writes=[cw], allow_slow_non_contiguous=True)
        cbias = self.load_gain("cbias", I["conv_b"][0], kch=NFT)
        carry = b.sb("carry", [128, NFT, 2], F32)
        b.op("pool", lambda e: e.memset(carry[:], 0.0), writes=[carry])
        xt = [b.sb(f"fxt{i}", [128, D], F32) for i in range(2)]
        junk = b.sb("fjunk", [128, D], BF16)
        ss = [b.sb(f"fss{i}", [128, 1], F32) for i in range(2)]
        hb = [b.sb(f"fhb{i}", [128, D], BF16) for i in range(2)]
        hT1 = [b.sb(f"fhT{i}", [128, 8, 128], BF16) for i in range(2)]
        hTg = b.sb("fhTg", [128, 8, TG], BF16)
        ub = [b.sb(f"ub{i}", [128, TG + 2], F32) for i in range(2)]
        cv = [b.sb(f"cv{i}", [128, TG], F32) for i in range(2)]
        sgl = b.sb("sgl", [128, TG], F32)
        actT = b.sb("actT", [128, 22, TG], BF16)
        self._val = b.sb("fval", [128, 22, TG], BF16)
        ot = xt
        pt = b.ps("fpt", [128, 8, 128], BF16)
        pu = [b.ps(f"fpu{i}", [128, 512], F32) for i in range(3)]
        pd = [b.ps(f"fpd{i}", [128, 512], F32) for i in range(2)]
        ng = getattr(self, "nt_limit", NT) * 128 // TG
        for gi in range(ng):
            for s_ in range(TG // 128):
                t = gi * (TG // 128) + s_
                self.make_hT(self.x1_d, t, xt[s_], junk, ss[s_], hb[s_], pt, hT1[s_], self.ident)
                b.op("pool", lambda e: e.tensor_copy(out=hTg[:, :, s_ * 128:(s_ + 1) * 128], in_=hT1[s_][:]), reads=[hT1[s_]], writes=[hTg])
            for ft in range(NFT):
                p = pu[ft % 3]
                u = ub[ft % 2]
                c_ = cv[(ft // 22) % 2] if False else cv[ft % 2]
                for c in range(8):
                    b.op("pe", lambda e: e.matmul(p[:, 0:TG], lhsT=wu[:, c, ft * 128:(ft + 1) * 128], rhs=hTg[:, c, :], start=(c == 0), stop=(c == 7)),
                         reads=[wu, hTg], writes=[p])
                b.op("act", lambda e: e.copy(out=u[:, 2:TG + 2], in_=p[:, 0:TG]), reads=[p], writes=[u])
                b.op("pool", lambda e: e.tensor_copy(out=u[:, 0:2], in_=carry[:, ft, :]), reads=[carry], writes=[u])
                b.op("pool", lambda e: e.tensor_copy(out=carry[:, ft, :], in_=u[:, TG:TG + 2]), reads=[u], writes=[carry])
                b.op("dve", lambda e: e.tensor_scalar(out=c_[:], in0=u[:, 0:TG], scalar1=cw[:, 0, ft:ft + 1], scalar2=cbias[:, ft:ft + 1], op0=ALU.mult, op1=ALU.add),
                     reads=[u, cw, cbias], writes=[c_])
                b.op("dve", lambda e: e.scalar_tensor_tensor(out=c_[:], in0=u[:, 1:TG + 1], scalar=cw[:, 1, ft:ft + 1], in1=c_[:], op0=ALU.mult, op1=ALU.add),
                     reads=[u, cw, c_], writes=[c_])
                if ft < 22:
                    b.op("dve", lambda e: e.scalar_tensor_tensor(out=self._val[:, ft, :], in0=u[:, 2:TG + 2], scalar=cw[:, 2, ft:ft + 1], in1=c_[:], op0=ALU.mult, op1=ALU.add),
                         reads=[u, cw, c_], writes=[self._val])
                else:
                    b.op("dve", lambda e: e.scalar_tensor_tensor(out=c_[:], in0=u[:, 2:TG + 2], scalar=cw[:, 2, ft:ft + 1], in1=c_[:], op0=ALU.mult, op1=ALU.add),
                         reads=[u, cw, c_], writes=[c_])
                    b.op("act", lambda e: e.activation(out=sgl[:], in_=c_[:], func=AF.Silu), reads=[c_], writes=[sgl])
                    b.op("dve", lambda e: e.tensor_tensor(out=actT[:, ft - 22, :], in0=sgl[:], in1=self._val[:, ft - 22, :], op=ALU.mult),
                         reads=[sgl, self._val], writes=[actT])
            for s_ in range(TG // 128):
                t = gi * (TG // 128) + s_
                for n in range(2):
                    for f in range(22):
                        b.op("pe", lambda e: e.matmul(pd[n][:, :], lhsT=actT[:, f, s_ * 128:(s_ + 1) * 128], rhs=wd[:, f, n * 512:(n + 1) * 512], start=(f == 0), stop=(f == 21)),
                             reads=[actT, wd], writes=[pd[n]])
                    b.op("dve", lambda e: e.tensor_tensor(out=ot[s_][:, n * 512:(n + 1) * 512], in0=pd[n][:, :], in1=xt[s_][:, n * 512:(n + 1) * 512], op=ALU.add),
                         reads=[pd[n], xt[s_]], writes=[ot[s_]])
                b.dma("pool", self.out[t * 128:(t + 1) * 128, :], ot[s_][:], reads=[ot[s_]])


Prog.phase_merge = _phase_merge
Prog.phase_ffn = _phase_ffn


def _phase_rwkv(self):
    b = self.b
    I = self.inp
    TG = 256
    NCH = TG // 64
    tt = lambda eng, out, in0, in1, op, rd, wr: b.op(eng, lambda e: e.tensor_tensor(out=out, in0=in0, in1=in1, op=op), reads=rd, writes=wr)
    with b.scope():
        gat = self.load_gain("gat3", I["attn_norm_g"][0])
        stage = [b.sb(f"rst{i}", [128, 1792], F32) for i in range(2)]
        wr = b.sb("wr", [128, 8, 1792], BF16)
        self.load_weight(wr, I["w_in"][0][:, RW0:RW0 + 1792], 1792, gvec=gat, stage=stage)

        def colvec(name, src, n):
            t = b.sb(name, [64, n], F32)
            b.dma("sp", t[:], src.rearrange("(c p) -> p c", p=64), writes=[t], allow_slow_non_contiguous=True)
            return t
        mu = colvec("mu", I["rwkv_mu"][0], 28)
        w0 = colvec("w0", I["rwkv_w0"][0], 8)
        a0 = colvec("a0", I["rwkv_a0"][0], 8)
        k_k = colvec("k_k", I["rwkv_k_k"][0], 8)
        k_a = colvec("k_a", I["rwkv_k_a"][0], 8)
        r_k = colvec("r_k", I["rwkv_r_k"][0].rearrange("h d -> (h d)"), 8)
        w2s = b.sb("w2s", [64, 512], F32)
        a2s = b.sb("a2s", [64, 512], F32)
        g2s = b.sb("g2s", [64, 2, 512], F32)
        b.dma("sp", w2s[:], I["rwkv_w2"][0], writes=[w2s])
        b.dma("sp", a2s[:], I["rwkv_a2"][0], writes=[a2s])
        b.dma("sp", g2s[:], I["rwkv_g2"][0].rearrange("(two l) f -> l two f", two=2), writes=[g2s])
        lng = b.sb("lng", [64, 512], F32)
        lnb = b.sb("lnb", [64, 512], F32)
        b.dma("sp", lng[:], I["rwkv_ln_g"][0].partition_broadcast(64), writes=[lng])
        b.dma("sp", lnb[:], I["rwkv_ln_b"][0].partition_broadcast(64), writes=[lnb])
        msk = b.sb("rmsk", [64, 3, 64], F32)
        b.dma("sp", msk[:], I["rwmask"], writes=[msk])
        rstm = b.sb("rstm", [64, TG], F32)
        b.dma("sp", rstm[:], I["rwreset"][:, 0:TG], writes=[rstm])
        ones = b.sb("ones64", [64, 64], F32)
        b.op("pool", lambda e: e.memset(ones[:], 1.0), writes=[ones])
        idf = self.identf
        carry = b.sb("rcarry", [64, 28], F32)
        b.op("pool", lambda e: e.memset(carry[:], 0.0), writes=[carry])
        Hs = [[b.sb(f"H{h}_{i}", [64, 64], F32) for i in range(2)] for h in range(8)]
        for h in range(8):
            b.op("pool", lambda e: e.memset(Hs[h][0][:], 0.0), writes=[Hs[h][0]])
        xt = [b.sb(f"rxt{i}", [128, D], F32) for i in range(2)]
        junk = b.sb("rjunk", [128, D], BF16)
        ss = [b.sb(f"rss{i}", [128, 1], F32) for i in range(2)]
        hb = [b.sb(f"rhb{i}", [128, D], BF16) for i in range(2)]
        hT1 = [b.sb(f"rhT{i}", [128, 8, 128], BF16) for i in range(2)]
        hTg = b.sb("rhTg", [128, 8, TG], BF16)
        pbuf = [b.sb(f"rpb{i}", [64, TG + 1], F32) for i in range(2)]
        dtmp = b.sb("rdtmp", [64, TG], F32)
        X = [b.sb(f"rX{w}", [64, 8, TG], F32) for w in range(3)]
        xs = b.sb("rxs", [64, 4, TG], F32)
        BV = b.sb("rBV", [64, 8, TG], F32)
        Ytm = b.sb("rYtm", [64, NCH, 8, 64], F32)
        sqv = b.sb("rsqv", [64, NCH, 8, 64], F32)
        st1 = b.sb("rst1", [64, NCH * 8], F32)
        st2 = b.sb("rst2", [64, NCH * 8], F32)
        T = {n: b.sb("r" + n, [64, TG], F32) for n in ["lw", "as", "kk", "sq", "kkn", "bv", "kp", "t1", "L", "Lx", "Ep", "Em", "Ex", "BT", "KT", "BG", "KG", "rk"]}
        AR = b.sb("rAR", [64, NCH, 2, 64], F32)
        TM = [b.sb(f"rTM{i}", [64, 3, 64], F32) for i in range(2)]
        XM = [b.sb(f"rXM{i}", [64, 4, 64], F32) for i in range(2)]
        AA = [b.sb(f"rAA{i}", [64, 2, 64], F32) for i in range(3)]
        PP = [b.sb(f"rPP{i}", [64, 64], F32) for i in range(3)]
        Xs = b.sb("rXs", [64, 64], F32)
        Us = b.sb("rUs", [64, 64], F32)
        obf = [b.sb(f"robf{i}", [64, TG], BF16) for i in range(2)]
        otmp = b.sb("rotmp", [64, TG], F32)
        pt = b.ps("rpt", [128, 8, 128], BF16)
        pp = [b.ps(f"rpp{i}", [128, 512], F32) for i in range(2)]
        pq = [b.ps(f"rpq{i}", [128, 512], F32) for i in range(2)]
        pd = [b.ps(f"rpd{i}", [128, 512], F32) for i in range(2)]
        pz = b.ps("rpz", [128, 512], F32)
        cnt = {"pp": 0, "pq": 0, "pd": 0, "aa": 0, "ppb": 0, "tm": 0, "xm": 0, "pb": 0}

        def nxt(k, lst):
            cnt[k] += 1
            return lst[cnt[k] % len(lst)]

        ngr = getattr(self, "nrg_limit", S // TG)
        for gi in range(ngr):
            q0 = gi * TG
            for s_ in range(TG // 128):
                t = gi * (TG // 128) + s_
                self.make_hT(I["x"], t, xt[s_], junk, ss[s_], hb[s_], pt, hT1[s_], self.ident)
                b.op("pool", lambda e: e.tensor_copy(out=hTg[:, :, s_ * 128:(s_ + 1) * 128], in_=hT1[s_][:]), reads=[hT1[s_]], writes=[hTg])

            def proj_lerp(fc, out_ap, out_buf, post=None):
                p = nxt("pp", pp)
                for c in range(8):
                    b.op("pe", lambda e: e.matmul(p[0:64, 0:TG], lhsT=wr[:, c, fc * 64:(fc + 1) * 64], rhs=hTg[:, c, :], start=(c == 0), stop=(c == 7)),
                         reads=[wr, hTg], writes=[p])
                pb_ = nxt("pb", pbuf)
                b.op("act", lambda e: e.copy(out=pb_[:, 1:TG + 1], in_=p[0:64, 0:TG]), reads=[p], writes=[pb_])
                b.op("pool", lambda e: e.tensor_copy(out=pb_[:, 0:1], in_=carry[:, fc:fc + 1]), reads=[carry], writes=[pb_])
                b.op("pool", lambda e: e.tensor_copy(out=carry[:, fc:fc + 1], in_=pb_[:, TG:TG + 1]), reads=[pb_], writes=[carry])
                tt("dve", dtmp[:], pb_[:, 0:TG], pb_[:, 1:TG + 1], ALU.subtract, [pb_], [dtmp])
                b.op("dve", lambda e: e.scalar_tensor_tensor(out=out_ap, in0=dtmp[:], scalar=mu[:, fc:fc + 1], in1=pb_[:, 1:TG + 1], op0=ALU.mult, op1=ALU.add),
                     reads=[dtmp, mu, pb_], writes=[out_buf])

            for w in range(3):
                for h in range(8):
                    proj_lerp(w * 8 + h, X[w][:, h, :], X[w])
            for j in range(4):
                proj_lerp(24 + j, xs[:, j, :], xs)
            b.op("act", lambda e: e.activation(out=xs[:, 0, :], in_=xs[:, 0, :], func=AF.Tanh), reads=[xs], writes=[xs])
            b.op("act", lambda e: e.activation(out=xs[:, 2:4, :], in_=xs[:, 2:4, :], func=AF.Sigmoid), reads=[xs], writes=[xs])

            for h in range(8):
                hs = slice(h * 64, (h + 1) * 64)
                R_, K_, V_ = X[0][:, h, :], X[1][:, h, :], X[2][:, h, :]
                p = nxt("pp", pp)
                b.op("pe", lambda e: e.matmul(p[0:64, 0:TG], lhsT=w2s[:, hs], rhs=xs[:, 0, :], start=True, stop=True), reads=[w2s, xs], writes=[p])
                b.op("act", lambda e: e.activation(out=T["lw"][:], in_=p[0:64, 0:TG], func=AF.Sigmoid, bias=w0[:, h:h + 1]), reads=[p, w0], writes=[T["lw"]])
                b.op("pool", lambda e: e.tensor_scalar_mul(out=T["lw"][:], in0=T["lw"][:], scalar1=-0.6065306597126334), reads=[T["lw"]], writes=[T["lw"]])
                p = nxt("pp", pp)
                b.op("pe", lambda e: e.matmul(p[0:64, 0:TG], lhsT=a2s[:, hs], rhs=xs[:, 1, :], start=True, stop=True), reads=[a2s, xs], writes=[p])
                b.op("act", lambda e: e.activation(out=T["as"][:], in_=p[0:64, 0:TG], func=AF.Sigmoid, bias=a0[:, h:h + 1]), reads=[p, a0], writes=[T["as"]])
                b.op("dve", lambda e: e.tensor_scalar_mul(out=T["kk"][:], in0=K_, scalar1=k_k[:, h:h + 1]), reads=[X[1], k_k], writes=[T["kk"]])
                tt("pool", T["sq"][:], T["kk"][:], T["kk"][:], ALU.mult, [T["kk"]], [T["sq"]])
                p = nxt("pp", pp)
                b.op("pe", lambda e: e.matmul(p[0:64, 0:TG], lhsT=ones[:], rhs=T["sq"][:], start=True, stop=True), reads=[ones, T["sq"]], writes=[p])
                b.op("act", lambda e: e.activation(out=T["sq"][:], in_=p[0:64, 0:TG], func=AF.Sqrt), reads=[p], writes=[T["sq"]])
                b.op("dve", lambda e: e.tensor_scalar_max(out=T["sq"][:], in0=T["sq"][:], scalar1=1e-12), reads=[T["sq"]], writes=[T["sq"]])
                b.op("dve", lambda e: e.reciprocal(out=T["sq"][:], in_=T["sq"][:]), reads=[T["sq"]], writes=[T["sq"]])
                tt("dve", T["kkn"][:], T["kk"][:], T["sq"][:], ALU.mult, [T["kk"], T["sq"]], [T["kkn"]])
                tt("pool", T["bv"][:], T["kkn"][:], T["as"][:], ALU.mult, [T["kkn"], T["as"]], [T["bv"]])
                b.op("dve", lambda e: e.tensor_scalar(out=T["t1"][:], in0=T["as"][:], scalar1=-1.0, scalar2=k_a[:, h:h + 1], op0=ALU.add, op1=ALU.mult),
                     reads=[T["as"], k_a], writes=[T["t1"]])
                b.op("dve", lambda e: e.scalar_tensor_tensor(out=T["kp"][:], in0=T["t1"][:], scalar=1.0, in1=K_, op0=ALU.add, op1=ALU.mult),
                     reads=[T["t1"], X[1]], writes=[T["kp"]])
                tt("pool", T["rk"][:], R_, T["kp"][:], ALU.mult, [X[0], T["kp"]], [T["rk"]])
                b.op("pool", lambda e: e.tensor_scalar_mul(out=T["rk"][:], in0=T["rk"][:], scalar1=r_k[:, h:h + 1]), reads=[T["rk"], r_k], writes=[T["rk"]])
                p = nxt("pp", pp)
                b.op("pe", lambda e: e.matmul(p[0:64, 0:TG], lhsT=ones[:], rhs=T["rk"][:], start=True, stop=True), reads=[ones, T["rk"]], writes=[p])
                tt("dve", BV[:, h, :], p[0:64, 0:TG], V_, ALU.mult, [p, X[2]], [BV])
                b.op("dve", lambda e: e.tensor_tensor_scan(out=T["L"][:], data0=rstm[:], data1=T["lw"][:], initial=0.0, op0=ALU.mult, op1=ALU.add),
                     reads=[rstm, T["lw"]], writes=[T["L"]])
                tt("pool", T["Lx"][:], T["L"][:], T["lw"][:], ALU.subtract, [T["L"], T["lw"]], [T["Lx"]])
                b.op("act", lambda e: e.activation(out=T["Ep"][:], in_=T["L"][:], func=AF.Exp), reads=[T["L"]], writes=[T["Ep"]])
                b.op("act", lambda e: e.activation(out=T["Em"][:], in_=T["L"][:], func=AF.Exp, scale=-1.0), reads=[T["L"]], writes=[T["Em"]])
                b.op("act", lambda e: e.activation(out=T["Ex"][:], in_=T["Lx"][:], func=AF.Exp), reads=[T["Lx"]], writes=[T["Ex"]])
                c3 = lambda ap: ap.rearrange("p (c t) -> p c t", t=64)
                b.op("dve", lambda e: e.scalar_tensor_tensor(out=AR[:, :, 0, :], in0=c3(T["kkn"][:]), scalar=-1.0, in1=c3(T["Ex"][:]), op0=ALU.mult, op1=ALU.mult),
                     reads=[T["kkn"], T["Ex"]], writes=[AR])
                tt("pool", AR[:, :, 1, :], c3(R_), c3(T["Ep"][:]), ALU.mult, [X[0], T["Ep"]], [AR])
                tt("dve", T["BT"][:], T["bv"][:], T["Em"][:], ALU.mult, [T["bv"], T["Em"]], [T["BT"]])
                tt("pool", T["KT"][:], T["kp"][:], T["Em"][:], ALU.mult, [T["kp"], T["Em"]], [T["KT"]])
                gC = c3(T["Ep"][:])[:, :, 63:64].to_broadcast([64, NCH, 64])
                tt("dve", c3(T["BG"][:]), c3(T["BT"][:]), gC, ALU.mult, [T["BT"], T["Ep"]], [T["BG"]])
                tt("pool", c3(T["KG"][:]), c3(T["KT"][:]), gC, ALU.mult, [T["KT"], T["Ep"]], [T["KG"]])
                for c in range(NCH):
                    cs = slice(c * 64, (c + 1) * 64)
                    Hc = Hs[h][(gi * NCH + c) % 2]
                    Hn = Hs[h][(gi * NCH + c + 1) % 2]
                    p = nxt("pq", pq)
                    for j, (src, sb_) in enumerate([(V_[:, cs], X[2]), (T["BG"][:, cs], T["BG"]), (T["KG"][:, cs], T["KG"])]):
                        b.op("pe", lambda e: e.transpose(out=p[0:64, j * 64:(j + 1) * 64], in_=src, identity=idf[0:64, 0:64]), reads=[sb_, idf], writes=[p])
                    tm = nxt("tm", TM)
                    b.op("act", lambda e: e.copy(out=tm[:].rearrange("p a b -> p (a b)"), in_=p[0:64, 0:192]), reads=[p], writes=[tm])
                    p = nxt("pq", pq)
                    arc = AR[:, c, :, :].rearrange("p a t -> p (a t)")
                    b.op("pe", lambda e: e.matmul(p[0:64, 0:128], lhsT=T["BT"][:, cs], rhs=arc, start=True, stop=True), reads=[T["BT"], AR], writes=[p])
                    b.op("pe", lambda e: e.matmul(p[0:64, 128:256], lhsT=T["KT"][:, cs], rhs=arc, start=True, stop=True), reads=[T["KT"], AR], writes=[p])
                    b.op("pe", lambda e: e.matmul(p[0:64, 256:320], lhsT=AR[:, c, 0, :], rhs=T["BT"][:, cs], start=True, stop=True), reads=[T["BT"], AR], writes=[p])
                    xm = nxt("xm", XM)
                    tt("dve", xm[:].rearrange("p (a m) t -> p a m t", a=2), p[0:64, 0:256].rearrange("p (a m t) -> p a m t", a=2, m=2),
                       msk[:, None, 0:2, :].to_broadcast([64, 2, 2, 64]), ALU.mult, [p, msk], [xm])
                    aa = nxt("aa", AA)
                    b.op("pool", lambda e: e.tensor_copy(out=aa[:, 0, :], in_=xm[:, 0, :]), reads=[xm], writes=[aa])
                    tt("dve", aa[:, 1, :], p[0:64, 256:320], msk[:, 2, :], ALU.mult, [p, msk], [aa])
                    P_ = nxt("ppb", PP)
                    tt("pool", P_[:], xm[:, 0, :], idf[0:64, 0:64], ALU.add, [xm, idf], [P_])
                    for step in range(5):
                        pdb = nxt("pd", pd)
                        b.op("pe", lambda e: e.matmul(pdb[0:64, 0:64], lhsT=aa[:, 1, :], rhs=aa[:, 0, :], start=True, stop=True), reads=[aa], writes=[pdb])
                        b.op("pe", lambda e: e.matmul(pdb[0:64, 64:128], lhsT=aa[:, 0, :], rhs=aa[:, 1, :], start=True, stop=True), reads=[aa], writes=[pdb])
                        aa2 = nxt("aa", AA)
                        b.op("act", lambda e: e.copy(out=aa2[:].rearrange("p a t -> p (a t)"), in_=pdb[0:64, 0:128]), reads=[pdb], writes=[aa2])
                        b.op("pe", lambda e: e.matmul(pdb[0:64, 128:192], lhsT=aa2[:, 1, :], rhs=P_[:], start=True, stop=True), reads=[aa2, P_], writes=[pdb])
                        P2 = nxt("ppb", PP)
                        tt("dve", P2[:], pdb[0:64, 128:192], P_[:], ALU.add, [pdb, P_], [P2])
                        aa, P_ = aa2, P2
                    b.op("pe", lambda e: e.matmul(pz[0:64, 0:64], lhsT=xm[:, 2, :], rhs=tm[:, 0, :], start=True, stop=False), reads=[xm, tm], writes=[pz])
                    b.op("pe", lambda e: e.matmul(pz[0:64, 0:64], lhsT=AR[:, c, 0, :], rhs=Hc[:], start=False, stop=True), reads=[AR, Hc], writes=[pz])
                    b.op("act", lambda e: e.copy(out=Xs[:], in_=pz[0:64, 0:64]), reads=[pz], writes=[Xs])
                    b.op("pe", lambda e: e.matmul(pz[0:64, 64:128], lhsT=P_[:], rhs=Xs[:], start=True, stop=True), reads=[P_, Xs], writes=[pz])
                    b.op("act", lambda e: e.copy(out=Us[:], in_=pz[0:64, 64:128]), reads=[pz], writes=[Us])
                    b.op("pe", lambda e: e.matmul(pz[0:64, 128:192], lhsT=AR[:, c, 1, :], rhs=Hc[:], start=True, stop=False), reads=[AR, Hc], writes=[pz])
                    b.op("pe", lambda e: e.matmul(pz[0:64, 128:192], lhsT=xm[:, 1, :], rhs=Us[:], start=False, stop=False), reads=[xm, Us], writes=[pz])
                    b.op("pe", lambda e: e.matmul(pz[0:64, 128:192], lhsT=xm[:, 3, :], rhs=tm[:, 0, :], start=False, stop=True), reads=[xm, tm], writes=[pz])
                    b.op("pe", lambda e: e.matmul(pz[0:64, 192:256], lhsT=tm[:, 1, :], rhs=Us[:], start=True, stop=False), reads=[tm, Us], writes=[pz])
                    b.op("pe", lambda e: e.matmul(pz[0:64, 192:256], lhsT=tm[:, 2, :], rhs=tm[:, 0, :], start=False, stop=True), reads=[tm], writes=[pz])
                    b.op("act", lambda e: e.copy(out=Ytm[:, c, h, :], in_=pz[0:64, 128:192]), reads=[pz], writes=[Ytm])
                    b.op("dve", lambda e: e.scalar_tensor_tensor(out=Hn[:], in0=Hc[:], scalar=T["Ep"][:, c * 64 + 63:c * 64 + 64], in1=pz[0:64, 192:256],
                                                                 op0=ALU.mult, op1=ALU.add), reads=[Hc, T["Ep"], pz], writes=[Hn])
            Y3 = Ytm[:].rearrange("p c h i -> p (c h) i")
            S3 = sqv[:].rearrange("p c h i -> p (c h) i")
            b.op("dve", lambda e: e.tensor_reduce(out=st1[:], in_=Y3, axis=AX.X, op=ALU.add), reads=[Ytm], writes=[st1])
            b.op("pool", lambda e: e.tensor_scalar_mul(out=st1[:], in0=st1[:], scalar1=1.0 / 64), reads=[st1], writes=[st1])
            tt("dve", Y3, Y3, st1[:].unsqueeze(2).to_broadcast([64, NCH * 8, 64]), ALU.subtract, [Ytm, st1], [Ytm])
            tt("pool", S3, Y3, Y3, ALU.mult, [Ytm], [sqv])
            b.op("dve", lambda e: e.tensor_reduce(out=st2[:], in_=S3, axis=AX.X, op=ALU.add), reads=[sqv], writes=[st2])
            b.op("act", lambda e: e.activation(out=st2[:], in_=st2[:], func=AF.Sqrt, scale=1.0 / 64, bias=64e-5), reads=[st2], writes=[st2])
            b.op("dve", lambda e: e.reciprocal(out=st2[:], in_=st2[:]), reads=[st2], writes=[st2])
            tt("dve", Y3, Y3, st2[:].unsqueeze(2).to_broadcast([64, NCH * 8, 64]), ALU.mult, [Ytm, st2], [Ytm])
            lg = lng[:].rearrange("p (h i) -> p h i", i=64)[:, None, :, :].to_broadcast([64, NCH, 8, 64])
            lb = lnb[:].rearrange("p (h i) -> p h i", i=64)[:, None, :, :].to_broadcast([64, NCH, 8, 64])
            tt("pool", Ytm[:], Ytm[:], lg, ALU.mult, [Ytm, lng], [Ytm])
            tt("dve", Ytm[:], Ytm[:], lb, ALU.add, [Ytm, lnb], [Ytm])
            for h in range(8):
                p = nxt("pq", pq)
                for c in range(NCH):
                    b.op("pe", lambda e: e.transpose(out=p[0:64, c * 64:(c + 1) * 64], in_=Ytm[:, c, h, :], identity=idf[0:64, 0:64]), reads=[Ytm, idf], writes=[p])
                tt("dve", otmp[:], p[0:64, 0:TG], BV[:, h, :], ALU.add, [p, BV], [otmp])
                pg_ = nxt("pp", pp)
                b.op("pe", lambda e: e.matmul(pg_[0:64, 0:TG], lhsT=g2s[:, 0, h * 64:(h + 1) * 64], rhs=xs[:, 2, :], start=True, stop=False), reads=[g2s, xs], writes=[pg_])
                b.op("pe", lambda e: e.matmul(pg_[0:64, 0:TG], lhsT=g2s[:, 1, h * 64:(h + 1) * 64], rhs=xs[:, 3, :], start=False, stop=True), reads=[g2s, xs], writes=[pg_])
                ob_ = obf[h % 2]
                tt("dve", ob_[:], otmp[:], pg_[0:64, 0:TG], ALU.mult, [otmp, pg_], [ob_])
                b.dma("pool", self.obT_d[h // 2, (h % 2) * 64:(h % 2) * 64 + 64, q0:q0 + TG], ob_[:], reads=[ob_], writes=[self.obT_d])
        if "rwkv" in self.debug:
            d = self.dbg_out("obT", [4, 128, S], BF16)
            b.dma("pool", d, self.obT_d[:], reads=[self.obT_d])


Prog.phase_rwkv = _phase_rwkv


def build_full():
    p = Prog()
    b = p.b
    p.alloc_root()
    with b.scope():
        p.alloc_persistent()
        p.phase_nsa_proj()
        p.phase_attn2()
    p.phase_rwkv3()
    p.phase_merge()
    p.phase_ffn2()
    p.finish()
    return p


def kernel(**inputs):
    p = build_full()
    consts = host_consts(inputs["rel_bias"])
    shared = {k: np.ascontiguousarray(np.asarray(inputs[k], np.float32)) for k in W_SPECS if k != "x"}
    shared.update(consts)
    x = np.asarray(inputs["x"], np.float32)
    in_maps = []
    for c in range(8):
        m = dict(shared)
        m["x"] = np.ascontiguousarray(x[c])
        in_maps.append(m)
    res = run_bass_kernel_spmd(p.nc, in_maps, core_ids=list(range(8)))
    return np.stack([np.asarray(r["out"], np.float32) for r in res.results], axis=0)


def _phase_rwkv2(self):
    b = self.b
    I = self.inp
    TG = 128
    NCH = 2
    tt = lambda eng, out, in0, in1, op, rd, wr: b.op(eng, lambda e: e.tensor_tensor(out=out, in0=in0, in1=in1, op=op), reads=rd, writes=wr)
    with b.scope():
        W1 = b.sb("W1", [128, 8, 1792], BF16)
        W2 = b.sb("W2", [128, 8, 1792], BF16)
        with b.scope():
            gat = self.load_gain("gat3", I["attn_norm_g"][0])
            stage = [b.sb(f"rst{i}", [128, 1792], F32) for i in range(2)]
            tmpw = [b.sb(f"rtw{i}", [128, 1792], F32) for i in range(2)]
            mur = self.bcast_row("mur", I["rwkv_mu"][0], 1792)
            for c in range(8):
                st = stage[c % 2]
                tw_ = tmpw[c % 2]
                b.dma("sp", st[:], I["w_in"][0][c * 128:(c + 1) * 128, RW0:RW0 + 1792], writes=[st])
                tt("dve", tw_[:], st[:], mur[:], ALU.mult, [st, mur], [tw_])
                b.op("act", lambda e: e.activation(out=W2[:, c, :], in_=tw_[:], func=AF.Copy, scale=gat[:, c:c + 1]), reads=[tw_, gat], writes=[W2])
                tt("pool", st[:], st[:], tw_[:], ALU.subtract, [st, tw_], [st])
                b.op("act", lambda e: e.activation(out=W1[:, c, :], in_=st[:], func=AF.Copy, scale=gat[:, c:c + 1]), reads=[st, gat], writes=[W1])

        def colvec(name, src, n):
            t = b.sb(name, [64, n], F32)
            b.dma("sp", t[:], src.rearrange("(c p) -> p c", p=64), writes=[t], allow_slow_non_contiguous=True)
            return t
        w0 = colvec("w0", I["rwkv_w0"][0], 8)
        a0 = colvec("a0", I["rwkv_a0"][0], 8)
        k_k = colvec("k_k", I["rwkv_k_k"][0], 8)
        k_a = colvec("k_a", I["rwkv_k_a"][0], 8)
        r_k = colvec("r_k", I["rwkv_r_k"][0].rearrange("h d -> (h d)"), 8)
        w2s = b.sb("w2s", [64, 512], F32)
        a2s = b.sb("a2s", [64, 512], F32)
        g2s = b.sb("g2s", [64, 2, 512], F32)
        b.dma("sp", w2s[:], I["rwkv_w2"][0], writes=[w2s])
        b.dma("sp", a2s[:], I["rwkv_a2"][0], writes=[a2s])
        b.dma("sp", g2s[:], I["rwkv_g2"][0].rearrange("(two l) f -> l two f", two=2), writes=[g2s])
        lng = b.sb("lng", [64, 512], F32)
        lnb = b.sb("lnb", [64, 512], F32)
        b.dma("sp", lng[:], I["rwkv_ln_g"][0].partition_broadcast(64), writes=[lng])
        b.dma("sp", lnb[:], I["rwkv_ln_b"][0].partition_broadcast(64), writes=[lnb])
        msk = b.sb("rmsk", [64, 3, 64], F32)
        b.dma("sp", msk[:], I["rwmask"], writes=[msk])
        rstm = b.sb("rstm", [64, 8 * TG], F32)
        b.dma("sp", rstm[:], I["rwreset"], writes=[rstm])
        ones = b.sb("ones64", [64, 64], F32)
        b.op("pool", lambda e: e.memset(ones[:], 1.0), writes=[ones])
        idf = self.identf
        Hst = b.sb("rH", [64, 2, 8, 64], F32)
        b.op("pool", lambda e: e.memset(Hst[:], 0.0), writes=[Hst])
        xt = [b.sb(f"rxt{i}", [128, D], F32) for i in range(1)] * 2
        junk = b.sb("rjunk", [128, D], BF16)
        ss = [b.sb(f"rss{i}", [128, 1], F32) for i in range(1)] * 2
        hb = [b.sb(f"rhb{i}", [128, D], BF16) for i in range(1)] * 2
        hT1 = [b.sb(f"rhT{i}", [128, 8, 128], BF16) for i in range(1)] * 2
        hTs = b.sb("rhTs", [128, 8, TG + 1], BF16)
        b.op("pool", lambda e: e.memset(hTs[:], 0.0), writes=[hTs])
        XL = b.sb("rXL", [64, 20, TG], F32)
        Vtm = b.sb("rVtm", [64, NCH, 512], F32)
        names = ["LW", "AS", "KKN", "BVc", "KP", "RK", "L", "EP", "EM", "BG", "KG"]
        T = {n: b.sb("r" + n, [64, 8, TG], F32) for n in names}
        T["NR"] = T["RK"]
        T["T1"] = T["BG"]
        T["KK"] = T["KG"]
        T["EX"] = T["L"]
        T["BT"] = T["LW"]
        T["KT"] = T["AS"]
        AR = b.sb("rAR", [64, 8, NCH, 2, 64], F32)
        BON = b.sb("rBON", [64, NCH * 8], F32)
        Ytm = b.sb("rYtm", [64, NCH, 8, 64], F32)
        sqv = b.sb("rsqv", [64, NCH, 8, 64], F32)
        st1 = b.sb("rst1", [64, NCH * 8], F32)
        st2 = b.sb("rst2", [64, NCH * 8], F32)
        TM4 = [b.sb(f"rTM{i}", [64, 4, 2, 64], F32) for i in range(2)]
        XM4 = [b.sb(f"rXM{i}", [64, 4, 4, 64], F32) for i in range(2)]
        AA4 = [b.sb(f"rAA{i}", [64, 4, 2, 64], F32) for i in range(2)]
        PP4 = [b.sb(f"rPP{i}", [64, 4, 64], F32) for i in range(2)]
        Xs4 = b.sb("rXs4", [64, 4, 64], F32)
        Us4 = b.sb("rUs4", [64, 4, 64], F32)
        Ht4 = b.sb("rHt4", [64, 4, 64], F32)
        OBb = b.sb("rOBb", [64, NCH, 512], BF16)
        obT = [b.sb(f"robT{i}", [128, 4, TG], BF16) for i in range(1)] * 2
        pt = b.ps("rpt", [128, 8, 128], BF16)
        pP = b.ps("rpP", [128, 512], F32)
        pA = b.ps("rpA", [128, 1024], F32)
        pB = b.ps("rpB", [128, 512], F32)
        pC = b.ps("rpC", [128, 512], F32)
        pD = b.ps("rpD", [128, 512], F32)
        pZ = b.ps("rpZ", [128, 512], F32)
        cnt = {}

        def nxt(k, lst):
            cnt[k] = cnt.get(k, 0) + 1
            return lst[cnt[k] % len(lst)]
        bc = lambda v: v[:].unsqueeze(2).to_broadcast([64, 8, TG])
        f2 = lambda t_: t_[:].rearrange("p h t -> p (h t)")
        c16 = lambda t_: t_[:].rearrange("p h (c t) -> p (h c) t", t=64)

        ngr = getattr(self, "nrg_limit", S // TG)
        for gi in range(ngr):
            q0 = gi * TG
            i = gi % 2
            self.make_hT(I["x"], gi, xt[i], junk, ss[i], hb[i], pt, hT1[i], self.ident)
            b.op("pool", lambda e: e.tensor_copy(out=hTs[:, :, 0:1], in_=hTs[:, :, TG:TG + 1]), reads=[hTs], writes=[hTs])
            b.op("pool", lambda e: e.tensor_copy(out=hTs[:, :, 1:TG + 1], in_=hT1[i][:]), reads=[hT1[i]], writes=[hTs])
            ftiles = list(range(0, 16)) + [24, 25, 26, 27]
            for q4 in range(5):
                for j in range(4):
                    fc = ftiles[q4 * 4 + j]
                    for c in range(8):
                        b.op("pe", lambda e: e.matmul(pP[0:64, j * TG:(j + 1) * TG], lhsT=W1[:, c, fc * 64:(fc + 1) * 64], rhs=hTs[:, c, 1:TG + 1], start=(c == 0), stop=False),
                             reads=[W1, hTs], writes=[pP])
                    for c in range(8):
                        b.op("pe", lambda e: e.matmul(pP[0:64, j * TG:(j + 1) * TG], lhsT=W2[:, c, fc * 64:(fc + 1) * 64], rhs=hTs[:, c, 0:TG], start=False, stop=(c == 7)),
                             reads=[W2, hTs], writes=[pP])
                b.op("act", lambda e: e.copy(out=XL[:, q4 * 4:(q4 + 1) * 4, :].rearrange("p a t -> p (a t)"), in_=pP[0:64, :]), reads=[pP], writes=[XL])
            for c_ in range(NCH):
                for c in range(8):
                    b.op("pe", lambda e: e.matmul(pP[0:64, :], lhsT=hTs[:, c, 1 + c_ * 64:1 + (c_ + 1) * 64], rhs=W1[:, c, 1024:1536], start=(c == 0), stop=False),
                         reads=[W1, hTs], writes=[pP])
                for c in range(8):
                    b.op("pe", lambda e: e.matmul(pP[0:64, :], lhsT=hTs[:, c, c_ * 64:(c_ + 1) * 64], rhs=W2[:, c, 1024:1536], start=False, stop=(c == 7)),
                         reads=[W2, hTs], writes=[pP])
                b.op("act", lambda e: e.copy(out=Vtm[:, c_, :], in_=pP[0:64, :]), reads=[pP], writes=[Vtm])
            R_ = XL[:, 0:8, :]
            K_ = XL[:, 8:16, :]
            b.op("act", lambda e: e.activation(out=XL[:, 16, :], in_=XL[:, 16, :], func=AF.Tanh), reads=[XL], writes=[XL])
            b.op("act", lambda e: e.activation(out=XL[:, 18:20, :], in_=XL[:, 18:20, :], func=AF.Sigmoid), reads=[XL], writes=[XL])
            for (ws_, src, bias_, dst) in [(w2s, 16, w0, "LW"), (a2s, 17, a0, "AS")]:
                for half in range(2):
                    for j in range(4):
                        h = half * 4 + j
                        b.op("pe", lambda e: e.matmul(pP[0:64, j * TG:(j + 1) * TG], lhsT=ws_[:, h * 64:(h + 1) * 64], rhs=XL[:, src, :], start=True, stop=True),
                             reads=[ws_, XL], writes=[pP])
                    for j in range(4):
                        h = half * 4 + j
                        b.op("act", lambda e: e.activation(out=T[dst][:, h, :], in_=pP[0:64, j * TG:(j + 1) * TG], func=AF.Sigmoid, bias=bias_[:, h:h + 1]),
                             reads=[pP, bias_], writes=[T[dst]])
            b.op("pool", lambda e: e.tensor_scalar_mul(out=f2(T["LW"]), in0=f2(T["LW"]), scalar1=-0.6065306597126334), reads=[T["LW"]], writes=[T["LW"]])
            tt("dve", T["KK"][:], K_, bc(k_k), ALU.mult, [XL, k_k], [T["KK"]])
            tt("pool", T["NR"][:], T["KK"][:], T["KK"][:], ALU.mult, [T["KK"]], [T["NR"]])
            for half in range(2):
                b.op("pe", lambda e: e.matmul(pP[0:64, :], lhsT=ones[:], rhs=T["NR"][:, half * 4:(half + 1) * 4, :].rearrange("p h t -> p (h t)"), start=True, stop=True),
                     reads=[ones, T["NR"]], writes=[pP])
                b.op("act", lambda e: e.activation(out=T["KKN"][:, half * 4:(half + 1) * 4, :].rearrange("p h t -> p (h t)"), in_=pP[0:64, :], func=AF.Sqrt),
                     reads=[pP], writes=[T["KKN"]])
            b.op("dve", lambda e: e.tensor_scalar_max(out=f2(T["KKN"]), in0=f2(T["KKN"]), scalar1=1e-12), reads=[T["KKN"]], writes=[T["KKN"]])
            b.op("dve", lambda e: e.reciprocal(out=f2(T["KKN"]), in_=f2(T["KKN"])), reads=[T["KKN"]], writes=[T["KKN"]])
            tt("dve", T["KKN"][:], T["KKN"][:], T["KK"][:], ALU.mult, [T["KKN"], T["KK"]], [T["KKN"]])
            tt("pool", T["BVc"][:], T["KKN"][:], T["AS"][:], ALU.mult, [T["KKN"], T["AS"]], [T["BVc"]])
            b.op("pool", lambda e: e.tensor_scalar_add(out=f2(T["T1"]), in0=f2(T["AS"]), scalar1=-1.0), reads=[T["AS"]], writes=[T["T1"]])
            tt("pool", T["T1"][:], T["T1"][:], bc(k_a), ALU.mult, [T["T1"], k_a], [T["T1"]])
            b.op("dve", lambda e: e.scalar_tensor_tensor(out=f2(T["KP"]), in0=f2(T["T1"]), scalar=1.0, in1=K_.rearrange("p h t -> p (h t)"), op0=ALU.add, op1=ALU.mult),
                 reads=[T["T1"], XL], writes=[T["KP"]])
            tt("pool", T["RK"][:], R_, T["KP"][:], ALU.mult, [XL, T["KP"]], [T["RK"]])
            tt("pool", T["RK"][:], T["RK"][:], bc(r_k), ALU.mult, [T["RK"], r_k], [T["RK"]])
            for c_ in range(NCH):
                for h in range(8):
                    b.op("pe", lambda e: e.matmul(pD[0:64, c_ * 8 + h:c_ * 8 + h + 1], lhsT=T["RK"][:, h, c_ * 64:(c_ + 1) * 64], rhs=ones[:, 0:1], start=True, stop=True),
                         reads=[T["RK"], ones], writes=[pD])
            b.op("act", lambda e: e.copy(out=BON[:], in_=pD[0:64, 0:NCH * 8]), reads=[pD], writes=[BON])
            b.op("dve", lambda e: e.tensor_tensor_scan(out=f2(T["L"]), data0=rstm[:], data1=f2(T["LW"]), initial=0.0, op0=ALU.mult, op1=ALU.add),
                 reads=[rstm, T["LW"]], writes=[T["L"]])
            b.op("act", lambda e: e.activation(out=f2(T["EP"]), in_=f2(T["L"]), func=AF.Exp), reads=[T["L"]], writes=[T["EP"]])
            b.op("act", lambda e: e.activation(out=f2(T["EM"]), in_=f2(T["L"]), func=AF.Exp, scale=-1.0), reads=[T["L"]], writes=[T["EM"]])
            tt("pool", T["L"][:], T["L"][:], T["LW"][:], ALU.subtract, [T["L"], T["LW"]], [T["L"]])
            b.op("act", lambda e: e.activation(out=f2(T["EX"]), in_=f2(T["L"]), func=AF.Exp), reads=[T["L"]], writes=[T["EX"]])
            ar0 = AR[:, :, :, 0, :].rearrange("p h c t -> p (h c) t")
            ar1 = AR[:, :, :, 1, :].rearrange("p h c t -> p (h c) t")
            b.op("dve", lambda e: e.scalar_tensor_tensor(out=ar0, in0=c16(T["KKN"]), scalar=-1.0, in1=c16(T["EX"]), op0=ALU.mult, op1=ALU.mult),
                 reads=[T["KKN"], T["EX"]], writes=[AR])
            tt("pool", ar1, R_.rearrange("p h (c t) -> p (h c) t", t=64), c16(T["EP"]), ALU.mult, [XL, T["EP"]], [AR])
            tt("dve", T["BT"][:], T["BVc"][:], T["EM"][:], ALU.mult, [T["BVc"], T["EM"]], [T["BT"]])
            tt("pool", T["KT"][:], T["KP"][:], T["EM"][:], ALU.mult, [T["KP"], T["EM"]], [T["KT"]])
            gC = c16(T["EP"])[:, :, 63:64].to_broadcast([64, 16, 64])
            tt("dve", c16(T["BG"]), c16(T["BT"]), gC, ALU.mult, [T["BT"], T["EP"]], [T["BG"]])
            tt("pool", c16(T["KG"]), c16(T["KT"]), gC, ALU.mult, [T["KT"], T["EP"]], [T["KG"]])
            for c_ in range(NCH):
                cs = slice(c_ * 64, (c_ + 1) * 64)
                cur = (gi * NCH + c_) % 2
                for hb_ in range(2):
                    heads = list(range(hb_ * 4, hb_ * 4 + 4))
                    for j, h in enumerate(heads):
                        b.op("pe", lambda e: e.transpose(out=pC[0:64, j * 128:j * 128 + 64], in_=T["BG"][:, h, cs], identity=idf[0:64, 0:64]), reads=[T["BG"], idf], writes=[pC])
                        b.op("pe", lambda e: e.transpose(out=pC[0:64, j * 128 + 64:(j + 1) * 128], in_=T["KG"][:, h, cs], identity=idf[0:64, 0:64]), reads=[T["KG"], idf], writes=[pC])
                    tm = nxt("tm", TM4)
                    b.op("act", lambda e: e.copy(out=tm[:].rearrange("p h a t -> p (h a t)"), in_=pC[0:64, 0:512]), reads=[pC], writes=[tm])
                    for j, h in enumerate(heads):
                        arc = AR[:, h, c_, :, :].rearrange("p a t -> p (a t)")
                        b.op("pe", lambda e: e.matmul(pA[0:64, j * 256:j * 256 + 128], lhsT=T["BT"][:, h, cs], rhs=arc, start=True, stop=True), reads=[T["BT"], AR], writes=[pA])
                        b.op("pe", lambda e: e.matmul(pA[0:64, j * 256 + 128:(j + 1) * 256], lhsT=T["KT"][:, h, cs], rhs=arc, start=True, stop=True), reads=[T["KT"], AR], writes=[pA])
                        b.op("pe", lambda e: e.matmul(pB[0:64, j * 64:(j + 1) * 64], lhsT=AR[:, h, c_, 0, :], rhs=T["BT"][:, h, cs], start=True, stop=True), reads=[T["BT"], AR], writes=[pB])
                    xm = nxt("xm", XM4)
                    tt("dve", xm[:].rearrange("p h (a m) t -> p (h a) m t", a=2), pA[0:64, :].rearrange("p (ha m t) -> p ha m t", m=2, t=64),
                       msk[:, None, 0:2, :].to_broadcast([64, 8, 2, 64]), ALU.mult, [pA, msk], [xm])
                    aa = nxt("aa", AA4)
                    b.op("pool", lambda e: e.tensor_copy(out=aa[:, :, 0, :], in_=xm[:, :, 0, :]), reads=[xm], writes=[aa])
                    tt("dve", aa[:, :, 1, :], pB[0:64, 0:256].rearrange("p (h t) -> p h t", t=64), msk[:, 2:3, :].to_broadcast([64, 4, 64]), ALU.mult, [pB, msk], [aa])
                    P_ = nxt("pp4", PP4)
                    tt("pool", P_[:], xm[:, :, 0, :], idf[0:64, None, 0:64].to_broadcast([64, 4, 64]), ALU.add, [xm, idf], [P_])
                    for step in range(5):
                        for j in range(4):
                            b.op("pe", lambda e: e.matmul(pD[0:64, j * 128:j * 128 + 64], lhsT=aa[:, j, 1, :], rhs=aa[:, j, 0, :], start=True, stop=True), reads=[aa], writes=[pD])
                            b.op("pe", lambda e: e.matmul(pD[0:64, j * 128 + 64:(j + 1) * 128], lhsT=aa[:, j, 0, :], rhs=aa[:, j, 1, :], start=True, stop=True), reads=[aa], writes=[pD])
                        aa2 = nxt("aa", AA4)
                        b.op("act", lambda e: e.copy(out=aa2[:].rearrange("p h a t -> p (h a t)"), in_=pD[0:64, :]), reads=[pD], writes=[aa2])
                        for j in range(4):
                            b.op("pe", lambda e: e.matmul(pB[0:64, 256 + j * 64:256 + (j + 1) * 64], lhsT=aa2[:, j, 1, :], rhs=P_[:, j, :], start=True, stop=True), reads=[aa2, P_], writes=[pB])
                        P2 = nxt("pp4", PP4)
                        tt("dve", P2[:], pB[0:64, 256:512].rearrange("p (h t) -> p h t", t=64), P_[:], ALU.add, [pB, P_], [P2])
                        aa, P_ = aa2, P2
                    for j, h in enumerate(heads):
                        b.op("pe", lambda e: e.matmul(pZ[0:64, j * 64:(j + 1) * 64], lhsT=xm[:, j, 2, :], rhs=Vtm[:, c_, h * 64:(h + 1) * 64], start=True, stop=False), reads=[xm, Vtm], writes=[pZ])
                        b.op("pe", lambda e: e.matmul(pZ[0:64, j * 64:(j + 1) * 64], lhsT=AR[:, h, c_, 0, :], rhs=Hst[:, cur, h, :], start=False, stop=True), reads=[AR, Hst], writes=[pZ])
                    b.op("act", lambda e: e.copy(out=Xs4[:].rearrange("p h t -> p (h t)"), in_=pZ[0:64, 0:256]), reads=[pZ], writes=[Xs4])
                    for j in range(4):
                        b.op("pe", lambda e: e.matmul(pZ[0:64, 256 + j * 64:256 + (j + 1) * 64], lhsT=P_[:, j, :], rhs=Xs4[:, j, :], start=True, stop=True), reads=[P_, Xs4], writes=[pZ])
                    b.op("act", lambda e: e.copy(out=Us4[:].rearrange("p h t -> p (h t)"), in_=pZ[0:64, 256:512]), reads=[pZ], writes=[Us4])
                    for j, h in enumerate(heads):
                        o = slice(j * 64, (j + 1) * 64)
                        vh = Vtm[:, c_, h * 64:(h + 1) * 64]
                        b.op("pe", lambda e: e.matmul(pZ[0:64, o], lhsT=AR[:, h, c_, 1, :], rhs=Hst[:, cur, h, :], start=True, stop=False), reads=[AR, Hst], writes=[pZ])
                        b.op("pe", lambda e: e.matmul(pZ[0:64, o], lhsT=xm[:, j, 1, :], rhs=Us4[:, j, :], start=False, stop=False), reads=[xm, Us4], writes=[pZ])
                        b.op("pe", lambda e: e.matmul(pZ[0:64, o], lhsT=xm[:, j, 3, :], rhs=vh, start=False, stop=True), reads=[xm, Vtm], writes=[pZ])
                    for j, h in enumerate(heads):
                        o = slice(256 + j * 64, 256 + (j + 1) * 64)
                        vh = Vtm[:, c_, h * 64:(h + 1) * 64]
                        b.op("pe", lambda e: e.matmul(pZ[0:64, o], lhsT=tm[:, j, 0, :], rhs=Us4[:, j, :], start=True, stop=False), reads=[tm, Us4], writes=[pZ])
                        b.op("pe", lambda e: e.matmul(pZ[0:64, o], lhsT=tm[:, j, 1, :], rhs=vh, start=False, stop=True), reads=[tm, Vtm], writes=[pZ])
                    b.op("act", lambda e: e.copy(out=Ytm[:, c_, hb_ * 4:(hb_ + 1) * 4, :].rearrange("p h t -> p (h t)"), in_=pZ[0:64, 0:256]), reads=[pZ], writes=[Ytm])
                    gH = T["EP"][:, hb_ * 4:(hb_ + 1) * 4, c_ * 64 + 63:c_ * 64 + 64].to_broadcast([64, 4, 64])
                    tt("pool", Ht4[:], Hst[:, cur, hb_ * 4:(hb_ + 1) * 4, :], gH, ALU.mult, [Hst, T["EP"]], [Ht4])
                    tt("dve", Hst[:, 1 - cur, hb_ * 4:(hb_ + 1) * 4, :], pZ[0:64, 256:512].rearrange("p (h t) -> p h t", t=64), Ht4[:], ALU.add, [pZ, Ht4], [Hst])
            Y3 = Ytm[:].rearrange("p c h i -> p (c h) i")
            S3 = sqv[:].rearrange("p c h i -> p (c h) i")
            b.op("dve", lambda e: e.tensor_reduce(out=st1[:], in_=Y3, axis=AX.X, op=ALU.add), reads=[Ytm], writes=[st1])
            b.op("pool", lambda e: e.tensor_scalar_mul(out=st1[:], in0=st1[:], scalar1=1.0 / 64), reads=[st1], writes=[st1])
            tt("dve", Y3, Y3, st1[:].unsqueeze(2).to_broadcast([64, NCH * 8, 64]), ALU.subtract, [Ytm, st1], [Ytm])
            tt("pool", S3, Y3, Y3, ALU.mult, [Ytm], [sqv])
            b.op("dve", lambda e: e.tensor_reduce(out=st2[:], in_=S3, axis=AX.X, op=ALU.add), reads=[sqv], writes=[st2])
            b.op("act", lambda e: e.activation(out=st2[:], in_=st2[:], func=AF.Sqrt, scale=1.0 / 64, bias=64e-5), reads=[st2], writes=[st2])
            b.op("dve", lambda e: e.reciprocal(out=st2[:], in_=st2[:]), reads=[st2], writes=[st2])
            tt("dve", Y3, Y3, st2[:].unsqueeze(2).to_broadcast([64, NCH * 8, 64]), ALU.mult, [Ytm, st2], [Ytm])
            lg = lng[:].rearrange("p (h i) -> p h i", i=64)[:, None, :, :].to_broadcast([64, NCH, 8, 64])
            lb = lnb[:].rearrange("p (h i) -> p h i", i=64)[:, None, :, :].to_broadcast([64, NCH, 8, 64])
            tt("pool", Ytm[:], Ytm[:], lg, ALU.mult, [Ytm, lng], [Ytm])
            tt("dve", Ytm[:], Ytm[:], lb, ALU.add, [Ytm, lnb], [Ytm])
            V3 = Vtm[:].rearrange("p c (h i) -> p (c h) i", i=64)
            tt("pool", S3, V3, BON[:].unsqueeze(2).to_broadcast([64, NCH * 8, 64]), ALU.mult, [Vtm, BON], [sqv])
            tt("dve", Y3, Y3, S3, ALU.add, [Ytm, sqv], [Ytm])
            for c_ in range(NCH):
                for two in range(2):
                    b.op("pe", lambda e: e.matmul(pP[0:64, :], lhsT=XL[:, 18 + two, c_ * 64:(c_ + 1) * 64], rhs=g2s[:, two, :], start=(two == 0), stop=(two == 1)),
                         reads=[XL, g2s], writes=[pP])
                tt("dve", OBb[:, c_, :], Ytm[:, c_, :, :].rearrange("p h i -> p (h i)"), pP[0:64, :], ALU.mult, [Ytm, pP], [OBb])
                for k4 in range(4):
                    b.op("pe", lambda e: e.transpose(out=pt[:, k4, c_ * 64:(c_ + 1) * 64], in_=OBb[:, c_, k4 * 128:(k4 + 1) * 128], identity=self.ident[0:64, 0:64]),
                         reads=[OBb, self.ident], writes=[pt])
            ot = obT[gi % 2]
            b.op("act", lambda e: e.copy(out=ot[:], in_=pt[:, 0:4, :]), reads=[pt], writes=[ot])
            b.dma("pool", self.obT_d[:, :, q0:q0 + TG].rearrange("c p t -> p c t"), ot[:], reads=[ot], writes=[self.obT_d])
        if "rwkv" in self.debug:
            d = self.dbg_out("obT", [4, 128, S], BF16)
            b.dma("pool", d, self.obT_d[:], reads=[self.obT_d])


Prog.phase_rwkv2 = _phase_rwkv2


def _phase_rwkv3(self):
    b = self.b
    I = self.inp
    TG = 128
    NCH = 2
    CHDT = mybir.dt.float32r if getattr(self, "use_f32r", True) else F32
    tt = lambda eng, out, in0, in1, op, rd, wr: b.op(eng, lambda e: e.tensor_tensor(out=out, in0=in0, in1=in1, op=op), reads=rd, writes=wr)
    with b.scope():
        W1 = b.sb("W1", [128, 8, 1792], BF16)
        with b.scope():
            gat = self.load_gain("gat3", I["attn_norm_g"][0])
            stage = [b.sb(f"rst{i}", [128, 1792], F32) for i in range(2)]
            self.load_weight(W1, I["w_in"][0][:, RW0:RW0 + 1792], 1792, gvec=gat, stage=stage)

        def colvec(name, src, n):
            t = b.sb(name, [64, n], F32)
            b.dma("sp", t[:], src.rearrange("(c p) -> p c", p=64), writes=[t], allow_slow_non_contiguous=True)
            return t
        mu = colvec("mu", I["rwkv_mu"][0], 28)
        w0 = colvec("w0", I["rwkv_w0"][0], 8)
        a0 = colvec("a0", I["rwkv_a0"][0], 8)
        k_k = colvec("k_k", I["rwkv_k_k"][0], 8)
        k_a = colvec("k_a", I["rwkv_k_a"][0], 8)
        r_k = colvec("r_k", I["rwkv_r_k"][0].rearrange("h d -> (h d)"), 8)
        w2s = b.sb("w2s", [64, 512], F32)
        a2s = b.sb("a2s", [64, 512], F32)
        g2s = b.sb("g2s", [64, 2, 512], F32)
        b.dma("sp", w2s[:], I["rwkv_w2"][0], writes=[w2s])
        b.dma("sp", a2s[:], I["rwkv_a2"][0], writes=[a2s])
        b.dma("sp", g2s[:], I["rwkv_g2"][0].rearrange("(two l) f -> l two f", two=2), writes=[g2s])
        lng = b.sb("lng", [64, 512], F32)
        lnb = b.sb("lnb", [64, 512], F32)
        b.dma("sp", lng[:], I["rwkv_ln_g"][0].partition_broadcast(64), writes=[lng])
        b.dma("sp", lnb[:], I["rwkv_ln_b"][0].partition_broadcast(64), writes=[lnb])
        msk = b.sb("rmsk", [64, 3, 64], F32)
        b.dma("sp", msk[:], I["rwmask"], writes=[msk])
        rstm = b.sb("rstm", [64, 8 * TG], F32)
        b.dma("sp", rstm[:], I["rwreset"], writes=[rstm])
        ones = b.sb("ones64", [64, 64], F32)
        b.op("pool", lambda e: e.memset(ones[:], 1.0), writes=[ones])
        idf = self.identf
        Hst = b.sb("rH", [64, 2, 8, 64], CHDT)
        b.op("pool", lambda e: e.memset(Hst[:].bitcast(F32), 0.0), writes=[Hst])
        xt = [b.sb(f"rxt{i}", [128, D], F32) for i in range(1)] * 2
        junk = b.sb("rjunk", [128, D], BF16)
        ss = [b.sb(f"rss{i}", [128, 1], F32) for i in range(1)] * 2
        hb = [b.sb(f"rhb{i}", [128, D], BF16) for i in range(1)] * 2
        hT1 = [b.sb(f"rhT{i}", [128, 8, 128], BF16) for i in range(1)] * 2
        PB = b.sb("rPB", [64, 28, TG + 1], F32)
        b.op("pool", lambda e: e.memset(PB[:], 0.0), writes=[PB])
        XL = b.sb("rXL", [64, 28, TG], F32)
        VT2 = [b.sb(f"rVtm{i}", [64, NCH, 512], CHDT) for i in range(2)]
        SXG2 = [b.sb(f"rSXG{i}", [64, 2, TG], F32) for i in range(2)]
        names = ["LW", "AS", "KKN", "BVc", "KP", "RK", "L", "EP", "EM", "BG", "KG"]
        T = {n: b.sb("r" + n, [64, 8, TG], F32) for n in names}
        T["NR"] = T["RK"]
        T["T1"] = T["BG"]
        T["KK"] = T["KG"]
        T["EX"] = T["L"]
        T["BT"] = b.sb("rBTr", [64, 8, TG], CHDT)
        T["KT"] = b.sb("rKTr", [64, 8, TG], CHDT)
        AR = b.sb("rAR", [64, 8, NCH, 2, 64], CHDT)
        BON2 = [b.sb(f"rBON{i}", [64, NCH * 8], F32) for i in range(2)]
        Ytm = b.sb("rYtm", [64, NCH, 8, 64], F32)
        sqv = b.sb("rsqv", [64, NCH, 8, 64], F32)
        st1 = b.sb("rst1", [64, NCH * 8], F32)
        st2 = b.sb("rst2", [64, NCH * 8], F32)
        TM4 = [b.sb(f"rTM{i}", [64, 4, 2, 64], CHDT) for i in range(2)]
        XM4 = [b.sb(f"rXM{i}", [64, 4, 4, 64], CHDT) for i in range(2)]
        AA4 = [[b.sb(f"rAA{u}_{i}", [64, 4, 2, 64], CHDT) for i in range(2)] for u in range(2)]
        PP4 = [[b.sb(f"rPP{u}_{i}", [64, 4, 64], CHDT) for i in range(2)] for u in range(2)]
        Xs8 = b.sb("rXs8", [64, 8, 64], CHDT)
        Us8 = b.sb("rUs8", [64, 8, 64], CHDT)
        Ht8 = b.sb("rHt8", [64, 8, 64], F32)
        OBb = b.sb("rOBb", [64, NCH, 512], BF16)
        obT = [b.sb(f"robT{i}", [128, 4, TG], BF16) for i in range(1)] * 2
        pt = b.ps("rpt", [128, 8, 128], BF16)
        pP = b.ps("rpP", [128, 512], F32)
        pA = b.ps("rpA", [128, 1024], F32)
        pB = b.ps("rpB", [128, 512], F32)
        pC = b.ps("rpC", [128, 512], F32)
        pD = b.ps("rpD", [128, 512], F32)
        pZ = b.ps("rpZ", [128, 512], F32)
        cnt = {}

        def nxt(k, lst):
            cnt[k] = cnt.get(k, 0) + 1
            return lst[cnt[k] % len(lst)]
        bc = lambda v: v[:].unsqueeze(2).to_broadcast([64, 8, TG])
        f2 = lambda t_: t_[:].rearrange("p h t -> p (h t)")
        c16 = lambda t_: t_[:].rearrange("p h (c t) -> p (h c) t", t=64)

        ngr = getattr(self, "nrg_limit", S // TG)
        RR = lambda ap: ap

        def emit_inproj_head(gi):
            i = gi % 2
            self.make_hT(I["x"], gi, xt[i], junk, ss[i], hb[i], pt, hT1[i], self.ident)
            b.op("dve", lambda e: e.tensor_copy(out=PB[:, :, 0:1], in_=PB[:, :, TG:TG + 1]), reads=[PB], writes=[PB])

        def emit_inproj_rounds(gi, rounds):
            i = gi % 2
            for r7 in rounds:
                for j in range(4):
                    fc = r7 * 4 + j
                    for c in range(8):
                        b.op("pe", lambda e: e.matmul(pP[0:64, j * TG:(j + 1) * TG], lhsT=W1[:, c, fc * 64:(fc + 1) * 64], rhs=hT1[i][:, c, :], start=(c == 0), stop=(c == 7)),
                             reads=[W1, hT1[i]], writes=[pP])
                b.op("act", lambda e: e.copy(out=PB[:, r7 * 4:(r7 + 1) * 4, 1:TG + 1], in_=pP[0:64, :].rearrange("p (a t) -> p a t", t=TG)), reads=[pP], writes=[PB])

        emit_inproj_head(0)
        emit_inproj_rounds(0, range(7))
        def prep(gi, hook=None):
            Vtm, BON, SXG = VT2[gi % 2], BON2[gi % 2], SXG2[gi % 2]
            tt("dve", XL[:], PB[:, :, 0:TG], PB[:, :, 1:TG + 1], ALU.subtract, [PB], [XL])
            tt("dve", XL[:], XL[:], mu[:].unsqueeze(2).to_broadcast([64, 28, TG]), ALU.mult, [XL, mu], [XL])
            tt("dve", XL[:], XL[:], PB[:, :, 1:TG + 1], ALU.add, [XL, PB], [XL])
            if gi + 1 < ngr:
                emit_inproj_head(gi + 1)
            for c_ in range(NCH):
                for h in range(8):
                    b.op("pe", lambda e: e.transpose(out=pC[0:64, h * 64:(h + 1) * 64], in_=XL[:, 16 + h, c_ * 64:(c_ + 1) * 64], identity=idf[0:64, 0:64]), reads=[XL, idf], writes=[pC])
                b.op("act", lambda e: e.copy(out=Vtm[:, c_, :], in_=pC[0:64, :]), reads=[pC], writes=[Vtm])
            R_ = XL[:, 0:8, :]
            K_ = XL[:, 8:16, :]
            b.op("act", lambda e: e.activation(out=XL[:, 24, :], in_=XL[:, 24, :], func=AF.Tanh), reads=[XL], writes=[XL])
            b.op("act", lambda e: e.activation(out=SXG[:], in_=XL[:, 26:28, :], func=AF.Sigmoid), reads=[XL], writes=[SXG])
            for (ws_, src, bias_, dst) in [(w2s, 24, w0, "LW"), (a2s, 25, a0, "AS")]:
                for half in range(2):
                    for j in range(4):
                        h = half * 4 + j
                        b.op("pe", lambda e: e.matmul(pP[0:64, j * TG:(j + 1) * TG], lhsT=ws_[:, h * 64:(h + 1) * 64], rhs=XL[:, src, :], start=True, stop=True),
                             reads=[ws_, XL], writes=[pP])
                    for j in range(4):
                        h = half * 4 + j
                        b.op("act", lambda e: e.activation(out=T[dst][:, h, :], in_=pP[0:64, j * TG:(j + 1) * TG], func=AF.Sigmoid, bias=bias_[:, h:h + 1]),
                             reads=[pP, bias_], writes=[T[dst]])
            b.op("dve", lambda e: e.tensor_scalar_mul(out=f2(T["LW"]), in0=f2(T["LW"]), scalar1=-0.6065306597126334), reads=[T["LW"]], writes=[T["LW"]])
            tt("dve", T["KK"][:], K_, bc(k_k), ALU.mult, [XL, k_k], [T["KK"]])
            tt("dve", T["NR"][:], T["KK"][:], T["KK"][:], ALU.mult, [T["KK"]], [T["NR"]])
            for half in range(2):
                b.op("pe", lambda e: e.matmul(pP[0:64, :], lhsT=ones[:], rhs=T["NR"][:, half * 4:(half + 1) * 4, :].rearrange("p h t -> p (h t)"), start=True, stop=True),
                     reads=[ones, T["NR"]], writes=[pP])
                b.op("act", lambda e: e.activation(out=T["KKN"][:, half * 4:(half + 1) * 4, :].rearrange("p h t -> p (h t)"), in_=pP[0:64, :], func=AF.Sqrt),
                     reads=[pP], writes=[T["KKN"]])
            if gi + 1 < ngr:
                emit_inproj_rounds(gi + 1, range(0, 4))
            b.op("dve", lambda e: e.tensor_scalar_max(out=f2(T["KKN"]), in0=f2(T["KKN"]), scalar1=1e-12), reads=[T["KKN"]], writes=[T["KKN"]])
            b.op("dve", lambda e: e.reciprocal(out=f2(T["KKN"]), in_=f2(T["KKN"])), reads=[T["KKN"]], writes=[T["KKN"]])
            tt("dve", T["KKN"][:], T["KKN"][:], T["KK"][:], ALU.mult, [T["KKN"], T["KK"]], [T["KKN"]])
            tt("dve", T["BVc"][:], T["KKN"][:], T["AS"][:], ALU.mult, [T["KKN"], T["AS"]], [T["BVc"]])
            b.op("dve", lambda e: e.tensor_scalar_add(out=f2(T["T1"]), in0=f2(T["AS"]), scalar1=-1.0), reads=[T["AS"]], writes=[T["T1"]])
            tt("dve", T["T1"][:], T["T1"][:], bc(k_a), ALU.mult, [T["T1"], k_a], [T["T1"]])
            b.op("dve", lambda e: e.scalar_tensor_tensor(out=f2(T["KP"]), in0=f2(T["T1"]), scalar=1.0, in1=K_.rearrange("p h t -> p (h t)"), op0=ALU.add, op1=ALU.mult),
                 reads=[T["T1"], XL], writes=[T["KP"]])
            tt("dve", T["RK"][:], R_, T["KP"][:], ALU.mult, [XL, T["KP"]], [T["RK"]])
            tt("dve", T["RK"][:], T["RK"][:], bc(r_k), ALU.mult, [T["RK"], r_k], [T["RK"]])
            for c_ in range(NCH):
                for h in range(8):
                    b.op("pe", lambda e: e.matmul(pD[0:64, c_ * 8 + h:c_ * 8 + h + 1], lhsT=T["RK"][:, h, c_ * 64:(c_ + 1) * 64], rhs=ones[:, 0:1], start=True, stop=True),
                         reads=[T["RK"], ones], writes=[pD])
            b.op("act", lambda e: e.copy(out=BON[:], in_=pD[0:64, 0:NCH * 8]), reads=[pD], writes=[BON])
            if gi + 1 < ngr:
                emit_inproj_rounds(gi + 1, range(4, 7))
            b.op("dve", lambda e: e.tensor_tensor_scan(out=f2(T["L"]), data0=rstm[:], data1=f2(T["LW"]), initial=0.0, op0=ALU.mult, op1=ALU.add),
                 reads=[rstm, T["LW"]], writes=[T["L"]])
            b.op("act", lambda e: e.activation(out=f2(T["EP"]), in_=f2(T["L"]), func=AF.Exp), reads=[T["L"]], writes=[T["EP"]])
            b.op("act", lambda e: e.activation(out=f2(T["EM"]), in_=f2(T["L"]), func=AF.Exp, scale=-1.0), reads=[T["L"]], writes=[T["EM"]])
            tt("dve", T["L"][:], T["L"][:], T["LW"][:], ALU.subtract, [T["L"], T["LW"]], [T["L"]])
            b.op("act", lambda e: e.activation(out=f2(T["EX"]), in_=f2(T["L"]), func=AF.Exp), reads=[T["L"]], writes=[T["EX"]])
            ar0 = AR[:, :, :, 0, :].rearrange("p h c t -> p (h c) t")
            ar1 = AR[:, :, :, 1, :].rearrange("p h c t -> p (h c) t")
            b.op("dve", lambda e: e.scalar_tensor_tensor(out=ar0, in0=c16(T["KKN"]), scalar=-1.0, in1=c16(T["EX"]), op0=ALU.mult, op1=ALU.mult),
                 reads=[T["KKN"], T["EX"]], writes=[AR])
            tt("dve", ar1, R_.rearrange("p h (c t) -> p (h c) t", t=64), c16(T["EP"]), ALU.mult, [XL, T["EP"]], [AR])
            tt("dve", T["BT"][:], T["BVc"][:], T["EM"][:], ALU.mult, [T["BVc"], T["EM"]], [T["BT"]])
            tt("dve", T["KT"][:], T["KP"][:], T["EM"][:], ALU.mult, [T["KP"], T["EM"]], [T["KT"]])
            gC = c16(T["EP"])[:, :, 63:64].to_broadcast([64, 16, 64])
            tt("dve", c16(T["BG"]), c16(T["BT"]), gC, ALU.mult, [T["BT"], T["EP"]], [T["BG"]])
            tt("dve", c16(T["KG"]), c16(T["KT"]), gC, ALU.mult, [T["KT"], T["EP"]], [T["KG"]])

        def chains(gi, hook=None):
            Vtm, BON, SXG = VT2[gi % 2], BON2[gi % 2], SXG2[gi % 2]
            for c_ in range(NCH):
                cs = slice(c_ * 64, (c_ + 1) * 64)
                cur = (gi * NCH + c_) % 2
                U_ = []
                for u in range(2):
                    heads = list(range(u * 4, u * 4 + 4))
                    pBu = pB if u == 0 else pC
                    for j, h in enumerate(heads):
                        b.op("pe", lambda e: e.transpose(out=pZ[0:64, j * 128:j * 128 + 64], in_=T["BG"][:, h, cs], identity=idf[0:64, 0:64]), reads=[T["BG"], idf], writes=[pZ])
                        b.op("pe", lambda e: e.transpose(out=pZ[0:64, j * 128 + 64:(j + 1) * 128], in_=T["KG"][:, h, cs], identity=idf[0:64, 0:64]), reads=[T["KG"], idf], writes=[pZ])
                    tm = TM4[u]
                    b.op("act", lambda e: e.copy(out=tm[:].rearrange("p h a t -> p (h a t)"), in_=pZ[0:64, 0:512]), reads=[pZ], writes=[tm])
                    for j, h in enumerate(heads):
                        arc = AR[:, h, c_, :, :].rearrange("p a t -> p (a t)")
                        b.op("pe", lambda e: e.matmul(pA[0:64, j * 256:j * 256 + 128], lhsT=T["BT"][:, h, cs], rhs=arc, start=True, stop=True), reads=[T["BT"], AR], writes=[pA])
                        b.op("pe", lambda e: e.matmul(pA[0:64, j * 256 + 128:(j + 1) * 256], lhsT=T["KT"][:, h, cs], rhs=arc, start=True, stop=True), reads=[T["KT"], AR], writes=[pA])
                        b.op("pe", lambda e: e.matmul(pBu[0:64, j * 64:(j + 1) * 64], lhsT=AR[:, h, c_, 0, :], rhs=T["BT"][:, h, cs], start=True, stop=True), reads=[T["BT"], AR], writes=[pBu])
                    xm = XM4[u]
                    tt("dve", xm[:].rearrange("p h (a m) t -> p (h a) m t", a=2), pA[0:64, :].rearrange("p (ha m t) -> p ha m t", m=2, t=64),
                       msk[:, None, 0:2, :].to_broadcast([64, 8, 2, 64]), ALU.mult, [pA, msk], [xm])
                    aa = AA4[u][0]
                    b.op("dve", lambda e: e.tensor_copy(out=aa[:, :, 0, :], in_=xm[:, :, 0, :]), reads=[xm], writes=[aa])
                    tt("dve", aa[:, :, 1, :], pBu[0:64, 0:256].rearrange("p (h t) -> p h t", t=64), msk[:, 2:3, :].to_broadcast([64, 4, 64]), ALU.mult, [pBu, msk], [aa])
                    P_ = PP4[u][0]
                    tt("dve", P_[:], xm[:, :, 0, :], idf[0:64, None, 0:64].to_broadcast([64, 4, 64]), ALU.add, [xm, idf], [P_])
                    U_.append(dict(tm=tm, xm=xm, aa=aa, P=P_, pB=pBu, pD=(pD if u == 0 else pP), k=0))
                for step in range(5):
                    if hook is not None and c_ == 0:
                        next(hook, None)
                        next(hook, None)
                    for u_ in U_:
                        aa, pDu = u_["aa"], u_["pD"]
                        for j in range(4):
                            b.op("pe", lambda e: e.matmul(pDu[0:64, j * 128:j * 128 + 64], lhsT=RR(aa[:, j, 1, :]), rhs=RR(aa[:, j, 0, :]), start=True, stop=True), reads=[aa], writes=[pDu])
                            b.op("pe", lambda e: e.matmul(pDu[0:64, j * 128 + 64:(j + 1) * 128], lhsT=RR(aa[:, j, 0, :]), rhs=RR(aa[:, j, 1, :]), start=True, stop=True), reads=[aa], writes=[pDu])
                    for ui, u_ in enumerate(U_):
                        u_["k"] += 1
                        aa2 = AA4[ui][u_["k"] % 2]
                        b.op("act", lambda e: e.copy(out=aa2[:].rearrange("p h a t -> p (h a t)"), in_=u_["pD"][0:64, :]), reads=[u_["pD"]], writes=[aa2])
                        u_["aa"] = aa2
                    for u_ in U_:
                        for j in range(4):
                            b.op("pe", lambda e: e.matmul(u_["pB"][0:64, 256 + j * 64:256 + (j + 1) * 64], lhsT=RR(u_["aa"][:, j, 1, :]), rhs=RR(u_["P"][:, j, :]), start=True, stop=True),
                                 reads=[u_["aa"], u_["P"]], writes=[u_["pB"]])
                    for ui, u_ in enumerate(U_):
                        P2 = PP4[ui][u_["k"] % 2]
                        tt("dve", P2[:], u_["pB"][0:64, 256:512].rearrange("p (h t) -> p h t", t=64), u_["P"][:], ALU.add, [u_["pB"], u_["P"]], [P2])
                        u_["P"] = P2
                if hook is not None and c_ == 0:
                    for _ in hook:
                        pass
                for h in range(8):
                    u_, j = U_[h // 4], h % 4
                    o = slice(h * 64, (h + 1) * 64)
                    b.op("pe", lambda e: e.matmul(pA[0:64, o], lhsT=u_["xm"][:, j, 2, :], rhs=Vtm[:, c_, o], start=True, stop=False), reads=[u_["xm"], Vtm], writes=[pA])
                    b.op("pe", lambda e: e.matmul(pA[0:64, o], lhsT=AR[:, h, c_, 0, :], rhs=Hst[:, cur, h, :], start=False, stop=True), reads=[AR, Hst], writes=[pA])
                b.op("act", lambda e: e.copy(out=Xs8[:].rearrange("p h t -> p (h t)"), in_=pA[0:64, 0:512]), reads=[pA], writes=[Xs8])
                for h in range(8):
                    u_, j = U_[h // 4], h % 4
                    b.op("pe", lambda e: e.matmul(pA[0:64, 512 + h * 64:512 + (h + 1) * 64], lhsT=u_["P"][:, j, :], rhs=Xs8[:, h, :], start=True, stop=True), reads=[u_["P"], Xs8], writes=[pA])
                b.op("act", lambda e: e.copy(out=Us8[:].rearrange("p h t -> p (h t)"), in_=pA[0:64, 512:1024]), reads=[pA], writes=[Us8])
                for h in range(8):
                    u_, j = U_[h // 4], h % 4
                    o = slice(h * 64, (h + 1) * 64)
                    b.op("pe", lambda e: e.matmul(pD[0:64, o], lhsT=u_["tm"][:, j, 0, :], rhs=Us8[:, h, :], start=True, stop=False), reads=[u_["tm"], Us8], writes=[pD])
                    b.op("pe", lambda e: e.matmul(pD[0:64, o], lhsT=u_["tm"][:, j, 1, :], rhs=Vtm[:, c_, o], start=False, stop=True), reads=[u_["tm"], Vtm], writes=[pD])
                tt("dve", Ht8[:], Hst[:, cur, :, :], T["EP"][:, :, c_ * 64 + 63:c_ * 64 + 64].to_broadcast([64, 8, 64]), ALU.mult, [Hst, T["EP"]], [Ht8])
                tt("dve", Hst[:, 1 - cur, :, :], pD[0:64, :].rearrange("p (h t) -> p h t", t=64), Ht8[:], ALU.add, [pD, Ht8], [Hst])
                for h in range(8):
                    u_, j = U_[h // 4], h % 4
                    o = slice(h * 64, (h + 1) * 64)
                    b.op("pe", lambda e: e.matmul(pZ[0:64, o], lhsT=AR[:, h, c_, 1, :], rhs=Hst[:, cur, h, :], start=True, stop=False), reads=[AR, Hst], writes=[pZ])
                    b.op("pe", lambda e: e.matmul(pZ[0:64, o], lhsT=u_["xm"][:, j, 1, :], rhs=Us8[:, h, :], start=False, stop=False), reads=[u_["xm"], Us8], writes=[pZ])
                    b.op("pe", lambda e: e.matmul(pZ[0:64, o], lhsT=u_["xm"][:, j, 3, :], rhs=Vtm[:, c_, o], start=False, stop=True), reads=[u_["xm"], Vtm], writes=[pZ])
                b.op("act", lambda e: e.copy(out=Ytm[:, c_, :, :].rearrange("p h t -> p (h t)"), in_=pZ[0:64, :]), reads=[pZ], writes=[Ytm])

        def post(gi):
            q0 = gi * TG
            Vtm, BON, SXG = VT2[gi % 2], BON2[gi % 2], SXG2[gi % 2]
            Y3 = Ytm[:].rearrange("p c h i -> p (c h) i")
            S3 = sqv[:].rearrange("p c h i -> p (c h) i")
            b.op("dve", lambda e: e.tensor_reduce(out=st1[:], in_=Y3, axis=AX.X, op=ALU.add), reads=[Ytm], writes=[st1])
            b.op("dve", lambda e: e.tensor_scalar_mul(out=st1[:], in0=st1[:], scalar1=1.0 / 64), reads=[st1], writes=[st1])
            yield
            tt("dve", Y3, Y3, st1[:].unsqueeze(2).to_broadcast([64, NCH * 8, 64]), ALU.subtract, [Ytm, st1], [Ytm])
            tt("dve", S3, Y3, Y3, ALU.mult, [Ytm], [sqv])
            yield
            b.op("dve", lambda e: e.tensor_reduce(out=st2[:], in_=S3, axis=AX.X, op=ALU.add), reads=[sqv], writes=[st2])
            b.op("act", lambda e: e.activation(out=st2[:], in_=st2[:], func=AF.Sqrt, scale=1.0 / 64, bias=64e-5), reads=[st2], writes=[st2])
            yield
            b.op("dve", lambda e: e.reciprocal(out=st2[:], in_=st2[:]), reads=[st2], writes=[st2])
            tt("dve", Y3, Y3, st2[:].unsqueeze(2).to_broadcast([64, NCH * 8, 64]), ALU.mult, [Ytm, st2], [Ytm])
            yield
            lg = lng[:].rearrange("p (h i) -> p h i", i=64)[:, None, :, :].to_broadcast([64, NCH, 8, 64])
            lb = lnb[:].rearrange("p (h i) -> p h i", i=64)[:, None, :, :].to_broadcast([64, NCH, 8, 64])
            tt("dve", Ytm[:], Ytm[:], lg, ALU.mult, [Ytm, lng], [Ytm])
            tt("dve", Ytm[:], Ytm[:], lb, ALU.add, [Ytm, lnb], [Ytm])
            yield
            V3 = Vtm[:].rearrange("p c (h i) -> p (c h) i", i=64)
            tt("dve", S3, V3, BON[:].unsqueeze(2).to_broadcast([64, NCH * 8, 64]), ALU.mult, [Vtm, BON], [sqv])
            tt("dve", Y3, Y3, S3, ALU.add, [Ytm, sqv], [Ytm])
            yield
            for c_ in range(NCH):
                for two in range(2):
                    b.op("pe", lambda e: e.matmul(pP[0:64, :], lhsT=SXG[:, two, c_ * 64:(c_ + 1) * 64], rhs=g2s[:, two, :], start=(two == 0), stop=(two == 1)),
                         reads=[SXG, g2s], writes=[pP])
                tt("dve", OBb[:, c_, :], Ytm[:, c_, :, :].rearrange("p h i -> p (h i)"), pP[0:64, :], ALU.mult, [Ytm, pP], [OBb])
                for k4 in range(4):
                    b.op("pe", lambda e: e.transpose(out=pt[:, k4, c_ * 64:(c_ + 1) * 64], in_=OBb[:, c_, k4 * 128:(k4 + 1) * 128], identity=self.ident[0:64, 0:64]),
                         reads=[OBb, self.ident], writes=[pt])
                yield
            ot = obT[gi % 2]
            b.op("act", lambda e: e.copy(out=ot[:], in_=pt[:, 0:4, :]), reads=[pt], writes=[ot])
            b.dma("pool", self.obT_d[:, :, q0:q0 + TG].rearrange("c p t -> p c t"), ot[:], reads=[ot], writes=[self.obT_d])

        prep(0)
        for gi in range(ngr):
            pg = post(gi - 1) if gi > 0 else None
            chains(gi, hook=pg)
            if pg is not None:
                for _ in pg:
                    pass
            if gi + 1 < ngr:
                prep(gi + 1)
        for _ in post(ngr - 1):
            pass
        if "rwkv" in self.debug:
            d = self.dbg_out("obT", [4, 128, S], BF16)
            b.dma("pool", d, self.obT_d[:], reads=[self.obT_d])


Prog.phase_rwkv3 = _phase_rwkv3


def _phase_ffn2(self):
    b = self.b
    I = self.inp
    TG = 256
    NFT = 44
    with b.scope():
        gf = self.load_gain("gf", I["ffn_norm_g"][0])
        stage = [b.sb(f"fst{i}", [128, 1024], F32) for i in range(2)]
        wu = b.sb("wu", [128, 8, 2 * DFF], BF16)
        for n in range(8):
            for c in range(8):
                st = stage[c % 2]
                b.dma("sp", st[:, 0:704], I["w_up"][0][c * 128:(c + 1) * 128, n * 704:(n + 1) * 704], writes=[st])
                b.op("act", lambda e: e.activation(out=wu[:, c, n * 704:(n + 1) * 704], in_=st[:, 0:704], func=AF.Copy, scale=gf[:, c:c + 1]),
                     reads=[st, gf], writes=[wu])
        wd = b.sb("wd", [128, 22, D], BF16)
        self.load_weight(wd, I["w_down"][0], D, kch=22, stage=stage, eng="dve")
        cw = b.sb("cw", [128, 3, NFT], F32)
        for j in range(3):
            b.dma("sp", cw[:, j, :], I["conv_w"][0][j].rearrange("(c p) -> p c", p=128), writes=[cw], allow_slow_non_contiguous=True)
        cbias = self.load_gain("cbias", I["conv_b"][0], kch=NFT)
        xt = [b.sb(f"fxt{i}", [128, D], F32) for i in range(4)]
        junk = b.sb("fjunk", [128, D], BF16)
        ss = [b.sb(f"fss{i}", [128, 1], F32) for i in range(2)]
        hb = [b.sb(f"fhb{i}", [128, D], BF16) for i in range(2)]
        hTg2 = [b.sb(f"fhTg{i}", [128, 8, TG + 2], BF16) for i in range(2)]
        for i_ in range(2):
            b.op("pool", lambda e: e.memset(hTg2[i_][:], 0.0), writes=[hTg2[i_]])
        cv = [b.sb(f"cv{i}", [128, TG], F32) for i in range(5)]
        sgl = [b.sb(f"sgl{i}", [128, TG], BF16) for i in range(4)]
        actT = b.sb("actT", [128, 22, TG], BF16)
        val = b.sb("fval", [128, 22, TG], BF16)
        pt = b.ps("fpt", [128, 8, 128], BF16)
        pu = [b.ps(f"fpu{i}", [128, 512], F32) for i in range(4)]
        pd = [b.ps(f"fpd{i}", [128, 512], F32) for i in range(2)]
        ng = getattr(self, "nt_limit", NT) * 128 // TG
        def stageH(gi):
            hTg = hTg2[gi % 2]
            if gi > 0:
                prev = hTg2[(gi - 1) % 2]
                b.op("pool", lambda e: e.tensor_copy(out=hTg[:, :, 0:2], in_=prev[:, :, TG:TG + 2]), reads=[prev], writes=[hTg])
            for s_ in range(TG // 128):
                t = gi * (TG // 128) + s_
                xi = (gi % 2) * 2 + s_
                self.make_hT(self.x1_d, t, xt[xi], junk, ss[s_], hb[s_], pt, hTg, self.ident, hT_ap=hTg[:, :, 2 + s_ * 128:2 + (s_ + 1) * 128])

        stageH(0)
        for gi in range(ng):
            hTg = hTg2[gi % 2]
            if gi + 1 < ng:
                stageH(gi + 1)
            pending = []
            for ft in range(NFT):
                p = pu[ft % 4]
                c_ = cv[ft % 5]
                for c in range(8):
                    b.op("pe", lambda e: e.matmul(p[:, 0:TG + 2], lhsT=wu[:, c, ft * 128:(ft + 1) * 128], rhs=hTg[:, c, :], start=(c == 0), stop=(c == 7)),
                         reads=[wu, hTg], writes=[p])
                b.op("act", lambda e: e.activation(out=c_[:], in_=p[:, 0:TG], func=AF.Identity, scale=cw[:, 0, ft:ft + 1], bias=cbias[:, ft:ft + 1]),
                     reads=[p, cw, cbias], writes=[c_])
                b.op("dve", lambda e: e.scalar_tensor_tensor(out=c_[:], in0=p[:, 1:TG + 1], scalar=cw[:, 1, ft:ft + 1], in1=c_[:], op0=ALU.mult, op1=ALU.add),
                     reads=[p, cw, c_], writes=[c_])
                if ft < 22:
                    b.op("dve", lambda e: e.scalar_tensor_tensor(out=val[:, ft, :], in0=p[:, 2:TG + 2], scalar=cw[:, 2, ft:ft + 1], in1=c_[:], op0=ALU.mult, op1=ALU.add),
                         reads=[p, cw, c_], writes=[val])
                else:
                    b.op("dve", lambda e: e.scalar_tensor_tensor(out=c_[:], in0=p[:, 2:TG + 2], scalar=cw[:, 2, ft:ft + 1], in1=c_[:], op0=ALU.mult, op1=ALU.add),
                         reads=[p, cw, c_], writes=[c_])
                    pending.append((ft, c_))
                if len(pending) > (1 if ft < NFT - 1 else 0):
                    while len(pending) > (1 if ft < NFT - 1 else 0):
                        f_, cc = pending.pop(0)
                        sg_ = sgl[f_ % 4]
                        b.op("act", lambda e: e.activation(out=sg_[:], in_=cc[:], func=AF.Silu), reads=[cc], writes=[sg_])
                        eng_ = "pool" if f_ % 2 == 0 else "dve"
                        b.op(eng_, lambda e: e.tensor_tensor(out=actT[:, f_ - 22, :], in0=sg_[:], in1=val[:, f_ - 22, :], op=ALU.mult),
                             reads=[sg_, val], writes=[actT])
            for s_ in range(TG // 128):
                t = gi * (TG // 128) + s_
                for n in range(2):
                    for f in range(22):
                        b.op("pe", lambda e: e.matmul(pd[n][:, :], lhsT=actT[:, f, s_ * 128:(s_ + 1) * 128], rhs=wd[:, f, n * 512:(n + 1) * 512], start=(f == 0), stop=(f == 21)),
                             reads=[actT, wd], writes=[pd[n]])
                    xo = xt[(gi % 2) * 2 + s_]
                    b.op("dve", lambda e: e.tensor_tensor(out=xo[:, n * 512:(n + 1) * 512], in0=pd[n][:, :], in1=xo[:, n * 512:(n + 1) * 512], op=ALU.add),
                         reads=[pd[n], xo], writes=[xo])
                b.dma("pool", self.out[t * 128:(t + 1) * 128, :], xt[(gi % 2) * 2 + s_][:], reads=[xt[(gi % 2) * 2 + s_]])


Prog.phase_ffn2 = _phase_ffn2
```

```python
import contextlib
import numpy as np
import ml_dtypes
import concourse.bass as bass
import concourse.mybir as mybir
from concourse.bass_utils import run_bass_kernel_spmd

F32 = mybir.dt.float32
BF16 = mybir.dt.bfloat16
AF = mybir.ActivationFunctionType
ALU = mybir.AluOpType
AX = mybir.AxisListType

S = 4096
D = 1024
NT = S // 128
IN_WIDTH = 5144
RW0 = 1304
GA0 = 3096
GB0 = 4120
DFF = 2816
RMS_EPS = 1e-6


class Buf:
    def __init__(self, t, name):
        self.t = t
        self.name = name
        self.w = None
        self.r = {}
        self.psum = False

    def __getitem__(self, idx):
        return self.t[idx]


class Builder:
    SEM_ROLL = 30000

    def __init__(self, nc):
        self.nc = nc
        self.stack = contextlib.ExitStack()
        self.root = self.stack
        self.eng = {"pe": nc.tensor, "act": nc.scalar, "dve": nc.vector,
                    "pool": nc.gpsimd, "sp": nc.sync}
        self.sem = {}
        self.cnt = {}
        self.seen = {e: {} for e in self.eng}
        self.nsem = 0
        self.lanes = {}
        self.lane_rr = {}
        self.last_tok = {}
        for e in self.eng:
            self._roll(e)

    def newsem(self, name):
        self.nsem += 1
        return self.root.enter_context(self.nc.semaphore(f"{name}_{self.nsem}"))

    def sb(self, name, shape, dt=F32):
        self.nsem += 1
        name = f"sb{self.nsem}_{name}"
        return Buf(self.stack.enter_context(self.nc.sbuf_tensor(name, list(shape), dt)), name)

    def ps(self, name, shape, dt=F32):
        self.nsem += 1
        name = f"ps{self.nsem}_{name}"
        bf = Buf(self.stack.enter_context(self.nc.psum_tensor(name, list(shape), dt)), name)
        bf.psum = True
        return bf

    def dram(self, name, shape, dt=F32, kind="Internal"):
        return Buf(self.nc.dram_tensor(name, list(shape), dt, kind=kind), name)

    def _roll(self, e):
        self.sem[e] = self.newsem("s" + e)
        self.cnt[e] = 0

    def _wait(self, e, tok):
        sem, val = tok
        k = id(sem)
        if self.seen[e].get(k, 0) < val:
            self.eng[e].wait_ge(sem, val)
            self.seen[e][k] = val

    def _deps(self, e, reads, writes):
        for b in reads:
            if b.w is not None:
                we, tok = b.w
                self._wait(e, tok)
            if b.psum:
                for re_, tok in b.r.items():
                    if re_ != e:
                        self._wait(e, tok)
        for b in writes:
            if b.w is not None:
                we, tok = b.w
                if we != e:
                    self._wait(e, tok)
            for re_, tok in b.r.items():
                if re_ != e:
                    self._wait(e, tok)

    def op(self, e, fn, reads=(), writes=()):
        if self.cnt[e] >= self.SEM_ROLL:
            self._roll(e)
        self._deps(e, reads, writes)
        ins = fn(self.eng[e])
        self.cnt[e] += 1
        tok = (self.sem[e], self.cnt[e])
        ins.then_inc(self.sem[e], 1)
        self.last_tok[e] = tok
        for b in reads:
            b.r[e] = tok
        for b in writes:
            b.w = (e, tok)
            b.r = {}
        return tok

    def dma(self, q, out, in_, reads=(), writes=(), nlanes=6, **kw):
        if q not in self.lanes:
            self.lanes[q] = [[self.newsem("l" + q), 0] for _ in range(nlanes)]
            self.lane_rr[q] = 0
        li = self.lane_rr[q]
        self.lane_rr[q] = (li + 1) % len(self.lanes[q])
        lane = self.lanes[q][li]
        if lane[1] >= 1800:
            self._wait(q, (lane[0], 16 * lane[1]))
            lane[0] = self.newsem("l" + q)
            lane[1] = 0
        if lane[1] > 0:
            self._wait(q, (lane[0], 16 * lane[1]))
        self._deps_dma(q, reads, writes)
        ins = self.eng[q].dma_start(out=out, in_=in_, **kw)
        lane[1] += 1
        tok = (lane[0], 16 * lane[1])
        ins.then_inc(lane[0], 16)
        key = "dma_" + q + str(li)
        for b in reads:
            b.r[key] = tok
        for b in writes:
            b.w = (key, tok)
            b.r = {}
        return tok

    def _deps_dma(self, q, reads, writes):
        for b in reads:
            if b.w is not None:
                self._wait(q, b.w[1])
        for b in writes:
            if b.w is not None:
                self._wait(q, b.w[1])
            for re_, tok in b.r.items():
                self._wait(q, tok)

    def barrier(self):
        toks = list(self.last_tok.values())
        for q, lanes in self.lanes.items():
            for lane in lanes:
                if lane[1] > 0:
                    toks.append((lane[0], 16 * lane[1]))
        for e in self.eng:
            for tok in toks:
                self._wait(e, tok)

    def wait_all_on(self, e):
        toks = list(self.last_tok.values())
        for q, lanes in self.lanes.items():
            for lane in lanes:
                if lane[1] > 0:
                    toks.append((lane[0], 16 * lane[1]))
        for tok in toks:
            self._wait(e, tok)

    @contextlib.contextmanager
    def scope(self):
        old = self.stack
        self.stack = contextlib.ExitStack()
        try:
            yield
            self.barrier()
        finally:
            self.stack.close()
            self.stack = old

    def close(self):
        self.stack.close()


NEG = -30000.0


def _bucket(dist):
    n = np.maximum(dist, 0)
    ratio = np.log(np.maximum(n, 1).astype(np.float32) / np.float32(16.0)) / np.float32(np.log(8.0))
    large = np.minimum(16 + (ratio * 16).astype(np.int32), 31)
    return np.where(n < 16, n, large)


def host_consts(rel_bias):
    rel = np.asarray(rel_bias, np.float32)
    c = {}
    c["ident"] = np.eye(128, dtype=np.float32).astype(ml_dtypes.bfloat16)
    c["identf"] = np.eye(128, dtype=np.float32)
    kp = np.arange(128)[:, None]
    cc = np.arange(640)[None, :]
    dist = cc - kp
    bt = rel[_bucket(dist)]
    tw = np.where(((dist >= 0) & (dist < 512))[..., None], bt, np.float32(NEG))
    ts = np.where((dist >= 0)[..., None], bt, np.float32(NEG))
    c["tw"] = np.ascontiguousarray(tw.transpose(0, 2, 1)).astype(np.float32)
    c["ts"] = np.ascontiguousarray(ts.transpose(0, 2, 1)).astype(np.float32)
    cidx = np.arange(256)[:, None]
    qidx = np.arange(S)[None, :]
    dc = qidx - 16 * cidx - 31
    bcg = rel[_bucket(dc)]
    ok = (dc >= 0) & (cidx < 255)
    bc = np.where(ok[..., None], bcg, np.float32(NEG))
    c["biasc"] = np.ascontiguousarray(bc.transpose(2, 0, 1)).reshape(8, 2, 128, S).astype(np.float32)
    A = np.zeros((256, 64), np.float32)
    Wt = (1, 2, 2, 2, 1)
    for ci in range(255):
        for j in range(64):
            o = ci + 1 - 4 * j
            if 0 <= o <= 4:
                A[ci, j] = Wt[o]
    c["amat"] = A.reshape(2, 128, 64)
    E = np.zeros((64, S), np.float32)
    E[np.arange(S) // 64, np.arange(S)] = 1.0
    c["emat"] = E.astype(ml_dtypes.bfloat16)
    qp = np.arange(128)[:, None, None]
    qt = np.arange(32)[None, :, None]
    j = np.arange(64)[None, None, :]
    cur = (128 * qt + qp) // 64
    cand = (j >= 1) & (j <= cur - 2)
    c["candneg"] = np.where(cand, 0.0, -1e9).astype(np.float32)
    c["fz"] = ((j == 0) | (j == cur) | (j == cur - 1)).astype(np.float32)
    tri = np.triu(np.ones((64, 64), np.float32))
    c["rwmask"] = np.ascontiguousarray(np.stack([np.triu(np.ones((64, 64), np.float32), 1), tri, np.tril(np.ones((64, 64), np.float32), -1)], axis=1))
    rr = np.ones((64, 1024), np.float32)
    rr[:, ::64] = 0.0
    c["rwreset"] = rr
    c["b31"] = np.ascontiguousarray(np.broadcast_to(rel[31][None, :], (128, 8))).astype(np.float32)
    return c


CONST_SPECS = {
    "ident": ([128, 128], BF16), "identf": ([128, 128], F32),
    "tw": ([128, 8, 640], F32), "ts": ([128, 8, 640], F32),
    "biasc": ([8, 2, 128, S], F32), "amat": ([2, 128, 64], F32),
    "emat": ([64, S], BF16), "candneg": ([128, 32, 64], F32), "fz": ([128, 32, 64], F32),
    "b31": ([128, 8], F32), "rwmask": ([64, 3, 64], F32), "rwreset": ([64, 1024], F32),
}

W_SPECS = {
    "x": [S, D], "attn_norm_g": [1, D], "w_in": [1, D, IN_WIDTH], "q_norm_g": [1, 64], "k_norm_g": [1, 64],
    "cmp_pe_k": [1, 32, 64], "cmp_w1_k": [1, 2048, 256], "cmp_w2_k": [1, 256, 64],
    "cmp_pe_v": [1, 32, 64], "cmp_w1_v": [1, 2048, 256], "cmp_w2_v": [1, 256, 64],
    "rwkv_mu": [1, 1792], "rwkv_w0": [1, 512], "rwkv_w2": [1, 64, 512], "rwkv_a0": [1, 512],
    "rwkv_a2": [1, 64, 512], "rwkv_g2": [1, 128, 512], "rwkv_k_k": [1, 512], "rwkv_k_a": [1, 512],
    "rwkv_r_k": [1, 8, 64], "rwkv_ln_g": [1, 512], "rwkv_ln_b": [1, 512],
    "w_proj_a": [1, 512, D], "w_proj_b": [1, 512, D], "w_out": [1, D, D], "ffn_norm_g": [1, D],
    "w_up": [1, D, 2 * DFF], "conv_w": [1, 3, 2 * DFF], "conv_b": [1, 2 * DFF], "w_down": [1, DFF, D],
}


class Prog:
    def __init__(self, debug=()):
        self.debug = set(debug)
        nc = bass.Bass("TRN2", target_bir_lowering=False)
        self.nc = nc
        self.inp = {}
        for k, shp in W_SPECS.items():
            self.inp[k] = nc.dram_tensor(k, list(shp), F32, kind="ExternalInput").ap()
        for k, (shp, dt) in CONST_SPECS.items():
            self.inp[k] = nc.dram_tensor(k, list(shp), dt, kind="ExternalInput").ap()
        self.out = nc.dram_tensor("out", [S, D], F32, kind="ExternalOutput").ap()
        self.dbg = {}
        self.b = Builder(nc)

    def dbg_out(self, name, shape, dt=F32):
        t = self.nc.dram_tensor("dbg_" + name, list(shape), dt, kind="ExternalOutput").ap()
        self.dbg[name] = t
        return t

    def load_weight(self, dst, src, ncols, gvec=None, kch=8, stage=None, eng="act"):
        b = self.b
        for c in range(kch):
            st = stage[c % len(stage)]
            b.dma("sp", st[:, :ncols], src[c * 128:(c + 1) * 128, :], writes=[st])
            if gvec is not None:
                b.op(eng, lambda e: e.activation(out=dst[:, c, :], in_=st[:, :ncols], func=AF.Copy, scale=gvec[:, c:c + 1])
                     if eng == "act" else e.tensor_scalar_mul(out=dst[:, c, :], in0=st[:, :ncols], scalar1=gvec[:, c:c + 1]),
                     reads=[st, gvec], writes=[dst])
            else:
                b.op(eng, lambda e: e.copy(out=dst[:, c, :], in_=st[:, :ncols]) if eng == "act"
                     else e.tensor_copy(out=dst[:, c, :], in_=st[:, :ncols]), reads=[st], writes=[dst])

    def load_gain(self, name, src_vec, kch=8):
        b = self.b
        g = b.sb(name, [128, kch], F32)
        b.dma("sp", g[:], src_vec.rearrange("(c p) -> p c", p=128), writes=[g], allow_slow_non_contiguous=True)
        return g

    def bcast_row(self, name, src_row, n):
        b = self.b
        t = b.sb(name, [128, n], F32)
        b.dma("sp", t[:], src_row.partition_broadcast(128), writes=[t])
        return t

    def make_hT(self, x_ap, t, xt, junk, ss, hb, pt, hT, ident, hT_ap=None):
        b = self.b
        b.dma("sp", xt[:], x_ap[t * 128:(t + 1) * 128, :], writes=[xt])
        b.op("act", lambda e: e.activation(out=junk[:], in_=xt[:], func=AF.Square, accum_out=ss[:]), reads=[xt], writes=[junk, ss])
        b.op("act", lambda e: e.activation(out=ss[:], in_=ss[:], func=AF.Sqrt, scale=1.0 / D, bias=RMS_EPS), reads=[ss], writes=[ss])
        b.op("dve", lambda e: e.reciprocal(out=ss[:], in_=ss[:]), reads=[ss], writes=[ss])
        b.op("dve", lambda e: e.tensor_scalar_mul(out=hb[:], in0=xt[:], scalar1=ss[:]), reads=[xt, ss], writes=[hb])
        for c in range(8):
            b.op("pe", lambda e: e.transpose(out=pt[:, c, :], in_=hb[:, c * 128:(c + 1) * 128], identity=ident[:]),
                 reads=[hb, ident], writes=[pt])
        b.op("act", lambda e: e.copy(out=(hT[:] if hT_ap is None else hT_ap), in_=pt[:]), reads=[pt], writes=[hT])

    def alloc_root(self):
        b = self.b
        I = self.inp
        self.ident = b.sb("ident", [128, 128], BF16)
        b.dma("sp", self.ident[:], I["ident"], writes=[self.ident])
        self.identf = b.sb("identf", [128, 128], F32)
        b.dma("sp", self.identf[:], I["identf"], writes=[self.identf])

    def alloc_persistent(self):
        b = self.b
        I = self.inp
        if not hasattr(self, "ident"):
            self.alloc_root()
        self.ksE = b.sb("ksE", [128, 2, S], BF16)
        self.kwT = b.sb("kwT", [64, 2, S], BF16)
        self.vaug_s = b.sb("vaug_s", [128, NT, 2, 65], BF16)
        self.vaug_w = b.sb("vaug_w", [128, NT, 2, 65], BF16)
        self.gts = b.sb("gts", [128, NT, 24], F32)
        self.kcT = b.sb("kcT", [64, 2, 256], BF16)
        self.vcA = b.sb("vcA", [128, 2, 2, 130], mybir.dt.float32r)
        self.qT_d = b.dram("qT_d", [8, 64, S], BF16)
        self.oaT_d = b.dram("oaT_d", [4, 128, S], BF16)
        self.obT_d = b.dram("obT_d", [4, 128, S], BF16)
        for g in range(2):
            b.dma("sp", self.ksE[64:128, g, :], I["emat"], writes=[self.ksE])
        b.op("pool", lambda e: e.memset(self.vaug_s[:, :, :, 64:65], 1.0), writes=[self.vaug_s])
        b.op("pool", lambda e: e.memset(self.vaug_w[:, :, :, 64:65], 1.0), writes=[self.vaug_w])
        onesf = b.sb("onesf", [128, 4], F32)
        b.op("pool", lambda e: e.memset(onesf[:], 1.0), writes=[onesf])
        zf = b.sb("zerof", [128, 4 * 130], F32)
        b.op("pool", lambda e: e.memset(zf[:], 0.0), writes=[zf])
        b.op("dve", lambda e: e.tensor_copy(out=self.vcA[:].rearrange("p a c n -> p (a c n)"), in_=zf[:]), reads=[zf], writes=[self.vcA])
        b.op("dve", lambda e: e.tensor_copy(out=self.vcA[:, :, :, 64:65], in_=onesf[:].rearrange("p (a c) -> p a c", a=2).unsqueeze(3)), reads=[onesf], writes=[self.vcA])
        amst = b.sb("amst", [128, 2, 64], F32)
        for g in range(2):
            for ct in range(2):
                b.dma("sp", amst[:, ct, :], I["amat"][ct], writes=[amst])
                b.op("dve", lambda e: e.tensor_copy(out=self.vcA[:, g, ct, 65:129], in_=amst[:, ct, :]), reads=[amst], writes=[self.vcA])

    def phase_nsa_proj(self):
        b = self.b
        I = self.inp
        with b.scope():
            gat = self.load_gain("gat", I["attn_norm_g"][0])
            wn = b.sb("wn", [128, 8, RW0], BF16)
            stage = [b.sb(f"wst{i}", [128, RW0], F32) for i in range(2)]
            self.load_weight(wn, I["w_in"][0][:, 0:RW0], RW0, gvec=gat, stage=stage)
            gq = self.bcast_row("gq", I["q_norm_g"][0], 64)
            gk = self.bcast_row("gk", I["k_norm_g"][0], 64)
            gq_rep = b.sb("gq_rep", [128, 8, 64], F32)
            gk_rep = b.sb("gk_rep", [128, 2, 64], F32)
            b.op("act", lambda e: e.activation(out=gq_rep[:], in_=gq[:, None, :].to_broadcast([128, 8, 64]), func=AF.Copy, scale=0.125),
                 reads=[gq], writes=[gq_rep])
            b.op("act", lambda e: e.activation(out=gk_rep[:], in_=gk[:, None, :].to_broadcast([128, 2, 64]), func=AF.Copy, scale=1.0),
                 reads=[gk], writes=[gk_rep])
            if getattr(self, 'stop_at', 99) <= 0:
                return
            kcdup = b.sb("kcdup", [128, 2, S + 1], BF16)
            vcdup = b.sb("vcdup", [128, 2, S + 1], BF16)
            xt = [b.sb(f"xt{i}", [128, D], F32) for i in range(2)]
            junk = b.sb("junk", [128, D], BF16)
            ss = [b.sb(f"ss{i}", [128, 1], F32) for i in range(2)]
            hb = [b.sb(f"hb{i}", [128, D], BF16) for i in range(2)]
            hT = [b.sb(f"hT{i}", [128, 8, 128], BF16) for i in range(2)]
            sq_ = [b.sb(f"sq{i}", [128, 12, 64], F32) for i in range(2)]
            ssq_ = [b.sb(f"ssq{i}", [128, 12], F32) for i in range(2)]
            tmpq_ = [b.sb(f"tmpq{i}", [128, 8, 64], F32) for i in range(2)]
            tmpk_ = [b.sb(f"tmpk{i}", [128, 4, 64], F32) for i in range(2)]
            qb_ = [b.sb(f"qb{i}", [128, 512], BF16) for i in range(2)]
            kb_ = [b.sb(f"kb{i}", [128, 4, 64], BF16) for i in range(2)]
            cb_ = [b.sb(f"cb{i}", [128, 4, 2, 64], BF16) for i in range(2)]
            qst = [b.sb(f"qst{i}", [64, 8, 128], BF16) for i in range(2)]
            pt = b.ps("pt", [128, 8, 128], BF16)
            pm = [b.ps(f"pm{i}", [128, 512], F32) for i in range(3)]
            ptq_ = [b.ps(f"ptq{i}", [128, 8, 128], BF16) for i in range(2)]
            ptk_ = [b.ps(f"ptk{i}", [128, 8, 128], BF16) for i in range(2)]
            colgroups = [(0, 512), (512, 1024), (1024, RW0)]
            pmS = [[b.sb(f"pmS{i}_{n}", [128, 512], F32) for n in range(3)] for i in range(2)]
            ntl = getattr(self, 'nt_limit', NT)

            def stageA(t):
                    i = t % 2
                    self.make_hT(I["x"], t, xt[i], junk, ss[i], hb[i], pt, hT[i], self.ident)
                    sq, ssq, tmpq, tmpk, qb, kb, cb, ptq, ptk = sq_[i], ssq_[i], tmpq_[i], tmpk_[i], qb_[i], kb_[i], cb_[i], ptq_[i], ptk_[i]
                    for n, (c0, c1) in enumerate(colgroups):
                        for c in range(8):
                            b.op("pe", lambda e: e.matmul(pm[n][:, :c1 - c0], lhsT=hT[i][:, c, :], rhs=wn[:, c, c0:c1],
                                                          start=(c == 0), stop=(c == 7)), reads=[hT[i], wn], writes=[pm[n]])

            def stageA2(t):
                    i = t % 2
                    b.op("act", lambda e: e.copy(out=pmS[i][0][:], in_=pm[0][:]), reads=[pm[0]], writes=[pmS[i][0]])
                    b.op("dve", lambda e: e.tensor_copy(out=pmS[i][1][:], in_=pm[1][:]), reads=[pm[1]], writes=[pmS[i][1]])
                    b.op("act", lambda e: e.copy(out=pmS[i][2][:, 0:RW0 - 1024], in_=pm[2][:, 0:RW0 - 1024]), reads=[pm[2]], writes=[pmS[i][2]])

            def stageB(t):
                    i = t % 2
                    sq, ssq, tmpq, tmpk, qb, kb, cb, ptq, ptk = sq_[i], ssq_[i], tmpq_[i], tmpk_[i], qb_[i], kb_[i], cb_[i], ptq_[i], ptk_[i]
                    b.op("act", lambda e: e.activation(out=sq[:, 0:8, :], in_=pmS[i][0][:, 0:512].rearrange("p (h d) -> p h d", d=64), func=AF.Square),
                         reads=[pmS[i][0]], writes=[sq])
                    b.op("act", lambda e: e.activation(out=sq[:, 8:10, :], in_=pmS[i][1][:, 256:384].rearrange("p (h d) -> p h d", d=64), func=AF.Square),
                         reads=[pmS[i][1]], writes=[sq])
                    b.op("act", lambda e: e.activation(out=sq[:, 10:12, :], in_=pmS[i][2][:, 0:128].rearrange("p (h d) -> p h d", d=64), func=AF.Square),
                         reads=[pmS[i][2]], writes=[sq])
                    b.op("dve", lambda e: e.tensor_reduce(out=ssq[:], in_=sq[:], axis=AX.X, op=ALU.add), reads=[sq], writes=[ssq])
                    b.op("act", lambda e: e.activation(out=ssq[:], in_=ssq[:], func=AF.Sqrt, scale=1.0 / 64, bias=RMS_EPS), reads=[ssq], writes=[ssq])
                    b.op("dve", lambda e: e.reciprocal(out=ssq[:], in_=ssq[:]), reads=[ssq], writes=[ssq])
                    if getattr(self, 'stop_at', 99) <= 2:
                        return
                    b.op("dve", lambda e: e.tensor_tensor(out=tmpq[:], in0=pmS[i][0][:, 0:512].rearrange("p (h d) -> p h d", d=64),
                                                          in1=ssq[:, 0:8].unsqueeze(2).to_broadcast([128, 8, 64]), op=ALU.mult),
                         reads=[pmS[i][0], ssq], writes=[tmpq])
                    b.op("pool", lambda e: e.tensor_tensor(out=qb[:].rearrange("p (h d) -> p h d", d=64), in0=tmpq[:], in1=gq_rep[:], op=ALU.mult),
                         reads=[tmpq, gq_rep], writes=[qb])
                    for h in range(8):
                        b.op("pe", lambda e: e.transpose(out=ptq[0:64, h, :], in_=qb[:, h * 64:(h + 1) * 64], identity=self.ident[:]),
                             reads=[qb, self.ident], writes=[ptq])
                    b.op("act", lambda e: e.copy(out=qst[i][:], in_=ptq[0:64, :, :]), reads=[ptq], writes=[qst[i]])
                    b.dma("pool", self.qT_d[:, :, t * 128:(t + 1) * 128].rearrange("h d t -> d h t"), qst[i][:], reads=[qst[i]], writes=[self.qT_d])
                    if getattr(self, 'stop_at', 99) <= 3:
                        return
                    b.op("dve", lambda e: e.tensor_tensor(out=tmpk[:, 0:2, :], in0=pmS[i][1][:, 256:384].rearrange("p (h d) -> p h d", d=64),
                                                          in1=ssq[:, 8:10].unsqueeze(2).to_broadcast([128, 2, 64]), op=ALU.mult),
                         reads=[pmS[i][1], ssq], writes=[tmpk])
                    b.op("dve", lambda e: e.tensor_tensor(out=tmpk[:, 2:4, :], in0=pmS[i][2][:, 0:128].rearrange("p (h d) -> p h d", d=64),
                                                          in1=ssq[:, 10:12].unsqueeze(2).to_broadcast([128, 2, 64]), op=ALU.mult),
                         reads=[pmS[i][2], ssq], writes=[tmpk])
                    b.op("pool", lambda e: e.tensor_tensor(out=kb[:].rearrange("p (a g) d -> p a g d", a=2), in0=tmpk[:].rearrange("p (a g) d -> p a g d", a=2),
                                                           in1=gk_rep[:, None, :, :].to_broadcast([128, 2, 2, 64]), op=ALU.mult),
                         reads=[tmpk, gk_rep], writes=[kb])
                    for j in range(4):
                        b.op("pe", lambda e: e.transpose(out=ptk[0:64, j, :], in_=kb[:, j, :], identity=self.ident[:]),
                             reads=[kb, self.ident], writes=[ptk])
                    if getattr(self, 'stop_at', 99) <= 4:
                        return
                    for du in range(2):
                        b.op("act", lambda e: e.copy(out=cb[:, :, du, :], in_=pmS[i][1][:, 0:256].rearrange("p (a d) -> p a d", d=64)),
                             reads=[pmS[i][1]], writes=[cb])
                    for j in range(4):
                        b.op("pe", lambda e: e.transpose(out=ptk[:, 4 + j, :], in_=cb[:, j, :, :].rearrange("p a d -> p (a d)"), identity=self.ident[:]),
                             reads=[cb, self.ident], writes=[ptk])
                    c0 = t * 128
                    b.op("dve", lambda e: e.tensor_copy(out=self.ksE[0:64, :, c0:c0 + 128], in_=ptk[0:64, 0:2, :]), reads=[ptk], writes=[self.ksE])
                    b.op("dve", lambda e: e.tensor_copy(out=self.kwT[0:64, :, c0:c0 + 128], in_=ptk[0:64, 2:4, :]), reads=[ptk], writes=[self.kwT])
                    b.op("act", lambda e: e.copy(out=kcdup[0:64, :, 1 + c0:1 + c0 + 128], in_=ptk[0:64, 4:6, :]), reads=[ptk], writes=[kcdup])
                    b.op("act", lambda e: e.copy(out=kcdup[64:128, :, c0:c0 + 128], in_=ptk[64:128, 4:6, :]), reads=[ptk], writes=[kcdup])
                    b.op("dve", lambda e: e.tensor_copy(out=vcdup[0:64, :, 1 + c0:1 + c0 + 128], in_=ptk[0:64, 6:8, :]), reads=[ptk], writes=[vcdup])
                    b.op("dve", lambda e: e.tensor_copy(out=vcdup[64:128, :, c0:c0 + 128], in_=ptk[64:128, 6:8, :]), reads=[ptk], writes=[vcdup])
                    if getattr(self, 'stop_at', 99) <= 5:
                        return
                    b.op("act", lambda e: e.copy(out=self.vaug_s[:, t, :, 0:64], in_=pmS[i][1][:, 384:512].rearrange("p (g d) -> p g d", d=64)),
                         reads=[pmS[i][1]], writes=[self.vaug_s])
                    b.op("act", lambda e: e.copy(out=self.vaug_w[:, t, :, 0:64], in_=pmS[i][2][:, 128:256].rearrange("p (g d) -> p g d", d=64)),
                         reads=[pmS[i][2]], writes=[self.vaug_w])
                    b.op("act", lambda e: e.activation(out=self.gts[:, t, :], in_=pmS[i][2][:, 256:280], func=AF.Sigmoid), reads=[pmS[i][2]], writes=[self.gts])

            stageA(0)
            stageA2(0)
            for t in range(ntl):
                if t + 1 < ntl:
                    stageA(t + 1)
                stageB(t)
                if t + 1 < ntl:
                    stageA2(t + 1)
            if "nsa_proj" in self.debug:
                d = self.dbg_out("ksE", [128, 2, S], BF16)
                b.dma("pool", d, self.ksE[:], reads=[self.ksE])
                d = self.dbg_out("kcdup", [128, 2, S + 1], BF16)
                b.dma("pool", d, kcdup[:], reads=[kcdup])
                d = self.dbg_out("vaug_w", [128, NT, 2, 65], BF16)
                b.dma("pool", d, self.vaug_w[:], reads=[self.vaug_w])
                d = self.dbg_out("gts", [128, NT, 24], F32)
                b.dma("pool", d, self.gts[:], reads=[self.gts])
            if not getattr(self, 'skip_compress', False):
                self.compress(kcdup, vcdup, gk_rep, [pm[0], pm[1]], pm[2], ptk_[0])

    def compress(self, kcdup, vcdup, gk_rep, ph, po, ptc):
        b = self.b
        I = self.inp
        C2 = 2.0 * 0.7978845608028654
        w1 = b.sb("w1", [128, 16, 256], BF16)
        w2 = b.sb("w2", [128, 2, 64], BF16)
        w1st = [b.sb(f"w1st{i}", [128, 256], F32) for i in range(2)]
        peT = b.sb("peT", [128, 16], F32)
        peTb = b.sb("peTb", [128, 16], BF16)
        hTc = b.sb("hTc", [128, 2, 256], BF16)
        pbias = b.sb("pbias", [128, 2], F32)
        xh = b.sb("xh", [128, 255], F32)
        x2 = b.sb("x2", [128, 255], F32)
        sg = b.sb("sg", [128, 255], F32)
        ctmp = b.sb("ctmp", [128, 64], F32)
        csq = b.sb("csq", [128, 64], F32)
        cs1 = b.sb("cs1", [128, 1], F32)
        kcb = b.sb("kcb", [128, 64], BF16)
        b.op("pool", lambda e: e.memset(hTc[:], 0.0), writes=[hTc])
        for kv, (dup, pe_n, w1_n, w2_n) in enumerate([(kcdup, "cmp_pe_k", "cmp_w1_k", "cmp_w2_k"), (vcdup, "cmp_pe_v", "cmp_w1_v", "cmp_w2_v")]):
            self.load_weight(w1, I[w1_n][0], 256, kch=16, stage=w1st, eng="dve")
            self.load_weight(w2, I[w2_n][0], 64, kch=2, stage=w1st, eng="dve")
            for two in range(2):
                b.dma("sp", peT[two * 64:(two + 1) * 64, :], I[pe_n][0].rearrange("(pp two) d -> two d pp", two=2)[two],
                      writes=[peT], allow_slow_non_contiguous=True)
            b.op("dve", lambda e: e.tensor_copy(out=peTb[:], in_=peT[:]), reads=[peT], writes=[peTb])
            for ft in range(2):
                for pp in range(16):
                    b.op("pe", lambda e: e.matmul(po[:, ft:ft + 1], lhsT=w1[:, pp, ft * 128:(ft + 1) * 128], rhs=peTb[:, pp:pp + 1],
                                                  start=(pp == 0), stop=(pp == 15)), reads=[w1, peTb], writes=[po])
            b.op("dve", lambda e: e.tensor_copy(out=pbias[:], in_=po[:, 0:2]), reads=[po], writes=[pbias])
            for g in range(2):
                for ft in range(2):
                    p = ph[ft]
                    for pp in range(16):
                        b.op("pe", lambda e: e.matmul(p[:, 0:255], lhsT=w1[:, pp, ft * 128:(ft + 1) * 128],
                                                      rhs=dup[:, g, 1 + 2 * pp:1 + 2 * pp + 16 * 254 + 1:16],
                                                      start=(pp == 0), stop=(pp == 15)), reads=[w1, dup], writes=[p])
                    b.op("act", lambda e: e.activation(out=xh[:], in_=p[:, 0:255], func=AF.Identity, bias=pbias[:, ft:ft + 1]), reads=[p, pbias], writes=[xh])
                    b.op("dve", lambda e: e.tensor_tensor(out=x2[:], in0=xh[:], in1=xh[:], op=ALU.mult), reads=[xh], writes=[x2])
                    b.op("dve", lambda e: e.tensor_scalar(out=x2[:], in0=x2[:], scalar1=0.044715, scalar2=1.0, op0=ALU.mult, op1=ALU.add), reads=[x2], writes=[x2])
                    b.op("dve", lambda e: e.tensor_tensor(out=x2[:], in0=x2[:], in1=xh[:], op=ALU.mult), reads=[x2, xh], writes=[x2])
                    b.op("act", lambda e: e.activation(out=sg[:], in_=x2[:], func=AF.Sigmoid, scale=C2), reads=[x2], writes=[sg])
                    b.op("dve", lambda e: e.tensor_tensor(out=hTc[:, ft, 0:255], in0=xh[:], in1=sg[:], op=ALU.mult), reads=[xh, sg], writes=[hTc])
                for ct in range(2):
                    for ft in range(2):
                        b.op("pe", lambda e: e.matmul(po[:, 64:128], lhsT=hTc[:, ft, ct * 128:(ct + 1) * 128], rhs=w2[:, ft, :],
                                                      start=(ft == 0), stop=(ft == 1)), reads=[hTc, w2], writes=[po])
                    if kv == 0:
                        b.op("act", lambda e: e.activation(out=csq[:], in_=po[:, 64:128], func=AF.Square, accum_out=cs1[:]), reads=[po], writes=[csq, cs1])
                        b.op("act", lambda e: e.activation(out=cs1[:], in_=cs1[:], func=AF.Sqrt, scale=1.0 / 64, bias=RMS_EPS), reads=[cs1], writes=[cs1])
                        b.op("dve", lambda e: e.reciprocal(out=cs1[:], in_=cs1[:]), reads=[cs1], writes=[cs1])
                        b.op("dve", lambda e: e.tensor_scalar_mul(out=ctmp[:], in0=po[:, 64:128], scalar1=cs1[:]), reads=[po, cs1], writes=[ctmp])
                        b.op("dve", lambda e: e.tensor_tensor(out=kcb[:], in0=ctmp[:], in1=gk_rep[:, 0, :], op=ALU.mult), reads=[ctmp, gk_rep], writes=[kcb])
                        b.op("pe", lambda e: e.transpose(out=ptc[0:64, 0, :], in_=kcb[:], identity=self.ident[:]), reads=[kcb, self.ident], writes=[ptc])
                        b.op("dve", lambda e: e.tensor_copy(out=self.kcT[:, g, ct * 128:(ct + 1) * 128], in_=ptc[0:64, 0, :]), reads=[ptc], writes=[self.kcT])
                    else:
                        b.op("dve", lambda e: e.tensor_copy(out=self.vcA[:, g, ct, 0:64], in_=po[:, 64:128]), reads=[po], writes=[self.vcA])
        if "compress" in self.debug:
            d = self.dbg_out("kcT", [64, 2, 256], BF16)
            b.dma("pool", d, self.kcT[:], reads=[self.kcT])
            d = self.dbg_out("vcA", [128, 2, 2, 130], F32)
            b.dma("pool", d, self.vcA[:].bitcast(F32), reads=[self.vcA])

    def finish(self):
        b = self.b
        b.wait_all_on("pool")
        b.barrier()
        b.close()
        return self.nc


def _phase_attn(self):
    b = self.b
    I = self.inp
    with b.scope():
        tw = b.sb("tw", [128, 8, 640], F32)
        ts = b.sb("ts", [128, 8, 640], F32)
        b.dma("sp", tw[:], I["tw"], writes=[tw])
        b.dma("sp", ts[:], I["ts"], writes=[ts])
        candneg = b.sb("candneg", [128, 32, 64], F32)
        fz = b.sb("fz", [128, 32, 64], F32)
        b.dma("sp", candneg[:], I["candneg"], writes=[candneg])
        b.dma("sp", fz[:], I["fz"], writes=[fz])
        b31 = b.sb("b31", [128, 8], F32)
        b.dma("sp", b31[:], I["b31"], writes=[b31])
        kwp = b.sb("kwp", [128, 2, S], BF16)
        b.op("pool", lambda e: e.memset(kwp[64:128, :, :], 0.0), writes=[kwp])
        b.op("pool", lambda e: e.tensor_copy(out=kwp[0:64, :, :], in_=self.kwT[:]), reads=[self.kwT], writes=[kwp])
        kcp = b.sb("kcp", [128, 2, 256], BF16)
        b.op("pool", lambda e: e.memset(kcp[64:128, :, :], 0.0), writes=[kcp])
        b.op("pool", lambda e: e.tensor_copy(out=kcp[0:64, :, :], in_=self.kcT[:]), reads=[self.kcT], writes=[kcp])
        zer = b.sb("zer", [128, 512], BF16)
        b.op("pool", lambda e: e.memset(zer[:], 0.0), writes=[zer])
        qm = [b.sb(f"qm{i}", [128, 8, 512], BF16) for i in range(2)]
        bct = [b.sb(f"bct{i}", [128, 512], F32) for i in range(3)]
        scf = [b.sb(f"scf{i}", [128, 640], F32) for i in range(2)]
        pcT = [b.sb(f"pcT{i}", [128, 2, 512], F32) for i in range(2)]
        pT = [b.sb(f"pT{i}", [128, 640], BF16) for i in range(3)]
        oacc = b.sb("oacc", [128, 4, 512], F32)
        imp = b.sb("imp", [128, 4, 2, 64], F32)
        impm = b.sb("impm", [128, 64], F32)
        impm2 = b.sb("impm2", [128, 64], F32)
        m8a = b.sb("m8a", [128, 8], F32)
        m8b = b.sb("m8b", [128, 8], F32)
        msk = b.sb("msk", [128, 64], F32)
        mb = b.sb("mb", [128, 128], BF16)
        b.op("pool", lambda e: e.memset(mb[:], 0.0), writes=[mb])
        rs = b.sb("rs", [128, 4], F32)
        rg = b.sb("rg", [128, 4], F32)
        oab = b.sb("oab", [128, 512], BF16)
        oaT = [b.sb(f"oaT{i}", [128, 4, 128], BF16) for i in range(2)]
        pS = [b.ps(f"pS{i}", [128, 512], F32) for i in range(2)]
        pS2 = b.ps("pS2", [128, 512], F32)
        pO = [b.ps(f"pO{i}", [128, 512], F32) for i in range(3)]
        pTr = b.ps("pTr", [128, 8, 128], BF16)
        nrot = {"bct": 0, "scf": 0, "pT": 0, "pS": 0}

        def rot(name, lst):
            nrot[name] += 1
            return lst[nrot[name] % len(lst)]

        def finalize(po, ncol_off, h, qs, branch, first):
            qt = qs_base + qs
            o0 = ncol_off
            b.op("dve", lambda e: e.tensor_scalar_max(out=rs[:, 0:1], in0=po[:, o0 + 64:o0 + 65], scalar1=1e-30), reads=[po], writes=[rs])
            b.op("dve", lambda e: e.reciprocal(out=rs[:, 1:2], in_=rs[:, 0:1]), reads=[rs], writes=[rs])
            b.op("dve", lambda e: e.tensor_tensor(out=rg[:, 0:1], in0=rs[:, 1:2], in1=self.gts[:, qt, h * 3 + branch:h * 3 + branch + 1], op=ALU.mult),
                 reads=[rs, self.gts], writes=[rg])
            if first:
                b.op("dve", lambda e: e.tensor_scalar_mul(out=oacc[:, qs, h * 64:(h + 1) * 64], in0=po[:, o0:o0 + 64], scalar1=rg[:, 0:1]),
                     reads=[po, rg], writes=[oacc])
            else:
                b.op("dve", lambda e: e.scalar_tensor_tensor(out=oacc[:, qs, h * 64:(h + 1) * 64], in0=po[:, o0:o0 + 64], scalar=rg[:, 0:1],
                                                             in1=oacc[:, qs, h * 64:(h + 1) * 64], op0=ALU.mult, op1=ALU.add),
                     reads=[po, rg, oacc], writes=[oacc])

        nqg = getattr(self, "nqg_limit", 8)
        for qg in range(nqg):
            qs_base = 4 * qg
            q0 = 512 * qg
            Q = qm[qg % 2]
            b.dma("sp", Q[0:64, :, :], self.qT_d[:, :, q0:q0 + 512].rearrange("h d t -> d h t"), reads=[self.qT_d], writes=[Q])
            if qg < 2:
                b.op("pool", lambda e: e.memset(Q[64:128, :, :], 0.0), writes=[Q])
            for h in range(8):
                g = h // 4
                pc = pcT[h % 2]
                for ct in range(2):
                    p = rot("pS", pS)
                    b.op("pe", lambda e: e.matmul(p[:, :], lhsT=kcp[:, g, ct * 128:(ct + 1) * 128], rhs=Q[:, h, :], start=True, stop=True),
                         reads=[kcp, Q], writes=[p])
                    bt = rot("bct", bct)
                    b.dma("sp", bt[:], I["biasc"][h, ct, :, q0:q0 + 512], writes=[bt])
                    sc = rot("scf", scf)
                    b.op("dve", lambda e: e.tensor_tensor(out=sc[:, 0:512], in0=p[:, :], in1=bt[:], op=ALU.add), reads=[p, bt], writes=[sc])
                    b.op("act", lambda e: e.activation(out=pc[:, ct, :], in_=sc[:, 0:512], func=AF.Exp), reads=[sc], writes=[pc])
                po = pO[0]
                for qs in range(4):
                    for ct in range(2):
                        b.op("pe", lambda e: e.matmul(po[:, qs * 128:qs * 128 + 129] if False else po[:, 0:129], lhsT=pc[:, ct, qs * 128:(qs + 1) * 128],
                                                      rhs=self.vcA[:, g, ct, :], start=(ct == 0), stop=(ct == 1)), reads=[pc, self.vcA], writes=[po])
                    finalize(po, 0, h, qs, 0, True)
                    if h % 4 == 0:
                        b.op("dve", lambda e: e.tensor_scalar_mul(out=imp[:, qs, g, :], in0=po[:, 65:129], scalar1=rs[:, 1:2]), reads=[po, rs], writes=[imp])
                    else:
                        b.op("dve", lambda e: e.scalar_tensor_tensor(out=imp[:, qs, g, :], in0=po[:, 65:129], scalar=rs[:, 1:2], in1=imp[:, qs, g, :],
                                                                     op0=ALU.mult, op1=ALU.add), reads=[po, rs, imp], writes=[imp])
            if qg >= 2:
                for qs in range(4):
                    qt = qs_base + qs
                    for g in range(2):
                        b.op("dve", lambda e: e.tensor_tensor(out=impm[:], in0=imp[:, qs, g, :], in1=candneg[:, qt, :], op=ALU.add), reads=[imp, candneg], writes=[impm])
                        b.op("dve", lambda e: e.max(out=m8a[:], in_=impm[:]), reads=[impm], writes=[m8a])
                        b.op("dve", lambda e: e.match_replace(out=impm2[:], in_to_replace=m8a[:], in_values=impm[:], imm_value=-1e9), reads=[m8a, impm], writes=[impm2])
                        b.op("dve", lambda e: e.max(out=m8b[:], in_=impm2[:]), reads=[impm2], writes=[m8b])
                        b.op("dve", lambda e: e.tensor_scalar(out=msk[:], in0=impm[:], scalar1=m8b[:, 4:5], scalar2=None, op0=ALU.is_ge), reads=[impm, m8b], writes=[msk])
                        b.op("dve", lambda e: e.tensor_tensor(out=msk[:], in0=msk[:], in1=fz[:, qt, :], op=ALU.max), reads=[msk, fz], writes=[msk])
                        b.op("dve", lambda e: e.tensor_scalar(out=mb[:, 64:128], in0=msk[:], scalar1=-NEG, scalar2=NEG, op0=ALU.mult, op1=ALU.add), reads=[msk], writes=[mb])
                        b.op("pe", lambda e: e.transpose(out=pTr[:, 0, :], in_=mb[:], identity=self.ident[:]), reads=[mb, self.ident], writes=[pTr])
                        b.op("act", lambda e: e.copy(out=Q[64:128, 4 * g:4 * g + 4, qs * 128:(qs + 1) * 128],
                                                     in_=pTr[64:128, 0:1, :].to_broadcast([64, 4, 128])), reads=[pTr], writes=[Q])
            for h in range(8):
                g = h // 4
                po_s, po_w = pO[1], pO[2]
                for po in (po_s, po_w):
                    b.op("pe", lambda e: e.matmul(po[:, 0:260], lhsT=zer[:, 0:128], rhs=zer[:, 0:260], start=True, stop=True), reads=[zer], writes=[po])
                nkt = 4 * (qg + 1)
                for kt in range(nkt):
                    dlt = 4 * qg - kt
                    qstart = 0 if dlt >= 0 else -dlt * 128
                    N = 512 - qstart
                    p = rot("pS", pS)
                    b.op("pe", lambda e: e.matmul(p[:, 0:N], lhsT=self.ksE[:, g, kt * 128:(kt + 1) * 128], rhs=Q[:, h, qstart:512], start=True, stop=True),
                         reads=[self.ksE, Q], writes=[p])
                    pt_ = rot("pT", pT)
                    if dlt <= 1:
                        c0 = 128 if dlt == 1 else 0
                        sc = rot("scf", scf)
                        b.op("dve", lambda e: e.tensor_tensor(out=sc[:, 0:N], in0=p[:, 0:N], in1=ts[:, h, c0:c0 + N], op=ALU.add), reads=[p, ts], writes=[sc])
                        b.op("act", lambda e: e.activation(out=pt_[:, 0:N], in_=sc[:, 0:N], func=AF.Exp), reads=[sc], writes=[pt_])
                    else:
                        b.op("act", lambda e: e.activation(out=pt_[:, 0:N], in_=p[:, 0:N], func=AF.Exp, bias=b31[:, h:h + 1]), reads=[p, b31], writes=[pt_])
                    for qs in range(qstart // 128, 4):
                        o = qs * 128 - qstart
                        b.op("pe", lambda e: e.matmul(po_s[:, qs * 65:(qs + 1) * 65], lhsT=pt_[:, o:o + 128], rhs=self.vaug_s[:, kt, g, :],
                                                      start=False, stop=(kt == nkt - 1), skip_group_check=True), reads=[pt_, self.vaug_s], writes=[po_s])
                kts = [kt for kt in range(4 * qg - 4, 4 * qg + 4) if kt >= 0]
                for kt in kts:
                    qs_lo = max(0, kt - 4 * qg)
                    qs_hi = min(3, kt + 4 - 4 * qg)
                    N = (qs_hi - qs_lo + 1) * 128
                    c0 = 128 * (4 * qg + qs_lo - kt)
                    p = rot("pS", pS)
                    b.op("pe", lambda e: e.matmul(p[:, 0:N], lhsT=kwp[:, g, kt * 128:(kt + 1) * 128], rhs=Q[:, h, qs_lo * 128:(qs_hi + 1) * 128], start=True, stop=True),
                         reads=[kwp, Q], writes=[p])
                    sc = rot("scf", scf)
                    b.op("dve", lambda e: e.tensor_tensor(out=sc[:, 0:N], in0=p[:, 0:N], in1=tw[:, h, c0:c0 + N], op=ALU.add), reads=[p, tw], writes=[sc])
                    pt_ = rot("pT", pT)
                    b.op("act", lambda e: e.activation(out=pt_[:, 0:N], in_=sc[:, 0:N], func=AF.Exp), reads=[sc], writes=[pt_])
                    for qs in range(qs_lo, qs_hi + 1):
                        o = (qs - qs_lo) * 128
                        b.op("pe", lambda e: e.matmul(po_w[:, qs * 65:(qs + 1) * 65], lhsT=pt_[:, o:o + 128], rhs=self.vaug_w[:, kt, g, :],
                                                      start=False, stop=(kt == kts[-1]), skip_group_check=True), reads=[pt_, self.vaug_w], writes=[po_w])
                for qs in range(4):
                    finalize(po_s, qs * 65, h, qs, 1, False)
                    finalize(po_w, qs * 65, h, qs, 2, False)
            for qs in range(4):
                qt = qs_base + qs
                ot = oaT[qs % 2]
                b.op("act", lambda e: e.copy(out=oab[:], in_=oacc[:, qs, :]), reads=[oacc], writes=[oab])
                for c in range(4):
                    b.op("pe", lambda e: e.transpose(out=pTr[:, 4 + c, :], in_=oab[:, c * 128:(c + 1) * 128], identity=self.ident[:]), reads=[oab, self.ident], writes=[pTr])
                b.op("act", lambda e: e.copy(out=ot[:], in_=pTr[:, 4:8, :]), reads=[pTr], writes=[ot])
                b.dma("pool", self.oaT_d[:, :, qt * 128:(qt + 1) * 128].rearrange("c p t -> p c t"), ot[:], reads=[ot], writes=[self.oaT_d])
        if "attn" in self.debug:
            d = self.dbg_out("oaT", [4, 128, S], BF16)
            b.dma("pool", d, self.oaT_d[:], reads=[self.oaT_d])


Prog.phase_attn = _phase_attn


def _phase_attn2(self):
    b = self.b
    I = self.inp
    with b.scope():
        tw = b.sb("tw", [128, 8, 640], F32)
        ts = b.sb("ts", [128, 8, 640], F32)
        b.dma("sp", tw[:], I["tw"], writes=[tw])
        b.dma("sp", ts[:], I["ts"], writes=[ts])
        candneg = b.sb("candneg", [128, 32, 64], F32)
        fz = b.sb("fz", [128, 32, 64], F32)
        b.dma("sp", candneg[:], I["candneg"], writes=[candneg])
        b.dma("sp", fz[:], I["fz"], writes=[fz])
        b31 = b.sb("b31", [128, 8], F32)
        b.dma("sp", b31[:], I["b31"], writes=[b31])
        kwp = b.sb("kwp", [128, 2, S], BF16)
        b.op("pool", lambda e: e.memset(kwp[64:128, :, :], 0.0), writes=[kwp])
        b.op("pool", lambda e: e.tensor_copy(out=kwp[0:64, :, :], in_=self.kwT[:]), reads=[self.kwT], writes=[kwp])
        kcp = b.sb("kcp", [128, 2, 256], BF16)
        b.op("pool", lambda e: e.memset(kcp[64:128, :, :], 0.0), writes=[kcp])
        b.op("pool", lambda e: e.tensor_copy(out=kcp[0:64, :, :], in_=self.kcT[:]), reads=[self.kcT], writes=[kcp])
        zer = b.sb("zer", [128, 512], BF16)
        b.op("pool", lambda e: e.memset(zer[:], 0.0), writes=[zer])
        qm = [b.sb(f"qm{i}", [128, 8, 512], BF16) for i in range(2)]
        bct = [b.sb(f"bct{i}", [128, 512], F32) for i in range(3)]
        scf = [b.sb(f"scf{i}", [128, 640], F32) for i in range(3)]
        pcT = [b.sb(f"pcT{i}", [128, 2, 512], mybir.dt.float32r) for i in range(2)]
        pT = [b.sb(f"pT{i}", [128, 640], BF16) for i in range(4)]
        oacc = b.sb("oacc", [128, 4, 512], F32)
        imp = b.sb("imp", [128, 4, 2, 64], F32)
        impm = b.sb("impm", [128, 64], F32)
        impm2 = b.sb("impm2", [128, 64], F32)
        m8a = b.sb("m8a", [128, 8], F32)
        m8b = b.sb("m8b", [128, 8], F32)
        msk = b.sb("msk", [128, 64], F32)
        mb = b.sb("mb", [128, 128], BF16)
        impm8 = b.sb("impm8", [128, 8, 64], F32)
        impm28 = b.sb("impm28", [128, 8, 64], F32)
        m8a8 = b.sb("m8a8", [128, 8, 8], F32)
        m8b8 = b.sb("m8b8", [128, 8, 8], F32)
        msk8 = b.sb("msk8", [128, 8, 64], F32)
        mb8 = b.sb("mb8", [128, 8, 128], BF16)
        b.op("pool", lambda e: e.memset(mb8[:], 0.0), writes=[mb8])
        b.op("pool", lambda e: e.memset(mb[:], 0.0), writes=[mb])
        rs = b.sb("rs", [128, 4], F32)
        rg = b.sb("rg", [128, 4], F32)
        oab = b.sb("oab", [128, 512], BF16)
        oaT = [b.sb(f"oaT{i}", [128, 4, 128], BF16) for i in range(2)]
        pS = [b.ps(f"pS{i}", [128, 512], F32) for i in range(3)]
        pOs = [b.ps(f"pOs{i}", [128, 512], F32) for i in range(2)]
        pOw = [b.ps(f"pOw{i}", [128, 512], F32) for i in range(2)]
        pTr = b.ps("pTr", [128, 8, 128], BF16)
        nrot = {"bct": 0, "scf": 0, "pT": 0, "pS": 0}

        def rot(name, lst):
            nrot[name] += 1
            return lst[nrot[name] % len(lst)]

        def finalize(po, ncol_off, h, qs, branch, first):
            qt = qs_base + qs
            o0 = ncol_off
            b.op("dve", lambda e: e.tensor_scalar_max(out=rs[:, 0:1], in0=po[:, o0 + 64:o0 + 65], scalar1=1e-30), reads=[po], writes=[rs])
            b.op("dve", lambda e: e.reciprocal(out=rs[:, 1:2], in_=rs[:, 0:1]), reads=[rs], writes=[rs])
            b.op("dve", lambda e: e.tensor_tensor(out=rg[:, 0:1], in0=rs[:, 1:2], in1=self.gts[:, qt, h * 3 + branch:h * 3 + branch + 1], op=ALU.mult),
                 reads=[rs, self.gts], writes=[rg])
            if first:
                b.op("dve", lambda e: e.tensor_scalar_mul(out=oacc[:, qs, h * 64:(h + 1) * 64], in0=po[:, o0:o0 + 64], scalar1=rg[:, 0:1]),
                     reads=[po, rg], writes=[oacc])
            else:
                b.op("dve", lambda e: e.scalar_tensor_tensor(out=oacc[:, qs, h * 64:(h + 1) * 64], in0=po[:, o0:o0 + 64], scalar=rg[:, 0:1],
                                                             in1=oacc[:, qs, h * 64:(h + 1) * 64], op0=ALU.mult, op1=ALU.add),
                     reads=[po, rg, oacc], writes=[oacc])

        nqg = getattr(self, "nqg_limit", 8)
        for qg in range(nqg):
            qs_base = 4 * qg
            q0 = 512 * qg
            Q = qm[qg % 2]
            b.dma("sp", Q[0:64, :, :], self.qT_d[:, :, q0:q0 + 512].rearrange("h d t -> d h t"), reads=[self.qT_d], writes=[Q])
            if qg < 2:
                b.op("pool", lambda e: e.memset(Q[64:128, :, :], 0.0), writes=[Q])
            def cmpS(h):
                g = h // 4
                pc = pcT[h % 2]
                for ct in range(2):
                    p = rot("pS", pS)
                    b.op("pe", lambda e: e.matmul(p[:, :], lhsT=kcp[:, g, ct * 128:(ct + 1) * 128], rhs=Q[:, h, :], start=True, stop=True),
                         reads=[kcp, Q], writes=[p])
                    bt = rot("bct", bct)
                    b.dma("sp", bt[:], I["biasc"][h, ct, :, q0:q0 + 512], writes=[bt])
                    sc = rot("scf", scf)
                    b.op("dve", lambda e: e.tensor_tensor(out=sc[:, 0:512], in0=p[:, :], in1=bt[:], op=ALU.add), reads=[p, bt], writes=[sc])
                    b.op("act", lambda e: e.activation(out=pc[:, ct, :], in_=sc[:, 0:512], func=AF.Exp), reads=[sc], writes=[pc])

            def cmpPV(h):
                g = h // 4
                pc = pcT[h % 2]
                for qs in range(4):
                    po = [pOs[0], pOs[1], pOw[0], pOw[1]][qs]
                    for ct in range(2):
                        b.op("pe", lambda e: e.matmul(po[:, 0:130], lhsT=pc[:, ct, qs * 128:(qs + 1) * 128],
                                                      rhs=self.vcA[:, g, ct, :], start=(ct == 0), stop=(ct == 1)), reads=[pc, self.vcA], writes=[po])
                    finalize(po, 0, h, qs, 0, True)
                    if h % 4 == 0:
                        b.op("dve", lambda e: e.tensor_scalar_mul(out=imp[:, qs, g, :], in0=po[:, 65:129], scalar1=rs[:, 1:2]), reads=[po, rs], writes=[imp])
                    else:
                        b.op("dve", lambda e: e.scalar_tensor_tensor(out=imp[:, qs, g, :], in0=po[:, 65:129], scalar=rs[:, 1:2], in1=imp[:, qs, g, :],
                                                                     op0=ALU.mult, op1=ALU.add), reads=[po, rs, imp], writes=[imp])

            cmpS(0)
            for h in range(8):
                if h + 1 < 8:
                    cmpS(h + 1)
                cmpPV(h)
            if qg >= 2:
                I8 = imp[:].rearrange("p q g j -> p (q g) j")
                cn8 = candneg[:, qs_base:qs_base + 4, :].unsqueeze(2).to_broadcast([128, 4, 2, 64])
                fz8 = fz[:, qs_base:qs_base + 4, :].unsqueeze(2).to_broadcast([128, 4, 2, 64])
                b.op("dve", lambda e: e.tensor_tensor(out=impm8[:].rearrange("p (q g) j -> p q g j", g=2), in0=imp[:], in1=cn8, op=ALU.add), reads=[imp, candneg], writes=[impm8])
                for k in range(8):
                    b.op("dve", lambda e: e.max(out=m8a8[:, k, :], in_=impm8[:, k, :]), reads=[impm8], writes=[m8a8])
                for k in range(8):
                    b.op("dve", lambda e: e.match_replace(out=impm28[:, k, :], in_to_replace=m8a8[:, k, :], in_values=impm8[:, k, :], imm_value=-1e9), reads=[m8a8, impm8], writes=[impm28])
                for k in range(8):
                    b.op("dve", lambda e: e.max(out=m8b8[:, k, :], in_=impm28[:, k, :]), reads=[impm28], writes=[m8b8])
                b.op("dve", lambda e: e.tensor_tensor(out=msk8[:], in0=impm8[:], in1=m8b8[:, :, 4:5].to_broadcast([128, 8, 64]), op=ALU.is_ge), reads=[impm8, m8b8], writes=[msk8])
                b.op("dve", lambda e: e.tensor_tensor(out=msk8[:].rearrange("p (q g) j -> p q g j", g=2), in0=msk8[:].rearrange("p (q g) j -> p q g j", g=2), in1=fz8, op=ALU.max),
                     reads=[msk8, fz], writes=[msk8])
                b.op("dve", lambda e: e.tensor_scalar(out=mb8[:, :, 64:128], in0=msk8[:], scalar1=-NEG, scalar2=NEG, op0=ALU.mult, op1=ALU.add), reads=[msk8], writes=[mb8])
                for k in range(8):
                    b.op("pe", lambda e: e.transpose(out=pTr[:, k, :], in_=mb8[:, k, :], identity=self.ident[:]), reads=[mb8, self.ident], writes=[pTr])
                for g in range(2):
                    src = pTr[64:128, :, :].rearrange("p (q g) t -> p q g t", g=2)[:, :, g, :]
                    b.op("act", lambda e: e.copy(out=Q[64:128, 4 * g:4 * g + 4, :].rearrange("p r (q t) -> p r q t", t=128),
                                                 in_=src.unsqueeze(1).to_broadcast([64, 4, 4, 128])), reads=[pTr], writes=[Q])
            jobs = []
            for h in range(8):
                g = h // 4
                nkt = 4 * (qg + 1)
                for kt in range(nkt):
                    dlt = 4 * qg - kt
                    qstart = 0 if dlt >= 0 else -dlt * 128
                    jobs.append(dict(kind="s", h=h, g=g, kt=kt, qlo=qstart // 128, qhi=3, first=(kt == 0), last=False, lastkt=(kt == nkt - 1),
                                     tab=(ts, (128 if dlt == 1 else 0)) if dlt <= 1 else None))
                kts = [kt for kt in range(4 * qg - 4, 4 * qg + 4) if kt >= 0]
                for kt in kts:
                    qs_lo = max(0, kt - 4 * qg)
                    qs_hi = min(3, kt + 4 - 4 * qg)
                    jobs.append(dict(kind="w", h=h, g=g, kt=kt, qlo=qs_lo, qhi=qs_hi, first=False, last=(kt == kts[-1]), lastkt=(kt == kts[-1]),
                                     tab=(tw, 128 * (4 * qg + qs_lo - kt))))

            def emitS(j):
                h, g, kt = j["h"], j["g"], j["kt"]
                N = (j["qhi"] - j["qlo"] + 1) * 128
                p = rot("pS", pS)
                kmat = self.ksE if j["kind"] == "s" else kwp
                b.op("pe", lambda e: e.matmul(p[:, 0:N], lhsT=kmat[:, g, kt * 128:(kt + 1) * 128], rhs=Q[:, h, j["qlo"] * 128:(j["qhi"] + 1) * 128], start=True, stop=True),
                     reads=[kmat, Q], writes=[p])
                j["p"] = p
                j["N"] = N

            def emitE(j):
                h = j["h"]
                p, N = j["p"], j["N"]
                pt_ = rot("pT", pT)
                if j["tab"] is not None:
                    tab, c0 = j["tab"]
                    sc = rot("scf", scf)
                    b.op("dve", lambda e: e.tensor_tensor(out=sc[:, 0:N], in0=p[:, 0:N], in1=tab[:, h, c0:c0 + N], op=ALU.add), reads=[p, tab], writes=[sc])
                    b.op("act", lambda e: e.activation(out=pt_[:, 0:N], in_=sc[:, 0:N], func=AF.Exp), reads=[sc], writes=[pt_])
                else:
                    b.op("act", lambda e: e.activation(out=pt_[:, 0:N], in_=p[:, 0:N], func=AF.Exp, bias=b31[:, h:h + 1]), reads=[p, b31], writes=[pt_])
                j["pt"] = pt_

            def emitPV(j):
                h, g, kt = j["h"], j["g"], j["kt"]
                po_s, po_w = pOs[h % 2], pOw[h % 2]
                if j["first"]:
                    for po in (po_s, po_w):
                        b.op("pe", lambda e: e.matmul(po[:, 0:260], lhsT=zer[:, 0:128], rhs=zer[:, 0:260], start=True, stop=True), reads=[zer], writes=[po])
                po = po_s if j["kind"] == "s" else po_w
                va = self.vaug_s if j["kind"] == "s" else self.vaug_w
                for qs in range(j["qlo"], j["qhi"] + 1):
                    o = (qs - j["qlo"]) * 128
                    b.op("pe", lambda e: e.matmul(po[:, qs * 65:(qs + 1) * 65], lhsT=j["pt"][:, o:o + 128], rhs=va[:, kt, g, :],
                                                  start=False, stop=j["lastkt"], skip_group_check=True), reads=[j["pt"], va], writes=[po])
                if j["last"]:
                    for qs in range(4):
                        finalize(po_s, qs * 65, h, qs, 1, False)
                        finalize(po_w, qs * 65, h, qs, 2, False)

            LA = 2
            for i_ in range(len(jobs) + LA):
                if i_ < len(jobs):
                    emitS(jobs[i_])
                if i_ >= LA:
                    emitE(jobs[i_ - LA])
                    emitPV(jobs[i_ - LA])
            for qs in range(4):
                qt = qs_base + qs
                ot = oaT[qs % 2]
                b.op("act", lambda e: e.copy(out=oab[:], in_=oacc[:, qs, :]), reads=[oacc], writes=[oab])
                for c in range(4):
                    b.op("pe", lambda e: e.transpose(out=pTr[:, 4 + c, :], in_=oab[:, c * 128:(c + 1) * 128], identity=self.ident[:]), reads=[oab, self.ident], writes=[pTr])
                b.op("act", lambda e: e.copy(out=ot[:], in_=pTr[:, 4:8, :]), reads=[pTr], writes=[ot])
                b.dma("pool", self.oaT_d[:, :, qt * 128:(qt + 1) * 128].rearrange("c p t -> p c t"), ot[:], reads=[ot], writes=[self.oaT_d])
        if "attn" in self.debug:
            d = self.dbg_out("oaT", [4, 128, S], BF16)
            b.dma("pool", d, self.oaT_d[:], reads=[self.oaT_d])


Prog.phase_attn2 = _phase_attn2


def _phase_merge(self):
    b = self.b
    I = self.inp
    self.x1_d = b.dram("x1_d", [S, D], F32)
    with b.scope():
        gat = self.load_gain("gat2", I["attn_norm_g"][0])
        stage = [b.sb(f"mst{i}", [128, 1024], F32) for i in range(2)]
        wg = b.sb("wg", [128, 8, 2048], BF16)
        for n in range(2):
            for c in range(8):
                st = stage[c % 2]
                b.dma("sp", st[:], I["w_in"][0][c * 128:(c + 1) * 128, GA0 + n * 1024:GA0 + (n + 1) * 1024], writes=[st])
                b.op("act", lambda e: e.activation(out=wg[:, c, n * 1024:(n + 1) * 1024], in_=st[:], func=AF.Copy, scale=gat[:, c:c + 1]),
                     reads=[st, gat], writes=[wg])
        wa = b.sb("wa", [128, 4, 1024], BF16)
        wb = b.sb("wb", [128, 4, 1024], BF16)
        wo = b.sb("wo", [128, 8, 1024], BF16)
        self.load_weight(wa, I["w_proj_a"][0], 1024, kch=4, stage=stage, eng="dve")
        self.load_weight(wb, I["w_proj_b"][0], 1024, kch=4, stage=stage, eng="dve")
        self.load_weight(wo, I["w_out"][0], 1024, kch=8, stage=stage, eng="dve")
        xt = [b.sb(f"mxt{i}", [128, D], F32) for i in range(2)]
        junk = b.sb("mjunk", [128, D], BF16)
        ss = [b.sb(f"mss{i}", [128, 1], F32) for i in range(2)]
        hb = [b.sb(f"mhb{i}", [128, D], BF16) for i in range(2)]
        hT = [b.sb(f"mhT{i}", [128, 8, 128], BF16) for i in range(2)]
        oat = [b.sb(f"oat{i}", [128, 4, 128], BF16) for i in range(2)]
        obt = [b.sb(f"obt{i}", [128, 4, 128], BF16) for i in range(2)]
        sg = b.sb("msg", [128, 2048], F32)
        m1 = b.sb("m1", [128, 1024], F32)
        m2 = b.sb("m2", [128, 1024], F32)
        mgb = b.sb("mgb", [128, 1024], BF16)
        mT = b.sb("mT", [128, 8, 128], BF16)
        x1t = [b.sb(f"x1t{i}", [128, D], F32) for i in range(2)]
        pt = b.ps("mpt", [128, 8, 128], BF16)
        pg = [b.ps(f"mpg{i}", [128, 512], F32) for i in range(2)]
        pa = [b.ps(f"mpa{i}", [128, 512], F32) for i in range(2)]
        pb = [b.ps(f"mpb{i}", [128, 512], F32) for i in range(2)]
        for t in range(getattr(self, "nt_limit", NT)):
            i = t % 2
            self.make_hT(I["x"], t, xt[i], junk, ss[i], hb[i], pt, hT[i], self.ident)
            b.dma("sp", oat[i][:], self.oaT_d[:, :, t * 128:(t + 1) * 128].rearrange("c p t -> p c t"), reads=[self.oaT_d], writes=[oat[i]])
            b.dma("sp", obt[i][:], self.obT_d[:, :, t * 128:(t + 1) * 128].rearrange("c p t -> p c t"), reads=[self.obT_d], writes=[obt[i]])
            for n in range(4):
                p = pg[n % 2]
                for c in range(8):
                    b.op("pe", lambda e: e.matmul(p[:, :], lhsT=hT[i][:, c, :], rhs=wg[:, c, n * 512:(n + 1) * 512], start=(c == 0), stop=(c == 7)),
                         reads=[hT[i], wg], writes=[p])
                b.op("act", lambda e: e.activation(out=sg[:, n * 512:(n + 1) * 512], in_=p[:, :], func=AF.Sigmoid), reads=[p], writes=[sg])
            for n in range(2):
                for c in range(4):
                    b.op("pe", lambda e: e.matmul(pa[n][:, :], lhsT=oat[i][:, c, :], rhs=wa[:, c, n * 512:(n + 1) * 512], start=(c == 0), stop=(c == 3)),
                         reads=[oat[i], wa], writes=[pa[n]])
                for c in range(4):
                    b.op("pe", lambda e: e.matmul(pb[n][:, :], lhsT=obt[i][:, c, :], rhs=wb[:, c, n * 512:(n + 1) * 512], start=(c == 0), stop=(c == 3)),
                         reads=[obt[i], wb], writes=[pb[n]])
                b.op("dve", lambda e: e.tensor_tensor(out=m1[:, n * 512:(n + 1) * 512], in0=pa[n][:, :], in1=sg[:, n * 512:(n + 1) * 512], op=ALU.mult),
                     reads=[pa[n], sg], writes=[m1])
                b.op("dve", lambda e: e.tensor_tensor(out=m2[:, n * 512:(n + 1) * 512], in0=pb[n][:, :], in1=sg[:, 1024 + n * 512:1024 + (n + 1) * 512], op=ALU.mult),
                     reads=[pb[n], sg], writes=[m2])
            b.op("pool", lambda e: e.tensor_tensor(out=mgb[:], in0=m1[:], in1=m2[:], op=ALU.add), reads=[m1, m2], writes=[mgb])
            for c in range(8):
                b.op("pe", lambda e: e.transpose(out=pt[:, c, :], in_=mgb[:, c * 128:(c + 1) * 128], identity=self.ident[:]), reads=[mgb, self.ident], writes=[pt])
            b.op("act", lambda e: e.copy(out=mT[:], in_=pt[:]), reads=[pt], writes=[mT])
            for n in range(2):
                for c in range(8):
                    b.op("pe", lambda e: e.matmul(pa[n][:, :], lhsT=mT[:, c, :], rhs=wo[:, c, n * 512:(n + 1) * 512], start=(c == 0), stop=(c == 7)),
                         reads=[mT, wo], writes=[pa[n]])
                b.op("dve", lambda e: e.tensor_tensor(out=x1t[i][:, n * 512:(n + 1) * 512], in0=pa[n][:, :], in1=xt[i][:, n * 512:(n + 1) * 512], op=ALU.add),
                     reads=[pa[n], xt[i]], writes=[x1t[i]])
            b.dma("pool", self.x1_d[t * 128:(t + 1) * 128, :], x1t[i][:], reads=[x1t[i]], writes=[self.x1_d])
        if "merge" in self.debug:
            d = self.dbg_out("x1", [S, D], F32)
            b.dma("pool", d, self.x1_d[:], reads=[self.x1_d])


def _phase_ffn(self):
    b = self.b
    I = self.inp
    TG = 128
    NFT = 44
    with b.scope():
        gf = self.load_gain("gf", I["ffn_norm_g"][0])
        stage = [b.sb(f"fst{i}", [128, 1024], F32) for i in range(2)]
        wu = b.sb("wu", [128, 8, 2 * DFF], BF16)
        for n in range(8):
            for c in range(8):
                st = stage[c % 2]
                b.dma("sp", st[:, 0:704], I["w_up"][0][c * 128:(c + 1) * 128, n * 704:(n + 1) * 704], writes=[st])
                b.op("act", lambda e: e.activation(out=wu[:, c, n * 704:(n + 1) * 704], in_=st[:, 0:704], func=AF.Copy, scale=gf[:, c:c + 1]),
                     reads=[st, gf], writes=[wu])
        wd = b.sb("wd", [128, 22, D], BF16)
        self.load_weight(wd, I["w_down"][0], D, kch=22, stage=stage, eng="dve")
        cw = b.sb("cw", [128, 3, NFT], F32)
        for j in range(3):
            b.dma("sp", cw[:, j, :], I["conv_w"][0][j].rearrange("(c p) -> p c", p=128), writes=[cw], allow_slow_non_contiguous=True)
        cbias = self.load_gain("cbias", I["conv_b"][0], kch=NFT)
        carry = b.sb("carry", [128, NFT, 2], F32)
        b.op("pool", lambda e: e.memset(carry[:], 0.0), writes=[carry])
        xt = [b.sb(f"fxt{i}", [128, D], F32) for i in range(2)]
        junk = b.sb("fjunk", [128, D], BF16)
        ss = [b.sb(f"fss{i}", [128, 1], F32) for i in range(2)]
        hb = [b.sb(f"fhb{i}", [128, D], BF16) for i in range(2)]
        hT1 = [b.sb(f"fhT{i}", [128, 8, 128], BF16) for i in range(2)]
        hTg = b.sb("fhTg", [128, 8, TG], BF16)
        ub = [b.sb(f"ub{i}", [128, TG + 2], F32) for i in range(2)]
        cv = [b.sb(f"cv{i}", [128, TG], F32) for i in range(2)]
        sgl = b.sb("sgl", [128, TG], F32)
        actT = b.sb("actT", [128, 22, TG], BF16)
        self._val = b.sb("fval", [128, 22, TG], BF16)
        ot = xt
        pt = b.ps("fpt", [128, 8, 128], BF16)
        pu = [b.ps(f"fpu{i}", [128, 512], F32) for i in range(3)]
        pd = [b.ps(f"fpd{i}", [128, 512], F32) for i in range(2)]
        ng = getattr(self, "nt_limit", NT) * 128 // TG
        for gi in range(ng):
            for s_ in range(TG // 128):
                t = gi * (TG // 128) + s_
                self.make_hT(self.x1_d, t, xt[s_], junk, ss[s_], hb[s_], pt, hT1[s_], self.ident)
                b.op("pool", lambda e: e.tensor_copy(out=hTg[:, :, s_ * 128:(s_ + 1) * 128], in_=hT1[s_][:]), reads=[hT1[s_]], writes=[hTg])
            for ft in range(NFT):
                p = pu[ft % 3]
                u = ub[ft % 2]
                c_ = cv[(ft // 22) % 2] if False else cv[ft % 2]
                for c in range(8):
                    b.op("pe", lambda e: e.matmul(p[:, 0:TG], lhsT=wu[:, c, ft * 128:(ft + 1) * 128], rhs=hTg[:, c, :], start=(c == 0), stop=(c == 7)),
                         reads=[wu, hTg], writes=[p])
                b.op("act", lambda e: e.copy(out=u[:, 2:TG + 2], in_=p[:, 0:TG]), reads=[p], writes=[u])
                b.op("pool", lambda e: e.tensor_copy(out=u[:, 0:2], in_=carry[:, ft, :]), reads=[carry], writes=[u])
                b.op("pool", lambda e: e.tensor_copy(out=carry[:, ft, :], in_=u[:, TG:TG + 2]), reads=[u], writes=[carry])
                b.op("dve", lambda e: e.tensor_scalar(out=c_[:], in0=u[:, 0:TG], scalar1=cw[:, 0, ft:ft + 1], scalar2=cbias[:, ft:ft + 1], op0=ALU.mult, op1=ALU.add),
                     reads=[u, cw, cbias], writes=[c_])
                b.op("dve", lambda e: e.scalar_tensor_tensor(out=c_[:], in0=u[:, 1:TG + 1], scalar=cw[:, 1, ft:ft + 1], in1=c_[:], op0=ALU.mult, op1=ALU.add),
                     reads=[u, cw, c_], writes=[c_])
                if ft < 22:
                    b.op("dve", lambda e: e.scalar_tensor_tensor(out=self._val[:, ft, :], in0=u[:, 2:TG + 2], scalar=cw[:, 2, ft:ft + 1], in1=c_[:], op0=ALU.mult, op1=ALU.add),
                         reads=[u, cw, c_], writes=[self._val])
                else:
                    b.op("dve", lambda e: e.scalar_tensor_tensor(out=c_[:], in0=u[:, 2:TG + 2], scalar=cw[:, 2, ft:ft + 1], in1=c_[:], op0=ALU.mult, op1=ALU.add),
                         reads=[u, cw, c_], writes=[c_])
                    b.op("act", lambda e: e.activation(out=sgl[:], in_=c_[:], func=AF.Silu), reads=[c_], writes=[sgl])
                    b.op("dve", lambda e: e.tensor_tensor(out=actT[:, ft - 22, :], in0=sgl[:], in1=self._val[:, ft - 22, :], op=ALU.mult),
                         reads=[sgl, self._val], writes=[actT])
            for s_ in range(TG // 128):
                t = gi * (TG // 128) + s_
                for n in range(2):
                    for f in range(22):
                        b.op("pe", lambda e: e.matmul(pd[n][:, :], lhsT=actT[:, f, s_ * 128:(s_ + 1) * 128], rhs=wd[:, f, n * 512:(n + 1) * 512], start=(f == 0), stop=(f == 21)),
                             reads=[actT, wd], writes=[pd[n]])
                    b.op("dve", lambda e: e.tensor_tensor(out=ot[s_][:, n * 512:(n + 1) * 512], in0=pd[n][:, :], in1=xt[s_][:, n * 512:(n + 1) * 512], op=ALU.add),
                         reads=[pd[n], xt[s_]], writes=[ot[s_]])
                b.dma("pool", self.out[t * 128:(t + 1) * 128, :], ot[s_][:], reads=[ot[s_]])


Prog.phase_merge = _phase_merge
Prog.phase_ffn = _phase_ffn


def _phase_rwkv(self):
    b = self.b
    I = self.inp
    TG = 256
    NCH = TG // 64
    tt = lambda eng, out, in0, in1, op, rd, wr: b.op(eng, lambda e: e.tensor_tensor(out=out, in0=in0, in1=in1, op=op), reads=rd, writes=wr)
    with b.scope():
        gat = self.load_gain("gat3", I["attn_norm_g"][0])
        stage = [b.sb(f"rst{i}", [128, 1792], F32) for i in range(2)]
        wr = b.sb("wr", [128, 8, 1792], BF16)
        self.load_weight(wr, I["w_in"][0][:, RW0:RW0 + 1792], 1792, gvec=gat, stage=stage)

        def colvec(name, src, n):
            t = b.sb(name, [64, n], F32)
            b.dma("sp", t[:], src.rearrange("(c p) -> p c", p=64), writes=[t], allow_slow_non_contiguous=True)
            return t
        mu = colvec("mu", I["rwkv_mu"][0], 28)
        w0 = colvec("w0", I["rwkv_w0"][0], 8)
        a0 = colvec("a0", I["rwkv_a0"][0], 8)
        k_k = colvec("k_k", I["rwkv_k_k"][0], 8)
        k_a = colvec("k_a", I["rwkv_k_a"][0], 8)
        r_k = colvec("r_k", I["rwkv_r_k"][0].rearrange("h d -> (h d)"), 8)
        w2s = b.sb("w2s", [64, 512], F32)
        a2s = b.sb("a2s", [64, 512], F32)
        g2s = b.sb("g2s", [64, 2, 512], F32)
        b.dma("sp", w2s[:], I["rwkv_w2"][0], writes=[w2s])
        b.dma("sp", a2s[:], I["rwkv_a2"][0], writes=[a2s])
        b.dma("sp", g2s[:], I["rwkv_g2"][0].rearrange("(two l) f -> l two f", two=2), writes=[g2s])
        lng = b.sb("lng", [64, 512], F32)
        lnb = b.sb("lnb", [64, 512], F32)
        b.dma("sp", lng[:], I["rwkv_ln_g"][0].partition_broadcast(64), writes=[lng])
        b.dma("sp", lnb[:], I["rwkv_ln_b"][0].partition_broadcast(64), writes=[lnb])
        msk = b.sb("rmsk", [64, 3, 64], F32)
        b.dma("sp", msk[:], I["rwmask"], writes=[msk])
        rstm = b.sb("rstm", [64, TG], F32)
        b.dma("sp", rstm[:], I["rwreset"][:, 0:TG], writes=[rstm])
        ones = b.sb("ones64", [64, 64], F32)
        b.op("pool", lambda e: e.memset(ones[:], 1.0), writes=[ones])
        idf = self.identf
        carry = b.sb("rcarry", [64, 28], F32)
        b.op("pool", lambda e: e.memset(carry[:], 0.0), writes=[carry])
        Hs = [[b.sb(f"H{h}_{i}", [64, 64], F32) for i in range(2)] for h in range(8)]
        for h in range(8):
            b.op("pool", lambda e: e.memset(Hs[h][0][:], 0.0), writes=[Hs[h][0]])
        xt = [b.sb(f"rxt{i}", [128, D], F32) for i in range(2)]
        junk = b.sb("rjunk", [128, D], BF16)
        ss = [b.sb(f"rss{i}", [128, 1], F32) for i in range(2)]
        hb = [b.sb(f"rhb{i}", [128, D], BF16) for i in range(2)]
        hT1 = [b.sb(f"rhT{i}", [128, 8, 128], BF16) for i in range(2)]
        hTg = b.sb("rhTg", [128, 8, TG], BF16)
        pbuf = [b.sb(f"rpb{i}", [64, TG + 1], F32) for i in range(2)]
        dtmp = b.sb("rdtmp", [64, TG], F32)
        X = [b.sb(f"rX{w}", [64, 8, TG], F32) for w in range(3)]
        xs = b.sb("rxs", [64, 4, TG], F32)
        BV = b.sb("rBV", [64, 8, TG], F32)
        Ytm = b.sb("rYtm", [64, NCH, 8, 64], F32)
        sqv = b.sb("rsqv", [64, NCH, 8, 64], F32)
        st1 = b.sb("rst1", [64, NCH * 8], F32)
        st2 = b.sb("rst2", [64, NCH * 8], F32)
        T = {n: b.sb("r" + n, [64, TG], F32) for n in ["lw", "as", "kk", "sq", "kkn", "bv", "kp", "t1", "L", "Lx", "Ep", "Em", "Ex", "BT", "KT", "BG", "KG", "rk"]}
        AR = b.sb("rAR", [64, NCH, 2, 64], F32)
        TM = [b.sb(f"rTM{i}", [64, 3, 64], F32) for i in range(2)]
        XM = [b.sb(f"rXM{i}", [64, 4, 64], F32) for i in range(2)]
        AA = [b.sb(f"rAA{i}", [64, 2, 64], F32) for i in range(3)]
        PP = [b.sb(f"rPP{i}", [64, 64], F32) for i in range(3)]
        Xs = b.sb("rXs", [64, 64], F32)
        Us = b.sb("rUs", [64, 64], F32)
        obf = [b.sb(f"robf{i}", [64, TG], BF16) for i in range(2)]
        otmp = b.sb("rotmp", [64, TG], F32)
        pt = b.ps("rpt", [128, 8, 128], BF16)
        pp = [b.ps(f"rpp{i}", [128, 512], F32) for i in range(2)]
        pq = [b.ps(f"rpq{i}", [128, 512], F32) for i in range(2)]
        pd = [b.ps(f"rpd{i}", [128, 512], F32) for i in range(2)]
        pz = b.ps("rpz", [128, 512], F32)
        cnt = {"pp": 0, "pq": 0, "pd": 0, "aa": 0, "ppb": 0, "tm": 0, "xm": 0, "pb": 0}

        def nxt(k, lst):
            cnt[k] += 1
            return lst[cnt[k] % len(lst)]

        ngr = getattr(self, "nrg_limit", S // TG)
        for gi in range(ngr):
            q0 = gi * TG
            for s_ in range(TG // 128):
                t = gi * (TG // 128) + s_
                self.make_hT(I["x"], t, xt[s_], junk, ss[s_], hb[s_], pt, hT1[s_], self.ident)
                b.op("pool", lambda e: e.tensor_copy(out=hTg[:, :, s_ * 128:(s_ + 1) * 128], in_=hT1[s_][:]), reads=[hT1[s_]], writes=[hTg])

            def proj_lerp(fc, out_ap, out_buf, post=None):
                p = nxt("pp", pp)
                for c in range(8):
                    b.op("pe", lambda e: e.matmul(p[0:64, 0:TG], lhsT=wr[:, c, fc * 64:(fc + 1) * 64], rhs=hTg[:, c, :], start=(c == 0), stop=(c == 7)),
                         reads=[wr, hTg], writes=[p])
                pb_ = nxt("pb", pbuf)
                b.op("act", lambda e: e.copy(out=pb_[:, 1:TG + 1], in_=p[0:64, 0:TG]), reads=[p], writes=[pb_])
                b.op("pool", lambda e: e.tensor_copy(out=pb_[:, 0:1], in_=carry[:, fc:fc + 1]), reads=[carry], writes=[pb_])
                b.op("pool", lambda e: e.tensor_copy(out=carry[:, fc:fc + 1], in_=pb_[:, TG:TG + 1]), reads=[pb_], writes=[carry])
                tt("dve", dtmp[:], pb_[:, 0:TG], pb_[:, 1:TG + 1], ALU.subtract, [pb_], [dtmp])
                b.op("dve", lambda e: e.scalar_tensor_tensor(out=out_ap, in0=dtmp[:], scalar=mu[:, fc:fc + 1], in1=pb_[:, 1:TG + 1], op0=ALU.mult, op1=ALU.add),
                     reads=[dtmp, mu, pb_], writes=[out_buf])

            for w in range(3):
                for h in range(8):
                    proj_lerp(w * 8 + h, X[w][:, h, :], X[w])
            for j in range(4):
                proj_lerp(24 + j, xs[:, j, :], xs)
            b.op("act", lambda e: e.activation(out=xs[:, 0, :], in_=xs[:, 0, :], func=AF.Tanh), reads=[xs], writes=[xs])
            b.op("act", lambda e: e.activation(out=xs[:, 2:4, :], in_=xs[:, 2:4, :], func=AF.Sigmoid), reads=[xs], writes=[xs])

            for h in range(8):
                hs = slice(h * 64, (h + 1) * 64)
                R_, K_, V_ = X[0][:, h, :], X[1][:, h, :], X[2][:, h, :]
                p = nxt("pp", pp)
                b.op("pe", lambda e: e.matmul(p[0:64, 0:TG], lhsT=w2s[:, hs], rhs=xs[:, 0, :], start=True, stop=True), reads=[w2s, xs], writes=[p])
                b.op("act", lambda e: e.activation(out=T["lw"][:], in_=p[0:64, 0:TG], func=AF.Sigmoid, bias=w0[:, h:h + 1]), reads=[p, w0], writes=[T["lw"]])
                b.op("pool", lambda e: e.tensor_scalar_mul(out=T["lw"][:], in0=T["lw"][:], scalar1=-0.6065306597126334), reads=[T["lw"]], writes=[T["lw"]])
                p = nxt("pp", pp)
                b.op("pe", lambda e: e.matmul(p[0:64, 0:TG], lhsT=a2s[:, hs], rhs=xs[:, 1, :], start=True, stop=True), reads=[a2s, xs], writes=[p])
                b.op("act", lambda e: e.activation(out=T["as"][:], in_=p[0:64, 0:TG], func=AF.Sigmoid, bias=a0[:, h:h + 1]), reads=[p, a0], writes=[T["as"]])
                b.op("dve", lambda e: e.tensor_scalar_mul(out=T["kk"][:], in0=K_, scalar1=k_k[:, h:h + 1]), reads=[X[1], k_k], writes=[T["kk"]])
                tt("pool", T["sq"][:], T["kk"][:], T["kk"][:], ALU.mult, [T["kk"]], [T["sq"]])
                p = nxt("pp", pp)
                b.op("pe", lambda e: e.matmul(p[0:64, 0:TG], lhsT=ones[:], rhs=T["sq"][:], start=True, stop=True), reads=[ones, T["sq"]], writes=[p])
                b.op("act", lambda e: e.activation(out=T["sq"][:], in_=p[0:64, 0:TG], func=AF.Sqrt), reads=[p], writes=[T["sq"]])
                b.op("dve", lambda e: e.tensor_scalar_max(out=T["sq"][:], in0=T["sq"][:], scalar1=1e-12), reads=[T["sq"]], writes=[T["sq"]])
                b.op("dve", lambda e: e.reciprocal(out=T["sq"][:], in_=T["sq"][:]), reads=[T["sq"]], writes=[T["sq"]])
                tt("dve", T["kkn"][:], T["kk"][:], T["sq"][:], ALU.mult, [T["kk"], T["sq"]], [T["kkn"]])
                tt("pool", T["bv"][:], T["kkn"][:], T["as"][:], ALU.mult, [T["kkn"], T["as"]], [T["bv"]])
                b.op("dve", lambda e: e.tensor_scalar(out=T["t1"][:], in0=T["as"][:], scalar1=-1.0, scalar2=k_a[:, h:h + 1], op0=ALU.add, op1=ALU.mult),
                     reads=[T["as"], k_a], writes=[T["t1"]])
                b.op("dve", lambda e: e.scalar_tensor_tensor(out=T["kp"][:], in0=T["t1"][:], scalar=1.0, in1=K_, op0=ALU.add, op1=ALU.mult),
                     reads=[T["t1"], X[1]], writes=[T["kp"]])
                tt("pool", T["rk"][:], R_, T["kp"][:], ALU.mult, [X[0], T["kp"]], [T["rk"]])
                b.op("pool", lambda e: e.tensor_scalar_mul(out=T["rk"][:], in0=T["rk"][:], scalar1=r_k[:, h:h + 1]), reads=[T["rk"], r_k], writes=[T["rk"]])
                p = nxt("pp", pp)
                b.op("pe", lambda e: e.matmul(p[0:64, 0:TG], lhsT=ones[:], rhs=T["rk"][:], start=True, stop=True), reads=[ones, T["rk"]], writes=[p])
                tt("dve", BV[:, h, :], p[0:64, 0:TG], V_, ALU.mult, [p, X[2]], [BV])
                b.op("dve", lambda e: e.tensor_tensor_scan(out=T["L"][:], data0=rstm[:], data1=T["lw"][:], initial=0.0, op0=ALU.mult, op1=ALU.add),
                     reads=[rstm, T["lw"]], writes=[T["L"]])
                tt("pool", T["Lx"][:], T["L"][:], T["lw"][:], ALU.subtract, [T["L"], T["lw"]], [T["Lx"]])
                b.op("act", lambda e: e.activation(out=T["Ep"][:], in_=T["L"][:], func=AF.Exp), reads=[T["L"]], writes=[T["Ep"]])
                b.op("act", lambda e: e.activation(out=T["Em"][:], in_=T["L"][:], func=AF.Exp, scale=-1.0), reads=[T["L"]], writes=[T["Em"]])
                b.op("act", lambda e: e.activation(out=T["Ex"][:], in_=T["Lx"][:], func=AF.Exp), reads=[T["Lx"]], writes=[T["Ex"]])
                c3 = lambda ap: ap.rearrange("p (c t) -> p c t", t=64)
                b.op("dve", lambda e: e.scalar_tensor_tensor(out=AR[:, :, 0, :], in0=c3(T["kkn"][:]), scalar=-1.0, in1=c3(T["Ex"][:]), op0=ALU.mult, op1=ALU.mult),
                     reads=[T["kkn"], T["Ex"]], writes=[AR])
                tt("pool", AR[:, :, 1, :], c3(R_), c3(T["Ep"][:]), ALU.mult, [X[0], T["Ep"]], [AR])
                tt("dve", T["BT"][:], T["bv"][:], T["Em"][:], ALU.mult, [T["bv"], T["Em"]], [T["BT"]])
                tt("pool", T["KT"][:], T["kp"][:], T["Em"][:], ALU.mult, [T["kp"], T["Em"]], [T["KT"]])
                gC = c3(T["Ep"][:])[:, :, 63:64].to_broadcast([64, NCH, 64])
                tt("dve", c3(T["BG"][:]), c3(T["BT"][:]), gC, ALU.mult, [T["BT"], T["Ep"]], [T["BG"]])
                tt("pool", c3(T["KG"][:]), c3(T["KT"][:]), gC, ALU.mult, [T["KT"], T["Ep"]], [T["KG"]])
                for c in range(NCH):
                    cs = slice(c * 64, (c + 1) * 64)
                    Hc = Hs[h][(gi * NCH + c) % 2]
                    Hn = Hs[h][(gi * NCH + c + 1) % 2]
                    p = nxt("pq", pq)
                    for j, (src, sb_) in enumerate([(V_[:, cs], X[2]), (T["BG"][:, cs], T["BG"]), (T["KG"][:, cs], T["KG"])]):
                        b.op("pe", lambda e: e.transpose(out=p[0:64, j * 64:(j + 1) * 64], in_=src, identity=idf[0:64, 0:64]), reads=[sb_, idf], writes=[p])
                    tm = nxt("tm", TM)
                    b.op("act", lambda e: e.copy(out=tm[:].rearrange("p a b -> p (a b)"), in_=p[0:64, 0:192]), reads=[p], writes=[tm])
                    p = nxt("pq", pq)
                    arc = AR[:, c, :, :].rearrange("p a t -> p (a t)")
                    b.op("pe", lambda e: e.matmul(p[0:64, 0:128], lhsT=T["BT"][:, cs], rhs=arc, start=True, stop=True), reads=[T["BT"], AR], writes=[p])
                    b.op("pe", lambda e: e.matmul(p[0:64, 128:256], lhsT=T["KT"][:, cs], rhs=arc, start=True, stop=True), reads=[T["KT"], AR], writes=[p])
                    b.op("pe", lambda e: e.matmul(p[0:64, 256:320], lhsT=AR[:, c, 0, :], rhs=T["BT"][:, cs], start=True, stop=True), reads=[T["BT"], AR], writes=[p])
                    xm = nxt("xm", XM)
                    tt("dve", xm[:].rearrange("p (a m) t -> p a m t", a=2), p[0:64, 0:256].rearrange("p (a m t) -> p a m t", a=2, m=2),
                       msk[:, None, 0:2, :].to_broadcast([64, 2, 2, 64]), ALU.mult, [p, msk], [xm])
                    aa = nxt("aa", AA)
                    b.op("pool", lambda e: e.tensor_copy(out=aa[:, 0, :], in_=xm[:, 0, :]), reads=[xm], writes=[aa])
                    tt("dve", aa[:, 1, :], p[0:64, 256:320], msk[:, 2, :], ALU.mult, [p, msk], [aa])
                    P_ = nxt("ppb", PP)
                    tt("pool", P_[:], xm[:, 0, :], idf[0:64, 0:64], ALU.add, [xm, idf], [P_])
                    for step in range(5):
                        pdb = nxt("pd", pd)
                        b.op("pe", lambda e: e.matmul(pdb[0:64, 0:64], lhsT=aa[:, 1, :], rhs=aa[:, 0, :], start=True, stop=True), reads=[aa], writes=[pdb])
                        b.op("pe", lambda e: e.matmul(pdb[0:64, 64:128], lhsT=aa[:, 0, :], rhs=aa[:, 1, :], start=True, stop=True), reads=[aa], writes=[pdb])
                        aa2 = nxt("aa", AA)
                        b.op("act", lambda e: e.copy(out=aa2[:].rearrange("p a t -> p (a t)"), in_=pdb[0:64, 0:128]), reads=[pdb], writes=[aa2])
                        b.op("pe", lambda e: e.matmul(pdb[0:64, 128:192], lhsT=aa2[:, 1, :], rhs=P_[:], start=True, stop=True), reads=[aa2, P_], writes=[pdb])
                        P2 = nxt("ppb", PP)
                        tt("dve", P2[:], pdb[0:64, 128:192], P_[:], ALU.add, [pdb, P_], [P2])
                        aa, P_ = aa2, P2
                    b.op("pe", lambda e: e.matmul(pz[0:64, 0:64], lhsT=xm[:, 2, :], rhs=tm[:, 0, :], start=True, stop=False), reads=[xm, tm], writes=[pz])
                    b.op("pe", lambda e: e.matmul(pz[0:64, 0:64], lhsT=AR[:, c, 0, :], rhs=Hc[:], start=False, stop=True), reads=[AR, Hc], writes=[pz])
                    b.op("act", lambda e: e.copy(out=Xs[:], in_=pz[0:64, 0:64]), reads=[pz], writes=[Xs])
                    b.op("pe", lambda e: e.matmul(pz[0:64, 64:128], lhsT=P_[:], rhs=Xs[:], start=True, stop=True), reads=[P_, Xs], writes=[pz])
                    b.op("act", lambda e: e.copy(out=Us[:], in_=pz[0:64, 64:128]), reads=[pz], writes=[Us])
                    b.op("pe", lambda e: e.matmul(pz[0:64, 128:192], lhsT=AR[:, c, 1, :], rhs=Hc[:], start=True, stop=False), reads=[AR, Hc], writes=[pz])
                    b.op("pe", lambda e: e.matmul(pz[0:64, 128:192], lhsT=xm[:, 1, :], rhs=Us[:], start=False, stop=False), reads=[xm, Us], writes=[pz])
                    b.op("pe", lambda e: e.matmul(pz[0:64, 128:192], lhsT=xm[:, 3, :], rhs=tm[:, 0, :], start=False, stop=True), reads=[xm, tm], writes=[pz])
                    b.op("pe", lambda e: e.matmul(pz[0:64, 192:256], lhsT=tm[:, 1, :], rhs=Us[:], start=True, stop=False), reads=[tm, Us], writes=[pz])
                    b.op("pe", lambda e: e.matmul(pz[0:64, 192:256], lhsT=tm[:, 2, :], rhs=tm[:, 0, :], start=False, stop=True), reads=[tm], writes=[pz])
                    b.op("act", lambda e: e.copy(out=Ytm[:, c, h, :], in_=pz[0:64, 128:192]), reads=[pz], writes=[Ytm])
                    b.op("dve", lambda e: e.scalar_tensor_tensor(out=Hn[:], in0=Hc[:], scalar=T["Ep"][:, c * 64 + 63:c * 64 + 64], in1=pz[0:64, 192:256],
                                                                 op0=ALU.mult, op1=ALU.add), reads=[Hc, T["Ep"], pz], writes=[Hn])
            Y3 = Ytm[:].rearrange("p c h i -> p (c h) i")
            S3 = sqv[:].rearrange("p c h i -> p (c h) i")
            b.op("dve", lambda e: e.tensor_reduce(out=st1[:], in_=Y3, axis=AX.X, op=ALU.add), reads=[Ytm], writes=[st1])
            b.op("pool", lambda e: e.tensor_scalar_mul(out=st1[:], in0=st1[:], scalar1=1.0 / 64), reads=[st1], writes=[st1])
            tt("dve", Y3, Y3, st1[:].unsqueeze(2).to_broadcast([64, NCH * 8, 64]), ALU.subtract, [Ytm, st1], [Ytm])
            tt("pool", S3, Y3, Y3, ALU.mult, [Ytm], [sqv])
            b.op("dve", lambda e: e.tensor_reduce(out=st2[:], in_=S3, axis=AX.X, op=ALU.add), reads=[sqv], writes=[st2])
            b.op("act", lambda e: e.activation(out=st2[:], in_=st2[:], func=AF.Sqrt, scale=1.0 / 64, bias=64e-5), reads=[st2], writes=[st2])
            b.op("dve", lambda e: e.reciprocal(out=st2[:], in_=st2[:]), reads=[st2], writes=[st2])
            tt("dve", Y3, Y3, st2[:].unsqueeze(2).to_broadcast([64, NCH * 8, 64]), ALU.mult, [Ytm, st2], [Ytm])
            lg = lng[:].rearrange("p (h i) -> p h i", i=64)[:, None, :, :].to_broadcast([64, NCH, 8, 64])
            lb = lnb[:].rearrange("p (h i) -> p h i", i=64)[:, None, :, :].to_broadcast([64, NCH, 8, 64])
            tt("pool", Ytm[:], Ytm[:], lg, ALU.mult, [Ytm, lng], [Ytm])
            tt("dve", Ytm[:], Ytm[:], lb, ALU.add, [Ytm, lnb], [Ytm])
            for h in range(8):
                p = nxt("pq", pq)
                for c in range(NCH):
                    b.op("pe", lambda e: e.transpose(out=p[0:64, c * 64:(c + 1) * 64], in_=Ytm[:, c, h, :], identity=idf[0:64, 0:64]), reads=[Ytm, idf], writes=[p])
                tt("dve", otmp[:], p[0:64, 0:TG], BV[:, h, :], ALU.add, [p, BV], [otmp])
                pg_ = nxt("pp", pp)
                b.op("pe", lambda e: e.matmul(pg_[0:64, 0:TG], lhsT=g2s[:, 0, h * 64:(h + 1) * 64], rhs=xs[:, 2, :], start=True, stop=False), reads=[g2s, xs], writes=[pg_])
                b.op("pe", lambda e: e.matmul(pg_[0:64, 0:TG], lhsT=g2s[:, 1, h * 64:(h + 1) * 64], rhs=xs[:, 3, :], start=False, stop=True), reads=[g2s, xs], writes=[pg_])
                ob_ = obf[h % 2]
                tt("dve", ob_[:], otmp[:], pg_[0:64, 0:TG], ALU.mult, [otmp, pg_], [ob_])
                b.dma("pool", self.obT_d[h // 2, (h % 2) * 64:(h % 2) * 64 + 64, q0:q0 + TG], ob_[:], reads=[ob_], writes=[self.obT_d])
        if "rwkv" in self.debug:
            d = self.dbg_out("obT", [4, 128, S], BF16)
            b.dma("pool", d, self.obT_d[:], reads=[self.obT_d])


Prog.phase_rwkv = _phase_rwkv


def build_full():
    p = Prog()
    b = p.b
    p.alloc_root()
    with b.scope():
        p.alloc_persistent()
        p.phase_nsa_proj()
        p.phase_attn2()
    p.phase_rwkv3()
    p.phase_merge()
    p.phase_ffn2()
    p.finish()
    return p


def kernel(**inputs):
    p = build_full()
    consts = host_consts(inputs["rel_bias"])
    shared = {k: np.ascontiguousarray(np.asarray(inputs[k], np.float32)) for k in W_SPECS if k != "x"}
    shared.update(consts)
    x = np.asarray(inputs["x"], np.float32)
    in_maps = []
    for c in range(8):
        m = dict(shared)
        m["x"] = np.ascontiguousarray(x[c])
        in_maps.append(m)
    res = run_bass_kernel_spmd(p.nc, in_maps, core_ids=list(range(8)))
    return np.stack([np.asarray(r["out"], np.float32) for r in res.results], axis=0)


def _phase_rwkv2(self):
    b = self.b
    I = self.inp
    TG = 128
    NCH = 2
    tt = lambda eng, out, in0, in1, op, rd, wr: b.op(eng, lambda e: e.tensor_tensor(out=out, in0=in0, in1=in1, op=op), reads=rd, writes=wr)
    with b.scope():
        W1 = b.sb("W1", [128, 8, 1792], BF16)
        W2 = b.sb("W2", [128, 8, 1792], BF16)
        with b.scope():
            gat = self.load_gain("gat3", I["attn_norm_g"][0])
            stage = [b.sb(f"rst{i}", [128, 1792], F32) for i in range(2)]
            tmpw = [b.sb(f"rtw{i}", [128, 1792], F32) for i in range(2)]
            mur = self.bcast_row("mur", I["rwkv_mu"][0], 1792)
            for c in range(8):
                st = stage[c % 2]
                tw_ = tmpw[c % 2]
                b.dma("sp", st[:], I["w_in"][0][c * 128:(c + 1) * 128, RW0:RW0 + 1792], writes=[st])
                tt("dve", tw_[:], st[:], mur[:], ALU.mult, [st, mur], [tw_])
                b.op("act", lambda e: e.activation(out=W2[:, c, :], in_=tw_[:], func=AF.Copy, scale=gat[:, c:c + 1]), reads=[tw_, gat], writes=[W2])
                tt("pool", st[:], st[:], tw_[:], ALU.subtract, [st, tw_], [st])
                b.op("act", lambda e: e.activation(out=W1[:, c, :], in_=st[:], func=AF.Copy, scale=gat[:, c:c + 1]), reads=[st, gat], writes=[W1])

        def colvec(name, src, n):
            t = b.sb(name, [64, n], F32)
            b.dma("sp", t[:], src.rearrange("(c p) -> p c", p=64), writes=[t], allow_slow_non_contiguous=True)
            return t
        w0 = colvec("w0", I["rwkv_w0"][0], 8)
        a0 = colvec("a0", I["rwkv_a0"][0], 8)
        k_k = colvec("k_k", I["rwkv_k_k"][0], 8)
        k_a = colvec("k_a", I["rwkv_k_a"][0], 8)
        r_k = colvec("r_k", I["rwkv_r_k"][0].rearrange("h d -> (h d)"), 8)
        w2s = b.sb("w2s", [64, 512], F32)
        a2s = b.sb("a2s", [64, 512], F32)
        g2s = b.sb("g2s", [64, 2, 512], F32)
        b.dma("sp", w2s[:], I["rwkv_w2"][0], writes=[w2s])
        b.dma("sp", a2s[:], I["rwkv_a2"][0], writes=[a2s])
        b.dma("sp", g2s[:], I["rwkv_g2"][0].rearrange("(two l) f -> l two f", two=2), writes=[g2s])
        lng = b.sb("lng", [64, 512], F32)
        lnb = b.sb("lnb", [64, 512], F32)
        b.dma("sp", lng[:], I["rwkv_ln_g"][0].partition_broadcast(64), writes=[lng])
        b.dma("sp", lnb[:], I["rwkv_ln_b"][0].partition_broadcast(64), writes=[lnb])
        msk = b.sb("rmsk", [64, 3, 64], F32)
        b.dma("sp", msk[:], I["rwmask"], writes=[msk])
        rstm = b.sb("rstm", [64, 8 * TG], F32)
        b.dma("sp", rstm[:], I["rwreset"], writes=[rstm])
        ones = b.sb("ones64", [64, 64], F32)
        b.op("pool", lambda e: e.memset(ones[:], 1.0), writes=[ones])
        idf = self.identf
        Hst = b.sb("rH", [64, 2, 8, 64], F32)
        b.op("pool", lambda e: e.memset(Hst[:], 0.0), writes=[Hst])
        xt = [b.sb(f"rxt{i}", [128, D], F32) for i in range(1)] * 2
        junk = b.sb("rjunk", [128, D], BF16)
        ss = [b.sb(f"rss{i}", [128, 1], F32) for i in range(1)] * 2
        hb = [b.sb(f"rhb{i}", [128, D], BF16) for i in range(1)] * 2
        hT1 = [b.sb(f"rhT{i}", [128, 8, 128], BF16) for i in range(1)] * 2
        hTs = b.sb("rhTs", [128, 8, TG + 1], BF16)
        b.op("pool", lambda e: e.memset(hTs[:], 0.0), writes=[hTs])
        XL = b.sb("rXL", [64, 20, TG], F32)
        Vtm = b.sb("rVtm", [64, NCH, 512], F32)
        names = ["LW", "AS", "KKN", "BVc", "KP", "RK", "L", "EP", "EM", "BG", "KG"]
        T = {n: b.sb("r" + n, [64, 8, TG], F32) for n in names}
        T["NR"] = T["RK"]
        T["T1"] = T["BG"]
        T["KK"] = T["KG"]
        T["EX"] = T["L"]
        T["BT"] = T["LW"]
        T["KT"] = T["AS"]
        AR = b.sb("rAR", [64, 8, NCH, 2, 64], F32)
        BON = b.sb("rBON", [64, NCH * 8], F32)
        Ytm = b.sb("rYtm", [64, NCH, 8, 64], F32)
        sqv = b.sb("rsqv", [64, NCH, 8, 64], F32)
        st1 = b.sb("rst1", [64, NCH * 8], F32)
        st2 = b.sb("rst2", [64, NCH * 8], F32)
        TM4 = [b.sb(f"rTM{i}", [64, 4, 2, 64], F32) for i in range(2)]
        XM4 = [b.sb(f"rXM{i}", [64, 4, 4, 64], F32) for i in range(2)]
        AA4 = [b.sb(f"rAA{i}", [64, 4, 2, 64], F32) for i in range(2)]
        PP4 = [b.sb(f"rPP{i}", [64, 4, 64], F32) for i in range(2)]
        Xs4 = b.sb("rXs4", [64, 4, 64], F32)
        Us4 = b.sb("rUs4", [64, 4, 64], F32)
        Ht4 = b.sb("rHt4", [64, 4, 64], F32)
        OBb = b.sb("rOBb", [64, NCH, 512], BF16)
        obT = [b.sb(f"robT{i}", [128, 4, TG], BF16) for i in range(1)] * 2
        pt = b.ps("rpt", [128, 8, 128], BF16)
        pP = b.ps("rpP", [128, 512], F32)
        pA = b.ps("rpA", [128, 1024], F32)
        pB = b.ps("rpB", [128, 512], F32)
        pC = b.ps("rpC", [128, 512], F32)
        pD = b.ps("rpD", [128, 512], F32)
        pZ = b.ps("rpZ", [128, 512], F32)
        cnt = {}

        def nxt(k, lst):
            cnt[k] = cnt.get(k, 0) + 1
            return lst[cnt[k] % len(lst)]
        bc = lambda v: v[:].unsqueeze(2).to_broadcast([64, 8, TG])
        f2 = lambda t_: t_[:].rearrange("p h t -> p (h t)")
        c16 = lambda t_: t_[:].rearrange("p h (c t) -> p (h c) t", t=64)

        ngr = getattr(self, "nrg_limit", S // TG)
        for gi in range(ngr):
            q0 = gi * TG
            i = gi % 2
            self.make_hT(I["x"], gi, xt[i], junk, ss[i], hb[i], pt, hT1[i], self.ident)
            b.op("pool", lambda e: e.tensor_copy(out=hTs[:, :, 0:1], in_=hTs[:, :, TG:TG + 1]), reads=[hTs], writes=[hTs])
            b.op("pool", lambda e: e.tensor_copy(out=hTs[:, :, 1:TG + 1], in_=hT1[i][:]), reads=[hT1[i]], writes=[hTs])
            ftiles = list(range(0, 16)) + [24, 25, 26, 27]
            for q4 in range(5):
                for j in range(4):
                    fc = ftiles[q4 * 4 + j]
                    for c in range(8):
                        b.op("pe", lambda e: e.matmul(pP[0:64, j * TG:(j + 1) * TG], lhsT=W1[:, c, fc * 64:(fc + 1) * 64], rhs=hTs[:, c, 1:TG + 1], start=(c == 0), stop=False),
                             reads=[W1, hTs], writes=[pP])
                    for c in range(8):
                        b.op("pe", lambda e: e.matmul(pP[0:64, j * TG:(j + 1) * TG], lhsT=W2[:, c, fc * 64:(fc + 1) * 64], rhs=hTs[:, c, 0:TG], start=False, stop=(c == 7)),
                             reads=[W2, hTs], writes=[pP])
                b.op("act", lambda e: e.copy(out=XL[:, q4 * 4:(q4 + 1) * 4, :].rearrange("p a t -> p (a t)"), in_=pP[0:64, :]), reads=[pP], writes=[XL])
            for c_ in range(NCH):
                for c in range(8):
                    b.op("pe", lambda e: e.matmul(pP[0:64, :], lhsT=hTs[:, c, 1 + c_ * 64:1 + (c_ + 1) * 64], rhs=W1[:, c, 1024:1536], start=(c == 0), stop=False),
                         reads=[W1, hTs], writes=[pP])
                for c in range(8):
                    b.op("pe", lambda e: e.matmul(pP[0:64, :], lhsT=hTs[:, c, c_ * 64:(c_ + 1) * 64], rhs=W2[:, c, 1024:1536], start=False, stop=(c == 7)),
                         reads=[W2, hTs], writes=[pP])
                b.op("act", lambda e: e.copy(out=Vtm[:, c_, :], in_=pP[0:64, :]), reads=[pP], writes=[Vtm])
            R_ = XL[:, 0:8, :]
            K_ = XL[:, 8:16, :]
            b.op("act", lambda e: e.activation(out=XL[:, 16, :], in_=XL[:, 16, :], func=AF.Tanh), reads=[XL], writes=[XL])
            b.op("act", lambda e: e.activation(out=XL[:, 18:20, :], in_=XL[:, 18:20, :], func=AF.Sigmoid), reads=[XL], writes=[XL])
            for (ws_, src, bias_, dst) in [(w2s, 16, w0, "LW"), (a2s, 17, a0, "AS")]:
                for half in range(2):
                    for j in range(4):
                        h = half * 4 + j
                        b.op("pe", lambda e: e.matmul(pP[0:64, j * TG:(j + 1) * TG], lhsT=ws_[:, h * 64:(h + 1) * 64], rhs=XL[:, src, :], start=True, stop=True),
                             reads=[ws_, XL], writes=[pP])
                    for j in range(4):
                        h = half * 4 + j
                        b.op("act", lambda e: e.activation(out=T[dst][:, h, :], in_=pP[0:64, j * TG:(j + 1) * TG], func=AF.Sigmoid, bias=bias_[:, h:h + 1]),
                             reads=[pP, bias_], writes=[T[dst]])
            b.op("pool", lambda e: e.tensor_scalar_mul(out=f2(T["LW"]), in0=f2(T["LW"]), scalar1=-0.6065306597126334), reads=[T["LW"]], writes=[T["LW"]])
            tt("dve", T["KK"][:], K_, bc(k_k), ALU.mult, [XL, k_k], [T["KK"]])
            tt("pool", T["NR"][:], T["KK"][:], T["KK"][:], ALU.mult, [T["KK"]], [T["NR"]])
            for half in range(2):
                b.op("pe", lambda e: e.matmul(pP[0:64, :], lhsT=ones[:], rhs=T["NR"][:, half * 4:(half + 1) * 4, :].rearrange("p h t -> p (h t)"), start=True, stop=True),
                     reads=[ones, T["NR"]], writes=[pP])
                b.op("act", lambda e: e.activation(out=T["KKN"][:, half * 4:(half + 1) * 4, :].rearrange("p h t -> p (h t)"), in_=pP[0:64, :], func=AF.Sqrt),
                     reads=[pP], writes=[T["KKN"]])
            b.op("dve", lambda e: e.tensor_scalar_max(out=f2(T["KKN"]), in0=f2(T["KKN"]), scalar1=1e-12), reads=[T["KKN"]], writes=[T["KKN"]])
            b.op("dve", lambda e: e.reciprocal(out=f2(T["KKN"]), in_=f2(T["KKN"])), reads=[T["KKN"]], writes=[T["KKN"]])
            tt("dve", T["KKN"][:], T["KKN"][:], T["KK"][:], ALU.mult, [T["KKN"], T["KK"]], [T["KKN"]])
            tt("pool", T["BVc"][:], T["KKN"][:], T["AS"][:], ALU.mult, [T["KKN"], T["AS"]], [T["BVc"]])
            b.op("pool", lambda e: e.tensor_scalar_add(out=f2(T["T1"]), in0=f2(T["AS"]), scalar1=-1.0), reads=[T["AS"]], writes=[T["T1"]])
            tt("pool", T["T1"][:], T["T1"][:], bc(k_a), ALU.mult, [T["T1"], k_a], [T["T1"]])
            b.op("dve", lambda e: e.scalar_tensor_tensor(out=f2(T["KP"]), in0=f2(T["T1"]), scalar=1.0, in1=K_.rearrange("p h t -> p (h t)"), op0=ALU.add, op1=ALU.mult),
                 reads=[T["T1"], XL], writes=[T["KP"]])
            tt("pool", T["RK"][:], R_, T["KP"][:], ALU.mult, [XL, T["KP"]], [T["RK"]])
            tt("pool", T["RK"][:], T["RK"][:], bc(r_k), ALU.mult, [T["RK"], r_k], [T["RK"]])
            for c_ in range(NCH):
                for h in range(8):
                    b.op("pe", lambda e: e.matmul(pD[0:64, c_ * 8 + h:c_ * 8 + h + 1], lhsT=T["RK"][:, h, c_ * 64:(c_ + 1) * 64], rhs=ones[:, 0:1], start=True, stop=True),
                         reads=[T["RK"], ones], writes=[pD])
            b.op("act", lambda e: e.copy(out=BON[:], in_=pD[0:64, 0:NCH * 8]), reads=[pD], writes=[BON])
            b.op("dve", lambda e: e.tensor_tensor_scan(out=f2(T["L"]), data0=rstm[:], data1=f2(T["LW"]), initial=0.0, op0=ALU.mult, op1=ALU.add),
                 reads=[rstm, T["LW"]], writes=[T["L"]])
            b.op("act", lambda e: e.activation(out=f2(T["EP"]), in_=f2(T["L"]), func=AF.Exp), reads=[T["L"]], writes=[T["EP"]])
            b.op("act", lambda e: e.activation(out=f2(T["EM"]), in_=f2(T["L"]), func=AF.Exp, scale=-1.0), reads=[T["L"]], writes=[T["EM"]])
            tt("pool", T["L"][:], T["L"][:], T["LW"][:], ALU.subtract, [T["L"], T["LW"]], [T["L"]])
            b.op("act", lambda e: e.activation(out=f2(T["EX"]), in_=f2(T["L"]), func=AF.Exp), reads=[T["L"]], writes=[T["EX"]])
            ar0 = AR[:, :, :, 0, :].rearrange("p h c t -> p (h c) t")
            ar1 = AR[:, :, :, 1, :].rearrange("p h c t -> p (h c) t")
            b.op("dve", lambda e: e.scalar_tensor_tensor(out=ar0, in0=c16(T["KKN"]), scalar=-1.0, in1=c16(T["EX"]), op0=ALU.mult, op1=ALU.mult),
                 reads=[T["KKN"], T["EX"]], writes=[AR])
            tt("pool", ar1, R_.rearrange("p h (c t) -> p (h c) t", t=64), c16(T["EP"]), ALU.mult, [XL, T["EP"]], [AR])
            tt("dve", T["BT"][:], T["BVc"][:], T["EM"][:], ALU.mult, [T["BVc"], T["EM"]], [T["BT"]])
            tt("pool", T["KT"][:], T["KP"][:], T["EM"][:], ALU.mult, [T["KP"], T["EM"]], [T["KT"]])
            gC = c16(T["EP"])[:, :, 63:64].to_broadcast([64, 16, 64])
            tt("dve", c16(T["BG"]), c16(T["BT"]), gC, ALU.mult, [T["BT"], T["EP"]], [T["BG"]])
            tt("pool", c16(T["KG"]), c16(T["KT"]), gC, ALU.mult, [T["KT"], T["EP"]], [T["KG"]])
            for c_ in range(NCH):
                cs = slice(c_ * 64, (c_ + 1) * 64)
                cur = (gi * NCH + c_) % 2
                for hb_ in range(2):
                    heads = list(range(hb_ * 4, hb_ * 4 + 4))
                    for j, h in enumerate(heads):
                        b.op("pe", lambda e: e.transpose(out=pC[0:64, j * 128:j * 128 + 64], in_=T["BG"][:, h, cs], identity=idf[0:64, 0:64]), reads=[T["BG"], idf], writes=[pC])
                        b.op("pe", lambda e: e.transpose(out=pC[0:64, j * 128 + 64:(j + 1) * 128], in_=T["KG"][:, h, cs], identity=idf[0:64, 0:64]), reads=[T["KG"], idf], writes=[pC])
                    tm = nxt("tm", TM4)
                    b.op("act", lambda e: e.copy(out=tm[:].rearrange("p h a t -> p (h a t)"), in_=pC[0:64, 0:512]), reads=[pC], writes=[tm])
                    for j, h in enumerate(heads):
                        arc = AR[:, h, c_, :, :].rearrange("p a t -> p (a t)")
                        b.op("pe", lambda e: e.matmul(pA[0:64, j * 256:j * 256 + 128], lhsT=T["BT"][:, h, cs], rhs=arc, start=True, stop=True), reads=[T["BT"], AR], writes=[pA])
                        b.op("pe", lambda e: e.matmul(pA[0:64, j * 256 + 128:(j + 1) * 256], lhsT=T["KT"][:, h, cs], rhs=arc, start=True, stop=True), reads=[T["KT"], AR], writes=[pA])
                        b.op("pe", lambda e: e.matmul(pB[0:64, j * 64:(j + 1) * 64], lhsT=AR[:, h, c_, 0, :], rhs=T["BT"][:, h, cs], start=True, stop=True), reads=[T["BT"], AR], writes=[pB])
                    xm = nxt("xm", XM4)
                    tt("dve", xm[:].rearrange("p h (a m) t -> p (h a) m t", a=2), pA[0:64, :].rearrange("p (ha m t) -> p ha m t", m=2, t=64),
                       msk[:, None, 0:2, :].to_broadcast([64, 8, 2, 64]), ALU.mult, [pA, msk], [xm])
                    aa = nxt("aa", AA4)
                    b.op("pool", lambda e: e.tensor_copy(out=aa[:, :, 0, :], in_=xm[:, :, 0, :]), reads=[xm], writes=[aa])
                    tt("dve", aa[:, :, 1, :], pB[0:64, 0:256].rearrange("p (h t) -> p h t", t=64), msk[:, 2:3, :].to_broadcast([64, 4, 64]), ALU.mult, [pB, msk], [aa])
                    P_ = nxt("pp4", PP4)
                    tt("pool", P_[:], xm[:, :, 0, :], idf[0:64, None, 0:64].to_broadcast([64, 4, 64]), ALU.add, [xm, idf], [P_])
                    for step in range(5):
                        for j in range(4):
                            b.op("pe", lambda e: e.matmul(pD[0:64, j * 128:j * 128 + 64], lhsT=aa[:, j, 1, :], rhs=aa[:, j, 0, :], start=True, stop=True), reads=[aa], writes=[pD])
                            b.op("pe", lambda e: e.matmul(pD[0:64, j * 128 + 64:(j + 1) * 128], lhsT=aa[:, j, 0, :], rhs=aa[:, j, 1, :], start=True, stop=True), reads=[aa], writes=[pD])
                        aa2 = nxt("aa", AA4)
                        b.op("act", lambda e: e.copy(out=aa2[:].rearrange("p h a t -> p (h a t)"), in_=pD[0:64, :]), reads=[pD], writes=[aa2])
                        for j in range(4):
                            b.op("pe", lambda e: e.matmul(pB[0:64, 256 + j * 64:256 + (j + 1) * 64], lhsT=aa2[:, j, 1, :], rhs=P_[:, j, :], start=True, stop=True), reads=[aa2, P_], writes=[pB])
                        P2 = nxt("pp4", PP4)
                        tt("dve", P2[:], pB[0:64, 256:512].rearrange("p (h t) -> p h t", t=64), P_[:], ALU.add, [pB, P_], [P2])
                        aa, P_ = aa2, P2
                    for j, h in enumerate(heads):
                        b.op("pe", lambda e: e.matmul(pZ[0:64, j * 64:(j + 1) * 64], lhsT=xm[:, j, 2, :], rhs=Vtm[:, c_, h * 64:(h + 1) * 64], start=True, stop=False), reads=[xm, Vtm], writes=[pZ])
                        b.op("pe", lambda e: e.matmul(pZ[0:64, j * 64:(j + 1) * 64], lhsT=AR[:, h, c_, 0, :], rhs=Hst[:, cur, h, :], start=False, stop=True), reads=[AR, Hst], writes=[pZ])
                    b.op("act", lambda e: e.copy(out=Xs4[:].rearrange("p h t -> p (h t)"), in_=pZ[0:64, 0:256]), reads=[pZ], writes=[Xs4])
                    for j in range(4):
                        b.op("pe", lambda e: e.matmul(pZ[0:64, 256 + j * 64:256 + (j + 1) * 64], lhsT=P_[:, j, :], rhs=Xs4[:, j, :], start=True, stop=True), reads=[P_, Xs4], writes=[pZ])
                    b.op("act", lambda e: e.copy(out=Us4[:].rearrange("p h t -> p (h t)"), in_=pZ[0:64, 256:512]), reads=[pZ], writes=[Us4])
                    for j, h in enumerate(heads):
                        o = slice(j * 64, (j + 1) * 64)
                        vh = Vtm[:, c_, h * 64:(h + 1) * 64]
                        b.op("pe", lambda e: e.matmul(pZ[0:64, o], lhsT=AR[:, h, c_, 1, :], rhs=Hst[:, cur, h, :], start=True, stop=False), reads=[AR, Hst], writes=[pZ])
                        b.op("pe", lambda e: e.matmul(pZ[0:64, o], lhsT=xm[:, j, 1, :], rhs=Us4[:, j, :], start=False, stop=False), reads=[xm, Us4], writes=[pZ])
                        b.op("pe", lambda e: e.matmul(pZ[0:64, o], lhsT=xm[:, j, 3, :], rhs=vh, start=False, stop=True), reads=[xm, Vtm], writes=[pZ])
                    for j, h in enumerate(heads):
                        o = slice(256 + j * 64, 256 + (j + 1) * 64)
                        vh = Vtm[:, c_, h * 64:(h + 1) * 64]
                        b.op("pe", lambda e: e.matmul(pZ[0:64, o], lhsT=tm[:, j, 0, :], rhs=Us4[:, j, :], start=True, stop=False), reads=[tm, Us4], writes=[pZ])
                        b.op("pe", lambda e: e.matmul(pZ[0:64, o], lhsT=tm[:, j, 1, :], rhs=vh, start=False, stop=True), reads=[tm, Vtm], writes=[pZ])
                    b.op("act", lambda e: e.copy(out=Ytm[:, c_, hb_ * 4:(hb_ + 1) * 4, :].rearrange("p h t -> p (h t)"), in_=pZ[0:64, 0:256]), reads=[pZ], writes=[Ytm])
                    gH = T["EP"][:, hb_ * 4:(hb_ + 1) * 4, c_ * 64 + 63:c_ * 64 + 64].to_broadcast([64, 4, 64])
                    tt("pool", Ht4[:], Hst[:, cur, hb_ * 4:(hb_ + 1) * 4, :], gH, ALU.mult, [Hst, T["EP"]], [Ht4])
                    tt("dve", Hst[:, 1 - cur, hb_ * 4:(hb_ + 1) * 4, :], pZ[0:64, 256:512].rearrange("p (h t) -> p h t", t=64), Ht4[:], ALU.add, [pZ, Ht4], [Hst])
            Y3 = Ytm[:].rearrange("p c h i -> p (c h) i")
            S3 = sqv[:].rearrange("p c h i -> p (c h) i")
            b.op("dve", lambda e: e.tensor_reduce(out=st1[:], in_=Y3, axis=AX.X, op=ALU.add), reads=[Ytm], writes=[st1])
            b.op("pool", lambda e: e.tensor_scalar_mul(out=st1[:], in0=st1[:], scalar1=1.0 / 64), reads=[st1], writes=[st1])
            tt("dve", Y3, Y3, st1[:].unsqueeze(2).to_broadcast([64, NCH * 8, 64]), ALU.subtract, [Ytm, st1], [Ytm])
            tt("pool", S3, Y3, Y3, ALU.mult, [Ytm], [sqv])
            b.op("dve", lambda e: e.tensor_reduce(out=st2[:], in_=S3, axis=AX.X, op=ALU.add), reads=[sqv], writes=[st2])
            b.op("act", lambda e: e.activation(out=st2[:], in_=st2[:], func=AF.Sqrt, scale=1.0 / 64, bias=64e-5), reads=[st2], writes=[st2])
            b.op("dve", lambda e: e.reciprocal(out=st2[:], in_=st2[:]), reads=[st2], writes=[st2])
            tt("dve", Y3, Y3, st2[:].unsqueeze(2).to_broadcast([64, NCH * 8, 64]), ALU.mult, [Ytm, st2], [Ytm])
            lg = lng[:].rearrange("p (h i) -> p h i", i=64)[:, None, :, :].to_broadcast([64, NCH, 8, 64])
            lb = lnb[:].rearrange("p (h i) -> p h i", i=64)[:, None, :, :].to_broadcast([64, NCH, 8, 64])
            tt("pool", Ytm[:], Ytm[:], lg, ALU.mult, [Ytm, lng], [Ytm])
            tt("dve", Ytm[:], Ytm[:], lb, ALU.add, [Ytm, lnb], [Ytm])
            V3 = Vtm[:].rearrange("p c (h i) -> p (c h) i", i=64)
            tt("pool", S3, V3, BON[:].unsqueeze(2).to_broadcast([64, NCH * 8, 64]), ALU.mult, [Vtm, BON], [sqv])
            tt("dve", Y3, Y3, S3, ALU.add, [Ytm, sqv], [Ytm])
            for c_ in range(NCH):
                for two in range(2):
                    b.op("pe", lambda e: e.matmul(pP[0:64, :], lhsT=XL[:, 18 + two, c_ * 64:(c_ + 1) * 64], rhs=g2s[:, two, :], start=(two == 0), stop=(two == 1)),
                         reads=[XL, g2s], writes=[pP])
                tt("dve", OBb[:, c_, :], Ytm[:, c_, :, :].rearrange("p h i -> p (h i)"), pP[0:64, :], ALU.mult, [Ytm, pP], [OBb])
                for k4 in range(4):
                    b.op("pe", lambda e: e.transpose(out=pt[:, k4, c_ * 64:(c_ + 1) * 64], in_=OBb[:, c_, k4 * 128:(k4 + 1) * 128], identity=self.ident[0:64, 0:64]),
                         reads=[OBb, self.ident], writes=[pt])
            ot = obT[gi % 2]
            b.op("act", lambda e: e.copy(out=ot[:], in_=pt[:, 0:4, :]), reads=[pt], writes=[ot])
            b.dma("pool", self.obT_d[:, :, q0:q0 + TG].rearrange("c p t -> p c t"), ot[:], reads=[ot], writes=[self.obT_d])
        if "rwkv" in self.debug:
            d = self.dbg_out("obT", [4, 128, S], BF16)
            b.dma("pool", d, self.obT_d[:], reads=[self.obT_d])


Prog.phase_rwkv2 = _phase_rwkv2


def _phase_rwkv3(self):
    b = self.b
    I = self.inp
    TG = 128
    NCH = 2
    CHDT = mybir.dt.float32r if getattr(self, "use_f32r", True) else F32
    tt = lambda eng, out, in0, in1, op, rd, wr: b.op(eng, lambda e: e.tensor_tensor(out=out, in0=in0, in1=in1, op=op), reads=rd, writes=wr)
    with b.scope():
        W1 = b.sb("W1", [128, 8, 1792], BF16)
        with b.scope():
            gat = self.load_gain("gat3", I["attn_norm_g"][0])
            stage = [b.sb(f"rst{i}", [128, 1792], F32) for i in range(2)]
            self.load_weight(W1, I["w_in"][0][:, RW0:RW0 + 1792], 1792, gvec=gat, stage=stage)

        def colvec(name, src, n):
            t = b.sb(name, [64, n], F32)
            b.dma("sp", t[:], src.rearrange("(c p) -> p c", p=64), writes=[t], allow_slow_non_contiguous=True)
            return t
        mu = colvec("mu", I["rwkv_mu"][0], 28)
        w0 = colvec("w0", I["rwkv_w0"][0], 8)
        a0 = colvec("a0", I["rwkv_a0"][0], 8)
        k_k = colvec("k_k", I["rwkv_k_k"][0], 8)
        k_a = colvec("k_a", I["rwkv_k_a"][0], 8)
        r_k = colvec("r_k", I["rwkv_r_k"][0].rearrange("h d -> (h d)"), 8)
        w2s = b.sb("w2s", [64, 512], F32)
        a2s = b.sb("a2s", [64, 512], F32)
        g2s = b.sb("g2s", [64, 2, 512], F32)
        b.dma("sp", w2s[:], I["rwkv_w2"][0], writes=[w2s])
        b.dma("sp", a2s[:], I["rwkv_a2"][0], writes=[a2s])
        b.dma("sp", g2s[:], I["rwkv_g2"][0].rearrange("(two l) f -> l two f", two=2), writes=[g2s])
        lng = b.sb("lng", [64, 512], F32)
        lnb = b.sb("lnb", [64, 512], F32)
        b.dma("sp", lng[:], I["rwkv_ln_g"][0].partition_broadcast(64), writes=[lng])
        b.dma("sp", lnb[:], I["rwkv_ln_b"][0].partition_broadcast(64), writes=[lnb])
        msk = b.sb("rmsk", [64, 3, 64], F32)
        b.dma("sp", msk[:], I["rwmask"], writes=[msk])
        rstm = b.sb("rstm", [64, 8 * TG], F32)
        b.dma("sp", rstm[:], I["rwreset"], writes=[rstm])
        ones = b.sb("ones64", [64, 64], F32)
        b.op("pool", lambda e: e.memset(ones[:], 1.0), writes=[ones])
        idf = self.identf
        Hst = b.sb("rH", [64, 2, 8, 64], CHDT)
        b.op("pool", lambda e: e.memset(Hst[:].bitcast(F32), 0.0), writes=[Hst])
        xt = [b.sb(f"rxt{i}", [128, D], F32) for i in range(1)] * 2
        junk = b.sb("rjunk", [128, D], BF16)
        ss = [b.sb(f"rss{i}", [128, 1], F32) for i in range(1)] * 2
        hb = [b.sb(f"rhb{i}", [128, D], BF16) for i in range(1)] * 2
        hT1 = [b.sb(f"rhT{i}", [128, 8, 128], BF16) for i in range(1)] * 2
        PB = b.sb("rPB", [64, 28, TG + 1], F32)
        b.op("pool", lambda e: e.memset(PB[:], 0.0), writes=[PB])
        XL = b.sb("rXL", [64, 28, TG], F32)
        VT2 = [b.sb(f"rVtm{i}", [64, NCH, 512], CHDT) for i in range(2)]
        SXG2 = [b.sb(f"rSXG{i}", [64, 2, TG], F32) for i in range(2)]
        names = ["LW", "AS", "KKN", "BVc", "KP", "RK", "L", "EP", "EM", "BG", "KG"]
        T = {n: b.sb("r" + n, [64, 8, TG], F32) for n in names}
        T["NR"] = T["RK"]
        T["T1"] = T["BG"]
        T["KK"] = T["KG"]
        T["EX"] = T["L"]
        T["BT"] = b.sb("rBTr", [64, 8, TG], CHDT)
        T["KT"] = b.sb("rKTr", [64, 8, TG], CHDT)
        AR = b.sb("rAR", [64, 8, NCH, 2, 64], CHDT)
        BON2 = [b.sb(f"rBON{i}", [64, NCH * 8], F32) for i in range(2)]
        Ytm = b.sb("rYtm", [64, NCH, 8, 64], F32)
        sqv = b.sb("rsqv", [64, NCH, 8, 64], F32)
        st1 = b.sb("rst1", [64, NCH * 8], F32)
        st2 = b.sb("rst2", [64, NCH * 8], F32)
        TM4 = [b.sb(f"rTM{i}", [64, 4, 2, 64], CHDT) for i in range(2)]
        XM4 = [b.sb(f"rXM{i}", [64, 4, 4, 64], CHDT) for i in range(2)]
        AA4 = [[b.sb(f"rAA{u}_{i}", [64, 4, 2, 64], CHDT) for i in range(2)] for u in range(2)]
        PP4 = [[b.sb(f"rPP{u}_{i}", [64, 4, 64], CHDT) for i in range(2)] for u in range(2)]
        Xs8 = b.sb("rXs8", [64, 8, 64], CHDT)
        Us8 = b.sb("rUs8", [64, 8, 64], CHDT)
        Ht8 = b.sb("rHt8", [64, 8, 64], F32)
        OBb = b.sb("rOBb", [64, NCH, 512], BF16)
        obT = [b.sb(f"robT{i}", [128, 4, TG], BF16) for i in range(1)] * 2
        pt = b.ps("rpt", [128, 8, 128], BF16)
        pP = b.ps("rpP", [128, 512], F32)
        pA = b.ps("rpA", [128, 1024], F32)
        pB = b.ps("rpB", [128, 512], F32)
        pC = b.ps("rpC", [128, 512], F32)
        pD = b.ps("rpD", [128, 512], F32)
        pZ = b.ps("rpZ", [128, 512], F32)
        cnt = {}

        def nxt(k, lst):
            cnt[k] = cnt.get(k, 0) + 1
            return lst[cnt[k] % len(lst)]
        bc = lambda v: v[:].unsqueeze(2).to_broadcast([64, 8, TG])
        f2 = lambda t_: t_[:].rearrange("p h t -> p (h t)")
        c16 = lambda t_: t_[:].rearrange("p h (c t) -> p (h c) t", t=64)

        ngr = getattr(self, "nrg_limit", S // TG)
        RR = lambda ap: ap

        def emit_inproj_head(gi):
            i = gi % 2
            self.make_hT(I["x"], gi, xt[i], junk, ss[i], hb[i], pt, hT1[i], self.ident)
            b.op("dve", lambda e: e.tensor_copy(out=PB[:, :, 0:1], in_=PB[:, :, TG:TG + 1]), reads=[PB], writes=[PB])

        def emit_inproj_rounds(gi, rounds):
            i = gi % 2
            for r7 in rounds:
                for j in range(4):
                    fc = r7 * 4 + j
                    for c in range(8):
                        b.op("pe", lambda e: e.matmul(pP[0:64, j * TG:(j + 1) * TG], lhsT=W1[:, c, fc * 64:(fc + 1) * 64], rhs=hT1[i][:, c, :], start=(c == 0), stop=(c == 7)),
                             reads=[W1, hT1[i]], writes=[pP])
                b.op("act", lambda e: e.copy(out=PB[:, r7 * 4:(r7 + 1) * 4, 1:TG + 1], in_=pP[0:64, :].rearrange("p (a t) -> p a t", t=TG)), reads=[pP], writes=[PB])

        emit_inproj_head(0)
        emit_inproj_rounds(0, range(7))
        def prep(gi, hook=None):
            Vtm, BON, SXG = VT2[gi % 2], BON2[gi % 2], SXG2[gi % 2]
            tt("dve", XL[:], PB[:, :, 0:TG], PB[:, :, 1:TG + 1], ALU.subtract, [PB], [XL])
            tt("dve", XL[:], XL[:], mu[:].unsqueeze(2).to_broadcast([64, 28, TG]), ALU.mult, [XL, mu], [XL])
            tt("dve", XL[:], XL[:], PB[:, :, 1:TG + 1], ALU.add, [XL, PB], [XL])
            if gi + 1 < ngr:
                emit_inproj_head(gi + 1)
            for c_ in range(NCH):
                for h in range(8):
                    b.op("pe", lambda e: e.transpose(out=pC[0:64, h * 64:(h + 1) * 64], in_=XL[:, 16 + h, c_ * 64:(c_ + 1) * 64], identity=idf[0:64, 0:64]), reads=[XL, idf], writes=[pC])
                b.op("act", lambda e: e.copy(out=Vtm[:, c_, :], in_=pC[0:64, :]), reads=[pC], writes=[Vtm])
            R_ = XL[:, 0:8, :]
            K_ = XL[:, 8:16, :]
            b.op("act", lambda e: e.activation(out=XL[:, 24, :], in_=XL[:, 24, :], func=AF.Tanh), reads=[XL], writes=[XL])
            b.op("act", lambda e: e.activation(out=SXG[:], in_=XL[:, 26:28, :], func=AF.Sigmoid), reads=[XL], writes=[SXG])
            for (ws_, src, bias_, dst) in [(w2s, 24, w0, "LW"), (a2s, 25, a0, "AS")]:
                for half in range(2):
                    for j in range(4):
                        h = half * 4 + j
                        b.op("pe", lambda e: e.matmul(pP[0:64, j * TG:(j + 1) * TG], lhsT=ws_[:, h * 64:(h + 1) * 64], rhs=XL[:, src, :], start=True, stop=True),
                             reads=[ws_, XL], writes=[pP])
                    for j in range(4):
                        h = half * 4 + j
                        b.op("act", lambda e: e.activation(out=T[dst][:, h, :], in_=pP[0:64, j * TG:(j + 1) * TG], func=AF.Sigmoid, bias=bias_[:, h:h + 1]),
                             reads=[pP, bias_], writes=[T[dst]])
            b.op("dve", lambda e: e.tensor_scalar_mul(out=f2(T["LW"]), in0=f2(T["LW"]), scalar1=-0.6065306597126334), reads=[T["LW"]], writes=[T["LW"]])
            tt("dve", T["KK"][:], K_, bc(k_k), ALU.mult, [XL, k_k], [T["KK"]])
            tt("dve", T["NR"][:], T["KK"][:], T["KK"][:], ALU.mult, [T["KK"]], [T["NR"]])
            for half in range(2):
                b.op("pe", lambda e: e.matmul(pP[0:64, :], lhsT=ones[:], rhs=T["NR"][:, half * 4:(half + 1) * 4, :].rearrange("p h t -> p (h t)"), start=True, stop=True),
                     reads=[ones, T["NR"]], writes=[pP])
                b.op("act", lambda e: e.activation(out=T["KKN"][:, half * 4:(half + 1) * 4, :].rearrange("p h t -> p (h t)"), in_=pP[0:64, :], func=AF.Sqrt),
                     reads=[pP], writes=[T["KKN"]])
            if gi + 1 < ngr:
                emit_inproj_rounds(gi + 1, range(0, 4))
            b.op("dve", lambda e: e.tensor_scalar_max(out=f2(T["KKN"]), in0=f2(T["KKN"]), scalar1=1e-12), reads=[T["KKN"]], writes=[T["KKN"]])
            b.op("dve", lambda e: e.reciprocal(out=f2(T["KKN"]), in_=f2(T["KKN"])), reads=[T["KKN"]], writes=[T["KKN"]])
            tt("dve", T["KKN"][:], T["KKN"][:], T["KK"][:], ALU.mult, [T["KKN"], T["KK"]], [T["KKN"]])
            tt("dve", T["BVc"][:], T["KKN"][:], T["AS"][:], ALU.mult, [T["KKN"], T["AS"]], [T["BVc"]])
            b.op("dve", lambda e: e.tensor_scalar_add(out=f2(T["T1"]), in0=f2(T["AS"]), scalar1=-1.0), reads=[T["AS"]], writes=[T["T1"]])
            tt("dve", T["T1"][:], T["T1"][:], bc(k_a), ALU.mult, [T["T1"], k_a], [T["T1"]])
            b.op("dve", lambda e: e.scalar_tensor_tensor(out=f2(T["KP"]), in0=f2(T["T1"]), scalar=1.0, in1=K_.rearrange("p h t -> p (h t)"), op0=ALU.add, op1=ALU.mult),
                 reads=[T["T1"], XL], writes=[T["KP"]])
            tt("dve", T["RK"][:], R_, T["KP"][:], ALU.mult, [XL, T["KP"]], [T["RK"]])
            tt("dve", T["RK"][:], T["RK"][:], bc(r_k), ALU.mult, [T["RK"], r_k], [T["RK"]])
            for c_ in range(NCH):
                for h in range(8):
                    b.op("pe", lambda e: e.matmul(pD[0:64, c_ * 8 + h:c_ * 8 + h + 1], lhsT=T["RK"][:, h, c_ * 64:(c_ + 1) * 64], rhs=ones[:, 0:1], start=True, stop=True),
                         reads=[T["RK"], ones], writes=[pD])
            b.op("act", lambda e: e.copy(out=BON[:], in_=pD[0:64, 0:NCH * 8]), reads=[pD], writes=[BON])
            if gi + 1 < ngr:
                emit_inproj_rounds(gi + 1, range(4, 7))
            b.op("dve", lambda e: e.tensor_tensor_scan(out=f2(T["L"]), data0=rstm[:], data1=f2(T["LW"]), initial=0.0, op0=ALU.mult, op1=ALU.add),
                 reads=[rstm, T["LW"]], writes=[T["L"]])
            b.op("act", lambda e: e.activation(out=f2(T["EP"]), in_=f2(T["L"]), func=AF.Exp), reads=[T["L"]], writes=[T["EP"]])
            b.op("act", lambda e: e.activation(out=f2(T["EM"]), in_=f2(T["L"]), func=AF.Exp, scale=-1.0), reads=[T["L"]], writes=[T["EM"]])
            tt("dve", T["L"][:], T["L"][:], T["LW"][:], ALU.subtract, [T["L"], T["LW"]], [T["L"]])
            b.op("act", lambda e: e.activation(out=f2(T["EX"]), in_=f2(T["L"]), func=AF.Exp), reads=[T["L"]], writes=[T["EX"]])
            ar0 = AR[:, :, :, 0, :].rearrange("p h c t -> p (h c) t")
            ar1 = AR[:, :, :, 1, :].rearrange("p h c t -> p (h c) t")
            b.op("dve", lambda e: e.scalar_tensor_tensor(out=ar0, in0=c16(T["KKN"]), scalar=-1.0, in1=c16(T["EX"]), op0=ALU.mult, op1=ALU.mult),
                 reads=[T["KKN"], T["EX"]], writes=[AR])
            tt("dve", ar1, R_.rearrange("p h (c t) -> p (h c) t", t=64), c16(T["EP"]), ALU.mult, [XL, T["EP"]], [AR])
            tt("dve", T["BT"][:], T["BVc"][:], T["EM"][:], ALU.mult, [T["BVc"], T["EM"]], [T["BT"]])
            tt("dve", T["KT"][:], T["KP"][:], T["EM"][:], ALU.mult, [T["KP"], T["EM"]], [T["KT"]])
            gC = c16(T["EP"])[:, :, 63:64].to_broadcast([64, 16, 64])
            tt("dve", c16(T["BG"]), c16(T["BT"]), gC, ALU.mult, [T["BT"], T["EP"]], [T["BG"]])
            tt("dve", c16(T["KG"]), c16(T["KT"]), gC, ALU.mult, [T["KT"], T["EP"]], [T["KG"]])

        def chains(gi, hook=None):
            Vtm, BON, SXG = VT2[gi % 2], BON2[gi % 2], SXG2[gi % 2]
            for c_ in range(NCH):
                cs = slice(c_ * 64, (c_ + 1) * 64)
                cur = (gi * NCH + c_) % 2
                U_ = []
                for u in range(2):
                    heads = list(range(u * 4, u * 4 + 4))
                    pBu = pB if u == 0 else pC
                    for j, h in enumerate(heads):
                        b.op("pe", lambda e: e.transpose(out=pZ[0:64, j * 128:j * 128 + 64], in_=T["BG"][:, h, cs], identity=idf[0:64, 0:64]), reads=[T["BG"], idf], writes=[pZ])
                        b.op("pe", lambda e: e.transpose(out=pZ[0:64, j * 128 + 64:(j + 1) * 128], in_=T["KG"][:, h, cs], identity=idf[0:64, 0:64]), reads=[T["KG"], idf], writes=[pZ])
                    tm = TM4[u]
                    b.op("act", lambda e: e.copy(out=tm[:].rearrange("p h a t -> p (h a t)"), in_=pZ[0:64, 0:512]), reads=[pZ], writes=[tm])
                    for j, h in enumerate(heads):
                        arc = AR[:, h, c_, :, :].rearrange("p a t -> p (a t)")
                        b.op("pe", lambda e: e.matmul(pA[0:64, j * 256:j * 256 + 128], lhsT=T["BT"][:, h, cs], rhs=arc, start=True, stop=True), reads=[T["BT"], AR], writes=[pA])
                        b.op("pe", lambda e: e.matmul(pA[0:64, j * 256 + 128:(j + 1) * 256], lhsT=T["KT"][:, h, cs], rhs=arc, start=True, stop=True), reads=[T["KT"], AR], writes=[pA])
                        b.op("pe", lambda e: e.matmul(pBu[0:64, j * 64:(j + 1) * 64], lhsT=AR[:, h, c_, 0, :], rhs=T["BT"][:, h, cs], start=True, stop=True), reads=[T["BT"], AR], writes=[pBu])
                    xm = XM4[u]
                    tt("dve", xm[:].rearrange("p h (a m) t -> p (h a) m t", a=2), pA[0:64, :].rearrange("p (ha m t) -> p ha m t", m=2, t=64),
                       msk[:, None, 0:2, :].to_broadcast([64, 8, 2, 64]), ALU.mult, [pA, msk], [xm])
                    aa = AA4[u][0]
                    b.op("dve", lambda e: e.tensor_copy(out=aa[:, :, 0, :], in_=xm[:, :, 0, :]), reads=[xm], writes=[aa])
                    tt("dve", aa[:, :, 1, :], pBu[0:64, 0:256].rearrange("p (h t) -> p h t", t=64), msk[:, 2:3, :].to_broadcast([64, 4, 64]), ALU.mult, [pBu, msk], [aa])
                    P_ = PP4[u][0]
                    tt("dve", P_[:], xm[:, :, 0, :], idf[0:64, None, 0:64].to_broadcast([64, 4, 64]), ALU.add, [xm, idf], [P_])
                    U_.append(dict(tm=tm, xm=xm, aa=aa, P=P_, pB=pBu, pD=(pD if u == 0 else pP), k=0))
                for step in range(5):
                    if hook is not None and c_ == 0:
                        next(hook, None)
                        next(hook, None)
                    for u_ in U_:
                        aa, pDu = u_["aa"], u_["pD"]
                        for j in range(4):
                            b.op("pe", lambda e: e.matmul(pDu[0:64, j * 128:j * 128 + 64], lhsT=RR(aa[:, j, 1, :]), rhs=RR(aa[:, j, 0, :]), start=True, stop=True), reads=[aa], writes=[pDu])
                            b.op("pe", lambda e: e.matmul(pDu[0:64, j * 128 + 64:(j + 1) * 128], lhsT=RR(aa[:, j, 0, :]), rhs=RR(aa[:, j, 1, :]), start=True, stop=True), reads=[aa], writes=[pDu])
                    for ui, u_ in enumerate(U_):
                        u_["k"] += 1
                        aa2 = AA4[ui][u_["k"] % 2]
                        b.op("act", lambda e: e.copy(out=aa2[:].rearrange("p h a t -> p (h a t)"), in_=u_["pD"][0:64, :]), reads=[u_["pD"]], writes=[aa2])
                        u_["aa"] = aa2
                    for u_ in U_:
                        for j in range(4):
                            b.op("pe", lambda e: e.matmul(u_["pB"][0:64, 256 + j * 64:256 + (j + 1) * 64], lhsT=RR(u_["aa"][:, j, 1, :]), rhs=RR(u_["P"][:, j, :]), start=True, stop=True),
                                 reads=[u_["aa"], u_["P"]], writes=[u_["pB"]])
                    for ui, u_ in enumerate(U_):
                        P2 = PP4[ui][u_["k"] % 2]
                        tt("dve", P2[:], u_["pB"][0:64, 256:512].rearrange("p (h t) -> p h t", t=64), u_["P"][:], ALU.add, [u_["pB"], u_["P"]], [P2])
                        u_["P"] = P2
                if hook is not None and c_ == 0:
                    for _ in hook:
                        pass
                for h in range(8):
                    u_, j = U_[h // 4], h % 4
                    o = slice(h * 64, (h + 1) * 64)
                    b.op("pe", lambda e: e.matmul(pA[0:64, o], lhsT=u_["xm"][:, j, 2, :], rhs=Vtm[:, c_, o], start=True, stop=False), reads=[u_["xm"], Vtm], writes=[pA])
                    b.op("pe", lambda e: e.matmul(pA[0:64, o], lhsT=AR[:, h, c_, 0, :], rhs=Hst[:, cur, h, :], start=False, stop=True), reads=[AR, Hst], writes=[pA])
                b.op("act", lambda e: e.copy(out=Xs8[:].rearrange("p h t -> p (h t)"), in_=pA[0:64, 0:512]), reads=[pA], writes=[Xs8])
                for h in range(8):
                    u_, j = U_[h // 4], h % 4
                    b.op("pe", lambda e: e.matmul(pA[0:64, 512 + h * 64:512 + (h + 1) * 64], lhsT=u_["P"][:, j, :], rhs=Xs8[:, h, :], start=True, stop=True), reads=[u_["P"], Xs8], writes=[pA])
                b.op("act", lambda e: e.copy(out=Us8[:].rearrange("p h t -> p (h t)"), in_=pA[0:64, 512:1024]), reads=[pA], writes=[Us8])
                for h in range(8):
                    u_, j = U_[h // 4], h % 4
                    o = slice(h * 64, (h + 1) * 64)
                    b.op("pe", lambda e: e.matmul(pD[0:64, o], lhsT=u_["tm"][:, j, 0, :], rhs=Us8[:, h, :], start=True, stop=False), reads=[u_["tm"], Us8], writes=[pD])
                    b.op("pe", lambda e: e.matmul(pD[0:64, o], lhsT=u_["tm"][:, j, 1, :], rhs=Vtm[:, c_, o], start=False, stop=True), reads=[u_["tm"], Vtm], writes=[pD])
                tt("dve", Ht8[:], Hst[:, cur, :, :], T["EP"][:, :, c_ * 64 + 63:c_ * 64 + 64].to_broadcast([64, 8, 64]), ALU.mult, [Hst, T["EP"]], [Ht8])
                tt("dve", Hst[:, 1 - cur, :, :], pD[0:64, :].rearrange("p (h t) -> p h t", t=64), Ht8[:], ALU.add, [pD, Ht8], [Hst])
                for h in range(8):
                    u_, j = U_[h // 4], h % 4
                    o = slice(h * 64, (h + 1) * 64)
                    b.op("pe", lambda e: e.matmul(pZ[0:64, o], lhsT=AR[:, h, c_, 1, :], rhs=Hst[:, cur, h, :], start=True, stop=False), reads=[AR, Hst], writes=[pZ])
                    b.op("pe", lambda e: e.matmul(pZ[0:64, o], lhsT=u_["xm"][:, j, 1, :], rhs=Us8[:, h, :], start=False, stop=False), reads=[u_["xm"], Us8], writes=[pZ])
                    b.op("pe", lambda e: e.matmul(pZ[0:64, o], lhsT=u_["xm"][:, j, 3, :], rhs=Vtm[:, c_, o], start=False, stop=True), reads=[u_["xm"], Vtm], writes=[pZ])
                b.op("act", lambda e: e.copy(out=Ytm[:, c_, :, :].rearrange("p h t -> p (h t)"), in_=pZ[0:64, :]), reads=[pZ], writes=[Ytm])

        def post(gi):
            q0 = gi * TG
            Vtm, BON, SXG = VT2[gi % 2], BON2[gi % 2], SXG2[gi % 2]
            Y3 = Ytm[:].rearrange("p c h i -> p (c h) i")
            S3 = sqv[:].rearrange("p c h i -> p (c h) i")
            b.op("dve", lambda e: e.tensor_reduce(out=st1[:], in_=Y3, axis=AX.X, op=ALU.add), reads=[Ytm], writes=[st1])
            b.op("dve", lambda e: e.tensor_scalar_mul(out=st1[:], in0=st1[:], scalar1=1.0 / 64), reads=[st1], writes=[st1])
            yield
            tt("dve", Y3, Y3, st1[:].unsqueeze(2).to_broadcast([64, NCH * 8, 64]), ALU.subtract, [Ytm, st1], [Ytm])
            tt("dve", S3, Y3, Y3, ALU.mult, [Ytm], [sqv])
            yield
            b.op("dve", lambda e: e.tensor_reduce(out=st2[:], in_=S3, axis=AX.X, op=ALU.add), reads=[sqv], writes=[st2])
            b.op("act", lambda e: e.activation(out=st2[:], in_=st2[:], func=AF.Sqrt, scale=1.0 / 64, bias=64e-5), reads=[st2], writes=[st2])
            yield
            b.op("dve", lambda e: e.reciprocal(out=st2[:], in_=st2[:]), reads=[st2], writes=[st2])
            tt("dve", Y3, Y3, st2[:].unsqueeze(2).to_broadcast([64, NCH * 8, 64]), ALU.mult, [Ytm, st2], [Ytm])
            yield
            lg = lng[:].rearrange("p (h i) -> p h i", i=64)[:, None, :, :].to_broadcast([64, NCH, 8, 64])
            lb = lnb[:].rearrange("p (h i) -> p h i", i=64)[:, None, :, :].to_broadcast([64, NCH, 8, 64])
            tt("dve", Ytm[:], Ytm[:], lg, ALU.mult, [Ytm, lng], [Ytm])
            tt("dve", Ytm[:], Ytm[:], lb, ALU.add, [Ytm, lnb], [Ytm])
            yield
            V3 = Vtm[:].rearrange("p c (h i) -> p (c h) i", i=64)
            tt("dve", S3, V3, BON[:].unsqueeze(2).to_broadcast([64, NCH * 8, 64]), ALU.mult, [Vtm, BON], [sqv])
            tt("dve", Y3, Y3, S3, ALU.add, [Ytm, sqv], [Ytm])
            yield
            for c_ in range(NCH):
                for two in range(2):
                    b.op("pe", lambda e: e.matmul(pP[0:64, :], lhsT=SXG[:, two, c_ * 64:(c_ + 1) * 64], rhs=g2s[:, two, :], start=(two == 0), stop=(two == 1)),
                         reads=[SXG, g2s], writes=[pP])
                tt("dve", OBb[:, c_, :], Ytm[:, c_, :, :].rearrange("p h i -> p (h i)"), pP[0:64, :], ALU.mult, [Ytm, pP], [OBb])
                for k4 in range(4):
                    b.op("pe", lambda e: e.transpose(out=pt[:, k4, c_ * 64:(c_ + 1) * 64], in_=OBb[:, c_, k4 * 128:(k4 + 1) * 128], identity=self.ident[0:64, 0:64]),
                         reads=[OBb, self.ident], writes=[pt])
                yield
            ot = obT[gi % 2]
            b.op("act", lambda e: e.copy(out=ot[:], in_=pt[:, 0:4, :]), reads=[pt], writes=[ot])
            b.dma("pool", self.obT_d[:, :, q0:q0 + TG].rearrange("c p t -> p c t"), ot[:], reads=[ot], writes=[self.obT_d])

        prep(0)
        for gi in range(ngr):
            pg = post(gi - 1) if gi > 0 else None
            chains(gi, hook=pg)
            if pg is not None:
                for _ in pg:
                    pass
            if gi + 1 < ngr:
                prep(gi + 1)
        for _ in post(ngr - 1):
            pass
        if "rwkv" in self.debug:
            d = self.dbg_out("obT", [4, 128, S], BF16)
            b.dma("pool", d, self.obT_d[:], reads=[self.obT_d])


Prog.phase_rwkv3 = _phase_rwkv3


def _phase_ffn2(self):
    b = self.b
    I = self.inp
    TG = 256
    NFT = 44
    with b.scope():
        gf = self.load_gain("gf", I["ffn_norm_g"][0])
        stage = [b.sb(f"fst{i}", [128, 1024], F32) for i in range(2)]
        wu = b.sb("wu", [128, 8, 2 * DFF], BF16)
        for n in range(8):
            for c in range(8):
                st = stage[c % 2]
                b.dma("sp", st[:, 0:704], I["w_up"][0][c * 128:(c + 1) * 128, n * 704:(n + 1) * 704], writes=[st])
                b.op("act", lambda e: e.activation(out=wu[:, c, n * 704:(n + 1) * 704], in_=st[:, 0:704], func=AF.Copy, scale=gf[:, c:c + 1]),
                     reads=[st, gf], writes=[wu])
        wd = b.sb("wd", [128, 22, D], BF16)
        self.load_weight(wd, I["w_down"][0], D, kch=22, stage=stage, eng="dve")
        cw = b.sb("cw", [128, 3, NFT], F32)
        for j in range(3):
            b.dma("sp", cw[:, j, :], I["conv_w"][0][j].rearrange("(c p) -> p c", p=128), writes=[cw], allow_slow_non_contiguous=True)
        cbias = self.load_gain("cbias", I["conv_b"][0], kch=NFT)
        xt = [b.sb(f"fxt{i}", [128, D], F32) for i in range(4)]
        junk = b.sb("fjunk", [128, D], BF16)
        ss = [b.sb(f"fss{i}", [128, 1], F32) for i in range(2)]
        hb = [b.sb(f"fhb{i}", [128, D], BF16) for i in range(2)]
        hTg2 = [b.sb(f"fhTg{i}", [128, 8, TG + 2], BF16) for i in range(2)]
        for i_ in range(2):
            b.op("pool", lambda e: e.memset(hTg2[i_][:], 0.0), writes=[hTg2[i_]])
        cv = [b.sb(f"cv{i}", [128, TG], F32) for i in range(5)]
        sgl = [b.sb(f"sgl{i}", [128, TG], BF16) for i in range(4)]
        actT = b.sb("actT", [128, 22, TG], BF16)
        val = b.sb("fval", [128, 22, TG], BF16)
        pt = b.ps("fpt", [128, 8, 128], BF16)
        pu = [b.ps(f"fpu{i}", [128, 512], F32) for i in range(4)]
        pd = [b.ps(f"fpd{i}", [128, 512], F32) for i in range(2)]
        ng = getattr(self, "nt_limit", NT) * 128 // TG
        def stageH(gi):
            hTg = hTg2[gi % 2]
            if gi > 0:
                prev = hTg2[(gi - 1) % 2]
                b.op("pool", lambda e: e.tensor_copy(out=hTg[:, :, 0:2], in_=prev[:, :, TG:TG + 2]), reads=[prev], writes=[hTg])
            for s_ in range(TG // 128):
                t = gi * (TG // 128) + s_
                xi = (gi % 2) * 2 + s_
                self.make_hT(self.x1_d, t, xt[xi], junk, ss[s_], hb[s_], pt, hTg, self.ident, hT_ap=hTg[:, :, 2 + s_ * 128:2 + (s_ + 1) * 128])

        stageH(0)
        for gi in range(ng):
            hTg = hTg2[gi % 2]
            if gi + 1 < ng:
                stageH(gi + 1)
            pending = []
            for ft in range(NFT):
                p = pu[ft % 4]
                c_ = cv[ft % 5]
                for c in range(8):
                    b.op("pe", lambda e: e.matmul(p[:, 0:TG + 2], lhsT=wu[:, c, ft * 128:(ft + 1) * 128], rhs=hTg[:, c, :], start=(c == 0), stop=(c == 7)),
                         reads=[wu, hTg], writes=[p])
                b.op("act", lambda e: e.activation(out=c_[:], in_=p[:, 0:TG], func=AF.Identity, scale=cw[:, 0, ft:ft + 1], bias=cbias[:, ft:ft + 1]),
                     reads=[p, cw, cbias], writes=[c_])
                b.op("dve", lambda e: e.scalar_tensor_tensor(out=c_[:], in0=p[:, 1:TG + 1], scalar=cw[:, 1, ft:ft + 1], in1=c_[:], op0=ALU.mult, op1=ALU.add),
                     reads=[p, cw, c_], writes=[c_])
                if ft < 22:
                    b.op("dve", lambda e: e.scalar_tensor_tensor(out=val[:, ft, :], in0=p[:, 2:TG + 2], scalar=cw[:, 2, ft:ft + 1], in1=c_[:], op0=ALU.mult, op1=ALU.add),
                         reads=[p, cw, c_], writes=[val])
                else:
                    b.op("dve", lambda e: e.scalar_tensor_tensor(out=c_[:], in0=p[:, 2:TG + 2], scalar=cw[:, 2, ft:ft + 1], in1=c_[:], op0=ALU.mult, op1=ALU.add),
                         reads=[p, cw, c_], writes=[c_])
                    pending.append((ft, c_))
                if len(pending) > (1 if ft < NFT - 1 else 0):
                    while len(pending) > (1 if ft < NFT - 1 else 0):
                        f_, cc = pending.pop(0)
                        sg_ = sgl[f_ % 4]
                        b.op("act", lambda e: e.activation(out=sg_[:], in_=cc[:], func=AF.Silu), reads=[cc], writes=[sg_])
                        eng_ = "pool" if f_ % 2 == 0 else "dve"
                        b.op(eng_, lambda e: e.tensor_tensor(out=actT[:, f_ - 22, :], in0=sg_[:], in1=val[:, f_ - 22, :], op=ALU.mult),
                             reads=[sg_, val], writes=[actT])
            for s_ in range(TG // 128):
                t = gi * (TG // 128) + s_
                for n in range(2):
                    for f in range(22):
                        b.op("pe", lambda e: e.matmul(pd[n][:, :], lhsT=actT[:, f, s_ * 128:(s_ + 1) * 128], rhs=wd[:, f, n * 512:(n + 1) * 512], start=(f == 0), stop=(f == 21)),
                             reads=[actT, wd], writes=[pd[n]])
                    xo = xt[(gi % 2) * 2 + s_]
                    b.op("dve", lambda e: e.tensor_tensor(out=xo[:, n * 512:(n + 1) * 512], in0=pd[n][:, :], in1=xo[:, n * 512:(n + 1) * 512], op=ALU.add),
                         reads=[pd[n], xo], writes=[xo])
                b.dma("pool", self.out[t * 128:(t + 1) * 128, :], xt[(gi % 2) * 2 + s_][:], reads=[xt[(gi % 2) * 2 + s_]])


Prog.phase_ffn2 = _phase_ffn2
```

```python
import contextlib
import numpy as np
import ml_dtypes
import concourse.bass as bass
import concourse.mybir as mybir
from concourse.bass_utils import run_bass_kernel_spmd

F32 = mybir.dt.float32
BF16 = mybir.dt.bfloat16
AF = mybir.ActivationFunctionType
ALU = mybir.AluOpType
AX = mybir.AxisListType

S = 4096
D = 1024
NT = S // 128
IN_WIDTH = 5144
RW0 = 1304
GA0 = 3096
GB0 = 4120
DFF = 2816
RMS_EPS = 1e-6


class Buf:
    def __init__(self, t, name):
        self.t = t
        self.name = name
        self.w = None
        self.r = {}
        self.psum = False

    def __getitem__(self, idx):
        return self.t[idx]


class Builder:
    SEM_ROLL = 30000

    def __init__(self, nc):
        self.nc = nc
        self.stack = contextlib.ExitStack()
        self.root = self.stack
        self.eng = {"pe": nc.tensor, "act": nc.scalar, "dve": nc.vector,
                    "pool": nc.gpsimd, "sp": nc.sync}
        self.sem = {}
        self.cnt = {}
        self.seen = {e: {} for e in self.eng}
        self.nsem = 0
        self.lanes = {}
        self.lane_rr = {}
        self.last_tok = {}
        for e in self.eng:
            self._roll(e)

    def newsem(self, name):
        self.nsem += 1
        return self.root.enter_context(self.nc.semaphore(f"{name}_{self.nsem}"))

    def sb(self, name, shape, dt=F32):
        self.nsem += 1
        name = f"sb{self.nsem}_{name}"
        return Buf(self.stack.enter_context(self.nc.sbuf_tensor(name, list(shape), dt)), name)

    def ps(self, name, shape, dt=F32):
        self.nsem += 1
        name = f"ps{self.nsem}_{name}"
        bf = Buf(self.stack.enter_context(self.nc.psum_tensor(name, list(shape), dt)), name)
        bf.psum = True
        return bf

    def dram(self, name, shape, dt=F32, kind="Internal"):
        return Buf(self.nc.dram_tensor(name, list(shape), dt, kind=kind), name)

    def _roll(self, e):
        self.sem[e] = self.newsem("s" + e)
        self.cnt[e] = 0

    def _wait(self, e, tok):
        sem, val = tok
        k = id(sem)
        if self.seen[e].get(k, 0) < val:
            self.eng[e].wait_ge(sem, val)
            self.seen[e][k] = val

    def _deps(self, e, reads, writes):
        for b in reads:
            if b.w is not None:
                we, tok = b.w
                self._wait(e, tok)
            if b.psum:
                for re_, tok in b.r.items():
                    if re_ != e:
                        self._wait(e, tok)
        for b in writes:
            if b.w is not None:
                we, tok = b.w
                if we != e:
                    self._wait(e, tok)
            for re_, tok in b.r.items():
                if re_ != e:
                    self._wait(e, tok)

    def op(self, e, fn, reads=(), writes=()):
        if self.cnt[e] >= self.SEM_ROLL:
            self._roll(e)
        self._deps(e, reads, writes)
        ins = fn(self.eng[e])
        self.cnt[e] += 1
        tok = (self.sem[e], self.cnt[e])
        ins.then_inc(self.sem[e], 1)
        self.last_tok[e] = tok
        for b in reads:
            b.r[e] = tok
        for b in writes:
            b.w = (e, tok)
            b.r = {}
        return tok

    def dma(self, q, out, in_, reads=(), writes=(), nlanes=6, **kw):
        if q not in self.lanes:
            self.lanes[q] = [[self.newsem("l" + q), 0] for _ in range(nlanes)]
            self.lane_rr[q] = 0
        li = self.lane_rr[q]
        self.lane_rr[q] = (li + 1) % len(self.lanes[q])
        lane = self.lanes[q][li]
        if lane[1] >= 1800:
            self._wait(q, (lane[0], 16 * lane[1]))
            lane[0] = self.newsem("l" + q)
            lane[1] = 0
        if lane[1] > 0:
            self._wait(q, (lane[0], 16 * lane[1]))
        self._deps_dma(q, reads, writes)
        ins = self.eng[q].dma_start(out=out, in_=in_, **kw)
        lane[1] += 1
        tok = (lane[0], 16 * lane[1])
        ins.then_inc(lane[0], 16)
        key = "dma_" + q + str(li)
        for b in reads:
            b.r[key] = tok
        for b in writes:
            b.w = (key, tok)
            b.r = {}
        return tok

    def _deps_dma(self, q, reads, writes):
        for b in reads:
            if b.w is not None:
                self._wait(q, b.w[1])
        for b in writes:
            if b.w is not None:
                self._wait(q, b.w[1])
            for re_, tok in b.r.items():
                self._wait(q, tok)

    def barrier(self):
        toks = list(self.last_tok.values())
        for q, lanes in self.lanes.items():
            for lane in lanes:
                if lane[1] > 0:
                    toks.append((lane[0], 16 * lane[1]))
        for e in self.eng:
            for tok in toks:
                self._wait(e, tok)

    def wait_all_on(self, e):
        toks = list(self.last_tok.values())
        for q, lanes in self.lanes.items():
            for lane in lanes:
                if lane[1] > 0:
                    toks.append((lane[0], 16 * lane[1]))
        for tok in toks:
            self._wait(e, tok)

    @contextlib.contextmanager
    def scope(self):
        old = self.stack
        self.stack = contextlib.ExitStack()
        try:
            yield
            self.barrier()
        finally:
            self.stack.close()
            self.stack = old

    def close(self):
        self.stack.close()


NEG = -30000.0


def _bucket(dist):
    n = np.maximum(dist, 0)
    ratio = np.log(np.maximum(n, 1).astype(np.float32) / np.float32(16.0)) / np.float32(np.log(8.0))
    large = np.minimum(16 + (ratio * 16).astype(np.int32), 31)
    return np.where(n < 16, n, large)


def host_consts(rel_bias):
    rel = np.asarray(rel_bias, np.float32)
    c = {}
    c["ident"] = np.eye(128, dtype=np.float32).astype(ml_dtypes.bfloat16)
    c["identf"] = np.eye(128, dtype=np.float32)
    kp = np.arange(128)[:, None]
    cc = np.arange(640)[None, :]
    dist = cc - kp
    bt = rel[_bucket(dist)]
    tw = np.where(((dist >= 0) & (dist < 512))[..., None], bt, np.float32(NEG))
    ts = np.where((dist >= 0)[..., None], bt, np.float32(NEG))
    c["tw"] = np.ascontiguousarray(tw.transpose(0, 2, 1)).astype(np.float32)
    c["ts"] = np.ascontiguousarray(ts.transpose(0, 2, 1)).astype(np.float32)
    cidx = np.arange(256)[:, None]
    qidx = np.arange(S)[None, :]
    dc = qidx - 16 * cidx - 31
    bcg = rel[_bucket(dc)]
    ok = (dc >= 0) & (cidx < 255)
    bc = np.where(ok[..., None], bcg, np.float32(NEG))
    c["biasc"] = np.ascontiguousarray(bc.transpose(2, 0, 1)).reshape(8, 2, 128, S).astype(np.float32)
    A = np.zeros((256, 64), np.float32)
    Wt = (1, 2, 2, 2, 1)
    for ci in range(255):
        for j in range(64):
            o = ci + 1 - 4 * j
            if 0 <= o <= 4:
                A[ci, j] = Wt[o]
    c["amat"] = A.reshape(2, 128, 64)
    E = np.zeros((64, S), np.float32)
    E[np.arange(S) // 64, np.arange(S)] = 1.0
    c["emat"] = E.astype(ml_dtypes.bfloat16)
    qp = np.arange(128)[:, None, None]
    qt = np.arange(32)[None, :, None]
    j = np.arange(64)[None, None, :]
    cur = (128 * qt + qp) // 64
    cand = (j >= 1) & (j <= cur - 2)
    c["candneg"] = np.where(cand, 0.0, -1e9).astype(np.float32)
    c["fz"] = ((j == 0) | (j == cur) | (j == cur - 1)).astype(np.float32)
    tri = np.triu(np.ones((64, 64), np.float32))
    c["rwmask"] = np.ascontiguousarray(np.stack([np.triu(np.ones((64, 64), np.float32), 1), tri, np.tril(np.ones((64, 64), np.float32), -1)], axis=1))
    rr = np.ones((64, 1024), np.float32)
    rr[:, ::64] = 0.0
    c["rwreset"] = rr
    c["b31"] = np.ascontiguousarray(np.broadcast_to(rel[31][None, :], (128, 8))).astype(np.float32)
    return c


CONST_SPECS = {
    "ident": ([128, 128], BF16), "identf": ([128, 128], F32),
    "tw": ([128, 8, 640], F32), "ts": ([128, 8, 640], F32),
    "biasc": ([8, 2, 128, S], F32), "amat": ([2, 128, 64], F32),
    "emat": ([64, S], BF16), "candneg": ([128, 32, 64], F32), "fz": ([128, 32, 64], F32),
    "b31": ([128, 8], F32), "rwmask": ([64, 3, 64], F32), "rwreset": ([64, 1024], F32),
}

W_SPECS = {
    "x": [S, D], "attn_norm_g": [1, D], "w_in": [1, D, IN_WIDTH], "q_norm_g": [1, 64], "k_norm_g": [1, 64],
    "cmp_pe_k": [1, 32, 64], "cmp_w1_k": [1, 2048, 256], "cmp_w2_k": [1, 256, 64],
    "cmp_pe_v": [1, 32, 64], "cmp_w1_v": [1, 2048, 256], "cmp_w2_v": [1, 256, 64],
    "rwkv_mu": [1, 1792], "rwkv_w0": [1, 512], "rwkv_w2": [1, 64, 512], "rwkv_a0": [1, 512],
    "rwkv_a2": [1, 64, 512], "rwkv_g2": [1, 128, 512], "rwkv_k_k": [1, 512], "rwkv_k_a": [1, 512],
    "rwkv_r_k": [1, 8, 64], "rwkv_ln_g": [1, 512], "rwkv_ln_b": [1, 512],
    "w_proj_a": [1, 512, D], "w_proj_b": [1, 512, D], "w_out": [1, D, D], "ffn_norm_g": [1, D],
    "w_up": [1, D, 2 * DFF], "conv_w": [1, 3, 2 * DFF], "conv_b": [1, 2 * DFF], "w_down": [1, DFF, D],
}


class Prog:
    def __init__(self, debug=()):
        self.debug = set(debug)
        nc = bass.Bass("TRN2", target_bir_lowering=False)
        self.nc = nc
        self.inp = {}
        for k, shp in W_SPECS.items():
            self.inp[k] = nc.dram_tensor(k, list(shp), F32, kind="ExternalInput").ap()
        for k, (shp, dt) in CONST_SPECS.items():
            self.inp[k] = nc.dram_tensor(k, list(shp), dt, kind="ExternalInput").ap()
        self.out = nc.dram_tensor("out", [S, D], F32, kind="ExternalOutput").ap()
        self.dbg = {}
        self.b = Builder(nc)

    def dbg_out(self, name, shape, dt=F32):
        t = self.nc.dram_tensor("dbg_" + name, list(shape), dt, kind="ExternalOutput").ap()
        self.dbg[name] = t
        return t

    def load_weight(self, dst, src, ncols, gvec=None, kch=8, stage=None, eng="act"):
        b = self.b
        for c in range(kch):
            st = stage[c % len(stage)]
            b.dma("sp", st[:, :ncols], src[c * 128:(c + 1) * 128, :], writes=[st])
            if gvec is not None:
                b.op(eng, lambda e: e.activation(out=dst[:, c, :], in_=st[:, :ncols], func=AF.Copy, scale=gvec[:, c:c + 1])
                     if eng == "act" else e.tensor_scalar_mul(out=dst[:, c, :], in0=st[:, :ncols], scalar1=gvec[:, c:c + 1]),
                     reads=[st, gvec], writes=[dst])
            else:
                b.op(eng, lambda e: e.copy(out=dst[:, c, :], in_=st[:, :ncols]) if eng == "act"
                     else e.tensor_copy(out=dst[:, c, :], in_=st[:, :ncols]), reads=[st], writes=[dst])

    def load_gain(self, name, src_vec, kch=8):
        b = self.b
        g = b.sb(name, [128, kch], F32)
        b.dma("sp", g[:], src_vec.rearrange("(c p) -> p c", p=128), writes=[g], allow_slow_non_contiguous=True)
        return g

    def bcast_row(self, name, src_row, n):
        b = self.b
        t = b.sb(name, [128, n], F32)
        b.dma("sp", t[:], src_row.partition_broadcast(128), writes=[t])
        return t

    def make_hT(self, x_ap, t, xt, junk, ss, hb, pt, hT, ident, hT_ap=None):
        b = self.b
        b.dma("sp", xt[:], x_ap[t * 128:(t + 1) * 128, :], writes=[xt])
        b.op("act", lambda e: e.activation(out=junk[:], in_=xt[:], func=AF.Square, accum_out=ss[:]), reads=[xt], writes=[junk, ss])
        b.op("act", lambda e: e.activation(out=ss[:], in_=ss[:], func=AF.Sqrt, scale=1.0 / D, bias=RMS_EPS), reads=[ss], writes=[ss])
        b.op("dve", lambda e: e.reciprocal(out=ss[:], in_=ss[:]), reads=[ss], writes=[ss])
        b.op("dve", lambda e: e.tensor_scalar_mul(out=hb[:], in0=xt[:], scalar1=ss[:]), reads=[xt, ss], writes=[hb])
        for c in range(8):
            b.op("pe", lambda e: e.transpose(out=pt[:, c, :], in_=hb[:, c * 128:(c + 1) * 128], identity=ident[:]),
                 reads=[hb, ident], writes=[pt])
        b.op("act", lambda e: e.copy(out=(hT[:] if hT_ap is None else hT_ap), in_=pt[:]), reads=[pt], writes=[hT])

    def alloc_root(self):
        b = self.b
        I = self.inp
        self.ident = b.sb("ident", [128, 128], BF16)
        b.dma("sp", self.ident[:], I["ident"], writes=[self.ident])
        self.identf = b.sb("identf", [128, 128], F32)
        b.dma("sp", self.identf[:], I["identf"], writes=[self.identf])

    def alloc_persistent(self):
        b = self.b
        I = self.inp
        if not hasattr(self, "ident"):
            self.alloc_root()
        self.ksE = b.sb("ksE", [128, 2, S], BF16)
        self.kwT = b.sb("kwT", [64, 2, S], BF16)
        self.vaug_s = b.sb("vaug_s", [128, NT, 2, 65], BF16)
        self.vaug_w = b.sb("vaug_w", [128, NT, 2, 65], BF16)
        self.gts = b.sb("gts", [128, NT, 24], F32)
        self.kcT = b.sb("kcT", [64, 2, 256], BF16)
        self.vcA = b.sb("vcA", [128, 2, 2, 130], mybir.dt.float32r)
        self.qT_d = b.dram("qT_d", [8, 64, S], BF16)
        self.oaT_d = b.dram("oaT_d", [4, 128, S], BF16)
        self.obT_d = b.dram("obT_d", [4, 128, S], BF16)
        for g in range(2):
            b.dma("sp", self.ksE[64:128, g, :], I["emat"], writes=[self.ksE])
        b.op("pool", lambda e: e.memset(self.vaug_s[:, :, :, 64:65], 1.0), writes=[self.vaug_s])
        b.op("pool", lambda e: e.memset(self.vaug_w[:, :, :, 64:65], 1.0), writes=[self.vaug_w])
        onesf = b.sb("onesf", [128, 4], F32)
        b.op("pool", lambda e: e.memset(onesf[:], 1.0), writes=[onesf])
        zf = b.sb("zerof", [128, 4 * 130], F32)
        b.op("pool", lambda e: e.memset(zf[:], 0.0), writes=[zf])
        b.op("dve", lambda e: e.tensor_copy(out=self.vcA[:].rearrange("p a c n -> p (a c n)"), in_=zf[:]), reads=[zf], writes=[self.vcA])
        b.op("dve", lambda e: e.tensor_copy(out=self.vcA[:, :, :, 64:65], in_=onesf[:].rearrange("p (a c) -> p a c", a=2).unsqueeze(3)), reads=[onesf], writes=[self.vcA])
        amst = b.sb("amst", [128, 2, 64], F32)
        for g in range(2):
            for ct in range(2):
                b.dma("sp", amst[:, ct, :], I["amat"][ct], writes=[amst])
                b.op("dve", lambda e: e.tensor_copy(out=self.vcA[:, g, ct, 65:129], in_=amst[:, ct, :]), reads=[amst], writes=[self.vcA])

    def phase_nsa_proj(self):
        b = self.b
        I = self.inp
        with b.scope():
            gat = self.load_gain("gat", I["attn_norm_g"][0])
            wn = b.sb("wn", [128, 8, RW0], BF16)
            stage = [b.sb(f"wst{i}", [128, RW0], F32) for i in range(2)]
            self.load_weight(wn, I["w_in"][0][:, 0:RW0], RW0, gvec=gat, stage=stage)
            gq = self.bcast_row("gq", I["q_norm_g"][0], 64)
            gk = self.bcast_row("gk", I["k_norm_g"][0], 64)
            gq_rep = b.sb("gq_rep", [128, 8, 64], F32)
            gk_rep = b.sb("gk_rep", [128, 2, 64], F32)
            b.op("act", lambda e: e.activation(out=gq_rep[:], in_=gq[:, None, :].to_broadcast([128, 8, 64]), func=AF.Copy, scale=0.125),
                 reads=[gq], writes=[gq_rep])
            b.op("act", lambda e: e.activation(out=gk_rep[:], in_=gk[:, None, :].to_broadcast([128, 2, 64]), func=AF.Copy, scale=1.0),
                 reads=[gk], writes=[gk_rep])
            if getattr(self, 'stop_at', 99) <= 0:
                return
            kcdup = b.sb("kcdup", [128, 2, S + 1], BF16)
            vcdup = b.sb("vcdup", [128, 2, S + 1], BF16)
            xt = [b.sb(f"xt{i}", [128, D], F32) for i in range(2)]
            junk = b.sb("junk", [128, D], BF16)
            ss = [b.sb(f"ss{i}", [128, 1], F32) for i in range(2)]
            hb = [b.sb(f"hb{i}", [128, D], BF16) for i in range(2)]
            hT = [b.sb(f"hT{i}", [128, 8, 128], BF16) for i in range(2)]
            sq_ = [b.sb(f"sq{i}", [128, 12, 64], F32) for i in range(2)]
            ssq_ = [b.sb(f"ssq{i}", [128, 12], F32) for i in range(2)]
            tmpq_ = [b.sb(f"tmpq{i}", [128, 8, 64], F32) for i in range(2)]
            tmpk_ = [b.sb(f"tmpk{i}", [128, 4, 64], F32) for i in range(2)]
            qb_ = [b.sb(f"qb{i}", [128, 512], BF16) for i in range(2)]
            kb_ = [b.sb(f"kb{i}", [128, 4, 64], BF16) for i in range(2)]
            cb_ = [b.sb(f"cb{i}", [128, 4, 2, 64], BF16) for i in range(2)]
            qst = [b.sb(f"qst{i}", [64, 8, 128], BF16) for i in range(2)]
            pt = b.ps("pt", [128, 8, 128], BF16)
            pm = [b.ps(f"pm{i}", [128, 512], F32) for i in range(3)]
            ptq_ = [b.ps(f"ptq{i}", [128, 8, 128], BF16) for i in range(2)]
            ptk_ = [b.ps(f"ptk{i}", [128, 8, 128], BF16) for i in range(2)]
            colgroups = [(0, 512), (512, 1024), (1024, RW0)]
            pmS = [[b.sb(f"pmS{i}_{n}", [128, 512], F32) for n in range(3)] for i in range(2)]
            ntl = getattr(self, 'nt_limit', NT)

            def stageA(t):
                    i = t % 2
                    self.make_hT(I["x"], t, xt[i], junk, ss[i], hb[i], pt, hT[i], self.ident)
                    sq, ssq, tmpq, tmpk, qb, kb, cb, ptq, ptk = sq_[i], ssq_[i], tmpq_[i], tmpk_[i], qb_[i], kb_[i], cb_[i], ptq_[i], ptk_[i]
                    for n, (c0, c1) in enumerate(colgroups):
                        for c in range(8):
                            b.op("pe", lambda e: e.matmul(pm[n][:, :c1 - c0], lhsT=hT[i][:, c, :], rhs=wn[:, c, c0:c1],
                                                          start=(c == 0), stop=(c == 7)), reads=[hT[i], wn], writes=[pm[n]])

            def stageA2(t):
                    i = t % 2
                    b.op("act", lambda e: e.copy(out=pmS[i][0][:], in_=pm[0][:]), reads=[pm[0]], writes=[pmS[i][0]])
                    b.op("dve", lambda e: e.tensor_copy(out=pmS[i][1][:], in_=pm[1][:]), reads=[pm[1]], writes=[pmS[i][1]])
                    b.op("act", lambda e: e.copy(out=pmS[i][2][:, 0:RW0 - 1024], in_=pm[2][:, 0:RW0 - 1024]), reads=[pm[2]], writes=[pmS[i][2]])

            def stageB(t):
                    i = t % 2
                    sq, ssq, tmpq, tmpk, qb, kb, cb, ptq, ptk = sq_[i], ssq_[i], tmpq_[i], tmpk_[i], qb_[i], kb_[i], cb_[i], ptq_[i], ptk_[i]
                    b.op("act", lambda e: e.activation(out=sq[:, 0:8, :], in_=pmS[i][0][:, 0:512].rearrange("p (h d) -> p h d", d=64), func=AF.Square),
                         reads=[pmS[i][0]], writes=[sq])
                    b.op("act", lambda e: e.activation(out=sq[:, 8:10, :], in_=pmS[i][1][:, 256:384].rearrange("p (h d) -> p h d", d=64), func=AF.Square),
                         reads=[pmS[i][1]], writes=[sq])
                    b.op("act", lambda e: e.activation(out=sq[:, 10:12, :], in_=pmS[i][2][:, 0:128].rearrange("p (h d) -> p h d", d=64), func=AF.Square),
                         reads=[pmS[i][2]], writes=[sq])
                    b.op("dve", lambda e: e.tensor_reduce(out=ssq[:], in_=sq[:], axis=AX.X, op=ALU.add), reads=[sq], writes=[ssq])
                    b.op("act", lambda e: e.activation(out=ssq[:], in_=ssq[:], func=AF.Sqrt, scale=1.0 / 64, bias=RMS_EPS), reads=[ssq], writes=[ssq])
                    b.op("dve", lambda e: e.reciprocal(out=ssq[:], in_=ssq[:]), reads=[ssq], writes=[ssq])
                    if getattr(self, 'stop_at', 99) <= 2:
                        return
                    b.op("dve", lambda e: e.tensor_tensor(out=tmpq[:], in0=pmS[i][0][:, 0:512].rearrange("p (h d) -> p h d", d=64),
                                                          in1=ssq[:, 0:8].unsqueeze(2).to_broadcast([128, 8, 64]), op=ALU.mult),
                         reads=[pmS[i][0], ssq], writes=[tmpq])
                    b.op("pool", lambda e: e.tensor_tensor(out=qb[:].rearrange("p (h d) -> p h d", d=64), in0=tmpq[:], in1=gq_rep[:], op=ALU.mult),
                         reads=[tmpq, gq_rep], writes=[qb])
                    for h in range(8):
                        b.op("pe", lambda e: e.transpose(out=ptq[0:64, h, :], in_=qb[:, h * 64:(h + 1) * 64], identity=self.ident[:]),
                             reads=[qb, self.ident], writes=[ptq])
                    b.op("act", lambda e: e.copy(out=qst[i][:], in_=ptq[0:64, :, :]), reads=[ptq], writes=[qst[i]])
                    b.dma("pool", self.qT_d[:, :, t * 128:(t + 1) * 128].rearrange("h d t -> d h t"), qst[i][:], reads=[qst[i]], writes=[self.qT_d])
                    if getattr(self, 'stop_at', 99) <= 3:
                        return
                    b.op("dve", lambda e: e.tensor_tensor(out=tmpk[:, 0:2, :], in0=pmS[i][1][:, 256:384].rearrange("p (h d) -> p h d", d=64),
                                                          in1=ssq[:, 8:10].unsqueeze(2).to_broadcast([128, 2, 64]), op=ALU.mult),
                         reads=[pmS[i][1], ssq], writes=[tmpk])
                    b.op("dve", lambda e: e.tensor_tensor(out=tmpk[:, 2:4, :], in0=pmS[i][2][:, 0:128].rearrange("p (h d) -> p h d", d=64),
                                                          in1=ssq[:, 10:12].unsqueeze(2).to_broadcast([128, 2, 64]), op=ALU.mult),
                         reads=[pmS[i][2], ssq], writes=[tmpk])
                    b.op("pool", lambda e: e.tensor_tensor(out=kb[:].rearrange("p (a g) d -> p a g d", a=2), in0=tmpk[:].rearrange("p (a g) d -> p a g d", a=2),
                                                           in1=gk_rep[:, None, :, :].to_broadcast([128, 2, 2, 64]), op=ALU.mult),
                         reads=[tmpk, gk_rep], writes=[kb])
                    for j in range(4):
                        b.op("pe", lambda e: e.transpose(out=ptk[0:64, j, :], in_=kb[:, j, :], identity=self.ident[:]),
                             reads=[kb, self.ident], writes=[ptk])
                    if getattr(self, 'stop_at', 99) <= 4:
                        return
                    for du in range(2):
                        b.op("act", lambda e: e.copy(out=cb[:, :, du, :], in_=pmS[i][1][:, 0:256].rearrange("p (a d) -> p a d", d=64)),
                             reads=[pmS[i][1]], writes=[cb])
                    for j in range(4):
                        b.op("pe", lambda e: e.transpose(out=ptk[:, 4 + j, :], in_=cb[:, j, :, :].rearrange("p a d -> p (a d)"), identity=self.ident[:]),
                             reads=[cb, self.ident], writes=[ptk])
                    c0 = t * 128
                    b.op("dve", lambda e: e.tensor_copy(out=self.ksE[0:64, :, c0:c0 + 128], in_=ptk[0:64, 0:2, :]), reads=[ptk], writes=[self.ksE])
                    b.op("dve", lambda e: e.tensor_copy(out=self.kwT[0:64, :, c0:c0 + 128], in_=ptk[0:64, 2:4, :]), reads=[ptk], writes=[self.kwT])
                    b.op("act", lambda e: e.copy(out=kcdup[0:64, :, 1 + c0:1 + c0 + 128], in_=ptk[0:64, 4:6, :]), reads=[ptk], writes=[kcdup])
                    b.op("act", lambda e: e.copy(out=kcdup[64:128, :, c0:c0 + 128], in_=ptk[64:128, 4:6, :]), reads=[ptk], writes=[kcdup])
                    b.op("dve", lambda e: e.tensor_copy(out=vcdup[0:64, :, 1 + c0:1 + c0 + 128], in_=ptk[0:64, 6:8, :]), reads=[ptk], writes=[vcdup])
                    b.op("dve", lambda e: e.tensor_copy(out=vcdup[64:128, :, c0:c0 + 128], in_=ptk[64:128, 6:8, :]), reads=[ptk], writes=[vcdup])
                    if getattr(self, 'stop_at', 99) <= 5:
                        return
                    b.op("act", lambda e: e.copy(out=self.vaug_s[:, t, :, 0:64], in_=pmS[i][1][:, 384:512].rearrange("p (g d) -> p g d", d=64)),
                         reads=[pmS[i][1]], writes=[self.vaug_s])
                    b.op("act", lambda e: e.copy(out=self.vaug_w[:, t, :, 0:64], in_=pmS[i][2][:, 128:256].rearrange("p (g d) -> p g d", d=64)),
                         reads=[pmS[i][2]], writes=[self.vaug_w])
                    b.op("act", lambda e: e.activation(out=self.gts[:, t, :], in_=pmS[i][2][:, 256:280], func=AF.Sigmoid), reads=[pmS[i][2]], writes=[self.gts])

            stageA(0)
            stageA2(0)
            for t in range(ntl):
                if t + 1 < ntl:
                    stageA(t + 1)
                stageB(t)
                if t + 1 < ntl:
                    stageA2(t + 1)
            if "nsa_proj" in self.debug:
                d = self.dbg_out("ksE", [128, 2, S], BF16)
                b.dma("pool", d, self.ksE[:], reads=[self.ksE])
                d = self.dbg_out("kcdup", [128, 2, S + 1], BF16)
                b.dma("pool", d, kcdup[:], reads=[kcdup])
                d = self.dbg_out("vaug_w", [128, NT, 2, 65], BF16)
                b.dma("pool", d, self.vaug_w[:], reads=[self.vaug_w])
                d = self.dbg_out("gts", [128, NT, 24], F32)
                b.dma("pool", d, self.gts[:], reads=[self.gts])
            if not getattr(self, 'skip_compress', False):
                self.compress(kcdup, vcdup, gk_rep, [pm[0], pm[1]], pm[2], ptk_[0])

    def compress(self, kcdup, vcdup, gk_rep, ph, po, ptc):
        b = self.b
        I = self.inp
        C2 = 2.0 * 0.7978845608028654
        w1 = b.sb("w1", [128, 16, 256], BF16)
        w2 = b.sb("w2", [128, 2, 64], BF16)
        w1st = [b.sb(f"w1st{i}", [128, 256], F32) for i in range(2)]
        peT = b.sb("peT", [128, 16], F32)
        peTb = b.sb("peTb", [128, 16], BF16)
        hTc = b.sb("hTc", [128, 2, 256], BF16)
        pbias = b.sb("pbias", [128, 2], F32)
        xh = b.sb("xh", [128, 255], F32)
        x2 = b.sb("x2", [128, 255], F32)
        sg = b.sb("sg", [128, 255], F32)
        ctmp = b.sb("ctmp", [128, 64], F32)
        csq = b.sb("csq", [128, 64], F32)
        cs1 = b.sb("cs1", [128, 1], F32)
        kcb = b.sb("kcb", [128, 64], BF16)
        b.op("pool", lambda e: e.memset(hTc[:], 0.0), writes=[hTc])
        for kv, (dup, pe_n, w1_n, w2_n) in enumerate([(kcdup, "cmp_pe_k", "cmp_w1_k", "cmp_w2_k"), (vcdup, "cmp_pe_v", "cmp_w1_v", "cmp_w2_v")]):
            self.load_weight(w1, I[w1_n][0], 256, kch=16, stage=w1st, eng="dve")
            self.load_weight(w2, I[w2_n][0], 64, kch=2, stage=w1st, eng="dve")
            for two in range(2):
                b.dma("sp", peT[two * 64:(two + 1) * 64, :], I[pe_n][0].rearrange("(pp two) d -> two d pp", two=2)[two],
                      writes=[peT], allow_slow_non_contiguous=True)
            b.op("dve", lambda e: e.tensor_copy(out=peTb[:], in_=peT[:]), reads=[peT], writes=[peTb])
            for ft in range(2):
                for pp in range(16):
                    b.op("pe", lambda e: e.matmul(po[:, ft:ft + 1], lhsT=w1[:, pp, ft * 128:(ft + 1) * 128], rhs=peTb[:, pp:pp + 1],
                                                  start=(pp == 0), stop=(pp == 15)), reads=[w1, peTb], writes=[po])
            b.op("dve", lambda e: e.tensor_copy(out=pbias[:], in_=po[:, 0:2]), reads=[po], writes=[pbias])
            for g in range(2):
                for ft in range(2):
                    p = ph[ft]
                    for pp in range(16):
                        b.op("pe", lambda e: e.matmul(p[:, 0:255], lhsT=w1[:, pp, ft * 128:(ft + 1) * 128],
                                                      rhs=dup[:, g, 1 + 2 * pp:1 + 2 * pp + 16 * 254 + 1:16],
                                                      start=(pp == 0), stop=(pp == 15)), reads=[w1, dup], writes=[p])
                    b.op("act", lambda e: e.activation(out=xh[:], in_=p[:, 0:255], func=AF.Identity, bias=pbias[:, ft:ft + 1]), reads=[p, pbias], writes=[xh])
                    b.op("dve", lambda e: e.tensor_tensor(out=x2[:], in0=xh[:], in1=xh[:], op=ALU.mult), reads=[xh], writes=[x2])
                    b.op("dve", lambda e: e.tensor_scalar(out=x2[:], in0=x2[:], scalar1=0.044715, scalar2=1.0, op0=ALU.mult, op1=ALU.add), reads=[x2], writes=[x2])
                    b.op("dve", lambda e: e.tensor_tensor(out=x2[:], in0=x2[:], in1=xh[:], op=ALU.mult), reads=[x2, xh], writes=[x2])
                    b.op("act", lambda e: e.activation(out=sg[:], in_=x2[:], func=AF.Sigmoid, scale=C2), reads=[x2], writes=[sg])
                    b.op("dve", lambda e: e.tensor_tensor(out=hTc[:, ft, 0:255], in0=xh[:], in1=sg[:], op=ALU.mult), reads=[xh, sg], writes=[hTc])
                for ct in range(2):
                    for ft in range(2):
                        b.op("pe", lambda e: e.matmul(po[:, 64:128], lhsT=hTc[:, ft, ct * 128:(ct + 1) * 128], rhs=w2[:, ft, :],
                                                      start=(ft == 0), stop=(ft == 1)), reads=[hTc, w2], writes=[po])
                    if kv == 0:
                        b.op("act", lambda e: e.activation(out=csq[:], in_=po[:, 64:128], func=AF.Square, accum_out=cs1[:]), reads=[po], writes=[csq, cs1])
                        b.op("act", lambda e: e.activation(out=cs1[:], in_=cs1[:], func=AF.Sqrt, scale=1.0 / 64, bias=RMS_EPS), reads=[cs1], writes=[cs1])
                        b.op("dve", lambda e: e.reciprocal(out=cs1[:], in_=cs1[:]), reads=[cs1], writes=[cs1])
                        b.op("dve", lambda e: e.tensor_scalar_mul(out=ctmp[:], in0=po[:, 64:128], scalar1=cs1[:]), reads=[po, cs1], writes=[ctmp])
                        b.op("dve", lambda e: e.tensor_tensor(out=kcb[:], in0=ctmp[:], in1=gk_rep[:, 0, :], op=ALU.mult), reads=[ctmp, gk_rep], writes=[kcb])
                        b.op("pe", lambda e: e.transpose(out=ptc[0:64, 0, :], in_=kcb[:], identity=self.ident[:]), reads=[kcb, self.ident], writes=[ptc])
                        b.op("dve", lambda e: e.tensor_copy(out=self.kcT[:, g, ct * 128:(ct + 1) * 128], in_=ptc[0:64, 0, :]), reads=[ptc], writes=[self.kcT])
                    else:
                        b.op("dve", lambda e: e.tensor_copy(out=self.vcA[:, g, ct, 0:64], in_=po[:, 64:128]), reads=[po], writes=[self.vcA])
        if "compress" in self.debug:
            d = self.dbg_out("kcT", [64, 2, 256], BF16)
            b.dma("pool", d, self.kcT[:], reads=[self.kcT])
            d = self.dbg_out("vcA", [128, 2, 2, 130], F32)
            b.dma("pool", d, self.vcA[:].bitcast(F32), reads=[self.vcA])

    def finish(self):
        b = self.b
        b.wait_all_on("pool")
        b.barrier()
        b.close()
        return self.nc


def _phase_attn(self):
    b = self.b
    I = self.inp
    with b.scope():
        tw = b.sb("tw", [128, 8, 640], F32)
        ts = b.sb("ts", [128, 8, 640], F32)
        b.dma("sp", tw[:], I["tw"], writes=[tw])
        b.dma("sp", ts[:], I["ts"], writes=[ts])
        candneg = b.sb("candneg", [128, 32, 64], F32)
        fz = b.sb("fz", [128, 32, 64], F32)
        b.dma("sp", candneg[:], I["candneg"], writes=[candneg])
        b.dma("sp", fz[:], I["fz"], writes=[fz])
        b31 = b.sb("b31", [128, 8], F32)
        b.dma("sp", b31[:], I["b31"], writes=[b31])
        kwp = b.sb("kwp", [128, 2, S], BF16)
        b.op("pool", lambda e: e.memset(kwp[64:128, :, :], 0.0), writes=[kwp])
        b.op("pool", lambda e: e.tensor_copy(out=kwp[0:64, :, :], in_=self.kwT[:]), reads=[self.kwT], writes=[kwp])
        kcp = b.sb("kcp", [128, 2, 256], BF16)
        b.op("pool", lambda e: e.memset(kcp[64:128, :, :], 0.0), writes=[kcp])
        b.op("pool", lambda e: e.tensor_copy(out=kcp[0:64, :, :], in_=self.kcT[:]), reads=[self.kcT], writes=[kcp])
        zer = b.sb("zer", [128, 512], BF16)
        b.op("pool", lambda e: e.memset(zer[:], 0.0), writes=[zer])
        qm = [b.sb(f"qm{i}", [128, 8, 512], BF16) for i in range(2)]
        bct = [b.sb(f"bct{i}", [128, 512], F32) for i in range(3)]
        scf = [b.sb(f"scf{i}", [128, 640], F32) for i in range(2)]
        pcT = [b.sb(f"pcT{i}", [128, 2, 512], F32) for i in range(2)]
        pT = [b.sb(f"pT{i}", [128, 640], BF16) for i in range(3)]
        oacc = b.sb("oacc", [128, 4, 512], F32)
        imp = b.sb("imp", [128, 4, 2, 64], F32)
        impm = b.sb("impm", [128, 64], F32)
        impm2 = b.sb("impm2", [128, 64], F32)
        m8a = b.sb("m8a", [128, 8], F32)
        m8b = b.sb("m8b", [128, 8], F32)
        msk = b.sb("msk", [128, 64], F32)
        mb = b.sb("mb", [128, 128], BF16)
        b.op("pool", lambda e: e.memset(mb[:], 0.0), writes=[mb])
        rs = b.sb("rs", [128, 4], F32)
        rg = b.sb("rg", [128, 4], F32)
        oab = b.sb("oab", [128, 512], BF16)
        oaT = [b.sb(f"oaT{i}", [128, 4, 128], BF16) for i in range(2)]
        pS = [b.ps(f"pS{i}", [128, 512], F32) for i in range(2)]
        pS2 = b.ps("pS2", [128, 512], F32)
        pO = [b.ps(f"pO{i}", [128, 512], F32) for i in range(3)]
        pTr = b.ps("pTr", [128, 8, 128], BF16)
        nrot = {"bct": 0, "scf": 0, "pT": 0, "pS": 0}

        def rot(name, lst):
            nrot[name] += 1
            return lst[nrot[name] % len(lst)]

        def finalize(po, ncol_off, h, qs, branch, first):
            qt = qs_base + qs
            o0 = ncol_off
            b.op("dve", lambda e: e.tensor_scalar_max(out=rs[:, 0:1], in0=po[:, o0 + 64:o0 + 65], scalar1=1e-30), reads=[po], writes=[rs])
            b.op("dve", lambda e: e.reciprocal(out=rs[:, 1:2], in_=rs[:, 0:1]), reads=[rs], writes=[rs])
            b.op("dve", lambda e: e.tensor_tensor(out=rg[:, 0:1], in0=rs[:, 1:2], in1=self.gts[:, qt, h * 3 + branch:h * 3 + branch + 1], op=ALU.mult),
                 reads=[rs, self.gts], writes=[rg])
            if first:
                b.op("dve", lambda e: e.tensor_scalar_mul(out=oacc[:, qs, h * 64:(h + 1) * 64], in0=po[:, o0:o0 + 64], scalar1=rg[:, 0:1]),
                     reads=[po, rg], writes=[oacc])
            else:
                b.op("dve", lambda e: e.scalar_tensor_tensor(out=oacc[:, qs, h * 64:(h + 1) * 64], in0=po[:, o0:o0 + 64], scalar=rg[:, 0:1],
                                                             in1=oacc[:, qs, h * 64:(h + 1) * 64], op0=ALU.mult, op1=ALU.add),
                     reads=[po, rg, oacc], writes=[oacc])

        nqg = getattr(self, "nqg_limit", 8)
        for qg in range(nqg):
            qs_base = 4 * qg
            q0 = 512 * qg
            Q = qm[qg % 2]
            b.dma("sp", Q[0:64, :, :], self.qT_d[:, :, q0:q0 + 512].rearrange("h d t -> d h t"), reads=[self.qT_d], writes=[Q])
            if qg < 2:
                b.op("pool", lambda e: e.memset(Q[64:128, :, :], 0.0), writes=[Q])
            for h in range(8):
                g = h // 4
                pc = pcT[h % 2]
                for ct in range(2):
                    p = rot("pS", pS)
                    b.op("pe", lambda e: e.matmul(p[:, :], lhsT=kcp[:, g, ct * 128:(ct + 1) * 128], rhs=Q[:, h, :], start=True, stop=True),
                         reads=[kcp, Q], writes=[p])
                    bt = rot("bct", bct)
                    b.dma("sp", bt[:], I["biasc"][h, ct, :, q0:q0 + 512], writes=[bt])
                    sc = rot("scf", scf)
                    b.op("dve", lambda e: e.tensor_tensor(out=sc[:, 0:512], in0=p[:, :], in1=bt[:], op=ALU.add), reads=[p, bt], writes=[sc])
                    b.op("act", lambda e: e.activation(out=pc[:, ct, :], in_=sc[:, 0:512], func=AF.Exp), reads=[sc], writes=[pc])
                po = pO[0]
                for qs in range(4):
                    for ct in range(2):
                        b.op("pe", lambda e: e.matmul(po[:, qs * 128:qs * 128 + 129] if False else po[:, 0:129], lhsT=pc[:, ct, qs * 128:(qs + 1) * 128],
                                                      rhs=self.vcA[:, g, ct, :], start=(ct == 0), stop=(ct == 1)), reads=[pc, self.vcA], writes=[po])
                    finalize(po, 0, h, qs, 0, True)
                    if h % 4 == 0:
                        b.op("dve", lambda e: e.tensor_scalar_mul(out=imp[:, qs, g, :], in0=po[:, 65:129], scalar1=rs[:, 1:2]), reads=[po, rs], writes=[imp])
                    else:
                        b.op("dve", lambda e: e.scalar_tensor_tensor(out=imp[:, qs, g, :], in0=po[:, 65:129], scalar=rs[:, 1:2], in1=imp[:, qs, g, :],
                                                                     op0=ALU.mult, op1=ALU.add), reads=[po, rs, imp], writes=[imp])
            if qg >= 2:
                for qs in range(4):
                    qt = qs_base + qs
                    for g in range(2):
                        b.op("dve", lambda e: e.tensor_tensor(out=impm[:], in0=imp[:, qs, g, :], in1=candneg[:, qt, :], op=ALU.add), reads=[imp, candneg], writes=[impm])
                        b.op("dve", lambda e: e.max(out=m8a[:], in_=impm[:]), reads=[impm], writes=[m8a])
                        b.op("dve", lambda e: e.match_replace(out=impm2[:], in_to_replace=m8a[:], in_values=impm[:], imm_value=-1e9), reads=[m8a, impm], writes=[impm2])
                        b.op("dve", lambda e: e.max(out=m8b[:], in_=impm2[:]), reads=[impm2], writes=[m8b])
                        b.op("dve", lambda e: e.tensor_scalar(out=msk[:], in0=impm[:], scalar1=m8b[:, 4:5], scalar2=None, op0=ALU.is_ge), reads=[impm, m8b], writes=[msk])
                        b.op("dve", lambda e: e.tensor_tensor(out=msk[:], in0=msk[:], in1=fz[:, qt, :], op=ALU.max), reads=[msk, fz], writes=[msk])
                        b.op("dve", lambda e: e.tensor_scalar(out=mb[:, 64:128], in0=msk[:], scalar1=-NEG, scalar2=NEG, op0=ALU.mult, op1=ALU.add), reads=[msk], writes=[mb])
                        b.op("pe", lambda e: e.transpose(out=pTr[:, 0, :], in_=mb[:], identity=self.ident[:]), reads=[mb, self.ident], writes=[pTr])
                        b.op("act", lambda e: e.copy(out=Q[64:128, 4 * g:4 * g + 4, qs * 128:(qs + 1) * 128],
                                                     in_=pTr[64:128, 0:1, :].to_broadcast([64, 4, 128])), reads=[pTr], writes=[Q])
            for h in range(8):
                g = h // 4
                po_s, po_w = pO[1], pO[2]
                for po in (po_s, po_w):
                    b.op("pe", lambda e: e.matmul(po[:, 0:260], lhsT=zer[:, 0:128], rhs=zer[:, 0:260], start=True, stop=True), reads=[zer], writes=[po])
                nkt = 4 * (qg + 1)
                for kt in range(nkt):
                    dlt = 4 * qg - kt
                    qstart = 0 if dlt >= 0 else -dlt * 128
                    N = 512 - qstart
                    p = rot("pS", pS)
                    b.op("pe", lambda e: e.matmul(p[:, 0:N], lhsT=self.ksE[:, g, kt * 128:(kt + 1) * 128], rhs=Q[:, h, qstart:512], start=True, stop=True),
                         reads=[self.ksE, Q], writes=[p])
                    pt_ = rot("pT", pT)
                    if dlt <= 1:
                        c0 = 128 if dlt == 1 else 0
                        sc = rot("scf", scf)
                        b.op("dve", lambda e: e.tensor_tensor(out=sc[:, 0:N], in0=p[:, 0:N], in1=ts[:, h, c0:c0 + N], op=ALU.add), reads=[p, ts], writes=[sc])
                        b.op("act", lambda e: e.activation(out=pt_[:, 0:N], in_=sc[:, 0:N], func=AF.Exp), reads=[sc], writes=[pt_])
                    else:
                        b.op("act", lambda e: e.activation(out=pt_[:, 0:N], in_=p[:, 0:N], func=AF.Exp, bias=b31[:, h:h + 1]), reads=[p, b31], writes=[pt_])
                    for qs in range(qstart // 128, 4):
                        o = qs * 128 - qstart
                        b.op("pe", lambda e: e.matmul(po_s[:, qs * 65:(qs + 1) * 65], lhsT=pt_[:, o:o + 128], rhs=self.vaug_s[:, kt, g, :],
                                                      start=False, stop=(kt == nkt - 1), skip_group_check=True), reads=[pt_, self.vaug_s], writes=[po_s])
                kts = [kt for kt in range(4 * qg - 4, 4 * qg + 4) if kt >= 0]
                for kt in kts:
                    qs_lo = max(0, kt - 4 * qg)
                    qs_hi = min(3, kt + 4 - 4 * qg)
                    N = (qs_hi - qs_lo + 1) * 128
                    c0 = 128 * (4 * qg + qs_lo - kt)
                    p = rot("pS", pS)
                    b.op("pe", lambda e: e.matmul(p[:, 0:N], lhsT=kwp[:, g, kt * 128:(kt + 1) * 128], rhs=Q[:, h, qs_lo * 128:(qs_hi + 1) * 128], start=True, stop=True),
                         reads=[kwp, Q], writes=[p])
                    sc = rot("scf", scf)
                    b.op("dve", lambda e: e.tensor_tensor(out=sc[:, 0:N], in0=p[:, 0:N], in1=tw[:, h, c0:c0 + N], op=ALU.add), reads=[p, tw], writes=[sc])
                    pt_ = rot("pT", pT)
                    b.op("act", lambda e: e.activation(out=pt_[:, 0:N], in_=sc[:, 0:N], func=AF.Exp), reads=[sc], writes=[pt_])
                    for qs in range(qs_lo, qs_hi + 1):
                        o = (qs - qs_lo) * 128
                        b.op("pe", lambda e: e.matmul(po_w[:, qs * 65:(qs + 1) * 65], lhsT=pt_[:, o:o + 128], rhs=self.vaug_w[:, kt, g, :],
                                                      start=False, stop=(kt == kts[-1]), skip_group_check=True), reads=[pt_, self.vaug_w], writes=[po_w])
                for qs in range(4):
                    finalize(po_s, qs * 65, h, qs, 1, False)
                    finalize(po_w, qs * 65, h, qs, 2, False)
            for qs in range(4):
                qt = qs_base + qs
                ot = oaT[qs % 2]
                b.op("act", lambda e: e.copy(out=oab[:], in_=oacc[:, qs, :]), reads=[oacc], writes=[oab])
                for c in range(4):
                    b.op("pe", lambda e: e.transpose(out=pTr[:, 4 + c, :], in_=oab[:, c * 128:(c + 1) * 128], identity=self.ident[:]), reads=[oab, self.ident], writes=[pTr])
                b.op("act", lambda e: e.copy(out=ot[:], in_=pTr[:, 4:8, :]), reads=[pTr], writes=[ot])
                b.dma("pool", self.oaT_d[:, :, qt * 128:(qt + 1) * 128].rearrange("c p t -> p c t"), ot[:], reads=[ot], writes=[self.oaT_d])
        if "attn" in self.debug:
            d = self.dbg_out("oaT", [4, 128, S], BF16)
            b.dma("pool", d, self.oaT_d[:], reads=[self.oaT_d])


Prog.phase_attn = _phase_attn


def _phase_attn2(self):
    b = self.b
    I = self.inp
    with b.scope():
        tw = b.sb("tw", [128, 8, 640], F32)
        ts = b.sb("ts", [128, 8, 640], F32)
        b.dma("sp", tw[:], I["tw"], writes=[tw])
        b.dma("sp", ts[:], I["ts"], writes=[ts])
        candneg = b.sb("candneg", [128, 32, 64], F32)
        fz = b.sb("fz", [128, 32, 64], F32)
        b.dma("sp", candneg[:], I["candneg"], writes=[candneg])
        b.dma("sp", fz[:], I["fz"], writes=[fz])
        b31 = b.sb("b31", [128, 8], F32)
        b.dma("sp", b31[:], I["b31"], writes=[b31])
        kwp = b.sb("kwp", [128, 2, S], BF16)
        b.op("pool", lambda e: e.memset(kwp[64:128, :, :], 0.0), writes=[kwp])
        b.op("pool", lambda e: e.tensor_copy(out=kwp[0:64, :, :], in_=self.kwT[:]), reads=[self.kwT], writes=[kwp])
        kcp = b.sb("kcp", [128, 2, 256], BF16)
        b.op("pool", lambda e: e.memset(kcp[64:128, :, :], 0.0), writes=[kcp])
        b.op("pool", lambda e: e.tensor_copy(out=kcp[0:64, :, :], in_=self.kcT[:]), reads=[self.kcT], writes=[kcp])
        zer = b.sb("zer", [128, 512], BF16)
        b.op("pool", lambda e: e.memset(zer[:], 0.0), writes=[zer])
        qm = [b.sb(f"qm{i}", [128, 8, 512], BF16) for i in range(2)]
        bct = [b.sb(f"bct{i}", [128, 512], F32) for i in range(3)]
        scf = [b.sb(f"scf{i}", [128, 640], F32) for i in range(3)]
        pcT = [b.sb(f"pcT{i}", [128, 2, 512], mybir.dt.float32r) for i in range(2)]
        pT = [b.sb(f"pT{i}", [128, 640], BF16) for i in range(4)]
        oacc = b.sb("oacc", [128, 4, 512], F32)
        imp = b.sb("imp", [128, 4, 2, 64], F32)
        impm = b.sb("impm", [128, 64], F32)
        impm2 = b.sb("impm2", [128, 64], F32)
        m8a = b.sb("m8a", [128, 8], F32)
        m8b = b.sb("m8b", [128, 8], F32)
        msk = b.sb("msk", [128, 64], F32)
        mb = b.sb("mb", [128, 128], BF16)
        impm8 = b.sb("impm8", [128, 8, 64], F32)
        impm28 = b.sb("impm28", [128, 8, 64], F32)
        m8a8 = b.sb("m8a8", [128, 8, 8], F32)
        m8b8 = b.sb("m8b8", [128, 8, 8], F32)
        msk8 = b.sb("msk8", [128, 8, 64], F32)
        mb8 = b.sb("mb8", [128, 8, 128], BF16)
        b.op("pool", lambda e: e.memset(mb8[:], 0.0), writes=[mb8])
        b.op("pool", lambda e: e.memset(mb[:], 0.0), writes=[mb])
        rs = b.sb("rs", [128, 4], F32)
        rg = b.sb("rg", [128, 4], F32)
        oab = b.sb("oab", [128, 512], BF16)
        oaT = [b.sb(f"oaT{i}", [128, 4, 128], BF16) for i in range(2)]
        pS = [b.ps(f"pS{i}", [128, 512], F32) for i in range(3)]
        pOs = [b.ps(f"pOs{i}", [128, 512], F32) for i in range(2)]
        pOw = [b.ps(f"pOw{i}", [128, 512], F32) for i in range(2)]
        pTr = b.ps("pTr", [128, 8, 128], BF16)
        nrot = {"bct": 0, "scf": 0, "pT": 0, "pS": 0}

        def rot(name, lst):
            nrot[name] += 1
            return lst[nrot[name] % len(lst)]

        def finalize(po, ncol_off, h, qs, branch, first):
            qt = qs_base + qs
            o0 = ncol_off
            b.op("dve", lambda e: e.tensor_scalar_max(out=rs[:, 0:1], in0=po[:, o0 + 64:o0 + 65], scalar1=1e-30), reads=[po], writes=[rs])
            b.op("dve", lambda e: e.reciprocal(out=rs[:, 1:2], in_=rs[:, 0:1]), reads=[rs], writes=[rs])
            b.op("dve", lambda e: e.tensor_tensor(out=rg[:, 0:1], in0=rs[:, 1:2], in1=self.gts[:, qt, h * 3 + branch:h * 3 + branch + 1], op=ALU.mult),
                 reads=[rs, self.gts], writes=[rg])
            if first:
                b.op("dve", lambda e: e.tensor_scalar_mul(out=oacc[:, qs, h * 64:(h + 1) * 64], in0=po[:, o0:o0 + 64], scalar1=rg[:, 0:1]),
                     reads=[po, rg], writes=[oacc])
            else:
                b.op("dve", lambda e: e.scalar_tensor_tensor(out=oacc[:, qs, h * 64:(h + 1) * 64], in0=po[:, o0:o0 + 64], scalar=rg[:, 0:1],
                                                             in1=oacc[:, qs, h * 64:(h + 1) * 64], op0=ALU.mult, op1=ALU.add),
                     reads=[po, rg, oacc], writes=[oacc])

        nqg = getattr(self, "nqg_limit", 8)
        for qg in range(nqg):
            qs_base = 4 * qg
            q0 = 512 * qg
            Q = qm[qg % 2]
            b.dma("sp", Q[0:64, :, :], self.qT_d[:, :, q0:q0 + 512].rearrange("h d t -> d h t"), reads=[self.qT_d], writes=[Q])
            if qg < 2:
                b.op("pool", lambda e: e.memset(Q[64:128, :, :], 0.0), writes=[Q])
            def cmpS(h):
                g = h // 4
                pc = pcT[h % 2]
                for ct in range(2):
                    p = rot("pS", pS)
                    b.op("pe", lambda e: e.matmul(p[:, :], lhsT=kcp[:, g, ct * 128:(ct + 1) * 128], rhs=Q[:, h, :], start=True, stop=True),
                         reads=[kcp, Q], writes=[p])
                    bt = rot("bct", bct)
                    b.dma("sp", bt[:], I["biasc"][h, ct, :, q0:q0 + 512], writes=[bt])
                    sc = rot("scf", scf)
                    b.op("dve", lambda e: e.tensor_tensor(out=sc[:, 0:512], in0=p[:, :], in1=bt[:], op=ALU.add), reads=[p, bt], writes=[sc])
                    b.op("act", lambda e: e.activation(out=pc[:, ct, :], in_=sc[:, 0:512], func=AF.Exp), reads=[sc], writes=[pc])

            def cmpPV(h):
                g = h // 4
                pc = pcT[h % 2]
                for qs in range(4):
                    po = [pOs[0], pOs[1], pOw[0], pOw[1]][qs]
                    for ct in range(2):
                        b.op("pe", lambda e: e.matmul(po[:, 0:130], lhsT=pc[:, ct, qs * 128:(qs + 1) * 128],
                                                      rhs=self.vcA[:, g, ct, :], start=(ct == 0), stop=(ct == 1)), reads=[pc, self.vcA], writes=[po])
                    finalize(po, 0, h, qs, 0, True)
                    if h % 4 == 0:
                        b.op("dve", lambda e: e.tensor_scalar_mul(out=imp[:, qs, g, :], in0=po[:, 65:129], scalar1=rs[:, 1:2]), reads=[po, rs], writes=[imp])
                    else:
                        b.op("dve", lambda e: e.scalar_tensor_tensor(out=imp[:, qs, g, :], in0=po[:, 65:129], scalar=rs[:, 1:2], in1=imp[:, qs, g, :],
                                                                     op0=ALU.mult, op1=ALU.add), reads=[po, rs, imp], writes=[imp])

            cmpS(0)
            for h in range(8):
                if h + 1 < 8:
                    cmpS(h + 1)
                cmpPV(h)
            if qg >= 2:
                I8 = imp[:].rearrange("p q g j -> p (q g) j")
                cn8 = candneg[:, qs_base:qs_base + 4, :].unsqueeze(2).to_broadcast([128, 4, 2, 64])
                fz8 = fz[:, qs_base:qs_base + 4, :].unsqueeze(2).to_broadcast([128, 4, 2, 64])
                b.op("dve", lambda e: e.tensor_tensor(out=impm8[:].rearrange("p (q g) j -> p q g j", g=2), in0=imp[:], in1=cn8, op=ALU.add), reads=[imp, candneg], writes=[impm8])
                for k in range(8):
                    b.op("dve", lambda e: e.max(out=m8a8[:, k, :], in_=impm8[:, k, :]), reads=[impm8], writes=[m8a8])
                for k in range(8):
                    b.op("dve", lambda e: e.match_replace(out=impm28[:, k, :], in_to_replace=m8a8[:, k, :], in_values=impm8[:, k, :], imm_value=-1e9), reads=[m8a8, impm8], writes=[impm28])
                for k in range(8):
                    b.op("dve", lambda e: e.max(out=m8b8[:, k, :], in_=impm28[:, k, :]), reads=[impm28], writes=[m8b8])
                b.op("dve", lambda e: e.tensor_tensor(out=msk8[:], in0=impm8[:], in1=m8b8[:, :, 4:5].to_broadcast([128, 8, 64]), op=ALU.is_ge), reads=[impm8, m8b8], writes=[msk8])
                b.op("dve", lambda e: e.tensor_tensor(out=msk8[:].rearrange("p (q g) j -> p q g j", g=2), in0=msk8[:].rearrange("p (q g) j -> p q g j", g=2), in1=fz8, op=ALU.max),
                     reads=[msk8, fz], writes=[msk8])
                b.op("dve", lambda e: e.tensor_scalar(out=mb8[:, :, 64:128], in0=msk8[:], scalar1=-NEG, scalar2=NEG, op0=ALU.mult, op1=ALU.add), reads=[msk8], writes=[mb8])
                for k in range(8):
                    b.op("pe", lambda e: e.transpose(out=pTr[:, k, :], in_=mb8[:, k, :], identity=self.ident[:]), reads=[mb8, self.ident], writes=[pTr])
                for g in range(2):
                    src = pTr[64:128, :, :].rearrange("p (q g) t -> p q g t", g=2)[:, :, g, :]
                    b.op("act", lambda e: e.copy(out=Q[64:128, 4 * g:4 * g + 4, :].rearrange("p r (q t) -> p r q t", t=128),
                                                 in_=src.unsqueeze(1).to_broadcast([64, 4, 4, 128])), reads=[pTr], writes=[Q])
            jobs = []
            for h in range(8):
                g = h // 4
                nkt = 4 * (qg + 1)
                for kt in range(nkt):
                    dlt = 4 * qg - kt
                    qstart = 0 if dlt >= 0 else -dlt * 128
                    jobs.append(dict(kind="s", h=h, g=g, kt=kt, qlo=qstart // 128, qhi=3, first=(kt == 0), last=False, lastkt=(kt == nkt - 1),
                                     tab=(ts, (128 if dlt == 1 else 0)) if dlt <= 1 else None))
                kts = [kt for kt in range(4 * qg - 4, 4 * qg + 4) if kt >= 0]
                for kt in kts:
                    qs_lo = max(0, kt - 4 * qg)
                    qs_hi = min(3, kt + 4 - 4 * qg)
                    jobs.append(dict(kind="w", h=h, g=g, kt=kt, qlo=qs_lo, qhi=qs_hi, first=False, last=(kt == kts[-1]), lastkt=(kt == kts[-1]),
                                     tab=(tw, 128 * (4 * qg + qs_lo - kt))))

            def emitS(j):
                h, g, kt = j["h"], j["g"], j["kt"]
                N = (j["qhi"] - j["qlo"] + 1) * 128
                p = rot("pS", pS)
                kmat = self.ksE if j["kind"] == "s" else kwp
                b.op("pe", lambda e: e.matmul(p[:, 0:N], lhsT=kmat[:, g, kt * 128:(kt + 1) * 128], rhs=Q[:, h, j["qlo"] * 128:(j["qhi"] + 1) * 128], start=True, stop=True),
                     reads=[kmat, Q], writes=[p])
                j["p"] = p
                j["N"] = N

            def emitE(j):
                h = j["h"]
                p, N = j["p"], j["N"]
                pt_ = rot("pT", pT)
                if j["tab"] is not None:
                    tab, c0 = j["tab"]
                    sc = rot("scf", scf)
                    b.op("dve", lambda e: e.tensor_tensor(out=sc[:, 0:N], in0=p[:, 0:N], in1=tab[:, h, c0:c0 + N], op=ALU.add), reads=[p, tab], writes=[sc])
                    b.op("act", lambda e: e.activation(out=pt_[:, 0:N], in_=sc[:, 0:N], func=AF.Exp), reads=[sc], writes=[pt_])
                else:
                    b.op("act", lambda e: e.activation(out=pt_[:, 0:N], in_=p[:, 0:N], func=AF.Exp, bias=b31[:, h:h + 1]), reads=[p, b31], writes=[pt_])
                j["pt"] = pt_

            def emitPV(j):
                h, g, kt = j["h"], j["g"], j["kt"]
                po_s, po_w = pOs[h % 2], pOw[h % 2]
                if j["first"]:
                    for po in (po_s, po_w):
                        b.op("pe", lambda e: e.matmul(po[:, 0:260], lhsT=zer[:, 0:128], rhs=zer[:, 0:260], start=True, stop=True), reads=[zer], writes=[po])
                po = po_s if j["kind"] == "s" else po_w
                va = self.vaug_s if j["kind"] == "s" else self.vaug_w
                for qs in range(j["qlo"], j["qhi"] + 1):
                    o = (qs - j["qlo"]) * 128
                    b.op("pe", lambda e: e.matmul(po[:, qs * 65:(qs + 1) * 65], lhsT=j["pt"][:, o:o + 128], rhs=va[:, kt, g, :],
                                                  start=False, stop=j["lastkt"], skip_group_check=True), reads=[j["pt"], va], writes=[po])
                if j["last"]:
                    for qs in range(4):
                        finalize(po_s, qs * 65, h, qs, 1, False)
                        finalize(po_w, qs * 65, h, qs, 2, False)

            LA = 2
            for i_ in range(len(jobs) + LA):
                if i_ < len(jobs):
                    emitS(jobs[i_])
                if i_ >= LA:
                    emitE(jobs[i_ - LA])
                    emitPV(jobs[i_ - LA])
            for qs in range(4):
                qt = qs_base + qs
                ot = oaT[qs % 2]
                b.op("act", lambda e: e.copy(out=oab[:], in_=oacc[:, qs, :]), reads=[oacc], writes=[oab])
                for c in range(4):
                    b.op("pe", lambda e: e.transpose(out=pTr[:, 4 + c, :], in_=oab[:, c * 128:(c + 1) * 128], identity=self.ident[:]), reads=[oab, self.ident], writes=[pTr])
                b.op("act", lambda e: e.copy(out=ot[:], in_=pTr[:, 4:8, :]), reads=[pTr], writes=[ot])
                b.dma("pool", self.oaT_d[:, :, qt * 128:(qt + 1) * 128].rearrange("c p t -> p c t"), ot[:], reads=[ot], writes=[self.oaT_d])
        if "attn" in self.debug:
            d = self.dbg_out("oaT", [4, 128, S], BF16)
            b.dma("pool", d, self.oaT_d[:], reads=[self.oaT_d])


Prog.phase_attn2 = _phase_attn2


def _phase_merge(self):
    b = self.b
    I = self.inp
    self.x1_d = b.dram("x1_d", [S, D], F32)
    with b.scope():
        gat = self.load_gain("gat2", I["attn_norm_g"][0])
        stage = [b.sb(f"mst{i}", [128, 1024], F32) for i in range(2)]
        wg = b.sb("wg", [128, 8, 2048], BF16)
        for n in range(2):
            for c in range(8):
                st = stage[c % 2]
                b.dma("sp", st[:], I["w_in"][0][c * 128:(c + 1) * 128, GA0 + n * 1024:GA0 + (n + 1) * 1024], writes=[st])
                b.op("act", lambda e: e.activation(out=wg[:, c, n * 1024:(n + 1) * 1024], in_=st[:], func=AF.Copy, scale=gat[:, c:c + 1]),
                     reads=[st, gat], writes=[wg])
        wa = b.sb("wa", [128, 4, 1024], BF16)
        wb = b.sb("wb", [128, 4, 1024], BF16)
        wo = b.sb("wo", [128, 8, 1024], BF16)
        self.load_weight(wa, I["w_proj_a"][0], 1024, kch=4, stage=stage, eng="dve")
        self.load_weight(wb, I["w_proj_b"][0], 1024, kch=4, stage=stage, eng="dve")
        self.load_weight(wo, I["w_out"][0], 1024, kch=8, stage=stage, eng="dve")
        xt = [b.sb(f"mxt{i}", [128, D], F32) for i in range(2)]
        junk = b.sb("mjunk", [128, D], BF16)
        ss = [b.sb(f"mss{i}", [128, 1], F32) for i in range(2)]
        hb = [b.sb(f"mhb{i}", [128, D], BF16) for i in range(2)]
        hT = [b.sb(f"mhT{i}", [128, 8, 128], BF16) for i in range(2)]
        oat = [b.sb(f"oat{i}", [128, 4, 128], BF16) for i in range(2)]
        obt = [b.sb(f"obt{i}", [128, 4, 128], BF16) for i in range(2)]
        sg = b.sb("msg", [128, 2048], F32)
        m1 = b.sb("m1", [128, 1024], F32)
        m2 = b.sb("m2", [128, 1024], F32)
        mgb = b.sb("mgb", [128, 1024], BF16)
        mT = b.sb("mT", [128, 8, 128], BF16)
        x1t = [b.sb(f"x1t{i}", [128, D], F32) for i in range(2)]
        pt = b.ps("mpt", [128, 8, 128], BF16)
        pg = [b.ps(f"mpg{i}", [128, 512], F32) for i in range(2)]
        pa = [b.ps(f"mpa{i}", [128, 512], F32) for i in range(2)]
        pb = [b.ps(f"mpb{i}", [128, 512], F32) for i in range(2)]
        for t in range(getattr(self, "nt_limit", NT)):
            i = t % 2
            self.make_hT(I["x"], t, xt[i], junk, ss[i], hb[i], pt, hT[i], self.ident)
            b.dma("sp", oat[i][:], self.oaT_d[:, :, t * 128:(t + 1) * 128].rearrange("c p t -> p c t"), reads=[self.oaT_d], writes=[oat[i]])
            b.dma("sp", obt[i][:], self.obT_d[:, :, t * 128:(t + 1) * 128].rearrange("c p t -> p c t"), reads=[self.obT_d], writes=[obt[i]])
            for n in range(4):
                p = pg[n % 2]
                for c in range(8):
                    b.op("pe", lambda e: e.matmul(p[:, :], lhsT=hT[i][:, c, :], rhs=wg[:, c, n * 512:(n + 1) * 512], start=(c == 0), stop=(c == 7)),
                         reads=[hT[i], wg], writes=[p])
                b.op("act", lambda e: e.activation(out=sg[:, n * 512:(n + 1) * 512], in_=p[:, :], func=AF.Sigmoid), reads=[p], writes=[sg])
            for n in range(2):
                for c in range(4):
                    b.op("pe", lambda e: e.matmul(pa[n][:, :], lhsT=oat[i][:, c, :], rhs=wa[:, c, n * 512:(n + 1) * 512], start=(c == 0), stop=(c == 3)),
                         reads=[oat[i], wa], writes=[pa[n]])
                for c in range(4):
                    b.op("pe", lambda e: e.matmul(pb[n][:, :], lhsT=obt[i][:, c, :], rhs=wb[:, c, n * 512:(n + 1) * 512], start=(c == 0), stop=(c == 3)),
                         reads=[obt[i], wb], writes=[pb[n]])
                b.op("dve", lambda e: e.tensor_tensor(out=m1[:, n * 512:(n + 1) * 512], in0=pa[n][:, :], in1=sg[:, n * 512:(n + 1) * 512], op=ALU.mult),
                     reads=[pa[n], sg], writes=[m1])
                b.op("dve", lambda e: e.tensor_tensor(out=m2[:, n * 512:(n + 1) * 512], in0=pb[n][:, :], in1=sg[:, 1024 + n * 512:1024 + (n + 1) * 512], op=ALU.mult),
                     reads=[pb[n], sg], writes=[m2])
            b.op("pool", lambda e: e.tensor_tensor(out=mgb[:], in0=m1[:], in1=m2[:], op=ALU.add), reads=[m1, m2], writes=[mgb])
            for c in range(8):
                b.op("pe", lambda e: e.transpose(out=pt[:, c, :], in_=mgb[:, c * 128:(c + 1) * 128], identity=self.ident[:]), reads=[mgb, self.ident], writes=[pt])
            b.op("act", lambda e: e.copy(out=mT[:], in_=pt[:]), reads=[pt], writes=[mT])
            for n in range(2):
                for c in range(8):
                    b.op("pe", lambda e: e.matmul(pa[n][:, :], lhsT=mT[:, c, :], rhs=wo[:, c, n * 512:(n + 1) * 512], start=(c == 0), stop=(c == 7)),
                         reads=[mT, wo], writes=[pa[n]])
                b.op("dve", lambda e: e.tensor_tensor(out=x1t[i][:, n * 512:(n + 1) * 512], in0=pa[n][:, :], in1=xt[i][:, n * 512:(n + 1) * 512], op=ALU.add),
                     reads=[pa[n], xt[i]], writes=[x1t[i]])
            b.dma("pool", self.x1_d[t * 128:(t + 1) * 128, :], x1t[i][:], reads=[x1t[i]], writes=[self.x1_d])
        if "merge" in self.debug:
            d = self.dbg_out("x1", [S, D], F32)
            b.dma("pool", d, self.x1_d[:], reads=[self.x1_d])


def _phase_ffn(self):
    b = self.b
    I = self.inp
    TG = 128
    NFT = 44
    with b.scope():
        gf = self.load_gain("gf", I["ffn_norm_g"][0])
        stage = [b.sb(f"fst{i}", [128, 1024], F32) for i in range(2)]
        wu = b.sb("wu", [128, 8, 2 * DFF], BF16)
        for n in range(8):
            for c in range(8):
                st = stage[c % 2]
                b.dma("sp", st[:, 0:704], I["w_up"][0][c * 128:(c + 1) * 128, n * 704:(n + 1) * 704], writes=[st])
                b.op("act", lambda e: e.activation(out=wu[:, c, n * 704:(n + 1) * 704], in_=st[:, 0:704], func=AF.Copy, scale=gf[:, c:c + 1]),
                     reads=[st, gf], writes=[wu])
        wd = b.sb("wd", [128, 22, D], BF16)
        self.load_weight(wd, I["w_down"][0], D, kch=22, stage=stage, eng="dve")
        cw = b.sb("cw", [128, 3, NFT], F32)
        for j in range(3):
            b.dma("sp", cw[:, j, :], I["conv_w"][0][j].rearrange("(c p) -> p c", p=128), writes=[cw], allow_slow_non_contiguous=True)
        cbias = self.load_gain("cbias", I["conv_b"][0], kch=NFT)
        carry = b.sb("carry", [128, NFT, 2], F32)
        b.op("pool", lambda e: e.memset(carry[:], 0.0), writes=[carry])
        xt = [b.sb(f"fxt{i}", [128, D], F32) for i in range(2)]
        junk = b.sb("fjunk", [128, D], BF16)
        ss = [b.sb(f"fss{i}", [128, 1], F32) for i in range(2)]
        hb = [b.sb(f"fhb{i}", [128, D], BF16) for i in range(2)]
        hT1 = [b.sb(f"fhT{i}", [128, 8, 128], BF16) for i in range(2)]
        hTg = b.sb("fhTg", [128, 8, TG], BF16)
        ub = [b.sb(f"ub{i}", [128, TG + 2], F32) for i in range(2)]
        cv = [b.sb(f"cv{i}", [128, TG], F32) for i in range(2)]
        sgl = b.sb("sgl", [128, TG], F32)
        actT = b.sb("actT", [128, 22, TG], BF16)
        self._val = b.sb("fval", [128, 22, TG], BF16)
        ot = xt
        pt = b.ps("fpt", [128, 8, 128], BF16)
        pu = [b.ps(f"fpu{i}", [128, 512], F32) for i in range(3)]
        pd = [b.ps(f"fpd{i}", [128, 512], F32) for i in range(2)]
        ng = getattr(self, "nt_limit", NT) * 128 // TG
        for gi in range(ng):
            for s_ in range(TG // 128):
                t = gi * (TG // 128) + s_
                self.make_hT(self.x1_d, t, xt[s_], junk, ss[s_], hb[s_], pt, hT1[s_], self.ident)
                b.op("pool", lambda e: e.tensor_copy(out=hTg[:, :, s_ * 128:(s_ + 1) * 128], in_=hT1[s_][:]), reads=[hT1[s_]], writes=[hTg])
            for ft in range(NFT):
                p = pu[ft % 3]
                u = ub[ft % 2]
                c_ = cv[(ft // 22) % 2] if False else cv[ft % 2]
                for c in range(8):
                    b.op("pe", lambda e: e.matmul(p[:, 0:TG], lhsT=wu[:, c, ft * 128:(ft + 1) * 128], rhs=hTg[:, c, :], start=(c == 0), stop=(c == 7)),
                         reads=[wu, hTg], writes=[p])
                b.op("act", lambda e: e.copy(out=u[:, 2:TG + 2], in_=p[:, 0:TG]), reads=[p], writes=[u])
                b.op("pool", lambda e: e.tensor_copy(out=u[:, 0:2], in_=carry[:, ft, :]), reads=[carry], writes=[u])
                b.op("pool", lambda e: e.tensor_copy(out=carry[:, ft, :], in_=u[:, TG:TG + 2]), reads=[u], writes=[carry])
                b.op("dve", lambda e: e.tensor_scalar(out=c_[:], in0=u[:, 0:TG], scalar1=cw[:, 0, ft:ft + 1], scalar2=cbias[:, ft:ft + 1], op0=ALU.mult, op1=ALU.add),
                     reads=[u, cw, cbias], writes=[c_])
                b.op("dve", lambda e: e.scalar_tensor_tensor(out=c_[:], in0=u[:, 1:TG + 1], scalar=cw[:, 1, ft:ft + 1], in1=c_[:], op0=ALU.mult, op1=ALU.add),
                     reads=[u, cw, c_], writes=[c_])
                if ft < 22:
                    b.op("dve", lambda e: e.scalar_tensor_tensor(out=self._val[:, ft, :], in0=u[:, 2:TG + 2], scalar=cw[:, 2, ft:ft + 1], in1=c_[:], op0=ALU.mult, op1=ALU.add),
                         reads=[u, cw, c_], writes=[self._val])
                else:
                    b.op("dve", lambda e: e.scalar_tensor_tensor(out=c_[:], in0=u[:, 2:TG + 2], scalar=cw[:, 2, ft:ft + 1], in1=c_[:], op0=ALU.mult, op1=ALU.add),
                         reads=[u, cw, c_], writes=[c_])
                    b.op("act", lambda e: e.activation(out=sgl[:], in_=c_[:], func=AF.Silu), reads=[c_], writes=[sgl])
                    b.op("dve", lambda e: e.tensor_tensor(out=actT[:, ft - 22, :], in0=sgl[:], in1=self._val[:, ft - 22, :], op=ALU.mult),
                         reads=[sgl, self._val], writes=[actT])
            for s_ in range(TG // 128):
                t = gi * (TG // 128) + s_
                for n in range(2):
                    for f in range(22):
                        b.op("pe", lambda e: e.matmul(pd[n][:, :], lhsT=actT[:, f, s_ * 128:(s_ + 1) * 128], rhs=wd[:, f, n * 512:(n + 1) * 512], start=(f == 0), stop=(f == 21)),
                             reads=[actT, wd], writes=[pd[n]])
                    b.op("dve", lambda e: e.tensor_tensor(out=ot[s_][:, n * 512:(n + 1) * 512], in0=pd[n][:, :], in1=xt[s_][:, n * 512:(n + 1) * 512], op=ALU.add),
                         reads=[pd[n], xt[s_]], writes=[ot[s_]])
                b.dma("pool", self.out[t * 128:(t + 1) * 128, :], ot[s_][:], reads=[ot[s_]])


Prog.phase_merge = _phase_merge
Prog.phase_ffn = _phase_ffn


def _phase_rwkv(self):
    b = self.b
    I = self.inp
    TG = 256
    NCH = TG // 64
    tt = lambda eng, out, in0, in1, op, rd, wr: b.op(eng, lambda e: e.tensor_tensor(out=out, in0=in0, in1=in1, op=op), reads=rd, writes=wr)
    with b.scope():
        gat = self.load_gain("gat3", I["attn_norm_g"][0])
        stage = [b.sb(f"rst{i}", [128, 1792], F32) for i in range(2)]
        wr = b.sb("wr", [128, 8, 1792], BF16)
        self.load_weight(wr, I["w_in"][0][:, RW0:RW0 + 1792], 1792, gvec=gat, stage=stage)

        def colvec(name, src, n):
            t = b.sb(name, [64, n], F32)
            b.dma("sp", t[:], src.rearrange("(c p) -> p c", p=64), writes=[t], allow_slow_non_contiguous=True)
            return t
        mu = colvec("mu", I["rwkv_mu"][0], 28)
        w0 = colvec("w0", I["rwkv_w0"][0], 8)
        a0 = colvec("a0", I["rwkv_a0"][0], 8)
        k_k = colvec("k_k", I["rwkv_k_k"][0], 8)
        k_a = colvec("k_a", I["rwkv_k_a"][0], 8)
        r_k = colvec("r_k", I["rwkv_r_k"][0].rearrange("h d -> (h d)"), 8)
        w2s = b.sb("w2s", [64, 512], F32)
        a2s = b.sb("a2s", [64, 512], F32)
        g2s = b.sb("g2s", [64, 2, 512], F32)
        b.dma("sp", w2s[:], I["rwkv_w2"][0], writes=[w2s])
        b.dma("sp", a2s[:], I["rwkv_a2"][0], writes=[a2s])
        b.dma("sp", g2s[:], I["rwkv_g2"][0].rearrange("(two l) f -> l two f", two=2), writes=[g2s])
        lng = b.sb("lng", [64, 512], F32)
        lnb = b.sb("lnb", [64, 512], F32)
        b.dma("sp", lng[:], I["rwkv_ln_g"][0].partition_broadcast(64), writes=[lng])
        b.dma("sp", lnb[:], I["rwkv_ln_b"][0].partition_broadcast(64), writes=[lnb])
        msk = b.sb("rmsk", [64, 3, 64], F32)
        b.dma("sp", msk[:], I["rwmask"], writes=[msk])
        rstm = b.sb("rstm", [64, TG], F32)
        b.dma("sp", rstm[:], I["rwreset"][:, 0:TG], writes=[rstm])
        ones = b.sb("ones64", [64, 64], F32)
        b.op("pool", lambda e: e.memset(ones[:], 1.0), writes=[ones])
        idf = self.identf
        carry = b.sb("rcarry", [64, 28], F32)
        b.op("pool", lambda e: e.memset(carry[:], 0.0), writes=[carry])
        Hs = [[b.sb(f"H{h}_{i}", [64, 64], F32) for i in range(2)] for h in range(8)]
        for h in range(8):
            b.op("pool", lambda e: e.memset(Hs[h][0][:], 0.0), writes=[Hs[h][0]])
        xt = [b.sb(f"rxt{i}", [128, D], F32) for i in range(2)]
        junk = b.sb("rjunk", [128, D], BF16)
        ss = [b.sb(f"rss{i}", [128, 1], F32) for i in range(2)]
        hb = [b.sb(f"rhb{i}", [128, D], BF16) for i in range(2)]
        hT1 = [b.sb(f"rhT{i}", [128, 8, 128], BF16) for i in range(2)]
        hTg = b.sb("rhTg", [128, 8, TG], BF16)
        pbuf = [b.sb(f"rpb{i}", [64, TG + 1], F32) for i in range(2)]
        dtmp = b.sb("rdtmp", [64, TG], F32)
        X = [b.sb(f"rX{w}", [64, 8, TG], F32) for w in range(3)]
        xs = b.sb("rxs", [64, 4, TG], F32)
        BV = b.sb("rBV", [64, 8, TG], F32)
        Ytm = b.sb("rYtm", [64, NCH, 8, 64], F32)
        sqv = b.sb("rsqv", [64, NCH, 8, 64], F32)
        st1 = b.sb("rst1", [64, NCH * 8], F32)
        st2 = b.sb("rst2", [64, NCH * 8], F32)
        T = {n: b.sb("r" + n, [64, TG], F32) for n in ["lw", "as", "kk", "sq", "kkn", "bv", "kp", "t1", "L", "Lx", "Ep", "Em", "Ex", "BT", "KT", "BG", "KG", "rk"]}
        AR = b.sb("rAR", [64, NCH, 2, 64], F32)
        TM = [b.sb(f"rTM{i}", [64, 3, 64], F32) for i in range(2)]
        XM = [b.sb(f"rXM{i}", [64, 4, 64], F32) for i in range(2)]
        AA = [b.sb(f"rAA{i}", [64, 2, 64], F32) for i in range(3)]
        PP = [b.sb(f"rPP{i}", [64, 64], F32) for i in range(3)]
        Xs = b.sb("rXs", [64, 64], F32)
        Us = b.sb("rUs", [64, 64], F32)
        obf = [b.sb(f"robf{i}", [64, TG], BF16) for i in range(2)]
        otmp = b.sb("rotmp", [64, TG], F32)
        pt = b.ps("rpt", [128, 8, 128], BF16)
        pp = [b.ps(f"rpp{i}", [128, 512], F32) for i in range(2)]
        pq = [b.ps(f"rpq{i}", [128, 512], F32) for i in range(2)]
        pd = [b.ps(f"rpd{i}", [128, 512], F32) for i in range(2)]
        pz = b.ps("rpz", [128, 512], F32)
        cnt = {"pp": 0, "pq": 0, "pd": 0, "aa": 0, "ppb": 0, "tm": 0, "xm": 0, "pb": 0}

        def nxt(k, lst):
            cnt[k] += 1
            return lst[cnt[k] % len(lst)]

        ngr = getattr(self, "nrg_limit", S // TG)
        for gi in range(ngr):
            q0 = gi * TG
            for s_ in range(TG // 128):
                t = gi * (TG // 128) + s_
                self.make_hT(I["x"], t, xt[s_], junk, ss[s_], hb[s_], pt, hT1[s_], self.ident)
                b.op("pool", lambda e: e.tensor_copy(out=hTg[:, :, s_ * 128:(s_ + 1) * 128], in_=hT1[s_][:]), reads=[hT1[s_]], writes=[hTg])

            def proj_lerp(fc, out_ap, out_buf, post=None):
                p = nxt("pp", pp)
                for c in range(8):
                    b.op("pe", lambda e: e.matmul(p[0:64, 0:TG], lhsT=wr[:, c, fc * 64:(fc + 1) * 64], rhs=hTg[:, c, :], start=(c == 0), stop=(c == 7)),
                         reads=[wr, hTg], writes=[p])
                pb_ = nxt("pb", pbuf)
                b.op("act", lambda e: e.copy(out=pb_[:, 1:TG + 1], in_=p[0:64, 0:TG]), reads=[p], writes=[pb_])
                b.op("pool", lambda e: e.tensor_copy(out=pb_[:, 0:1], in_=carry[:, fc:fc + 1]), reads=[carry], writes=[pb_])
                b.op("pool", lambda e: e.tensor_copy(out=carry[:, fc:fc + 1], in_=pb_[:, TG:TG + 1]), reads=[pb_], writes=[carry])
                tt("dve", dtmp[:], pb_[:, 0:TG], pb_[:, 1:TG + 1], ALU.subtract, [pb_], [dtmp])
                b.op("dve", lambda e: e.scalar_tensor_tensor(out=out_ap, in0=dtmp[:], scalar=mu[:, fc:fc + 1], in1=pb_[:, 1:TG + 1], op0=ALU.mult, op1=ALU.add),
                     reads=[dtmp, mu, pb_], writes=[out_buf])

            for w in range(3):
                for h in range(8):
                    proj_lerp(w * 8 + h, X[w][:, h, :], X[w])
            for j in range(4):
                proj_lerp(24 + j, xs[:, j, :], xs)
            b.op("act", lambda e: e.activation(out=xs[:, 0, :], in_=xs[:, 0, :], func=AF.Tanh), reads=[xs], writes=[xs])
            b.op("act", lambda e: e.activation(out=xs[:, 2:4, :], in_=xs[:, 2:4, :], func=AF.Sigmoid), reads=[xs], writes=[xs])

            for h in range(8):
                hs = slice(h * 64, (h + 1) * 64)
                R_, K_, V_ = X[0][:, h, :], X[1][:, h, :], X[2][:, h, :]
                p = nxt("pp", pp)
                b.op("pe", lambda e: e.matmul(p[0:64, 0:TG], lhsT=w2s[:, hs], rhs=xs[:, 0, :], start=True, stop=True), reads=[w2s, xs], writes=[p])
                b.op("act", lambda e: e.activation(out=T["lw"][:], in_=p[0:64, 0:TG], func=AF.Sigmoid, bias=w0[:, h:h + 1]), reads=[p, w0], writes=[T["lw"]])
                b.op("pool", lambda e: e.tensor_scalar_mul(out=T["lw"][:], in0=T["lw"][:], scalar1=-0.6065306597126334), reads=[T["lw"]], writes=[T["lw"]])
                p = nxt("pp", pp)
                b.op("pe", lambda e: e.matmul(p[0:64, 0:TG], lhsT=a2s[:, hs], rhs=xs[:, 1, :], start=True, stop=True), reads=[a2s, xs], writes=[p])
                b.op("act", lambda e: e.activation(out=T["as"][:], in_=p[0:64, 0:TG], func=AF.Sigmoid, bias=a0[:, h:h + 1]), reads=[p, a0], writes=[T["as"]])
                b.op("dve", lambda e: e.tensor_scalar_mul(out=T["kk"][:], in0=K_, scalar1=k_k[:, h:h + 1]), reads=[X[1], k_k], writes=[T["kk"]])
                tt("pool", T["sq"][:], T["kk"][:], T["kk"][:], ALU.mult, [T["kk"]], [T["sq"]])
                p = nxt("pp", pp)
                b.op("pe", lambda e: e.matmul(p[0:64, 0:TG], lhsT=ones[:], rhs=T["sq"][:], start=True, stop=True), reads=[ones, T["sq"]], writes=[p])
                b.op("act", lambda e: e.activation(out=T["sq"][:], in_=p[0:64, 0:TG], func=AF.Sqrt), reads=[p], writes=[T["sq"]])
                b.op("dve", lambda e: e.tensor_scalar_max(out=T["sq"][:], in0=T["sq"][:], scalar1=1e-12), reads=[T["sq"]], writes=[T["sq"]])
                b.op("dve", lambda e: e.reciprocal(out=T["sq"][:], in_=T["sq"][:]), reads=[T["sq"]], writes=[T["sq"]])
                tt("dve", T["kkn"][:], T["kk"][:], T["sq"][:], ALU.mult, [T["kk"], T["sq"]], [T["kkn"]])
                tt("pool", T["bv"][:], T["kkn"][:], T["as"][:], ALU.mult, [T["kkn"], T["as"]], [T["bv"]])
                b.op("dve", lambda e: e.tensor_scalar(out=T["t1"][:], in0=T["as"][:], scalar1=-1.0, scalar2=k_a[:, h:h + 1], op0=ALU.add, op1=ALU.mult),
                     reads=[T["as"], k_a], writes=[T["t1"]])
                b.op("dve", lambda e: e.scalar_tensor_tensor(out=T["kp"][:], in0=T["t1"][:], scalar=1.0, in1=K_, op0=ALU.add, op1=ALU.mult),
                     reads=[T["t1"], X[1]], writes=[T["kp"]])
                tt("pool", T["rk"][:], R_, T["kp"][:], ALU.mult, [X[0], T["kp"]], [T["rk"]])
                b.op("pool", lambda e: e.tensor_scalar_mul(out=T["rk"][:], in0=T["rk"][:], scalar1=r_k[:, h:h + 1]), reads=[T["rk"], r_k], writes=[T["rk"]])
                p = nxt("pp", pp)
                b.op("pe", lambda e: e.matmul(p[0:64, 0:TG], lhsT=ones[:], rhs=T["rk"][:], start=True, stop=True), reads=[ones, T["rk"]], writes=[p])
                tt("dve", BV[:, h, :], p[0:64, 0:TG], V_, ALU.mult, [p, X[2]], [BV])
                b.op("dve", lambda e: e.tensor_tensor_scan(out=T["L"][:], data0=rstm[:], data1=T["lw"][:], initial=0.0, op0=ALU.mult, op1=ALU.add),
                     reads=[rstm, T["lw"]], writes=[T["L"]])
                tt("pool", T["Lx"][:], T["L"][:], T["lw"][:], ALU.subtract, [T["L"], T["lw"]], [T["Lx"]])
                b.op("act", lambda e: e.activation(out=T["Ep"][:], in_=T["L"][:], func=AF.Exp), reads=[T["L"]], writes=[T["Ep"]])
                b.op("act", lambda e: e.activation(out=T["Em"][:], in_=T["L"][:], func=AF.Exp, scale=-1.0), reads=[T["L"]], writes=[T["Em"]])
                b.op("act", lambda e: e.activation(out=T["Ex"][:], in_=T["Lx"][:], func=AF.Exp), reads=[T["Lx"]], writes=[T["Ex"]])
                c3 = lambda ap: ap.rearrange("p (c t) -> p c t", t=64)
                b.op("dve", lambda e: e.scalar_tensor_tensor(out=AR[:, :, 0, :], in0=c3(T["kkn"][:]), scalar=-1.0, in1=c3(T["Ex"][:]), op0=ALU.mult, op1=ALU.mult),
                     reads=[T["kkn"], T["Ex"]], writes=[AR])
                tt("pool", AR[:, :, 1, :], c3(R_), c3(T["Ep"][:]), ALU.mult, [X[0], T["Ep"]], [AR])
                tt("dve", T["BT"][:], T["bv"][:], T["Em"][:], ALU.mult, [T["bv"], T["Em"]], [T["BT"]])
                tt("pool", T["KT"][:], T["kp"][:], T["Em"][:], ALU.mult, [T["kp"], T["Em"]], [T["KT"]])
                gC = c3(T["Ep"][:])[:, :, 63:64].to_broadcast([64, NCH, 64])
                tt("dve", c3(T["BG"][:]), c3(T["BT"][:]), gC, ALU.mult, [T["BT"], T["Ep"]], [T["BG"]])
                tt("pool", c3(T["KG"][:]), c3(T["KT"][:]), gC, ALU.mult, [T["KT"], T["Ep"]], [T["KG"]])
                for c in range(NCH):
                    cs = slice(c * 64, (c + 1) * 64)
                    Hc = Hs[h][(gi * NCH + c) % 2]
                    Hn = Hs[h][(gi * NCH + c + 1) % 2]
                    p = nxt("pq", pq)
                    for j, (src, sb_) in enumerate([(V_[:, cs], X[2]), (T["BG"][:, cs], T["BG"]), (T["KG"][:, cs], T["KG"])]):
                        b.op("pe", lambda e: e.transpose(out=p[0:64, j * 64:(j + 1) * 64], in_=src, identity=idf[0:64, 0:64]), reads=[sb_, idf], writes=[p])
                    tm = nxt("tm", TM)
                    b.op("act", lambda e: e.copy(out=tm[:].rearrange("p a b -> p (a b)"), in_=p[0:64, 0:192]), reads=[p], writes=[tm])
                    p = nxt("pq", pq)
                    arc = AR[:, c, :, :].rearrange("p a t -> p (a t)")
                    b.op("pe", lambda e: e.matmul(p[0:64, 0:128], lhsT=T["BT"][:, cs], rhs=arc, start=True, stop=True), reads=[T["BT"], AR], writes=[p])
                    b.op("pe", lambda e: e.matmul(p[0:64, 128:256], lhsT=T["KT"][:, cs], rhs=arc, start=True, stop=True), reads=[T["KT"], AR], writes=[p])
                    b.op("pe", lambda e: e.matmul(p[0:64, 256:320], lhsT=AR[:, c, 0, :], rhs=T["BT"][:, cs], start=True, stop=True), reads=[T["BT"], AR], writes=[p])
                    xm = nxt("xm", XM)
                    tt("dve", xm[:].rearrange("p (a m) t -> p a m t", a=2), p[0:64, 0:256].rearrange("p (a m t) -> p a m t", a=2, m=2),
                       msk[:, None, 0:2, :].to_broadcast([64, 2, 2, 64]), ALU.mult, [p, msk], [xm])
                    aa = nxt("aa", AA)
                    b.op("pool", lambda e: e.tensor_copy(out=aa[:, 0, :], in_=xm[:, 0, :]), reads=[xm], writes=[aa])
                    tt("dve", aa[:, 1, :], p[0:64, 256:320], msk[:, 2, :], ALU.mult, [p, msk], [aa])
                    P_ = nxt("ppb", PP)
                    tt("pool", P_[:], xm[:, 0, :], idf[0:64, 0:64], ALU.add, [xm, idf], [P_])
                    for step in range(5):
                        pdb = nxt("pd", pd)
                        b.op("pe", lambda e: e.matmul(pdb[0:64, 0:64], lhsT=aa[:, 1, :], rhs=aa[:, 0, :], start=True, stop=True), reads=[aa], writes=[pdb])
                        b.op("pe", lambda e: e.matmul(pdb[0:64, 64:128], lhsT=aa[:, 0, :], rhs=aa[:, 1, :], start=True, stop=True), reads=[aa], writes=[pdb])
                        aa2 = nxt("aa", AA)
                        b.op("act", lambda e: e.copy(out=aa2[:].rearrange("p a t -> p (a t)"), in_=pdb[0:64, 0:128]), reads=[pdb], writes=[aa2])
                        b.op("pe", lambda e: e.matmul(pdb[0:64, 128:192], lhsT=aa2[:, 1, :], rhs=P_[:], start=True, stop=True), reads=[aa2, P_], writes=[pdb])
                        P2 = nxt("ppb", PP)
                        tt("dve", P2[:], pdb[0:64, 128:192], P_[:], ALU.add, [pdb, P_], [P2])
                        aa, P_ = aa2, P2
                    b.op("pe", lambda e: e.matmul(pz[0:64, 0:64], lhsT=xm[:, 2, :], rhs=tm[:, 0, :], start=True, stop=False), reads=[xm, tm], writes=[pz])
                    b.op("pe", lambda e: e.matmul(pz[0:64, 0:64], lhsT=AR[:, c, 0, :], rhs=Hc[:], start=False, stop=True), reads=[AR, Hc], writes=[pz])
                    b.op("act", lambda e: e.copy(out=Xs[:], in_=pz[0:64, 0:64]), reads=[pz], writes=[Xs])
                    b.op("pe", lambda e: e.matmul(pz[0:64, 64:128], lhsT=P_[:], rhs=Xs[:], start=True, stop=True), reads=[P_, Xs], writes=[pz])
                    b.op("act", lambda e: e.copy(out=Us[:], in_=pz[0:64, 64:128]), reads=[pz], writes=[Us])
                    b.op("pe", lambda e: e.matmul(pz[0:64, 128:192], lhsT=AR[:, c, 1, :], rhs=Hc[:], start=True, stop=False), reads=[AR, Hc], writes=[pz])
                    b.op("pe", lambda e: e.matmul(pz[0:64, 128:192], lhsT=xm[:, 1, :], rhs=Us[:], start=False, stop=False), reads=[xm, Us], writes=[pz])
                    b.op("pe", lambda e: e.matmul(pz[0:64, 128:192], lhsT=xm[:, 3, :], rhs=tm[:, 0, :], start=False, stop=True), reads=[xm, tm], writes=[pz])
                    b.op("pe", lambda e: e.matmul(pz[0:64, 192:256], lhsT=tm[:, 1, :], rhs=Us[:], start=True, stop=False), reads=[tm, Us], writes=[pz])
                    b.op("pe", lambda e: e.matmul(pz[0:64, 192:256], lhsT=tm[:, 2, :], rhs=tm[:, 0, :], start=False, stop=True), reads=[tm], writes=[pz])
                    b.op("act", lambda e: e.copy(out=Ytm[:, c, h, :], in_=pz[0:64, 128:192]), reads=[pz], writes=[Ytm])
                    b.op("dve", lambda e: e.scalar_tensor_tensor(out=Hn[:], in0=Hc[:], scalar=T["Ep"][:, c * 64 + 63:c * 64 + 64], in1=pz[0:64, 192:256],
                                                                 op0=ALU.mult, op1=ALU.add), reads=[Hc, T["Ep"], pz], writes=[Hn])
            Y3 = Ytm[:].rearrange("p c h i -> p (c h) i")
            S3 = sqv[:].rearrange("p c h i -> p (c h) i")
            b.op("dve", lambda e: e.tensor_reduce(out=st1[:], in_=Y3, axis=AX.X, op=ALU.add), reads=[Ytm], writes=[st1])
            b.op("pool", lambda e: e.tensor_scalar_mul(out=st1[:], in0=st1[:], scalar1=1.0 / 64), reads=[st1], writes=[st1])
            tt("dve", Y3, Y3, st1[:].unsqueeze(2).to_broadcast([64, NCH * 8, 64]), ALU.subtract, [Ytm, st1], [Ytm])
            tt("pool", S3, Y3, Y3, ALU.mult, [Ytm], [sqv])
            b.op("dve", lambda e: e.tensor_reduce(out=st2[:], in_=S3, axis=AX.X, op=ALU.add), reads=[sqv], writes=[st2])
            b.op("act", lambda e: e.activation(out=st2[:], in_=st2[:], func=AF.Sqrt, scale=1.0 / 64, bias=64e-5), reads=[st2], writes=[st2])
            b.op("dve", lambda e: e.reciprocal(out=st2[:], in_=st2[:]), reads=[st2], writes=[st2])
            tt("dve", Y3, Y3, st2[:].unsqueeze(2).to_broadcast([64, NCH * 8, 64]), ALU.mult, [Ytm, st2], [Ytm])
            lg = lng[:].rearrange("p (h i) -> p h i", i=64)[:, None, :, :].to_broadcast([64, NCH, 8, 64])
            lb = lnb[:].rearrange("p (h i) -> p h i", i=64)[:, None, :, :].to_broadcast([64, NCH, 8, 64])
            tt("pool", Ytm[:], Ytm[:], lg, ALU.mult, [Ytm, lng], [Ytm])
            tt("dve", Ytm[:], Ytm[:], lb, ALU.add, [Ytm, lnb], [Ytm])
            for h in range(8):
                p = nxt("pq", pq)
                for c in range(NCH):
                    b.op("pe", lambda e: e.transpose(out=p[0:64, c * 64:(c + 1) * 64], in_=Ytm[:, c, h, :], identity=idf[0:64, 0:64]), reads=[Ytm, idf], writes=[p])
                tt("dve", otmp[:], p[0:64, 0:TG], BV[:, h, :], ALU.add, [p, BV], [otmp])
                pg_ = nxt("pp", pp)
                b.op("pe", lambda e: e.matmul(pg_[0:64, 0:TG], lhsT=g2s[:, 0, h * 64:(h + 1) * 64], rhs=xs[:, 2, :], start=True, stop=False), reads=[g2s, xs], writes=[pg_])
                b.op("pe", lambda e: e.matmul(pg_[0:64, 0:TG], lhsT=g2s[:, 1, h * 64:(h + 1) * 64], rhs=xs[:, 3, :], start=False, stop=True), reads=[g2s, xs], writes=[pg_])
                ob_ = obf[h % 2]
                tt("dve", ob_[:], otmp[:], pg_[0:64, 0:TG], ALU.mult, [otmp, pg_], [ob_])
                b.dma("pool", self.obT_d[h // 2, (h % 2) * 64:(h % 2) * 64 + 64, q0:q0 + TG], ob_[:], reads=[ob_], writes=[self.obT_d])
        if "rwkv" in self.debug:
            d = self.dbg_out("obT", [4, 128, S], BF16)
            b.dma("pool", d, self.obT_d[:], reads=[self.obT_d])


Prog.phase_rwkv = _phase_rwkv


def build_full():
    p = Prog()
    b = p.b
    p.alloc_root()
    with b.scope():
        p.alloc_persistent()
        p.phase_nsa_proj()
        p.phase_attn2()
    p.phase_rwkv3()
    p.phase_merge()
    p.phase_ffn2()
    p.finish()
    return p


def kernel(**inputs):
    p = build_full()
    consts = host_consts(inputs["rel_bias"])
    shared = {k: np.ascontiguousarray(np.asarray(inputs[k], np.float32)) for k in W_SPECS if k != "x"}
    shared.update(consts)
    x = np.asarray(inputs["x"], np.float32)
    in_maps = []
    for c in range(8):
        m = dict(shared)
        m["x"] = np.ascontiguousarray(x[c])
        in_maps.append(m)
    res = run_bass_kernel_spmd(p.nc, in_maps, core_ids=list(range(8)))
    return np.stack([np.asarray(r["out"], np.float32) for r in res.results], axis=0)


def _phase_rwkv2(self):
    b = self.b
    I = self.inp
    TG = 128
    NCH = 2
    tt = lambda eng, out, in0, in1, op, rd, wr: b.op(eng, lambda e: e.tensor_tensor(out=out, in0=in0, in1=in1, op=op), reads=rd, writes=wr)
    with b.scope():
        W1 = b.sb("W1", [128, 8, 1792], BF16)
        W2 = b.sb("W2", [128, 8, 1792], BF16)
        with b.scope():
            gat = self.load_gain("gat3", I["attn_norm_g"][0])
            stage = [b.sb(f"rst{i}", [128, 1792], F32) for i in range(2)]
            tmpw = [b.sb(f"rtw{i}", [128, 1792], F32) for i in range(2)]
            mur = self.bcast_row("mur", I["rwkv_mu"][0], 1792)
            for c in range(8):
                st = stage[c % 2]
                tw_ = tmpw[c % 2]
                b.dma("sp", st[:], I["w_in"][0][c * 128:(c + 1) * 128, RW0:RW0 + 1792], writes=[st])
                tt("dve", tw_[:], st[:], mur[:], ALU.mult, [st, mur], [tw_])
                b.op("act", lambda e: e.activation(out=W2[:, c, :], in_=tw_[:], func=AF.Copy, scale=gat[:, c:c + 1]), reads=[tw_, gat], writes=[W2])
                tt("pool", st[:], st[:], tw_[:], ALU.subtract, [st, tw_], [st])
                b.op("act", lambda e: e.activation(out=W1[:, c, :], in_=st[:], func=AF.Copy, scale=gat[:, c:c + 1]), reads=[st, gat], writes=[W1])

        def colvec(name, src, n):
            t = b.sb(name, [64, n], F32)
            b.dma("sp", t[:], src.rearrange("(c p) -> p c", p=64), writes=[t], allow_slow_non_contiguous=True)
            return t
        w0 = colvec("w0", I["rwkv_w0"][0], 8)
        a0 = colvec("a0", I["rwkv_a0"][0], 8)
        k_k = colvec("k_k", I["rwkv_k_k"][0], 8)
        k_a = colvec("k_a", I["rwkv_k_a"][0], 8)
        r_k = colvec("r_k", I["rwkv_r_k"][0].rearrange("h d -> (h d)"), 8)
        w2s = b.sb("w2s", [64, 512], F32)
        a2s = b.sb("a2s", [64, 512], F32)
        g2s = b.sb("g2s", [64, 2, 512], F32)
        b.dma("sp", w2s[:], I["rwkv_w2"][0], writes=[w2s])
        b.dma("sp", a2s[:], I["rwkv_a2"][0], writes=[a2s])
        b.dma("sp", g2s[:], I["rwkv_g2"][0].rearrange("(two l) f -> l two f", two=2), writes=[g2s])
        lng = b.sb("lng", [64, 512], F32)
        lnb = b.sb("lnb", [64, 512], F32)
        b.dma("sp", lng[:], I["rwkv_ln_g"][0].partition_broadcast(64), writes=[lng])
        b.dma("sp", lnb[:], I["rwkv_ln_b"][0].partition_broadcast(64), writes=[lnb])
        msk = b.sb("rmsk", [64, 3, 64], F32)
        b.dma("sp", msk[:], I["rwmask"], writes=[msk])
        rstm = b.sb("rstm", [64, 8 * TG], F32)
        b.dma("sp", rstm[:], I["rwreset"], writes=[rstm])
        ones = b.sb("ones64", [64, 64], F32)
        b.op("pool", lambda e: e.memset(ones[:], 1.0), writes=[ones])
        idf = self.identf
        Hst = b.sb("rH", [64, 2, 8, 64], F32)
        b.op("pool", lambda e: e.memset(Hst[:], 0.0), writes=[Hst])
        xt = [b.sb(f"rxt{i}", [128, D], F32) for i in range(1)] * 2
        junk = b.sb("rjunk", [128, D], BF16)
        ss = [b.sb(f"rss{i}", [128, 1], F32) for i in range(1)] * 2
        hb = [b.sb(f"rhb{i}", [128, D], BF16) for i in range(1)] * 2
        hT1 = [b.sb(f"rhT{i}", [128, 8, 128], BF16) for i in range(1)] * 2
        hTs = b.sb("rhTs", [128, 8, TG + 1], BF16)
        b.op("pool", lambda e: e.memset(hTs[:], 0.0), writes=[hTs])
        XL = b.sb("rXL", [64, 20, TG], F32)
        Vtm = b.sb("rVtm", [64, NCH, 512], F32)
        names = ["LW", "AS", "KKN", "BVc", "KP", "RK", "L", "EP", "EM", "BG", "KG"]
        T = {n: b.sb("r" + n, [64, 8, TG], F32) for n in names}
        T["NR"] = T["RK"]
        T["T1"] = T["BG"]
        T["KK"] = T["KG"]
        T["EX"] = T["L"]
        T["BT"] = T["LW"]
        T["KT"] = T["AS"]
        AR = b.sb("rAR", [64, 8, NCH, 2, 64], F32)
        BON = b.sb("rBON", [64, NCH * 8], F32)
        Ytm = b.sb("rYtm", [64, NCH, 8, 64], F32)
        sqv = b.sb("rsqv", [64, NCH, 8, 64], F32)
        st1 = b.sb("rst1", [64, NCH * 8], F32)
        st2 = b.sb("rst2", [64, NCH * 8], F32)
        TM4 = [b.sb(f"rTM{i}", [64, 4, 2, 64], F32) for i in range(2)]
        XM4 = [b.sb(f"rXM{i}", [64, 4, 4, 64], F32) for i in range(2)]
        AA4 = [b.sb(f"rAA{i}", [64, 4, 2, 64], F32) for i in range(2)]
        PP4 = [b.sb(f"rPP{i}", [64, 4, 64], F32) for i in range(2)]
        Xs4 = b.sb("rXs4", [64, 4, 64], F32)
        Us4 = b.sb("rUs4", [64, 4, 64], F32)
        Ht4 = b.sb("rHt4", [64, 4, 64], F32)
        OBb = b.sb("rOBb", [64, NCH, 512], BF16)
        obT = [b.sb(f"robT{i}", [128, 4, TG], BF16) for i in range(1)] * 2
        pt = b.ps("rpt", [128, 8, 128], BF16)
        pP = b.ps("rpP", [128, 512], F32)
        pA = b.ps("rpA", [128, 1024], F32)
        pB = b.ps("rpB", [128, 512], F32)
        pC = b.ps("rpC", [128, 512], F32)
        pD = b.ps("rpD", [128, 512], F32)
        pZ = b.ps("rpZ", [128, 512], F32)
        cnt = {}

        def nxt(k, lst):
            cnt[k] = cnt.get(k, 0) + 1
            return lst[cnt[k] % len(lst)]
        bc = lambda v: v[:].unsqueeze(2).to_broadcast([64, 8, TG])
        f2 = lambda t_: t_[:].rearrange("p h t -> p (h t)")
        c16 = lambda t_: t_[:].rearrange("p h (c t) -> p (h c) t", t=64)

        ngr = getattr(self, "nrg_limit", S // TG)
        for gi in range(ngr):
            q0 = gi * TG
            i = gi % 2
            self.make_hT(I["x"], gi, xt[i], junk, ss[i], hb[i], pt, hT1[i], self.ident)
            b.op("pool", lambda e: e.tensor_copy(out=hTs[:, :, 0:1], in_=hTs[:, :, TG:TG + 1]), reads=[hTs], writes=[hTs])
            b.op("pool", lambda e: e.tensor_copy(out=hTs[:, :, 1:TG + 1], in_=hT1[i][:]), reads=[hT1[i]], writes=[hTs])
            ftiles = list(range(0, 16)) + [24, 25, 26, 27]
            for q4 in range(5):
                for j in range(4):
                    fc = ftiles[q4 * 4 + j]
                    for c in range(8):
                        b.op("pe", lambda e: e.matmul(pP[0:64, j * TG:(j + 1) * TG], lhsT=W1[:, c, fc * 64:(fc + 1) * 64], rhs=hTs[:, c, 1:TG + 1], start=(c == 0), stop=False),
                             reads=[W1, hTs], writes=[pP])
                    for c in range(8):
                        b.op("pe", lambda e: e.matmul(pP[0:64, j * TG:(j + 1) * TG], lhsT=W2[:, c, fc * 64:(fc + 1) * 64], rhs=hTs[:, c, 0:TG], start=False, stop=(c == 7)),
                             reads=[W2, hTs], writes=[pP])
                b.op("act", lambda e: e.copy(out=XL[:, q4 * 4:(q4 + 1) * 4, :].rearrange("p a t -> p (a t)"), in_=pP[0:64, :]), reads=[pP], writes=[XL])
            for c_ in range(NCH):
                for c in range(8):
                    b.op("pe", lambda e: e.matmul(pP[0:64, :], lhsT=hTs[:, c, 1 + c_ * 64:1 + (c_ + 1) * 64], rhs=W1[:, c, 1024:1536], start=(c == 0), stop=False),
                         reads=[W1, hTs], writes=[pP])
                for c in range(8):
                    b.op("pe", lambda e: e.matmul(pP[0:64, :], lhsT=hTs[:, c, c_ * 64:(c_ + 1) * 64], rhs=W2[:, c, 1024:1536], start=False, stop=(c == 7)),
                         reads=[W2, hTs], writes=[pP])
                b.op("act", lambda e: e.copy(out=Vtm[:, c_, :], in_=pP[0:64, :]), reads=[pP], writes=[Vtm])
            R_ = XL[:, 0:8, :]
            K_ = XL[:, 8:16, :]
            b.op("act", lambda e: e.activation(out=XL[:, 16, :], in_=XL[:, 16, :], func=AF.Tanh), reads=[XL], writes=[XL])
            b.op("act", lambda e: e.activation(out=XL[:, 18:20, :], in_=XL[:, 18:20, :], func=AF.Sigmoid), reads=[XL], writes=[XL])
            for (ws_, src, bias_, dst) in [(w2s, 16, w0, "LW"), (a2s, 17, a0, "AS")]:
                for half in range(2):
                    for j in range(4):
                        h = half * 4 + j
                        b.op("pe", lambda e: e.matmul(pP[0:64, j * TG:(j + 1) * TG], lhsT=ws_[:, h * 64:(h + 1) * 64], rhs=XL[:, src, :], start=True, stop=True),
                             reads=[ws_, XL], writes=[pP])
                    for j in range(4):
                        h = half * 4 + j
                        b.op("act", lambda e: e.activation(out=T[dst][:, h, :], in_=pP[0:64, j * TG:(j + 1) * TG], func=AF.Sigmoid, bias=bias_[:, h:h + 1]),
                             reads=[pP, bias_], writes=[T[dst]])
            b.op("pool", lambda e: e.tensor_scalar_mul(out=f2(T["LW"]), in0=f2(T["LW"]), scalar1=-0.6065306597126334), reads=[T["LW"]], writes=[T["LW"]])
            tt("dve", T["KK"][:], K_, bc(k_k), ALU.mult, [XL, k_k], [T["KK"]])
            tt("pool", T["NR"][:], T["KK"][:], T["KK"][:], ALU.mult, [T["KK"]], [T["NR"]])
            for half in range(2):
                b.op("pe", lambda e: e.matmul(pP[0:64, :], lhsT=ones[:], rhs=T["NR"][:, half * 4:(half + 1) * 4, :].rearrange("p h t -> p (h t)"), start=True, stop=True),
                     reads=[ones, T["NR"]], writes=[pP])
                b.op("act", lambda e: e.activation(out=T["KKN"][:, half * 4:(half + 1) * 4, :].rearrange("p h t -> p (h t)"), in_=pP[0:64, :], func=AF.Sqrt),
                     reads=[pP], writes=[T["KKN"]])
            b.op("dve", lambda e: e.tensor_scalar_max(out=f2(T["KKN"]), in0=f2(T["KKN"]), scalar1=1e-12), reads=[T["KKN"]], writes=[T["KKN"]])
            b.op("dve", lambda e: e.reciprocal(out=f2(T["KKN"]), in_=f2(T["KKN"])), reads=[T["KKN"]], writes=[T["KKN"]])
            tt("dve", T["KKN"][:], T["KKN"][:], T["KK"][:], ALU.mult, [T["KKN"], T["KK"]], [T["KKN"]])
            tt("pool", T["BVc"][:], T["KKN"][:], T["AS"][:], ALU.mult, [T["KKN"], T["AS"]], [T["BVc"]])
            b.op("pool", lambda e: e.tensor_scalar_add(out=f2(T["T1"]), in0=f2(T["AS"]), scalar1=-1.0), reads=[T["AS"]], writes=[T["T1"]])
            tt("pool", T["T1"][:], T["T1"][:], bc(k_a), ALU.mult, [T["T1"], k_a], [T["T1"]])
            b.op("dve", lambda e: e.scalar_tensor_tensor(out=f2(T["KP"]), in0=f2(T["T1"]), scalar=1.0, in1=K_.rearrange("p h t -> p (h t)"), op0=ALU.add, op1=ALU.mult),
                 reads=[T["T1"], XL], writes=[T["KP"]])
            tt("pool", T["RK"][:], R_, T["KP"][:], ALU.mult, [XL, T["KP"]], [T["RK"]])
            tt("pool", T["RK"][:], T["RK"][:], bc(r_k), ALU.mult, [T["RK"], r_k], [T["RK"]])
            for c_ in range(NCH):
                for h in range(8):
                    b.op("pe", lambda e: e.matmul(pD[0:64, c_ * 8 + h:c_ * 8 + h + 1], lhsT=T["RK"][:, h, c_ * 64:(c_ + 1) * 64], rhs=ones[:, 0:1], start=True, stop=True),
                         reads=[T["RK"], ones], writes=[pD])
            b.op("act", lambda e: e.copy(out=BON[:], in_=pD[0:64, 0:NCH * 8]), reads=[pD], writes=[BON])
            b.op("dve", lambda e: e.tensor_tensor_scan(out=f2(T["L"]), data0=rstm[:], data1=f2(T["LW"]), initial=0.0, op0=ALU.mult, op1=ALU.add),
                 reads=[rstm, T["LW"]], writes=[T["L"]])
            b.op("act", lambda e: e.activation(out=f2(T["EP"]), in_=f2(T["L"]), func=AF.Exp), reads=[T["L"]], writes=[T["EP"]])
            b.op("act", lambda e: e.activation(out=f2(T["EM"]), in_=f2(T["L"]), func=AF.Exp, scale=-1.0), reads=[T["L"]], writes=[T["EM"]])
            tt("pool", T["L"][:], T["L"][:], T["LW"][:], ALU.subtract, [T["L"], T["LW"]], [T["L"]])
            b.op("act", lambda e: e.activation(out=f2(T["EX"]), in_=f2(T["L"]), func=AF.Exp), reads=[T["L"]], writes=[T["EX"]])
            ar0 = AR[:, :, :, 0, :].rearrange("p h c t -> p (h c) t")
            ar1 = AR[:, :, :, 1, :].rearrange("p h c t -> p (h c) t")
            b.op("dve", lambda e: e.scalar_tensor_tensor(out=ar0, in0=c16(T["KKN"]), scalar=-1.0, in1=c16(T["EX"]), op0=ALU.mult, op1=ALU.mult),
                 reads=[T["KKN"], T["EX"]], writes=[AR])
            tt("pool", ar1, R_.rearrange("p h (c t) -> p (h c) t", t=64), c16(T["EP"]), ALU.mult, [XL, T["EP"]], [AR])
            tt("dve", T["BT"][:], T["BVc"][:], T["EM"][:], ALU.mult, [T["BVc"], T["EM"]], [T["BT"]])
            tt("pool", T["KT"][:], T["KP"][:], T["EM"][:], ALU.mult, [T["KP"], T["EM"]], [T["KT"]])
            gC = c16(T["EP"])[:, :, 63:64].to_broadcast([64, 16, 64])
            tt("dve", c16(T["BG"]), c16(T["BT"]), gC, ALU.mult, [T["BT"], T["EP"]], [T["BG"]])
            tt("pool", c16(T["KG"]), c16(T["KT"]), gC, ALU.mult, [T["KT"], T["EP"]], [T["KG"]])
            for c_ in range(NCH):
                cs = slice(c_ * 64, (c_ + 1) * 64)
                cur = (gi * NCH + c_) % 2
                for hb_ in range(2):
                    heads = list(range(hb_ * 4, hb_ * 4 + 4))
                    for j, h in enumerate(heads):
                        b.op("pe", lambda e: e.transpose(out=pC[0:64, j * 128:j * 128 + 64], in_=T["BG"][:, h, cs], identity=idf[0:64, 0:64]), reads=[T["BG"], idf], writes=[pC])
                        b.op("pe", lambda e: e.transpose(out=pC[0:64, j * 128 + 64:(j + 1) * 128], in_=T["KG"][:, h, cs], identity=idf[0:64, 0:64]), reads=[T["KG"], idf], writes=[pC])
                    tm = nxt("tm", TM4)
                    b.op("act", lambda e: e.copy(out=tm[:].rearrange("p h a t -> p (h a t)"), in_=pC[0:64, 0:512]), reads=[pC], writes=[tm])
                    for j, h in enumerate(heads):
                        arc = AR[:, h, c_, :, :].rearrange("p a t -> p (a t)")
                        b.op("pe", lambda e: e.matmul(pA[0:64, j * 256:j * 256 + 128], lhsT=T["BT"][:, h, cs], rhs=arc, start=True, stop=True), reads=[T["BT"], AR], writes=[pA])
                        b.op("pe", lambda e: e.matmul(pA[0:64, j * 256 + 128:(j + 1) * 256], lhsT=T["KT"][:, h, cs], rhs=arc, start=True, stop=True), reads=[T["KT"], AR], writes=[pA])
                        b.op("pe", lambda e: e.matmul(pB[0:64, j * 64:(j + 1) * 64], lhsT=AR[:, h, c_, 0, :], rhs=T["BT"][:, h, cs], start=True, stop=True), reads=[T["BT"], AR], writes=[pB])
                    xm = nxt("xm", XM4)
                    tt("dve", xm[:].rearrange("p h (a m) t -> p (h a) m t", a=2), pA[0:64, :].rearrange("p (ha m t) -> p ha m t", m=2, t=64),
                       msk[:, None, 0:2, :].to_broadcast([64, 8, 2, 64]), ALU.mult, [pA, msk], [xm])
                    aa = nxt("aa", AA4)
                    b.op("pool", lambda e: e.tensor_copy(out=aa[:, :, 0, :], in_=xm[:, :, 0, :]), reads=[xm], writes=[aa])
                    tt("dve", aa[:, :, 1, :], pB[0:64, 0:256].rearrange("p (h t) -> p h t", t=64), msk[:, 2:3, :].to_broadcast([64, 4, 64]), ALU.mult, [pB, msk], [aa])
                    P_ = nxt("pp4", PP4)
                    tt("pool", P_[:], xm[:, :, 0, :], idf[0:64, None, 0:64].to_broadcast([64, 4, 64]), ALU.add, [xm, idf], [P_])
                    for step in range(5):
                        for j in range(4):
                            b.op("pe", lambda e: e.matmul(pD[0:64, j * 128:j * 128 + 64], lhsT=aa[:, j, 1, :], rhs=aa[:, j, 0, :], start=True, stop=True), reads=[aa], writes=[pD])
                            b.op("pe", lambda e: e.matmul(pD[0:64, j * 128 + 64:(j + 1) * 128], lhsT=aa[:, j, 0, :], rhs=aa[:, j, 1, :], start=True, stop=True), reads=[aa], writes=[pD])
                        aa2 = nxt("aa", AA4)
                        b.op("act", lambda e: e.copy(out=aa2[:].rearrange("p h a t -> p (h a t)"), in_=pD[0:64, :]), reads=[pD], writes=[aa2])
                        for j in range(4):
                            b.op("pe", lambda e: e.matmul(pB[0:64, 256 + j * 64:256 + (j + 1) * 64], lhsT=aa2[:, j, 1, :], rhs=P_[:, j, :], start=True, stop=True), reads=[aa2, P_], writes=[pB])
                        P2 = nxt("pp4", PP4)
                        tt("dve", P2[:], pB[0:64, 256:512].rearrange("p (h t) -> p h t", t=64), P_[:], ALU.add, [pB, P_], [P2])
                        aa, P_ = aa2, P2
                    for j, h in enumerate(heads):
                        b.op("pe", lambda e: e.matmul(pZ[0:64, j * 64:(j + 1) * 64], lhsT=xm[:, j, 2, :], rhs=Vtm[:, c_, h * 64:(h + 1) * 64], start=True, stop=False), reads=[xm, Vtm], writes=[pZ])
                        b.op("pe", lambda e: e.matmul(pZ[0:64, j * 64:(j + 1) * 64], lhsT=AR[:, h, c_, 0, :], rhs=Hst[:, cur, h, :], start=False, stop=True), reads=[AR, Hst], writes=[pZ])
                    b.op("act", lambda e: e.copy(out=Xs4[:].rearrange("p h t -> p (h t)"), in_=pZ[0:64, 0:256]), reads=[pZ], writes=[Xs4])
                    for j in range(4):
                        b.op("pe", lambda e: e.matmul(pZ[0:64, 256 + j * 64:256 + (j + 1) * 64], lhsT=P_[:, j, :], rhs=Xs4[:, j, :], start=True, stop=True), reads=[P_, Xs4], writes=[pZ])
                    b.op("act", lambda e: e.copy(out=Us4[:].rearrange("p h t -> p (h t)"), in_=pZ[0:64, 256:512]), reads=[pZ], writes=[Us4])
                    for j, h in enumerate(heads):
                        o = slice(j * 64, (j + 1) * 64)
                        vh = Vtm[:, c_, h * 64:(h + 1) * 64]
                        b.op("pe", lambda e: e.matmul(pZ[0:64, o], lhsT=AR[:, h, c_, 1, :], rhs=Hst[:, cur, h, :], start=True, stop=False), reads=[AR, Hst], writes=[pZ])
                        b.op("pe", lambda e: e.matmul(pZ[0:64, o], lhsT=xm[:, j, 1, :], rhs=Us4[:, j, :], start=False, stop=False), reads=[xm, Us4], writes=[pZ])
                        b.op("pe", lambda e: e.matmul(pZ[0:64, o], lhsT=xm[:, j, 3, :], rhs=vh, start=False, stop=True), reads=[xm, Vtm], writes=[pZ])
                    for j, h in enumerate(heads):
                        o = slice(256 + j * 64, 256 + (j + 1) * 64)
                        vh = Vtm[:, c_, h * 64:(h + 1) * 64]
                        b.op("pe", lambda e: e.matmul(pZ[0:64, o], lhsT=tm[:, j, 0, :], rhs=Us4[:, j, :], start=True, stop=False), reads=[tm, Us4], writes=[pZ])
                        b.op("pe", lambda e: e.matmul(pZ[0:64, o], lhsT=tm[:, j, 1, :], rhs=vh, start=False, stop=True), reads=[tm, Vtm], writes=[pZ])
                    b.op("act", lambda e: e.copy(out=Ytm[:, c_, hb_ * 4:(hb_ + 1) * 4, :].rearrange("p h t -> p (h t)"), in_=pZ[0:64, 0:256]), reads=[pZ], writes=[Ytm])
                    gH = T["EP"][:, hb_ * 4:(hb_ + 1) * 4, c_ * 64 + 63:c_ * 64 + 64].to_broadcast([64, 4, 64])
                    tt("pool", Ht4[:], Hst[:, cur, hb_ * 4:(hb_ + 1) * 4, :], gH, ALU.mult, [Hst, T["EP"]], [Ht4])
                    tt("dve", Hst[:, 1 - cur, hb_ * 4:(hb_ + 1) * 4, :], pZ[0:64, 256:512].rearrange("p (h t) -> p h t", t=64), Ht4[:], ALU.add, [pZ, Ht4], [Hst])
            Y3 = Ytm[:].rearrange("p c h i -> p (c h) i")
            S3 = sqv[:].rearrange("p c h i -> p (c h) i")
            b.op("dve", lambda e: e.tensor_reduce(out=st1[:], in_=Y3, axis=AX.X, op=ALU.add), reads=[Ytm], writes=[st1])
            b.op("pool", lambda e: e.tensor_scalar_mul(out=st1[:], in0=st1[:], scalar1=1.0 / 64), reads=[st1], writes=[st1])
            tt("dve", Y3, Y3, st1[:].unsqueeze(2).to_broadcast([64, NCH * 8, 64]), ALU.subtract, [Ytm, st1], [Ytm])
            tt("pool", S3, Y3, Y3, ALU.mult, [Ytm], [sqv])
            b.op("dve", lambda e: e.tensor_reduce(out=st2[:], in_=S3, axis=AX.X, op=ALU.add), reads=[sqv], writes=[st2])
            b.op("act", lambda e: e.activation(out=st2[:], in_=st2[:], func=AF.Sqrt, scale=1.0 / 64, bias=64e-5), reads=[st2], writes=[st2])
            b.op("dve", lambda e: e.reciprocal(out=st2[:], in_=st2[:]), reads=[st2], writes=[st2])
            tt("dve", Y3, Y3, st2[:].unsqueeze(2).to_broadcast([64, NCH * 8, 64]), ALU.mult, [Ytm, st2], [Ytm])
            lg = lng[:].rearrange("p (h i) -> p h i", i=64)[:, None, :, :].to_broadcast([64, NCH, 8, 64])
            lb = lnb[:].rearrange("p (h i) -> p h i", i=64)[:, None, :, :].to_broadcast([64, NCH, 8, 64])
            tt("pool", Ytm[:], Ytm[:], lg, ALU.mult, [Ytm, lng], [Ytm])
            tt("dve", Ytm[:], Ytm[:], lb, ALU.add, [Ytm, lnb], [Ytm])
            V3 = Vtm[:].rearrange("p c (h i) -> p (c h) i", i=64)
            tt("pool", S3, V3, BON[:].unsqueeze(2).to_broadcast([64, NCH * 8, 64]), ALU.mult, [Vtm, BON], [sqv])
            tt("dve", Y3, Y3, S3, ALU.add, [Ytm, sqv], [Ytm])
            for c_ in range(NCH):
                for two in range(2):
                    b.op("pe", lambda e: e.matmul(pP[0:64, :], lhsT=XL[:, 18 + two, c_ * 64:(c_ + 1) * 64], rhs=g2s[:, two, :], start=(two == 0), stop=(two == 1)),
                         reads=[XL, g2s], writes=[pP])
                tt("dve", OBb[:, c_, :], Ytm[:, c_, :, :].rearrange("p h i -> p (h i)"), pP[0:64, :], ALU.mult, [Ytm, pP], [OBb])
                for k4 in range(4):
                    b.op("pe", lambda e: e.transpose(out=pt[:, k4, c_ * 64:(c_ + 1) * 64], in_=OBb[:, c_, k4 * 128:(k4 + 1) * 128], identity=self.ident[0:64, 0:64]),
                         reads=[OBb, self.ident], writes=[pt])
            ot = obT[gi % 2]
            b.op("act", lambda e: e.copy(out=ot[:], in_=pt[:, 0:4, :]), reads=[pt], writes=[ot])
            b.dma("pool", self.obT_d[:, :, q0:q0 + TG].rearrange("c p t -> p c t"), ot[:], reads=[ot], writes=[self.obT_d])
        if "rwkv" in self.debug:
            d = self.dbg_out("obT", [4, 128, S], BF16)
            b.dma("pool", d, self.obT_d[:], reads=[self.obT_d])


Prog.phase_rwkv2 = _phase_rwkv2


def _phase_rwkv3(self):
    b = self.b
    I = self.inp
    TG = 128
    NCH = 2
    CHDT = mybir.dt.float32r if getattr(self, "use_f32r", True) else F32
    tt = lambda eng, out, in0, in1, op, rd, wr: b.op(eng, lambda e: e.tensor_tensor(out=out, in0=in0, in1=in1, op=op), reads=rd, writes=wr)
    with b.scope():
        W1 = b.sb("W1", [128, 8, 1792], BF16)
        with b.scope():
            gat = self.load_gain("gat3", I["attn_norm_g"][0])
            stage = [b.sb(f"rst{i}", [128, 1792], F32) for i in range(2)]
            self.load_weight(W1, I["w_in"][0][:, RW0:RW0 + 1792], 1792, gvec=gat, stage=stage)

        def colvec(name, src, n):
            t = b.sb(name, [64, n], F32)
            b.dma("sp", t[:], src.rearrange("(c p) -> p c", p=64), writes=[t], allow_slow_non_contiguous=True)
            return t
        mu = colvec("mu", I["rwkv_mu"][0], 28)
        w0 = colvec("w0", I["rwkv_w0"][0], 8)
        a0 = colvec("a0", I["rwkv_a0"][0], 8)
        k_k = colvec("k_k", I["rwkv_k_k"][0], 8)
        k_a = colvec("k_a", I["rwkv_k_a"][0], 8)
        r_k = colvec("r_k", I["rwkv_r_k"][0].rearrange("h d -> (h d)"), 8)
        w2s = b.sb("w2s", [64, 512], F32)
        a2s = b.sb("a2s", [64, 512], F32)
        g2s = b.sb("g2s", [64, 2, 512], F32)
        b.dma("sp", w2s[:], I["rwkv_w2"][0], writes=[w2s])
        b.dma("sp", a2s[:], I["rwkv_a2"][0], writes=[a2s])
        b.dma("sp", g2s[:], I["rwkv_g2"][0].rearrange("(two l) f -> l two f", two=2), writes=[g2s])
        lng = b.sb("lng", [64, 512], F32)
        lnb = b.sb("lnb", [64, 512], F32)
        b.dma("sp", lng[:], I["rwkv_ln_g"][0].partition_broadcast(64), writes=[lng])
        b.dma("sp", lnb[:], I["rwkv_ln_b"][0].partition_broadcast(64), writes=[lnb])
        msk = b.sb("rmsk", [64, 3, 64], F32)
        b.dma("sp", msk[:], I["rwmask"], writes=[msk])
        rstm = b.sb("rstm", [64, 8 * TG], F32)
        b.dma("sp", rstm[:], I["rwreset"], writes=[rstm])
        ones = b.sb("ones64", [64, 64], F32)
        b.op("pool", lambda e: e.memset(ones[:], 1.0), writes=[ones])
        idf = self.identf
        Hst = b.sb("rH", [64, 2, 8, 64], CHDT)
        b.op("pool", lambda e: e.memset(Hst[:].bitcast(F32), 0.0), writes=[Hst])
        xt = [b.sb(f"rxt{i}", [128, D], F32) for i in range(1)] * 2
        junk = b.sb("rjunk", [128, D], BF16)
        ss = [b.sb(f"rss{i}", [128, 1], F32) for i in range(1)] * 2
        hb = [b.sb(f"rhb{i}", [128, D], BF16) for i in range(1)] * 2
        hT1 = [b.sb(f"rhT{i}", [128, 8, 128], BF16) for i in range(1)] * 2
        PB = b.sb("rPB", [64, 28, TG + 1], F32)
        b.op("pool", lambda e: e.memset(PB[:], 0.0), writes=[PB])
        XL = b.sb("rXL", [64, 28, TG], F32)
        VT2 = [b.sb(f"rVtm{i}", [64, NCH, 512], CHDT) for i in range(2)]
        SXG2 = [b.sb(f"rSXG{i}", [64, 2, TG], F32) for i in range(2)]
        names = ["LW", "AS", "KKN", "BVc", "KP", "RK", "L", "EP", "EM", "BG", "KG"]
        T = {n: b.sb("r" + n, [64, 8, TG], F32) for n in names}
        T["NR"] = T["RK"]
        T["T1"] = T["BG"]
        T["KK"] = T["KG"]
        T["EX"] = T["L"]
        T["BT"] = b.sb("rBTr", [64, 8, TG], CHDT)
        T["KT"] = b.sb("rKTr", [64, 8, TG], CHDT)
        AR = b.sb("rAR", [64, 8, NCH, 2, 64], CHDT)
        BON2 = [b.sb(f"rBON{i}", [64, NCH * 8], F32) for i in range(2)]
        Ytm = b.sb("rYtm", [64, NCH, 8, 64], F32)
        sqv = b.sb("rsqv", [64, NCH, 8, 64], F32)
        st1 = b.sb("rst1", [64, NCH * 8], F32)
        st2 = b.sb("rst2", [64, NCH * 8], F32)
        TM4 = [b.sb(f"rTM{i}", [64, 4, 2, 64], CHDT) for i in range(2)]
        XM4 = [b.sb(f"rXM{i}", [64, 4, 4, 64], CHDT) for i in range(2)]
        AA4 = [[b.sb(f"rAA{u}_{i}", [64, 4, 2, 64], CHDT) for i in range(2)] for u in range(2)]
        PP4 = [[b.sb(f"rPP{u}_{i}", [64, 4, 64], CHDT) for i in range(2)] for u in range(2)]
        PF = [b.sb(f"rPF{u}", [64, 4, 64], CHDT) for u in range(2)]
        Xs8 = b.sb("rXs8", [64, 8, 64], CHDT)
        Us8 = b.sb("rUs8", [64, 8, 64], CHDT)
        Ht8 = b.sb("rHt8", [64, 8, 64], F32)
        OBb = b.sb("rOBb", [64, NCH, 512], BF16)
        obT = [b.sb(f"robT{i}", [128, 4, TG], BF16) for i in range(1)] * 2
        pt = b.ps("rpt", [128, 8, 128], BF16)
        pP = b.ps("rpP", [128, 512], F32)
        pA = b.ps("rpA", [128, 1024], F32)
        pB = b.ps("rpB", [128, 512], F32)
        pC = b.ps("rpC", [128, 512], F32)
        pD = b.ps("rpD", [128, 512], F32)
        pZ = b.ps("rpZ", [128, 512], F32)
        cnt = {}

        def nxt(k, lst):
            cnt[k] = cnt.get(k, 0) + 1
            return lst[cnt[k] % len(lst)]
        bc = lambda v: v[:].unsqueeze(2).to_broadcast([64, 8, TG])
        f2 = lambda t_: t_[:].rearrange("p h t -> p (h t)")
        c16 = lambda t_: t_[:].rearrange("p h (c t) -> p (h c) t", t=64)

        ngr = getattr(self, "nrg_limit", S // TG)
        RR = lambda ap: ap

        def emit_inproj_head(gi):
            i = gi % 2
            self.make_hT(I["x"], gi, xt[i], junk, ss[i], hb[i], pt, hT1[i], self.ident)
            b.op("dve", lambda e: e.tensor_copy(out=PB[:, :, 0:1], in_=PB[:, :, TG:TG + 1]), reads=[PB], writes=[PB])

        def emit_inproj_rounds(gi, rounds):
            i = gi % 2
            for r7 in rounds:
                for j in range(4):
                    fc = r7 * 4 + j
                    for c in range(8):
                        b.op("pe", lambda e: e.matmul(pP[0:64, j * TG:(j + 1) * TG], lhsT=W1[:, c, fc * 64:(fc + 1) * 64], rhs=hT1[i][:, c, :], start=(c == 0), stop=(c == 7)),
                             reads=[W1, hT1[i]], writes=[pP])
                b.op("act", lambda e: e.copy(out=PB[:, r7 * 4:(r7 + 1) * 4, 1:TG + 1], in_=pP[0:64, :].rearrange("p (a t) -> p a t", t=TG)), reads=[pP], writes=[PB])

        emit_inproj_head(0)
        emit_inproj_rounds(0, range(7))
        def prep(gi, hook=None):
            Vtm, BON, SXG = VT2[gi % 2], BON2[gi % 2], SXG2[gi % 2]
            tt("dve", XL[:], PB[:, :, 0:TG], PB[:, :, 1:TG + 1], ALU.subtract, [PB], [XL])
            tt("dve", XL[:], XL[:], mu[:].unsqueeze(2).to_broadcast([64, 28, TG]), ALU.mult, [XL, mu], [XL])
            tt("dve", XL[:], XL[:], PB[:, :, 1:TG + 1], ALU.add, [XL, PB], [XL])
            if gi + 1 < ngr:
                emit_inproj_head(gi + 1)
            for c_ in range(NCH):
                for h in range(8):
                    b.op("pe", lambda e: e.transpose(out=pC[0:64, h * 64:(h + 1) * 64], in_=XL[:, 16 + h, c_ * 64:(c_ + 1) * 64], identity=idf[0:64, 0:64]), reads=[XL, idf], writes=[pC])
                b.op("act", lambda e: e.copy(out=Vtm[:, c_, :], in_=pC[0:64, :]), reads=[pC], writes=[Vtm])
            R_ = XL[:, 0:8, :]
            K_ = XL[:, 8:16, :]
            b.op("act", lambda e: e.activation(out=XL[:, 24, :], in_=XL[:, 24, :], func=AF.Tanh), reads=[XL], writes=[XL])
            b.op("act", lambda e: e.activation(out=SXG[:], in_=XL[:, 26:28, :], func=AF.Sigmoid), reads=[XL], writes=[SXG])
            for (ws_, src, bias_, dst) in [(w2s, 24, w0, "LW"), (a2s, 25, a0, "AS")]:
                for half in range(2):
                    for j in range(4):
                        h = half * 4 + j
                        b.op("pe", lambda e: e.matmul(pP[0:64, j * TG:(j + 1) * TG], lhsT=ws_[:, h * 64:(h + 1) * 64], rhs=XL[:, src, :], start=True, stop=True),
                             reads=[ws_, XL], writes=[pP])
                    for j in range(4):
                        h = half * 4 + j
                        b.op("act", lambda e: e.activation(out=T[dst][:, h, :], in_=pP[0:64, j * TG:(j + 1) * TG], func=AF.Sigmoid, bias=bias_[:, h:h + 1]),
                             reads=[pP, bias_], writes=[T[dst]])
            b.op("dve", lambda e: e.tensor_scalar_mul(out=f2(T["LW"]), in0=f2(T["LW"]), scalar1=-0.6065306597126334), reads=[T["LW"]], writes=[T["LW"]])
            tt("dve", T["KK"][:], K_, bc(k_k), ALU.mult, [XL, k_k], [T["KK"]])
            tt("dve", T["NR"][:], T["KK"][:], T["KK"][:], ALU.mult, [T["KK"]], [T["NR"]])
            for half in range(2):
                b.op("pe", lambda e: e.matmul(pP[0:64, :], lhsT=ones[:], rhs=T["NR"][:, half * 4:(half + 1) * 4, :].rearrange("p h t -> p (h t)"), start=True, stop=True),
                     reads=[ones, T["NR"]], writes=[pP])
                b.op("act", lambda e: e.activation(out=T["KKN"][:, half * 4:(half + 1) * 4, :].rearrange("p h t -> p (h t)"), in_=pP[0:64, :], func=AF.Sqrt),
                     reads=[pP], writes=[T["KKN"]])
            if gi + 1 < ngr:
                emit_inproj_rounds(gi + 1, range(0, 4))
            b.op("dve", lambda e: e.tensor_scalar_max(out=f2(T["KKN"]), in0=f2(T["KKN"]), scalar1=1e-12), reads=[T["KKN"]], writes=[T["KKN"]])
            b.op("dve", lambda e: e.reciprocal(out=f2(T["KKN"]), in_=f2(T["KKN"])), reads=[T["KKN"]], writes=[T["KKN"]])
            tt("dve", T["KKN"][:], T["KKN"][:], T["KK"][:], ALU.mult, [T["KKN"], T["KK"]], [T["KKN"]])
            tt("dve", T["BVc"][:], T["KKN"][:], T["AS"][:], ALU.mult, [T["KKN"], T["AS"]], [T["BVc"]])
            b.op("dve", lambda e: e.tensor_scalar_add(out=f2(T["T1"]), in0=f2(T["AS"]), scalar1=-1.0), reads=[T["AS"]], writes=[T["T1"]])
            tt("dve", T["T1"][:], T["T1"][:], bc(k_a), ALU.mult, [T["T1"], k_a], [T["T1"]])
            b.op("dve", lambda e: e.scalar_tensor_tensor(out=f2(T["KP"]), in0=f2(T["T1"]), scalar=1.0, in1=K_.rearrange("p h t -> p (h t)"), op0=ALU.add, op1=ALU.mult),
                 reads=[T["T1"], XL], writes=[T["KP"]])
            tt("dve", T["RK"][:], R_, T["KP"][:], ALU.mult, [XL, T["KP"]], [T["RK"]])
            tt("dve", T["RK"][:], T["RK"][:], bc(r_k), ALU.mult, [T["RK"], r_k], [T["RK"]])
            for c_ in range(NCH):
                for h in range(8):
                    b.op("pe", lambda e: e.matmul(pD[0:64, c_ * 8 + h:c_ * 8 + h + 1], lhsT=T["RK"][:, h, c_ * 64:(c_ + 1) * 64], rhs=ones[:, 0:1], start=True, stop=True),
                         reads=[T["RK"], ones], writes=[pD])
            b.op("act", lambda e: e.copy(out=BON[:], in_=pD[0:64, 0:NCH * 8]), reads=[pD], writes=[BON])
            if gi + 1 < ngr:
                emit_inproj_rounds(gi + 1, range(4, 7))
            b.op("dve", lambda e: e.tensor_tensor_scan(out=f2(T["L"]), data0=rstm[:], data1=f2(T["LW"]), initial=0.0, op0=ALU.mult, op1=ALU.add),
                 reads=[rstm, T["LW"]], writes=[T["L"]])
            b.op("act", lambda e: e.activation(out=f2(T["EP"]), in_=f2(T["L"]), func=AF.Exp), reads=[T["L"]], writes=[T["EP"]])
            b.op("act", lambda e: e.activation(out=f2(T["EM"]), in_=f2(T["L"]), func=AF.Exp, scale=-1.0), reads=[T["L"]], writes=[T["EM"]])
            tt("dve", T["L"][:], T["L"][:], T["LW"][:], ALU.subtract, [T["L"], T["LW"]], [T["L"]])
            b.op("act", lambda e: e.activation(out=f2(T["EX"]), in_=f2(T["L"]), func=AF.Exp), reads=[T["L"]], writes=[T["EX"]])
            ar0 = AR[:, :, :, 0, :].rearrange("p h c t -> p (h c) t")
            ar1 = AR[:, :, :, 1, :].rearrange("p h c t -> p (h c) t")
            b.op("dve", lambda e: e.scalar_tensor_tensor(out=ar0, in0=c16(T["KKN"]), scalar=-1.0, in1=c16(T["EX"]), op0=ALU.mult, op1=ALU.mult),
                 reads=[T["KKN"], T["EX"]], writes=[AR])
            tt("dve", ar1, R_.rearrange("p h (c t) -> p (h c) t", t=64), c16(T["EP"]), ALU.mult, [XL, T["EP"]], [AR])
            tt("dve", T["BT"][:], T["BVc"][:], T["EM"][:], ALU.mult, [T["BVc"], T["EM"]], [T["BT"]])
            tt("dve", T["KT"][:], T["KP"][:], T["EM"][:], ALU.mult, [T["KP"], T["EM"]], [T["KT"]])
            gC = c16(T["EP"])[:, :, 63:64].to_broadcast([64, 16, 64])
            tt("dve", c16(T["BG"]), c16(T["BT"]), gC, ALU.mult, [T["BT"], T["EP"]], [T["BG"]])
            tt("dve", c16(T["KG"]), c16(T["KT"]), gC, ALU.mult, [T["KT"], T["EP"]], [T["KG"]])

        def chains(gi, hook=None):
            Vtm, BON, SXG = VT2[gi % 2], BON2[gi % 2], SXG2[gi % 2]
            for c_ in range(NCH):
                cs = slice(c_ * 64, (c_ + 1) * 64)
                cur = (gi * NCH + c_) % 2
                U_ = []
                for u in range(2):
                    heads = list(range(u * 4, u * 4 + 4))
                    pBu = pB if u == 0 else pC
                    for j, h in enumerate(heads):
                        b.op("pe", lambda e: e.transpose(out=pZ[0:64, j * 128:j * 128 + 64], in_=T["BG"][:, h, cs], identity=idf[0:64, 0:64]), reads=[T["BG"], idf], writes=[pZ])
                        b.op("pe", lambda e: e.transpose(out=pZ[0:64, j * 128 + 64:(j + 1) * 128], in_=T["KG"][:, h, cs], identity=idf[0:64, 0:64]), reads=[T["KG"], idf], writes=[pZ])
                    tm = TM4[u]
                    b.op("act", lambda e: e.copy(out=tm[:].rearrange("p h a t -> p (h a t)"), in_=pZ[0:64, 0:512]), reads=[pZ], writes=[tm])
                    for j, h in enumerate(heads):
                        arc = AR[:, h, c_, :, :].rearrange("p a t -> p (a t)")
                        b.op("pe", lambda e: e.matmul(pA[0:64, j * 256:j * 256 + 128], lhsT=T["BT"][:, h, cs], rhs=arc, start=True, stop=True), reads=[T["BT"], AR], writes=[pA])
                        b.op("pe", lambda e: e.matmul(pA[0:64, j * 256 + 128:(j + 1) * 256], lhsT=T["KT"][:, h, cs], rhs=arc, start=True, stop=True), reads=[T["KT"], AR], writes=[pA])
                        b.op("pe", lambda e: e.matmul(pBu[0:64, j * 64:(j + 1) * 64], lhsT=AR[:, h, c_, 0, :], rhs=T["BT"][:, h, cs], start=True, stop=True), reads=[T["BT"], AR], writes=[pBu])
                    xm = XM4[u]
                    tt("dve", xm[:].rearrange("p h (a m) t -> p (h a) m t", a=2), pA[0:64, :].rearrange("p (ha m t) -> p ha m t", m=2, t=64),
                       msk[:, None, 0:2, :].to_broadcast([64, 8, 2, 64]), ALU.mult, [pA, msk], [xm])
                    ap_ = AA4[u][0]
                    b.op("dve", lambda e: e.tensor_copy(out=ap_[:, :, 0, :], in_=xm[:, :, 0, :]), reads=[xm], writes=[ap_])
                    b.op("dve", lambda e: e.tensor_copy(out=ap_[:, :, 1, :], in_=idf[0:64, None, 0:64].to_broadcast([64, 4, 64])), reads=[idf], writes=[ap_])
                    at_ = PP4[u][0]
                    tt("dve", at_[:], pBu[0:64, 0:256].rearrange("p (h t) -> p h t", t=64), msk[:, 2:3, :].to_broadcast([64, 4, 64]), ALU.mult, [pBu, msk], [at_])
                    U_.append(dict(tm=tm, xm=xm, ap=ap_, at=at_, pB=pBu, pD=(pD if u == 0 else pP), k=0))
                for step in range(5):
                    if hook is not None and c_ == 0:
                        next(hook, None)
                        next(hook, None)
                    for u_ in U_:
                        ap_, at_, pDu, pBu = u_["ap"], u_["at"], u_["pD"], u_["pB"]
                        for j in range(4):
                            b.op("pe", lambda e: e.matmul(pDu[0:64, j * 128:(j + 1) * 128], lhsT=at_[:, j, :], rhs=ap_[:, j, :, :].rearrange("p a t -> p (a t)"), start=True, stop=True),
                                 reads=[at_, ap_], writes=[pDu])
                            b.op("pe", lambda e: e.matmul(pBu[0:64, 256 + j * 64:256 + (j + 1) * 64], lhsT=ap_[:, j, 0, :], rhs=at_[:, j, :], start=True, stop=True),
                                 reads=[at_, ap_], writes=[pBu])
                    for ui, u_ in enumerate(U_):
                        u_["k"] += 1
                        ap2, at2 = AA4[ui][u_["k"] % 2], PP4[ui][u_["k"] % 2]
                        pDv = u_["pD"][0:64, :].rearrange("p (h a t) -> p h a t", a=2, t=64)
                        b.op("act", lambda e: e.copy(out=ap2[:, :, 0, :], in_=pDv[:, :, 0, :]), reads=[u_["pD"]], writes=[ap2])
                        tt("dve", ap2[:, :, 1, :], pDv[:, :, 1, :], u_["ap"][:, :, 1, :], ALU.add, [u_["pD"], u_["ap"]], [ap2])
                        b.op("act", lambda e: e.copy(out=at2[:], in_=u_["pB"][0:64, 256:512].rearrange("p (h t) -> p h t", t=64)), reads=[u_["pB"]], writes=[at2])
                        u_["ap"], u_["at"] = ap2, at2
                for u_ in U_:
                    for j in range(4):
                        b.op("pe", lambda e: e.matmul(u_["pB"][0:64, 256 + j * 64:256 + (j + 1) * 64], lhsT=u_["at"][:, j, :], rhs=u_["ap"][:, j, 1, :], start=True, stop=True),
                             reads=[u_["at"], u_["ap"]], writes=[u_["pB"]])
                for ui, u_ in enumerate(U_):
                    pf = PF[ui]
                    tt("dve", pf[:], u_["pB"][0:64, 256:512].rearrange("p (h t) -> p h t", t=64), u_["ap"][:, :, 1, :], ALU.add, [u_["pB"], u_["ap"]], [pf])
                    u_["P"] = pf
                if hook is not None and c_ == 0:
                    for _ in hook:
                        pass
                for h in range(8):
                    u_, j = U_[h // 4], h % 4
                    o = slice(h * 64, (h + 1) * 64)
                    b.op("pe", lambda e: e.matmul(pA[0:64, o], lhsT=u_["xm"][:, j, 2, :], rhs=Vtm[:, c_, o], start=True, stop=False), reads=[u_["xm"], Vtm], writes=[pA])
                    b.op("pe", lambda e: e.matmul(pA[0:64, o], lhsT=AR[:, h, c_, 0, :], rhs=Hst[:, cur, h, :], start=False, stop=True), reads=[AR, Hst], writes=[pA])
                b.op("act", lambda e: e.copy(out=Xs8[:].rearrange("p h t -> p (h t)"), in_=pA[0:64, 0:512]), reads=[pA], writes=[Xs8])
                for h in range(8):
                    u_, j = U_[h // 4], h % 4
                    b.op("pe", lambda e: e.matmul(pA[0:64, 512 + h * 64:512 + (h + 1) * 64], lhsT=u_["P"][:, j, :], rhs=Xs8[:, h, :], start=True, stop=True), reads=[u_["P"], Xs8], writes=[pA])
                b.op("act", lambda e: e.copy(out=Us8[:].rearrange("p h t -> p (h t)"), in_=pA[0:64, 512:1024]), reads=[pA], writes=[Us8])
                for h in range(8):
                    u_, j = U_[h // 4], h % 4
                    o = slice(h * 64, (h + 1) * 64)
                    b.op("pe", lambda e: e.matmul(pD[0:64, o], lhsT=u_["tm"][:, j, 0, :], rhs=Us8[:, h, :], start=True, stop=False), reads=[u_["tm"], Us8], writes=[pD])
                    b.op("pe", lambda e: e.matmul(pD[0:64, o], lhsT=u_["tm"][:, j, 1, :], rhs=Vtm[:, c_, o], start=False, stop=True), reads=[u_["tm"], Vtm], writes=[pD])
                tt("dve", Ht8[:], Hst[:, cur, :, :], T["EP"][:, :, c_ * 64 + 63:c_ * 64 + 64].to_broadcast([64, 8, 64]), ALU.mult, [Hst, T["EP"]], [Ht8])
                tt("dve", Hst[:, 1 - cur, :, :], pD[0:64, :].rearrange("p (h t) -> p h t", t=64), Ht8[:], ALU.add, [pD, Ht8], [Hst])
                for h in range(8):
                    u_, j = U_[h // 4], h % 4
                    o = slice(h * 64, (h + 1) * 64)
                    b.op("pe", lambda e: e.matmul(pZ[0:64, o], lhsT=AR[:, h, c_, 1, :], rhs=Hst[:, cur, h, :], start=True, stop=False), reads=[AR, Hst], writes=[pZ])
                    b.op("pe", lambda e: e.matmul(pZ[0:64, o], lhsT=u_["xm"][:, j, 1, :], rhs=Us8[:, h, :], start=False, stop=False), reads=[u_["xm"], Us8], writes=[pZ])
                    b.op("pe", lambda e: e.matmul(pZ[0:64, o], lhsT=u_["xm"][:, j, 3, :], rhs=Vtm[:, c_, o], start=False, stop=True), reads=[u_["xm"], Vtm], writes=[pZ])
                b.op("act", lambda e: e.copy(out=Ytm[:, c_, :, :].rearrange("p h t -> p (h t)"), in_=pZ[0:64, :]), reads=[pZ], writes=[Ytm])

        def post(gi):
            q0 = gi * TG
            Vtm, BON, SXG = VT2[gi % 2], BON2[gi % 2], SXG2[gi % 2]
            Y3 = Ytm[:].rearrange("p c h i -> p (c h) i")
            S3 = sqv[:].rearrange("p c h i -> p (c h) i")
            b.op("dve", lambda e: e.tensor_reduce(out=st1[:], in_=Y3, axis=AX.X, op=ALU.add), reads=[Ytm], writes=[st1])
            b.op("dve", lambda e: e.tensor_scalar_mul(out=st1[:], in0=st1[:], scalar1=1.0 / 64), reads=[st1], writes=[st1])
            yield
            tt("dve", Y3, Y3, st1[:].unsqueeze(2).to_broadcast([64, NCH * 8, 64]), ALU.subtract, [Ytm, st1], [Ytm])
            tt("dve", S3, Y3, Y3, ALU.mult, [Ytm], [sqv])
            yield
            b.op("dve", lambda e: e.tensor_reduce(out=st2[:], in_=S3, axis=AX.X, op=ALU.add), reads=[sqv], writes=[st2])
            b.op("act", lambda e: e.activation(out=st2[:], in_=st2[:], func=AF.Sqrt, scale=1.0 / 64, bias=64e-5), reads=[st2], writes=[st2])
            yield
            b.op("dve", lambda e: e.reciprocal(out=st2[:], in_=st2[:]), reads=[st2], writes=[st2])
            tt("dve", Y3, Y3, st2[:].unsqueeze(2).to_broadcast([64, NCH * 8, 64]), ALU.mult, [Ytm, st2], [Ytm])
            yield
            lg = lng[:].rearrange("p (h i) -> p h i", i=64)[:, None, :, :].to_broadcast([64, NCH, 8, 64])
            lb = lnb[:].rearrange("p (h i) -> p h i", i=64)[:, None, :, :].to_broadcast([64, NCH, 8, 64])
            tt("dve", Ytm[:], Ytm[:], lg, ALU.mult, [Ytm, lng], [Ytm])
            tt("dve", Ytm[:], Ytm[:], lb, ALU.add, [Ytm, lnb], [Ytm])
            yield
            V3 = Vtm[:].rearrange("p c (h i) -> p (c h) i", i=64)
            tt("dve", S3, V3, BON[:].unsqueeze(2).to_broadcast([64, NCH * 8, 64]), ALU.mult, [Vtm, BON], [sqv])
            tt("dve", Y3, Y3, S3, ALU.add, [Ytm, sqv], [Ytm])
            yield
            for c_ in range(NCH):
                for two in range(2):
                    b.op("pe", lambda e: e.matmul(pP[0:64, :], lhsT=SXG[:, two, c_ * 64:(c_ + 1) * 64], rhs=g2s[:, two, :], start=(two == 0), stop=(two == 1)),
                         reads=[SXG, g2s], writes=[pP])
                tt("dve", OBb[:, c_, :], Ytm[:, c_, :, :].rearrange("p h i -> p (h i)"), pP[0:64, :], ALU.mult, [Ytm, pP], [OBb])
                for k4 in range(4):
                    b.op("pe", lambda e: e.transpose(out=pt[:, k4, c_ * 64:(c_ + 1) * 64], in_=OBb[:, c_, k4 * 128:(k4 + 1) * 128], identity=self.ident[0:64, 0:64]),
                         reads=[OBb, self.ident], writes=[pt])
                yield
            ot = obT[gi % 2]
            b.op("act", lambda e: e.copy(out=ot[:], in_=pt[:, 0:4, :]), reads=[pt], writes=[ot])
            b.dma("pool", self.obT_d[:, :, q0:q0 + TG].rearrange("c p t -> p c t"), ot[:], reads=[ot], writes=[self.obT_d])

        prep(0)
        for gi in range(ngr):
            pg = post(gi - 1) if gi > 0 else None
            chains(gi, hook=pg)
            if pg is not None:
                for _ in pg:
                    pass
            if gi + 1 < ngr:
                prep(gi + 1)
        for _ in post(ngr - 1):
            pass
        if "rwkv" in self.debug:
            d = self.dbg_out("obT", [4, 128, S], BF16)
            b.dma("pool", d, self.obT_d[:], reads=[self.obT_d])


Prog.phase_rwkv3 = _phase_rwkv3


def _phase_ffn2(self):
    b = self.b
    I = self.inp
    TG = 256
    NFT = 44
    with b.scope():
        gf = self.load_gain("gf", I["ffn_norm_g"][0])
        stage = [b.sb(f"fst{i}", [128, 1024], F32) for i in range(2)]
        wu = b.sb("wu", [128, 8, 2 * DFF], BF16)
        for n in range(8):
            for c in range(8):
                st = stage[c % 2]
                b.dma("sp", st[:, 0:704], I["w_up"][0][c * 128:(c + 1) * 128, n * 704:(n + 1) * 704], writes=[st])
                b.op("act", lambda e: e.activation(out=wu[:, c, n * 704:(n + 1) * 704], in_=st[:, 0:704], func=AF.Copy, scale=gf[:, c:c + 1]),
                     reads=[st, gf], writes=[wu])
        wd = b.sb("wd", [128, 22, D], BF16)
        self.load_weight(wd, I["w_down"][0], D, kch=22, stage=stage, eng="dve")
        cw = b.sb("cw", [128, 3, NFT], F32)
        for j in range(3):
            b.dma("sp", cw[:, j, :], I["conv_w"][0][j].rearrange("(c p) -> p c", p=128), writes=[cw], allow_slow_non_contiguous=True)
        cbias = self.load_gain("cbias", I["conv_b"][0], kch=NFT)
        xt = [b.sb(f"fxt{i}", [128, D], F32) for i in range(4)]
        junk = b.sb("fjunk", [128, D], BF16)
        ss = [b.sb(f"fss{i}", [128, 1], F32) for i in range(2)]
        hb = [b.sb(f"fhb{i}", [128, D], BF16) for i in range(2)]
        hTg2 = [b.sb(f"fhTg{i}", [128, 8, TG + 2], BF16) for i in range(2)]
        for i_ in range(2):
            b.op("pool", lambda e: e.memset(hTg2[i_][:], 0.0), writes=[hTg2[i_]])
        cv = [b.sb(f"cv{i}", [128, TG], F32) for i in range(5)]
        sgl = [b.sb(f"sgl{i}", [128, TG], BF16) for i in range(4)]
        actT = b.sb("actT", [128, 22, TG], BF16)
        val = b.sb("fval", [128, 22, TG], BF16)
        pt = b.ps("fpt", [128, 8, 128], BF16)
        pu = [b.ps(f"fpu{i}", [128, 512], F32) for i in range(4)]
        pd = [b.ps(f"fpd{i}", [128, 512], F32) for i in range(2)]
        ng = getattr(self, "nt_limit", NT) * 128 // TG
        def stageH(gi):
            hTg = hTg2[gi % 2]
            if gi > 0:
                prev = hTg2[(gi - 1) % 2]
                b.op("pool", lambda e: e.tensor_copy(out=hTg[:, :, 0:2], in_=prev[:, :, TG:TG + 2]), reads=[prev], writes=[hTg])
            for s_ in range(TG // 128):
                t = gi * (TG // 128) + s_
                xi = (gi % 2) * 2 + s_
                self.make_hT(self.x1_d, t, xt[xi], junk, ss[s_], hb[s_], pt, hTg, self.ident, hT_ap=hTg[:, :, 2 + s_ * 128:2 + (s_ + 1) * 128])

        stageH(0)
        for gi in range(ng):
            hTg = hTg2[gi % 2]
            if gi + 1 < ng:
                stageH(gi + 1)
            pending = []
            for ft in range(NFT):
                p = pu[ft % 4]
                c_ = cv[ft % 5]
                for c in range(8):
                    b.op("pe", lambda e: e.matmul(p[:, 0:TG + 2], lhsT=wu[:, c, ft * 128:(ft + 1) * 128], rhs=hTg[:, c, :], start=(c == 0), stop=(c == 7)),
                         reads=[wu, hTg], writes=[p])
                b.op("act", lambda e: e.activation(out=c_[:], in_=p[:, 0:TG], func=AF.Identity, scale=cw[:, 0, ft:ft + 1], bias=cbias[:, ft:ft + 1]),
                     reads=[p, cw, cbias], writes=[c_])
                b.op("dve", lambda e: e.scalar_tensor_tensor(out=c_[:], in0=p[:, 1:TG + 1], scalar=cw[:, 1, ft:ft + 1], in1=c_[:], op0=ALU.mult, op1=ALU.add),
                     reads=[p, cw, c_], writes=[c_])
                if ft < 22:
                    b.op("dve", lambda e: e.scalar_tensor_tensor(out=val[:, ft, :], in0=p[:, 2:TG + 2], scalar=cw[:, 2, ft:ft + 1], in1=c_[:], op0=ALU.mult, op1=ALU.add),
                         reads=[p, cw, c_], writes=[val])
                else:
                    b.op("dve", lambda e: e.scalar_tensor_tensor(out=c_[:], in0=p[:, 2:TG + 2], scalar=cw[:, 2, ft:ft + 1], in1=c_[:], op0=ALU.mult, op1=ALU.add),
                         reads=[p, cw, c_], writes=[c_])
                    pending.append((ft, c_))
                if len(pending) > (1 if ft < NFT - 1 else 0):
                    while len(pending) > (1 if ft < NFT - 1 else 0):
                        f_, cc = pending.pop(0)
                        sg_ = sgl[f_ % 4]
                        b.op("act", lambda e: e.activation(out=sg_[:], in_=cc[:], func=AF.Silu), reads=[cc], writes=[sg_])
                        eng_ = "pool" if f_ % 2 == 0 else "dve"
                        b.op(eng_, lambda e: e.tensor_tensor(out=actT[:, f_ - 22, :], in0=sg_[:], in1=val[:, f_ - 22, :], op=ALU.mult),
                             reads=[sg_, val], writes=[actT])
            for s_ in range(TG // 128):
                t = gi * (TG // 128) + s_
                for n in range(2):
                    for f in range(22):
                        b.op("pe", lambda e: e.matmul(pd[n][:, :], lhsT=actT[:, f, s_ * 128:(s_ + 1) * 128], rhs=wd[:, f, n * 512:(n + 1) * 512], start=(f == 0), stop=(f == 21)),
                             reads=[actT, wd], writes=[pd[n]])
                    xo = xt[(gi % 2) * 2 + s_]
                    b.op("dve", lambda e: e.tensor_tensor(out=xo[:, n * 512:(n + 1) * 512], in0=pd[n][:, :], in1=xo[:, n * 512:(n + 1) * 512], op=ALU.add),
                         reads=[pd[n], xo], writes=[xo])
                b.dma("pool", self.out[t * 128:(t + 1) * 128, :], xt[(gi % 2) * 2 + s_][:], reads=[xt[(gi % 2) * 2 + s_]])


Prog.phase_ffn2 = _phase_ffn2
```
